# Optimizing a Trainium2 kernel written in Bass

```python
import math
import jax, jax.numpy as jnp
from jax import lax
import numpy as np

D_MODEL = 1024
BATCH = 2
SEQ = 8192
DEPTH = 1
DEC_BATCH = 128
DEC_SEQ = 1
PAST_LEN = 8192
PAGE_SIZE = 128

SSM_WIDTH = D_MODEL
SSM_GROUP = 16
SSM_GROUPS = SSM_WIDTH // SSM_GROUP
SSM_STATE = 64
SSM_CHUNK = 128
N_HEADS = 16
HEAD_DIM = D_MODEL // N_HEADS
N_KV_HEADS = 4
KV_GROUP = N_HEADS // N_KV_HEADS
ATTN_WIDTH = N_HEADS * HEAD_DIM
KV_WIDTH = N_KV_HEADS * HEAD_DIM
WINDOW = 128
ATTN_BLOCK = WINDOW
N_BUCKETS = 32
MAX_DISTANCE = WINDOW
NEG_INF = -1e30
LN_EPS = 1e-5
DEEPNORM_ALPHA = (2 * DEPTH) ** 0.25
DEEPNORM_BETA = (8 * DEPTH) ** -0.25
SPLITS = (SSM_WIDTH, SSM_WIDTH, ATTN_WIDTH, KV_WIDTH, KV_WIDTH, ATTN_WIDTH, D_MODEL, D_MODEL)
D_IN = SSM_WIDTH * 2 + ATTN_WIDTH * 2 + KV_WIDTH * 2 + D_MODEL * 2

kernel_name = "hybrid_s5_swa_gated_decoder_step"


def layer_norm(x, g, b):
    xf = x.astype(jnp.float32)
    mu = xf.mean(-1, keepdims=True)
    var = jnp.square(xf - mu).mean(-1, keepdims=True)
    return ((xf - mu) * lax.rsqrt(var + LN_EPS) * g.astype(jnp.float32) + b.astype(jnp.float32)).astype(x.dtype)


def rel_bucket(dist):
    max_exact = N_BUCKETS // 2
    df = jnp.maximum(dist, 1).astype(jnp.float32)
    large = max_exact + (jnp.log(df / max_exact) / math.log(MAX_DISTANCE / max_exact)
                         * (N_BUCKETS - max_exact)).astype(jnp.int32)
    large = jnp.minimum(large, N_BUCKETS - 1)
    return jnp.where(dist < max_exact, dist, large)


def rel_bias_from_dist(dist, rel_bias):
    b = rel_bias.astype(jnp.float32)[rel_bucket(jnp.clip(dist, 0, WINDOW))]
    return jnp.transpose(b, (2, 0, 1)).reshape(N_KV_HEADS, KV_GROUP, *dist.shape)


def sink_softmax(s, sinks):
    sk = sinks.astype(jnp.float32).reshape(N_KV_HEADS, KV_GROUP, 1)
    m = jnp.maximum(s.max(-1), sk)
    e = jnp.exp(s - m[..., None])
    return e / (e.sum(-1, keepdims=True) + jnp.exp(sk - m)[..., None])


def swa_prompt(q, k, v, sinks, rel_bias):
    b, l = q.shape[:2]
    nb = l // ATTN_BLOCK
    qb = q.reshape(b, nb, ATTN_BLOCK, N_KV_HEADS, KV_GROUP, HEAD_DIM)

    def band(t):
        tb = t.reshape(b, nb, ATTN_BLOCK, N_KV_HEADS, HEAD_DIM)
        prev = jnp.concatenate([jnp.zeros_like(tb[:, :1]), tb[:, :-1]], axis=1)
        return jnp.concatenate([prev, tb], axis=2)

    kk, vv = band(k), band(v)
    s = jnp.einsum('bnqkgd,bnskd->bnkgqs', qb, kk, preferred_element_type=jnp.float32) * (HEAD_DIM ** -0.5)
    qi = jnp.arange(ATTN_BLOCK)[:, None]
    kj = jnp.arange(2 * ATTN_BLOCK)[None, :]
    dist = qi + ATTN_BLOCK - kj
    key_pos = jnp.arange(nb)[:, None, None] * ATTN_BLOCK + kj[None] - ATTN_BLOCK
    mask = ((dist >= 0) & (dist <= WINDOW))[None] & (key_pos >= 0)
    s = s + rel_bias_from_dist(dist, rel_bias)
    s = jnp.where(mask[None, :, None, None], s, NEG_INF)
    p = sink_softmax(s, sinks)
    o = jnp.einsum('bnkgqs,bnskd->bnqkgd', p.astype(vv.dtype), vv)
    return o.reshape(b, l, ATTN_WIDTH)


def swa_sample(q, k, v, k_buf, v_buf, sinks, rel_bias):
    b, t = q.shape[:2]
    qg = q.reshape(b, t, N_KV_HEADS, KV_GROUP, HEAD_DIM)
    kk = jnp.concatenate([k_buf.astype(k.dtype), k], axis=1)
    vv = jnp.concatenate([v_buf.astype(v.dtype), v], axis=1)
    s = jnp.einsum('btkgd,bskd->bkgts', qg, kk, preferred_element_type=jnp.float32) * (HEAD_DIM ** -0.5)
    dist = jnp.arange(t)[:, None] + WINDOW - jnp.arange(WINDOW + t)[None, :]
    mask = (dist >= 0) & (dist <= WINDOW)
    s = s + rel_bias_from_dist(dist, rel_bias)
    s = jnp.where(mask, s, NEG_INF)
    p = sink_softmax(s, sinks)
    o = jnp.einsum('bkgts,bskd->btkgd', p.astype(vv.dtype), vv)
    return o.reshape(b, t, ATTN_WIDTH), kk[:, -WINDOW:], vv[:, -WINDOW:]


def ssm_discretise(lam_re, lam_im, log_delta, b_re, b_im):
    lr = lam_re.astype(jnp.float32)
    li = lam_im.astype(jnp.float32)
    dt = jnp.exp(log_delta.astype(jnp.float32))[:, None]
    mag = jnp.exp(lr * dt)
    ar, ai = mag * jnp.cos(li * dt), mag * jnp.sin(li * dt)
    den = lr * lr + li * li
    nr = ar - 1.0
    cr = (nr * lr + ai * li) / den
    ci = (ai * lr - nr * li) / den
    br, bi = b_re.astype(jnp.float32), b_im.astype(jnp.float32)
    bbr = cr[..., None] * br - ci[..., None] * bi
    bbi = cr[..., None] * bi + ci[..., None] * br
    return ar, ai, bbr, bbi


def complex_combine(e1, e2):
    a1r, a1i, b1r, b1i = e1
    a2r, a2i, b2r, b2i = e2
    return (a2r * a1r - a2i * a1i,
            a2r * a1i + a2i * a1r,
            a2r * b1r - a2i * b1i + b2r,
            a2r * b1i + a2i * b1r + b2i)


def ssm_segment(u, hr, hi, ar, ai, bbr, bbi, c_re, c_im):
    xr = jnp.einsum('btgc,gpc->btgp', u, bbr)
    xi = jnp.einsum('btgc,gpc->btgp', u, bbi)
    xr = xr.at[:, 0].add(ar * hr - ai * hi)
    xi = xi.at[:, 0].add(ar * hi + ai * hr)
    shp = xr.shape
    _, _, sr, si = lax.associative_scan(
        complex_combine, (jnp.broadcast_to(ar, shp), jnp.broadcast_to(ai, shp), xr, xi), axis=1)
    y = (jnp.einsum('btgp,gcp->btgc', sr, c_re.astype(jnp.float32))
         - jnp.einsum('btgp,gcp->btgc', si, c_im.astype(jnp.float32)))
    return y, sr[:, -1], si[:, -1]


def ssm_mixer(u, h0r, h0i, lam_re, lam_im, log_delta, b_re, b_im, c_re, c_im, d_skip):
    b, l, _ = u.shape
    uf = u.astype(jnp.float32)
    ar, ai, bbr, bbi = ssm_discretise(lam_re, lam_im, log_delta, b_re, b_im)
    chunk = SSM_CHUNK if l % SSM_CHUNK == 0 else l
    nc = l // chunk
    xs = uf.reshape(b, nc, chunk, SSM_GROUPS, SSM_GROUP).transpose(1, 0, 2, 3, 4)

    def step(carry, uc):
        y, hr, hi = ssm_segment(uc, carry[0], carry[1], ar, ai, bbr, bbi, c_re, c_im)
        return (hr, hi), y

    (hr, hi), ys = lax.scan(step, (h0r.astype(jnp.float32), h0i.astype(jnp.float32)), xs)
    y = ys.transpose(1, 0, 2, 3, 4).reshape(b, l, SSM_WIDTH)
    return y + d_skip.astype(jnp.float32) * uf, hr, hi


def decoder_layer(x, c, h0r, h0i, k_buf, v_buf, lp, rel_bias):
    (w_ada, b_ada, w_in, lam_re, lam_im, log_delta, b_re, b_im, c_re, c_im, d_skip,
     w_glu, b_glu, sinks, w_branch_s, w_branch_a, w_out, ln_g, ln_b) = lp
    b, l, _ = x.shape
    mod = jax.nn.silu(c) @ w_ada + b_ada
    shift, scale, gate = jnp.split(mod, 3, axis=-1)
    h = x * (1.0 + scale[:, None]) + shift[:, None]
    proj = h @ w_in
    points = [int(p) for p in np.cumsum(SPLITS)[:-1]]
    u_s, z_s, q, k, v, z_a, g_s, g_a = jnp.split(proj, points, axis=-1)
    y_s, hr, hi = ssm_mixer(u_s, h0r, h0i, lam_re, lam_im, log_delta, b_re, b_im, c_re, c_im, d_skip)
    y_s = jax.nn.gelu(y_s)
    y_s = y_s * jax.nn.sigmoid(y_s @ w_glu.astype(jnp.float32) + b_glu.astype(jnp.float32))
    y_s = (y_s * jax.nn.silu(z_s.astype(jnp.float32))).astype(x.dtype)
    b_s = y_s @ w_branch_s
    k = k.reshape(b, l, N_KV_HEADS, HEAD_DIM)
    v = v.reshape(b, l, N_KV_HEADS, HEAD_DIM)
    if k_buf is None:
        o_a = swa_prompt(q, k, v, sinks, rel_bias)
        new_k, new_v = k[:, -WINDOW:], v[:, -WINDOW:]
    else:
        o_a, new_k, new_v = swa_sample(q, k, v, k_buf, v_buf, sinks, rel_bias)
    b_a = (o_a * jax.nn.silu(z_a)) @ w_branch_a
    m = jax.nn.sigmoid(g_s) * b_s + jax.nn.sigmoid(g_a) * b_a
    out = m @ w_out
    y = layer_norm(DEEPNORM_ALPHA * x + gate[:, None] * out, ln_g, ln_b)
    return y, hr, hi, new_k, new_v


def setup_inputs(seed: int = 0) -> dict:
    key = jax.random.key(seed)
    ks = iter(jax.random.split(key, 40))
    nrm = lambda shape, s: jax.random.normal(next(ks), shape, jnp.float32) * s
    inputs = {}
    inputs['x_prompt'] = nrm((BATCH, SEQ, D_MODEL), 1.0)
    inputs['x_sample'] = nrm((DEC_BATCH, DEC_SEQ, D_MODEL), 1.0)
    inputs['c_prompt'] = nrm((BATCH, D_MODEL), 1.0)
    inputs['c_sample'] = nrm((DEC_BATCH, D_MODEL), 1.0)
    inputs['state_ssm_re'] = nrm((DEPTH, DEC_BATCH, SSM_GROUPS, SSM_STATE), 0.3)
    inputs['state_ssm_im'] = nrm((DEPTH, DEC_BATCH, SSM_GROUPS, SSM_STATE), 0.3)
    inputs['cache_swa_k'] = nrm((DEPTH, DEC_BATCH, WINDOW, N_KV_HEADS, HEAD_DIM), 1.0)
    inputs['cache_swa_v'] = nrm((DEPTH, DEC_BATCH, WINDOW, N_KV_HEADS, HEAD_DIM), 1.0)
    inputs['w_ada'] = nrm((DEPTH, D_MODEL, 3 * D_MODEL), 0.5 * D_MODEL ** -0.5)
    inputs['b_ada'] = nrm((DEPTH, 3 * D_MODEL), 0.02)
    inputs['w_in'] = nrm((DEPTH, D_MODEL, D_IN), D_MODEL ** -0.5)
    inputs['ssm_lambda_re'] = -0.5 + nrm((DEPTH, SSM_GROUPS, SSM_STATE), 0.01)
    inputs['ssm_lambda_im'] = (jnp.pi * jnp.arange(SSM_STATE, dtype=jnp.float32))[None, None] + nrm((DEPTH, SSM_GROUPS, SSM_STATE), 0.01)
    inputs['ssm_log_delta'] = jax.random.uniform(next(ks), (DEPTH, SSM_GROUPS), jnp.float32, math.log(1e-3), math.log(1e-1))
    inputs['ssm_b_re'] = nrm((DEPTH, SSM_GROUPS, SSM_STATE, SSM_GROUP), (2 * SSM_GROUP) ** -0.5)
    inputs['ssm_b_im'] = nrm((DEPTH, SSM_GROUPS, SSM_STATE, SSM_GROUP), (2 * SSM_GROUP) ** -0.5)
    inputs['ssm_c_re'] = nrm((DEPTH, SSM_GROUPS, SSM_GROUP, SSM_STATE), SSM_STATE ** -0.5)
    inputs['ssm_c_im'] = nrm((DEPTH, SSM_GROUPS, SSM_GROUP, SSM_STATE), SSM_STATE ** -0.5)
    inputs['ssm_d'] = nrm((DEPTH, SSM_WIDTH), 1.0)
    inputs['w_glu'] = nrm((DEPTH, SSM_WIDTH, SSM_WIDTH), SSM_WIDTH ** -0.5)
    inputs['b_glu'] = nrm((DEPTH, SSM_WIDTH), 0.02)
    inputs['attn_sinks'] = nrm((DEPTH, N_HEADS), 0.5)
    inputs['rel_bias'] = nrm((N_BUCKETS, N_HEADS), 0.1)
    inputs['w_branch_s'] = nrm((DEPTH, SSM_WIDTH, D_MODEL), SSM_WIDTH ** -0.5)
    inputs['w_branch_a'] = nrm((DEPTH, ATTN_WIDTH, D_MODEL), ATTN_WIDTH ** -0.5)
    inputs['w_out'] = nrm((DEPTH, D_MODEL, D_MODEL), DEEPNORM_BETA * D_MODEL ** -0.5)
    inputs['ln_g'] = 1.0 + nrm((DEPTH, D_MODEL), 0.02)
    inputs['ln_b'] = nrm((DEPTH, D_MODEL), 0.02)
    return inputs


def reference(x_prompt, x_sample, c_prompt, c_sample, state_ssm_re, state_ssm_im, cache_swa_k, cache_swa_v,
              w_ada, b_ada, w_in, ssm_lambda_re, ssm_lambda_im, ssm_log_delta, ssm_b_re, ssm_b_im,
              ssm_c_re, ssm_c_im, ssm_d, w_glu, b_glu, attn_sinks, rel_bias, w_branch_s, w_branch_a,
              w_out, ln_g, ln_b):
    yp, ys = x_prompt, x_sample
    p_hr, p_hi, p_k, p_v = [], [], [], []
    s_hr, s_hi, s_k, s_v = [], [], [], []
    zeros_state = jnp.zeros((x_prompt.shape[0], SSM_GROUPS, SSM_STATE), jnp.float32)
    for i in range(DEPTH):
        lp = (w_ada[i], b_ada[i], w_in[i], ssm_lambda_re[i], ssm_lambda_im[i], ssm_log_delta[i],
              ssm_b_re[i], ssm_b_im[i], ssm_c_re[i], ssm_c_im[i], ssm_d[i], w_glu[i], b_glu[i],
              attn_sinks[i], w_branch_s[i], w_branch_a[i], w_out[i], ln_g[i], ln_b[i])
        yp, hr, hi, nk, nv = decoder_layer(yp, c_prompt, zeros_state, zeros_state, None, None, lp, rel_bias)
        p_hr.append(hr); p_hi.append(hi); p_k.append(nk); p_v.append(nv)
        ys, hr, hi, nk, nv = decoder_layer(ys, c_sample, state_ssm_re[i], state_ssm_im[i],
                                           cache_swa_k[i], cache_swa_v[i], lp, rel_bias)
        s_hr.append(hr); s_hi.append(hi); s_k.append(nk); s_v.append(nv)
    return (yp, ys,
            jnp.stack(p_hr), jnp.stack(p_hi), jnp.stack(p_k), jnp.stack(p_v),
            jnp.stack(s_hr), jnp.stack(s_hi), jnp.stack(s_k), jnp.stack(s_v))
```

```python
import math
import os
from contextlib import ExitStack
import numpy as np
import ml_dtypes
import concourse.bass as bass
import concourse.mybir as mybir
from concourse.bass_utils import run_bass_kernel_spmd

F32 = mybir.dt.float32
BF16 = mybir.dt.bfloat16
U8 = mybir.dt.uint8
ALU = mybir.AluOpType
AF = mybir.ActivationFunctionType
AX = mybir.AxisListType

PE, ACT, DVE, POOL, SP = "tensor", "scalar", "vector", "gpsimd", "sync"
ENGS = [PE, ACT, DVE, POOL, SP]

NCORES = 8
D = 1024
NP = 2048
NS = 16
NT = NP + NS
TBS = [(0, 512), (512, 512), (1024, 512), (1536, 512), (2048, 16)]
DIN = 6656
OFF_U, OFF_ZS, OFF_Q, OFF_K, OFF_V, OFF_ZA, OFF_GS, OFF_GA = 0, 1024, 2048, 3072, 3328, 3584, 4608, 5632
ALPHA = 2.0 ** 0.25
LN_EPS = 1e-5
DEBUG = False


class Dep:
    __slots__ = ("sem", "val")

    def __init__(self, sem, val):
        self.sem = sem
        self.val = val


class Buf:
    __slots__ = ("w", "r", "name")

    def __init__(self, name=""):
        self.w = []
        self.r = []
        self.name = name


class _RProxy:
    def __init__(self, bufs):
        self.bufs = bufs

    def append(self, h):
        for b in self.bufs:
            b.r.append(h)

    def __len__(self):
        return 0


class BufGroup:
    def __init__(self, bufs):
        self.bufs = list(bufs)
        self.r = _RProxy(self.bufs)

    @property
    def w(self):
        return [h for b in self.bufs for h in b.w]


def handoff(new_bufs, old_bufs):
    deps = []
    for b in old_bufs:
        deps.extend(b.w)
        deps.extend(b.r)
    for nb in new_bufs:
        nb.r = list(nb.r) + deps


class DSem:
    def __init__(self, h):
        self.h = h
        self.cnt = 0


class Prog:
    def __init__(self, nc, stack):
        self.nc = nc
        self.q = {e: [] for e in ENGS}
        self.esem = {}
        self.cnt = {e: 0 for e in ENGS}
        self.allsems = []
        for e in [PE, ACT, DVE, POOL]:
            self.esem[e] = nc.alloc_semaphore("s_" + e)
            self.allsems.append(self.esem[e])
        self.seen = {}
        self.stack = stack
        self.nd = 0
        self.ninst = {e: 0 for e in ENGS}
        self.dead = False
        self.stop = int(os.environ.get("KSTOP", "99"))

    def phase(self, n):
        self.dead = n > self.stop

    def dsem(self, name=None):
        self.nd += 1
        h = self.nc.alloc_semaphore(f"d{self.nd}_{name or 'm'}")
        self.allsems.append(h)
        return DSem(h)

    def _waits(self, eng, deps):
        ws = []
        for d in deps:
            if d is None:
                continue
            k = (eng, id(d.sem))
            if self.seen.get(k, 0) >= d.val:
                continue
            self.seen[k] = d.val
            ws.append((d.sem, d.val))
        return ws

    @staticmethod
    def _compact(lst):
        best = {}
        for d in lst:
            k = id(d.sem)
            if k not in best or best[k].val < d.val:
                best[k] = d
        return list(best.values())

    @staticmethod
    def _bufdeps(reads, writes, wadd=()):
        deps = []
        for b in reads:
            deps.extend(b.w)
        for b in writes:
            deps.extend(b.w)
            deps.extend(b.r)
        for b in wadd:
            deps.extend(b.r)
        return deps

    @classmethod
    def _update(cls, h, reads, writes, wadd=()):
        for b in reads:
            b.r.append(h)
            if len(b.r) > 32:
                b.r = cls._compact(b.r)
        for b in writes:
            b.w = [h]
            b.r = []
        for b in wadd:
            b.w.append(h)
            if len(b.w) > 32:
                b.w = cls._compact(b.w)

    def op(self, eng, fn, reads=(), writes=(), deps=(), sig=True, wadd=()):
        if self.dead:
            return None
        alld = list(deps) + self._bufdeps(reads, writes, wadd)
        ws = self._waits(eng, alld)
        h = None
        if sig:
            self.cnt[eng] += 1
            h = Dep(self.esem[eng], self.cnt[eng])
        sem = self.esem[eng] if sig else None
        self.ninst[eng] += 1 + len(ws)

        def run(e, ws=ws, fn=fn, sem=sem):
            for (s, v) in ws:
                e.wait_ge(s, v)
            ins = fn(e)
            if sem is not None:
                ins.then_inc(sem, 1)
        self.q[eng].append(run)
        if h is not None:
            self._update(h, reads, writes, wadd)
        return h

    def attach(self, h, reads=(), writes=(), wadd=()):
        if self.dead or h is None:
            return
        self._update(h, reads, writes, wadd)

    def dma(self, eng, ds, out, in_, reads=(), writes=(), deps=(), wadd=(), **kw):
        if self.dead:
            return None
        alld = list(deps) + self._bufdeps(reads, writes, wadd)
        ws = self._waits(eng, alld)
        ds.cnt += 16
        h = Dep(ds.h, ds.cnt)
        self.ninst[eng] += 1 + len(ws)

        def run(e, ws=ws, out=out, in_=in_, kw=kw, sh=ds.h):
            for (s, v) in ws:
                e.wait_ge(s, v)
            e.dma_start(out=out, in_=in_, **kw).then_inc(sh, 16)
        self.q[eng].append(run)
        self._update(h, reads, writes, wadd)
        return h

    def raw(self, eng, fn, ds, inc, reads=(), writes=(), deps=()):
        if self.dead:
            return None
        alld = list(deps) + self._bufdeps(reads, writes)
        ws = self._waits(eng, alld)
        ds.cnt += inc
        h = Dep(ds.h, ds.cnt)

        def run(e, ws=ws, fn=fn, sh=ds.h, inc=inc):
            for (s, v) in ws:
                e.wait_ge(s, v)
            fn(e).then_inc(sh, inc)
        self.q[eng].append(run)
        self._update(h, reads, writes)
        return h

    def wait(self, eng, deps):
        ws = self._waits(eng, deps)

        def run(e, ws=ws):
            for (s, v) in ws:
                e.wait_ge(s, v)
        self.q[eng].append(run)

    def emit(self):
        nc = self.nc
        with nc.Block() as block:
            @block.tensor
            def _(e):
                for f in self.q[PE]:
                    f(e)

            @block.scalar
            def _(e):
                for f in self.q[ACT]:
                    f(e)

            @block.vector
            def _(e):
                for f in self.q[DVE]:
                    f(e)

            @block.gpsimd
            def _(e):
                for f in self.q[POOL]:
                    f(e)

            @block.sync
            def _(e):
                for f in self.q[SP]:
                    f(e)


def _dsize(dt):
    return {F32: 4, BF16: 2, U8: 1}[dt]


class Arena:
    def __init__(self, nc, stack, nbytes):
        self.t = stack.enter_context(nc.sbuf_tensor("arena", [128, nbytes], U8))
        self.nbytes = nbytes

    def carve(self, off, shape, dt):
        n = int(np.prod(shape)) * _dsize(dt)
        assert off % 4 == 0 and off + n <= self.nbytes, (off, n, self.nbytes)
        v = self.t[:, off:off + n]
        if dt != U8:
            v = v.bitcast(dt)
        if len(shape) > 1:
            names = [f"a{i}" for i in range(len(shape))]
            pat = "p (" + " ".join(names) + ") -> p " + " ".join(names)
            v = v.rearrange(pat, **{names[i]: shape[i] for i in range(len(shape))})
        return v


O_HT = 0
O_HTH = 33024
O_CONST = 35072
O_W = 45312
O_A = 61696
O_B = 94720
O_C = 127744
ARENA = 212000
C_SIZE = ARENA - O_C


def build():
    nc = bass.Bass("TRN2", target_bir_lowering=False)

    def din(name, shape, dt=F32):
        return nc.dram_tensor(name, list(shape), dt, kind="ExternalInput").ap()

    def dout(name, shape, dt=F32):
        return nc.dram_tensor(name, list(shape), dt, kind="ExternalOutput").ap()

    xprev = din("xprev", [3, NP, D]); xp = din("xp", [NP, D]); xh = din("xh", [128, D]); xs = din("xs", [NS, D]); ccin = din("cc", [17, D])
    st_re = din("st_re", [NS, 4096]); st_im = din("st_im", [NS, 4096])
    ck = din("ck", [NS, 128, 256]); cv = din("cv", [NS, 128, 256])
    w_ada = din("w_ada", [D, 3072]); b_ada = din("b_ada", [3072]); w_in = din("w_in", [D, DIN])
    lam_re = din("lam_re", [64, 64]); lam_im = din("lam_im", [64, 64]); log_delta = din("log_delta", [64])
    b_re = din("b_re", [4096, 16]); b_im = din("b_im", [4096, 16])
    c_re = din("c_re", [1024, 64]); c_im = din("c_im", [1024, 64])
    ssm_d = din("ssm_d", [1024]); w_glu = din("w_glu", [D, D]); b_glu = din("b_glu", [D])
    sinks = din("sinks", [16]); rel_bias = din("rel_bias", [32, 16])
    w_bs = din("w_bs", [D, D]); w_ba = din("w_ba", [D, D]); w_out = din("w_out", [D, D])
    ln_g = din("ln_g", [D]); ln_b = din("ln_b", [D])
    rtab = din("rtab", [32, 384]); maskc = din("maskc", [128, 256]); bmaskc = din("bmaskc", [128, 128])
    diagc = din("diagc", [16, 64]); flagsc = din("flags", [32])

    yp = dout("yp", [NP, D]); ys = dout("ys", [NS, D])
    pst_re = dout("pst_re", [32, 128]); pst_im = dout("pst_im", [32, 128])
    pck = dout("pck", [128, 256]); pcv = dout("pcv", [128, 256])
    sst_re = dout("sst_re", [NS, 4096]); sst_im = dout("sst_im", [NS, 4096])
    sck = dout("sck", [NS, 128, 256]); scv = dout("scv", [NS, 128, 256])
    dbg = {}
    if DEBUG:
        dbg["hT"] = dout("dbg_hT", [128, 8, NT], BF16)
        dbg["uT"] = dout("dbg_uT", [128, 8, NT], BF16)
        dbg["pw"] = dout("dbg_pw", [128, 9 * 2 * 32])
        dbg["hend"] = dout("dbg_hend", [128, 2 * 32 * 16])
        dbg["bb"] = dout("dbg_bb", [128, 2 * 32 * 16]); dbg["ccm"] = dout("dbg_ccm", [128, 2 * 32 * 16])
        dbg["R8"] = dout("dbg_R8", [128, 16 * 2 * 32]); dbg["R128"] = dout("dbg_R128", [128, 16 * 2 * 32])
        dbg["A2k"] = dout("dbg_A2k", [128, 3 * 2 * 32])
        dbg["WinL"] = dout("dbg_WinL", [128, 8 * 8 * 2 * 128], BF16)
        dbg["X0"] = dout("dbg_X0", [128, 512])
        dbg["KL"] = dout("dbg_KL", [128, 2 * 8 * 128], BF16); dbg["Ca"] = dout("dbg_Ca", [128, 2 * 4 * 9 * 2 * 32], BF16)
        dbg["Hb"] = dout("dbg_Hb", [128, 2 * 2048], BF16); dbg["Xs"] = dout("dbg_Xs", [128, 2 * 2048])
        dbg["carry"] = dout("dbg_carry", [128, 17 * 2 * 32])
        dbg["yT"] = dout("dbg_yT", [128, 8, NT], BF16)
        dbg["gbs"] = dout("dbg_gbs", [128, 8, NT], BF16)
        dbg["oT"] = dout("dbg_oT", [128, 8, NT], BF16)
        dbg["mT"] = dout("dbg_mT", [128, 8, NT], BF16)
        dbg["modT"] = dout("dbg_modT", [128, 24 * 17])

    ib = nc.dram_tensor("cc_ib", [128, 64], F32, kind="Internal")
    ob = nc.dram_tensor("cc_ob", [NCORES * 128, 64], F32, kind="Internal")

    st = ExitStack()
    with st:
        P = Prog(nc, st)
        AR = Arena(nc, st, ARENA)
        cv_ = AR.carve
        ps = [st.enter_context(nc.psum_tensor(f"ps{i}", [128, 512], F32)) for i in range(8)]
        psb = [Buf(f"ps{i}") for i in range(8)]
        dout_sem = P.dsem("dout")
        misc_sem = P.dsem("misc")

        def misc_load(eng, out, in_, buf, wadd=False, **kw):
            if P.dead:
                return None
            if wadd:
                return P.dma(eng, P.dsem(), out, in_, wadd=[buf], **kw)
            return P.dma(eng, P.dsem(), out, in_, writes=[buf], **kw)

        hT = cv_(O_HT, [8, NT], BF16)
        hTh = cv_(O_HTH, [8, 128], BF16)
        hT_b = [Buf(f"hT{i}") for i in range(len(TBS))]
        hTh_b = Buf("hTh")
        o = O_CONST
        ident = cv_(o, [128], F32); o += 512
        modT = cv_(o, [24, 17], F32); o += 1664
        op1p = cv_(o, [8, 17], F32); o += 576
        flags = cv_(o, [32], F32); o += 128
        Dm = cv_(o, [8], F32); o += 32
        bglu = cv_(o, [8], F32); o += 32
        ES = cv_(o, [4, 4, 128], BF16); o += 4096
        onesd = cv_(o, [128], BF16); o += 256
        EBself = cv_(o, [16], F32); o += 64
        bmask = cv_(o, [128], F32); o += 512
        ones1 = cv_(o, [128], F32); o += 512
        assert o <= O_CONST + 10240
        b_ident = Buf(); b_modT = Buf(); b_flags = Buf(); b_Dm = Buf(); b_bglu = Buf(); b_ES = Buf()
        b_ones = Buf(); b_EBself = Buf(); b_bmask = Buf()
        wslot = [cv_(O_W + 8192 * i, [8, 512], BF16) for i in range(2)]
        wslot_b = [Buf("w0"), Buf("w1")]
        wsem = [P.dsem("w0"), P.dsem("w1")]
        wctr = [0]
        RA = cv_(O_A, [8, NT], BF16)
        RB = cv_(O_B, [8, NT], BF16)

        rr = {"i": 0}

        def bank():
            i = rr["i"] % 8
            rr["i"] += 1
            return i

        def load_w(src2d, ncols):
            s = wctr[0] % 2
            wctr[0] += 1
            P.dma(POOL, wsem[s], wslot[s][:, :, 0:ncols], src2d.rearrange("(k p) f -> p k f", p=128),
                  writes=[wslot_b[s]])
            return s

        def evac_copy(i, out_ap, in_ap, reads, writes, wadd=()):
            if i % 2 == 0:
                return P.op(ACT, lambda e: e.activation(out=out_ap, in_=in_ap, func=AF.Copy), reads=reads, writes=writes, wadd=wadd)
            return P.op(DVE, lambda e: e.tensor_copy(out=out_ap, in_=in_ap), reads=reads, writes=writes, wadd=wadd)

        def proj_fm(src2d, ncols, rhs_of, rhs_bufs, evac, tbs=TBS):
            s = load_w(src2d, ncols)
            for oc in range(ncols // 128):
                for tbi, (t0, n) in enumerate(tbs):
                    b = bank()
                    for k in range(8):
                        last = (k == 7)
                        P.op(PE, lambda e, b=b, k=k, oc=oc, tbi=tbi, t0=t0, n=n, s=s: e.matmul(
                            ps[b][:, 0:n], lhsT=wslot[s][:, k, oc * 128:(oc + 1) * 128], rhs=rhs_of(k, t0, n),
                            start=(k == 0), stop=(k == 7)),
                            reads=[wslot_b[s], rhs_bufs[tbi]] if k == 0 else [], writes=[psb[b]] if k == 0 else [],
                            sig=last)
                        if last and not P.dead:
                            h = Dep(P.esem[PE], P.cnt[PE])
                            P.attach(h, reads=[wslot_b[s], rhs_bufs[tbi]], writes=[psb[b]])
                    evac(oc, tbi, ps[b][:, 0:n], psb[b])

        def cmul(eng, dst_r, dst_i, xr, xi, yr, yi, t1, t2, bufs_r, bufs_w, tb):
            P.op(eng, lambda e: e.tensor_tensor(out=t1, in0=xr, in1=yr, op=ALU.mult), reads=bufs_r, writes=[tb])
            P.op(eng, lambda e: e.tensor_tensor(out=t2, in0=xi, in1=yi, op=ALU.mult), reads=bufs_r, writes=[tb])
            P.op(eng, lambda e: e.tensor_tensor(out=dst_r, in0=t1, in1=t2, op=ALU.subtract), reads=[tb], writes=bufs_w)
            P.op(eng, lambda e: e.tensor_tensor(out=t1, in0=xr, in1=yi, op=ALU.mult), reads=bufs_r + bufs_w, writes=[tb])
            P.op(eng, lambda e: e.tensor_tensor(out=t2, in0=xi, in1=yr, op=ALU.mult), reads=bufs_r + bufs_w, writes=[tb])
            P.op(eng, lambda e: e.tensor_tensor(out=dst_i, in0=t1, in1=t2, op=ALU.add), reads=[tb], writes=bufs_w)

        P.phase(0)
        P.op(POOL, lambda e: e.memset(ident, 0.0), writes=[b_ident])
        P.op(POOL, lambda e: e.affine_select(out=ident, in_=ident, pattern=[[-1, 128]], compare_op=ALU.not_equal,
                                             fill=1.0, base=0, channel_multiplier=1), writes=[b_ident])
        misc_load(SP, flags, flagsc.rearrange("(o n) -> o n", o=1).to_broadcast([128, 32]), b_flags)
        misc_load(SP, bmask, bmaskc, b_bmask)
        P.op(POOL, lambda e: e.memset(ones1[0:1, :], 1.0), writes=[b_ones])
        P.op(POOL, lambda e: e.memset(onesd[0:1, 0:64], 0.0), wadd=[b_ones])
        P.op(POOL, lambda e: e.memset(onesd[0:1, 64:128], 1.0), wadd=[b_ones])

        P.phase(1)
        c_t = cv_(O_C + 62208, [1024], F32); c_sg = cv_(O_C + 66304, [1024], F32)
        ccT = cv_(O_C + 70400, [8, 17], BF16); badain = cv_(O_C + 70912, [128], F32); badaT = cv_(O_C + 71424, [24], F32)
        b_ct = Buf(); b_csg = Buf(); b_ccT = Buf(); b_bin = Buf(); b_baT = Buf()
        misc_load(SP, c_t[0:17, :], ccin, b_ct)
        misc_load(SP, badain[0:24, :], b_ada.rearrange("(c p) -> c p", p=128), b_bin)
        P.op(ACT, lambda e: e.activation(out=c_sg[0:17, :], in_=c_t[0:17, :], func=AF.Sigmoid), reads=[b_ct], writes=[b_csg])
        P.op(DVE, lambda e: e.tensor_tensor(out=c_sg[0:17, :], in0=c_sg[0:17, :], in1=c_t[0:17, :], op=ALU.mult),
             reads=[b_ct], writes=[b_csg])
        bk = bank()
        for k in range(8):
            P.op(PE, lambda e, k=k: e.transpose(out=ps[bk][:, 17 * k:17 * k + 17], in_=c_sg[0:17, 128 * k:128 * k + 128],
                                                identity=ident[0:17, 0:17]),
                 reads=[b_csg, b_ident], writes=[psb[bk]] if k == 0 else [], wadd=[psb[bk]] if k > 0 else [])
        P.op(DVE, lambda e: e.tensor_copy(out=ccT.rearrange("p k s -> p (k s)"), in_=ps[bk][:, 0:136]), reads=[psb[bk]], writes=[b_ccT])
        bk2 = bank()
        P.op(PE, lambda e: e.transpose(out=ps[bk2][:, 0:24], in_=badain[0:24, :], identity=ident[0:24, 0:24]),
             reads=[b_bin, b_ident], writes=[psb[bk2]])
        P.op(DVE, lambda e: e.tensor_copy(out=badaT, in_=ps[bk2][:, 0:24]), reads=[psb[bk2]], writes=[b_baT])
        bkm = bank()
        hlast = None
        for blk in range(6):
            s = load_w(w_ada[:, 512 * blk:512 * blk + 512], 512)
            for oc in range(4):
                f = 4 * blk + oc
                for k in range(8):
                    first = (blk == 0 and oc == 0 and k == 0)
                    lastk = (k == 7)
                    hlast = P.op(PE, lambda e, f=f, k=k, oc=oc, s=s: e.matmul(
                        ps[bkm][:, 17 * f:17 * f + 17], lhsT=wslot[s][:, k, oc * 128:(oc + 1) * 128], rhs=ccT[:, k, :],
                        start=(k == 0), stop=(k == 7)),
                        reads=[wslot_b[s], b_ccT] if k == 0 else [], writes=[psb[bkm]] if first else [], sig=lastk and oc == 3)
            P.attach(hlast, reads=[wslot_b[s]], wadd=[psb[bkm]])
        P.op(DVE, lambda e: e.tensor_tensor(out=modT, in0=ps[bkm][:, 0:408].rearrange("p (f s) -> p f s", f=24),
                                            in1=badaT.unsqueeze(2).to_broadcast([128, 24, 17]), op=ALU.add),
             reads=[psb[bkm], b_baT], writes=[b_modT])
        P.op(DVE, lambda e: e.tensor_scalar(out=op1p, in0=modT[:, 8:16, :], scalar1=1.0, scalar2=None, op0=ALU.add),
             reads=[b_modT], wadd=[b_modT])

        xst = [cv_(O_C + 71552 + 4096 * i, [1024], F32) for i in range(2)]
        xst_b = [Buf(), Buf()]
        xsem = [P.dsem("x0"), P.dsem("x1")]
        hTs_b = hT_b[4]
        tmpS = cv_(O_C + 79744, [8, 16], F32)
        b_tmpS = Buf()

        def phase_a(xsrc, full):
            tiles = [("p", i) for i in range(16)] + ([("h", 0), ("s", 0)] if full else [])
            for ti, (kind, i) in enumerate(tiles):
                sl = ti % 2
                if kind == "p":
                    src, rows, dst, dbuf = xsrc[128 * i:128 * i + 128, :], 128, (lambda k, i=i: hT[:, k, 128 * i:128 * i + 128]), hT_b[i // 4]
                elif kind == "h":
                    src, rows, dst, dbuf = xh, 128, (lambda k: hTh[:, k, :]), hTh_b
                else:
                    src, rows, dst, dbuf = xs, NS, None, hTs_b
                P.dma(SP, xsem[sl], xst[sl][0:rows, :], src, writes=[xst_b[sl]])
                b0, b1 = bank(), bank()
                for k in range(8):
                    bb_ = b0 if k < 4 else b1
                    j = k % 4
                    P.op(PE, lambda e, k=k, bb_=bb_, j=j, sl=sl, rows=rows: e.transpose(
                        out=ps[bb_][:, rows * j:rows * j + rows], in_=xst[sl][0:rows, 128 * k:128 * k + 128],
                        identity=ident[0:rows, 0:rows]),
                        reads=[xst_b[sl], b_ident], writes=[psb[bb_]] if j == 0 else [], wadd=[psb[bb_]] if j > 0 else [])
                if kind != "s":
                    for k in range(8):
                        bb_ = b0 if k < 4 else b1
                        j = k % 4
                        src_ps = ps[bb_][:, 128 * j:128 * j + 128]
                        if k < 4:
                            P.op(DVE, lambda e, k=k, src_ps=src_ps, dst=dst: e.tensor_scalar(
                                out=dst(k), in0=src_ps, scalar1=op1p[:, k, 0:1], scalar2=modT[:, k, 0:1], op0=ALU.mult, op1=ALU.add),
                                reads=[psb[bb_], b_modT], wadd=[dbuf])
                        else:
                            P.op(ACT, lambda e, k=k, src_ps=src_ps, dst=dst: e.activation(
                                out=dst(k), in_=src_ps, func=AF.Identity, scale=op1p[:, k, 0:1], bias=modT[:, k, 0:1]),
                                reads=[psb[bb_], b_modT], wadd=[dbuf])
                else:
                    for half, bb_ in enumerate([b0, b1]):
                        P.op(DVE, lambda e, half=half, bb_=bb_: e.tensor_tensor(
                            out=tmpS[:, 4 * half:4 * half + 4, :], in0=ps[bb_][:, 0:64].rearrange("p (k s) -> p k s", k=4),
                            in1=op1p[:, 4 * half:4 * half + 4, 1:17], op=ALU.mult), reads=[psb[bb_], b_modT], wadd=[b_tmpS])
                    P.op(DVE, lambda e: e.tensor_tensor(out=hT[:, :, NP:NT], in0=tmpS, in1=modT[:, 0:8, 1:17], op=ALU.add),
                         reads=[b_tmpS, b_modT], wadd=[dbuf])

        uT = RA
        uT_b = [Buf(f"uT{g}") for g in range(8)]
        rhs_h = lambda k, t0, n: hT[:, k, t0:t0 + n]
        cnt = {"i": 0}

        def phase_b(full):
            for blk in range(2):
                def ev(oc, tbi, pap, pb, blk=blk):
                    t0, n = TBS[tbi]
                    g = 4 * blk + oc
                    evac_copy(0, uT[:, g, t0:t0 + n], pap, [pb], [], wadd=[uT_b[g]])
                proj_fm(w_in[:, OFF_U + 512 * blk:OFF_U + 512 * blk + 512], 512, rhs_h, hT_b, ev, tbs=TBS if full else TBS[:4])

        P.phase(3)
        phase_a(xprev[0], False)
        phase_b(False)
        P.phase(2)
        oc_ = O_C
        pw = cv_(oc_ + 0, [9, 2, 32], F32); bb = cv_(oc_ + 2304, [2, 32, 16], F32); ccm = cv_(oc_ + 6400, [2, 32, 16], F32)
        R8 = cv_(oc_ + 10496, [16, 2, 32], F32); R128 = cv_(oc_ + 14592, [16, 2, 32], F32); A2k = cv_(oc_ + 18688, [3, 2, 32], F32)
        Hend = cv_(oc_ + 19456, [2, 32, 16], F32); carry = cv_(oc_ + 23552, [17, 2, 32], F32)
        Sb = cv_(oc_ + 27904, [8, 2, 128], BF16); coef = cv_(oc_ + 32000, [12, 32], F32)
        misc = cv_(oc_ + 33536, [32, 32], F32)
        t1 = cv_(oc_ + 37632, [1024], F32); t2 = cv_(oc_ + 41728, [1024], F32)
        O_S = oc_ + 45824
        Sslot = [cv_(O_S + 8192 * i, [8, 2, 128], F32) for i in range(2)]
        O_CA = oc_ + 62208; O_KL = oc_ + 71424; O_HB = oc_ + 75520
        b_pw = Buf("pw"); b_bb = Buf("bb"); b_ccm = Buf("ccm"); b_R8 = Buf(); b_R128 = Buf(); b_A2k = Buf(); b_coef = Buf()
        b_misc = Buf("misc"); b_t = Buf("t12"); b_Sslot = [Buf("S0"), Buf("S1")]
        craw = [cv_(O_S + 2048 * i, [8, 64], F32) for i in range(2)]
        lamraw = cv_(O_S + 4096, [128], F32); ldraw = cv_(O_S + 4608, [64], F32)
        draw = cv_(O_S + 4864, [128], F32); bgraw = cv_(O_S + 5376, [128], F32)
        braw = [cv_(O_S + 8192 + 2048 * i, [32, 16], F32) for i in range(2)]
        b_craw = Buf(); b_lam = Buf(); b_ld = Buf(); b_draw = Buf(); b_braw = Buf()
        misc_load(SP, craw[0], c_re.rearrange("(t r) p -> r t p", r=128), b_craw, wadd=True)
        misc_load(SP, craw[1], c_im.rearrange("(t r) p -> r t p", r=128), b_craw, wadd=True)
        misc_load(SP, lamraw[0:64, 0:64], lam_re, b_lam, wadd=True)
        misc_load(SP, lamraw[0:64, 64:128], lam_im, b_lam, wadd=True)
        misc_load(SP, ldraw, log_delta.rearrange("(o n) -> o n", o=1).to_broadcast([128, 64]), b_ld)
        misc_load(SP, draw[0:8, :], ssm_d.rearrange("(c p) -> c p", p=128), b_draw, wadd=True)
        misc_load(SP, bgraw[0:8, :], b_glu.rearrange("(c p) -> c p", p=128), b_draw, wadd=True)
        for i, src in enumerate([b_re, b_im]):
            for q4 in range(4):
                misc_load(SP, braw[i][:, 8 * q4:8 * q4 + 8, :],
                          src[1024 * q4:1024 * q4 + 1024, :].rearrange("(gp q) c -> q gp c", q=128), b_braw, wadd=True)
        M_ = lambda i: misc[:, i, :]
        LR, LI, DT, TH, FR, FC, MAG, SN, CS, NR, DEN, CR, CI, G1, KF, TMPA = [M_(i) for i in range(16)]
        KI = misc[:, 16, :].bitcast(mybir.dt.int32)
        bkl = bank()
        P.op(PE, lambda e: e.transpose(out=ps[bkl][:, 0:64], in_=lamraw[0:64, :], identity=ident[0:64, 0:64]),
             reads=[b_lam, b_ident], writes=[psb[bkl]])
        P.op(DVE, lambda e: e.tensor_copy(out=LR[0:64, :], in_=ps[bkl][0:64, 0:64:2]), reads=[psb[bkl]], wadd=[b_misc])
        P.op(DVE, lambda e: e.tensor_copy(out=LR[64:128, :], in_=ps[bkl][0:64, 1:64:2]), reads=[psb[bkl]], wadd=[b_misc])
        P.op(DVE, lambda e: e.tensor_copy(out=LI[0:64, :], in_=ps[bkl][64:128, 0:64:2]), reads=[psb[bkl]], wadd=[b_misc])
        P.op(DVE, lambda e: e.tensor_copy(out=LI[64:128, :], in_=ps[bkl][64:128, 1:64:2]), reads=[psb[bkl]], wadd=[b_misc])
        bkd = bank()
        P.op(PE, lambda e: e.transpose(out=ps[bkd][:, 0:8], in_=draw[0:8, :], identity=ident[0:8, 0:8]),
             reads=[b_draw, b_ident], writes=[psb[bkd]])
        P.op(PE, lambda e: e.transpose(out=ps[bkd][:, 8:16], in_=bgraw[0:8, :], identity=ident[0:8, 0:8]),
             reads=[b_draw, b_ident], wadd=[psb[bkd]])
        P.op(DVE, lambda e: e.tensor_copy(out=Dm, in_=ps[bkd][:, 0:8]), reads=[psb[bkd]], writes=[b_Dm])
        P.op(DVE, lambda e: e.tensor_copy(out=bglu, in_=ps[bkd][:, 8:16]), reads=[psb[bkd]], writes=[b_bglu])
        for ri in range(2):
            for hb in range(2):
                bkc = bank()
                for tt in range(4):
                    t_ = 4 * hb + tt
                    P.op(PE, lambda e, ri=ri, t_=t_, tt=tt, bkc=bkc: e.transpose(
                        out=ps[bkc][0:64, 128 * tt:128 * tt + 128], in_=craw[ri][:, t_, :], identity=ident),
                        reads=[b_craw, b_ident], writes=[psb[bkc]] if tt == 0 else [], wadd=[psb[bkc]] if tt > 0 else [])
                for g2 in range(2):
                    src = ps[bkc][0:64, :].rearrange("p (tg g2 c) -> p tg g2 c", g2=2, c=16)[:, :, g2, :]
                    P.op(DVE, lambda e, ri=ri, hb=hb, g2=g2, src=src: e.tensor_copy(
                        out=ccm[64 * g2:64 * g2 + 64, ri, 16 * hb:16 * hb + 16, :], in_=src),
                        reads=[psb[bkc]], wadd=[b_ccm])
        P.op(ACT, lambda e: e.activation(out=DT[0:64, :], in_=ldraw[0:64, 0:64:2], func=AF.Exp), reads=[b_ld], wadd=[b_misc])
        P.op(ACT, lambda e: e.activation(out=DT[64:128, :], in_=ldraw[64:128, 1:64:2], func=AF.Exp), reads=[b_ld], wadd=[b_misc])
        G = DVE
        tt_ = lambda out, a, b_, op, **kw: P.op(G, lambda e: e.tensor_tensor(out=out, in0=a, in1=b_, op=op), reads=[b_misc], wadd=[b_misc], **kw)
        ts_ = lambda out, a, s1, s2, o0, o1: P.op(G, lambda e: e.tensor_scalar(out=out, in0=a, scalar1=s1, scalar2=s2, op0=o0, op1=o1), reads=[b_misc], wadd=[b_misc])
        tt_(TH, LI, DT, ALU.mult)
        ts_(FR, TH, 1.0 / (2 * math.pi), 0.0, ALU.mult, ALU.add)
        P.op(DVE, lambda e: e.tensor_copy(out=KI, in_=FR), reads=[b_misc], wadd=[b_misc])
        P.op(DVE, lambda e: e.tensor_copy(out=KF, in_=KI), reads=[b_misc], wadd=[b_misc])
        tt_(FR, FR, KF, ALU.subtract)
        ts_(FC, FR, 1.0, 0.25, ALU.mult, ALU.add)
        P.op(DVE, lambda e: e.tensor_single_scalar(out=G1, in_=FC, scalar=0.5, op=ALU.is_gt), reads=[b_misc], wadd=[b_misc])
        tt_(FC, FC, G1, ALU.subtract)
        TWO_PI = 6.283185
        P.op(ACT, lambda e: e.activation(out=SN, in_=FR, func=AF.Sin, scale=TWO_PI), reads=[b_misc], wadd=[b_misc])
        P.op(ACT, lambda e: e.activation(out=CS, in_=FC, func=AF.Sin, scale=TWO_PI), reads=[b_misc], wadd=[b_misc])
        tt_(TMPA, LR, DT, ALU.mult)
        P.op(ACT, lambda e: e.activation(out=MAG, in_=TMPA, func=AF.Exp), reads=[b_misc], wadd=[b_misc])
        P.op(G, lambda e: e.memset(pw[:, 0, 0, :], 1.0), wadd=[b_pw])
        P.op(G, lambda e: e.memset(pw[:, 0, 1, :], 0.0), wadd=[b_pw])
        P.op(G, lambda e: e.tensor_tensor(out=pw[:, 1, 0, :], in0=MAG, in1=CS, op=ALU.mult), reads=[b_misc], wadd=[b_pw])
        P.op(G, lambda e: e.tensor_tensor(out=pw[:, 1, 1, :], in0=MAG, in1=SN, op=ALU.mult), reads=[b_misc], wadd=[b_pw])

        def cm(dst, x, y, n, rb, wb):
            T1 = t1[:, 0:n * 32].rearrange("p (n g) -> p n g", n=n)
            T2 = t2[:, 0:n * 32].rearrange("p (n g) -> p n g", n=n)
            cmul(G, dst[:, :, 0, :], dst[:, :, 1, :], x[:, :, 0, :], x[:, :, 1, :], y[:, :, 0, :], y[:, :, 1, :], T1, T2, rb, wb, b_t)

        def bc(ap1, n):
            return ap1.to_broadcast([128, n, 2, 32])

        cm(pw[:, 2:3], pw[:, 1:2], pw[:, 1:2], 1, [b_pw], [b_pw])
        cm(pw[:, 3:5], pw[:, 1:3], bc(pw[:, 2:3], 2), 2, [b_pw], [b_pw])
        cm(pw[:, 5:9], pw[:, 1:5], bc(pw[:, 4:5], 4), 4, [b_pw], [b_pw])
        tt_(NR, pw[:, 1, 0, :], pw[:, 0, 0, :], ALU.subtract, deps=b_pw.w)
        tt_(DEN, LR, LR, ALU.mult)
        tt_(TMPA, LI, LI, ALU.mult)
        tt_(DEN, DEN, TMPA, ALU.add)
        P.op(DVE, lambda e: e.reciprocal(out=DEN, in_=DEN), reads=[b_misc], wadd=[b_misc])
        tt_(CR, NR, LR, ALU.mult)
        tt_(TMPA, pw[:, 1, 1, :], LI, ALU.mult)
        tt_(CR, CR, TMPA, ALU.add)
        tt_(CR, CR, DEN, ALU.mult)
        tt_(CI, pw[:, 1, 1, :], LR, ALU.mult)
        tt_(TMPA, NR, LI, ALU.mult)
        tt_(CI, CI, TMPA, ALU.subtract)
        tt_(CI, CI, DEN, ALU.mult)
        CRb = CR.unsqueeze(2).to_broadcast([128, 32, 16]); CIb = CI.unsqueeze(2).to_broadcast([128, 32, 16])
        T1b = t1[:, 0:512].rearrange("p (g c) -> p g c", g=32); T2b = t2[:, 0:512].rearrange("p (g c) -> p g c", g=32)
        P.op(G, lambda e: e.tensor_tensor(out=T1b, in0=braw[0], in1=CRb, op=ALU.mult), reads=[b_braw, b_misc, b_pw], writes=[b_t])
        P.op(G, lambda e: e.tensor_tensor(out=T2b, in0=braw[1], in1=CIb, op=ALU.mult), reads=[b_braw, b_misc], wadd=[b_t])
        P.op(G, lambda e: e.tensor_tensor(out=bb[:, 0], in0=T1b, in1=T2b, op=ALU.subtract), reads=[b_t], wadd=[b_bb])
        P.op(G, lambda e: e.tensor_tensor(out=T1b, in0=braw[1], in1=CRb, op=ALU.mult), reads=[b_braw, b_misc, b_bb], writes=[b_t])
        P.op(G, lambda e: e.tensor_tensor(out=T2b, in0=braw[0], in1=CIb, op=ALU.mult), reads=[b_braw, b_misc], wadd=[b_t])
        P.op(G, lambda e: e.tensor_tensor(out=bb[:, 1], in0=T1b, in1=T2b, op=ALU.add), reads=[b_t], wadd=[b_bb])

        def rev_table(R, A0, bufR, out_last):
            AW = misc[:, 20:22, :].rearrange("p (o r) g -> p o r g", o=1)
            AW2 = misc[:, 22:24, :].rearrange("p (o r) g -> p o r g", o=1)
            P.op(G, lambda e: e.memset(R[:, 15, 0, :], 1.0), wadd=[bufR])
            P.op(G, lambda e: e.memset(R[:, 15, 1, :], 0.0), wadd=[bufR])
            P.op(G, lambda e: e.tensor_copy(out=AW, in_=A0), reads=[b_pw, b_coef, b_misc], wadd=[b_misc])
            w = 1
            cur, nxt = AW, AW2
            while w <= 8:
                cm(R[:, 16 - 2 * w:16 - w], R[:, 16 - w:16], bc(cur, w), w, [bufR, b_misc], [bufR])
                cm(nxt, cur, cur, 1, [b_misc], [b_misc])
                cur, nxt = nxt, cur
                w *= 2
            P.op(G, lambda e: e.tensor_copy(out=out_last, in_=cur), reads=[b_misc], wadd=[b_coef])

        A128 = coef[:, 0:2, :].rearrange("p (o r) g -> p o r g", o=1)
        A2048 = coef[:, 2:4, :].rearrange("p (o r) g -> p o r g", o=1)
        rev_table(R8, pw[:, 8:9], b_R8, A128)
        rev_table(R128, A128, b_R128, A2048)
        P.op(G, lambda e: e.memset(A2k[:, 0, 0, :], 1.0), wadd=[b_A2k])
        P.op(G, lambda e: e.memset(A2k[:, 0, 1, :], 0.0), wadd=[b_A2k])
        P.op(G, lambda e: e.tensor_copy(out=A2k[:, 1:2], in_=A2048), reads=[b_coef], wadd=[b_A2k])
        cm(A2k[:, 2:3], A2048, A2048, 1, [b_coef], [b_A2k])
        P.op(G, lambda e: e.tensor_scalar(out=coef[:, 4, :], in0=pw[:, 8, 1, :], scalar1=-1.0, scalar2=0.0, op0=ALU.mult, op1=ALU.add),
             reads=[b_pw], wadd=[b_coef])
        P.op(G, lambda e: e.tensor_scalar(out=coef[:, 5, :], in0=coef[:, 1, :], scalar1=-1.0, scalar2=0.0, op0=ALU.mult, op1=ALU.add),
             reads=[b_coef], wadd=[b_coef])

        P.phase(5)
        WinL = cv_(O_B, [8, 8, 2, 128], BF16)
        b_WinL = [Buf(f"WinL{g}") for g in range(8)]
        b_Sb = Buf("Sb")
        handoff(b_Sslot, [b_craw, b_lam, b_ld, b_draw, b_braw])
        for i in range(2):
            P.op(POOL, lambda e, i=i: e.memset(Sslot[i].rearrange("p k r c -> p (k r c)"), 0.0), writes=[b_Sslot[i]])
        T1s = t1[:, 0:512].rearrange("p (k m c) -> p k m c", k=8, m=4)
        T2s = t2[:, 0:512].rearrange("p (k m c) -> p k m c", k=8, m=4)
        for gc in range(8):
            sl = gc % 2
            S = Sslot[sl]
            Sv = S.rearrange("p k r (m g c) -> p k r m g c", m=4, g=2)
            prk = pw[:, 0:8, 0, 4 * gc:4 * gc + 4].unsqueeze(3).to_broadcast([128, 8, 4, 16])
            pik = pw[:, 0:8, 1, 4 * gc:4 * gc + 4].unsqueeze(3).to_broadcast([128, 8, 4, 16])
            bbr = bb[:, 0, 4 * gc:4 * gc + 4, :].unsqueeze(1).to_broadcast([128, 8, 4, 16])
            bbi = bb[:, 1, 4 * gc:4 * gc + 4, :].unsqueeze(1).to_broadcast([128, 8, 4, 16])
            for ri in range(2):
                x1, x2 = (bbr, bbi) if ri == 0 else (bbi, bbr)
                op = ALU.subtract if ri == 0 else ALU.add
                P.op(POOL, lambda e, x1=x1, prk=prk: e.tensor_tensor(out=T1s, in0=prk, in1=x1, op=ALU.mult), reads=[b_pw, b_bb], writes=[b_t])
                P.op(POOL, lambda e, x2=x2, pik=pik: e.tensor_tensor(out=T2s, in0=pik, in1=x2, op=ALU.mult), reads=[b_pw, b_bb], wadd=[b_t])
                for g2 in range(2):
                    lo, hi = 64 * g2, 64 * g2 + 64
                    P.op(POOL, lambda e, ri=ri, g2=g2, lo=lo, hi=hi, op=op, Sv=Sv: e.tensor_tensor(
                        out=Sv[lo:hi, :, ri, :, g2, :], in0=T1s[lo:hi], in1=T2s[lo:hi], op=op),
                        reads=[b_t], wadd=[b_Sslot[sl]])
            P.op(POOL, lambda e, gc=gc, S=S: e.tensor_copy(out=Sb[:, gc], in_=S[:, 0]), reads=[b_Sslot[sl]], wadd=[b_Sb])
            for q4 in range(4):
                bkw = bank()
                for j in range(4):
                    k_, ri_ = (4 * q4 + j) // 2, (4 * q4 + j) % 2
                    P.op(PE, lambda e, bkw=bkw, j=j, k_=k_, ri_=ri_, S=S: e.transpose(
                        out=ps[bkw][:, 128 * j:128 * j + 128], in_=S[:, k_, ri_, :], identity=ident),
                        reads=[b_Sslot[sl], b_ident], writes=[psb[bkw]] if j == 0 else [], wadd=[psb[bkw]] if j > 0 else [])
                dstw = WinL[:, gc, 2 * q4:2 * q4 + 2].rearrange("p k r c -> p (k r c)")
                evac_copy(q4, dstw, ps[bkw][:, 0:512], [psb[bkw]], [], wadd=[b_WinL[gc]])

        t1p = cv_(O_S, [2, 16, 16], F32); t2p = cv_(O_S + 2048, [2, 16, 16], F32); cbp = cv_(O_S + 4096, [2, 16, 16], F32)
        b_p1 = Buf("p1tmp")
        handoff([b_p1], b_Sslot)
        b_Hend = Buf("Hend")

        def x_matmuls(gc, banks):
            for ri in range(2):
                for s_ in range(8):
                    for m in range(4):
                        first = (ri == 0 and s_ == 0)
                        last = (ri == 1 and s_ == 7)
                        bkx = banks[m]
                        P.op(PE, lambda e, m=m, ri=ri, s_=s_, bkx=bkx, gc=gc: e.matmul(
                            ps[bkx][:, 256 * ri:256 * ri + 256], lhsT=WinL[32 * m:32 * m + 32, gc, 7 - s_, ri, :],
                            rhs=uT[32 * m:32 * m + 32, gc, s_:NP:8], start=(s_ == 0), stop=(s_ == 7),
                            tile_position=(32 * m, 0)),
                            reads=[b_WinL[gc], uT_b[gc]] if first else [], writes=[psb[bkx]] if first else [], sig=last)
                        if last and not P.dead:
                            P.attach(Dep(P.esem[PE], P.cnt[PE]), reads=[b_WinL[gc], uT_b[gc]], writes=[psb[bkx]])

        def seg_reduce(src_ap, Rtab, gp0, ngp, out_ap, rbufs, wbuf):
            raise NotImplementedError

        def pass1():
            for gc in range(8):
                banks = [bank() for _ in range(4)]
                x_matmuls(gc, banks)
                if DEBUG and gc == 0 and os.environ.get('KX0'):
                    xdbg = cv_(O_S + 6144, [512], F32); b_xdbg = Buf()
                    P.op(DVE, lambda e: e.tensor_copy(out=xdbg, in_=ps[banks[1]][:, :]), reads=[psb[banks[1]]], writes=[b_xdbg])
                    P.dma(SP, dout_sem, dbg["X0"], xdbg, reads=[b_xdbg])
                for m in range(4):
                    gp = 4 * gc + m
                    X4 = ps[banks[m]][:, :].rearrange("p (r s i) -> p r s i", r=2, s=16)
                    Pr = R8[:, :, 0, gp].unsqueeze(1).unsqueeze(1).to_broadcast([128, 2, 16, 16])
                    Pi = R8[:, :, 1, gp].unsqueeze(1).unsqueeze(1).to_broadcast([128, 2, 16, 16])
                    P.op(DVE, lambda e, X4=X4, Pr=Pr: e.tensor_tensor(out=t1p, in0=X4, in1=Pr, op=ALU.mult),
                         reads=[psb[banks[m]], b_R8], writes=[b_p1])
                    P.op(DVE, lambda e, X4=X4, Pi=Pi: e.tensor_tensor(out=t2p, in0=X4, in1=Pi, op=ALU.mult),
                         reads=[psb[banks[m]], b_R8], wadd=[b_p1])
                    P.op(DVE, lambda e: e.tensor_tensor(out=cbp[:, 0], in0=t1p[:, 0], in1=t2p[:, 1], op=ALU.subtract), reads=[b_p1], wadd=[b_p1])
                    P.op(DVE, lambda e: e.tensor_tensor(out=cbp[:, 1], in0=t2p[:, 0], in1=t1p[:, 1], op=ALU.add), reads=[b_p1], wadd=[b_p1])
                    P.op(DVE, lambda e, gp=gp: e.tensor_reduce(out=Hend[:, :, gp, :], in_=cbp, axis=AX.X, op=ALU.add),
                         reads=[b_p1], wadd=[b_Hend])


        Ecore = cv_(O_S + 6144, [2, 32], F32); Eall = cv_(O_S + 6400, [8, 64], F32)
        te1 = cv_(O_S + 0, [2, 32, 16], F32); te2 = cv_(O_S + 8448, [2, 32, 16], F32)
        b_E = Buf("Ecore"); b_Eall = Buf("Eall"); b_te = b_p1
        handoff([b_E, b_Eall], b_Sslot)
        def ecore(j):
            Qr = R128[:, :, 0, :].rearrange("p i g -> p g i").unsqueeze(1).to_broadcast([128, 2, 32, 16])
            Qi = R128[:, :, 1, :].rearrange("p i g -> p g i").unsqueeze(1).to_broadcast([128, 2, 32, 16])
            P.op(DVE, lambda e: e.tensor_tensor(out=te1, in0=Hend, in1=Qr, op=ALU.mult), reads=[b_Hend, b_R128], writes=[b_te])
            P.op(DVE, lambda e: e.tensor_tensor(out=te2, in0=Hend, in1=Qi, op=ALU.mult), reads=[b_Hend, b_R128], wadd=[b_te])
            P.op(DVE, lambda e: e.tensor_tensor(out=te1[:, 0], in0=te1[:, 0], in1=te2[:, 1], op=ALU.subtract), reads=[b_te], writes=[b_te])
            P.op(DVE, lambda e: e.tensor_tensor(out=te2[:, 0], in0=te2[:, 0], in1=te1[:, 1], op=ALU.add), reads=[b_te], writes=[b_te])
            P.op(DVE, lambda e: e.tensor_reduce(out=Ecore[:, 0, :], in_=te1[:, 0], axis=AX.X, op=ALU.add), reads=[b_te], wadd=[b_E])
            P.op(DVE, lambda e: e.tensor_reduce(out=Ecore[:, 1, :], in_=te2[:, 0], axis=AX.X, op=ALU.add), reads=[b_te], wadd=[b_E])

            if j is not None:
                P.op(DVE, lambda e, j=j: e.tensor_copy(out=Eall[:, j, :], in_=Ecore.rearrange("p r g -> p (r g)")), reads=[b_E], wadd=[b_Eall])

        P.phase(3)
        srcs = [(xprev[1], False), (xprev[2], False), (xp, True)]
        phase_a(*srcs[0])
        for j in range(3):
            pass1()
            ecore(j)
            phase_b(srcs[j][1])
            if j < 2:
                phase_a(*srcs[j + 1])
        P.phase(6)
        pass1()
        P.phase(7)
        Sn = [cv_(O_S + 12544 + 256 * n, [2, 32], F32) for n in range(3)]
        b_Sn = Buf("Sn")
        handoff([b_Sn], b_Sslot)
        for n in range(3):
            Snf = Sn[n].rearrange("p r g -> p (r g)")
            P.op(DVE, lambda e, n=n, Snf=Snf: e.tensor_scalar(out=Snf, in0=Eall[:, n, :], scalar1=flags[:, 1 + n:2 + n], scalar2=None,
                                                            op0=ALU.mult), reads=[b_Eall, b_flags], wadd=[b_Sn])
        b_carry = Buf("carry")
        tq1 = cv_(O_S + 13312, [2, 32], F32); tq2 = cv_(O_S + 13568, [2, 32], F32)
        b_tq = Buf("tq")
        handoff([b_tq], b_Sslot)

        def cmul_small(dst, x, y, rb, wb):
            cmul(DVE, dst[:, 0, :], dst[:, 1, :], x[:, 0, :], x[:, 1, :], y[:, 0, :], y[:, 1, :], tq1[:, 0, :], tq1[:, 1, :], rb, wb, b_tq)

        cmul_small(carry[:, 1], Sn[1], A2k[:, 1], [b_Sn, b_A2k], [b_carry])
        cmul_small(carry[:, 2], Sn[2], A2k[:, 2], [b_Sn, b_A2k, b_carry], [b_carry])
        P.op(DVE, lambda e: e.tensor_tensor(out=Sn[0], in0=Sn[0], in1=carry[:, 1], op=ALU.add), reads=[b_Sn, b_carry], writes=[b_Sn])
        P.op(DVE, lambda e: e.tensor_tensor(out=carry[:, 0], in0=Sn[0], in1=carry[:, 2], op=ALU.add), reads=[b_Sn, b_carry], writes=[b_carry])
        A128v = coef[:, 0:2, :]
        for sg in range(16):
            cmul_small(carry[:, sg + 1], carry[:, sg], A128v, [b_carry, b_coef], [b_carry])
            P.op(DVE, lambda e, sg=sg: e.tensor_tensor(out=carry[:, sg + 1], in0=carry[:, sg + 1], in1=Hend[:, :, :, sg], op=ALU.add),
                 reads=[b_carry, b_Hend], writes=[b_carry])
        pstT = cv_(O_S + 13824, [2, 128], F32)
        b_pst = Buf()
        handoff([b_pst], b_Sslot)
        bkp = bank()
        for ri in range(2):
            P.op(PE, lambda e, ri=ri: e.transpose(out=ps[bkp][0:32, 128 * ri:128 * ri + 128], in_=carry[:, 16, ri, :], identity=ident),
                 reads=[b_carry, b_ident], writes=[psb[bkp]] if ri == 0 else [], wadd=[psb[bkp]] if ri == 1 else [])
        P.op(DVE, lambda e: e.tensor_copy(out=pstT[0:32].rearrange("p r c -> p (r c)"), in_=ps[bkp][0:32, 0:256]), reads=[psb[bkp]], writes=[b_pst])
        P.dma(SP, dout_sem, pst_re, pstT[0:32, 0, :], reads=[b_pst])
        P.dma(SP, dout_sem, pst_im, pstT[0:32, 1, :], reads=[b_pst])

        P.phase(8)
        CaBD = [cv_(O_CA + 4608 * i, [4, 9, 2, 32], BF16) for i in range(2)]
        KLs = [cv_(O_KL + 2048 * i, [8, 128], BF16) for i in range(2)]
        Hb = [cv_(O_HB + 4096 * i, [2, 4, 256], BF16) for i in range(2)]
        Xs = [cv_(O_S + 8192 * i, [2, 4, 256], F32) for i in range(2)]
        b_CaBD = [Buf("Ca0"), Buf("Ca1")]; b_KL = [Buf("KL0"), Buf("KL1")]; b_Hb = [Buf("Hb0"), Buf("Hb1")]; b_Xs = [Buf("Xs0"), Buf("Xs1")]
        old_c = [b_ct, b_csg, b_ccT, b_bin, b_baT, xst_b[0], xst_b[1]]
        handoff(b_CaBD + b_KL + b_Hb, old_c)
        handoff(b_Xs, [b_p1, b_E, b_Eall, b_te, b_Sn, b_tq, b_pst] + b_Sslot)
        KL0all = cv_(O_C + 10496, [8, 128], BF16); Ca1all = cv_(O_C + 14592, [32, 2, 32], BF16)
        b_KL0 = Buf("KL0all"); b_Ca1 = Buf("Ca1all")
        handoff([b_KL0], [b_R8]); handoff([b_Ca1], [b_R128])
        Q1e = [misc[:, 24:28, :].rearrange("p a g -> p (a g)").rearrange("p (r m s) -> p r m s", r=2, m=4),
               misc[:, 0:4, :].rearrange("p a g -> p (a g)").rearrange("p (r m s) -> p r m s", r=2, m=4)]
        Q2e = [misc[:, 28:32, :].rearrange("p a g -> p (a g)").rearrange("p (r m s) -> p r m s", r=2, m=4),
               misc[:, 4:8, :].rearrange("p a g -> p (a g)").rearrange("p (r m s) -> p r m s", r=2, m=4)]
        tmpK = misc[:, 16:20, :].rearrange("p a g -> p (a g)")
        b_Q = [Buf("Qdve"), Buf("Qpool")]
        b_tmpK = Buf("tmpK")
        handoff([b_tmpK] + b_Q, [b_misc])
        for i in range(2):
            P.op(POOL, lambda e, i=i: e.memset(CaBD[i].rearrange("p m n r c -> p (m n r c)"), 0.0), writes=[b_CaBD[i]])
        U1 = t1[:, 0:576].rearrange("p (m n c) -> p m n c", m=4, n=9)
        U2 = t2[:, 0:576].rearrange("p (m n c) -> p m n c", m=4, n=9)
        XB = [2, 3, 4, 5]
        YB = [6, 7]

        def emit_consts(gc):
            sl = gc % 2
            Ca = CaBD[sl]
            cre = ccm[:, 0, 4 * gc:4 * gc + 4, :].unsqueeze(2).to_broadcast([128, 4, 9, 16])
            cim = ccm[:, 1, 4 * gc:4 * gc + 4, :].unsqueeze(2).to_broadcast([128, 4, 9, 16])
            pr = pw[:, :, 0, 4 * gc:4 * gc + 4].rearrange("p n m -> p m n").unsqueeze(3).to_broadcast([128, 4, 9, 16])
            pi = pw[:, :, 1, 4 * gc:4 * gc + 4].rearrange("p n m -> p m n").unsqueeze(3).to_broadcast([128, 4, 9, 16])
            P.op(DVE, lambda e: e.tensor_tensor(out=U1, in0=cre, in1=pr, op=ALU.mult), reads=[b_ccm, b_pw], writes=[b_t])
            P.op(DVE, lambda e: e.tensor_tensor(out=U2, in0=cim, in1=pi, op=ALU.mult), reads=[b_ccm, b_pw], wadd=[b_t])
            for g2 in range(2):
                lo, hi = 64 * g2, 64 * g2 + 64
                P.op(DVE, lambda e, lo=lo, hi=hi, g2=g2: e.tensor_tensor(
                    out=Ca[lo:hi, :, :, 0, 16 * g2:16 * g2 + 16], in0=U1[lo:hi], in1=U2[lo:hi], op=ALU.subtract),
                    reads=[b_t], wadd=[b_CaBD[sl]])
            P.op(DVE, lambda e: e.tensor_tensor(out=U1, in0=cre, in1=pi, op=ALU.mult), reads=[b_ccm, b_pw], writes=[b_t])
            P.op(DVE, lambda e: e.tensor_tensor(out=U2, in0=cim, in1=pr, op=ALU.mult), reads=[b_ccm, b_pw], wadd=[b_t])
            P.op(DVE, lambda e: e.tensor_tensor(out=U1, in0=U1, in1=U2, op=ALU.add), reads=[b_t], writes=[b_t])
            for g2 in range(2):
                lo, hi = 64 * g2, 64 * g2 + 64
                P.op(DVE, lambda e, lo=lo, hi=hi, g2=g2: e.tensor_scalar(
                    out=Ca[lo:hi, :, :, 1, 16 * g2:16 * g2 + 16], in0=U1[lo:hi], scalar1=-1.0, scalar2=0.0, op0=ALU.mult, op1=ALU.add),
                    reads=[b_t], wadd=[b_CaBD[sl]])
            P.op(DVE, lambda e: e.tensor_copy(out=Ca1all[:, 4 * gc:4 * gc + 4], in_=Ca[:, :, 1, :, :]), reads=[b_CaBD[sl]], wadd=[b_Ca1])
            for hb in range(2):
                for tt in range(4):
                    tau = 4 * hb + tt
                    for ri in range(2):
                        P.op(PE, lambda e, hb=hb, tt=tt, tau=tau, ri=ri: e.matmul(
                            ps[hb][:, 128 * tt:128 * tt + 128], lhsT=Sb[:, gc, ri, :], rhs=Ca[:, :, tau, ri, :],
                            start=(ri == 0), stop=(ri == 1)),
                            reads=[b_Sb, b_CaBD[sl]] if (tt == 0 and ri == 0) else [],
                            writes=[psb[hb]] if (tt == 0 and ri == 0) else [], sig=(tt == 3 and ri == 1))
                if not P.dead:
                    P.attach(Dep(P.esem[PE], P.cnt[PE]), reads=[b_Sb, b_CaBD[sl]], writes=[psb[hb]])
            KL = KLs[sl]
            bmb3 = bmask.unsqueeze(1).to_broadcast([128, 3, 128]); bmb4 = bmask.unsqueeze(1).to_broadcast([128, 4, 128])
            P.op(DVE, lambda e: e.tensor_tensor(out=KL[:, 1:4, :], in0=ps[0][:, 128:512].rearrange("p (t c) -> p t c", t=3), in1=bmb3, op=ALU.mult),
                 reads=[psb[0], b_bmask], wadd=[b_KL[sl]])
            P.op(DVE, lambda e: e.tensor_tensor(out=tmpK, in0=ps[0][:, 0:128], in1=bmask, op=ALU.mult), reads=[psb[0], b_bmask], writes=[b_tmpK])
            P.op(DVE, lambda e: e.tensor_tensor(out=KL[:, 4:8, :], in0=ps[1][:, 0:512].rearrange("p (t c) -> p t c", t=4), in1=bmb4, op=ALU.mult),
                 reads=[psb[1], b_bmask], wadd=[b_KL[sl]])
            P.op(DVE, lambda e: e.scalar_tensor_tensor(out=KL[:, 0, :], in0=ident, scalar=Dm[:, gc:gc + 1], in1=tmpK, op0=ALU.mult, op1=ALU.add),
                 reads=[b_tmpK, b_ident, b_Dm], wadd=[b_KL[sl]])
            P.op(DVE, lambda e: e.tensor_copy(out=KL0all[:, gc, :], in_=KL[:, 0, :]), reads=[b_KL[sl]], wadd=[b_KL0])

        def emit_x_scan(gc):
            sl = gc % 2
            x_matmuls(gc, XB)
            X = Xs[sl]
            for m in range(4):
                P.op(ACT, lambda e, m=m: e.activation(out=X[:, :, m, :], in_=ps[XB[m]][:, :].rearrange("p (r j) -> p r j", r=2), func=AF.Copy),
                     reads=[psb[XB[m]]], wadd=[b_Xs[sl]])
            E, qi = (POOL, 1) if gc in (1, 4, 6) else (DVE, 0)
            Q1 = Q1e[qi]; Q2 = Q2e[qi]
            X5 = X.rearrange("p r m (s i) -> p r m s i", i=16)
            Ar = pw[:, 8, 0, 4 * gc:4 * gc + 4].unsqueeze(1).unsqueeze(3).to_broadcast([128, 2, 4, 16])
            Ai = pw[:, 8, 1, 4 * gc:4 * gc + 4].unsqueeze(2).to_broadcast([128, 4, 16])
            AiN = coef[:, 4, 4 * gc:4 * gc + 4].unsqueeze(2).to_broadcast([128, 4, 16])
            cview = carry[:, 0:16, :, 4 * gc:4 * gc + 4].rearrange("p s r m -> p r m s")
            for i in range(16):
                prev = cview if i == 0 else X5[:, :, :, :, i - 1]
                cur = X5[:, :, :, :, i]
                rb = [b_carry, b_pw, b_coef, b_Xs[sl]]
                P.op(E, lambda e, prev=prev: e.tensor_tensor(out=Q1, in0=prev, in1=Ar, op=ALU.mult), reads=rb, writes=[b_Q[qi]])
                P.op(E, lambda e, prev=prev: e.tensor_tensor(out=Q2[:, 0], in0=prev[:, 1], in1=AiN, op=ALU.mult), reads=rb, wadd=[b_Q[qi]])
                P.op(E, lambda e, prev=prev: e.tensor_tensor(out=Q2[:, 1], in0=prev[:, 0], in1=Ai, op=ALU.mult), reads=rb, wadd=[b_Q[qi]])
                P.op(E, lambda e, cur=cur: e.tensor_tensor(out=cur, in0=cur, in1=Q1, op=ALU.add), reads=[b_Q[qi]], writes=[b_Xs[sl]])
                P.op(E, lambda e, cur=cur: e.tensor_tensor(out=cur, in0=cur, in1=Q2, op=ALU.add), reads=[b_Q[qi]], writes=[b_Xs[sl]])
            H5 = Hb[sl].rearrange("p r m (s i) -> p r m s i", i=16)
            P.op(ACT, lambda e: e.activation(out=H5[:, :, :, :, 1:16].rearrange("p r m s i -> p (r m) s i"),
                                             in_=X5[:, :, :, :, 0:15].rearrange("p r m s i -> p (r m) s i"), func=AF.Copy),
                 reads=[b_Xs[sl]], writes=[b_Hb[sl]])
            P.op(ACT, lambda e: e.activation(out=H5[:, :, :, :, 0], in_=cview, func=AF.Copy), reads=[b_carry], wadd=[b_Hb[sl]])

        def emit_y(gc):
            sl = gc % 2
            KL = KLs[sl]; Ca = CaBD[sl]
            uview = uT[:, gc, 0:NP].rearrange("p (j s) -> p s j", s=8)
            for half in (1, 0):
                for tl in range(4):
                    t_lo = 4 * half + tl
                    bk_ = YB[tl // 2]
                    reg = ps[bk_][:, 256 * (tl % 2):256 * (tl % 2) + 256]
                    n_mm = (t_lo + 1) + 8
                    idx = 0
                    for s_ in range(t_lo + 1):
                        P.op(PE, lambda e, reg=reg, s_=s_, t_lo=t_lo: e.matmul(
                            reg, lhsT=KL[:, t_lo - s_, :], rhs=uview[:, s_, :], start=(s_ == 0), stop=False),
                            reads=[b_KL[sl], uT_b[gc], b_CaBD[sl], b_Hb[sl]] if idx == 0 else [],
                            writes=[psb[bk_]] if (idx == 0 and tl % 2 == 0) else [], sig=False)
                        idx += 1
                    for m in range(4):
                        for ri in range(2):
                            lastmm = (m == 3 and ri == 1)
                            P.op(PE, lambda e, reg=reg, m=m, ri=ri, t_lo=t_lo, lastmm=lastmm: e.matmul(
                                reg[32 * m:32 * m + 32, :], lhsT=Ca[:, m, t_lo + 1, ri, :], rhs=Hb[sl][:, ri, m, :],
                                start=False, stop=lastmm, tile_position=(0, 32 * m)), sig=lastmm)
                    if not P.dead:
                        P.attach(Dep(P.esem[PE], P.cnt[PE]), reads=[b_KL[sl], uT_b[gc], b_CaBD[sl], b_Hb[sl]],
                                 writes=[psb[bk_]] if tl % 2 == 1 else [], wadd=[psb[bk_]] if tl % 2 == 0 else [])
                for bi in range(2):
                    t0_ = 4 * half + 2 * bi
                    P.op(ACT, lambda e, bi=bi, t0_=t0_: e.activation(
                        out=uview[:, t0_:t0_ + 2, :], in_=ps[YB[bi]][:, :].rearrange("p (t j) -> p t j", t=2), func=AF.Gelu_apprx_tanh),
                        reads=[psb[YB[bi]]], writes=[uT_b[gc]])

        for step in range(9):
            if step < 8:
                emit_consts(step)
                emit_x_scan(step)
            if step >= 1:
                emit_y(step - 1)
        yT = uT
        yT_b = uT_b

        P.phase(9)
        all_ssm_tmp = b_Xs + b_Hb + b_Q + [b_tmpK, b_t, b_p1, b_E, b_Eall, b_te, b_Sn, b_tq, b_pst] + b_Sslot
        stile = [cv_(O_S + 2048 * i, [512], F32) for i in range(2)]
        Hsp = cv_(O_S + 4096, [2, 32, 16], F32); Hn = cv_(O_S + 8192, [2, 32, 16], F32)
        HbS = cv_(O_S + 12288, [2, 32, 16], BF16); Q1s = cv_(O_HB, [2, 32, 16], F32); Q2s = cv_(O_HB + 4096, [2, 32, 16], F32)
        b_stile = Buf(); b_Hsp = Buf(); b_Hn = Buf(); b_HbS = Buf(); b_Qs = Buf()
        handoff([b_stile, b_Hsp, b_Hn, b_HbS, b_Qs], all_ssm_tmp)
        misc_load(SP, stile[0], st_re.rearrange("s (gh f) -> (s gh) f", gh=8), b_stile, wadd=True)
        misc_load(SP, stile[1], st_im.rearrange("s (gh f) -> (s gh) f", gh=8), b_stile, wadd=True)
        for ri in range(2):
            bks = bank()
            for q4 in range(4):
                P.op(PE, lambda e, ri=ri, q4=q4, bks=bks: e.transpose(out=ps[bks][:, 128 * q4:128 * q4 + 128],
                                                                   in_=stile[ri][:, 128 * q4:128 * q4 + 128], identity=ident),
                     reads=[b_stile, b_ident], writes=[psb[bks]] if q4 == 0 else [], wadd=[psb[bks]] if q4 > 0 else [])
            for q4 in range(4):
                P.op(DVE, lambda e, ri=ri, q4=q4, bks=bks: e.tensor_copy(
                    out=Hsp[:, ri, q4:32:4, :], in_=ps[bks][:, 128 * q4:128 * q4 + 128].rearrange("p (s gh) -> p gh s", gh=8)),
                    reads=[psb[bks]], wadd=[b_Hsp])
        P.op(POOL, lambda e: e.tensor_copy(out=HbS, in_=Hsp), reads=[b_Hsp], writes=[b_HbS])
        xsb = [bank() for _ in range(4)]
        for m in range(4):
            for gc in range(8):
                for ri in range(2):
                    first = (gc == 0 and ri == 0); last = (gc == 7 and ri == 1)
                    P.op(PE, lambda e, m=m, gc=gc, ri=ri: e.matmul(
                        ps[xsb[m]][:, 32 * gc + 16 * ri:32 * gc + 16 * ri + 16], lhsT=WinL[32 * m:32 * m + 32, gc, 0, ri, :],
                        rhs=uT[32 * m:32 * m + 32, gc, NP:NT], start=True, stop=True, tile_position=(32 * m, 0)),
                        reads=b_WinL + uT_b if first else [], writes=[psb[xsb[m]]] if first else [], sig=last)
            if not P.dead:
                P.attach(Dep(P.esem[PE], P.cnt[PE]), reads=b_WinL + uT_b, writes=[psb[xsb[m]]])
        Ar1 = pw[:, 1, 0, :].unsqueeze(1).unsqueeze(3).to_broadcast([128, 2, 32, 16])
        Ai1 = pw[:, 1, 1, :].unsqueeze(2).to_broadcast([128, 32, 16])
        P.op(POOL, lambda e: e.tensor_scalar(out=coef[:, 6, :], in0=pw[:, 1, 1, :], scalar1=-1.0, scalar2=0.0, op0=ALU.mult, op1=ALU.add),
             reads=[b_pw], wadd=[b_coef])
        AiN1 = coef[:, 6, :].unsqueeze(2).to_broadcast([128, 32, 16])
        P.op(DVE, lambda e: e.tensor_tensor(out=Q1s, in0=Hsp, in1=Ar1, op=ALU.mult), reads=[b_Hsp, b_pw], writes=[b_Qs])
        P.op(DVE, lambda e: e.tensor_tensor(out=Q2s[:, 0], in0=Hsp[:, 1], in1=AiN1, op=ALU.mult), reads=[b_Hsp, b_coef], wadd=[b_Qs])
        P.op(DVE, lambda e: e.tensor_tensor(out=Q2s[:, 1], in0=Hsp[:, 0], in1=Ai1, op=ALU.mult), reads=[b_Hsp, b_pw], wadd=[b_Qs])
        P.op(DVE, lambda e: e.tensor_tensor(out=Hn, in0=Q1s, in1=Q2s, op=ALU.add), reads=[b_Qs], writes=[b_Hn])
        for m in range(4):
            P.op(DVE, lambda e, m=m: e.tensor_tensor(
                out=Hn[:, :, m:32:4, :], in0=ps[xsb[m]][:, 0:256].rearrange("p (gc r s) -> p r gc s", gc=8, r=2),
                in1=Hn[:, :, m:32:4, :], op=ALU.add), reads=[psb[xsb[m]], b_Hn], writes=[b_Hn])
        stg = cv_(O_HB + 8192 - 8192, [4, 128], F32)
        sout = [cv_(O_S + 2048 * i, [512], F32) for i in range(2)]
        b_stg = Buf(); b_sout = Buf()
        handoff([b_stg], [b_Qs]); handoff([b_sout], [b_stile])
        for ri in range(2):
            for q4 in range(4):
                P.op(POOL, lambda e, ri=ri, q4=q4: e.tensor_copy(out=stg[:, q4, :].rearrange("p (s gh) -> p gh s", gh=8),
                                                               in_=Hn[:, ri, q4:32:4, :]), reads=[b_Hn], writes=[b_stg] if q4 == 0 else [],
                     wadd=[b_stg] if q4 > 0 else [])
            bks = bank()
            for q4 in range(4):
                P.op(PE, lambda e, q4=q4, bks=bks: e.transpose(out=ps[bks][:, 128 * q4:128 * q4 + 128], in_=stg[:, q4, :], identity=ident),
                     reads=[b_stg, b_ident], writes=[psb[bks]] if q4 == 0 else [], wadd=[psb[bks]] if q4 > 0 else [])
            P.op(DVE, lambda e, ri=ri, bks=bks: e.tensor_copy(out=sout[ri], in_=ps[bks][:, 0:512]), reads=[psb[bks]], wadd=[b_sout])
            P.dma(SP, dout_sem, (sst_re if ri == 0 else sst_im).rearrange("s (gh f) -> (s gh) f", gh=8), sout[ri], reads=[b_sout])
        bky = bank()
        for gc in range(8):
            reg = ps[bky][:, 16 * gc:16 * gc + 16]
            P.op(PE, lambda e, gc=gc, reg=reg: e.matmul(reg, lhsT=KL0all[:, gc, :], rhs=uT[:, gc, NP:NT], start=True, stop=False),
                 reads=[b_KL0, b_Ca1, b_HbS] + uT_b if gc == 0 else [], writes=[psb[bky]] if gc == 0 else [], sig=False)
            for m in range(4):
                for ri in range(2):
                    lastmm = (m == 3 and ri == 1)
                    P.op(PE, lambda e, gc=gc, reg=reg, m=m, ri=ri, lastmm=lastmm: e.matmul(
                        reg[32 * m:32 * m + 32, :], lhsT=Ca1all[:, 4 * gc + m, ri, :], rhs=HbS[:, ri, 4 * gc + m, :],
                        start=False, stop=lastmm, tile_position=(0, 32 * m)), sig=(lastmm and gc == 7))
        if not P.dead:
            P.attach(Dep(P.esem[PE], P.cnt[PE]), reads=[b_KL0, b_Ca1, b_HbS] + uT_b, writes=[psb[bky]])
        P.op(ACT, lambda e: e.activation(out=uT[:, :, NP:NT], in_=ps[bky][:, 0:128].rearrange("p (g s) -> p g s", g=8),
                                         func=AF.Gelu_apprx_tanh), reads=[psb[bky]], writes=uT_b)

        P.phase(10)
        s2T = RB
        s2_b = [Buf(f"s2_{g}") for g in range(8)]
        handoff(s2_b, b_WinL)
        gtmp = [cv_(O_C + 1024 * i, [512], BF16) for i in range(4)]
        ftmp = [cv_(O_C + 4096 + 2048 * i, [512], F32) for i in range(2)]
        b_gtmp = [Buf() for _ in range(4)]; b_ftmp = [Buf(), Buf()]
        handoff(b_gtmp + b_ftmp, [b_pw, b_bb, b_ccm])
        rhs_y = lambda k, t0, n: yT[:, k, t0:t0 + n]
        yall_b = [yT_b] * 5
        ctr = {"i": 0}

        class AllOf:
            pass
        for blk in range(2):
            def ev_glu(oc, tbi, pap, pb, blk=blk):
                t0, n = TBS[tbi]
                g = 4 * blk + oc
                ctr["i"] += 1
                gi = ctr["i"] % 4
                P.op(ACT, lambda e: e.activation(out=gtmp[gi][:, 0:n], in_=pap, func=AF.Sigmoid, bias=bglu[:, g:g + 1], scale=1.0),
                     reads=[pb, b_bglu], writes=[b_gtmp[gi]])
                P.op(DVE, lambda e: e.tensor_tensor(out=s2T[:, g, t0:t0 + n], in0=yT[:, g, t0:t0 + n], in1=gtmp[gi][:, 0:n], op=ALU.mult),
                     reads=[b_gtmp[gi], yT_b[g]], wadd=[s2_b[g]])
            proj_fm(w_glu[:, 512 * blk:512 * blk + 512], 512, rhs_y, [BufGroup(yT_b)] * 5, ev_glu)

            def ev_zs(oc, tbi, pap, pb, blk=blk):
                t0, n = TBS[tbi]
                g = 4 * blk + oc
                ctr["i"] += 1
                gi = ctr["i"] % 4
                fi = ctr["i"] % 2
                P.op(ACT, lambda e: e.activation(out=gtmp[gi][:, 0:n], in_=pap, func=AF.Sigmoid), reads=[pb], writes=[b_gtmp[gi]])
                P.op(DVE, lambda e: e.tensor_tensor(out=ftmp[fi][:, 0:n], in0=pap, in1=gtmp[gi][:, 0:n], op=ALU.mult),
                     reads=[pb, b_gtmp[gi]], writes=[b_ftmp[fi]])
                P.op(POOL, lambda e: e.tensor_tensor(out=s2T[:, g, t0:t0 + n], in0=s2T[:, g, t0:t0 + n], in1=ftmp[fi][:, 0:n], op=ALU.mult),
                     reads=[b_ftmp[fi], s2_b[g]], wadd=[s2_b[g]])
            proj_fm(w_in[:, OFF_ZS + 512 * blk:OFF_ZS + 512 * blk + 512], 512, rhs_h, hT_b, ev_zs)

        P.phase(11)
        gbs = RA
        gbs_b = [Buf(f"gbs{g}") for g in range(8)]
        handoff(gbs_b, yT_b)
        rhs_s2 = lambda k, t0, n: s2T[:, k, t0:t0 + n]
        for blk in range(2):
            def ev_gs(oc, tbi, pap, pb, blk=blk):
                t0, n = TBS[tbi]
                g = 4 * blk + oc
                P.op(ACT, lambda e: e.activation(out=gbs[:, g, t0:t0 + n], in_=pap, func=AF.Sigmoid), reads=[pb], wadd=[gbs_b[g]])
            proj_fm(w_in[:, OFF_GS + 512 * blk:OFF_GS + 512 * blk + 512], 512, rhs_h, hT_b, ev_gs)

            def ev_bs(oc, tbi, pap, pb, blk=blk):
                t0, n = TBS[tbi]
                g = 4 * blk + oc
                P.op(DVE, lambda e: e.tensor_tensor(out=gbs[:, g, t0:t0 + n], in0=pap, in1=gbs[:, g, t0:t0 + n], op=ALU.mult),
                     reads=[pb, gbs_b[g]], wadd=[gbs_b[g]])
            proj_fm(w_bs[:, 512 * blk:512 * blk + 512], 512, rhs_s2, [BufGroup(s2_b)] * 5, ev_bs)

        P.phase(12)
        oT = RB
        oT_b = [Buf(f"oT{g}") for g in range(8)]
        handoff(oT_b, s2_b)
        oc_ = O_C
        qT = cv_(oc_ + 0, [2, NT], BF16); kT2 = cv_(oc_ + 8256, [128 + NT], BF16); Vaug = cv_(oc_ + 12640, [18, 128], BF16)
        EB = cv_(oc_ + 17248, [2, 16, 128], BF16); EB0 = cv_(oc_ + 25440, [16, 128], BF16)
        Et = [cv_(oc_ + 29536 + 1024 * i, [512], BF16) for i in range(4)]
        PT = [cv_(oc_ + 33632 + 1024 * i, [512], BF16) for i in range(4)]
        rc = [cv_(oc_ + 37728 + 2048 * i, [512], F32) for i in range(2)]
        maskt = cv_(oc_ + 41824, [2, 128], F32); RT = cv_(oc_ + 42848, [384], F32); relb = cv_(oc_ + 44384, [16], F32)
        es16 = cv_(oc_ + 44448, [16], F32); klast = cv_(oc_ + 44512, [256], F32); vlast = cv_(oc_ + 45536, [256], F32)
        knew = cv_(oc_ + 46560, [256], F32); vnew = cv_(oc_ + 47584, [256], F32)
        Kc = cv_(oc_ + 48608, [16, 256], F32)
        KcT = cv_(oc_ + 64992, [16, 2, 128], BF16)
        Vcs = cv_(oc_ + 73184, [16, 256], BF16)
        QsT = cv_(oc_ + 81376, [2, 4, 16], BF16)
        dgt = cv_(oc_ + 81632, [64], F32); vnb = cv_(oc_ + 81888, [256], BF16); pdg = cv_(oc_ + 82400, [64], BF16)
        esr = cv_(oc_ + 82528, [64], BF16); ebs = cv_(oc_ + 82656, [16], F32); rcs = cv_(oc_ + 82720, [128], F32)
        ptS = cv_(oc_ + 83232, [128], BF16); ones_k = cv_(oc_ + 83488, [128], BF16)
        attn_bufs = {n: Buf(n) for n in ["qT", "kT2", "Vaug", "EB", "EB0", "mask", "RT", "relb", "es16", "klast", "vlast", "knew", "vnew",
                                         "Kc", "KcT", "Vcs", "QsT", "dgt", "vnb", "pdg", "esr", "ebs", "rcs", "ptS", "ones_k"]}
        A = attn_bufs
        b_Et = [Buf() for _ in range(4)]; b_PT = [Buf() for _ in range(4)]; b_rc = [Buf(), Buf()]
        prev_c = [b_pw, b_bb, b_ccm, b_R8, b_R128, b_A2k, b_Hend, b_carry, b_Sb, b_coef, b_misc, b_t, b_KL0, b_Ca1,
                  b_stile, b_Hsp, b_Hn, b_HbS, b_Qs, b_stg, b_sout] + b_gtmp + b_ftmp + all_ssm_tmp + b_CaBD + b_KL
        handoff(list(A.values()) + b_Et + b_PT + b_rc, prev_c)
        misc_load(SP, RT[0:32, :], rtab, A["RT"]); misc_load(SP, relb[0:32, :], rel_bias, A["relb"])
        misc_load(SP, maskt.rearrange("p h q -> p (h q)"), maskc, A["mask"])
        misc_load(SP, es16[0:1, :], sinks.rearrange("(o n) -> o n", o=1), A["es16"])
        misc_load(SP, ebs[0:16, :], rel_bias[0:1, :].to_broadcast([16, 16]), A["ebs"])
        misc_load(SP, dgt[0:16, :], diagc, A["dgt"])
        P.op(ACT, lambda e: e.activation(out=es16[0:1, :], in_=es16[0:1, :], func=AF.Exp), reads=[A["es16"]], writes=[A["es16"]])
        P.op(ACT, lambda e: e.activation(out=ebs[0:16, :], in_=ebs[0:16, :], func=AF.Exp), reads=[A["ebs"]], writes=[A["ebs"]])
        for kv in range(4):
            for sl_, i in enumerate([0, 2, 1, 3]):
                h = 4 * kv + i
                P.op(DVE, lambda e, kv=kv, sl_=sl_, h=h: e.tensor_copy(out=ES[0:1, kv, sl_, :], in_=es16[0:1, h:h + 1].to_broadcast([1, 128])),
                     reads=[A["es16"]], wadd=[b_ES])
        P.op(POOL, lambda e: e.memset(ones_k, 1.0), writes=[A["ones_k"]])
        for half in range(2):
            for qb in range(4):
                bke = bank()
                for qq in range(32):
                    q = 32 * qb + qq
                    st_ = (127 - q) if half == 0 else (255 - q)
                    P.op(PE, lambda e, bke=bke, qq=qq, st_=st_: e.matmul(ps[bke][:, 16 * qq:16 * qq + 16], lhsT=RT[0:32, st_:st_ + 128],
                                                                       rhs=relb[0:32, :], start=True, stop=True),
                         reads=[A["RT"], A["relb"]] if qq == 0 else [], writes=[psb[bke]] if qq == 0 else [], sig=(qq == 31))
                if not P.dead:
                    P.attach(Dep(P.esem[PE], P.cnt[PE]), reads=[A["RT"], A["relb"]], writes=[psb[bke]])
                P.op(ACT, lambda e, bke=bke, half=half, qb=qb: e.activation(
                    out=EB[:, half, :, 32 * qb:32 * qb + 32], in_=ps[bke][:, 0:512].rearrange("p (q h) -> p h q", h=16), func=AF.Exp),
                    reads=[psb[bke]], wadd=[A["EB"]])
        P.op(DVE, lambda e: e.tensor_tensor(out=EB, in0=EB, in1=maskt.unsqueeze(2).to_broadcast([128, 2, 16, 128]), op=ALU.mult),
             reads=[A["EB"], A["mask"]], writes=[A["EB"]])
        P.op(DVE, lambda e: e.tensor_scalar(out=EB0, in0=EB[:, 0], scalar1=flags[:, 0:1], scalar2=None, op0=ALU.mult),
             reads=[A["EB"], b_flags], writes=[A["EB0"]])
        P.dma(SP, dout_sem, sck[:, 0:127, :], ck[:, 1:128, :])
        P.dma(SP, dout_sem, scv[:, 0:127, :], cv[:, 1:128, :])
        kc_sem = P.dsem("kc")
        P.dma(SP, kc_sem, Kc, ck.rearrange("s t f -> t s f"), writes=[A["Kc"]])
        for s_ in range(NS):
            bkt = bank()
            for kvp in range(2):
                P.op(PE, lambda e, s_=s_, kvp=kvp, bkt=bkt: e.transpose(out=ps[bkt][:, 128 * kvp:128 * kvp + 128],
                                                                     in_=Kc[:, s_, 128 * kvp:128 * kvp + 128], identity=ident),
                     reads=[A["Kc"], b_ident], writes=[psb[bkt]] if kvp == 0 else [], wadd=[psb[bkt]] if kvp == 1 else [])
            evac_copy(s_, KcT[:, s_].rearrange("p a t -> p (a t)"), ps[bkt][:, 0:256], [psb[bkt]], [], wadd=[A["KcT"]])
        P.dma(SP, kc_sem, Kc, cv.rearrange("s t f -> t s f"), reads=[A["KcT"]], writes=[A["Kc"]])
        P.op(POOL, lambda e: e.tensor_copy(out=Vcs, in_=Kc), reads=[A["Kc"]], writes=[A["Vcs"]])
        P.op(POOL, lambda e: e.memset(Vaug[:, :, 64:128], 1.0), writes=[A["Vaug"]])

        rhs_hh = lambda k, t0, n: hTh[:, k, 0:n]
        TB5 = TBS
        def attn_kv(kv):
            def ev_q(oc, tbi, pap, pb):
                t0, n = TBS[tbi]
                ctr["i"] += 1
                evac_copy(ctr["i"], qT[:, oc, t0:t0 + n], pap, [pb], [], wadd=[A["qT"]])
            A["qT"].r = list(A["qT"].r) + list(A["qT"].w); A["qT"].w = []
            proj_fm(w_in[:, OFF_Q + 256 * kv:OFF_Q + 256 * kv + 256], 256, rhs_h, hT_b, ev_q)
            s = wctr[0] % 2
            wctr[0] += 1
            for dup in range(2):
                P.dma(POOL, wsem[s], wslot[s][:, :, 64 * dup:64 * dup + 64],
                      w_in[:, OFF_K + 64 * kv:OFF_K + 64 * kv + 64].rearrange("(k p) f -> p k f", p=128),
                      writes=[wslot_b[s]] if dup == 0 else [], wadd=[wslot_b[s]] if dup == 1 else [])
            A["kT2"].r = list(A["kT2"].r) + list(A["kT2"].w); A["kT2"].w = []
            kblocks = [(hTh, hTh_b, 0, 128, 0)] + [(hT, hT_b[i], t0, n, 128 + t0) for i, (t0, n) in enumerate(TBS)]
            for bi_, (src, sb_, t0, n, c0) in enumerate(kblocks):
                bkk = bank()
                for k in range(8):
                    P.op(PE, lambda e, bkk=bkk, k=k, src=src, t0=t0, n=n, s=s: e.matmul(
                        ps[bkk][:, 0:n], lhsT=wslot[s][:, k, 0:128], rhs=src[:, k, t0:t0 + n], start=(k == 0), stop=(k == 7)),
                        reads=[wslot_b[s], sb_] if k == 0 else [], writes=[psb[bkk]] if k == 0 else [], sig=(k == 7))
                if not P.dead:
                    P.attach(Dep(P.esem[PE], P.cnt[PE]), reads=[wslot_b[s], sb_], writes=[psb[bkk]])
                evac_copy(bi_, kT2[:, c0:c0 + n], ps[bkk][:, 0:n], [psb[bkk]], [], wadd=[A["kT2"]])
            bkl_ = bank()
            for j_, (c0_, m_) in enumerate([(NP - 128, 128), (NP, NS)]):
                for k in range(8):
                    P.op(PE, lambda e, j_=j_, c0_=c0_, m_=m_, k=k, s=s: e.matmul(
                        ps[bkl_][0:m_, 64 * j_:64 * j_ + 64], lhsT=hT[:, k, c0_:c0_ + m_], rhs=wslot[s][:, k, 0:64],
                        start=(k == 0), stop=(k == 7)),
                        reads=[wslot_b[s], hT_b[3], hT_b[4]] if (k == 0 and j_ == 0) else [],
                        writes=[psb[bkl_]] if (k == 0 and j_ == 0) else [], sig=(k == 7 and j_ == 1))
            if not P.dead:
                P.attach(Dep(P.esem[PE], P.cnt[PE]), reads=[wslot_b[s], hT_b[3], hT_b[4]], writes=[psb[bkl_]])
            P.op(DVE, lambda e, kv=kv: e.tensor_copy(out=klast[:, 64 * kv:64 * kv + 64], in_=ps[bkl_][:, 0:64]), reads=[psb[bkl_]], wadd=[A["klast"]])
            P.op(DVE, lambda e, kv=kv: e.tensor_copy(out=knew[0:NS, 64 * kv:64 * kv + 64], in_=ps[bkl_][0:NS, 64:128]), reads=[psb[bkl_]], wadd=[A["knew"]])
            s = wctr[0] % 2
            wctr[0] += 1
            P.dma(POOL, wsem[s], wslot[s][:, :, 0:64], w_in[:, OFF_V + 64 * kv:OFF_V + 64 * kv + 64].rearrange("(k p) f -> p k f", p=128),
                  writes=[wslot_b[s]])
            A["Vaug"].r = list(A["Vaug"].r) + list(A["Vaug"].w); A["Vaug"].w = []
            vtiles = [(hTh, hTh_b, 0, 128)] + [(hT, hT_b[i // 4], 128 * i, 128) for i in range(16)] + [(hT, hT_b[4], NP, NS)]
            for grp in range(3):
                bkv = bank()
                tl_ = vtiles[8 * grp:8 * grp + 8]
                for j_, (src, sb_, c0_, m_) in enumerate(tl_):
                    for k in range(8):
                        firstg = (j_ == 0 and k == 0)
                        P.op(PE, lambda e, bkv=bkv, j_=j_, src=src, c0_=c0_, m_=m_, k=k, s=s: e.matmul(
                            ps[bkv][0:m_, 64 * j_:64 * j_ + 64], lhsT=src[:, k, c0_:c0_ + m_], rhs=wslot[s][:, k, 0:64],
                            start=(k == 0), stop=(k == 7)),
                            reads=[wslot_b[s], sb_, hTh_b] + hT_b if firstg else [], writes=[psb[bkv]] if firstg else [],
                            sig=(j_ == len(tl_) - 1 and k == 7))
                if not P.dead:
                    P.attach(Dep(P.esem[PE], P.cnt[PE]), reads=[wslot_b[s], hTh_b] + hT_b, writes=[psb[bkv]])
                nt_ = len(tl_)
                if grp < 2:
                    P.op(ACT, lambda e, bkv=bkv, grp=grp: e.activation(out=Vaug[:, 8 * grp:8 * grp + 8, 0:64],
                                                                     in_=ps[bkv][:, 0:512].rearrange("p (t d) -> p t d", d=64), func=AF.Copy),
                         reads=[psb[bkv]], wadd=[A["Vaug"]])
                    if grp == 1:
                        pass
                else:
                    P.op(ACT, lambda e, bkv=bkv: e.activation(out=Vaug[:, 16, 0:64], in_=ps[bkv][:, 0:64], func=AF.Copy),
                         reads=[psb[bkv]], wadd=[A["Vaug"]])
                    P.op(ACT, lambda e, bkv=bkv: e.activation(out=Vaug[0:NS, 17, 0:64], in_=ps[bkv][0:NS, 64:128], func=AF.Copy),
                         reads=[psb[bkv]], wadd=[A["Vaug"]])
                    P.op(ACT, lambda e, bkv=bkv, kv=kv: e.activation(out=vlast[:, 64 * kv:64 * kv + 64], in_=ps[bkv][:, 0:64], func=AF.Copy),
                         reads=[psb[bkv]], wadd=[A["vlast"]])
                    P.op(ACT, lambda e, bkv=bkv, kv=kv: e.activation(out=vnew[0:NS, 64 * kv:64 * kv + 64], in_=ps[bkv][0:NS, 64:128], func=AF.Copy),
                         reads=[psb[bkv]], wadd=[A["vnew"]])
            qv = lambda base, b_: qT[base:base + 64, 0:2, 128 * b_:128 * b_ + 128]
            def attn_s1(b_):
                ia = (2 * b_) % 4; ib = (2 * b_ + 1) % 4
                bA, bB = bank(), bank()
                kprev = slice(128 * b_, 128 * b_ + 128); kcur = slice(128 * b_ + 128, 128 * b_ + 256)
                seq = [(bA, 0, 0, kprev), (bB, 64, 0, kprev), (bA, 0, 1, kcur), (bB, 64, 1, kcur)]
                for (bk_, base, half, ks) in seq:
                    first = (half == 0)
                    P.op(PE, lambda e, bk_=bk_, base=base, half=half, ks=ks, b_=b_: e.matmul(
                        ps[bk_][:, 256 * half:256 * half + 256], lhsT=kT2[base:base + 64, ks], rhs=qv(base, b_), start=True, stop=True),
                        reads=[A["kT2"], A["qT"]] if first else [], writes=[psb[bk_]] if first else [], sig=(half == 1))
                    if half == 1 and not P.dead:
                        P.attach(Dep(P.esem[PE], P.cnt[PE]), reads=[A["kT2"], A["qT"]], writes=[psb[bk_]])
                for (bk_, ie, base_h) in [(bA, ia, 0), (bB, ib, 1)]:
                    P.op(ACT, lambda e, bk_=bk_, ie=ie: e.activation(out=Et[ie], in_=ps[bk_][:, :], func=AF.Exp, scale=0.125),
                         reads=[psb[bk_]], writes=[b_Et[ie]])
                    Ev = Et[ie].rearrange("p (h i q) -> p h i q", h=2, i=2)
                    Pv = PT[ie].rearrange("p (h i q) -> p h i q", h=2, i=2)
                    h0 = 4 * kv + base_h
                    eng = DVE if base_h == 0 else POOL
                    if b_ > 0:
                        P.op(eng, lambda e, Ev=Ev, Pv=Pv, h0=h0: e.tensor_tensor(out=Pv, in0=Ev, in1=EB[:, :, h0:h0 + 3:2, :], op=ALU.mult),
                             reads=[b_Et[ie], A["EB"]], writes=[b_PT[ie]])
                    else:
                        P.op(eng, lambda e, Ev=Ev, Pv=Pv, h0=h0: e.tensor_tensor(out=Pv[:, 0], in0=Ev[:, 0], in1=EB0[:, h0:h0 + 3:2, :], op=ALU.mult),
                             reads=[b_Et[ie], A["EB0"]], writes=[b_PT[ie]])
                        P.op(eng, lambda e, Ev=Ev, Pv=Pv, h0=h0: e.tensor_tensor(out=Pv[:, 1], in0=Ev[:, 1], in1=EB[:, 1, h0:h0 + 3:2, :], op=ALU.mult),
                             reads=[b_Et[ie], A["EB"]], wadd=[b_PT[ie]])

            def attn_s2(b_):
                ia = (2 * b_) % 4; ib = (2 * b_ + 1) % 4
                bO = bank()
                mm = [(ia, 0, b_, True), (ia, 1, b_ + 1, False), (ib, 0, b_, False), (ib, 1, b_ + 1, False)]
                for j_, (ip, half, tile, st_) in enumerate(mm):
                    cols = slice(0, 256) if ip == ia else slice(256, 512)
                    P.op(PE, lambda e, ip=ip, half=half, tile=tile, st_=st_, cols=cols: e.matmul(
                        ps[bO][:, cols], lhsT=Vaug[:, tile, :], rhs=PT[ip][:, 256 * half:256 * half + 256], start=st_, stop=False),
                        reads=[A["Vaug"], b_PT[ia], b_PT[ib], b_ES, b_ones] if j_ == 0 else [], writes=[psb[bO]] if j_ == 0 else [], sig=False)
                P.op(PE, lambda e, kv=kv: e.matmul(ps[bO][:, 0:512], lhsT=onesd[0:1, :], rhs=ES[0:1, kv].rearrange("p i q -> p (i q)"),
                                                   start=False, stop=True), sig=True)
                if not P.dead:
                    P.attach(Dep(P.esem[PE], P.cnt[PE]), reads=[A["Vaug"], b_PT[ia], b_PT[ib], b_ES, b_ones], writes=[psb[bO]])
                ir = b_ % 2
                P.op(ACT, lambda e, ir=ir: e.activation(out=rc[ir][64:128, :], in_=ps[bO][64:128, :], func=AF.Ln), reads=[psb[bO]], writes=[b_rc[ir]])
                P.op(ACT, lambda e, ir=ir: e.activation(out=rc[ir][64:128, :], in_=rc[ir][64:128, :], func=AF.Exp, scale=-1.0),
                     reads=[b_rc[ir]], writes=[b_rc[ir]])
                for par in range(2):
                    P.op(DVE, lambda e, par=par, ir=ir, b_=b_, kv=kv: e.tensor_tensor(
                        out=oT[64 * par:64 * par + 64, 2 * kv:2 * kv + 2, 128 * b_:128 * b_ + 128],
                        in0=ps[bO][0:64, 256 * par:256 * par + 256].rearrange("p (c q) -> p c q", c=2),
                        in1=rc[ir][64:128, 256 * par:256 * par + 256].rearrange("p (c q) -> p c q", c=2), op=ALU.mult),
                        reads=[psb[bO], b_rc[ir]], wadd=[oT_b[2 * kv], oT_b[2 * kv + 1]])

            attn_s1(0)
            for b_ in range(1, 16):
                attn_s1(b_)
                attn_s2(b_ - 1)
            attn_s2(15)
            base = 64 * (kv % 2)
            for i in range(4):
                hsrc = 64 * (i % 2)
                P.op(POOL, lambda e, i=i, hsrc=hsrc, base=base: e.tensor_copy(out=QsT[base:base + 64, 0, i, :], in_=qT[hsrc:hsrc + 64, i // 2, NP:NT]),
                     reads=[A["qT"]], writes=[A["QsT"]] if i == 0 else [], wadd=[A["QsT"]] if i > 0 else [])
            for sl_, i in enumerate([0, 1, 2, 3]):
                P.op(DVE, lambda e, i=i, kv=kv: e.tensor_copy(out=esr[0:1, :].rearrange("p (s i) -> p s i", i=4)[:, :, i],
                                                            in_=es16[0:1, 4 * kv + i:4 * kv + i + 1].to_broadcast([1, 16])),
                     reads=[A["es16"]], writes=[A["esr"]] if i == 0 else [], wadd=[A["esr"]] if i > 0 else [])
            bS, bD, bN = bank(), bank(), bank()
            Qsi = QsT[base:base + 64, 0].rearrange("p i s -> p s i")
            for s_ in range(NS):
                P.op(PE, lambda e, s_=s_, base=base, kv=kv: e.matmul(ps[bS][:, 4 * s_:4 * s_ + 4], lhsT=KcT[base:base + 64, s_, kv // 2, :],
                                                                  rhs=QsT[base:base + 64, 0, :, s_], start=True, stop=True),
                     reads=[A["KcT"], A["QsT"], A["kT2"]] if s_ == 0 else [], writes=[psb[bS]] if s_ == 0 else [], sig=False)
            P.op(PE, lambda e, base=base: e.matmul(ps[bS][0:NS, 64:128], lhsT=kT2[base:base + 64, 128 + NP:128 + NT], rhs=Qsi, start=True, stop=True), sig=True)
            if not P.dead:
                P.attach(Dep(P.esem[PE], P.cnt[PE]), reads=[A["KcT"], A["QsT"], A["kT2"]], writes=[psb[bS]])
            P.op(ACT, lambda e: e.activation(out=ptS[:, 0:64], in_=ps[bS][:, 0:64], func=AF.Exp, scale=0.125), reads=[psb[bS]], writes=[A["ptS"]])
            P.op(ACT, lambda e: e.activation(out=pdg[0:NS, :], in_=ps[bS][0:NS, 64:128], func=AF.Exp, scale=0.125), reads=[psb[bS]], writes=[A["pdg"]])
            P.op(DVE, lambda e, kv=kv: e.tensor_tensor(out=ptS[:, 0:64].rearrange("p (s i) -> p s i", i=4), in0=ptS[:, 0:64].rearrange("p (s i) -> p s i", i=4),
                                                in1=EB[:, 0, 4 * kv:4 * kv + 4, 0].unsqueeze(1).to_broadcast([128, 16, 4]), op=ALU.mult),
                 reads=[A["ptS"], A["EB"]], writes=[A["ptS"]])
            P.op(DVE, lambda e, kv=kv: e.tensor_tensor(out=pdg[0:NS, :].rearrange("p (s i) -> p s i", i=4), in0=pdg[0:NS, :].rearrange("p (s i) -> p s i", i=4),
                                                in1=ebs[0:NS, 4 * kv:4 * kv + 4].unsqueeze(1).to_broadcast([NS, 16, 4]), op=ALU.mult),
                 reads=[A["pdg"], A["ebs"]], writes=[A["pdg"]])
            P.op(DVE, lambda e: e.tensor_tensor(out=pdg[0:NS, :], in0=pdg[0:NS, :], in1=dgt[0:NS, :], op=ALU.mult),
                 reads=[A["pdg"], A["dgt"]], writes=[A["pdg"]])
            P.op(POOL, lambda e, kv=kv: e.tensor_copy(out=vnb[0:NS, 64 * kv:64 * kv + 64], in_=vnew[0:NS, 64 * kv:64 * kv + 64]),
                 reads=[A["vnew"]], writes=[A["vnb"]])
            P.op(PE, lambda e: e.matmul(ps[bD][:, 0:64], lhsT=ones_k, rhs=ptS[:, 0:64], start=True, stop=False),
                 reads=[A["ones_k"], A["ptS"], A["pdg"], A["esr"]], writes=[psb[bD]], sig=False)
            P.op(PE, lambda e: e.matmul(ps[bD][:, 0:64], lhsT=ones_k[0:NS, :], rhs=pdg[0:NS, :], start=False, stop=False), sig=False)
            P.op(PE, lambda e: e.matmul(ps[bD][:, 0:64], lhsT=ones_k[0:1, :], rhs=esr[0:1, :], start=False, stop=True), sig=True)
            if not P.dead:
                P.attach(Dep(P.esem[PE], P.cnt[PE]), reads=[A["ones_k"], A["ptS"], A["pdg"], A["esr"]], writes=[psb[bD]])
            ptv = ptS[:, 0:64].rearrange("p (s i) -> p s i", i=4)
            pdv = pdg[0:NS, :].rearrange("p (s i) -> p s i", i=4)
            psn = ps[bN][:, 0:64].rearrange("p (s i) -> p s i", i=4)
            for par in range(2):
                for s_ in range(NS):
                    P.op(PE, lambda e, par=par, s_=s_, kv=kv: e.matmul(
                        psn[64 * par:64 * par + 64, s_, par:4:2], lhsT=Vcs[:, s_, 64 * kv:64 * kv + 64], rhs=ptv[:, s_, par:4:2],
                        start=(s_ == 0), stop=False, tile_position=(0, 64 * par)),
                        reads=[A["Vcs"], A["ptS"], A["pdg"], A["vnb"]] if (par == 0 and s_ == 0) else [],
                        writes=[psb[bN]] if (par == 0 and s_ == 0) else [], sig=False)
                P.op(PE, lambda e, par=par, kv=kv: e.matmul(
                    psn[64 * par:64 * par + 64, :, par:4:2], lhsT=vnb[0:NS, 64 * kv:64 * kv + 64], rhs=pdv[:, :, par:4:2],
                    start=False, stop=True, tile_position=(0, 64 * par)), sig=(par == 1))
            if not P.dead:
                P.attach(Dep(P.esem[PE], P.cnt[PE]), reads=[A["Vcs"], A["ptS"], A["pdg"], A["vnb"]], writes=[psb[bN]])
            P.op(DVE, lambda e: e.reciprocal(out=rcs[:, 0:64], in_=ps[bD][:, 0:64]), reads=[psb[bD]], writes=[A["rcs"]])
            rcv = rcs[:, 0:64].rearrange("p (s i) -> p s i", i=4)
            for par in range(2):
                P.op(DVE, lambda e, par=par, kv=kv: e.tensor_tensor(
                    out=oT[64 * par:64 * par + 64, 2 * kv:2 * kv + 2, NP:NT],
                    in0=psn[64 * par:64 * par + 64, :, par:4:2].rearrange("p s c -> p c s"),
                    in1=rcv[64 * par:64 * par + 64, :, par:4:2].rearrange("p s c -> p c s"), op=ALU.mult),
                    reads=[psb[bN], A["rcs"]], wadd=[oT_b[2 * kv], oT_b[2 * kv + 1]])
        for kv in range(4):
            attn_kv(kv)
        P.dma(SP, dout_sem, pck, klast, reads=[A["klast"]])
        P.dma(SP, dout_sem, pcv, vlast, reads=[A["vlast"]])
        P.dma(SP, dout_sem, sck[:, 127, :], knew[0:NS, :], reads=[A["knew"]])
        P.dma(SP, dout_sem, scv[:, 127, :], vnew[0:NS, :], reads=[A["vnew"]])

        P.phase(13)
        sgaT = cv_(O_C + 0, [8, NT], BF16)
        sga_b = [Buf(f"sga{g}") for g in range(8)]
        gt2 = [cv_(O_C + 33024 + 1024 * i, [512], BF16) for i in range(4)]
        ft2 = [cv_(O_C + 37120 + 2048 * i, [512], F32) for i in range(2)]
        b_gt2 = [Buf() for _ in range(4)]; b_ft2 = [Buf(), Buf()]
        handoff(sga_b + b_gt2 + b_ft2, list(A.values()) + b_Et + b_PT + b_rc)
        for blk in range(2):
            def ev_za(oc, tbi, pap, pb, blk=blk):
                t0, n = TBS[tbi]
                g = 4 * blk + oc
                ctr["i"] += 1
                gi = ctr["i"] % 4; fi = ctr["i"] % 2
                P.op(ACT, lambda e: e.activation(out=gt2[gi][:, 0:n], in_=pap, func=AF.Sigmoid), reads=[pb], writes=[b_gt2[gi]])
                P.op(DVE, lambda e: e.tensor_tensor(out=ft2[fi][:, 0:n], in0=pap, in1=gt2[gi][:, 0:n], op=ALU.mult),
                     reads=[pb, b_gt2[gi]], writes=[b_ft2[fi]])
                P.op(POOL, lambda e: e.tensor_tensor(out=oT[:, g, t0:t0 + n], in0=oT[:, g, t0:t0 + n], in1=ft2[fi][:, 0:n], op=ALU.mult),
                     reads=[b_ft2[fi], oT_b[g]], wadd=[oT_b[g]])
            proj_fm(w_in[:, OFF_ZA + 512 * blk:OFF_ZA + 512 * blk + 512], 512, rhs_h, hT_b, ev_za)
        for blk in range(2):
            def ev_ga(oc, tbi, pap, pb, blk=blk):
                t0, n = TBS[tbi]
                g = 4 * blk + oc
                P.op(ACT, lambda e: e.activation(out=sgaT[:, g, t0:t0 + n], in_=pap, func=AF.Sigmoid), reads=[pb], wadd=[sga_b[g]])
            proj_fm(w_in[:, OFF_GA + 512 * blk:OFF_GA + 512 * blk + 512], 512, rhs_h, hT_b, ev_ga)
        mT = RA
        mT_b = gbs_b
        rhs_o = lambda k, t0, n: oT[:, k, t0:t0 + n]
        for blk in range(2):
            def ev_ba(oc, tbi, pap, pb, blk=blk):
                t0, n = TBS[tbi]
                g = 4 * blk + oc
                ctr["i"] += 1
                fi = ctr["i"] % 2
                P.op(DVE, lambda e: e.tensor_tensor(out=ft2[fi][:, 0:n], in0=pap, in1=sgaT[:, g, t0:t0 + n], op=ALU.mult),
                     reads=[pb, sga_b[g]], writes=[b_ft2[fi]])
                P.op(POOL, lambda e: e.tensor_tensor(out=mT[:, g, t0:t0 + n], in0=mT[:, g, t0:t0 + n], in1=ft2[fi][:, 0:n], op=ALU.add),
                     reads=[b_ft2[fi], mT_b[g]], wadd=[mT_b[g]])
            proj_fm(w_ba[:, 512 * blk:512 * blk + 512], 512, rhs_o, [BufGroup(oT_b)] * 5, ev_ba)

        P.phase(14)
        o2 = O_C + 8000
        NSL = 4
        GateB = cv_(o2 + 0, [1024], F32); LnG = cv_(o2 + 4096, [1024], F32); LnB = cv_(o2 + 8192, [1024], F32)
        gateS = cv_(o2 + 12288, [1024], F32); grow = cv_(o2 + 16384, [1024], F32)
        xt = [cv_(o2 + 20480 + 4096 * i, [1024], F32) for i in range(NSL)]
        rt = [cv_(o2 + 36864 + 4096 * i, [1024], F32) for i in range(NSL)]
        stt = cv_(o2 + 53248, [NSL, 2, 6], F32); mvt = cv_(o2 + 53504, [NSL, 2], F32); rsd = cv_(o2 + 53568, [NSL, 2], F32)
        mhalf = cv_(o2 + 53632, [1], F32)
        b_GateB = Buf(); b_LnG = Buf(); b_LnB = Buf(); b_gateS = Buf(); b_grow = Buf(); b_xt = [Buf() for _ in range(NSL)]; b_rt = [Buf() for _ in range(NSL)]
        b_stt = [Buf() for _ in range(NSL)]; b_mh = Buf()
        xsem2 = [P.dsem(f"xt{i}") for i in range(NSL)]; osem = [P.dsem(f"o{i}") for i in range(NSL)]
        handoff([b_GateB, b_LnG, b_LnB, b_gateS, b_grow] + b_xt + b_rt + b_stt + [b_mh], list(A.values()) + b_Et + b_PT + b_rc + sga_b + b_gt2 + b_ft2)
        misc_load(SP, LnG, ln_g.rearrange("(o n) -> o n", o=1).to_broadcast([128, 1024]), b_LnG)
        misc_load(SP, LnB, ln_b.rearrange("(o n) -> o n", o=1).to_broadcast([128, 1024]), b_LnB)
        P.op(POOL, lambda e: e.memset(mhalf, -0.5), writes=[b_mh])
        for hb in range(2):
            bkg = bank(); bkg2 = bank()
            for kk in range(4):
                k = 4 * hb + kk
                P.op(PE, lambda e, k=k, kk=kk, bkg=bkg: e.transpose(out=ps[bkg][0:1, 128 * kk:128 * kk + 128], in_=modT[:, 16 + k, 0:1], identity=ident),
                     reads=[b_modT, b_ident], writes=[psb[bkg]] if kk == 0 else [], wadd=[psb[bkg]] if kk > 0 else [])
                P.op(PE, lambda e, k=k, kk=kk, bkg2=bkg2: e.transpose(out=ps[bkg2][0:NS, 128 * kk:128 * kk + 128], in_=modT[:, 16 + k, 1:17], identity=ident),
                     reads=[b_modT, b_ident], writes=[psb[bkg2]] if kk == 0 else [], wadd=[psb[bkg2]] if kk > 0 else [])
            P.op(DVE, lambda e, hb=hb, bkg=bkg: e.tensor_copy(out=grow[0:1, 512 * hb:512 * hb + 512], in_=ps[bkg][0:1, 0:512]), reads=[psb[bkg]], wadd=[b_grow])
            P.op(DVE, lambda e, hb=hb, bkg2=bkg2: e.tensor_copy(out=gateS[0:NS, 512 * hb:512 * hb + 512], in_=ps[bkg2][0:NS, 0:512]), reads=[psb[bkg2]], wadd=[b_gateS])
        for hb in range(2):
            bkb = bank()
            P.op(PE, lambda e, hb=hb, bkb=bkb: e.matmul(ps[bkb][:, 0:512], lhsT=ones1[0:1, :], rhs=grow[0:1, 512 * hb:512 * hb + 512], start=True, stop=True),
                 reads=[b_grow, b_ones], writes=[psb[bkb]])
            P.op(DVE, lambda e, hb=hb, bkb=bkb: e.tensor_copy(out=GateB[:, 512 * hb:512 * hb + 512], in_=ps[bkb][:, 0:512]), reads=[psb[bkb]], wadd=[b_GateB])
        so = [load_w(w_out[:, 0:512], 512), load_w(w_out[:, 512:1024], 512)]
        for tt_i in range(17):
            rows, c0 = (128, 128 * tt_i) if tt_i < 16 else (NS, NP)
            sl = tt_i % NSL
            src = xp[c0:c0 + 128, :] if tt_i < 16 else xs
            P.dma(SP, xsem2[sl], xt[sl][0:rows, :], src, writes=[b_xt[sl]])
            gate_ap = GateB if tt_i < 16 else gateS
            gate_b = b_GateB if tt_i < 16 else b_gateS
            for fb in range(2):
                bko = bank()
                for k in range(8):
                    P.op(PE, lambda e, bko=bko, k=k, fb=fb, rows=rows, c0=c0: e.matmul(
                        ps[bko][0:rows, 0:512], lhsT=mT[:, k, c0:c0 + rows], rhs=wslot[so[fb]][:, k, 0:512], start=(k == 0), stop=(k == 7)),
                        reads=[wslot_b[so[fb]]] + mT_b if k == 0 else [], writes=[psb[bko]] if k == 0 else [], sig=(k == 7))
                if not P.dead:
                    P.attach(Dep(P.esem[PE], P.cnt[PE]), reads=[wslot_b[so[fb]]] + mT_b, writes=[psb[bko]])
                P.op(DVE, lambda e, bko=bko, fb=fb, rows=rows, sl=sl, gate_ap=gate_ap: e.tensor_tensor(
                    out=rt[sl][0:rows, 512 * fb:512 * fb + 512], in0=ps[bko][0:rows, 0:512], in1=gate_ap[0:rows, 512 * fb:512 * fb + 512], op=ALU.mult),
                    reads=[psb[bko], gate_b], writes=[b_rt[sl]] if fb == 0 else [], wadd=[b_rt[sl]] if fb == 1 else [])
            P.op(DVE, lambda e, rows=rows, sl=sl: e.scalar_tensor_tensor(out=rt[sl][0:rows, :], in0=xt[sl][0:rows, :], scalar=float(ALPHA),
                                                                         in1=rt[sl][0:rows, :], op0=ALU.mult, op1=ALU.add),
                 reads=[b_xt[sl], b_rt[sl]], writes=[b_rt[sl]])
            for hf in range(2):
                P.op(DVE, lambda e, rows=rows, sl=sl, hf=hf: e.bn_stats(out=stt[0:rows, sl, hf, :], in_=rt[sl][0:rows, 512 * hf:512 * hf + 512]),
                     reads=[b_rt[sl]], writes=[b_stt[sl]] if hf == 0 else [], wadd=[b_stt[sl]] if hf == 1 else [])
            P.op(DVE, lambda e, rows=rows, sl=sl: e.bn_aggr(out=mvt[0:rows, sl, :], in_=stt[0:rows, sl].rearrange("p a b -> p (a b)")),
                 reads=[b_stt[sl]], writes=[b_stt[sl]])
            P.op(POOL, lambda e, rows=rows, sl=sl: e.tensor_scalar(out=rsd[0:rows, sl, 0:1], in0=mvt[0:rows, sl, 1:2], scalar1=float(LN_EPS), scalar2=0.0,
                                                                   op0=ALU.add, op1=ALU.add), reads=[b_stt[sl]], writes=[b_stt[sl]])
            P.op(POOL, lambda e, rows=rows, sl=sl: e.tensor_tensor(out=rsd[0:rows, sl, 0:1], in0=rsd[0:rows, sl, 0:1], in1=mhalf[0:rows, :], op=ALU.pow),
                 reads=[b_stt[sl], b_mh], writes=[b_stt[sl]])
            P.op(POOL, lambda e, rows=rows, sl=sl: e.tensor_tensor(out=rsd[0:rows, sl, 1:2], in0=mvt[0:rows, sl, 0:1], in1=rsd[0:rows, sl, 0:1], op=ALU.mult),
                 reads=[b_stt[sl]], writes=[b_stt[sl]])
            P.op(POOL, lambda e, rows=rows, sl=sl: e.tensor_scalar(out=rsd[0:rows, sl, 1:2], in0=rsd[0:rows, sl, 1:2], scalar1=-1.0, scalar2=0.0,
                                                                   op0=ALU.mult, op1=ALU.add), reads=[b_stt[sl]], writes=[b_stt[sl]])
            P.op(ACT, lambda e, rows=rows, sl=sl: e.activation(out=xt[sl][0:rows, :], in_=rt[sl][0:rows, :], func=AF.Identity,
                                                               scale=rsd[0:rows, sl, 0:1], bias=rsd[0:rows, sl, 1:2]),
                 reads=[b_rt[sl], b_stt[sl]], writes=[b_xt[sl]])
            P.op(DVE, lambda e, rows=rows, sl=sl: e.tensor_tensor(out=xt[sl][0:rows, :], in0=xt[sl][0:rows, :], in1=LnG[0:rows, :], op=ALU.mult),
                 reads=[b_xt[sl], b_LnG], writes=[b_xt[sl]])
            P.op(POOL, lambda e, rows=rows, sl=sl: e.tensor_tensor(out=xt[sl][0:rows, :], in0=xt[sl][0:rows, :], in1=LnB[0:rows, :], op=ALU.add),
                 reads=[b_xt[sl], b_LnB], writes=[b_xt[sl]])
            dst = yp[c0:c0 + 128, :] if tt_i < 16 else ys
            P.dma(SP, osem[sl], dst, xt[sl][0:rows, :], reads=[b_xt[sl]])
            P.dma(SP, dout_sem, dst[0:1, 0:1], xt[sl][0:1, 0:1], reads=[b_xt[sl]]) if False else None
        final_deps = [Dep(o_.h, o_.cnt) for o_ in osem]

        P.dead = False
        if DEBUG:
            pass
        P.wait(SP, [Dep(dout_sem.h, dout_sem.cnt)] + final_deps)
        P.emit()
        nc.all_engine_barrier()
        nc.clear_and_free_semaphores(P.allsems)
        nc.all_engine_barrier()
        print("instruction counts:", P.ninst)
    return nc


def _bucket_np(dist):
    max_exact = 16
    df = np.maximum(dist, 1).astype(np.float32)
    large = max_exact + (np.log(df / np.float32(max_exact)) / np.float32(math.log(128 / max_exact)) * np.float32(16)).astype(np.int32)
    large = np.minimum(large, 31)
    return np.where(dist < max_exact, dist, large)


def _host_consts():
    R = np.zeros((32, 384), np.float32)
    for i in range(384):
        dist = 255 - i
        if 0 <= dist <= 128:
            R[int(_bucket_np(np.array([dist]))[0]), i] = 1.0
    j = np.arange(128)[:, None]
    q = np.arange(128)[None, :]
    mask = np.concatenate([(j >= q), (j <= q)], axis=1).astype(np.float32)
    r = np.arange(128)
    bmask = (r[:, None] // 32 == r[None, :] // 32).astype(np.float32)
    diag = np.zeros((16, 16, 4), np.float32)
    for s_ in range(16):
        diag[s_, s_, :] = 1.0
    return R, mask, bmask, diag.reshape(16, 64)


_NC_CACHE = {}


def kernel(x_prompt, x_sample, c_prompt, c_sample, state_ssm_re, state_ssm_im, cache_swa_k, cache_swa_v,
           w_ada, b_ada, w_in, ssm_lambda_re, ssm_lambda_im, ssm_log_delta, ssm_b_re, ssm_b_im,
           ssm_c_re, ssm_c_im, ssm_d, w_glu, b_glu, attn_sinks, rel_bias, w_branch_s, w_branch_a,
           w_out, ln_g, ln_b):
    f = lambda a: np.ascontiguousarray(np.asarray(a, dtype=np.float32))
    x_prompt = f(x_prompt); x_sample = f(x_sample); c_prompt = f(c_prompt); c_sample = f(c_sample)
    R, mask, bmask, diag = _host_consts()
    shared = {
        "w_ada": f(w_ada)[0], "b_ada": f(b_ada)[0], "w_in": f(w_in)[0],
        "lam_re": f(ssm_lambda_re)[0], "lam_im": f(ssm_lambda_im)[0], "log_delta": f(ssm_log_delta)[0],
        "b_re": f(ssm_b_re)[0].reshape(4096, 16), "b_im": f(ssm_b_im)[0].reshape(4096, 16),
        "c_re": f(ssm_c_re)[0].reshape(1024, 64), "c_im": f(ssm_c_im)[0].reshape(1024, 64),
        "ssm_d": f(ssm_d)[0], "w_glu": f(w_glu)[0], "b_glu": f(b_glu)[0], "sinks": f(attn_sinks)[0],
        "rel_bias": f(rel_bias), "w_bs": f(w_branch_s)[0], "w_ba": f(w_branch_a)[0], "w_out": f(w_out)[0],
        "ln_g": f(ln_g)[0], "ln_b": f(ln_b)[0],
        "rtab": R, "maskc": mask, "bmaskc": bmask, "diagc": diag,
    }
    sre = f(state_ssm_re)[0].reshape(128, 4096); sim = f(state_ssm_im)[0].reshape(128, 4096)
    ckk = f(cache_swa_k)[0].reshape(128, 128, 256); cvv = f(cache_swa_v)[0].reshape(128, 128, 256)
    in_maps = []
    for c in range(NCORES):
        b, qr = c // 4, c % 4
        t0 = NP * qr
        xh = x_prompt[b, t0 - 128:t0] if qr > 0 else np.zeros((128, D), np.float32)
        flags = np.zeros(32, np.float32)
        flags[0] = 1.0 if qr > 0 else 0.0
        xprev = np.zeros((3, NP, D), np.float32)
        for j in range(3):
            qq = qr - 1 - j
            if qq >= 0:
                flags[1 + j] = 1.0
                xprev[j] = x_prompt[b, NP * qq:NP * qq + NP]
        m = dict(shared)
        m.update({
            "xprev": xprev, "xp": np.ascontiguousarray(x_prompt[b, t0:t0 + NP]), "xh": np.ascontiguousarray(xh),
            "xs": np.ascontiguousarray(x_sample[NS * c:NS * c + NS, 0]),
            "cc": np.ascontiguousarray(np.concatenate([c_prompt[b:b + 1], c_sample[NS * c:NS * c + NS]], 0)),
            "st_re": np.ascontiguousarray(sre[NS * c:NS * c + NS]), "st_im": np.ascontiguousarray(sim[NS * c:NS * c + NS]),
            "ck": np.ascontiguousarray(ckk[NS * c:NS * c + NS]), "cv": np.ascontiguousarray(cvv[NS * c:NS * c + NS]),
            "flags": flags,
        })
        in_maps.append(m)
    nc = build()
    res = run_bass_kernel_spmd(nc, in_maps, core_ids=list(range(NCORES)))
    R_ = res.results
    kernel.last_results = R_
    y_prompt = np.stack([np.concatenate([R_[4 * b + q]["yp"] for q in range(4)], 0) for b in range(2)], 0)
    y_sample = np.concatenate([R_[c]["ys"] for c in range(NCORES)], 0).reshape(128, 1, D)
    p_hr = np.stack([R_[4 * b + 3]["pst_re"].reshape(64, 64) for b in range(2)], 0)[None]
    p_hi = np.stack([R_[4 * b + 3]["pst_im"].reshape(64, 64) for b in range(2)], 0)[None]
    p_k = np.stack([R_[4 * b + 3]["pck"].reshape(128, 4, 64) for b in range(2)], 0)[None]
    p_v = np.stack([R_[4 * b + 3]["pcv"].reshape(128, 4, 64) for b in range(2)], 0)[None]
    s_hr = np.concatenate([R_[c]["sst_re"] for c in range(NCORES)], 0).reshape(1, 128, 64, 64)
    s_hi = np.concatenate([R_[c]["sst_im"] for c in range(NCORES)], 0).reshape(1, 128, 64, 64)
    s_k = np.concatenate([R_[c]["sck"] for c in range(NCORES)], 0).reshape(1, 128, 128, 4, 64)
    s_v = np.concatenate([R_[c]["scv"] for c in range(NCORES)], 0).reshape(1, 128, 128, 4, 64)
    return (y_prompt.astype(np.float32), y_sample.astype(np.float32), p_hr.astype(np.float32), p_hi.astype(np.float32),
            p_k.astype(np.float32), p_v.astype(np.float32), s_hr.astype(np.float32), s_hi.astype(np.float32),
            s_k.astype(np.float32), s_v.astype(np.float32))
```

```python
import math
import os
from contextlib import ExitStack
import numpy as np
import ml_dtypes
import concourse.bass as bass
import concourse.mybir as mybir
from concourse.bass_utils import run_bass_kernel_spmd

F32 = mybir.dt.float32
BF16 = mybir.dt.bfloat16
U8 = mybir.dt.uint8
ALU = mybir.AluOpType
AF = mybir.ActivationFunctionType
AX = mybir.AxisListType

PE, ACT, DVE, POOL, SP = "tensor", "scalar", "vector", "gpsimd", "sync"
ENGS = [PE, ACT, DVE, POOL, SP]

NCORES = 8
D = 1024
NP = 2048
NS = 16
NT = NP + NS
TBS = [(0, 512), (512, 512), (1024, 512), (1536, 512), (2048, 16)]
DIN = 6656
OFF_U, OFF_ZS, OFF_Q, OFF_K, OFF_V, OFF_ZA, OFF_GS, OFF_GA = 0, 1024, 2048, 3072, 3328, 3584, 4608, 5632
ALPHA = 2.0 ** 0.25
LN_EPS = 1e-5
DEBUG = False


class Dep:
    __slots__ = ("sem", "val")

    def __init__(self, sem, val):
        self.sem = sem
        self.val = val


class Buf:
    __slots__ = ("w", "r", "name")

    def __init__(self, name=""):
        self.w = []
        self.r = []
        self.name = name


class _RProxy:
    def __init__(self, bufs):
        self.bufs = bufs

    def append(self, h):
        for b in self.bufs:
            b.r.append(h)

    def __len__(self):
        return 0


class BufGroup:
    def __init__(self, bufs):
        self.bufs = list(bufs)
        self.r = _RProxy(self.bufs)

    @property
    def w(self):
        return [h for b in self.bufs for h in b.w]


def handoff(new_bufs, old_bufs):
    deps = []
    for b in old_bufs:
        deps.extend(b.w)
        deps.extend(b.r)
    for nb in new_bufs:
        nb.r = list(nb.r) + deps


class DSem:
    def __init__(self, h):
        self.h = h
        self.cnt = 0


class Prog:
    def __init__(self, nc, stack):
        self.nc = nc
        self.q = {e: [] for e in ENGS}
        self.esem = {}
        self.cnt = {e: 0 for e in ENGS}
        self.allsems = []
        for e in [PE, ACT, DVE, POOL]:
            self.esem[e] = nc.alloc_semaphore("s_" + e)
            self.allsems.append(self.esem[e])
        self.seen = {}
        self.stack = stack
        self.nd = 0
        self.ninst = {e: 0 for e in ENGS}
        self.dead = False
        self.stop = int(os.environ.get("KSTOP", "99"))

    def phase(self, n):
        self.dead = n > self.stop

    def dsem(self, name=None):
        self.nd += 1
        h = self.nc.alloc_semaphore(f"d{self.nd}_{name or 'm'}")
        self.allsems.append(h)
        return DSem(h)

    def _waits(self, eng, deps):
        ws = []
        for d in deps:
            if d is None:
                continue
            k = (eng, id(d.sem))
            if self.seen.get(k, 0) >= d.val:
                continue
            self.seen[k] = d.val
            ws.append((d.sem, d.val))
        return ws

    @staticmethod
    def _compact(lst):
        best = {}
        for d in lst:
            k = id(d.sem)
            if k not in best or best[k].val < d.val:
                best[k] = d
        return list(best.values())

    @staticmethod
    def _bufdeps(reads, writes, wadd=()):
        deps = []
        for b in reads:
            deps.extend(b.w)
        for b in writes:
            deps.extend(b.w)
            deps.extend(b.r)
        for b in wadd:
            deps.extend(b.r)
        return deps

    @classmethod
    def _update(cls, h, reads, writes, wadd=()):
        for b in reads:
            b.r.append(h)
            if len(b.r) > 32:
                b.r = cls._compact(b.r)
        for b in writes:
            b.w = [h]
            b.r = []
        for b in wadd:
            b.w.append(h)
            if len(b.w) > 32:
                b.w = cls._compact(b.w)

    def op(self, eng, fn, reads=(), writes=(), deps=(), sig=True, wadd=()):
        if self.dead:
            return None
        alld = list(deps) + self._bufdeps(reads, writes, wadd)
        ws = self._waits(eng, alld)
        h = None
        if sig:
            self.cnt[eng] += 1
            h = Dep(self.esem[eng], self.cnt[eng])
        sem = self.esem[eng] if sig else None
        self.ninst[eng] += 1 + len(ws)

        def run(e, ws=ws, fn=fn, sem=sem):
            for (s, v) in ws:
                e.wait_ge(s, v)
            ins = fn(e)
            if sem is not None:
                ins.then_inc(sem, 1)
        self.q[eng].append(run)
        if h is not None:
            self._update(h, reads, writes, wadd)
        return h

    def attach(self, h, reads=(), writes=(), wadd=()):
        if self.dead or h is None:
            return
        self._update(h, reads, writes, wadd)

    def dma(self, eng, ds, out, in_, reads=(), writes=(), deps=(), wadd=(), **kw):
        if self.dead:
            return None
        alld = list(deps) + self._bufdeps(reads, writes, wadd)
        ws = self._waits(eng, alld)
        ds.cnt += 16
        h = Dep(ds.h, ds.cnt)
        self.ninst[eng] += 1 + len(ws)

        def run(e, ws=ws, out=out, in_=in_, kw=kw, sh=ds.h):
            for (s, v) in ws:
                e.wait_ge(s, v)
            e.dma_start(out=out, in_=in_, **kw).then_inc(sh, 16)
        self.q[eng].append(run)
        self._update(h, reads, writes, wadd)
        return h

    def raw(self, eng, fn, ds, inc, reads=(), writes=(), deps=()):
        if self.dead:
            return None
        alld = list(deps) + self._bufdeps(reads, writes)
        ws = self._waits(eng, alld)
        ds.cnt += inc
        h = Dep(ds.h, ds.cnt)

        def run(e, ws=ws, fn=fn, sh=ds.h, inc=inc):
            for (s, v) in ws:
                e.wait_ge(s, v)
            fn(e).then_inc(sh, inc)
        self.q[eng].append(run)
        self._update(h, reads, writes)
        return h

    def wait(self, eng, deps):
        ws = self._waits(eng, deps)

        def run(e, ws=ws):
            for (s, v) in ws:
                e.wait_ge(s, v)
        self.q[eng].append(run)

    def emit(self):
        nc = self.nc
        with nc.Block() as block:
            @block.tensor
            def _(e):
                for f in self.q[PE]:
                    f(e)

            @block.scalar
            def _(e):
                for f in self.q[ACT]:
                    f(e)

            @block.vector
            def _(e):
                for f in self.q[DVE]:
                    f(e)

            @block.gpsimd
            def _(e):
                for f in self.q[POOL]:
                    f(e)

            @block.sync
            def _(e):
                for f in self.q[SP]:
                    f(e)


def _dsize(dt):
    return {F32: 4, BF16: 2, U8: 1}[dt]


class Arena:
    def __init__(self, nc, stack, nbytes):
        self.t = stack.enter_context(nc.sbuf_tensor("arena", [128, nbytes], U8))
        self.nbytes = nbytes

    def carve(self, off, shape, dt):
        n = int(np.prod(shape)) * _dsize(dt)
        assert off % 4 == 0 and off + n <= self.nbytes, (off, n, self.nbytes)
        v = self.t[:, off:off + n]
        if dt != U8:
            v = v.bitcast(dt)
        if len(shape) > 1:
            names = [f"a{i}" for i in range(len(shape))]
            pat = "p (" + " ".join(names) + ") -> p " + " ".join(names)
            v = v.rearrange(pat, **{names[i]: shape[i] for i in range(len(shape))})
        return v


O_HT = 0
O_HTH = 33024
O_CONST = 35072
O_W = 45312
O_A = 61696
O_B = 94720
O_C = 127744
ARENA = 212000
C_SIZE = ARENA - O_C


def build():
    nc = bass.Bass("TRN2", target_bir_lowering=False)

    def din(name, shape, dt=F32):
        return nc.dram_tensor(name, list(shape), dt, kind="ExternalInput").ap()

    def dout(name, shape, dt=F32):
        return nc.dram_tensor(name, list(shape), dt, kind="ExternalOutput").ap()

    xprev = din("xprev", [3, NP, D]); xp = din("xp", [NP, D]); xh = din("xh", [128, D]); xs = din("xs", [NS, D]); ccin = din("cc", [17, D])
    st_re = din("st_re", [NS, 4096]); st_im = din("st_im", [NS, 4096])
    ck = din("ck", [NS, 128, 256]); cv = din("cv", [NS, 128, 256])
    w_ada = din("w_ada", [D, 3072]); b_ada = din("b_ada", [3072]); w_in = din("w_in", [D, DIN])
    lam_re = din("lam_re", [64, 64]); lam_im = din("lam_im", [64, 64]); log_delta = din("log_delta", [64])
    b_re = din("b_re", [4096, 16]); b_im = din("b_im", [4096, 16])
    c_re = din("c_re", [1024, 64]); c_im = din("c_im", [1024, 64])
    ssm_d = din("ssm_d", [1024]); w_glu = din("w_glu", [D, D]); b_glu = din("b_glu", [D])
    sinks = din("sinks", [16]); rel_bias = din("rel_bias", [32, 16])
    w_bs = din("w_bs", [D, D]); w_ba = din("w_ba", [D, D]); w_out = din("w_out", [D, D])
    ln_g = din("ln_g", [D]); ln_b = din("ln_b", [D])
    rtab = din("rtab", [32, 384]); maskc = din("maskc", [128, 256]); bmaskc = din("bmaskc", [128, 128])
    diagc = din("diagc", [16, 64]); flagsc = din("flags", [32])

    yp = dout("yp", [NP, D]); ys = dout("ys", [NS, D])
    pst_re = dout("pst_re", [32, 128]); pst_im = dout("pst_im", [32, 128])
    pck = dout("pck", [128, 256]); pcv = dout("pcv", [128, 256])
    sst_re = dout("sst_re", [NS, 4096]); sst_im = dout("sst_im", [NS, 4096])
    sck = dout("sck", [NS, 128, 256]); scv = dout("scv", [NS, 128, 256])
    dbg = {}
    if DEBUG:
        dbg["hT"] = dout("dbg_hT", [128, 8, NT], BF16)
        dbg["uT"] = dout("dbg_uT", [128, 8, NT], BF16)
        dbg["pw"] = dout("dbg_pw", [128, 9 * 2 * 32])
        dbg["hend"] = dout("dbg_hend", [128, 2 * 32 * 16])
        dbg["bb"] = dout("dbg_bb", [128, 2 * 32 * 16]); dbg["ccm"] = dout("dbg_ccm", [128, 2 * 32 * 16])
        dbg["R8"] = dout("dbg_R8", [128, 16 * 2 * 32]); dbg["R128"] = dout("dbg_R128", [128, 16 * 2 * 32])
        dbg["A2k"] = dout("dbg_A2k", [128, 3 * 2 * 32])
        dbg["WinL"] = dout("dbg_WinL", [128, 8 * 8 * 2 * 128], BF16)
        dbg["X0"] = dout("dbg_X0", [128, 512])
        dbg["KL"] = dout("dbg_KL", [128, 2 * 8 * 128], BF16); dbg["Ca"] = dout("dbg_Ca", [128, 2 * 4 * 9 * 2 * 32], BF16)
        dbg["Hb"] = dout("dbg_Hb", [128, 2 * 2048], BF16); dbg["Xs"] = dout("dbg_Xs", [128, 2 * 2048])
        dbg["carry"] = dout("dbg_carry", [128, 17 * 2 * 32])
        dbg["yT"] = dout("dbg_yT", [128, 8, NT], BF16)
        dbg["gbs"] = dout("dbg_gbs", [128, 8, NT], BF16)
        dbg["oT"] = dout("dbg_oT", [128, 8, NT], BF16)
        dbg["mT"] = dout("dbg_mT", [128, 8, NT], BF16)
        dbg["modT"] = dout("dbg_modT", [128, 24 * 17])

    ib = nc.dram_tensor("cc_ib", [128, 64], F32, kind="Internal")
    ob = nc.dram_tensor("cc_ob", [NCORES * 128, 64], F32, kind="Internal")

    st = ExitStack()
    with st:
        P = Prog(nc, st)
        AR = Arena(nc, st, ARENA)
        cv_ = AR.carve
        ps = [st.enter_context(nc.psum_tensor(f"ps{i}", [128, 512], F32)) for i in range(8)]
        psb = [Buf(f"ps{i}") for i in range(8)]
        dout_sem = P.dsem("dout")
        misc_sem = P.dsem("misc")

        def misc_load(eng, out, in_, buf, wadd=False, **kw):
            if P.dead:
                return None
            if wadd:
                return P.dma(eng, P.dsem(), out, in_, wadd=[buf], **kw)
            return P.dma(eng, P.dsem(), out, in_, writes=[buf], **kw)

        hT = cv_(O_HT, [8, NT], BF16)
        hTh = cv_(O_HTH, [8, 128], BF16)
        hT_b = [Buf(f"hT{i}") for i in range(len(TBS))]
        hTh_b = Buf("hTh")
        o = O_CONST
        ident = cv_(o, [128], F32); o += 512
        modT = cv_(o, [24, 17], F32); o += 1664
        op1p = cv_(o, [8, 17], F32); o += 576
        flags = cv_(o, [32], F32); o += 128
        Dm = cv_(o, [8], F32); o += 32
        bglu = cv_(o, [8], F32); o += 32
        ES = cv_(o, [4, 4, 128], BF16); o += 4096
        onesd = cv_(o, [128], BF16); o += 256
        EBself = cv_(o, [16], F32); o += 64
        bmask = cv_(o, [128], F32); o += 512
        ones1 = cv_(o, [128], F32); o += 512
        assert o <= O_CONST + 10240
        b_ident = Buf(); b_modT = Buf(); b_flags = Buf(); b_Dm = Buf(); b_bglu = Buf(); b_ES = Buf()
        b_ones = Buf(); b_EBself = Buf(); b_bmask = Buf()
        wslot = [cv_(O_W + 8192 * i, [8, 512], BF16) for i in range(2)]
        wslot_b = [Buf("w0"), Buf("w1")]
        wsem = [P.dsem("w0"), P.dsem("w1")]
        wctr = [0]
        RA = cv_(O_A, [8, NT], BF16)
        RB = cv_(O_B, [8, NT], BF16)

        rr = {"i": 0}

        def bank():
            i = rr["i"] % 8
            rr["i"] += 1
            return i

        def load_w(src2d, ncols):
            s = wctr[0] % 2
            wctr[0] += 1
            P.dma(POOL, wsem[s], wslot[s][:, :, 0:ncols], src2d.rearrange("(k p) f -> p k f", p=128),
                  writes=[wslot_b[s]])
            return s

        def evac_copy(i, out_ap, in_ap, reads, writes, wadd=()):
            if i % 2 == 0:
                return P.op(ACT, lambda e: e.activation(out=out_ap, in_=in_ap, func=AF.Copy), reads=reads, writes=writes, wadd=wadd)
            return P.op(DVE, lambda e: e.tensor_copy(out=out_ap, in_=in_ap), reads=reads, writes=writes, wadd=wadd)

        def proj_fm(src2d, ncols, rhs_of, rhs_bufs, evac, tbs=TBS):
            s = load_w(src2d, ncols)
            for oc in range(ncols // 128):
                for tbi, (t0, n) in enumerate(tbs):
                    b = bank()
                    for k in range(8):
                        last = (k == 7)
                        P.op(PE, lambda e, b=b, k=k, oc=oc, tbi=tbi, t0=t0, n=n, s=s: e.matmul(
                            ps[b][:, 0:n], lhsT=wslot[s][:, k, oc * 128:(oc + 1) * 128], rhs=rhs_of(k, t0, n),
                            start=(k == 0), stop=(k == 7)),
                            reads=[wslot_b[s], rhs_bufs[tbi]] if k == 0 else [], writes=[psb[b]] if k == 0 else [],
                            sig=last)
                        if last and not P.dead:
                            h = Dep(P.esem[PE], P.cnt[PE])
                            P.attach(h, reads=[wslot_b[s], rhs_bufs[tbi]], writes=[psb[b]])
                    evac(oc, tbi, ps[b][:, 0:n], psb[b])

        def cmul(eng, dst_r, dst_i, xr, xi, yr, yi, t1, t2, bufs_r, bufs_w, tb):
            P.op(eng, lambda e: e.tensor_tensor(out=t1, in0=xr, in1=yr, op=ALU.mult), reads=bufs_r, writes=[tb])
            P.op(eng, lambda e: e.tensor_tensor(out=t2, in0=xi, in1=yi, op=ALU.mult), reads=bufs_r, writes=[tb])
            P.op(eng, lambda e: e.tensor_tensor(out=dst_r, in0=t1, in1=t2, op=ALU.subtract), reads=[tb], writes=bufs_w)
            P.op(eng, lambda e: e.tensor_tensor(out=t1, in0=xr, in1=yi, op=ALU.mult), reads=bufs_r + bufs_w, writes=[tb])
            P.op(eng, lambda e: e.tensor_tensor(out=t2, in0=xi, in1=yr, op=ALU.mult), reads=bufs_r + bufs_w, writes=[tb])
            P.op(eng, lambda e: e.tensor_tensor(out=dst_i, in0=t1, in1=t2, op=ALU.add), reads=[tb], writes=bufs_w)

        P.phase(0)
        P.op(POOL, lambda e: e.memset(ident, 0.0), writes=[b_ident])
        P.op(POOL, lambda e: e.affine_select(out=ident, in_=ident, pattern=[[-1, 128]], compare_op=ALU.not_equal,
                                             fill=1.0, base=0, channel_multiplier=1), writes=[b_ident])
        misc_load(SP, flags, flagsc.rearrange("(o n) -> o n", o=1).to_broadcast([128, 32]), b_flags)
        misc_load(SP, bmask, bmaskc, b_bmask)
        P.op(POOL, lambda e: e.memset(ones1[0:1, :], 1.0), writes=[b_ones])
        P.op(POOL, lambda e: e.memset(onesd[0:1, 0:64], 0.0), wadd=[b_ones])
        P.op(POOL, lambda e: e.memset(onesd[0:1, 64:128], 1.0), wadd=[b_ones])

        P.phase(1)
        c_t = cv_(O_C + 62208, [1024], F32); c_sg = cv_(O_C + 66304, [1024], F32)
        ccT = cv_(O_C + 70400, [8, 17], BF16); badain = cv_(O_C + 70912, [128], F32); badaT = cv_(O_C + 71424, [24], F32)
        b_ct = Buf(); b_csg = Buf(); b_ccT = Buf(); b_bin = Buf(); b_baT = Buf()
        misc_load(SP, c_t[0:17, :], ccin, b_ct)
        misc_load(SP, badain[0:24, :], b_ada.rearrange("(c p) -> c p", p=128), b_bin)
        P.op(ACT, lambda e: e.activation(out=c_sg[0:17, :], in_=c_t[0:17, :], func=AF.Sigmoid), reads=[b_ct], writes=[b_csg])
        P.op(DVE, lambda e: e.tensor_tensor(out=c_sg[0:17, :], in0=c_sg[0:17, :], in1=c_t[0:17, :], op=ALU.mult),
             reads=[b_ct], writes=[b_csg])
        bk = bank()
        for k in range(8):
            P.op(PE, lambda e, k=k: e.transpose(out=ps[bk][:, 17 * k:17 * k + 17], in_=c_sg[0:17, 128 * k:128 * k + 128],
                                                identity=ident[0:17, 0:17]),
                 reads=[b_csg, b_ident], writes=[psb[bk]] if k == 0 else [], wadd=[psb[bk]] if k > 0 else [])
        P.op(DVE, lambda e: e.tensor_copy(out=ccT.rearrange("p k s -> p (k s)"), in_=ps[bk][:, 0:136]), reads=[psb[bk]], writes=[b_ccT])
        bk2 = bank()
        P.op(PE, lambda e: e.transpose(out=ps[bk2][:, 0:24], in_=badain[0:24, :], identity=ident[0:24, 0:24]),
             reads=[b_bin, b_ident], writes=[psb[bk2]])
        P.op(DVE, lambda e: e.tensor_copy(out=badaT, in_=ps[bk2][:, 0:24]), reads=[psb[bk2]], writes=[b_baT])
        bkm = bank()
        hlast = None
        for blk in range(6):
            s = load_w(w_ada[:, 512 * blk:512 * blk + 512], 512)
            for oc in range(4):
                f = 4 * blk + oc
                for k in range(8):
                    first = (blk == 0 and oc == 0 and k == 0)
                    lastk = (k == 7)
                    hlast = P.op(PE, lambda e, f=f, k=k, oc=oc, s=s: e.matmul(
                        ps[bkm][:, 17 * f:17 * f + 17], lhsT=wslot[s][:, k, oc * 128:(oc + 1) * 128], rhs=ccT[:, k, :],
                        start=(k == 0), stop=(k == 7)),
                        reads=[wslot_b[s], b_ccT] if k == 0 else [], writes=[psb[bkm]] if first else [], sig=lastk and oc == 3)
            P.attach(hlast, reads=[wslot_b[s]], wadd=[psb[bkm]])
        P.op(DVE, lambda e: e.tensor_tensor(out=modT, in0=ps[bkm][:, 0:408].rearrange("p (f s) -> p f s", f=24),
                                            in1=badaT.unsqueeze(2).to_broadcast([128, 24, 17]), op=ALU.add),
             reads=[psb[bkm], b_baT], writes=[b_modT])
        P.op(DVE, lambda e: e.tensor_scalar(out=op1p, in0=modT[:, 8:16, :], scalar1=1.0, scalar2=None, op0=ALU.add),
             reads=[b_modT], wadd=[b_modT])

        xst = [cv_(O_C + 71552 + 4096 * i, [1024], F32) for i in range(2)]
        xst_b = [Buf(), Buf()]
        xsem = [P.dsem("x0"), P.dsem("x1")]
        hTs_b = hT_b[4]
        tmpS = cv_(O_C + 79744, [8, 16], F32)
        b_tmpS = Buf()

        def phase_a(xsrc, full):
            tiles = [("p", i) for i in range(16)] + ([("h", 0), ("s", 0)] if full else [])
            for ti, (kind, i) in enumerate(tiles):
                sl = ti % 2
                if kind == "p":
                    src, rows, dst, dbuf = xsrc[128 * i:128 * i + 128, :], 128, (lambda k, i=i: hT[:, k, 128 * i:128 * i + 128]), hT_b[i // 4]
                elif kind == "h":
                    src, rows, dst, dbuf = xh, 128, (lambda k: hTh[:, k, :]), hTh_b
                else:
                    src, rows, dst, dbuf = xs, NS, None, hTs_b
                P.dma(SP, xsem[sl], xst[sl][0:rows, :], src, writes=[xst_b[sl]])
                b0, b1 = bank(), bank()
                for k in range(8):
                    bb_ = b0 if k < 4 else b1
                    j = k % 4
                    P.op(PE, lambda e, k=k, bb_=bb_, j=j, sl=sl, rows=rows: e.transpose(
                        out=ps[bb_][:, rows * j:rows * j + rows], in_=xst[sl][0:rows, 128 * k:128 * k + 128],
                        identity=ident[0:rows, 0:rows]),
                        reads=[xst_b[sl], b_ident], writes=[psb[bb_]] if j == 0 else [], wadd=[psb[bb_]] if j > 0 else [])
                if kind != "s":
                    for k in range(8):
                        bb_ = b0 if k < 4 else b1
                        j = k % 4
                        src_ps = ps[bb_][:, 128 * j:128 * j + 128]
                        if k < 4:
                            P.op(DVE, lambda e, k=k, src_ps=src_ps, dst=dst: e.tensor_scalar(
                                out=dst(k), in0=src_ps, scalar1=op1p[:, k, 0:1], scalar2=modT[:, k, 0:1], op0=ALU.mult, op1=ALU.add),
                                reads=[psb[bb_], b_modT], wadd=[dbuf])
                        else:
                            P.op(ACT, lambda e, k=k, src_ps=src_ps, dst=dst: e.activation(
                                out=dst(k), in_=src_ps, func=AF.Identity, scale=op1p[:, k, 0:1], bias=modT[:, k, 0:1]),
                                reads=[psb[bb_], b_modT], wadd=[dbuf])
                else:
                    for half, bb_ in enumerate([b0, b1]):
                        P.op(DVE, lambda e, half=half, bb_=bb_: e.tensor_tensor(
                            out=tmpS[:, 4 * half:4 * half + 4, :], in0=ps[bb_][:, 0:64].rearrange("p (k s) -> p k s", k=4),
                            in1=op1p[:, 4 * half:4 * half + 4, 1:17], op=ALU.mult), reads=[psb[bb_], b_modT], wadd=[b_tmpS])
                    P.op(DVE, lambda e: e.tensor_tensor(out=hT[:, :, NP:NT], in0=tmpS, in1=modT[:, 0:8, 1:17], op=ALU.add),
                         reads=[b_tmpS, b_modT], wadd=[dbuf])

        uT = RA
        uT_b = [Buf(f"uT{g}") for g in range(8)]
        rhs_h = lambda k, t0, n: hT[:, k, t0:t0 + n]
        cnt = {"i": 0}

        def phase_b(full):
            for blk in range(2):
                def ev(oc, tbi, pap, pb, blk=blk):
                    t0, n = TBS[tbi]
                    g = 4 * blk + oc
                    evac_copy(0, uT[:, g, t0:t0 + n], pap, [pb], [], wadd=[uT_b[g]])
                proj_fm(w_in[:, OFF_U + 512 * blk:OFF_U + 512 * blk + 512], 512, rhs_h, hT_b, ev, tbs=TBS if full else TBS[:4])

        P.phase(3)
        phase_a(xprev[0], False)
        phase_b(False)
        P.phase(2)
        oc_ = O_C
        pw = cv_(oc_ + 0, [9, 2, 32], F32); bb = cv_(oc_ + 2304, [2, 32, 16], F32); ccm = cv_(oc_ + 6400, [2, 32, 16], F32)
        R8 = cv_(oc_ + 10496, [16, 2, 32], F32); R128 = cv_(oc_ + 14592, [16, 2, 32], F32); A2k = cv_(oc_ + 18688, [3, 2, 32], F32)
        Hend = cv_(oc_ + 19456, [2, 32, 16], F32); carry = cv_(oc_ + 23552, [17, 2, 32], F32)
        Sb = cv_(oc_ + 27904, [8, 2, 128], BF16); coef = cv_(oc_ + 32000, [12, 32], F32)
        misc = cv_(oc_ + 33536, [32, 32], F32)
        t1 = cv_(oc_ + 37632, [1024], F32); t2 = cv_(oc_ + 41728, [1024], F32)
        O_S = oc_ + 45824
        Sslot = [cv_(O_S + 8192 * i, [8, 2, 128], F32) for i in range(2)]
        O_CA = oc_ + 62208; O_KL = oc_ + 71424; O_HB = oc_ + 75520
        b_pw = Buf("pw"); b_bb = Buf("bb"); b_ccm = Buf("ccm"); b_R8 = Buf(); b_R128 = Buf(); b_A2k = Buf(); b_coef = Buf()
        b_misc = Buf("misc"); b_t = Buf("t12"); b_Sslot = [Buf("S0"), Buf("S1")]
        craw = [cv_(O_S + 2048 * i, [8, 64], F32) for i in range(2)]
        lamraw = cv_(O_S + 4096, [128], F32); ldraw = cv_(O_S + 4608, [64], F32)
        draw = cv_(O_S + 4864, [128], F32); bgraw = cv_(O_S + 5376, [128], F32)
        braw = [cv_(O_S + 8192 + 2048 * i, [32, 16], F32) for i in range(2)]
        b_craw = Buf(); b_lam = Buf(); b_ld = Buf(); b_draw = Buf(); b_braw = Buf()
        misc_load(SP, craw[0], c_re.rearrange("(t r) p -> r t p", r=128), b_craw, wadd=True)
        misc_load(SP, craw[1], c_im.rearrange("(t r) p -> r t p", r=128), b_craw, wadd=True)
        misc_load(SP, lamraw[0:64, 0:64], lam_re, b_lam, wadd=True)
        misc_load(SP, lamraw[0:64, 64:128], lam_im, b_lam, wadd=True)
        misc_load(SP, ldraw, log_delta.rearrange("(o n) -> o n", o=1).to_broadcast([128, 64]), b_ld)
        misc_load(SP, draw[0:8, :], ssm_d.rearrange("(c p) -> c p", p=128), b_draw, wadd=True)
        misc_load(SP, bgraw[0:8, :], b_glu.rearrange("(c p) -> c p", p=128), b_draw, wadd=True)
        for i, src in enumerate([b_re, b_im]):
            for q4 in range(4):
                misc_load(SP, braw[i][:, 8 * q4:8 * q4 + 8, :],
                          src[1024 * q4:1024 * q4 + 1024, :].rearrange("(gp q) c -> q gp c", q=128), b_braw, wadd=True)
        M_ = lambda i: misc[:, i, :]
        LR, LI, DT, TH, FR, FC, MAG, SN, CS, NR, DEN, CR, CI, G1, KF, TMPA = [M_(i) for i in range(16)]
        KI = misc[:, 16, :].bitcast(mybir.dt.int32)
        bkl = bank()
        P.op(PE, lambda e: e.transpose(out=ps[bkl][:, 0:64], in_=lamraw[0:64, :], identity=ident[0:64, 0:64]),
             reads=[b_lam, b_ident], writes=[psb[bkl]])
        P.op(DVE, lambda e: e.tensor_copy(out=LR[0:64, :], in_=ps[bkl][0:64, 0:64:2]), reads=[psb[bkl]], wadd=[b_misc])
        P.op(DVE, lambda e: e.tensor_copy(out=LR[64:128, :], in_=ps[bkl][0:64, 1:64:2]), reads=[psb[bkl]], wadd=[b_misc])
        P.op(DVE, lambda e: e.tensor_copy(out=LI[0:64, :], in_=ps[bkl][64:128, 0:64:2]), reads=[psb[bkl]], wadd=[b_misc])
        P.op(DVE, lambda e: e.tensor_copy(out=LI[64:128, :], in_=ps[bkl][64:128, 1:64:2]), reads=[psb[bkl]], wadd=[b_misc])
        bkd = bank()
        P.op(PE, lambda e: e.transpose(out=ps[bkd][:, 0:8], in_=draw[0:8, :], identity=ident[0:8, 0:8]),
             reads=[b_draw, b_ident], writes=[psb[bkd]])
        P.op(PE, lambda e: e.transpose(out=ps[bkd][:, 8:16], in_=bgraw[0:8, :], identity=ident[0:8, 0:8]),
             reads=[b_draw, b_ident], wadd=[psb[bkd]])
        P.op(DVE, lambda e: e.tensor_copy(out=Dm, in_=ps[bkd][:, 0:8]), reads=[psb[bkd]], writes=[b_Dm])
        P.op(DVE, lambda e: e.tensor_copy(out=bglu, in_=ps[bkd][:, 8:16]), reads=[psb[bkd]], writes=[b_bglu])
        for ri in range(2):
            for hb in range(2):
                bkc = bank()
                for tt in range(4):
                    t_ = 4 * hb + tt
                    P.op(PE, lambda e, ri=ri, t_=t_, tt=tt, bkc=bkc: e.transpose(
                        out=ps[bkc][0:64, 128 * tt:128 * tt + 128], in_=craw[ri][:, t_, :], identity=ident),
                        reads=[b_craw, b_ident], writes=[psb[bkc]] if tt == 0 else [], wadd=[psb[bkc]] if tt > 0 else [])
                for g2 in range(2):
                    src = ps[bkc][0:64, :].rearrange("p (tg g2 c) -> p tg g2 c", g2=2, c=16)[:, :, g2, :]
                    P.op(DVE, lambda e, ri=ri, hb=hb, g2=g2, src=src: e.tensor_copy(
                        out=ccm[64 * g2:64 * g2 + 64, ri, 16 * hb:16 * hb + 16, :], in_=src),
                        reads=[psb[bkc]], wadd=[b_ccm])
        P.op(ACT, lambda e: e.activation(out=DT[0:64, :], in_=ldraw[0:64, 0:64:2], func=AF.Exp), reads=[b_ld], wadd=[b_misc])
        P.op(ACT, lambda e: e.activation(out=DT[64:128, :], in_=ldraw[64:128, 1:64:2], func=AF.Exp), reads=[b_ld], wadd=[b_misc])
        G = DVE
        tt_ = lambda out, a, b_, op, **kw: P.op(G, lambda e: e.tensor_tensor(out=out, in0=a, in1=b_, op=op), reads=[b_misc], wadd=[b_misc], **kw)
        ts_ = lambda out, a, s1, s2, o0, o1: P.op(G, lambda e: e.tensor_scalar(out=out, in0=a, scalar1=s1, scalar2=s2, op0=o0, op1=o1), reads=[b_misc], wadd=[b_misc])
        tt_(TH, LI, DT, ALU.mult)
        ts_(FR, TH, 1.0 / (2 * math.pi), 0.0, ALU.mult, ALU.add)
        P.op(DVE, lambda e: e.tensor_copy(out=KI, in_=FR), reads=[b_misc], wadd=[b_misc])
        P.op(DVE, lambda e: e.tensor_copy(out=KF, in_=KI), reads=[b_misc], wadd=[b_misc])
        tt_(FR, FR, KF, ALU.subtract)
        ts_(FC, FR, 1.0, 0.25, ALU.mult, ALU.add)
        P.op(DVE, lambda e: e.tensor_single_scalar(out=G1, in_=FC, scalar=0.5, op=ALU.is_gt), reads=[b_misc], wadd=[b_misc])
        tt_(FC, FC, G1, ALU.subtract)
        TWO_PI = 6.283185
        P.op(ACT, lambda e: e.activation(out=SN, in_=FR, func=AF.Sin, scale=TWO_PI), reads=[b_misc], wadd=[b_misc])
        P.op(ACT, lambda e: e.activation(out=CS, in_=FC, func=AF.Sin, scale=TWO_PI), reads=[b_misc], wadd=[b_misc])
        tt_(TMPA, LR, DT, ALU.mult)
        P.op(ACT, lambda e: e.activation(out=MAG, in_=TMPA, func=AF.Exp), reads=[b_misc], wadd=[b_misc])
        P.op(G, lambda e: e.memset(pw[:, 0, 0, :], 1.0), wadd=[b_pw])
        P.op(G, lambda e: e.memset(pw[:, 0, 1, :], 0.0), wadd=[b_pw])
        P.op(G, lambda e: e.tensor_tensor(out=pw[:, 1, 0, :], in0=MAG, in1=CS, op=ALU.mult), reads=[b_misc], wadd=[b_pw])
        P.op(G, lambda e: e.tensor_tensor(out=pw[:, 1, 1, :], in0=MAG, in1=SN, op=ALU.mult), reads=[b_misc], wadd=[b_pw])

        def cm(dst, x, y, n, rb, wb):
            T1 = t1[:, 0:n * 32].rearrange("p (n g) -> p n g", n=n)
            T2 = t2[:, 0:n * 32].rearrange("p (n g) -> p n g", n=n)
            cmul(G, dst[:, :, 0, :], dst[:, :, 1, :], x[:, :, 0, :], x[:, :, 1, :], y[:, :, 0, :], y[:, :, 1, :], T1, T2, rb, wb, b_t)

        def bc(ap1, n):
            return ap1.to_broadcast([128, n, 2, 32])

        cm(pw[:, 2:3], pw[:, 1:2], pw[:, 1:2], 1, [b_pw], [b_pw])
        cm(pw[:, 3:5], pw[:, 1:3], bc(pw[:, 2:3], 2), 2, [b_pw], [b_pw])
        cm(pw[:, 5:9], pw[:, 1:5], bc(pw[:, 4:5], 4), 4, [b_pw], [b_pw])
        tt_(NR, pw[:, 1, 0, :], pw[:, 0, 0, :], ALU.subtract, deps=b_pw.w)
        tt_(DEN, LR, LR, ALU.mult)
        tt_(TMPA, LI, LI, ALU.mult)
        tt_(DEN, DEN, TMPA, ALU.add)
        P.op(DVE, lambda e: e.reciprocal(out=DEN, in_=DEN), reads=[b_misc], wadd=[b_misc])
        tt_(CR, NR, LR, ALU.mult)
        tt_(TMPA, pw[:, 1, 1, :], LI, ALU.mult)
        tt_(CR, CR, TMPA, ALU.add)
        tt_(CR, CR, DEN, ALU.mult)
        tt_(CI, pw[:, 1, 1, :], LR, ALU.mult)
        tt_(TMPA, NR, LI, ALU.mult)
        tt_(CI, CI, TMPA, ALU.subtract)
        tt_(CI, CI, DEN, ALU.mult)
        CRb = CR.unsqueeze(2).to_broadcast([128, 32, 16]); CIb = CI.unsqueeze(2).to_broadcast([128, 32, 16])
        T1b = t1[:, 0:512].rearrange("p (g c) -> p g c", g=32); T2b = t2[:, 0:512].rearrange("p (g c) -> p g c", g=32)
        P.op(G, lambda e: e.tensor_tensor(out=T1b, in0=braw[0], in1=CRb, op=ALU.mult), reads=[b_braw, b_misc, b_pw], writes=[b_t])
        P.op(G, lambda e: e.tensor_tensor(out=T2b, in0=braw[1], in1=CIb, op=ALU.mult), reads=[b_braw, b_misc], wadd=[b_t])
        P.op(G, lambda e: e.tensor_tensor(out=bb[:, 0], in0=T1b, in1=T2b, op=ALU.subtract), reads=[b_t], wadd=[b_bb])
        P.op(G, lambda e: e.tensor_tensor(out=T1b, in0=braw[1], in1=CRb, op=ALU.mult), reads=[b_braw, b_misc, b_bb], writes=[b_t])
        P.op(G, lambda e: e.tensor_tensor(out=T2b, in0=braw[0], in1=CIb, op=ALU.mult), reads=[b_braw, b_misc], wadd=[b_t])
        P.op(G, lambda e: e.tensor_tensor(out=bb[:, 1], in0=T1b, in1=T2b, op=ALU.add), reads=[b_t], wadd=[b_bb])

        def rev_table(R, A0, bufR, out_last):
            AW = misc[:, 20:22, :].rearrange("p (o r) g -> p o r g", o=1)
            AW2 = misc[:, 22:24, :].rearrange("p (o r) g -> p o r g", o=1)
            P.op(G, lambda e: e.memset(R[:, 15, 0, :], 1.0), wadd=[bufR])
            P.op(G, lambda e: e.memset(R[:, 15, 1, :], 0.0), wadd=[bufR])
            P.op(G, lambda e: e.tensor_copy(out=AW, in_=A0), reads=[b_pw, b_coef, b_misc], wadd=[b_misc])
            w = 1
            cur, nxt = AW, AW2
            while w <= 8:
                cm(R[:, 16 - 2 * w:16 - w], R[:, 16 - w:16], bc(cur, w), w, [bufR, b_misc], [bufR])
                cm(nxt, cur, cur, 1, [b_misc], [b_misc])
                cur, nxt = nxt, cur
                w *= 2
            P.op(G, lambda e: e.tensor_copy(out=out_last, in_=cur), reads=[b_misc], wadd=[b_coef])

        A128 = coef[:, 0:2, :].rearrange("p (o r) g -> p o r g", o=1)
        A2048 = coef[:, 2:4, :].rearrange("p (o r) g -> p o r g", o=1)
        rev_table(R8, pw[:, 8:9], b_R8, A128)
        rev_table(R128, A128, b_R128, A2048)
        P.op(G, lambda e: e.memset(A2k[:, 0, 0, :], 1.0), wadd=[b_A2k])
        P.op(G, lambda e: e.memset(A2k[:, 0, 1, :], 0.0), wadd=[b_A2k])
        P.op(G, lambda e: e.tensor_copy(out=A2k[:, 1:2], in_=A2048), reads=[b_coef], wadd=[b_A2k])
        cm(A2k[:, 2:3], A2048, A2048, 1, [b_coef], [b_A2k])
        P.op(G, lambda e: e.tensor_scalar(out=coef[:, 4, :], in0=pw[:, 8, 1, :], scalar1=-1.0, scalar2=0.0, op0=ALU.mult, op1=ALU.add),
             reads=[b_pw], wadd=[b_coef])
        P.op(G, lambda e: e.tensor_scalar(out=coef[:, 5, :], in0=coef[:, 1, :], scalar1=-1.0, scalar2=0.0, op0=ALU.mult, op1=ALU.add),
             reads=[b_coef], wadd=[b_coef])

        P.phase(5)
        WinL = cv_(O_B, [8, 8, 2, 128], BF16)
        b_WinL = [Buf(f"WinL{g}") for g in range(8)]
        b_Sb = Buf("Sb")
        handoff(b_Sslot, [b_craw, b_lam, b_ld, b_draw, b_braw])
        for i in range(2):
            P.op(POOL, lambda e, i=i: e.memset(Sslot[i].rearrange("p k r c -> p (k r c)"), 0.0), writes=[b_Sslot[i]])
        T1s = t1[:, 0:512].rearrange("p (k m c) -> p k m c", k=8, m=4)
        T2s = t2[:, 0:512].rearrange("p (k m c) -> p k m c", k=8, m=4)
        for gc in range(8):
            sl = gc % 2
            S = Sslot[sl]
            Sv = S.rearrange("p k r (m g c) -> p k r m g c", m=4, g=2)
            prk = pw[:, 0:8, 0, 4 * gc:4 * gc + 4].unsqueeze(3).to_broadcast([128, 8, 4, 16])
            pik = pw[:, 0:8, 1, 4 * gc:4 * gc + 4].unsqueeze(3).to_broadcast([128, 8, 4, 16])
            bbr = bb[:, 0, 4 * gc:4 * gc + 4, :].unsqueeze(1).to_broadcast([128, 8, 4, 16])
            bbi = bb[:, 1, 4 * gc:4 * gc + 4, :].unsqueeze(1).to_broadcast([128, 8, 4, 16])
            for ri in range(2):
                x1, x2 = (bbr, bbi) if ri == 0 else (bbi, bbr)
                op = ALU.subtract if ri == 0 else ALU.add
                P.op(POOL, lambda e, x1=x1, prk=prk: e.tensor_tensor(out=T1s, in0=prk, in1=x1, op=ALU.mult), reads=[b_pw, b_bb], writes=[b_t])
                P.op(POOL, lambda e, x2=x2, pik=pik: e.tensor_tensor(out=T2s, in0=pik, in1=x2, op=ALU.mult), reads=[b_pw, b_bb], wadd=[b_t])
                for g2 in range(2):
                    lo, hi = 64 * g2, 64 * g2 + 64
                    P.op(POOL, lambda e, ri=ri, g2=g2, lo=lo, hi=hi, op=op, Sv=Sv: e.tensor_tensor(
                        out=Sv[lo:hi, :, ri, :, g2, :], in0=T1s[lo:hi], in1=T2s[lo:hi], op=op),
                        reads=[b_t], wadd=[b_Sslot[sl]])
            P.op(POOL, lambda e, gc=gc, S=S: e.tensor_copy(out=Sb[:, gc], in_=S[:, 0]), reads=[b_Sslot[sl]], wadd=[b_Sb])
            for q4 in range(4):
                bkw = bank()
                for j in range(4):
                    k_, ri_ = (4 * q4 + j) // 2, (4 * q4 + j) % 2
                    P.op(PE, lambda e, bkw=bkw, j=j, k_=k_, ri_=ri_, S=S: e.transpose(
                        out=ps[bkw][:, 128 * j:128 * j + 128], in_=S[:, k_, ri_, :], identity=ident),
                        reads=[b_Sslot[sl], b_ident], writes=[psb[bkw]] if j == 0 else [], wadd=[psb[bkw]] if j > 0 else [])
                dstw = WinL[:, gc, 2 * q4:2 * q4 + 2].rearrange("p k r c -> p (k r c)")
                evac_copy(q4, dstw, ps[bkw][:, 0:512], [psb[bkw]], [], wadd=[b_WinL[gc]])

        t1p = cv_(O_S, [2, 16, 16], F32); t2p = cv_(O_S + 2048, [2, 16, 16], F32); cbp = cv_(O_S + 4096, [2, 16, 16], F32)
        b_p1 = Buf("p1tmp")
        handoff([b_p1], b_Sslot)
        b_Hend = Buf("Hend")

        def x_matmuls(gc, banks):
            for ri in range(2):
                for s_ in range(8):
                    for m in range(4):
                        first = (ri == 0 and s_ == 0)
                        last = (ri == 1 and s_ == 7)
                        bkx = banks[m]
                        P.op(PE, lambda e, m=m, ri=ri, s_=s_, bkx=bkx, gc=gc: e.matmul(
                            ps[bkx][:, 256 * ri:256 * ri + 256], lhsT=WinL[32 * m:32 * m + 32, gc, 7 - s_, ri, :],
                            rhs=uT[32 * m:32 * m + 32, gc, s_:NP:8], start=(s_ == 0), stop=(s_ == 7),
                            tile_position=(32 * m, 0)),
                            reads=[b_WinL[gc], uT_b[gc]] if first else [], writes=[psb[bkx]] if first else [], sig=last)
                        if last and not P.dead:
                            P.attach(Dep(P.esem[PE], P.cnt[PE]), reads=[b_WinL[gc], uT_b[gc]], writes=[psb[bkx]])

        def seg_reduce(src_ap, Rtab, gp0, ngp, out_ap, rbufs, wbuf):
            raise NotImplementedError

        def pass1():
            for gc in range(8):
                banks = [bank() for _ in range(4)]
                x_matmuls(gc, banks)
                if DEBUG and gc == 0 and os.environ.get('KX0'):
                    xdbg = cv_(O_S + 6144, [512], F32); b_xdbg = Buf()
                    P.op(DVE, lambda e: e.tensor_copy(out=xdbg, in_=ps[banks[1]][:, :]), reads=[psb[banks[1]]], writes=[b_xdbg])
                    P.dma(SP, dout_sem, dbg["X0"], xdbg, reads=[b_xdbg])
                for m in range(4):
                    gp = 4 * gc + m
                    X4 = ps[banks[m]][:, :].rearrange("p (r s i) -> p r s i", r=2, s=16)
                    Pr = R8[:, :, 0, gp].unsqueeze(1).unsqueeze(1).to_broadcast([128, 2, 16, 16])
                    Pi = R8[:, :, 1, gp].unsqueeze(1).unsqueeze(1).to_broadcast([128, 2, 16, 16])
                    P.op(DVE, lambda e, X4=X4, Pr=Pr: e.tensor_tensor(out=t1p, in0=X4, in1=Pr, op=ALU.mult),
                         reads=[psb[banks[m]], b_R8], writes=[b_p1])
                    P.op(DVE, lambda e, X4=X4, Pi=Pi: e.tensor_tensor(out=t2p, in0=X4, in1=Pi, op=ALU.mult),
                         reads=[psb[banks[m]], b_R8], wadd=[b_p1])
                    P.op(DVE, lambda e: e.tensor_tensor(out=cbp[:, 0], in0=t1p[:, 0], in1=t2p[:, 1], op=ALU.subtract), reads=[b_p1], wadd=[b_p1])
                    P.op(DVE, lambda e: e.tensor_tensor(out=cbp[:, 1], in0=t2p[:, 0], in1=t1p[:, 1], op=ALU.add), reads=[b_p1], wadd=[b_p1])
                    P.op(DVE, lambda e, gp=gp: e.tensor_reduce(out=Hend[:, :, gp, :], in_=cbp, axis=AX.X, op=ALU.add),
                         reads=[b_p1], wadd=[b_Hend])


        Ecore = cv_(O_S + 6144, [2, 32], F32); Eall = cv_(O_S + 6400, [8, 64], F32)
        te1 = cv_(O_S + 0, [2, 32, 16], F32); te2 = cv_(O_S + 8448, [2, 32, 16], F32)
        b_E = Buf("Ecore"); b_Eall = Buf("Eall"); b_te = b_p1
        handoff([b_E, b_Eall], b_Sslot)
        def ecore(j):
            Qr = R128[:, :, 0, :].rearrange("p i g -> p g i").unsqueeze(1).to_broadcast([128, 2, 32, 16])
            Qi = R128[:, :, 1, :].rearrange("p i g -> p g i").unsqueeze(1).to_broadcast([128, 2, 32, 16])
            P.op(DVE, lambda e: e.tensor_tensor(out=te1, in0=Hend, in1=Qr, op=ALU.mult), reads=[b_Hend, b_R128], writes=[b_te])
            P.op(DVE, lambda e: e.tensor_tensor(out=te2, in0=Hend, in1=Qi, op=ALU.mult), reads=[b_Hend, b_R128], wadd=[b_te])
            P.op(DVE, lambda e: e.tensor_tensor(out=te1[:, 0], in0=te1[:, 0], in1=te2[:, 1], op=ALU.subtract), reads=[b_te], writes=[b_te])
            P.op(DVE, lambda e: e.tensor_tensor(out=te2[:, 0], in0=te2[:, 0], in1=te1[:, 1], op=ALU.add), reads=[b_te], writes=[b_te])
            P.op(DVE, lambda e: e.tensor_reduce(out=Ecore[:, 0, :], in_=te1[:, 0], axis=AX.X, op=ALU.add), reads=[b_te], wadd=[b_E])
            P.op(DVE, lambda e: e.tensor_reduce(out=Ecore[:, 1, :], in_=te2[:, 0], axis=AX.X, op=ALU.add), reads=[b_te], wadd=[b_E])

            if j is not None:
                P.op(DVE, lambda e, j=j: e.tensor_copy(out=Eall[:, j, :], in_=Ecore.rearrange("p r g -> p (r g)")), reads=[b_E], wadd=[b_Eall])

        P.phase(3)
        srcs = [(xprev[1], False), (xprev[2], False), (xp, True)]
        phase_a(*srcs[0])
        for j in range(3):
            pass1()
            ecore(j)
            phase_b(srcs[j][1])
            if j < 2:
                phase_a(*srcs[j + 1])
        P.phase(6)
        pass1()
        P.phase(7)
        Sn = [cv_(O_S + 12544 + 256 * n, [2, 32], F32) for n in range(3)]
        b_Sn = Buf("Sn")
        handoff([b_Sn], b_Sslot)
        for n in range(3):
            Snf = Sn[n].rearrange("p r g -> p (r g)")
            P.op(DVE, lambda e, n=n, Snf=Snf: e.tensor_scalar(out=Snf, in0=Eall[:, n, :], scalar1=flags[:, 1 + n:2 + n], scalar2=None,
                                                            op0=ALU.mult), reads=[b_Eall, b_flags], wadd=[b_Sn])
        b_carry = Buf("carry")
        tq1 = cv_(O_S + 13312, [2, 32], F32); tq2 = cv_(O_S + 13568, [2, 32], F32)
        b_tq = Buf("tq")
        handoff([b_tq], b_Sslot)

        def cmul_small(dst, x, y, rb, wb):
            cmul(DVE, dst[:, 0, :], dst[:, 1, :], x[:, 0, :], x[:, 1, :], y[:, 0, :], y[:, 1, :], tq1[:, 0, :], tq1[:, 1, :], rb, wb, b_tq)

        cmul_small(carry[:, 1], Sn[1], A2k[:, 1], [b_Sn, b_A2k], [b_carry])
        cmul_small(carry[:, 2], Sn[2], A2k[:, 2], [b_Sn, b_A2k, b_carry], [b_carry])
        P.op(DVE, lambda e: e.tensor_tensor(out=Sn[0], in0=Sn[0], in1=carry[:, 1], op=ALU.add), reads=[b_Sn, b_carry], writes=[b_Sn])
        P.op(DVE, lambda e: e.tensor_tensor(out=carry[:, 0], in0=Sn[0], in1=carry[:, 2], op=ALU.add), reads=[b_Sn, b_carry], writes=[b_carry])
        A128v = coef[:, 0:2, :]
        for sg in range(16):
            cmul_small(carry[:, sg + 1], carry[:, sg], A128v, [b_carry, b_coef], [b_carry])
            P.op(DVE, lambda e, sg=sg: e.tensor_tensor(out=carry[:, sg + 1], in0=carry[:, sg + 1], in1=Hend[:, :, :, sg], op=ALU.add),
                 reads=[b_carry, b_Hend], writes=[b_carry])
        pstT = cv_(O_S + 13824, [2, 128], F32)
        b_pst = Buf()
        handoff([b_pst], b_Sslot)
        bkp = bank()
        for ri in range(2):
            P.op(PE, lambda e, ri=ri: e.transpose(out=ps[bkp][0:32, 128 * ri:128 * ri + 128], in_=carry[:, 16, ri, :], identity=ident),
                 reads=[b_carry, b_ident], writes=[psb[bkp]] if ri == 0 else [], wadd=[psb[bkp]] if ri == 1 else [])
        P.op(DVE, lambda e: e.tensor_copy(out=pstT[0:32].rearrange("p r c -> p (r c)"), in_=ps[bkp][0:32, 0:256]), reads=[psb[bkp]], writes=[b_pst])
        P.dma(SP, dout_sem, pst_re, pstT[0:32, 0, :], reads=[b_pst])
        P.dma(SP, dout_sem, pst_im, pstT[0:32, 1, :], reads=[b_pst])

        P.phase(8)
        CaBD = [cv_(O_CA + 4608 * i, [4, 9, 2, 32], BF16) for i in range(2)]
        KLs = [cv_(O_KL + 2048 * i, [8, 128], BF16) for i in range(2)]
        Hb = [cv_(O_HB + 4096 * i, [2, 4, 256], BF16) for i in range(2)]
        Xs = [cv_(O_S + 8192 * i, [2, 4, 256], F32) for i in range(2)]
        b_CaBD = [Buf("Ca0"), Buf("Ca1")]; b_KL = [Buf("KL0"), Buf("KL1")]; b_Hb = [Buf("Hb0"), Buf("Hb1")]; b_Xs = [Buf("Xs0"), Buf("Xs1")]
        old_c = [b_ct, b_csg, b_ccT, b_bin, b_baT, xst_b[0], xst_b[1]]
        handoff(b_CaBD + b_KL + b_Hb, old_c)
        handoff(b_Xs, [b_p1, b_E, b_Eall, b_te, b_Sn, b_tq, b_pst] + b_Sslot)
        KL0all = cv_(O_C + 10496, [8, 128], BF16); Ca1all = cv_(O_C + 14592, [32, 2, 32], BF16)
        b_KL0 = Buf("KL0all"); b_Ca1 = Buf("Ca1all")
        handoff([b_KL0], [b_R8]); handoff([b_Ca1], [b_R128])
        Q1e = [misc[:, 24:28, :].rearrange("p a g -> p (a g)").rearrange("p (r m s) -> p r m s", r=2, m=4),
               misc[:, 0:4, :].rearrange("p a g -> p (a g)").rearrange("p (r m s) -> p r m s", r=2, m=4)]
        Q2e = [misc[:, 28:32, :].rearrange("p a g -> p (a g)").rearrange("p (r m s) -> p r m s", r=2, m=4),
               misc[:, 4:8, :].rearrange("p a g -> p (a g)").rearrange("p (r m s) -> p r m s", r=2, m=4)]
        tmpK = misc[:, 16:20, :].rearrange("p a g -> p (a g)")
        b_Q = [Buf("Qdve"), Buf("Qpool")]
        b_tmpK = Buf("tmpK")
        handoff([b_tmpK] + b_Q, [b_misc])
        for i in range(2):
            P.op(POOL, lambda e, i=i: e.memset(CaBD[i].rearrange("p m n r c -> p (m n r c)"), 0.0), writes=[b_CaBD[i]])
        U1 = t1[:, 0:576].rearrange("p (m n c) -> p m n c", m=4, n=9)
        U2 = t2[:, 0:576].rearrange("p (m n c) -> p m n c", m=4, n=9)
        XB = [2, 3, 4, 5]
        YB = [6, 7]

        def emit_consts(gc):
            sl = gc % 2
            Ca = CaBD[sl]
            cre = ccm[:, 0, 4 * gc:4 * gc + 4, :].unsqueeze(2).to_broadcast([128, 4, 9, 16])
            cim = ccm[:, 1, 4 * gc:4 * gc + 4, :].unsqueeze(2).to_broadcast([128, 4, 9, 16])
            pr = pw[:, :, 0, 4 * gc:4 * gc + 4].rearrange("p n m -> p m n").unsqueeze(3).to_broadcast([128, 4, 9, 16])
            pi = pw[:, :, 1, 4 * gc:4 * gc + 4].rearrange("p n m -> p m n").unsqueeze(3).to_broadcast([128, 4, 9, 16])
            P.op(DVE, lambda e: e.tensor_tensor(out=U1, in0=cre, in1=pr, op=ALU.mult), reads=[b_ccm, b_pw], writes=[b_t])
            P.op(DVE, lambda e: e.tensor_tensor(out=U2, in0=cim, in1=pi, op=ALU.mult), reads=[b_ccm, b_pw], wadd=[b_t])
            for g2 in range(2):
                lo, hi = 64 * g2, 64 * g2 + 64
                P.op(DVE, lambda e, lo=lo, hi=hi, g2=g2: e.tensor_tensor(
                    out=Ca[lo:hi, :, :, 0, 16 * g2:16 * g2 + 16], in0=U1[lo:hi], in1=U2[lo:hi], op=ALU.subtract),
                    reads=[b_t], wadd=[b_CaBD[sl]])
            P.op(DVE, lambda e: e.tensor_tensor(out=U1, in0=cre, in1=pi, op=ALU.mult), reads=[b_ccm, b_pw], writes=[b_t])
            P.op(DVE, lambda e: e.tensor_tensor(out=U2, in0=cim, in1=pr, op=ALU.mult), reads=[b_ccm, b_pw], wadd=[b_t])
            P.op(DVE, lambda e: e.tensor_tensor(out=U1, in0=U1, in1=U2, op=ALU.add), reads=[b_t], writes=[b_t])
            for g2 in range(2):
                lo, hi = 64 * g2, 64 * g2 + 64
                P.op(DVE, lambda e, lo=lo, hi=hi, g2=g2: e.tensor_scalar(
                    out=Ca[lo:hi, :, :, 1, 16 * g2:16 * g2 + 16], in0=U1[lo:hi], scalar1=-1.0, scalar2=0.0, op0=ALU.mult, op1=ALU.add),
                    reads=[b_t], wadd=[b_CaBD[sl]])
            P.op(DVE, lambda e: e.tensor_copy(out=Ca1all[:, 4 * gc:4 * gc + 4], in_=Ca[:, :, 1, :, :]), reads=[b_CaBD[sl]], wadd=[b_Ca1])
            for hb in range(2):
                for tt in range(4):
                    tau = 4 * hb + tt
                    for ri in range(2):
                        P.op(PE, lambda e, hb=hb, tt=tt, tau=tau, ri=ri: e.matmul(
                            ps[hb][:, 128 * tt:128 * tt + 128], lhsT=Sb[:, gc, ri, :], rhs=Ca[:, :, tau, ri, :],
                            start=(ri == 0), stop=(ri == 1)),
                            reads=[b_Sb, b_CaBD[sl]] if (tt == 0 and ri == 0) else [],
                            writes=[psb[hb]] if (tt == 0 and ri == 0) else [], sig=(tt == 3 and ri == 1))
                if not P.dead:
                    P.attach(Dep(P.esem[PE], P.cnt[PE]), reads=[b_Sb, b_CaBD[sl]], writes=[psb[hb]])
            KL = KLs[sl]
            bmb3 = bmask.unsqueeze(1).to_broadcast([128, 3, 128]); bmb4 = bmask.unsqueeze(1).to_broadcast([128, 4, 128])
            P.op(DVE, lambda e: e.tensor_tensor(out=KL[:, 1:4, :], in0=ps[0][:, 128:512].rearrange("p (t c) -> p t c", t=3), in1=bmb3, op=ALU.mult),
                 reads=[psb[0], b_bmask], wadd=[b_KL[sl]])
            P.op(DVE, lambda e: e.tensor_tensor(out=tmpK, in0=ps[0][:, 0:128], in1=bmask, op=ALU.mult), reads=[psb[0], b_bmask], writes=[b_tmpK])
            P.op(DVE, lambda e: e.tensor_tensor(out=KL[:, 4:8, :], in0=ps[1][:, 0:512].rearrange("p (t c) -> p t c", t=4), in1=bmb4, op=ALU.mult),
                 reads=[psb[1], b_bmask], wadd=[b_KL[sl]])
            P.op(DVE, lambda e: e.scalar_tensor_tensor(out=KL[:, 0, :], in0=ident, scalar=Dm[:, gc:gc + 1], in1=tmpK, op0=ALU.mult, op1=ALU.add),
                 reads=[b_tmpK, b_ident, b_Dm], wadd=[b_KL[sl]])
            P.op(DVE, lambda e: e.tensor_copy(out=KL0all[:, gc, :], in_=KL[:, 0, :]), reads=[b_KL[sl]], wadd=[b_KL0])

        def emit_x_scan(gc):
            sl = gc % 2
            x_matmuls(gc, XB)
            X = Xs[sl]
            for m in range(4):
                P.op(ACT, lambda e, m=m: e.activation(out=X[:, :, m, :], in_=ps[XB[m]][:, :].rearrange("p (r j) -> p r j", r=2), func=AF.Copy),
                     reads=[psb[XB[m]]], wadd=[b_Xs[sl]])
            E, qi = (POOL, 1) if gc in (1, 4, 6) else (DVE, 0)
            Q1 = Q1e[qi]; Q2 = Q2e[qi]
            X5 = X.rearrange("p r m (s i) -> p r m s i", i=16)
            Ar = pw[:, 8, 0, 4 * gc:4 * gc + 4].unsqueeze(1).unsqueeze(3).to_broadcast([128, 2, 4, 16])
            Ai = pw[:, 8, 1, 4 * gc:4 * gc + 4].unsqueeze(2).to_broadcast([128, 4, 16])
            AiN = coef[:, 4, 4 * gc:4 * gc + 4].unsqueeze(2).to_broadcast([128, 4, 16])
            cview = carry[:, 0:16, :, 4 * gc:4 * gc + 4].rearrange("p s r m -> p r m s")
            for i in range(16):
                prev = cview if i == 0 else X5[:, :, :, :, i - 1]
                cur = X5[:, :, :, :, i]
                rb = [b_carry, b_pw, b_coef, b_Xs[sl]]
                P.op(E, lambda e, prev=prev: e.tensor_tensor(out=Q1, in0=prev, in1=Ar, op=ALU.mult), reads=rb, writes=[b_Q[qi]])
                P.op(E, lambda e, prev=prev: e.tensor_tensor(out=Q2[:, 0], in0=prev[:, 1], in1=AiN, op=ALU.mult), reads=rb, wadd=[b_Q[qi]])
                P.op(E, lambda e, prev=prev: e.tensor_tensor(out=Q2[:, 1], in0=prev[:, 0], in1=Ai, op=ALU.mult), reads=rb, wadd=[b_Q[qi]])
                P.op(E, lambda e, cur=cur: e.tensor_tensor(out=cur, in0=cur, in1=Q1, op=ALU.add), reads=[b_Q[qi]], writes=[b_Xs[sl]])
                P.op(E, lambda e, cur=cur: e.tensor_tensor(out=cur, in0=cur, in1=Q2, op=ALU.add), reads=[b_Q[qi]], writes=[b_Xs[sl]])
            H5 = Hb[sl].rearrange("p r m (s i) -> p r m s i", i=16)
            P.op(ACT, lambda e: e.activation(out=H5[:, :, :, :, 1:16].rearrange("p r m s i -> p (r m) s i"),
                                             in_=X5[:, :, :, :, 0:15].rearrange("p r m s i -> p (r m) s i"), func=AF.Copy),
                 reads=[b_Xs[sl]], writes=[b_Hb[sl]])
            P.op(ACT, lambda e: e.activation(out=H5[:, :, :, :, 0], in_=cview, func=AF.Copy), reads=[b_carry], wadd=[b_Hb[sl]])

        def emit_y(gc):
            sl = gc % 2
            KL = KLs[sl]; Ca = CaBD[sl]
            uview = uT[:, gc, 0:NP].rearrange("p (j s) -> p s j", s=8)
            for half in (1, 0):
                for tl in range(4):
                    t_lo = 4 * half + tl
                    bk_ = YB[tl // 2]
                    reg = ps[bk_][:, 256 * (tl % 2):256 * (tl % 2) + 256]
                    n_mm = (t_lo + 1) + 8
                    idx = 0
                    for s_ in range(t_lo + 1):
                        P.op(PE, lambda e, reg=reg, s_=s_, t_lo=t_lo: e.matmul(
                            reg, lhsT=KL[:, t_lo - s_, :], rhs=uview[:, s_, :], start=(s_ == 0), stop=False),
                            reads=[b_KL[sl], uT_b[gc], b_CaBD[sl], b_Hb[sl]] if idx == 0 else [],
                            writes=[psb[bk_]] if (idx == 0 and tl % 2 == 0) else [], sig=False)
                        idx += 1
                    for m in range(4):
                        for ri in range(2):
                            lastmm = (m == 3 and ri == 1)
                            P.op(PE, lambda e, reg=reg, m=m, ri=ri, t_lo=t_lo, lastmm=lastmm: e.matmul(
                                reg[32 * m:32 * m + 32, :], lhsT=Ca[:, m, t_lo + 1, ri, :], rhs=Hb[sl][:, ri, m, :],
                                start=False, stop=lastmm, tile_position=(0, 32 * m)), sig=lastmm)
                    if not P.dead:
                        P.attach(Dep(P.esem[PE], P.cnt[PE]), reads=[b_KL[sl], uT_b[gc], b_CaBD[sl], b_Hb[sl]],
                                 writes=[psb[bk_]] if tl % 2 == 1 else [], wadd=[psb[bk_]] if tl % 2 == 0 else [])
                for bi in range(2):
                    t0_ = 4 * half + 2 * bi
                    P.op(ACT, lambda e, bi=bi, t0_=t0_: e.activation(
                        out=uview[:, t0_:t0_ + 2, :], in_=ps[YB[bi]][:, :].rearrange("p (t j) -> p t j", t=2), func=AF.Gelu_apprx_tanh),
                        reads=[psb[YB[bi]]], writes=[uT_b[gc]])

        for step in range(9):
            if step < 8:
                emit_consts(step)
                emit_x_scan(step)
            if step >= 1:
                emit_y(step - 1)
        yT = uT
        yT_b = uT_b

        P.phase(9)
        all_ssm_tmp = b_Xs + b_Hb + b_Q + [b_tmpK, b_t, b_p1, b_E, b_Eall, b_te, b_Sn, b_tq, b_pst] + b_Sslot
        stile = [cv_(O_S + 2048 * i, [512], F32) for i in range(2)]
        Hsp = cv_(O_S + 4096, [2, 32, 16], F32); Hn = cv_(O_S + 8192, [2, 32, 16], F32)
        HbS = cv_(O_S + 12288, [2, 32, 16], BF16); Q1s = cv_(O_HB, [2, 32, 16], F32); Q2s = cv_(O_HB + 4096, [2, 32, 16], F32)
        b_stile = Buf(); b_Hsp = Buf(); b_Hn = Buf(); b_HbS = Buf(); b_Qs = Buf()
        handoff([b_stile, b_Hsp, b_Hn, b_HbS, b_Qs], all_ssm_tmp)
        misc_load(SP, stile[0], st_re.rearrange("s (gh f) -> (s gh) f", gh=8), b_stile, wadd=True)
        misc_load(SP, stile[1], st_im.rearrange("s (gh f) -> (s gh) f", gh=8), b_stile, wadd=True)
        for ri in range(2):
            bks = bank()
            for q4 in range(4):
                P.op(PE, lambda e, ri=ri, q4=q4, bks=bks: e.transpose(out=ps[bks][:, 128 * q4:128 * q4 + 128],
                                                                   in_=stile[ri][:, 128 * q4:128 * q4 + 128], identity=ident),
                     reads=[b_stile, b_ident], writes=[psb[bks]] if q4 == 0 else [], wadd=[psb[bks]] if q4 > 0 else [])
            for q4 in range(4):
                P.op(DVE, lambda e, ri=ri, q4=q4, bks=bks: e.tensor_copy(
                    out=Hsp[:, ri, q4:32:4, :], in_=ps[bks][:, 128 * q4:128 * q4 + 128].rearrange("p (s gh) -> p gh s", gh=8)),
                    reads=[psb[bks]], wadd=[b_Hsp])
        P.op(POOL, lambda e: e.tensor_copy(out=HbS, in_=Hsp), reads=[b_Hsp], writes=[b_HbS])
        xsb = [bank() for _ in range(4)]
        for m in range(4):
            for gc in range(8):
                for ri in range(2):
                    first = (gc == 0 and ri == 0); last = (gc == 7 and ri == 1)
                    P.op(PE, lambda e, m=m, gc=gc, ri=ri: e.matmul(
                        ps[xsb[m]][:, 32 * gc + 16 * ri:32 * gc + 16 * ri + 16], lhsT=WinL[32 * m:32 * m + 32, gc, 0, ri, :],
                        rhs=uT[32 * m:32 * m + 32, gc, NP:NT], start=True, stop=True, tile_position=(32 * m, 0)),
                        reads=b_WinL + uT_b if first else [], writes=[psb[xsb[m]]] if first else [], sig=last)
            if not P.dead:
                P.attach(Dep(P.esem[PE], P.cnt[PE]), reads=b_WinL + uT_b, writes=[psb[xsb[m]]])
        Ar1 = pw[:, 1, 0, :].unsqueeze(1).unsqueeze(3).to_broadcast([128, 2, 32, 16])
        Ai1 = pw[:, 1, 1, :].unsqueeze(2).to_broadcast([128, 32, 16])
        P.op(POOL, lambda e: e.tensor_scalar(out=coef[:, 6, :], in0=pw[:, 1, 1, :], scalar1=-1.0, scalar2=0.0, op0=ALU.mult, op1=ALU.add),
             reads=[b_pw], wadd=[b_coef])
        AiN1 = coef[:, 6, :].unsqueeze(2).to_broadcast([128, 32, 16])
        P.op(DVE, lambda e: e.tensor_tensor(out=Q1s, in0=Hsp, in1=Ar1, op=ALU.mult), reads=[b_Hsp, b_pw], writes=[b_Qs])
        P.op(DVE, lambda e: e.tensor_tensor(out=Q2s[:, 0], in0=Hsp[:, 1], in1=AiN1, op=ALU.mult), reads=[b_Hsp, b_coef], wadd=[b_Qs])
        P.op(DVE, lambda e: e.tensor_tensor(out=Q2s[:, 1], in0=Hsp[:, 0], in1=Ai1, op=ALU.mult), reads=[b_Hsp, b_pw], wadd=[b_Qs])
        P.op(DVE, lambda e: e.tensor_tensor(out=Hn, in0=Q1s, in1=Q2s, op=ALU.add), reads=[b_Qs], writes=[b_Hn])
        for m in range(4):
            P.op(DVE, lambda e, m=m: e.tensor_tensor(
                out=Hn[:, :, m:32:4, :], in0=ps[xsb[m]][:, 0:256].rearrange("p (gc r s) -> p r gc s", gc=8, r=2),
                in1=Hn[:, :, m:32:4, :], op=ALU.add), reads=[psb[xsb[m]], b_Hn], writes=[b_Hn])
        stg = cv_(O_HB + 8192 - 8192, [4, 128], F32)
        sout = [cv_(O_S + 2048 * i, [512], F32) for i in range(2)]
        b_stg = Buf(); b_sout = Buf()
        handoff([b_stg], [b_Qs]); handoff([b_sout], [b_stile])
        for ri in range(2):
            for q4 in range(4):
                P.op(POOL, lambda e, ri=ri, q4=q4: e.tensor_copy(out=stg[:, q4, :].rearrange("p (s gh) -> p gh s", gh=8),
                                                               in_=Hn[:, ri, q4:32:4, :]), reads=[b_Hn], writes=[b_stg] if q4 == 0 else [],
                     wadd=[b_stg] if q4 > 0 else [])
            bks = bank()
            for q4 in range(4):
                P.op(PE, lambda e, q4=q4, bks=bks: e.transpose(out=ps[bks][:, 128 * q4:128 * q4 + 128], in_=stg[:, q4, :], identity=ident),
                     reads=[b_stg, b_ident], writes=[psb[bks]] if q4 == 0 else [], wadd=[psb[bks]] if q4 > 0 else [])
            P.op(DVE, lambda e, ri=ri, bks=bks: e.tensor_copy(out=sout[ri], in_=ps[bks][:, 0:512]), reads=[psb[bks]], wadd=[b_sout])
            P.dma(SP, dout_sem, (sst_re if ri == 0 else sst_im).rearrange("s (gh f) -> (s gh) f", gh=8), sout[ri], reads=[b_sout])
        bky = bank()
        for gc in range(8):
            reg = ps[bky][:, 16 * gc:16 * gc + 16]
            P.op(PE, lambda e, gc=gc, reg=reg: e.matmul(reg, lhsT=KL0all[:, gc, :], rhs=uT[:, gc, NP:NT], start=True, stop=False),
                 reads=[b_KL0, b_Ca1, b_HbS] + uT_b if gc == 0 else [], writes=[psb[bky]] if gc == 0 else [], sig=False)
            for m in range(4):
                for ri in range(2):
                    lastmm = (m == 3 and ri == 1)
                    P.op(PE, lambda e, gc=gc, reg=reg, m=m, ri=ri, lastmm=lastmm: e.matmul(
                        reg[32 * m:32 * m + 32, :], lhsT=Ca1all[:, 4 * gc + m, ri, :], rhs=HbS[:, ri, 4 * gc + m, :],
                        start=False, stop=lastmm, tile_position=(0, 32 * m)), sig=(lastmm and gc == 7))
        if not P.dead:
            P.attach(Dep(P.esem[PE], P.cnt[PE]), reads=[b_KL0, b_Ca1, b_HbS] + uT_b, writes=[psb[bky]])
        P.op(ACT, lambda e: e.activation(out=uT[:, :, NP:NT], in_=ps[bky][:, 0:128].rearrange("p (g s) -> p g s", g=8),
                                         func=AF.Gelu_apprx_tanh), reads=[psb[bky]], writes=uT_b)

        P.phase(10)
        s2T = RB
        s2_b = [Buf(f"s2_{g}") for g in range(8)]
        handoff(s2_b, b_WinL)
        gtmp = [cv_(O_C + 1024 * i, [512], BF16) for i in range(4)]
        ftmp = [cv_(O_C + 4096 + 2048 * i, [512], F32) for i in range(2)]
        b_gtmp = [Buf() for _ in range(4)]; b_ftmp = [Buf(), Buf()]
        handoff(b_gtmp + b_ftmp, [b_pw, b_bb, b_ccm])
        rhs_y = lambda k, t0, n: yT[:, k, t0:t0 + n]
        yall_b = [yT_b] * 5
        ctr = {"i": 0}

        class AllOf:
            pass
        for blk in range(2):
            def ev_glu(oc, tbi, pap, pb, blk=blk):
                t0, n = TBS[tbi]
                g = 4 * blk + oc
                ctr["i"] += 1
                gi = ctr["i"] % 4
                P.op(ACT, lambda e: e.activation(out=gtmp[gi][:, 0:n], in_=pap, func=AF.Sigmoid, bias=bglu[:, g:g + 1], scale=1.0),
                     reads=[pb, b_bglu], writes=[b_gtmp[gi]])
                P.op(DVE, lambda e: e.tensor_tensor(out=s2T[:, g, t0:t0 + n], in0=yT[:, g, t0:t0 + n], in1=gtmp[gi][:, 0:n], op=ALU.mult),
                     reads=[b_gtmp[gi], yT_b[g]], wadd=[s2_b[g]])
            proj_fm(w_glu[:, 512 * blk:512 * blk + 512], 512, rhs_y, [BufGroup(yT_b)] * 5, ev_glu)

            def ev_zs(oc, tbi, pap, pb, blk=blk):
                t0, n = TBS[tbi]
                g = 4 * blk + oc
                ctr["i"] += 1
                gi = ctr["i"] % 4
                fi = ctr["i"] % 2
                P.op(ACT, lambda e: e.activation(out=gtmp[gi][:, 0:n], in_=pap, func=AF.Sigmoid), reads=[pb], writes=[b_gtmp[gi]])
                P.op(DVE, lambda e: e.tensor_tensor(out=ftmp[fi][:, 0:n], in0=pap, in1=gtmp[gi][:, 0:n], op=ALU.mult),
                     reads=[pb, b_gtmp[gi]], writes=[b_ftmp[fi]])
                P.op(POOL, lambda e: e.tensor_tensor(out=s2T[:, g, t0:t0 + n], in0=s2T[:, g, t0:t0 + n], in1=ftmp[fi][:, 0:n], op=ALU.mult),
                     reads=[b_ftmp[fi], s2_b[g]], wadd=[s2_b[g]])
            proj_fm(w_in[:, OFF_ZS + 512 * blk:OFF_ZS + 512 * blk + 512], 512, rhs_h, hT_b, ev_zs)

        P.phase(11)
        gbs = RA
        gbs_b = [Buf(f"gbs{g}") for g in range(8)]
        handoff(gbs_b, yT_b)
        rhs_s2 = lambda k, t0, n: s2T[:, k, t0:t0 + n]
        for blk in range(2):
            def ev_gs(oc, tbi, pap, pb, blk=blk):
                t0, n = TBS[tbi]
                g = 4 * blk + oc
                P.op(ACT, lambda e: e.activation(out=gbs[:, g, t0:t0 + n], in_=pap, func=AF.Sigmoid), reads=[pb], wadd=[gbs_b[g]])
            proj_fm(w_in[:, OFF_GS + 512 * blk:OFF_GS + 512 * blk + 512], 512, rhs_h, hT_b, ev_gs)

            def ev_bs(oc, tbi, pap, pb, blk=blk):
                t0, n = TBS[tbi]
                g = 4 * blk + oc
                P.op(DVE, lambda e: e.tensor_tensor(out=gbs[:, g, t0:t0 + n], in0=pap, in1=gbs[:, g, t0:t0 + n], op=ALU.mult),
                     reads=[pb, gbs_b[g]], wadd=[gbs_b[g]])
            proj_fm(w_bs[:, 512 * blk:512 * blk + 512], 512, rhs_s2, [BufGroup(s2_b)] * 5, ev_bs)

        P.phase(12)
        oT = RB
        oT_b = [Buf(f"oT{g}") for g in range(8)]
        handoff(oT_b, s2_b)
        oc_ = O_C
        qT = cv_(oc_ + 0, [2, NT], BF16); kT2 = cv_(oc_ + 8256, [128 + NT], BF16); Vaug = cv_(oc_ + 12640, [18, 128], BF16)
        EB = cv_(oc_ + 17248, [2, 16, 128], BF16); EB0 = cv_(oc_ + 25440, [16, 128], BF16)
        Et = [cv_(oc_ + 29536 + 1024 * i, [512], BF16) for i in range(4)]
        PT = [cv_(oc_ + 33632 + 1024 * i, [512], BF16) for i in range(4)]
        rc = [cv_(oc_ + 37728 + 2048 * i, [512], F32) for i in range(2)]
        maskt = cv_(oc_ + 41824, [2, 128], F32); RT = cv_(oc_ + 42848, [384], F32); relb = cv_(oc_ + 44384, [16], F32)
        es16 = cv_(oc_ + 44448, [16], F32); klast = cv_(oc_ + 44512, [256], F32); vlast = cv_(oc_ + 45536, [256], F32)
        knew = cv_(oc_ + 46560, [256], F32); vnew = cv_(oc_ + 47584, [256], F32)
        Kc = cv_(oc_ + 48608, [16, 256], F32)
        KcT = cv_(oc_ + 64992, [16, 2, 128], BF16)
        Vcs = cv_(oc_ + 73184, [16, 256], BF16)
        QsT = cv_(oc_ + 81376, [2, 4, 16], BF16)
        dgt = cv_(oc_ + 81632, [64], F32); vnb = cv_(oc_ + 81888, [256], BF16); pdg = cv_(oc_ + 82400, [64], BF16)
        esr = cv_(oc_ + 82528, [64], BF16); ebs = cv_(oc_ + 82656, [16], F32); rcs = cv_(oc_ + 82720, [128], F32)
        ptS = cv_(oc_ + 83232, [128], BF16); ones_k = cv_(oc_ + 83488, [128], BF16)
        attn_bufs = {n: Buf(n) for n in ["qT", "kT2", "Vaug", "EB", "EB0", "mask", "RT", "relb", "es16", "klast", "vlast", "knew", "vnew",
                                         "Kc", "KcT", "Vcs", "QsT", "dgt", "vnb", "pdg", "esr", "ebs", "rcs", "ptS", "ones_k"]}
        A = attn_bufs
        b_Et = [Buf() for _ in range(4)]; b_PT = [Buf() for _ in range(4)]; b_rc = [Buf(), Buf()]
        prev_c = [b_pw, b_bb, b_ccm, b_R8, b_R128, b_A2k, b_Hend, b_carry, b_Sb, b_coef, b_misc, b_t, b_KL0, b_Ca1,
                  b_stile, b_Hsp, b_Hn, b_HbS, b_Qs, b_stg, b_sout] + b_gtmp + b_ftmp + all_ssm_tmp + b_CaBD + b_KL
        handoff(list(A.values()) + b_Et + b_PT + b_rc, prev_c)
        misc_load(SP, RT[0:32, :], rtab, A["RT"]); misc_load(SP, relb[0:32, :], rel_bias, A["relb"])
        misc_load(SP, maskt.rearrange("p h q -> p (h q)"), maskc, A["mask"])
        misc_load(SP, es16[0:1, :], sinks.rearrange("(o n) -> o n", o=1), A["es16"])
        misc_load(SP, ebs[0:16, :], rel_bias[0:1, :].to_broadcast([16, 16]), A["ebs"])
        misc_load(SP, dgt[0:16, :], diagc, A["dgt"])
        P.op(ACT, lambda e: e.activation(out=es16[0:1, :], in_=es16[0:1, :], func=AF.Exp), reads=[A["es16"]], writes=[A["es16"]])
        P.op(ACT, lambda e: e.activation(out=ebs[0:16, :], in_=ebs[0:16, :], func=AF.Exp), reads=[A["ebs"]], writes=[A["ebs"]])
        for kv in range(4):
            for sl_, i in enumerate([0, 2, 1, 3]):
                h = 4 * kv + i
                P.op(DVE, lambda e, kv=kv, sl_=sl_, h=h: e.tensor_copy(out=ES[0:1, kv, sl_, :], in_=es16[0:1, h:h + 1].to_broadcast([1, 128])),
                     reads=[A["es16"]], wadd=[b_ES])
        P.op(POOL, lambda e: e.memset(ones_k, 1.0), writes=[A["ones_k"]])
        for half in range(2):
            for qb in range(4):
                bke = bank()
                for qq in range(32):
                    q = 32 * qb + qq
                    st_ = (127 - q) if half == 0 else (255 - q)
                    P.op(PE, lambda e, bke=bke, qq=qq, st_=st_: e.matmul(ps[bke][:, 16 * qq:16 * qq + 16], lhsT=RT[0:32, st_:st_ + 128],
                                                                       rhs=relb[0:32, :], start=True, stop=True),
                         reads=[A["RT"], A["relb"]] if qq == 0 else [], writes=[psb[bke]] if qq == 0 else [], sig=(qq == 31))
                if not P.dead:
                    P.attach(Dep(P.esem[PE], P.cnt[PE]), reads=[A["RT"], A["relb"]], writes=[psb[bke]])
                P.op(ACT, lambda e, bke=bke, half=half, qb=qb: e.activation(
                    out=EB[:, half, :, 32 * qb:32 * qb + 32], in_=ps[bke][:, 0:512].rearrange("p (q h) -> p h q", h=16), func=AF.Exp),
                    reads=[psb[bke]], wadd=[A["EB"]])
        P.op(DVE, lambda e: e.tensor_tensor(out=EB, in0=EB, in1=maskt.unsqueeze(2).to_broadcast([128, 2, 16, 128]), op=ALU.mult),
             reads=[A["EB"], A["mask"]], writes=[A["EB"]])
        P.op(DVE, lambda e: e.tensor_scalar(out=EB0, in0=EB[:, 0], scalar1=flags[:, 0:1], scalar2=None, op0=ALU.mult),
             reads=[A["EB"], b_flags], writes=[A["EB0"]])
        P.dma(SP, dout_sem, sck[:, 0:127, :], ck[:, 1:128, :])
        P.dma(SP, dout_sem, scv[:, 0:127, :], cv[:, 1:128, :])
        kc_sem = P.dsem("kc")
        P.dma(SP, kc_sem, Kc, ck.rearrange("s t f -> t s f"), writes=[A["Kc"]])
        for s_ in range(NS):
            bkt = bank()
            for kvp in range(2):
                P.op(PE, lambda e, s_=s_, kvp=kvp, bkt=bkt: e.transpose(out=ps[bkt][:, 128 * kvp:128 * kvp + 128],
                                                                     in_=Kc[:, s_, 128 * kvp:128 * kvp + 128], identity=ident),
                     reads=[A["Kc"], b_ident], writes=[psb[bkt]] if kvp == 0 else [], wadd=[psb[bkt]] if kvp == 1 else [])
            evac_copy(s_, KcT[:, s_].rearrange("p a t -> p (a t)"), ps[bkt][:, 0:256], [psb[bkt]], [], wadd=[A["KcT"]])
        P.dma(SP, kc_sem, Kc, cv.rearrange("s t f -> t s f"), reads=[A["KcT"]], writes=[A["Kc"]])
        P.op(POOL, lambda e: e.tensor_copy(out=Vcs, in_=Kc), reads=[A["Kc"]], writes=[A["Vcs"]])
        P.op(POOL, lambda e: e.memset(Vaug[:, :, 64:128], 1.0), writes=[A["Vaug"]])

        rhs_hh = lambda k, t0, n: hTh[:, k, 0:n]
        TB5 = TBS
        def attn_kv(kv):
            def ev_q(oc, tbi, pap, pb):
                t0, n = TBS[tbi]
                ctr["i"] += 1
                evac_copy(ctr["i"], qT[:, oc, t0:t0 + n], pap, [pb], [], wadd=[A["qT"]])
            A["qT"].r = list(A["qT"].r) + list(A["qT"].w); A["qT"].w = []
            proj_fm(w_in[:, OFF_Q + 256 * kv:OFF_Q + 256 * kv + 256], 256, rhs_h, hT_b, ev_q)
            s = wctr[0] % 2
            wctr[0] += 1
            for dup in range(2):
                P.dma(POOL, wsem[s], wslot[s][:, :, 64 * dup:64 * dup + 64],
                      w_in[:, OFF_K + 64 * kv:OFF_K + 64 * kv + 64].rearrange("(k p) f -> p k f", p=128),
                      writes=[wslot_b[s]] if dup == 0 else [], wadd=[wslot_b[s]] if dup == 1 else [])
            A["kT2"].r = list(A["kT2"].r) + list(A["kT2"].w); A["kT2"].w = []
            kblocks = [(hTh, hTh_b, 0, 128, 0)] + [(hT, hT_b[i], t0, n, 128 + t0) for i, (t0, n) in enumerate(TBS)]
            for bi_, (src, sb_, t0, n, c0) in enumerate(kblocks):
                bkk = bank()
                for k in range(8):
                    P.op(PE, lambda e, bkk=bkk, k=k, src=src, t0=t0, n=n, s=s: e.matmul(
                        ps[bkk][:, 0:n], lhsT=wslot[s][:, k, 0:128], rhs=src[:, k, t0:t0 + n], start=(k == 0), stop=(k == 7)),
                        reads=[wslot_b[s], sb_] if k == 0 else [], writes=[psb[bkk]] if k == 0 else [], sig=(k == 7))
                if not P.dead:
                    P.attach(Dep(P.esem[PE], P.cnt[PE]), reads=[wslot_b[s], sb_], writes=[psb[bkk]])
                evac_copy(bi_, kT2[:, c0:c0 + n], ps[bkk][:, 0:n], [psb[bkk]], [], wadd=[A["kT2"]])
            bkl_ = bank()
            for j_, (c0_, m_) in enumerate([(NP - 128, 128), (NP, NS)]):
                for k in range(8):
                    P.op(PE, lambda e, j_=j_, c0_=c0_, m_=m_, k=k, s=s: e.matmul(
                        ps[bkl_][0:m_, 64 * j_:64 * j_ + 64], lhsT=hT[:, k, c0_:c0_ + m_], rhs=wslot[s][:, k, 0:64],
                        start=(k == 0), stop=(k == 7)),
                        reads=[wslot_b[s], hT_b[3], hT_b[4]] if (k == 0 and j_ == 0) else [],
                        writes=[psb[bkl_]] if (k == 0 and j_ == 0) else [], sig=(k == 7 and j_ == 1))
            if not P.dead:
                P.attach(Dep(P.esem[PE], P.cnt[PE]), reads=[wslot_b[s], hT_b[3], hT_b[4]], writes=[psb[bkl_]])
            P.op(DVE, lambda e, kv=kv: e.tensor_copy(out=klast[:, 64 * kv:64 * kv + 64], in_=ps[bkl_][:, 0:64]), reads=[psb[bkl_]], wadd=[A["klast"]])
            P.op(DVE, lambda e, kv=kv: e.tensor_copy(out=knew[0:NS, 64 * kv:64 * kv + 64], in_=ps[bkl_][0:NS, 64:128]), reads=[psb[bkl_]], wadd=[A["knew"]])
            s = wctr[0] % 2
            wctr[0] += 1
            P.dma(POOL, wsem[s], wslot[s][:, :, 0:64], w_in[:, OFF_V + 64 * kv:OFF_V + 64 * kv + 64].rearrange("(k p) f -> p k f", p=128),
                  writes=[wslot_b[s]])
            A["Vaug"].r = list(A["Vaug"].r) + list(A["Vaug"].w); A["Vaug"].w = []
            vtiles = [(hTh, hTh_b, 0, 128)] + [(hT, hT_b[i // 4], 128 * i, 128) for i in range(16)] + [(hT, hT_b[4], NP, NS)]
            for grp in range(3):
                bkv = bank()
                tl_ = vtiles[8 * grp:8 * grp + 8]
                for j_, (src, sb_, c0_, m_) in enumerate(tl_):
                    for k in range(8):
                        firstg = (j_ == 0 and k == 0)
                        P.op(PE, lambda e, bkv=bkv, j_=j_, src=src, c0_=c0_, m_=m_, k=k, s=s: e.matmul(
                            ps[bkv][0:m_, 64 * j_:64 * j_ + 64], lhsT=src[:, k, c0_:c0_ + m_], rhs=wslot[s][:, k, 0:64],
                            start=(k == 0), stop=(k == 7)),
                            reads=[wslot_b[s], sb_, hTh_b] + hT_b if firstg else [], writes=[psb[bkv]] if firstg else [],
                            sig=(j_ == len(tl_) - 1 and k == 7))
                if not P.dead:
                    P.attach(Dep(P.esem[PE], P.cnt[PE]), reads=[wslot_b[s], hTh_b] + hT_b, writes=[psb[bkv]])
                nt_ = len(tl_)
                if grp < 2:
                    P.op(ACT, lambda e, bkv=bkv, grp=grp: e.activation(out=Vaug[:, 8 * grp:8 * grp + 8, 0:64],
                                                                     in_=ps[bkv][:, 0:512].rearrange("p (t d) -> p t d", d=64), func=AF.Copy),
                         reads=[psb[bkv]], wadd=[A["Vaug"]])
                    if grp == 1:
                        pass
                else:
                    P.op(ACT, lambda e, bkv=bkv: e.activation(out=Vaug[:, 16, 0:64], in_=ps[bkv][:, 0:64], func=AF.Copy),
                         reads=[psb[bkv]], wadd=[A["Vaug"]])
                    P.op(ACT, lambda e, bkv=bkv: e.activation(out=Vaug[0:NS, 17, 0:64], in_=ps[bkv][0:NS, 64:128], func=AF.Copy),
                         reads=[psb[bkv]], wadd=[A["Vaug"]])
                    P.op(ACT, lambda e, bkv=bkv, kv=kv: e.activation(out=vlast[:, 64 * kv:64 * kv + 64], in_=ps[bkv][:, 0:64], func=AF.Copy),
                         reads=[psb[bkv]], wadd=[A["vlast"]])
                    P.op(ACT, lambda e, bkv=bkv, kv=kv: e.activation(out=vnew[0:NS, 64 * kv:64 * kv + 64], in_=ps[bkv][0:NS, 64:128], func=AF.Copy),
                         reads=[psb[bkv]], wadd=[A["vnew"]])
            qv = lambda base, b_: qT[base:base + 64, 0:2, 128 * b_:128 * b_ + 128]
            def attn_s1(b_):
                ia = (2 * b_) % 4; ib = (2 * b_ + 1) % 4
                bA, bB = bank(), bank()
                kprev = slice(128 * b_, 128 * b_ + 128); kcur = slice(128 * b_ + 128, 128 * b_ + 256)
                seq = [(bA, 0, 0, kprev), (bB, 64, 0, kprev), (bA, 0, 1, kcur), (bB, 64, 1, kcur)]
                for (bk_, base, half, ks) in seq:
                    first = (half == 0)
                    P.op(PE, lambda e, bk_=bk_, base=base, half=half, ks=ks, b_=b_: e.matmul(
                        ps[bk_][:, 256 * half:256 * half + 256], lhsT=kT2[base:base + 64, ks], rhs=qv(base, b_), start=True, stop=True),
                        reads=[A["kT2"], A["qT"]] if first else [], writes=[psb[bk_]] if first else [], sig=(half == 1))
                    if half == 1 and not P.dead:
                        P.attach(Dep(P.esem[PE], P.cnt[PE]), reads=[A["kT2"], A["qT"]], writes=[psb[bk_]])
                for (bk_, ie, base_h) in [(bA, ia, 0), (bB, ib, 1)]:
                    P.op(ACT, lambda e, bk_=bk_, ie=ie: e.activation(out=Et[ie], in_=ps[bk_][:, :], func=AF.Exp, scale=0.125),
                         reads=[psb[bk_]], writes=[b_Et[ie]])
                    Ev = Et[ie].rearrange("p (h i q) -> p h i q", h=2, i=2)
                    Pv = PT[ie].rearrange("p (h i q) -> p h i q", h=2, i=2)
                    h0 = 4 * kv + base_h
                    eng = DVE if base_h == 0 else POOL
                    if b_ > 0:
                        P.op(eng, lambda e, Ev=Ev, Pv=Pv, h0=h0: e.tensor_tensor(out=Pv, in0=Ev, in1=EB[:, :, h0:h0 + 3:2, :], op=ALU.mult),
                             reads=[b_Et[ie], A["EB"]], writes=[b_PT[ie]])
                    else:
                        P.op(eng, lambda e, Ev=Ev, Pv=Pv, h0=h0: e.tensor_tensor(out=Pv[:, 0], in0=Ev[:, 0], in1=EB0[:, h0:h0 + 3:2, :], op=ALU.mult),
                             reads=[b_Et[ie], A["EB0"]], writes=[b_PT[ie]])
                        P.op(eng, lambda e, Ev=Ev, Pv=Pv, h0=h0: e.tensor_tensor(out=Pv[:, 1], in0=Ev[:, 1], in1=EB[:, 1, h0:h0 + 3:2, :], op=ALU.mult),
                             reads=[b_Et[ie], A["EB"]], wadd=[b_PT[ie]])

            def attn_s2(b_):
                ia = (2 * b_) % 4; ib = (2 * b_ + 1) % 4
                bO = bank()
                mm = [(ia, 0, b_, True), (ia, 1, b_ + 1, False), (ib, 0, b_, False), (ib, 1, b_ + 1, False)]
                for j_, (ip, half, tile, st_) in enumerate(mm):
                    cols = slice(0, 256) if ip == ia else slice(256, 512)
                    P.op(PE, lambda e, ip=ip, half=half, tile=tile, st_=st_, cols=cols: e.matmul(
                        ps[bO][:, cols], lhsT=Vaug[:, tile, :], rhs=PT[ip][:, 256 * half:256 * half + 256], start=st_, stop=False),
                        reads=[A["Vaug"], b_PT[ia], b_PT[ib], b_ES, b_ones] if j_ == 0 else [], writes=[psb[bO]] if j_ == 0 else [], sig=False)
                P.op(PE, lambda e, kv=kv: e.matmul(ps[bO][:, 0:512], lhsT=onesd[0:1, :], rhs=ES[0:1, kv].rearrange("p i q -> p (i q)"),
                                                   start=False, stop=True), sig=True)
                if not P.dead:
                    P.attach(Dep(P.esem[PE], P.cnt[PE]), reads=[A["Vaug"], b_PT[ia], b_PT[ib], b_ES, b_ones], writes=[psb[bO]])
                ir = b_ % 2
                P.op(ACT, lambda e, ir=ir: e.activation(out=rc[ir][64:128, :], in_=ps[bO][64:128, :], func=AF.Ln), reads=[psb[bO]], writes=[b_rc[ir]])
                P.op(ACT, lambda e, ir=ir: e.activation(out=rc[ir][64:128, :], in_=rc[ir][64:128, :], func=AF.Exp, scale=-1.0),
                     reads=[b_rc[ir]], writes=[b_rc[ir]])
                for par in range(2):
                    P.op(DVE, lambda e, par=par, ir=ir, b_=b_, kv=kv: e.tensor_tensor(
                        out=oT[64 * par:64 * par + 64, 2 * kv:2 * kv + 2, 128 * b_:128 * b_ + 128],
                        in0=ps[bO][0:64, 256 * par:256 * par + 256].rearrange("p (c q) -> p c q", c=2),
                        in1=rc[ir][64:128, 256 * par:256 * par + 256].rearrange("p (c q) -> p c q", c=2), op=ALU.mult),
                        reads=[psb[bO], b_rc[ir]], wadd=[oT_b[2 * kv], oT_b[2 * kv + 1]])

            attn_s1(0)
            for b_ in range(1, 16):
                attn_s1(b_)
                attn_s2(b_ - 1)
            attn_s2(15)
            base = 64 * (kv % 2)
            for i in range(4):
                hsrc = 64 * (i % 2)
                P.op(POOL, lambda e, i=i, hsrc=hsrc, base=base: e.tensor_copy(out=QsT[base:base + 64, 0, i, :], in_=qT[hsrc:hsrc + 64, i // 2, NP:NT]),
                     reads=[A["qT"]], writes=[A["QsT"]] if i == 0 else [], wadd=[A["QsT"]] if i > 0 else [])
            for sl_, i in enumerate([0, 1, 2, 3]):
                P.op(DVE, lambda e, i=i, kv=kv: e.tensor_copy(out=esr[0:1, :].rearrange("p (s i) -> p s i", i=4)[:, :, i],
                                                            in_=es16[0:1, 4 * kv + i:4 * kv + i + 1].to_broadcast([1, 16])),
                     reads=[A["es16"]], writes=[A["esr"]] if i == 0 else [], wadd=[A["esr"]] if i > 0 else [])
            bS, bD, bN = bank(), bank(), bank()
            Qsi = QsT[base:base + 64, 0].rearrange("p i s -> p s i")
            for s_ in range(NS):
                P.op(PE, lambda e, s_=s_, base=base, kv=kv: e.matmul(ps[bS][:, 4 * s_:4 * s_ + 4], lhsT=KcT[base:base + 64, s_, kv // 2, :],
                                                                  rhs=QsT[base:base + 64, 0, :, s_], start=True, stop=True),
                     reads=[A["KcT"], A["QsT"], A["kT2"]] if s_ == 0 else [], writes=[psb[bS]] if s_ == 0 else [], sig=False)
            P.op(PE, lambda e, base=base: e.matmul(ps[bS][0:NS, 64:128], lhsT=kT2[base:base + 64, 128 + NP:128 + NT], rhs=Qsi, start=True, stop=True), sig=True)
            if not P.dead:
                P.attach(Dep(P.esem[PE], P.cnt[PE]), reads=[A["KcT"], A["QsT"], A["kT2"]], writes=[psb[bS]])
            P.op(ACT, lambda e: e.activation(out=ptS[:, 0:64], in_=ps[bS][:, 0:64], func=AF.Exp, scale=0.125), reads=[psb[bS]], writes=[A["ptS"]])
            P.op(ACT, lambda e: e.activation(out=pdg[0:NS, :], in_=ps[bS][0:NS, 64:128], func=AF.Exp, scale=0.125), reads=[psb[bS]], writes=[A["pdg"]])
            P.op(DVE, lambda e, kv=kv: e.tensor_tensor(out=ptS[:, 0:64].rearrange("p (s i) -> p s i", i=4), in0=ptS[:, 0:64].rearrange("p (s i) -> p s i", i=4),
                                                in1=EB[:, 0, 4 * kv:4 * kv + 4, 0].unsqueeze(1).to_broadcast([128, 16, 4]), op=ALU.mult),
                 reads=[A["ptS"], A["EB"]], writes=[A["ptS"]])
            P.op(DVE, lambda e, kv=kv: e.tensor_tensor(out=pdg[0:NS, :].rearrange("p (s i) -> p s i", i=4), in0=pdg[0:NS, :].rearrange("p (s i) -> p s i", i=4),
                                                in1=ebs[0:NS, 4 * kv:4 * kv + 4].unsqueeze(1).to_broadcast([NS, 16, 4]), op=ALU.mult),
                 reads=[A["pdg"], A["ebs"]], writes=[A["pdg"]])
            P.op(DVE, lambda e: e.tensor_tensor(out=pdg[0:NS, :], in0=pdg[0:NS, :], in1=dgt[0:NS, :], op=ALU.mult),
                 reads=[A["pdg"], A["dgt"]], writes=[A["pdg"]])
            P.op(POOL, lambda e, kv=kv: e.tensor_copy(out=vnb[0:NS, 64 * kv:64 * kv + 64], in_=vnew[0:NS, 64 * kv:64 * kv + 64]),
                 reads=[A["vnew"]], writes=[A["vnb"]])
            P.op(PE, lambda e: e.matmul(ps[bD][:, 0:64], lhsT=ones_k, rhs=ptS[:, 0:64], start=True, stop=False),
                 reads=[A["ones_k"], A["ptS"], A["pdg"], A["esr"]], writes=[psb[bD]], sig=False)
            P.op(PE, lambda e: e.matmul(ps[bD][:, 0:64], lhsT=ones_k[0:NS, :], rhs=pdg[0:NS, :], start=False, stop=False), sig=False)
            P.op(PE, lambda e: e.matmul(ps[bD][:, 0:64], lhsT=ones_k[0:1, :], rhs=esr[0:1, :], start=False, stop=True), sig=True)
            if not P.dead:
                P.attach(Dep(P.esem[PE], P.cnt[PE]), reads=[A["ones_k"], A["ptS"], A["pdg"], A["esr"]], writes=[psb[bD]])
            ptv = ptS[:, 0:64].rearrange("p (s i) -> p s i", i=4)
            pdv = pdg[0:NS, :].rearrange("p (s i) -> p s i", i=4)
            psn = ps[bN][:, 0:64].rearrange("p (s i) -> p s i", i=4)
            for par in range(2):
                for s_ in range(NS):
                    P.op(PE, lambda e, par=par, s_=s_, kv=kv: e.matmul(
                        psn[64 * par:64 * par + 64, s_, par:4:2], lhsT=Vcs[:, s_, 64 * kv:64 * kv + 64], rhs=ptv[:, s_, par:4:2],
                        start=(s_ == 0), stop=False, tile_position=(0, 64 * par)),
                        reads=[A["Vcs"], A["ptS"], A["pdg"], A["vnb"]] if (par == 0 and s_ == 0) else [],
                        writes=[psb[bN]] if (par == 0 and s_ == 0) else [], sig=False)
                P.op(PE, lambda e, par=par, kv=kv: e.matmul(
                    psn[64 * par:64 * par + 64, :, par:4:2], lhsT=vnb[0:NS, 64 * kv:64 * kv + 64], rhs=pdv[:, :, par:4:2],
                    start=False, stop=True, tile_position=(0, 64 * par)), sig=(par == 1))
            if not P.dead:
                P.attach(Dep(P.esem[PE], P.cnt[PE]), reads=[A["Vcs"], A["ptS"], A["pdg"], A["vnb"]], writes=[psb[bN]])
            P.op(DVE, lambda e: e.reciprocal(out=rcs[:, 0:64], in_=ps[bD][:, 0:64]), reads=[psb[bD]], writes=[A["rcs"]])
            rcv = rcs[:, 0:64].rearrange("p (s i) -> p s i", i=4)
            for par in range(2):
                P.op(DVE, lambda e, par=par, kv=kv: e.tensor_tensor(
                    out=oT[64 * par:64 * par + 64, 2 * kv:2 * kv + 2, NP:NT],
                    in0=psn[64 * par:64 * par + 64, :, par:4:2].rearrange("p s c -> p c s"),
                    in1=rcv[64 * par:64 * par + 64, :, par:4:2].rearrange("p s c -> p c s"), op=ALU.mult),
                    reads=[psb[bN], A["rcs"]], wadd=[oT_b[2 * kv], oT_b[2 * kv + 1]])
        for kv in range(4):
            attn_kv(kv)
        P.dma(SP, dout_sem, pck, klast, reads=[A["klast"]])
        P.dma(SP, dout_sem, pcv, vlast, reads=[A["vlast"]])
        P.dma(SP, dout_sem, sck[:, 127, :], knew[0:NS, :], reads=[A["knew"]])
        P.dma(SP, dout_sem, scv[:, 127, :], vnew[0:NS, :], reads=[A["vnew"]])

        P.phase(13)
        sgaT = cv_(O_C + 0, [8, NT], BF16)
        sga_b = [Buf(f"sga{g}") for g in range(8)]
        gt2 = [cv_(O_C + 33024 + 1024 * i, [512], BF16) for i in range(4)]
        ft2 = [cv_(O_C + 37120 + 2048 * i, [512], F32) for i in range(2)]
        b_gt2 = [Buf() for _ in range(4)]; b_ft2 = [Buf(), Buf()]
        handoff(sga_b + b_gt2 + b_ft2, list(A.values()) + b_Et + b_PT + b_rc)
        for blk in range(2):
            def ev_za(oc, tbi, pap, pb, blk=blk):
                t0, n = TBS[tbi]
                g = 4 * blk + oc
                ctr["i"] += 1
                gi = ctr["i"] % 4; fi = ctr["i"] % 2
                P.op(ACT, lambda e: e.activation(out=gt2[gi][:, 0:n], in_=pap, func=AF.Sigmoid), reads=[pb], writes=[b_gt2[gi]])
                P.op(DVE, lambda e: e.tensor_tensor(out=ft2[fi][:, 0:n], in0=pap, in1=gt2[gi][:, 0:n], op=ALU.mult),
                     reads=[pb, b_gt2[gi]], writes=[b_ft2[fi]])
                P.op(POOL, lambda e: e.tensor_tensor(out=oT[:, g, t0:t0 + n], in0=oT[:, g, t0:t0 + n], in1=ft2[fi][:, 0:n], op=ALU.mult),
                     reads=[b_ft2[fi], oT_b[g]], wadd=[oT_b[g]])
            proj_fm(w_in[:, OFF_ZA + 512 * blk:OFF_ZA + 512 * blk + 512], 512, rhs_h, hT_b, ev_za)
        for blk in range(2):
            def ev_ga(oc, tbi, pap, pb, blk=blk):
                t0, n = TBS[tbi]
                g = 4 * blk + oc
                P.op(ACT, lambda e: e.activation(out=sgaT[:, g, t0:t0 + n], in_=pap, func=AF.Sigmoid), reads=[pb], wadd=[sga_b[g]])
            proj_fm(w_in[:, OFF_GA + 512 * blk:OFF_GA + 512 * blk + 512], 512, rhs_h, hT_b, ev_ga)
        mT = RA
        mT_b = gbs_b
        rhs_o = lambda k, t0, n: oT[:, k, t0:t0 + n]
        for blk in range(2):
            def ev_ba(oc, tbi, pap, pb, blk=blk):
                t0, n = TBS[tbi]
                g = 4 * blk + oc
                ctr["i"] += 1
                fi = ctr["i"] % 2
                P.op(DVE, lambda e: e.tensor_tensor(out=ft2[fi][:, 0:n], in0=pap, in1=sgaT[:, g, t0:t0 + n], op=ALU.mult),
                     reads=[pb, sga_b[g]], writes=[b_ft2[fi]])
                P.op(POOL, lambda e: e.tensor_tensor(out=mT[:, g, t0:t0 + n], in0=mT[:, g, t0:t0 + n], in1=ft2[fi][:, 0:n], op=ALU.add),
                     reads=[b_ft2[fi], mT_b[g]], wadd=[mT_b[g]])
            proj_fm(w_ba[:, 512 * blk:512 * blk + 512], 512, rhs_o, [BufGroup(oT_b)] * 5, ev_ba)

        P.phase(14)
        o2 = O_C + 8000
        NSL = 4
        GateB = cv_(o2 + 0, [1024], F32); LnG = cv_(o2 + 4096, [1024], F32); LnB = cv_(o2 + 8192, [1024], F32)
        gateS = cv_(o2 + 12288, [1024], F32); grow = cv_(o2 + 16384, [1024], F32)
        xt = [cv_(o2 + 20480 + 4096 * i, [1024], F32) for i in range(NSL)]
        rt = [cv_(o2 + 36864 + 4096 * i, [1024], F32) for i in range(NSL)]
        stt = cv_(o2 + 53248, [NSL, 2, 6], F32); mvt = cv_(o2 + 53504, [NSL, 2], F32); rsd = cv_(o2 + 53568, [NSL, 2], F32)
        mhalf = cv_(o2 + 53632, [1], F32)
        b_GateB = Buf(); b_LnG = Buf(); b_LnB = Buf(); b_gateS = Buf(); b_grow = Buf(); b_xt = [Buf() for _ in range(NSL)]; b_rt = [Buf() for _ in range(NSL)]
        b_stt = [Buf() for _ in range(NSL)]; b_mh = Buf()
        xsem2 = [P.dsem(f"xt{i}") for i in range(NSL)]; osem = [P.dsem(f"o{i}") for i in range(NSL)]
        handoff([b_GateB, b_LnG, b_LnB, b_gateS, b_grow] + b_xt + b_rt + b_stt + [b_mh], list(A.values()) + b_Et + b_PT + b_rc + sga_b + b_gt2 + b_ft2)
        misc_load(SP, LnG, ln_g.rearrange("(o n) -> o n", o=1).to_broadcast([128, 1024]), b_LnG)
        misc_load(SP, LnB, ln_b.rearrange("(o n) -> o n", o=1).to_broadcast([128, 1024]), b_LnB)
        P.op(POOL, lambda e: e.memset(mhalf, -0.5), writes=[b_mh])
        for hb in range(2):
            bkg = bank(); bkg2 = bank()
            for kk in range(4):
                k = 4 * hb + kk
                P.op(PE, lambda e, k=k, kk=kk, bkg=bkg: e.transpose(out=ps[bkg][0:1, 128 * kk:128 * kk + 128], in_=modT[:, 16 + k, 0:1], identity=ident),
                     reads=[b_modT, b_ident], writes=[psb[bkg]] if kk == 0 else [], wadd=[psb[bkg]] if kk > 0 else [])
                P.op(PE, lambda e, k=k, kk=kk, bkg2=bkg2: e.transpose(out=ps[bkg2][0:NS, 128 * kk:128 * kk + 128], in_=modT[:, 16 + k, 1:17], identity=ident),
                     reads=[b_modT, b_ident], writes=[psb[bkg2]] if kk == 0 else [], wadd=[psb[bkg2]] if kk > 0 else [])
            P.op(DVE, lambda e, hb=hb, bkg=bkg: e.tensor_copy(out=grow[0:1, 512 * hb:512 * hb + 512], in_=ps[bkg][0:1, 0:512]), reads=[psb[bkg]], wadd=[b_grow])
            P.op(DVE, lambda e, hb=hb, bkg2=bkg2: e.tensor_copy(out=gateS[0:NS, 512 * hb:512 * hb + 512], in_=ps[bkg2][0:NS, 0:512]), reads=[psb[bkg2]], wadd=[b_gateS])
        for hb in range(2):
            bkb = bank()
            P.op(PE, lambda e, hb=hb, bkb=bkb: e.matmul(ps[bkb][:, 0:512], lhsT=ones1[0:1, :], rhs=grow[0:1, 512 * hb:512 * hb + 512], start=True, stop=True),
                 reads=[b_grow, b_ones], writes=[psb[bkb]])
            P.op(DVE, lambda e, hb=hb, bkb=bkb: e.tensor_copy(out=GateB[:, 512 * hb:512 * hb + 512], in_=ps[bkb][:, 0:512]), reads=[psb[bkb]], wadd=[b_GateB])
        so = [load_w(w_out[:, 0:512], 512), load_w(w_out[:, 512:1024], 512)]
        def x_load(ti_):
            rows_, c0_ = (128, 128 * ti_) if ti_ < 16 else (NS, NP)
            src_ = xp[c0_:c0_ + 128, :] if ti_ < 16 else xs
            P.dma(SP, xsem2[ti_ % NSL], xt[ti_ % NSL][0:rows_, :], src_, writes=[b_xt[ti_ % NSL]])
        for ti_ in range(NSL):
            x_load(ti_)
        for tt_i in range(17):
            rows, c0 = (128, 128 * tt_i) if tt_i < 16 else (NS, NP)
            sl = tt_i % NSL
            gate_ap = GateB if tt_i < 16 else gateS
            gate_b = b_GateB if tt_i < 16 else b_gateS
            for fb in range(2):
                bko = bank()
                for k in range(8):
                    P.op(PE, lambda e, bko=bko, k=k, fb=fb, rows=rows, c0=c0: e.matmul(
                        ps[bko][0:rows, 0:512], lhsT=mT[:, k, c0:c0 + rows], rhs=wslot[so[fb]][:, k, 0:512], start=(k == 0), stop=(k == 7)),
                        reads=[wslot_b[so[fb]]] + mT_b if k == 0 else [], writes=[psb[bko]] if k == 0 else [], sig=(k == 7))
                if not P.dead:
                    P.attach(Dep(P.esem[PE], P.cnt[PE]), reads=[wslot_b[so[fb]]] + mT_b, writes=[psb[bko]])
                P.op(DVE, lambda e, bko=bko, fb=fb, rows=rows, sl=sl, gate_ap=gate_ap: e.tensor_tensor(
                    out=rt[sl][0:rows, 512 * fb:512 * fb + 512], in0=ps[bko][0:rows, 0:512], in1=gate_ap[0:rows, 512 * fb:512 * fb + 512], op=ALU.mult),
                    reads=[psb[bko], gate_b], writes=[b_rt[sl]] if fb == 0 else [], wadd=[b_rt[sl]] if fb == 1 else [])
            P.op(DVE, lambda e, rows=rows, sl=sl: e.scalar_tensor_tensor(out=rt[sl][0:rows, :], in0=xt[sl][0:rows, :], scalar=float(ALPHA),
                                                                         in1=rt[sl][0:rows, :], op0=ALU.mult, op1=ALU.add),
                 reads=[b_xt[sl], b_rt[sl]], writes=[b_rt[sl]])
            for hf in range(2):
                P.op(DVE, lambda e, rows=rows, sl=sl, hf=hf: e.bn_stats(out=stt[0:rows, sl, hf, :], in_=rt[sl][0:rows, 512 * hf:512 * hf + 512]),
                     reads=[b_rt[sl]], writes=[b_stt[sl]] if hf == 0 else [], wadd=[b_stt[sl]] if hf == 1 else [])
            P.op(DVE, lambda e, rows=rows, sl=sl: e.bn_aggr(out=mvt[0:rows, sl, :], in_=stt[0:rows, sl].rearrange("p a b -> p (a b)")),
                 reads=[b_stt[sl]], writes=[b_stt[sl]])
            P.op(POOL, lambda e, rows=rows, sl=sl: e.tensor_scalar(out=rsd[0:rows, sl, 0:1], in0=mvt[0:rows, sl, 1:2], scalar1=float(LN_EPS), scalar2=0.0,
                                                                   op0=ALU.add, op1=ALU.add), reads=[b_stt[sl]], writes=[b_stt[sl]])
            P.op(POOL, lambda e, rows=rows, sl=sl: e.tensor_tensor(out=rsd[0:rows, sl, 0:1], in0=rsd[0:rows, sl, 0:1], in1=mhalf[0:rows, :], op=ALU.pow),
                 reads=[b_stt[sl], b_mh], writes=[b_stt[sl]])
            P.op(POOL, lambda e, rows=rows, sl=sl: e.tensor_tensor(out=rsd[0:rows, sl, 1:2], in0=mvt[0:rows, sl, 0:1], in1=rsd[0:rows, sl, 0:1], op=ALU.mult),
                 reads=[b_stt[sl]], writes=[b_stt[sl]])
            P.op(POOL, lambda e, rows=rows, sl=sl: e.tensor_scalar(out=rsd[0:rows, sl, 1:2], in0=rsd[0:rows, sl, 1:2], scalar1=-1.0, scalar2=0.0,
                                                                   op0=ALU.mult, op1=ALU.add), reads=[b_stt[sl]], writes=[b_stt[sl]])
            P.op(ACT, lambda e, rows=rows, sl=sl: e.activation(out=xt[sl][0:rows, :], in_=rt[sl][0:rows, :], func=AF.Identity,
                                                               scale=rsd[0:rows, sl, 0:1], bias=rsd[0:rows, sl, 1:2]),
                 reads=[b_rt[sl], b_stt[sl]], writes=[b_xt[sl]])
            P.op(DVE, lambda e, rows=rows, sl=sl: e.tensor_tensor(out=xt[sl][0:rows, :], in0=xt[sl][0:rows, :], in1=LnG[0:rows, :], op=ALU.mult),
                 reads=[b_xt[sl], b_LnG], writes=[b_xt[sl]])
            P.op(POOL, lambda e, rows=rows, sl=sl: e.tensor_tensor(out=xt[sl][0:rows, :], in0=xt[sl][0:rows, :], in1=LnB[0:rows, :], op=ALU.add),
                 reads=[b_xt[sl], b_LnB], writes=[b_xt[sl]])
            dst = yp[c0:c0 + 128, :] if tt_i < 16 else ys
            P.dma(SP, osem[sl], dst, xt[sl][0:rows, :], reads=[b_xt[sl]])
            if tt_i + NSL < 17:
                x_load(tt_i + NSL)
        final_deps = [Dep(o_.h, o_.cnt) for o_ in osem]

        P.dead = False
        if DEBUG:
            pass
        P.wait(SP, [Dep(dout_sem.h, dout_sem.cnt)] + final_deps)
        P.emit()
        nc.all_engine_barrier()
        nc.clear_and_free_semaphores(P.allsems)
        nc.all_engine_barrier()
        print("instruction counts:", P.ninst)
    return nc


def _bucket_np(dist):
    max_exact = 16
    df = np.maximum(dist, 1).astype(np.float32)
    large = max_exact + (np.log(df / np.float32(max_exact)) / np.float32(math.log(128 / max_exact)) * np.float32(16)).astype(np.int32)
    large = np.minimum(large, 31)
    return np.where(dist < max_exact, dist, large)


def _host_consts():
    R = np.zeros((32, 384), np.float32)
    for i in range(384):
        dist = 255 - i
        if 0 <= dist <= 128:
            R[int(_bucket_np(np.array([dist]))[0]), i] = 1.0
    j = np.arange(128)[:, None]
    q = np.arange(128)[None, :]
    mask = np.concatenate([(j >= q), (j <= q)], axis=1).astype(np.float32)
    r = np.arange(128)
    bmask = (r[:, None] // 32 == r[None, :] // 32).astype(np.float32)
    diag = np.zeros((16, 16, 4), np.float32)
    for s_ in range(16):
        diag[s_, s_, :] = 1.0
    return R, mask, bmask, diag.reshape(16, 64)


_NC_CACHE = {}


def kernel(x_prompt, x_sample, c_prompt, c_sample, state_ssm_re, state_ssm_im, cache_swa_k, cache_swa_v,
           w_ada, b_ada, w_in, ssm_lambda_re, ssm_lambda_im, ssm_log_delta, ssm_b_re, ssm_b_im,
           ssm_c_re, ssm_c_im, ssm_d, w_glu, b_glu, attn_sinks, rel_bias, w_branch_s, w_branch_a,
           w_out, ln_g, ln_b):
    f = lambda a: np.ascontiguousarray(np.asarray(a, dtype=np.float32))
    x_prompt = f(x_prompt); x_sample = f(x_sample); c_prompt = f(c_prompt); c_sample = f(c_sample)
    R, mask, bmask, diag = _host_consts()
    shared = {
        "w_ada": f(w_ada)[0], "b_ada": f(b_ada)[0], "w_in": f(w_in)[0],
        "lam_re": f(ssm_lambda_re)[0], "lam_im": f(ssm_lambda_im)[0], "log_delta": f(ssm_log_delta)[0],
        "b_re": f(ssm_b_re)[0].reshape(4096, 16), "b_im": f(ssm_b_im)[0].reshape(4096, 16),
        "c_re": f(ssm_c_re)[0].reshape(1024, 64), "c_im": f(ssm_c_im)[0].reshape(1024, 64),
        "ssm_d": f(ssm_d)[0], "w_glu": f(w_glu)[0], "b_glu": f(b_glu)[0], "sinks": f(attn_sinks)[0],
        "rel_bias": f(rel_bias), "w_bs": f(w_branch_s)[0], "w_ba": f(w_branch_a)[0], "w_out": f(w_out)[0],
        "ln_g": f(ln_g)[0], "ln_b": f(ln_b)[0],
        "rtab": R, "maskc": mask, "bmaskc": bmask, "diagc": diag,
    }
    sre = f(state_ssm_re)[0].reshape(128, 4096); sim = f(state_ssm_im)[0].reshape(128, 4096)
    ckk = f(cache_swa_k)[0].reshape(128, 128, 256); cvv = f(cache_swa_v)[0].reshape(128, 128, 256)
    in_maps = []
    for c in range(NCORES):
        b, qr = c // 4, c % 4
        t0 = NP * qr
        xh = x_prompt[b, t0 - 128:t0] if qr > 0 else np.zeros((128, D), np.float32)
        flags = np.zeros(32, np.float32)
        flags[0] = 1.0 if qr > 0 else 0.0
        xprev = np.zeros((3, NP, D), np.float32)
        for j in range(3):
            qq = qr - 1 - j
            if qq >= 0:
                flags[1 + j] = 1.0
                xprev[j] = x_prompt[b, NP * qq:NP * qq + NP]
        m = dict(shared)
        m.update({
            "xprev": xprev, "xp": np.ascontiguousarray(x_prompt[b, t0:t0 + NP]), "xh": np.ascontiguousarray(xh),
            "xs": np.ascontiguousarray(x_sample[NS * c:NS * c + NS, 0]),
            "cc": np.ascontiguousarray(np.concatenate([c_prompt[b:b + 1], c_sample[NS * c:NS * c + NS]], 0)),
            "st_re": np.ascontiguousarray(sre[NS * c:NS * c + NS]), "st_im": np.ascontiguousarray(sim[NS * c:NS * c + NS]),
            "ck": np.ascontiguousarray(ckk[NS * c:NS * c + NS]), "cv": np.ascontiguousarray(cvv[NS * c:NS * c + NS]),
            "flags": flags,
        })
        in_maps.append(m)
    nc = build()
    res = run_bass_kernel_spmd(nc, in_maps, core_ids=list(range(NCORES)))
    R_ = res.results
    kernel.last_results = R_
    y_prompt = np.stack([np.concatenate([R_[4 * b + q]["yp"] for q in range(4)], 0) for b in range(2)], 0)
    y_sample = np.concatenate([R_[c]["ys"] for c in range(NCORES)], 0).reshape(128, 1, D)
    p_hr = np.stack([R_[4 * b + 3]["pst_re"].reshape(64, 64) for b in range(2)], 0)[None]
    p_hi = np.stack([R_[4 * b + 3]["pst_im"].reshape(64, 64) for b in range(2)], 0)[None]
    p_k = np.stack([R_[4 * b + 3]["pck"].reshape(128, 4, 64) for b in range(2)], 0)[None]
    p_v = np.stack([R_[4 * b + 3]["pcv"].reshape(128, 4, 64) for b in range(2)], 0)[None]
    s_hr = np.concatenate([R_[c]["sst_re"] for c in range(NCORES)], 0).reshape(1, 128, 64, 64)
    s_hi = np.concatenate([R_[c]["sst_im"] for c in range(NCORES)], 0).reshape(1, 128, 64, 64)
    s_k = np.concatenate([R_[c]["sck"] for c in range(NCORES)], 0).reshape(1, 128, 128, 4, 64)
    s_v = np.concatenate([R_[c]["scv"] for c in range(NCORES)], 0).reshape(1, 128, 128, 4, 64)
    return (y_prompt.astype(np.float32), y_sample.astype(np.float32), p_hr.astype(np.float32), p_hi.astype(np.float32),
            p_k.astype(np.float32), p_v.astype(np.float32), s_hr.astype(np.float32), s_hi.astype(np.float32),
            s_k.astype(np.float32), s_v.astype(np.float32))
```

```python
import math
import os
from contextlib import ExitStack
import numpy as np
import ml_dtypes
import concourse.bass as bass
import concourse.mybir as mybir
from concourse.bass_utils import run_bass_kernel_spmd

F32 = mybir.dt.float32
BF16 = mybir.dt.bfloat16
U8 = mybir.dt.uint8
ALU = mybir.AluOpType
AF = mybir.ActivationFunctionType
AX = mybir.AxisListType

PE, ACT, DVE, POOL, SP = "tensor", "scalar", "vector", "gpsimd", "sync"
ENGS = [PE, ACT, DVE, POOL, SP]

NCORES = 8
D = 1024
NP = 2048
NS = 16
NT = NP + NS
TBS = [(0, 512), (512, 512), (1024, 512), (1536, 512), (2048, 16)]
DIN = 6656
OFF_U, OFF_ZS, OFF_Q, OFF_K, OFF_V, OFF_ZA, OFF_GS, OFF_GA = 0, 1024, 2048, 3072, 3328, 3584, 4608, 5632
ALPHA = 2.0 ** 0.25
LN_EPS = 1e-5
DEBUG = False


class Dep:
    __slots__ = ("sem", "val")

    def __init__(self, sem, val):
        self.sem = sem
        self.val = val


class Buf:
    __slots__ = ("w", "r", "name")

    def __init__(self, name=""):
        self.w = []
        self.r = []
        self.name = name


class _RProxy:
    def __init__(self, bufs):
        self.bufs = bufs

    def append(self, h):
        for b in self.bufs:
            b.r.append(h)

    def __len__(self):
        return 0


class BufGroup:
    def __init__(self, bufs):
        self.bufs = list(bufs)
        self.r = _RProxy(self.bufs)

    @property
    def w(self):
        return [h for b in self.bufs for h in b.w]


def handoff(new_bufs, old_bufs):
    deps = []
    for b in old_bufs:
        deps.extend(b.w)
        deps.extend(b.r)
    for nb in new_bufs:
        nb.r = list(nb.r) + deps


class DSem:
    def __init__(self, h):
        self.h = h
        self.cnt = 0


class Prog:
    def __init__(self, nc, stack):
        self.nc = nc
        self.q = {e: [] for e in ENGS}
        self.esem = {}
        self.cnt = {e: 0 for e in ENGS}
        self.allsems = []
        for e in [PE, ACT, DVE, POOL]:
            self.esem[e] = nc.alloc_semaphore("s_" + e)
            self.allsems.append(self.esem[e])
        self.seen = {}
        self.stack = stack
        self.nd = 0
        self.ninst = {e: 0 for e in ENGS}
        self.dead = False
        self.stop = int(os.environ.get("KSTOP", "99"))

    def phase(self, n):
        self.dead = n > self.stop

    def dsem(self, name=None):
        self.nd += 1
        h = self.nc.alloc_semaphore(f"d{self.nd}_{name or 'm'}")
        self.allsems.append(h)
        return DSem(h)

    def _waits(self, eng, deps):
        best = {}
        for d in deps:
            if d is None:
                continue
            k = id(d.sem)
            if k not in best or best[k].val < d.val:
                best[k] = d
        ws = []
        for d in best.values():
            k = (eng, id(d.sem))
            if self.seen.get(k, 0) >= d.val:
                continue
            self.seen[k] = d.val
            ws.append((d.sem, d.val))
        return ws

    @staticmethod
    def _compact(lst):
        best = {}
        for d in lst:
            k = id(d.sem)
            if k not in best or best[k].val < d.val:
                best[k] = d
        return list(best.values())

    @staticmethod
    def _bufdeps(reads, writes, wadd=()):
        deps = []
        for b in reads:
            deps.extend(b.w)
        for b in writes:
            deps.extend(b.w)
            deps.extend(b.r)
        for b in wadd:
            deps.extend(b.r)
        return deps

    @classmethod
    def _update(cls, h, reads, writes, wadd=()):
        for b in reads:
            b.r.append(h)
            if len(b.r) > 32:
                b.r = cls._compact(b.r)
        for b in writes:
            b.w = [h]
            b.r = []
        for b in wadd:
            b.w.append(h)
            if len(b.w) > 32:
                b.w = cls._compact(b.w)

    def op(self, eng, fn, reads=(), writes=(), deps=(), sig=True, wadd=()):
        if self.dead:
            return None
        alld = list(deps) + self._bufdeps(reads, writes, wadd)
        ws = self._waits(eng, alld)
        h = None
        if sig:
            self.cnt[eng] += 1
            h = Dep(self.esem[eng], self.cnt[eng])
        sem = self.esem[eng] if sig else None
        self.ninst[eng] += 1 + len(ws)

        def run(e, ws=ws, fn=fn, sem=sem):
            for (s, v) in ws:
                e.wait_ge(s, v)
            ins = fn(e)
            if sem is not None:
                ins.then_inc(sem, 1)
        self.q[eng].append(run)
        if h is not None:
            self._update(h, reads, writes, wadd)
        return h

    def attach(self, h, reads=(), writes=(), wadd=()):
        if self.dead or h is None:
            return
        self._update(h, reads, writes, wadd)

    def dma(self, eng, ds, out, in_, reads=(), writes=(), deps=(), wadd=(), **kw):
        if self.dead:
            return None
        alld = list(deps) + self._bufdeps(reads, writes, wadd)
        ws = self._waits(eng, alld)
        ds.cnt += 16
        h = Dep(ds.h, ds.cnt)
        self.ninst[eng] += 1 + len(ws)

        def run(e, ws=ws, out=out, in_=in_, kw=kw, sh=ds.h):
            for (s, v) in ws:
                e.wait_ge(s, v)
            e.dma_start(out=out, in_=in_, **kw).then_inc(sh, 16)
        self.q[eng].append(run)
        self._update(h, reads, writes, wadd)
        return h

    def raw(self, eng, fn, ds, inc, reads=(), writes=(), deps=()):
        if self.dead:
            return None
        alld = list(deps) + self._bufdeps(reads, writes)
        ws = self._waits(eng, alld)
        ds.cnt += inc
        h = Dep(ds.h, ds.cnt)

        def run(e, ws=ws, fn=fn, sh=ds.h, inc=inc):
            for (s, v) in ws:
                e.wait_ge(s, v)
            fn(e).then_inc(sh, inc)
        self.q[eng].append(run)
        self._update(h, reads, writes)
        return h

    def wait(self, eng, deps):
        ws = self._waits(eng, deps)

        def run(e, ws=ws):
            for (s, v) in ws:
                e.wait_ge(s, v)
        self.q[eng].append(run)

    def emit(self):
        nc = self.nc
        with nc.Block() as block:
            @block.tensor
            def _(e):
                for f in self.q[PE]:
                    f(e)

            @block.scalar
            def _(e):
                for f in self.q[ACT]:
                    f(e)

            @block.vector
            def _(e):
                for f in self.q[DVE]:
                    f(e)

            @block.gpsimd
            def _(e):
                for f in self.q[POOL]:
                    f(e)

            @block.sync
            def _(e):
                for f in self.q[SP]:
                    f(e)


def _dsize(dt):
    return {F32: 4, BF16: 2, U8: 1}[dt]


class Arena:
    def __init__(self, nc, stack, nbytes):
        self.t = stack.enter_context(nc.sbuf_tensor("arena", [128, nbytes], U8))
        self.nbytes = nbytes

    def carve(self, off, shape, dt):
        n = int(np.prod(shape)) * _dsize(dt)
        assert off % 4 == 0 and off + n <= self.nbytes, (off, n, self.nbytes)
        v = self.t[:, off:off + n]
        if dt != U8:
            v = v.bitcast(dt)
        if len(shape) > 1:
            names = [f"a{i}" for i in range(len(shape))]
            pat = "p (" + " ".join(names) + ") -> p " + " ".join(names)
            v = v.rearrange(pat, **{names[i]: shape[i] for i in range(len(shape))})
        return v


O_HT = 0
O_HTH = 33024
O_CONST = 35072
O_W = 45312
O_A = 61696
O_B = 94720
O_C = 127744
ARENA = 212000
C_SIZE = ARENA - O_C


def build():
    nc = bass.Bass("TRN2", target_bir_lowering=False)

    def din(name, shape, dt=F32):
        return nc.dram_tensor(name, list(shape), dt, kind="ExternalInput").ap()

    def dout(name, shape, dt=F32):
        return nc.dram_tensor(name, list(shape), dt, kind="ExternalOutput").ap()

    xprev = din("xprev", [3, NP, D]); xp = din("xp", [NP, D]); xh = din("xh", [128, D]); xs = din("xs", [NS, D]); ccin = din("cc", [17, D])
    st_re = din("st_re", [NS, 4096]); st_im = din("st_im", [NS, 4096])
    ck = din("ck", [NS, 128, 256]); cv = din("cv", [NS, 128, 256])
    w_ada = din("w_ada", [D, 3072]); b_ada = din("b_ada", [3072]); w_in = din("w_in", [D, DIN])
    lam_re = din("lam_re", [64, 64]); lam_im = din("lam_im", [64, 64]); log_delta = din("log_delta", [64])
    b_re = din("b_re", [4096, 16]); b_im = din("b_im", [4096, 16])
    c_re = din("c_re", [1024, 64]); c_im = din("c_im", [1024, 64])
    ssm_d = din("ssm_d", [1024]); w_glu = din("w_glu", [D, D]); b_glu = din("b_glu", [D])
    sinks = din("sinks", [16]); rel_bias = din("rel_bias", [32, 16])
    w_bs = din("w_bs", [D, D]); w_ba = din("w_ba", [D, D]); w_out = din("w_out", [D, D])
    ln_g = din("ln_g", [D]); ln_b = din("ln_b", [D])
    rtab = din("rtab", [32, 384]); maskc = din("maskc", [128, 256]); bmaskc = din("bmaskc", [128, 128])
    diagc = din("diagc", [16, 64]); flagsc = din("flags", [32])

    yp = dout("yp", [NP, D]); ys = dout("ys", [NS, D])
    pst_re = dout("pst_re", [32, 128]); pst_im = dout("pst_im", [32, 128])
    pck = dout("pck", [128, 256]); pcv = dout("pcv", [128, 256])
    sst_re = dout("sst_re", [NS, 4096]); sst_im = dout("sst_im", [NS, 4096])
    sck = dout("sck", [NS, 128, 256]); scv = dout("scv", [NS, 128, 256])
    dbg = {}
    if DEBUG:
        dbg["hT"] = dout("dbg_hT", [128, 8, NT], BF16)
        dbg["uT"] = dout("dbg_uT", [128, 8, NT], BF16)
        dbg["pw"] = dout("dbg_pw", [128, 9 * 2 * 32])
        dbg["hend"] = dout("dbg_hend", [128, 2 * 32 * 16])
        dbg["bb"] = dout("dbg_bb", [128, 2 * 32 * 16]); dbg["ccm"] = dout("dbg_ccm", [128, 2 * 32 * 16])
        dbg["R8"] = dout("dbg_R8", [128, 16 * 2 * 32]); dbg["R128"] = dout("dbg_R128", [128, 16 * 2 * 32])
        dbg["A2k"] = dout("dbg_A2k", [128, 3 * 2 * 32])
        dbg["WinL"] = dout("dbg_WinL", [128, 8 * 8 * 2 * 128], BF16)
        dbg["X0"] = dout("dbg_X0", [128, 512])
        dbg["KL"] = dout("dbg_KL", [128, 2 * 8 * 128], BF16); dbg["Ca"] = dout("dbg_Ca", [128, 2 * 4 * 9 * 2 * 32], BF16)
        dbg["Hb"] = dout("dbg_Hb", [128, 2 * 2048], BF16); dbg["Xs"] = dout("dbg_Xs", [128, 2 * 2048])
        dbg["carry"] = dout("dbg_carry", [128, 17 * 2 * 32])
        dbg["yT"] = dout("dbg_yT", [128, 8, NT], BF16)
        dbg["gbs"] = dout("dbg_gbs", [128, 8, NT], BF16)
        dbg["oT"] = dout("dbg_oT", [128, 8, NT], BF16)
        dbg["mT"] = dout("dbg_mT", [128, 8, NT], BF16)
        dbg["modT"] = dout("dbg_modT", [128, 24 * 17])

    ib = nc.dram_tensor("cc_ib", [128, 64], F32, kind="Internal")
    ob = nc.dram_tensor("cc_ob", [NCORES * 128, 64], F32, kind="Internal")

    st = ExitStack()
    with st:
        P = Prog(nc, st)
        AR = Arena(nc, st, ARENA)
        cv_ = AR.carve
        ps = [st.enter_context(nc.psum_tensor(f"ps{i}", [128, 512], F32)) for i in range(8)]
        psb = [Buf(f"ps{i}") for i in range(8)]
        dout_sem = P.dsem("dout")
        out_sems = []

        def osem_new(name):
            d_ = P.dsem(name)
            out_sems.append(d_)
            return d_
        misc_sem = P.dsem("misc")

        def misc_load(eng, out, in_, buf, wadd=False, **kw):
            if P.dead:
                return None
            if wadd:
                return P.dma(eng, P.dsem(), out, in_, wadd=[buf], **kw)
            return P.dma(eng, P.dsem(), out, in_, writes=[buf], **kw)

        hT = cv_(O_HT, [8, NT], BF16)
        hTh = cv_(O_HTH, [8, 128], BF16)
        hT_b = [Buf(f"hT{i}") for i in range(len(TBS))]
        hTh_b = Buf("hTh")
        o = O_CONST
        ident = cv_(o, [128], F32); o += 512
        modT = cv_(o, [24, 17], F32); o += 1664
        op1p = cv_(o, [8, 17], F32); o += 576
        flags = cv_(o, [32], F32); o += 128
        Dm = cv_(o, [8], F32); o += 32
        bglu = cv_(o, [8], F32); o += 32
        ES = cv_(o, [4, 4, 128], BF16); o += 4096
        onesd = cv_(o, [128], BF16); o += 256
        EBself = cv_(o, [16], F32); o += 64
        bmask = cv_(o, [128], F32); o += 512
        ones1 = cv_(o, [128], F32); o += 512
        assert o <= O_CONST + 10240
        b_ident = Buf(); b_modT = Buf(); b_flags = Buf(); b_Dm = Buf(); b_bglu = Buf(); b_ES = Buf()
        b_ones = Buf(); b_EBself = Buf(); b_bmask = Buf()
        wslot = [cv_(O_W + 8192 * i, [8, 512], BF16) for i in range(2)]
        wslot_b = [Buf("w0"), Buf("w1")]
        wsem = [P.dsem("w0"), P.dsem("w1")]
        wctr = [0]
        RA = cv_(O_A, [8, NT], BF16)
        RB = cv_(O_B, [8, NT], BF16)

        rr = {"i": 0}

        def bank():
            i = rr["i"] % 8
            rr["i"] += 1
            return i

        def load_w(src2d, ncols):
            s = wctr[0] % 2
            wctr[0] += 1
            P.dma(POOL, wsem[s], wslot[s][:, :, 0:ncols], src2d.rearrange("(k p) f -> p k f", p=128),
                  writes=[wslot_b[s]])
            return s

        def evac_copy(i, out_ap, in_ap, reads, writes, wadd=()):
            if i % 2 == 0:
                return P.op(ACT, lambda e: e.activation(out=out_ap, in_=in_ap, func=AF.Copy), reads=reads, writes=writes, wadd=wadd)
            return P.op(DVE, lambda e: e.tensor_copy(out=out_ap, in_=in_ap), reads=reads, writes=writes, wadd=wadd)

        def proj_fm(src2d, ncols, rhs_of, rhs_bufs, evac, tbs=TBS):
            s = load_w(src2d, ncols)
            for oc in range(ncols // 128):
                for tbi, (t0, n) in enumerate(tbs):
                    b = bank()
                    for k in range(8):
                        last = (k == 7)
                        P.op(PE, lambda e, b=b, k=k, oc=oc, tbi=tbi, t0=t0, n=n, s=s: e.matmul(
                            ps[b][:, 0:n], lhsT=wslot[s][:, k, oc * 128:(oc + 1) * 128], rhs=rhs_of(k, t0, n),
                            start=(k == 0), stop=(k == 7)),
                            reads=[wslot_b[s], rhs_bufs[tbi]] if k == 0 else [], writes=[psb[b]] if k == 0 else [],
                            sig=last)
                        if last and not P.dead:
                            h = Dep(P.esem[PE], P.cnt[PE])
                            P.attach(h, reads=[wslot_b[s], rhs_bufs[tbi]], writes=[psb[b]])
                    evac(oc, tbi, ps[b][:, 0:n], psb[b])

        def cmul(eng, dst_r, dst_i, xr, xi, yr, yi, t1, t2, bufs_r, bufs_w, tb):
            P.op(eng, lambda e: e.tensor_tensor(out=t1, in0=xr, in1=yr, op=ALU.mult), reads=bufs_r, writes=[tb])
            P.op(eng, lambda e: e.tensor_tensor(out=t2, in0=xi, in1=yi, op=ALU.mult), reads=bufs_r, writes=[tb])
            P.op(eng, lambda e: e.tensor_tensor(out=dst_r, in0=t1, in1=t2, op=ALU.subtract), reads=[tb], writes=bufs_w)
            P.op(eng, lambda e: e.tensor_tensor(out=t1, in0=xr, in1=yi, op=ALU.mult), reads=bufs_r + bufs_w, writes=[tb])
            P.op(eng, lambda e: e.tensor_tensor(out=t2, in0=xi, in1=yr, op=ALU.mult), reads=bufs_r + bufs_w, writes=[tb])
            P.op(eng, lambda e: e.tensor_tensor(out=dst_i, in0=t1, in1=t2, op=ALU.add), reads=[tb], writes=bufs_w)

        P.phase(0)
        P.op(POOL, lambda e: e.memset(ident, 0.0), writes=[b_ident])
        P.op(POOL, lambda e: e.affine_select(out=ident, in_=ident, pattern=[[-1, 128]], compare_op=ALU.not_equal,
                                             fill=1.0, base=0, channel_multiplier=1), writes=[b_ident])
        misc_load(SP, flags, flagsc.rearrange("(o n) -> o n", o=1).to_broadcast([128, 32]), b_flags)
        misc_load(SP, bmask, bmaskc, b_bmask)
        P.op(POOL, lambda e: e.memset(ones1[0:1, :], 1.0), writes=[b_ones])
        P.op(POOL, lambda e: e.memset(onesd[0:1, 0:64], 0.0), wadd=[b_ones])
        P.op(POOL, lambda e: e.memset(onesd[0:1, 64:128], 1.0), wadd=[b_ones])

        P.phase(1)
        c_t = cv_(O_C + 62208, [1024], F32); c_sg = cv_(O_C + 66304, [1024], F32)
        ccT = cv_(O_C + 70400, [8, 17], BF16); badain = cv_(O_C + 70912, [128], F32); badaT = cv_(O_C + 71424, [24], F32)
        b_ct = Buf(); b_csg = Buf(); b_ccT = Buf(); b_bin = Buf(); b_baT = Buf()
        misc_load(SP, c_t[0:17, :], ccin, b_ct)
        misc_load(SP, badain[0:24, :], b_ada.rearrange("(c p) -> c p", p=128), b_bin)
        P.op(ACT, lambda e: e.activation(out=c_sg[0:17, :], in_=c_t[0:17, :], func=AF.Sigmoid), reads=[b_ct], writes=[b_csg])
        P.op(DVE, lambda e: e.tensor_tensor(out=c_sg[0:17, :], in0=c_sg[0:17, :], in1=c_t[0:17, :], op=ALU.mult),
             reads=[b_ct], writes=[b_csg])
        bk = bank()
        for k in range(8):
            P.op(PE, lambda e, k=k: e.transpose(out=ps[bk][:, 17 * k:17 * k + 17], in_=c_sg[0:17, 128 * k:128 * k + 128],
                                                identity=ident[0:17, 0:17]),
                 reads=[b_csg, b_ident], writes=[psb[bk]] if k == 0 else [], wadd=[psb[bk]] if k > 0 else [])
        P.op(DVE, lambda e: e.tensor_copy(out=ccT.rearrange("p k s -> p (k s)"), in_=ps[bk][:, 0:136]), reads=[psb[bk]], writes=[b_ccT])
        bk2 = bank()
        P.op(PE, lambda e: e.transpose(out=ps[bk2][:, 0:24], in_=badain[0:24, :], identity=ident[0:24, 0:24]),
             reads=[b_bin, b_ident], writes=[psb[bk2]])
        P.op(DVE, lambda e: e.tensor_copy(out=badaT, in_=ps[bk2][:, 0:24]), reads=[psb[bk2]], writes=[b_baT])
        bkm = bank()
        hlast = None
        for blk in range(6):
            s = load_w(w_ada[:, 512 * blk:512 * blk + 512], 512)
            for oc in range(4):
                f = 4 * blk + oc
                for k in range(8):
                    first = (blk == 0 and oc == 0 and k == 0)
                    lastk = (k == 7)
                    hlast = P.op(PE, lambda e, f=f, k=k, oc=oc, s=s: e.matmul(
                        ps[bkm][:, 17 * f:17 * f + 17], lhsT=wslot[s][:, k, oc * 128:(oc + 1) * 128], rhs=ccT[:, k, :],
                        start=(k == 0), stop=(k == 7)),
                        reads=[wslot_b[s], b_ccT] if k == 0 else [], writes=[psb[bkm]] if first else [], sig=lastk and oc == 3)
            P.attach(hlast, reads=[wslot_b[s]], wadd=[psb[bkm]])
        P.op(DVE, lambda e: e.tensor_tensor(out=modT, in0=ps[bkm][:, 0:408].rearrange("p (f s) -> p f s", f=24),
                                            in1=badaT.unsqueeze(2).to_broadcast([128, 24, 17]), op=ALU.add),
             reads=[psb[bkm], b_baT], writes=[b_modT])
        P.op(DVE, lambda e: e.tensor_scalar(out=op1p, in0=modT[:, 8:16, :], scalar1=1.0, scalar2=None, op0=ALU.add),
             reads=[b_modT], wadd=[b_modT])

        xst = [cv_(O_C + 71552 + 4096 * i, [1024], F32) for i in range(2)]
        xst_b = [Buf(), Buf()]
        xsem = [P.dsem("x0"), P.dsem("x1")]
        hTs_b = hT_b[4]
        tmpS = cv_(O_C + 79744, [8, 16], F32)
        b_tmpS = Buf()

        def phase_a(xsrc, full):
            tiles = [("p", i) for i in range(16)] + ([("h", 0), ("s", 0)] if full else [])
            for ti, (kind, i) in enumerate(tiles):
                sl = ti % 2
                if kind == "p":
                    src, rows, dst, dbuf = xsrc[128 * i:128 * i + 128, :], 128, (lambda k, i=i: hT[:, k, 128 * i:128 * i + 128]), hT_b[i // 4]
                elif kind == "h":
                    src, rows, dst, dbuf = xh, 128, (lambda k: hTh[:, k, :]), hTh_b
                else:
                    src, rows, dst, dbuf = xs, NS, None, hTs_b
                P.dma(SP, xsem[sl], xst[sl][0:rows, :], src, writes=[xst_b[sl]])
                b0, b1 = bank(), bank()
                for k in range(8):
                    bb_ = b0 if k < 4 else b1
                    j = k % 4
                    P.op(PE, lambda e, k=k, bb_=bb_, j=j, sl=sl, rows=rows: e.transpose(
                        out=ps[bb_][:, rows * j:rows * j + rows], in_=xst[sl][0:rows, 128 * k:128 * k + 128],
                        identity=ident[0:rows, 0:rows]),
                        reads=[xst_b[sl], b_ident], writes=[psb[bb_]] if j == 0 else [], wadd=[psb[bb_]] if j > 0 else [])
                if kind != "s":
                    for k in range(8):
                        bb_ = b0 if k < 4 else b1
                        j = k % 4
                        src_ps = ps[bb_][:, 128 * j:128 * j + 128]
                        if k < 4:
                            P.op(DVE, lambda e, k=k, src_ps=src_ps, dst=dst: e.tensor_scalar(
                                out=dst(k), in0=src_ps, scalar1=op1p[:, k, 0:1], scalar2=modT[:, k, 0:1], op0=ALU.mult, op1=ALU.add),
                                reads=[psb[bb_], b_modT], wadd=[dbuf])
                        else:
                            P.op(ACT, lambda e, k=k, src_ps=src_ps, dst=dst: e.activation(
                                out=dst(k), in_=src_ps, func=AF.Identity, scale=op1p[:, k, 0:1], bias=modT[:, k, 0:1]),
                                reads=[psb[bb_], b_modT], wadd=[dbuf])
                else:
                    for half, bb_ in enumerate([b0, b1]):
                        P.op(DVE, lambda e, half=half, bb_=bb_: e.tensor_tensor(
                            out=tmpS[:, 4 * half:4 * half + 4, :], in0=ps[bb_][:, 0:64].rearrange("p (k s) -> p k s", k=4),
                            in1=op1p[:, 4 * half:4 * half + 4, 1:17], op=ALU.mult), reads=[psb[bb_], b_modT], wadd=[b_tmpS])
                    P.op(DVE, lambda e: e.tensor_tensor(out=hT[:, :, NP:NT], in0=tmpS, in1=modT[:, 0:8, 1:17], op=ALU.add),
                         reads=[b_tmpS, b_modT], wadd=[dbuf])

        uT = RA
        uT_b = [Buf(f"uT{g}") for g in range(8)]
        rhs_h = lambda k, t0, n: hT[:, k, t0:t0 + n]
        cnt = {"i": 0}

        def phase_b(full):
            for blk in range(2):
                def ev(oc, tbi, pap, pb, blk=blk):
                    t0, n = TBS[tbi]
                    g = 4 * blk + oc
                    evac_copy(0, uT[:, g, t0:t0 + n], pap, [pb], [], wadd=[uT_b[g]])
                proj_fm(w_in[:, OFF_U + 512 * blk:OFF_U + 512 * blk + 512], 512, rhs_h, hT_b, ev, tbs=TBS if full else TBS[:4])

        P.phase(3)
        phase_a(xprev[0], False)
        phase_b(False)
        P.phase(2)
        oc_ = O_C
        pw = cv_(oc_ + 0, [9, 2, 32], F32); bb = cv_(oc_ + 2304, [2, 32, 16], F32); ccm = cv_(oc_ + 6400, [2, 32, 16], F32)
        R8 = cv_(oc_ + 10496, [16, 2, 32], F32); R128 = cv_(oc_ + 14592, [16, 2, 32], F32); A2k = cv_(oc_ + 18688, [3, 2, 32], F32)
        Hend = cv_(oc_ + 19456, [2, 32, 16], F32); carry = cv_(oc_ + 23552, [17, 2, 32], F32)
        Sb = cv_(oc_ + 27904, [8, 2, 128], BF16); coef = cv_(oc_ + 32000, [12, 32], F32)
        misc = cv_(oc_ + 33536, [32, 32], F32)
        t1 = cv_(oc_ + 37632, [1024], F32); t2 = cv_(oc_ + 41728, [1024], F32)
        O_S = oc_ + 45824
        Sslot = [cv_(O_S + 8192 * i, [8, 2, 128], F32) for i in range(2)]
        O_CA = oc_ + 62208; O_KL = oc_ + 71424; O_HB = oc_ + 75520
        b_pw = Buf("pw"); b_bb = Buf("bb"); b_ccm = Buf("ccm"); b_R8 = Buf(); b_R128 = Buf(); b_A2k = Buf(); b_coef = Buf()
        b_misc = Buf("misc"); b_t = Buf("t12"); b_Sslot = [Buf("S0"), Buf("S1")]
        craw = [cv_(O_S + 2048 * i, [8, 64], F32) for i in range(2)]
        lamraw = cv_(O_S + 4096, [128], F32); ldraw = cv_(O_S + 4608, [64], F32)
        draw = cv_(O_S + 4864, [128], F32); bgraw = cv_(O_S + 5376, [128], F32)
        braw = [cv_(O_S + 8192 + 2048 * i, [32, 16], F32) for i in range(2)]
        b_craw = Buf(); b_lam = Buf(); b_ld = Buf(); b_draw = Buf(); b_braw = Buf()
        misc_load(SP, craw[0], c_re.rearrange("(t r) p -> r t p", r=128), b_craw, wadd=True)
        misc_load(SP, craw[1], c_im.rearrange("(t r) p -> r t p", r=128), b_craw, wadd=True)
        misc_load(SP, lamraw[0:64, 0:64], lam_re, b_lam, wadd=True)
        misc_load(SP, lamraw[0:64, 64:128], lam_im, b_lam, wadd=True)
        misc_load(SP, ldraw, log_delta.rearrange("(o n) -> o n", o=1).to_broadcast([128, 64]), b_ld)
        misc_load(SP, draw[0:8, :], ssm_d.rearrange("(c p) -> c p", p=128), b_draw, wadd=True)
        misc_load(SP, bgraw[0:8, :], b_glu.rearrange("(c p) -> c p", p=128), b_draw, wadd=True)
        for i, src in enumerate([b_re, b_im]):
            for q4 in range(4):
                misc_load(SP, braw[i][:, 8 * q4:8 * q4 + 8, :],
                          src[1024 * q4:1024 * q4 + 1024, :].rearrange("(gp q) c -> q gp c", q=128), b_braw, wadd=True)
        M_ = lambda i: misc[:, i, :]
        LR, LI, DT, TH, FR, FC, MAG, SN, CS, NR, DEN, CR, CI, G1, KF, TMPA = [M_(i) for i in range(16)]
        KI = misc[:, 16, :].bitcast(mybir.dt.int32)
        bkl = bank()
        P.op(PE, lambda e: e.transpose(out=ps[bkl][:, 0:64], in_=lamraw[0:64, :], identity=ident[0:64, 0:64]),
             reads=[b_lam, b_ident], writes=[psb[bkl]])
        P.op(DVE, lambda e: e.tensor_copy(out=LR[0:64, :], in_=ps[bkl][0:64, 0:64:2]), reads=[psb[bkl]], wadd=[b_misc])
        P.op(DVE, lambda e: e.tensor_copy(out=LR[64:128, :], in_=ps[bkl][0:64, 1:64:2]), reads=[psb[bkl]], wadd=[b_misc])
        P.op(DVE, lambda e: e.tensor_copy(out=LI[0:64, :], in_=ps[bkl][64:128, 0:64:2]), reads=[psb[bkl]], wadd=[b_misc])
        P.op(DVE, lambda e: e.tensor_copy(out=LI[64:128, :], in_=ps[bkl][64:128, 1:64:2]), reads=[psb[bkl]], wadd=[b_misc])
        bkd = bank()
        P.op(PE, lambda e: e.transpose(out=ps[bkd][:, 0:8], in_=draw[0:8, :], identity=ident[0:8, 0:8]),
             reads=[b_draw, b_ident], writes=[psb[bkd]])
        P.op(PE, lambda e: e.transpose(out=ps[bkd][:, 8:16], in_=bgraw[0:8, :], identity=ident[0:8, 0:8]),
             reads=[b_draw, b_ident], wadd=[psb[bkd]])
        P.op(DVE, lambda e: e.tensor_copy(out=Dm, in_=ps[bkd][:, 0:8]), reads=[psb[bkd]], writes=[b_Dm])
        P.op(DVE, lambda e: e.tensor_copy(out=bglu, in_=ps[bkd][:, 8:16]), reads=[psb[bkd]], writes=[b_bglu])
        for ri in range(2):
            for hb in range(2):
                bkc = bank()
                for tt in range(4):
                    t_ = 4 * hb + tt
                    P.op(PE, lambda e, ri=ri, t_=t_, tt=tt, bkc=bkc: e.transpose(
                        out=ps[bkc][0:64, 128 * tt:128 * tt + 128], in_=craw[ri][:, t_, :], identity=ident),
                        reads=[b_craw, b_ident], writes=[psb[bkc]] if tt == 0 else [], wadd=[psb[bkc]] if tt > 0 else [])
                for g2 in range(2):
                    src = ps[bkc][0:64, :].rearrange("p (tg g2 c) -> p tg g2 c", g2=2, c=16)[:, :, g2, :]
                    P.op(DVE, lambda e, ri=ri, hb=hb, g2=g2, src=src: e.tensor_copy(
                        out=ccm[64 * g2:64 * g2 + 64, ri, 16 * hb:16 * hb + 16, :], in_=src),
                        reads=[psb[bkc]], wadd=[b_ccm])
        P.op(ACT, lambda e: e.activation(out=DT[0:64, :], in_=ldraw[0:64, 0:64:2], func=AF.Exp), reads=[b_ld], wadd=[b_misc])
        P.op(ACT, lambda e: e.activation(out=DT[64:128, :], in_=ldraw[64:128, 1:64:2], func=AF.Exp), reads=[b_ld], wadd=[b_misc])
        G = DVE
        tt_ = lambda out, a, b_, op, **kw: P.op(G, lambda e: e.tensor_tensor(out=out, in0=a, in1=b_, op=op), reads=[b_misc], wadd=[b_misc], **kw)
        ts_ = lambda out, a, s1, s2, o0, o1: P.op(G, lambda e: e.tensor_scalar(out=out, in0=a, scalar1=s1, scalar2=s2, op0=o0, op1=o1), reads=[b_misc], wadd=[b_misc])
        tt_(TH, LI, DT, ALU.mult)
        ts_(FR, TH, 1.0 / (2 * math.pi), 0.0, ALU.mult, ALU.add)
        P.op(DVE, lambda e: e.tensor_copy(out=KI, in_=FR), reads=[b_misc], wadd=[b_misc])
        P.op(DVE, lambda e: e.tensor_copy(out=KF, in_=KI), reads=[b_misc], wadd=[b_misc])
        tt_(FR, FR, KF, ALU.subtract)
        ts_(FC, FR, 1.0, 0.25, ALU.mult, ALU.add)
        P.op(DVE, lambda e: e.tensor_single_scalar(out=G1, in_=FC, scalar=0.5, op=ALU.is_gt), reads=[b_misc], wadd=[b_misc])
        tt_(FC, FC, G1, ALU.subtract)
        TWO_PI = 6.283185
        P.op(ACT, lambda e: e.activation(out=SN, in_=FR, func=AF.Sin, scale=TWO_PI), reads=[b_misc], wadd=[b_misc])
        P.op(ACT, lambda e: e.activation(out=CS, in_=FC, func=AF.Sin, scale=TWO_PI), reads=[b_misc], wadd=[b_misc])
        tt_(TMPA, LR, DT, ALU.mult)
        P.op(ACT, lambda e: e.activation(out=MAG, in_=TMPA, func=AF.Exp), reads=[b_misc], wadd=[b_misc])
        P.op(G, lambda e: e.memset(pw[:, 0, 0, :], 1.0), wadd=[b_pw])
        P.op(G, lambda e: e.memset(pw[:, 0, 1, :], 0.0), wadd=[b_pw])
        P.op(G, lambda e: e.tensor_tensor(out=pw[:, 1, 0, :], in0=MAG, in1=CS, op=ALU.mult), reads=[b_misc], wadd=[b_pw])
        P.op(G, lambda e: e.tensor_tensor(out=pw[:, 1, 1, :], in0=MAG, in1=SN, op=ALU.mult), reads=[b_misc], wadd=[b_pw])

        def cm(dst, x, y, n, rb, wb):
            T1 = t1[:, 0:n * 32].rearrange("p (n g) -> p n g", n=n)
            T2 = t2[:, 0:n * 32].rearrange("p (n g) -> p n g", n=n)
            cmul(G, dst[:, :, 0, :], dst[:, :, 1, :], x[:, :, 0, :], x[:, :, 1, :], y[:, :, 0, :], y[:, :, 1, :], T1, T2, rb, wb, b_t)

        def bc(ap1, n):
            return ap1.to_broadcast([128, n, 2, 32])

        cm(pw[:, 2:3], pw[:, 1:2], pw[:, 1:2], 1, [b_pw], [b_pw])
        cm(pw[:, 3:5], pw[:, 1:3], bc(pw[:, 2:3], 2), 2, [b_pw], [b_pw])
        cm(pw[:, 5:9], pw[:, 1:5], bc(pw[:, 4:5], 4), 4, [b_pw], [b_pw])
        tt_(NR, pw[:, 1, 0, :], pw[:, 0, 0, :], ALU.subtract, deps=b_pw.w)
        tt_(DEN, LR, LR, ALU.mult)
        tt_(TMPA, LI, LI, ALU.mult)
        tt_(DEN, DEN, TMPA, ALU.add)
        P.op(DVE, lambda e: e.reciprocal(out=DEN, in_=DEN), reads=[b_misc], wadd=[b_misc])
        tt_(CR, NR, LR, ALU.mult)
        tt_(TMPA, pw[:, 1, 1, :], LI, ALU.mult)
        tt_(CR, CR, TMPA, ALU.add)
        tt_(CR, CR, DEN, ALU.mult)
        tt_(CI, pw[:, 1, 1, :], LR, ALU.mult)
        tt_(TMPA, NR, LI, ALU.mult)
        tt_(CI, CI, TMPA, ALU.subtract)
        tt_(CI, CI, DEN, ALU.mult)
        CRb = CR.unsqueeze(2).to_broadcast([128, 32, 16]); CIb = CI.unsqueeze(2).to_broadcast([128, 32, 16])
        T1b = t1[:, 0:512].rearrange("p (g c) -> p g c", g=32); T2b = t2[:, 0:512].rearrange("p (g c) -> p g c", g=32)
        P.op(G, lambda e: e.tensor_tensor(out=T1b, in0=braw[0], in1=CRb, op=ALU.mult), reads=[b_braw, b_misc, b_pw], writes=[b_t])
        P.op(G, lambda e: e.tensor_tensor(out=T2b, in0=braw[1], in1=CIb, op=ALU.mult), reads=[b_braw, b_misc], wadd=[b_t])
        P.op(G, lambda e: e.tensor_tensor(out=bb[:, 0], in0=T1b, in1=T2b, op=ALU.subtract), reads=[b_t], wadd=[b_bb])
        P.op(G, lambda e: e.tensor_tensor(out=T1b, in0=braw[1], in1=CRb, op=ALU.mult), reads=[b_braw, b_misc, b_bb], writes=[b_t])
        P.op(G, lambda e: e.tensor_tensor(out=T2b, in0=braw[0], in1=CIb, op=ALU.mult), reads=[b_braw, b_misc], wadd=[b_t])
        P.op(G, lambda e: e.tensor_tensor(out=bb[:, 1], in0=T1b, in1=T2b, op=ALU.add), reads=[b_t], wadd=[b_bb])

        def rev_table(R, A0, bufR, out_last):
            AW = misc[:, 20:22, :].rearrange("p (o r) g -> p o r g", o=1)
            AW2 = misc[:, 22:24, :].rearrange("p (o r) g -> p o r g", o=1)
            P.op(G, lambda e: e.memset(R[:, 15, 0, :], 1.0), wadd=[bufR])
            P.op(G, lambda e: e.memset(R[:, 15, 1, :], 0.0), wadd=[bufR])
            P.op(G, lambda e: e.tensor_copy(out=AW, in_=A0), reads=[b_pw, b_coef, b_misc], wadd=[b_misc])
            w = 1
            cur, nxt = AW, AW2
            while w <= 8:
                cm(R[:, 16 - 2 * w:16 - w], R[:, 16 - w:16], bc(cur, w), w, [bufR, b_misc], [bufR])
                cm(nxt, cur, cur, 1, [b_misc], [b_misc])
                cur, nxt = nxt, cur
                w *= 2
            P.op(G, lambda e: e.tensor_copy(out=out_last, in_=cur), reads=[b_misc], wadd=[b_coef])

        A128 = coef[:, 0:2, :].rearrange("p (o r) g -> p o r g", o=1)
        A2048 = coef[:, 2:4, :].rearrange("p (o r) g -> p o r g", o=1)
        rev_table(R8, pw[:, 8:9], b_R8, A128)
        rev_table(R128, A128, b_R128, A2048)
        P.op(G, lambda e: e.memset(A2k[:, 0, 0, :], 1.0), wadd=[b_A2k])
        P.op(G, lambda e: e.memset(A2k[:, 0, 1, :], 0.0), wadd=[b_A2k])
        P.op(G, lambda e: e.tensor_copy(out=A2k[:, 1:2], in_=A2048), reads=[b_coef], wadd=[b_A2k])
        cm(A2k[:, 2:3], A2048, A2048, 1, [b_coef], [b_A2k])
        P.op(G, lambda e: e.tensor_scalar(out=coef[:, 4, :], in0=pw[:, 8, 1, :], scalar1=-1.0, scalar2=0.0, op0=ALU.mult, op1=ALU.add),
             reads=[b_pw], wadd=[b_coef])
        P.op(G, lambda e: e.tensor_scalar(out=coef[:, 5, :], in0=coef[:, 1, :], scalar1=-1.0, scalar2=0.0, op0=ALU.mult, op1=ALU.add),
             reads=[b_coef], wadd=[b_coef])

        P.phase(5)
        WinL = cv_(O_B, [8, 8, 2, 128], BF16)
        b_WinL = [Buf(f"WinL{g}") for g in range(8)]
        b_Sb = Buf("Sb")
        handoff(b_Sslot, [b_craw, b_lam, b_ld, b_draw, b_braw])
        b_SInit = [Buf("SInit0"), Buf("SInit1")]
        for i in range(2):
            P.op(POOL, lambda e, i=i: e.memset(Sslot[i].rearrange("p k r c -> p (k r c)"), 0.0), writes=[b_Sslot[i], b_SInit[i]])
        T1s = t1[:, 0:512].rearrange("p (k m c) -> p k m c", k=8, m=4)
        T2s = t2[:, 0:512].rearrange("p (k m c) -> p k m c", k=8, m=4)
        for gc in range(8):
            sl = gc % 2
            S = Sslot[sl]
            Sv = S.rearrange("p k r (m g c) -> p k r m g c", m=4, g=2)
            prk = pw[:, 0:8, 0, 4 * gc:4 * gc + 4].unsqueeze(3).to_broadcast([128, 8, 4, 16])
            pik = pw[:, 0:8, 1, 4 * gc:4 * gc + 4].unsqueeze(3).to_broadcast([128, 8, 4, 16])
            bbr = bb[:, 0, 4 * gc:4 * gc + 4, :].unsqueeze(1).to_broadcast([128, 8, 4, 16])
            bbi = bb[:, 1, 4 * gc:4 * gc + 4, :].unsqueeze(1).to_broadcast([128, 8, 4, 16])
            for ri in range(2):
                x1, x2 = (bbr, bbi) if ri == 0 else (bbi, bbr)
                op = ALU.subtract if ri == 0 else ALU.add
                P.op(POOL, lambda e, x1=x1, prk=prk: e.tensor_tensor(out=T1s, in0=prk, in1=x1, op=ALU.mult), reads=[b_pw, b_bb], writes=[b_t])
                P.op(POOL, lambda e, x2=x2, pik=pik: e.tensor_tensor(out=T2s, in0=pik, in1=x2, op=ALU.mult), reads=[b_pw, b_bb], wadd=[b_t])
                for g2 in range(2):
                    lo, hi = 64 * g2, 64 * g2 + 64
                    P.op(POOL, lambda e, ri=ri, g2=g2, lo=lo, hi=hi, op=op, Sv=Sv: e.tensor_tensor(
                        out=Sv[lo:hi, :, ri, :, g2, :], in0=T1s[lo:hi], in1=T2s[lo:hi], op=op),
                        reads=[b_t, b_SInit[sl]], wadd=[b_Sslot[sl]])
            P.op(POOL, lambda e, gc=gc, S=S: e.tensor_copy(out=Sb[:, gc], in_=S[:, 0]), reads=[b_Sslot[sl]], wadd=[b_Sb])
            for q4 in range(4):
                bkw = bank()
                for j in range(4):
                    k_, ri_ = (4 * q4 + j) // 2, (4 * q4 + j) % 2
                    P.op(PE, lambda e, bkw=bkw, j=j, k_=k_, ri_=ri_, S=S: e.transpose(
                        out=ps[bkw][:, 128 * j:128 * j + 128], in_=S[:, k_, ri_, :], identity=ident),
                        reads=[b_Sslot[sl], b_ident], writes=[psb[bkw]] if j == 0 else [], wadd=[psb[bkw]] if j > 0 else [])
                dstw = WinL[:, gc, 2 * q4:2 * q4 + 2].rearrange("p k r c -> p (k r c)")
                evac_copy(q4, dstw, ps[bkw][:, 0:512], [psb[bkw]], [], wadd=[b_WinL[gc]])

        t1p = cv_(O_S, [2, 16, 16], F32); t2p = cv_(O_S + 2048, [2, 16, 16], F32); cbp = cv_(O_S + 4096, [2, 16, 16], F32)
        b_p1 = Buf("p1tmp")
        handoff([b_p1], b_Sslot)
        b_Hend = Buf("Hend")

        def x_matmuls(gc, banks):
            for ri in range(2):
                for s_ in range(8):
                    for m in range(4):
                        first = (ri == 0 and s_ == 0)
                        last = (ri == 1 and s_ == 7)
                        bkx = banks[m]
                        P.op(PE, lambda e, m=m, ri=ri, s_=s_, bkx=bkx, gc=gc: e.matmul(
                            ps[bkx][:, 256 * ri:256 * ri + 256], lhsT=WinL[32 * m:32 * m + 32, gc, 7 - s_, ri, :],
                            rhs=uT[32 * m:32 * m + 32, gc, s_:NP:8], start=(s_ == 0), stop=(s_ == 7),
                            tile_position=(32 * m, 0)),
                            reads=[b_WinL[gc], uT_b[gc]] if first else [], writes=[psb[bkx]] if first else [], sig=last)
                        if last and not P.dead:
                            P.attach(Dep(P.esem[PE], P.cnt[PE]), reads=[b_WinL[gc], uT_b[gc]], writes=[psb[bkx]])

        def seg_reduce(src_ap, Rtab, gp0, ngp, out_ap, rbufs, wbuf):
            raise NotImplementedError

        def pass1():
            for gc in range(8):
                banks = [bank() for _ in range(4)]
                x_matmuls(gc, banks)
                if DEBUG and gc == 0 and os.environ.get('KX0'):
                    xdbg = cv_(O_S + 6144, [512], F32); b_xdbg = Buf()
                    P.op(DVE, lambda e: e.tensor_copy(out=xdbg, in_=ps[banks[1]][:, :]), reads=[psb[banks[1]]], writes=[b_xdbg])
                    P.dma(SP, dout_sem, dbg["X0"], xdbg, reads=[b_xdbg])
                for m in range(4):
                    gp = 4 * gc + m
                    X4 = ps[banks[m]][:, :].rearrange("p (r s i) -> p r s i", r=2, s=16)
                    Pr = R8[:, :, 0, gp].unsqueeze(1).unsqueeze(1).to_broadcast([128, 2, 16, 16])
                    Pi = R8[:, :, 1, gp].unsqueeze(1).unsqueeze(1).to_broadcast([128, 2, 16, 16])
                    P.op(DVE, lambda e, X4=X4, Pr=Pr: e.tensor_tensor(out=t1p, in0=X4, in1=Pr, op=ALU.mult),
                         reads=[psb[banks[m]], b_R8], writes=[b_p1])
                    P.op(DVE, lambda e, X4=X4, Pi=Pi: e.tensor_tensor(out=t2p, in0=X4, in1=Pi, op=ALU.mult),
                         reads=[psb[banks[m]], b_R8], wadd=[b_p1])
                    P.op(DVE, lambda e: e.tensor_tensor(out=cbp[:, 0], in0=t1p[:, 0], in1=t2p[:, 1], op=ALU.subtract), reads=[b_p1], wadd=[b_p1])
                    P.op(DVE, lambda e: e.tensor_tensor(out=cbp[:, 1], in0=t2p[:, 0], in1=t1p[:, 1], op=ALU.add), reads=[b_p1], wadd=[b_p1])
                    P.op(DVE, lambda e, gp=gp: e.tensor_reduce(out=Hend[:, :, gp, :], in_=cbp, axis=AX.X, op=ALU.add),
                         reads=[b_p1], wadd=[b_Hend])


        Ecore = cv_(O_S + 6144, [2, 32], F32); Eall = cv_(O_S + 6400, [8, 64], F32)
        te1 = cv_(O_S + 0, [2, 32, 16], F32); te2 = cv_(O_S + 8448, [2, 32, 16], F32)
        b_E = Buf("Ecore"); b_Eall = Buf("Eall"); b_te = b_p1
        handoff([b_E, b_Eall], b_Sslot)
        def ecore(j):
            Qr = R128[:, :, 0, :].rearrange("p i g -> p g i").unsqueeze(1).to_broadcast([128, 2, 32, 16])
            Qi = R128[:, :, 1, :].rearrange("p i g -> p g i").unsqueeze(1).to_broadcast([128, 2, 32, 16])
            P.op(DVE, lambda e: e.tensor_tensor(out=te1, in0=Hend, in1=Qr, op=ALU.mult), reads=[b_Hend, b_R128], writes=[b_te])
            P.op(DVE, lambda e: e.tensor_tensor(out=te2, in0=Hend, in1=Qi, op=ALU.mult), reads=[b_Hend, b_R128], wadd=[b_te])
            P.op(DVE, lambda e: e.tensor_tensor(out=te1[:, 0], in0=te1[:, 0], in1=te2[:, 1], op=ALU.subtract), reads=[b_te], writes=[b_te])
            P.op(DVE, lambda e: e.tensor_tensor(out=te2[:, 0], in0=te2[:, 0], in1=te1[:, 1], op=ALU.add), reads=[b_te], writes=[b_te])
            P.op(DVE, lambda e: e.tensor_reduce(out=Ecore[:, 0, :], in_=te1[:, 0], axis=AX.X, op=ALU.add), reads=[b_te], wadd=[b_E])
            P.op(DVE, lambda e: e.tensor_reduce(out=Ecore[:, 1, :], in_=te2[:, 0], axis=AX.X, op=ALU.add), reads=[b_te], wadd=[b_E])

            if j is not None:
                P.op(DVE, lambda e, j=j: e.tensor_copy(out=Eall[:, j, :], in_=Ecore.rearrange("p r g -> p (r g)")), reads=[b_E], wadd=[b_Eall])

        P.phase(3)
        srcs = [(xprev[1], False), (xprev[2], False), (xp, True)]
        phase_a(*srcs[0])
        for j in range(3):
            pass1()
            ecore(j)
            phase_b(srcs[j][1])
            if j < 2:
                phase_a(*srcs[j + 1])
        P.phase(6)
        pass1()
        P.phase(7)
        Sn = [cv_(O_S + 12544 + 256 * n, [2, 32], F32) for n in range(3)]
        b_Sn = Buf("Sn")
        handoff([b_Sn], b_Sslot)
        for n in range(3):
            Snf = Sn[n].rearrange("p r g -> p (r g)")
            P.op(DVE, lambda e, n=n, Snf=Snf: e.tensor_scalar(out=Snf, in0=Eall[:, n, :], scalar1=flags[:, 1 + n:2 + n], scalar2=None,
                                                            op0=ALU.mult), reads=[b_Eall, b_flags], wadd=[b_Sn])
        b_carry = Buf("carry")
        tq1 = cv_(O_S + 13312, [2, 32], F32); tq2 = cv_(O_S + 13568, [2, 32], F32)
        b_tq = Buf("tq")
        handoff([b_tq], b_Sslot)

        def cmul_small(dst, x, y, rb, wb):
            cmul(DVE, dst[:, 0, :], dst[:, 1, :], x[:, 0, :], x[:, 1, :], y[:, 0, :], y[:, 1, :], tq1[:, 0, :], tq1[:, 1, :], rb, wb, b_tq)

        cmul_small(carry[:, 1], Sn[1], A2k[:, 1], [b_Sn, b_A2k], [b_carry])
        cmul_small(carry[:, 2], Sn[2], A2k[:, 2], [b_Sn, b_A2k, b_carry], [b_carry])
        P.op(DVE, lambda e: e.tensor_tensor(out=Sn[0], in0=Sn[0], in1=carry[:, 1], op=ALU.add), reads=[b_Sn, b_carry], writes=[b_Sn])
        P.op(DVE, lambda e: e.tensor_tensor(out=carry[:, 0], in0=Sn[0], in1=carry[:, 2], op=ALU.add), reads=[b_Sn, b_carry], writes=[b_carry])
        A128v = coef[:, 0:2, :]
        for sg in range(16):
            cmul_small(carry[:, sg + 1], carry[:, sg], A128v, [b_carry, b_coef], [b_carry])
            P.op(DVE, lambda e, sg=sg: e.tensor_tensor(out=carry[:, sg + 1], in0=carry[:, sg + 1], in1=Hend[:, :, :, sg], op=ALU.add),
                 reads=[b_carry, b_Hend], writes=[b_carry])
        pstT = cv_(O_S + 13824, [2, 128], F32)
        b_pst = Buf()
        handoff([b_pst], b_Sslot)
        bkp = bank()
        for ri in range(2):
            P.op(PE, lambda e, ri=ri: e.transpose(out=ps[bkp][0:32, 128 * ri:128 * ri + 128], in_=carry[:, 16, ri, :], identity=ident),
                 reads=[b_carry, b_ident], writes=[psb[bkp]] if ri == 0 else [], wadd=[psb[bkp]] if ri == 1 else [])
        P.op(DVE, lambda e: e.tensor_copy(out=pstT[0:32].rearrange("p r c -> p (r c)"), in_=ps[bkp][0:32, 0:256]), reads=[psb[bkp]], writes=[b_pst])
        P.dma(SP, osem_new("pre"), pst_re, pstT[0:32, 0, :], reads=[b_pst])
        P.dma(SP, osem_new("pim"), pst_im, pstT[0:32, 1, :], reads=[b_pst])

        P.phase(8)
        CaBD = [cv_(O_CA + 4608 * i, [4, 9, 2, 32], BF16) for i in range(2)]
        KLs = [cv_(O_KL + 2048 * i, [8, 128], BF16) for i in range(2)]
        Hb = [cv_(O_HB + 4096 * i, [2, 4, 256], BF16) for i in range(2)]
        Xs = [cv_(O_S + 8192 * i, [2, 4, 256], F32) for i in range(2)]
        b_CaBD = [Buf("Ca0"), Buf("Ca1")]; b_KL = [Buf("KL0"), Buf("KL1")]; b_Hb = [Buf("Hb0"), Buf("Hb1")]; b_Xs = [Buf("Xs0"), Buf("Xs1")]
        old_c = [b_ct, b_csg, b_ccT, b_bin, b_baT, xst_b[0], xst_b[1]]
        handoff(b_CaBD + b_KL + b_Hb, old_c)
        handoff(b_Xs, [b_p1, b_E, b_Eall, b_te, b_Sn, b_tq, b_pst] + b_Sslot)
        KL0all = cv_(O_C + 10496, [8, 128], BF16); Ca1all = cv_(O_C + 14592, [32, 2, 32], BF16)
        b_KL0 = Buf("KL0all"); b_Ca1 = Buf("Ca1all")
        handoff([b_KL0], [b_R8]); handoff([b_Ca1], [b_R128])
        Q1e = [misc[:, 24:28, :].rearrange("p a g -> p (a g)").rearrange("p (r m s) -> p r m s", r=2, m=4),
               misc[:, 0:4, :].rearrange("p a g -> p (a g)").rearrange("p (r m s) -> p r m s", r=2, m=4)]
        Q2e = [misc[:, 28:32, :].rearrange("p a g -> p (a g)").rearrange("p (r m s) -> p r m s", r=2, m=4),
               misc[:, 4:8, :].rearrange("p a g -> p (a g)").rearrange("p (r m s) -> p r m s", r=2, m=4)]
        tmpK = misc[:, 16:20, :].rearrange("p a g -> p (a g)")
        b_Q = [Buf("Qdve"), Buf("Qpool")]
        b_tmpK = Buf("tmpK")
        handoff([b_tmpK] + b_Q, [b_misc])
        b_CaInit = [Buf("CaInit0"), Buf("CaInit1")]
        for i in range(2):
            P.op(POOL, lambda e, i=i: e.memset(CaBD[i].rearrange("p m n r c -> p (m n r c)"), 0.0), writes=[b_CaBD[i], b_CaInit[i]])
        U1 = t1[:, 0:576].rearrange("p (m n c) -> p m n c", m=4, n=9)
        U2 = t2[:, 0:576].rearrange("p (m n c) -> p m n c", m=4, n=9)
        XB = [2, 3, 4, 5]
        YB = [6, 7]

        def emit_consts(gc):
            sl = gc % 2
            Ca = CaBD[sl]
            cre = ccm[:, 0, 4 * gc:4 * gc + 4, :].unsqueeze(2).to_broadcast([128, 4, 9, 16])
            cim = ccm[:, 1, 4 * gc:4 * gc + 4, :].unsqueeze(2).to_broadcast([128, 4, 9, 16])
            pr = pw[:, :, 0, 4 * gc:4 * gc + 4].rearrange("p n m -> p m n").unsqueeze(3).to_broadcast([128, 4, 9, 16])
            pi = pw[:, :, 1, 4 * gc:4 * gc + 4].rearrange("p n m -> p m n").unsqueeze(3).to_broadcast([128, 4, 9, 16])
            P.op(DVE, lambda e: e.tensor_tensor(out=U1, in0=cre, in1=pr, op=ALU.mult), reads=[b_ccm, b_pw], writes=[b_t])
            P.op(DVE, lambda e: e.tensor_tensor(out=U2, in0=cim, in1=pi, op=ALU.mult), reads=[b_ccm, b_pw], wadd=[b_t])
            for g2 in range(2):
                lo, hi = 64 * g2, 64 * g2 + 64
                P.op(DVE, lambda e, lo=lo, hi=hi, g2=g2: e.tensor_tensor(
                    out=Ca[lo:hi, :, :, 0, 16 * g2:16 * g2 + 16], in0=U1[lo:hi], in1=U2[lo:hi], op=ALU.subtract),
                    reads=[b_t, b_CaInit[sl]], wadd=[b_CaBD[sl]])
            P.op(DVE, lambda e: e.tensor_tensor(out=U1, in0=cre, in1=pi, op=ALU.mult), reads=[b_ccm, b_pw], writes=[b_t])
            P.op(DVE, lambda e: e.tensor_tensor(out=U2, in0=cim, in1=pr, op=ALU.mult), reads=[b_ccm, b_pw], wadd=[b_t])
            P.op(DVE, lambda e: e.tensor_tensor(out=U1, in0=U1, in1=U2, op=ALU.add), reads=[b_t], writes=[b_t])
            for g2 in range(2):
                lo, hi = 64 * g2, 64 * g2 + 64
                P.op(DVE, lambda e, lo=lo, hi=hi, g2=g2: e.tensor_scalar(
                    out=Ca[lo:hi, :, :, 1, 16 * g2:16 * g2 + 16], in0=U1[lo:hi], scalar1=-1.0, scalar2=0.0, op0=ALU.mult, op1=ALU.add),
                    reads=[b_t, b_CaInit[sl]], wadd=[b_CaBD[sl]])
            P.op(DVE, lambda e: e.tensor_copy(out=Ca1all[:, 4 * gc:4 * gc + 4], in_=Ca[:, :, 1, :, :]), reads=[b_CaBD[sl]], wadd=[b_Ca1])
            for hb in range(2):
                for tt in range(4):
                    tau = 4 * hb + tt
                    for ri in range(2):
                        P.op(PE, lambda e, hb=hb, tt=tt, tau=tau, ri=ri: e.matmul(
                            ps[hb][:, 128 * tt:128 * tt + 128], lhsT=Sb[:, gc, ri, :], rhs=Ca[:, :, tau, ri, :],
                            start=(ri == 0), stop=(ri == 1)),
                            reads=[b_Sb, b_CaBD[sl]] if (tt == 0 and ri == 0) else [],
                            writes=[psb[hb]] if (tt == 0 and ri == 0) else [], sig=(tt == 3 and ri == 1))
                if not P.dead:
                    P.attach(Dep(P.esem[PE], P.cnt[PE]), reads=[b_Sb, b_CaBD[sl]], writes=[psb[hb]])
            KL = KLs[sl]
            bmb3 = bmask.unsqueeze(1).to_broadcast([128, 3, 128]); bmb4 = bmask.unsqueeze(1).to_broadcast([128, 4, 128])
            P.op(DVE, lambda e: e.tensor_tensor(out=KL[:, 1:4, :], in0=ps[0][:, 128:512].rearrange("p (t c) -> p t c", t=3), in1=bmb3, op=ALU.mult),
                 reads=[psb[0], b_bmask], wadd=[b_KL[sl]])
            P.op(DVE, lambda e: e.tensor_tensor(out=tmpK, in0=ps[0][:, 0:128], in1=bmask, op=ALU.mult), reads=[psb[0], b_bmask], writes=[b_tmpK])
            P.op(DVE, lambda e: e.tensor_tensor(out=KL[:, 4:8, :], in0=ps[1][:, 0:512].rearrange("p (t c) -> p t c", t=4), in1=bmb4, op=ALU.mult),
                 reads=[psb[1], b_bmask], wadd=[b_KL[sl]])
            P.op(DVE, lambda e: e.scalar_tensor_tensor(out=KL[:, 0, :], in0=ident, scalar=Dm[:, gc:gc + 1], in1=tmpK, op0=ALU.mult, op1=ALU.add),
                 reads=[b_tmpK, b_ident, b_Dm], wadd=[b_KL[sl]])
            P.op(DVE, lambda e: e.tensor_copy(out=KL0all[:, gc, :], in_=KL[:, 0, :]), reads=[b_KL[sl]], wadd=[b_KL0])

        def emit_x_scan(gc):
            sl = gc % 2
            x_matmuls(gc, XB)
            X = Xs[sl]
            for m in range(4):
                P.op(ACT, lambda e, m=m: e.activation(out=X[:, :, m, :], in_=ps[XB[m]][:, :].rearrange("p (r j) -> p r j", r=2), func=AF.Copy),
                     reads=[psb[XB[m]]], wadd=[b_Xs[sl]])
            E, qi = (POOL, 1) if gc in (1, 4, 6) else (DVE, 0)
            Q1 = Q1e[qi]; Q2 = Q2e[qi]
            X5 = X.rearrange("p r m (s i) -> p r m s i", i=16)
            Ar = pw[:, 8, 0, 4 * gc:4 * gc + 4].unsqueeze(1).unsqueeze(3).to_broadcast([128, 2, 4, 16])
            Ai = pw[:, 8, 1, 4 * gc:4 * gc + 4].unsqueeze(2).to_broadcast([128, 4, 16])
            AiN = coef[:, 4, 4 * gc:4 * gc + 4].unsqueeze(2).to_broadcast([128, 4, 16])
            cview = carry[:, 0:16, :, 4 * gc:4 * gc + 4].rearrange("p s r m -> p r m s")
            for i in range(16):
                prev = cview if i == 0 else X5[:, :, :, :, i - 1]
                cur = X5[:, :, :, :, i]
                rb = [b_carry, b_pw, b_coef, b_Xs[sl]]
                P.op(E, lambda e, prev=prev: e.tensor_tensor(out=Q1, in0=prev, in1=Ar, op=ALU.mult), reads=rb, writes=[b_Q[qi]])
                P.op(E, lambda e, prev=prev: e.tensor_tensor(out=Q2[:, 0], in0=prev[:, 1], in1=AiN, op=ALU.mult), reads=rb, wadd=[b_Q[qi]])
                P.op(E, lambda e, prev=prev: e.tensor_tensor(out=Q2[:, 1], in0=prev[:, 0], in1=Ai, op=ALU.mult), reads=rb, wadd=[b_Q[qi]])
                P.op(E, lambda e, cur=cur: e.tensor_tensor(out=cur, in0=cur, in1=Q1, op=ALU.add), reads=[b_Q[qi]], writes=[b_Xs[sl]])
                P.op(E, lambda e, cur=cur: e.tensor_tensor(out=cur, in0=cur, in1=Q2, op=ALU.add), reads=[b_Q[qi]], writes=[b_Xs[sl]])
            H5 = Hb[sl].rearrange("p r m (s i) -> p r m s i", i=16)
            P.op(ACT, lambda e: e.activation(out=H5[:, :, :, :, 1:16].rearrange("p r m s i -> p (r m) s i"),
                                             in_=X5[:, :, :, :, 0:15].rearrange("p r m s i -> p (r m) s i"), func=AF.Copy),
                 reads=[b_Xs[sl]], writes=[b_Hb[sl]])
            P.op(ACT, lambda e: e.activation(out=H5[:, :, :, :, 0], in_=cview, func=AF.Copy), reads=[b_carry], wadd=[b_Hb[sl]])

        def emit_y(gc):
            sl = gc % 2
            KL = KLs[sl]; Ca = CaBD[sl]
            uview = uT[:, gc, 0:NP].rearrange("p (j s) -> p s j", s=8)
            for half in (1, 0):
                for tl in range(4):
                    t_lo = 4 * half + tl
                    bk_ = YB[tl // 2]
                    reg = ps[bk_][:, 256 * (tl % 2):256 * (tl % 2) + 256]
                    n_mm = (t_lo + 1) + 8
                    idx = 0
                    for s_ in range(t_lo + 1):
                        P.op(PE, lambda e, reg=reg, s_=s_, t_lo=t_lo: e.matmul(
                            reg, lhsT=KL[:, t_lo - s_, :], rhs=uview[:, s_, :], start=(s_ == 0), stop=False),
                            reads=[b_KL[sl], uT_b[gc], b_CaBD[sl], b_Hb[sl]] if idx == 0 else [],
                            writes=[psb[bk_]] if (idx == 0 and tl % 2 == 0) else [], sig=False)
                        idx += 1
                    for m in range(4):
                        for ri in range(2):
                            lastmm = (m == 3 and ri == 1)
                            P.op(PE, lambda e, reg=reg, m=m, ri=ri, t_lo=t_lo, lastmm=lastmm: e.matmul(
                                reg[32 * m:32 * m + 32, :], lhsT=Ca[:, m, t_lo + 1, ri, :], rhs=Hb[sl][:, ri, m, :],
                                start=False, stop=(ri == 1), tile_position=(0, 32 * m)), sig=lastmm)
                    if not P.dead:
                        P.attach(Dep(P.esem[PE], P.cnt[PE]), reads=[b_KL[sl], uT_b[gc], b_CaBD[sl], b_Hb[sl]],
                                 writes=[psb[bk_]] if tl % 2 == 1 else [], wadd=[psb[bk_]] if tl % 2 == 0 else [])
                for bi in range(2):
                    t0_ = 4 * half + 2 * bi
                    P.op(ACT, lambda e, bi=bi, t0_=t0_: e.activation(
                        out=uview[:, t0_:t0_ + 2, :], in_=ps[YB[bi]][:, :].rearrange("p (t j) -> p t j", t=2), func=AF.Gelu_apprx_tanh),
                        reads=[psb[YB[bi]]], writes=[uT_b[gc]])

        for step in range(9):
            if step < 8:
                emit_consts(step)
                emit_x_scan(step)
            if step >= 1:
                emit_y(step - 1)
        yT = uT
        yT_b = uT_b

        P.phase(9)
        all_ssm_tmp = b_Xs + b_Hb + b_Q + [b_tmpK, b_t, b_p1, b_E, b_Eall, b_te, b_Sn, b_tq, b_pst] + b_Sslot
        stile = [cv_(O_S + 2048 * i, [512], F32) for i in range(2)]
        Hsp = cv_(O_S + 4096, [2, 32, 16], F32); Hn = cv_(O_S + 8192, [2, 32, 16], F32)
        HbS = cv_(O_S + 12288, [2, 32, 16], BF16); Q1s = cv_(O_HB, [2, 32, 16], F32); Q2s = cv_(O_HB + 4096, [2, 32, 16], F32)
        b_stile = Buf(); b_Hsp = Buf(); b_Hn = Buf(); b_HbS = Buf(); b_Qs = Buf()
        handoff([b_stile, b_Hsp, b_Hn, b_HbS, b_Qs], all_ssm_tmp)
        misc_load(SP, stile[0], st_re.rearrange("s (gh f) -> (s gh) f", gh=8), b_stile, wadd=True)
        misc_load(SP, stile[1], st_im.rearrange("s (gh f) -> (s gh) f", gh=8), b_stile, wadd=True)
        for ri in range(2):
            bks = bank()
            for q4 in range(4):
                P.op(PE, lambda e, ri=ri, q4=q4, bks=bks: e.transpose(out=ps[bks][:, 128 * q4:128 * q4 + 128],
                                                                   in_=stile[ri][:, 128 * q4:128 * q4 + 128], identity=ident),
                     reads=[b_stile, b_ident], writes=[psb[bks]] if q4 == 0 else [], wadd=[psb[bks]] if q4 > 0 else [])
            for q4 in range(4):
                P.op(DVE, lambda e, ri=ri, q4=q4, bks=bks: e.tensor_copy(
                    out=Hsp[:, ri, q4:32:4, :], in_=ps[bks][:, 128 * q4:128 * q4 + 128].rearrange("p (s gh) -> p gh s", gh=8)),
                    reads=[psb[bks]], wadd=[b_Hsp])
        P.op(POOL, lambda e: e.tensor_copy(out=HbS, in_=Hsp), reads=[b_Hsp], writes=[b_HbS])
        xsb = [bank() for _ in range(4)]
        for m in range(4):
            for gc in range(8):
                for ri in range(2):
                    first = (gc == 0 and ri == 0); last = (gc == 7 and ri == 1)
                    P.op(PE, lambda e, m=m, gc=gc, ri=ri: e.matmul(
                        ps[xsb[m]][:, 32 * gc + 16 * ri:32 * gc + 16 * ri + 16], lhsT=WinL[32 * m:32 * m + 32, gc, 0, ri, :],
                        rhs=uT[32 * m:32 * m + 32, gc, NP:NT], start=True, stop=True, tile_position=(32 * m, 0)),
                        reads=b_WinL + uT_b if first else [], writes=[psb[xsb[m]]] if first else [], sig=last)
            if not P.dead:
                P.attach(Dep(P.esem[PE], P.cnt[PE]), reads=b_WinL + uT_b, writes=[psb[xsb[m]]])
        Ar1 = pw[:, 1, 0, :].unsqueeze(1).unsqueeze(3).to_broadcast([128, 2, 32, 16])
        Ai1 = pw[:, 1, 1, :].unsqueeze(2).to_broadcast([128, 32, 16])
        P.op(POOL, lambda e: e.tensor_scalar(out=coef[:, 6, :], in0=pw[:, 1, 1, :], scalar1=-1.0, scalar2=0.0, op0=ALU.mult, op1=ALU.add),
             reads=[b_pw], wadd=[b_coef])
        AiN1 = coef[:, 6, :].unsqueeze(2).to_broadcast([128, 32, 16])
        P.op(DVE, lambda e: e.tensor_tensor(out=Q1s, in0=Hsp, in1=Ar1, op=ALU.mult), reads=[b_Hsp, b_pw], writes=[b_Qs])
        P.op(DVE, lambda e: e.tensor_tensor(out=Q2s[:, 0], in0=Hsp[:, 1], in1=AiN1, op=ALU.mult), reads=[b_Hsp, b_coef], wadd=[b_Qs])
        P.op(DVE, lambda e: e.tensor_tensor(out=Q2s[:, 1], in0=Hsp[:, 0], in1=Ai1, op=ALU.mult), reads=[b_Hsp, b_pw], wadd=[b_Qs])
        P.op(DVE, lambda e: e.tensor_tensor(out=Hn, in0=Q1s, in1=Q2s, op=ALU.add), reads=[b_Qs], writes=[b_Hn])
        for m in range(4):
            P.op(DVE, lambda e, m=m: e.tensor_tensor(
                out=Hn[:, :, m:32:4, :], in0=ps[xsb[m]][:, 0:256].rearrange("p (gc r s) -> p r gc s", gc=8, r=2),
                in1=Hn[:, :, m:32:4, :], op=ALU.add), reads=[psb[xsb[m]], b_Hn], writes=[b_Hn])
        stg = cv_(O_HB + 8192 - 8192, [4, 128], F32)
        sout = [cv_(O_S + 2048 * i, [512], F32) for i in range(2)]
        b_stg = Buf(); b_sout = Buf()
        handoff([b_stg], [b_Qs]); handoff([b_sout], [b_stile])
        for ri in range(2):
            for q4 in range(4):
                P.op(POOL, lambda e, ri=ri, q4=q4: e.tensor_copy(out=stg[:, q4, :].rearrange("p (s gh) -> p gh s", gh=8),
                                                               in_=Hn[:, ri, q4:32:4, :]), reads=[b_Hn], writes=[b_stg] if q4 == 0 else [],
                     wadd=[b_stg] if q4 > 0 else [])
            bks = bank()
            for q4 in range(4):
                P.op(PE, lambda e, q4=q4, bks=bks: e.transpose(out=ps[bks][:, 128 * q4:128 * q4 + 128], in_=stg[:, q4, :], identity=ident),
                     reads=[b_stg, b_ident], writes=[psb[bks]] if q4 == 0 else [], wadd=[psb[bks]] if q4 > 0 else [])
            P.op(DVE, lambda e, ri=ri, bks=bks: e.tensor_copy(out=sout[ri], in_=ps[bks][:, 0:512]), reads=[psb[bks]], wadd=[b_sout])
            P.dma(SP, osem_new(f"sst{ri}"), (sst_re if ri == 0 else sst_im).rearrange("s (gh f) -> (s gh) f", gh=8), sout[ri], reads=[b_sout])
        bky = bank()
        for gc in range(8):
            reg = ps[bky][:, 16 * gc:16 * gc + 16]
            P.op(PE, lambda e, gc=gc, reg=reg: e.matmul(reg, lhsT=KL0all[:, gc, :], rhs=uT[:, gc, NP:NT], start=True, stop=False),
                 reads=[b_KL0, b_Ca1, b_HbS] + uT_b if gc == 0 else [], writes=[psb[bky]] if gc == 0 else [], sig=False)
            for m in range(4):
                for ri in range(2):
                    lastmm = (m == 3 and ri == 1)
                    P.op(PE, lambda e, gc=gc, reg=reg, m=m, ri=ri, lastmm=lastmm: e.matmul(
                        reg[32 * m:32 * m + 32, :], lhsT=Ca1all[:, 4 * gc + m, ri, :], rhs=HbS[:, ri, 4 * gc + m, :],
                        start=False, stop=(ri == 1), tile_position=(0, 32 * m)), sig=(lastmm and gc == 7))
        if not P.dead:
            P.attach(Dep(P.esem[PE], P.cnt[PE]), reads=[b_KL0, b_Ca1, b_HbS] + uT_b, writes=[psb[bky]])
        P.op(ACT, lambda e: e.activation(out=uT[:, :, NP:NT], in_=ps[bky][:, 0:128].rearrange("p (g s) -> p g s", g=8),
                                         func=AF.Gelu_apprx_tanh), reads=[psb[bky]], writes=uT_b)

        P.phase(10)
        s2T = RB
        s2_b = [Buf(f"s2_{g}") for g in range(8)]
        handoff(s2_b, b_WinL)
        gtmp = [cv_(O_C + 1024 * i, [512], BF16) for i in range(4)]
        ftmp = [cv_(O_C + 4096 + 2048 * i, [512], F32) for i in range(2)]
        b_gtmp = [Buf() for _ in range(4)]; b_ftmp = [Buf(), Buf()]
        handoff(b_gtmp + b_ftmp, [b_pw, b_bb, b_ccm])
        rhs_y = lambda k, t0, n: yT[:, k, t0:t0 + n]
        yall_b = [yT_b] * 5
        ctr = {"i": 0}

        class AllOf:
            pass
        for blk in range(2):
            def ev_glu(oc, tbi, pap, pb, blk=blk):
                t0, n = TBS[tbi]
                g = 4 * blk + oc
                ctr["i"] += 1
                gi = ctr["i"] % 4
                P.op(ACT, lambda e: e.activation(out=gtmp[gi][:, 0:n], in_=pap, func=AF.Sigmoid, bias=bglu[:, g:g + 1], scale=1.0),
                     reads=[pb, b_bglu], writes=[b_gtmp[gi]])
                P.op(DVE, lambda e: e.tensor_tensor(out=s2T[:, g, t0:t0 + n], in0=yT[:, g, t0:t0 + n], in1=gtmp[gi][:, 0:n], op=ALU.mult),
                     reads=[b_gtmp[gi], yT_b[g]], wadd=[s2_b[g]])
            proj_fm(w_glu[:, 512 * blk:512 * blk + 512], 512, rhs_y, [BufGroup(yT_b)] * 5, ev_glu)

            def ev_zs(oc, tbi, pap, pb, blk=blk):
                t0, n = TBS[tbi]
                g = 4 * blk + oc
                ctr["i"] += 1
                gi = ctr["i"] % 4
                fi = ctr["i"] % 2
                P.op(ACT, lambda e: e.activation(out=gtmp[gi][:, 0:n], in_=pap, func=AF.Sigmoid), reads=[pb], writes=[b_gtmp[gi]])
                P.op(DVE, lambda e: e.tensor_tensor(out=ftmp[fi][:, 0:n], in0=pap, in1=gtmp[gi][:, 0:n], op=ALU.mult),
                     reads=[pb, b_gtmp[gi]], writes=[b_ftmp[fi]])
                P.op(POOL, lambda e: e.tensor_tensor(out=s2T[:, g, t0:t0 + n], in0=s2T[:, g, t0:t0 + n], in1=ftmp[fi][:, 0:n], op=ALU.mult),
                     reads=[b_ftmp[fi], s2_b[g]], wadd=[s2_b[g]])
            proj_fm(w_in[:, OFF_ZS + 512 * blk:OFF_ZS + 512 * blk + 512], 512, rhs_h, hT_b, ev_zs)

        P.phase(11)
        gbs = RA
        gbs_b = [Buf(f"gbs{g}") for g in range(8)]
        handoff(gbs_b, yT_b)
        rhs_s2 = lambda k, t0, n: s2T[:, k, t0:t0 + n]
        for blk in range(2):
            def ev_gs(oc, tbi, pap, pb, blk=blk):
                t0, n = TBS[tbi]
                g = 4 * blk + oc
                P.op(ACT, lambda e: e.activation(out=gbs[:, g, t0:t0 + n], in_=pap, func=AF.Sigmoid), reads=[pb], wadd=[gbs_b[g]])
            proj_fm(w_in[:, OFF_GS + 512 * blk:OFF_GS + 512 * blk + 512], 512, rhs_h, hT_b, ev_gs)

            def ev_bs(oc, tbi, pap, pb, blk=blk):
                t0, n = TBS[tbi]
                g = 4 * blk + oc
                P.op(DVE, lambda e: e.tensor_tensor(out=gbs[:, g, t0:t0 + n], in0=pap, in1=gbs[:, g, t0:t0 + n], op=ALU.mult),
                     reads=[pb, gbs_b[g]], wadd=[gbs_b[g]])
            proj_fm(w_bs[:, 512 * blk:512 * blk + 512], 512, rhs_s2, [BufGroup(s2_b)] * 5, ev_bs)

        P.phase(12)
        oT = RB
        oT_b = [Buf(f"oT{g}") for g in range(8)]
        handoff(oT_b, s2_b)
        oc_ = O_C
        qT = cv_(oc_ + 0, [2, NT], BF16); kT2 = cv_(oc_ + 8256, [128 + NT], BF16); Vaug = cv_(oc_ + 12640, [18, 128], BF16)
        EB = cv_(oc_ + 17248, [2, 16, 128], BF16); EB0 = cv_(oc_ + 25440, [16, 128], BF16)
        Et = [cv_(oc_ + 29536 + 1024 * i, [512], BF16) for i in range(4)]
        PT = [cv_(oc_ + 33632 + 1024 * i, [512], BF16) for i in range(4)]
        rc = [cv_(oc_ + 37728 + 2048 * i, [512], F32) for i in range(2)]
        maskt = cv_(oc_ + 41824, [2, 128], F32); RT = cv_(oc_ + 42848, [384], F32); relb = cv_(oc_ + 44384, [16], F32)
        es16 = cv_(oc_ + 44448, [16], F32); klast = cv_(oc_ + 44512, [256], F32); vlast = cv_(oc_ + 45536, [256], F32)
        knew = cv_(oc_ + 46560, [256], F32); vnew = cv_(oc_ + 47584, [256], F32)
        Kc = cv_(oc_ + 48608, [16, 256], F32)
        KcT = cv_(oc_ + 64992, [16, 2, 128], BF16)
        Vcs = cv_(oc_ + 73184, [16, 256], BF16)
        QsT = cv_(oc_ + 81376, [2, 4, 16], BF16)
        dgt = cv_(oc_ + 81632, [64], F32); vnb = cv_(oc_ + 81888, [256], BF16); pdg = cv_(oc_ + 82400, [64], BF16)
        esr = cv_(oc_ + 82528, [64], BF16); ebs = cv_(oc_ + 82656, [16], F32); rcs = cv_(oc_ + 82720, [128], F32)
        ptS = cv_(oc_ + 83232, [128], BF16); ones_k = cv_(oc_ + 83488, [128], BF16)
        attn_bufs = {n: Buf(n) for n in ["qT", "kT2", "Vaug", "EB", "EB0", "mask", "RT", "relb", "es16", "klast", "vlast", "knew", "vnew",
                                         "Kc", "KcT", "Vcs", "QsT", "dgt", "vnb", "pdg", "esr", "ebs", "rcs", "ptS", "ones_k"]}
        A = attn_bufs
        b_Et = [Buf() for _ in range(4)]; b_PT = [Buf() for _ in range(4)]; b_rc = [Buf(), Buf()]
        prev_c = [b_pw, b_bb, b_ccm, b_R8, b_R128, b_A2k, b_Hend, b_carry, b_Sb, b_coef, b_misc, b_t, b_KL0, b_Ca1,
                  b_stile, b_Hsp, b_Hn, b_HbS, b_Qs, b_stg, b_sout] + b_gtmp + b_ftmp + all_ssm_tmp + b_CaBD + b_KL
        handoff(list(A.values()) + b_Et + b_PT + b_rc, prev_c)
        misc_load(SP, RT[0:32, :], rtab, A["RT"]); misc_load(SP, relb[0:32, :], rel_bias, A["relb"])
        misc_load(SP, maskt.rearrange("p h q -> p (h q)"), maskc, A["mask"])
        misc_load(SP, es16[0:1, :], sinks.rearrange("(o n) -> o n", o=1), A["es16"])
        misc_load(SP, ebs[0:16, :], rel_bias[0:1, :].to_broadcast([16, 16]), A["ebs"])
        misc_load(SP, dgt[0:16, :], diagc, A["dgt"])
        P.op(ACT, lambda e: e.activation(out=es16[0:1, :], in_=es16[0:1, :], func=AF.Exp), reads=[A["es16"]], writes=[A["es16"]])
        P.op(ACT, lambda e: e.activation(out=ebs[0:16, :], in_=ebs[0:16, :], func=AF.Exp), reads=[A["ebs"]], writes=[A["ebs"]])
        for kv in range(4):
            for sl_, i in enumerate([0, 2, 1, 3]):
                h = 4 * kv + i
                P.op(DVE, lambda e, kv=kv, sl_=sl_, h=h: e.tensor_copy(out=ES[0:1, kv, sl_, :], in_=es16[0:1, h:h + 1].to_broadcast([1, 128])),
                     reads=[A["es16"]], wadd=[b_ES])
        P.op(POOL, lambda e: e.memset(ones_k, 1.0), writes=[A["ones_k"]])
        for half in range(2):
            for qb in range(4):
                bke = bank()
                for qq in range(32):
                    q = 32 * qb + qq
                    st_ = (127 - q) if half == 0 else (255 - q)
                    P.op(PE, lambda e, bke=bke, qq=qq, st_=st_: e.matmul(ps[bke][:, 16 * qq:16 * qq + 16], lhsT=RT[0:32, st_:st_ + 128],
                                                                       rhs=relb[0:32, :], start=True, stop=True),
                         reads=[A["RT"], A["relb"]] if qq == 0 else [], writes=[psb[bke]] if qq == 0 else [], sig=(qq == 31))
                if not P.dead:
                    P.attach(Dep(P.esem[PE], P.cnt[PE]), reads=[A["RT"], A["relb"]], writes=[psb[bke]])
                P.op(ACT, lambda e, bke=bke, half=half, qb=qb: e.activation(
                    out=EB[:, half, :, 32 * qb:32 * qb + 32], in_=ps[bke][:, 0:512].rearrange("p (q h) -> p h q", h=16), func=AF.Exp),
                    reads=[psb[bke]], wadd=[A["EB"]])
        P.op(DVE, lambda e: e.tensor_tensor(out=EB, in0=EB, in1=maskt.unsqueeze(2).to_broadcast([128, 2, 16, 128]), op=ALU.mult),
             reads=[A["EB"], A["mask"]], writes=[A["EB"]])
        P.op(DVE, lambda e: e.tensor_scalar(out=EB0, in0=EB[:, 0], scalar1=flags[:, 0:1], scalar2=None, op0=ALU.mult),
             reads=[A["EB"], b_flags], writes=[A["EB0"]])
        P.dma(SP, dout_sem, sck[:, 0:127, :], ck[:, 1:128, :])
        P.dma(SP, dout_sem, scv[:, 0:127, :], cv[:, 1:128, :])
        kc_sem = P.dsem("kc")
        P.dma(SP, kc_sem, Kc, ck.rearrange("s t f -> t s f"), writes=[A["Kc"]])
        for s_ in range(NS):
            bkt = bank()
            for kvp in range(2):
                P.op(PE, lambda e, s_=s_, kvp=kvp, bkt=bkt: e.transpose(out=ps[bkt][:, 128 * kvp:128 * kvp + 128],
                                                                     in_=Kc[:, s_, 128 * kvp:128 * kvp + 128], identity=ident),
                     reads=[A["Kc"], b_ident], writes=[psb[bkt]] if kvp == 0 else [], wadd=[psb[bkt]] if kvp == 1 else [])
            evac_copy(s_, KcT[:, s_].rearrange("p a t -> p (a t)"), ps[bkt][:, 0:256], [psb[bkt]], [], wadd=[A["KcT"]])
        P.dma(SP, kc_sem, Kc, cv.rearrange("s t f -> t s f"), reads=[A["KcT"]], writes=[A["Kc"]])
        P.op(POOL, lambda e: e.tensor_copy(out=Vcs, in_=Kc), reads=[A["Kc"]], writes=[A["Vcs"]])
        P.op(POOL, lambda e: e.memset(Vaug[:, :, 64:128], 1.0), writes=[A["Vaug"]])

        rhs_hh = lambda k, t0, n: hTh[:, k, 0:n]
        TB5 = TBS
        def attn_kv(kv):
            def ev_q(oc, tbi, pap, pb):
                t0, n = TBS[tbi]
                ctr["i"] += 1
                evac_copy(ctr["i"], qT[:, oc, t0:t0 + n], pap, [pb], [], wadd=[A["qT"]])
            A["qT"].r = list(A["qT"].r) + list(A["qT"].w); A["qT"].w = []
            proj_fm(w_in[:, OFF_Q + 256 * kv:OFF_Q + 256 * kv + 256], 256, rhs_h, hT_b, ev_q)
            s = wctr[0] % 2
            wctr[0] += 1
            for dup in range(2):
                P.dma(POOL, wsem[s], wslot[s][:, :, 64 * dup:64 * dup + 64],
                      w_in[:, OFF_K + 64 * kv:OFF_K + 64 * kv + 64].rearrange("(k p) f -> p k f", p=128),
                      writes=[wslot_b[s]] if dup == 0 else [], wadd=[wslot_b[s]] if dup == 1 else [])
            A["kT2"].r = list(A["kT2"].r) + list(A["kT2"].w); A["kT2"].w = []
            kblocks = [(hTh, hTh_b, 0, 128, 0)] + [(hT, hT_b[i], t0, n, 128 + t0) for i, (t0, n) in enumerate(TBS)]
            for bi_, (src, sb_, t0, n, c0) in enumerate(kblocks):
                bkk = bank()
                for k in range(8):
                    P.op(PE, lambda e, bkk=bkk, k=k, src=src, t0=t0, n=n, s=s: e.matmul(
                        ps[bkk][:, 0:n], lhsT=wslot[s][:, k, 0:128], rhs=src[:, k, t0:t0 + n], start=(k == 0), stop=(k == 7)),
                        reads=[wslot_b[s], sb_] if k == 0 else [], writes=[psb[bkk]] if k == 0 else [], sig=(k == 7))
                if not P.dead:
                    P.attach(Dep(P.esem[PE], P.cnt[PE]), reads=[wslot_b[s], sb_], writes=[psb[bkk]])
                evac_copy(bi_, kT2[:, c0:c0 + n], ps[bkk][:, 0:n], [psb[bkk]], [], wadd=[A["kT2"]])
            bkl_ = bank()
            for j_, (c0_, m_) in enumerate([(NP - 128, 128), (NP, NS)]):
                for k in range(8):
                    P.op(PE, lambda e, j_=j_, c0_=c0_, m_=m_, k=k, s=s: e.matmul(
                        ps[bkl_][0:m_, 64 * j_:64 * j_ + 64], lhsT=hT[:, k, c0_:c0_ + m_], rhs=wslot[s][:, k, 0:64],
                        start=(k == 0), stop=(k == 7)),
                        reads=[wslot_b[s], hT_b[3], hT_b[4]] if (k == 0 and j_ == 0) else [],
                        writes=[psb[bkl_]] if (k == 0 and j_ == 0) else [], sig=(k == 7 and j_ == 1))
            if not P.dead:
                P.attach(Dep(P.esem[PE], P.cnt[PE]), reads=[wslot_b[s], hT_b[3], hT_b[4]], writes=[psb[bkl_]])
            P.op(DVE, lambda e, kv=kv: e.tensor_copy(out=klast[:, 64 * kv:64 * kv + 64], in_=ps[bkl_][:, 0:64]), reads=[psb[bkl_]], wadd=[A["klast"]])
            P.op(DVE, lambda e, kv=kv: e.tensor_copy(out=knew[0:NS, 64 * kv:64 * kv + 64], in_=ps[bkl_][0:NS, 64:128]), reads=[psb[bkl_]], wadd=[A["knew"]])
            s = wctr[0] % 2
            wctr[0] += 1
            P.dma(POOL, wsem[s], wslot[s][:, :, 0:64], w_in[:, OFF_V + 64 * kv:OFF_V + 64 * kv + 64].rearrange("(k p) f -> p k f", p=128),
                  writes=[wslot_b[s]])
            A["Vaug"].r = list(A["Vaug"].r) + list(A["Vaug"].w); A["Vaug"].w = []
            vtiles = [(hTh, hTh_b, 0, 128)] + [(hT, hT_b[i // 4], 128 * i, 128) for i in range(16)] + [(hT, hT_b[4], NP, NS)]
            for grp in range(3):
                bkv = bank()
                tl_ = vtiles[8 * grp:8 * grp + 8]
                for j_, (src, sb_, c0_, m_) in enumerate(tl_):
                    for k in range(8):
                        firstg = (j_ == 0 and k == 0)
                        P.op(PE, lambda e, bkv=bkv, j_=j_, src=src, c0_=c0_, m_=m_, k=k, s=s: e.matmul(
                            ps[bkv][0:m_, 64 * j_:64 * j_ + 64], lhsT=src[:, k, c0_:c0_ + m_], rhs=wslot[s][:, k, 0:64],
                            start=(k == 0), stop=(k == 7)),
                            reads=[wslot_b[s], sb_, hTh_b] + hT_b if firstg else [], writes=[psb[bkv]] if firstg else [],
                            sig=(j_ == len(tl_) - 1 and k == 7))
                if not P.dead:
                    P.attach(Dep(P.esem[PE], P.cnt[PE]), reads=[wslot_b[s], hTh_b] + hT_b, writes=[psb[bkv]])
                nt_ = len(tl_)
                if grp < 2:
                    P.op(ACT, lambda e, bkv=bkv, grp=grp: e.activation(out=Vaug[:, 8 * grp:8 * grp + 8, 0:64],
                                                                     in_=ps[bkv][:, 0:512].rearrange("p (t d) -> p t d", d=64), func=AF.Copy),
                         reads=[psb[bkv]], wadd=[A["Vaug"]])
                    if grp == 1:
                        pass
                else:
                    P.op(ACT, lambda e, bkv=bkv: e.activation(out=Vaug[:, 16, 0:64], in_=ps[bkv][:, 0:64], func=AF.Copy),
                         reads=[psb[bkv]], wadd=[A["Vaug"]])
                    P.op(ACT, lambda e, bkv=bkv: e.activation(out=Vaug[0:NS, 17, 0:64], in_=ps[bkv][0:NS, 64:128], func=AF.Copy),
                         reads=[psb[bkv]], wadd=[A["Vaug"]])
                    P.op(ACT, lambda e, bkv=bkv, kv=kv: e.activation(out=vlast[:, 64 * kv:64 * kv + 64], in_=ps[bkv][:, 0:64], func=AF.Copy),
                         reads=[psb[bkv]], wadd=[A["vlast"]])
                    P.op(ACT, lambda e, bkv=bkv, kv=kv: e.activation(out=vnew[0:NS, 64 * kv:64 * kv + 64], in_=ps[bkv][0:NS, 64:128], func=AF.Copy),
                         reads=[psb[bkv]], wadd=[A["vnew"]])
            qv = lambda base, b_: qT[base:base + 64, 0:2, 128 * b_:128 * b_ + 128]
            def attn_s1(b_):
                ia = (2 * b_) % 4; ib = (2 * b_ + 1) % 4
                bA, bB = bank(), bank()
                kprev = slice(128 * b_, 128 * b_ + 128); kcur = slice(128 * b_ + 128, 128 * b_ + 256)
                seq = [(bA, 0, 0, kprev), (bB, 64, 0, kprev), (bA, 0, 1, kcur), (bB, 64, 1, kcur)]
                for (bk_, base, half, ks) in seq:
                    first = (half == 0)
                    P.op(PE, lambda e, bk_=bk_, base=base, half=half, ks=ks, b_=b_: e.matmul(
                        ps[bk_][:, 256 * half:256 * half + 256], lhsT=kT2[base:base + 64, ks], rhs=qv(base, b_), start=True, stop=True),
                        reads=[A["kT2"], A["qT"]] if first else [], writes=[psb[bk_]] if first else [], sig=(half == 1))
                    if half == 1 and not P.dead:
                        P.attach(Dep(P.esem[PE], P.cnt[PE]), reads=[A["kT2"], A["qT"]], writes=[psb[bk_]])
                for (bk_, ie, base_h) in [(bA, ia, 0), (bB, ib, 1)]:
                    P.op(ACT, lambda e, bk_=bk_, ie=ie: e.activation(out=Et[ie], in_=ps[bk_][:, :], func=AF.Exp, scale=0.125),
                         reads=[psb[bk_]], writes=[b_Et[ie]])
                    Ev = Et[ie].rearrange("p (h i q) -> p h i q", h=2, i=2)
                    Pv = PT[ie].rearrange("p (h i q) -> p h i q", h=2, i=2)
                    h0 = 4 * kv + base_h
                    eng = DVE if base_h == 0 else POOL
                    if b_ > 0:
                        P.op(eng, lambda e, Ev=Ev, Pv=Pv, h0=h0: e.tensor_tensor(out=Pv, in0=Ev, in1=EB[:, :, h0:h0 + 3:2, :], op=ALU.mult),
                             reads=[b_Et[ie], A["EB"]], writes=[b_PT[ie]])
                    else:
                        P.op(eng, lambda e, Ev=Ev, Pv=Pv, h0=h0: e.tensor_tensor(out=Pv[:, 0], in0=Ev[:, 0], in1=EB0[:, h0:h0 + 3:2, :], op=ALU.mult),
                             reads=[b_Et[ie], A["EB0"]], writes=[b_PT[ie]])
                        P.op(eng, lambda e, Ev=Ev, Pv=Pv, h0=h0: e.tensor_tensor(out=Pv[:, 1], in0=Ev[:, 1], in1=EB[:, 1, h0:h0 + 3:2, :], op=ALU.mult),
                             reads=[b_Et[ie], A["EB"]], wadd=[b_PT[ie]])

            def attn_s2(b_):
                ia = (2 * b_) % 4; ib = (2 * b_ + 1) % 4
                bO = bank()
                mm = [(ia, 0, b_, True), (ia, 1, b_ + 1, False), (ib, 0, b_, False), (ib, 1, b_ + 1, False)]
                for j_, (ip, half, tile, st_) in enumerate(mm):
                    cols = slice(0, 256) if ip == ia else slice(256, 512)
                    P.op(PE, lambda e, ip=ip, half=half, tile=tile, st_=st_, cols=cols: e.matmul(
                        ps[bO][:, cols], lhsT=Vaug[:, tile, :], rhs=PT[ip][:, 256 * half:256 * half + 256], start=st_, stop=False),
                        reads=[A["Vaug"], b_PT[ia], b_PT[ib], b_ES, b_ones] if j_ == 0 else [], writes=[psb[bO]] if j_ == 0 else [], sig=False)
                P.op(PE, lambda e, kv=kv: e.matmul(ps[bO][:, 0:512], lhsT=onesd[0:1, :], rhs=ES[0:1, kv].rearrange("p i q -> p (i q)"),
                                                   start=False, stop=True), sig=True)
                if not P.dead:
                    P.attach(Dep(P.esem[PE], P.cnt[PE]), reads=[A["Vaug"], b_PT[ia], b_PT[ib], b_ES, b_ones], writes=[psb[bO]])
                ir = b_ % 2
                P.op(ACT, lambda e, ir=ir: e.activation(out=rc[ir][64:128, :], in_=ps[bO][64:128, :], func=AF.Ln), reads=[psb[bO]], writes=[b_rc[ir]])
                P.op(ACT, lambda e, ir=ir: e.activation(out=rc[ir][64:128, :], in_=rc[ir][64:128, :], func=AF.Exp, scale=-1.0),
                     reads=[b_rc[ir]], writes=[b_rc[ir]])
                for par in range(2):
                    P.op(DVE, lambda e, par=par, ir=ir, b_=b_, kv=kv: e.tensor_tensor(
                        out=oT[64 * par:64 * par + 64, 2 * kv:2 * kv + 2, 128 * b_:128 * b_ + 128],
                        in0=ps[bO][0:64, 256 * par:256 * par + 256].rearrange("p (c q) -> p c q", c=2),
                        in1=rc[ir][64:128, 256 * par:256 * par + 256].rearrange("p (c q) -> p c q", c=2), op=ALU.mult),
                        reads=[psb[bO], b_rc[ir]], wadd=[oT_b[2 * kv], oT_b[2 * kv + 1]])

            attn_s1(0)
            for b_ in range(1, 16):
                attn_s1(b_)
                attn_s2(b_ - 1)
            attn_s2(15)
            base = 64 * (kv % 2)
            for i in range(4):
                hsrc = 64 * (i % 2)
                P.op(POOL, lambda e, i=i, hsrc=hsrc, base=base: e.tensor_copy(out=QsT[base:base + 64, 0, i, :], in_=qT[hsrc:hsrc + 64, i // 2, NP:NT]),
                     reads=[A["qT"]], writes=[A["QsT"]] if i == 0 else [], wadd=[A["QsT"]] if i > 0 else [])
            for sl_, i in enumerate([0, 1, 2, 3]):
                P.op(DVE, lambda e, i=i, kv=kv: e.tensor_copy(out=esr[0:1, :].rearrange("p (s i) -> p s i", i=4)[:, :, i],
                                                            in_=es16[0:1, 4 * kv + i:4 * kv + i + 1].to_broadcast([1, 16])),
                     reads=[A["es16"]], writes=[A["esr"]] if i == 0 else [], wadd=[A["esr"]] if i > 0 else [])
            bS, bD, bN = bank(), bank(), bank()
            Qsi = QsT[base:base + 64, 0].rearrange("p i s -> p s i")
            for s_ in range(NS):
                P.op(PE, lambda e, s_=s_, base=base, kv=kv: e.matmul(ps[bS][:, 4 * s_:4 * s_ + 4], lhsT=KcT[base:base + 64, s_, kv // 2, :],
                                                                  rhs=QsT[base:base + 64, 0, :, s_], start=True, stop=True),
                     reads=[A["KcT"], A["QsT"], A["kT2"]] if s_ == 0 else [], writes=[psb[bS]] if s_ == 0 else [], sig=False)
            P.op(PE, lambda e, base=base: e.matmul(ps[bS][0:NS, 64:128], lhsT=kT2[base:base + 64, 128 + NP:128 + NT], rhs=Qsi, start=True, stop=True), sig=True)
            if not P.dead:
                P.attach(Dep(P.esem[PE], P.cnt[PE]), reads=[A["KcT"], A["QsT"], A["kT2"]], writes=[psb[bS]])
            P.op(ACT, lambda e: e.activation(out=ptS[:, 0:64], in_=ps[bS][:, 0:64], func=AF.Exp, scale=0.125), reads=[psb[bS]], writes=[A["ptS"]])
            P.op(ACT, lambda e: e.activation(out=pdg[0:NS, :], in_=ps[bS][0:NS, 64:128], func=AF.Exp, scale=0.125), reads=[psb[bS]], writes=[A["pdg"]])
            P.op(DVE, lambda e, kv=kv: e.tensor_tensor(out=ptS[:, 0:64].rearrange("p (s i) -> p s i", i=4), in0=ptS[:, 0:64].rearrange("p (s i) -> p s i", i=4),
                                                in1=EB[:, 0, 4 * kv:4 * kv + 4, 0].unsqueeze(1).to_broadcast([128, 16, 4]), op=ALU.mult),
                 reads=[A["ptS"], A["EB"]], writes=[A["ptS"]])
            P.op(DVE, lambda e, kv=kv: e.tensor_tensor(out=pdg[0:NS, :].rearrange("p (s i) -> p s i", i=4), in0=pdg[0:NS, :].rearrange("p (s i) -> p s i", i=4),
                                                in1=ebs[0:NS, 4 * kv:4 * kv + 4].unsqueeze(1).to_broadcast([NS, 16, 4]), op=ALU.mult),
                 reads=[A["pdg"], A["ebs"]], writes=[A["pdg"]])
            P.op(DVE, lambda e: e.tensor_tensor(out=pdg[0:NS, :], in0=pdg[0:NS, :], in1=dgt[0:NS, :], op=ALU.mult),
                 reads=[A["pdg"], A["dgt"]], writes=[A["pdg"]])
            P.op(POOL, lambda e, kv=kv: e.tensor_copy(out=vnb[0:NS, 64 * kv:64 * kv + 64], in_=vnew[0:NS, 64 * kv:64 * kv + 64]),
                 reads=[A["vnew"]], writes=[A["vnb"]])
            P.op(PE, lambda e: e.matmul(ps[bD][:, 0:64], lhsT=ones_k, rhs=ptS[:, 0:64], start=True, stop=False),
                 reads=[A["ones_k"], A["ptS"], A["pdg"], A["esr"]], writes=[psb[bD]], sig=False)
            P.op(PE, lambda e: e.matmul(ps[bD][:, 0:64], lhsT=ones_k[0:NS, :], rhs=pdg[0:NS, :], start=False, stop=False), sig=False)
            P.op(PE, lambda e: e.matmul(ps[bD][:, 0:64], lhsT=ones_k[0:1, :], rhs=esr[0:1, :], start=False, stop=True), sig=True)
            if not P.dead:
                P.attach(Dep(P.esem[PE], P.cnt[PE]), reads=[A["ones_k"], A["ptS"], A["pdg"], A["esr"]], writes=[psb[bD]])
            ptv = ptS[:, 0:64].rearrange("p (s i) -> p s i", i=4)
            pdv = pdg[0:NS, :].rearrange("p (s i) -> p s i", i=4)
            psn = ps[bN][:, 0:64].rearrange("p (s i) -> p s i", i=4)
            for par in range(2):
                for s_ in range(NS):
                    P.op(PE, lambda e, par=par, s_=s_, kv=kv: e.matmul(
                        psn[64 * par:64 * par + 64, s_, par:4:2], lhsT=Vcs[:, s_, 64 * kv:64 * kv + 64], rhs=ptv[:, s_, par:4:2],
                        start=(s_ == 0), stop=False, tile_position=(0, 64 * par)),
                        reads=[A["Vcs"], A["ptS"], A["pdg"], A["vnb"]] if (par == 0 and s_ == 0) else [],
                        writes=[psb[bN]] if (par == 0 and s_ == 0) else [], sig=False)
                P.op(PE, lambda e, par=par, kv=kv: e.matmul(
                    psn[64 * par:64 * par + 64, :, par:4:2], lhsT=vnb[0:NS, 64 * kv:64 * kv + 64], rhs=pdv[:, :, par:4:2],
                    start=False, stop=True, tile_position=(0, 64 * par)), sig=(par == 1))
            if not P.dead:
                P.attach(Dep(P.esem[PE], P.cnt[PE]), reads=[A["Vcs"], A["ptS"], A["pdg"], A["vnb"]], writes=[psb[bN]])
            P.op(DVE, lambda e: e.reciprocal(out=rcs[:, 0:64], in_=ps[bD][:, 0:64]), reads=[psb[bD]], writes=[A["rcs"]])
            rcv = rcs[:, 0:64].rearrange("p (s i) -> p s i", i=4)
            for par in range(2):
                P.op(DVE, lambda e, par=par, kv=kv: e.tensor_tensor(
                    out=oT[64 * par:64 * par + 64, 2 * kv:2 * kv + 2, NP:NT],
                    in0=psn[64 * par:64 * par + 64, :, par:4:2].rearrange("p s c -> p c s"),
                    in1=rcv[64 * par:64 * par + 64, :, par:4:2].rearrange("p s c -> p c s"), op=ALU.mult),
                    reads=[psb[bN], A["rcs"]], wadd=[oT_b[2 * kv], oT_b[2 * kv + 1]])
        for kv in range(4):
            attn_kv(kv)
        P.dma(SP, osem_new("pck"), pck, klast, reads=[A["klast"]])
        P.dma(SP, osem_new("pcv"), pcv, vlast, reads=[A["vlast"]])
        P.dma(SP, osem_new("sck"), sck[:, 127, :], knew[0:NS, :], reads=[A["knew"]])
        P.dma(SP, osem_new("scv"), scv[:, 127, :], vnew[0:NS, :], reads=[A["vnew"]])

        P.phase(13)
        sgaT = cv_(O_C + 0, [8, NT], BF16)
        sga_b = [Buf(f"sga{g}") for g in range(8)]
        gt2 = [cv_(O_C + 33024 + 1024 * i, [512], BF16) for i in range(4)]
        ft2 = [cv_(O_C + 37120 + 2048 * i, [512], F32) for i in range(2)]
        b_gt2 = [Buf() for _ in range(4)]; b_ft2 = [Buf(), Buf()]
        handoff(sga_b + b_gt2 + b_ft2, list(A.values()) + b_Et + b_PT + b_rc)
        for blk in range(2):
            def ev_za(oc, tbi, pap, pb, blk=blk):
                t0, n = TBS[tbi]
                g = 4 * blk + oc
                ctr["i"] += 1
                gi = ctr["i"] % 4; fi = ctr["i"] % 2
                P.op(ACT, lambda e: e.activation(out=gt2[gi][:, 0:n], in_=pap, func=AF.Sigmoid), reads=[pb], writes=[b_gt2[gi]])
                P.op(DVE, lambda e: e.tensor_tensor(out=ft2[fi][:, 0:n], in0=pap, in1=gt2[gi][:, 0:n], op=ALU.mult),
                     reads=[pb, b_gt2[gi]], writes=[b_ft2[fi]])
                P.op(POOL, lambda e: e.tensor_tensor(out=oT[:, g, t0:t0 + n], in0=oT[:, g, t0:t0 + n], in1=ft2[fi][:, 0:n], op=ALU.mult),
                     reads=[b_ft2[fi], oT_b[g]], wadd=[oT_b[g]])
            proj_fm(w_in[:, OFF_ZA + 512 * blk:OFF_ZA + 512 * blk + 512], 512, rhs_h, hT_b, ev_za)
        for blk in range(2):
            def ev_ga(oc, tbi, pap, pb, blk=blk):
                t0, n = TBS[tbi]
                g = 4 * blk + oc
                P.op(ACT, lambda e: e.activation(out=sgaT[:, g, t0:t0 + n], in_=pap, func=AF.Sigmoid), reads=[pb], wadd=[sga_b[g]])
            proj_fm(w_in[:, OFF_GA + 512 * blk:OFF_GA + 512 * blk + 512], 512, rhs_h, hT_b, ev_ga)
        mT = RA
        mT_b = gbs_b
        rhs_o = lambda k, t0, n: oT[:, k, t0:t0 + n]
        for blk in range(2):
            def ev_ba(oc, tbi, pap, pb, blk=blk):
                t0, n = TBS[tbi]
                g = 4 * blk + oc
                ctr["i"] += 1
                fi = ctr["i"] % 2
                P.op(DVE, lambda e: e.tensor_tensor(out=ft2[fi][:, 0:n], in0=pap, in1=sgaT[:, g, t0:t0 + n], op=ALU.mult),
                     reads=[pb, sga_b[g]], writes=[b_ft2[fi]])
                P.op(POOL, lambda e: e.tensor_tensor(out=mT[:, g, t0:t0 + n], in0=mT[:, g, t0:t0 + n], in1=ft2[fi][:, 0:n], op=ALU.add),
                     reads=[b_ft2[fi], mT_b[g]], wadd=[mT_b[g]])
            proj_fm(w_ba[:, 512 * blk:512 * blk + 512], 512, rhs_o, [BufGroup(oT_b)] * 5, ev_ba)

        P.phase(14)
        o2 = O_C + 8000
        NSL = 4
        GateB = cv_(o2 + 0, [1024], F32); LnG = cv_(o2 + 4096, [1024], F32); LnB = cv_(o2 + 8192, [1024], F32)
        gateS = cv_(o2 + 12288, [1024], F32); grow = cv_(o2 + 16384, [1024], F32)
        xt = [cv_(o2 + 20480 + 4096 * i, [1024], F32) for i in range(NSL)]
        rt = [cv_(o2 + 36864 + 4096 * i, [1024], F32) for i in range(NSL)]
        stt = cv_(o2 + 53248, [NSL, 2, 6], F32); mvt = cv_(o2 + 53504, [NSL, 2], F32); rsd = cv_(o2 + 53568, [NSL, 2], F32)
        mhalf = cv_(o2 + 53632, [1], F32)
        b_GateB = Buf(); b_LnG = Buf(); b_LnB = Buf(); b_gateS = Buf(); b_grow = Buf(); b_xt = [Buf() for _ in range(NSL)]; b_rt = [Buf() for _ in range(NSL)]
        b_stt = [Buf() for _ in range(NSL)]; b_mh = Buf()
        xsem2 = [P.dsem(f"xt{i}") for i in range(NSL)]; osem = [P.dsem(f"o{i}") for i in range(NSL)]
        handoff([b_GateB, b_LnG, b_LnB, b_gateS, b_grow] + b_xt + b_rt + b_stt + [b_mh], list(A.values()) + b_Et + b_PT + b_rc + sga_b + b_gt2 + b_ft2)
        misc_load(SP, LnG, ln_g.rearrange("(o n) -> o n", o=1).to_broadcast([128, 1024]), b_LnG)
        misc_load(SP, LnB, ln_b.rearrange("(o n) -> o n", o=1).to_broadcast([128, 1024]), b_LnB)
        P.op(POOL, lambda e: e.memset(mhalf, -0.5), writes=[b_mh])
        for hb in range(2):
            bkg = bank(); bkg2 = bank()
            for kk in range(4):
                k = 4 * hb + kk
                P.op(PE, lambda e, k=k, kk=kk, bkg=bkg: e.transpose(out=ps[bkg][0:1, 128 * kk:128 * kk + 128], in_=modT[:, 16 + k, 0:1], identity=ident),
                     reads=[b_modT, b_ident], writes=[psb[bkg]] if kk == 0 else [], wadd=[psb[bkg]] if kk > 0 else [])
                P.op(PE, lambda e, k=k, kk=kk, bkg2=bkg2: e.transpose(out=ps[bkg2][0:NS, 128 * kk:128 * kk + 128], in_=modT[:, 16 + k, 1:17], identity=ident),
                     reads=[b_modT, b_ident], writes=[psb[bkg2]] if kk == 0 else [], wadd=[psb[bkg2]] if kk > 0 else [])
            P.op(DVE, lambda e, hb=hb, bkg=bkg: e.tensor_copy(out=grow[0:1, 512 * hb:512 * hb + 512], in_=ps[bkg][0:1, 0:512]), reads=[psb[bkg]], wadd=[b_grow])
            P.op(DVE, lambda e, hb=hb, bkg2=bkg2: e.tensor_copy(out=gateS[0:NS, 512 * hb:512 * hb + 512], in_=ps[bkg2][0:NS, 0:512]), reads=[psb[bkg2]], wadd=[b_gateS])
        for hb in range(2):
            bkb = bank()
            P.op(PE, lambda e, hb=hb, bkb=bkb: e.matmul(ps[bkb][:, 0:512], lhsT=ones1[0:1, :], rhs=grow[0:1, 512 * hb:512 * hb + 512], start=True, stop=True),
                 reads=[b_grow, b_ones], writes=[psb[bkb]])
            P.op(DVE, lambda e, hb=hb, bkb=bkb: e.tensor_copy(out=GateB[:, 512 * hb:512 * hb + 512], in_=ps[bkb][:, 0:512]), reads=[psb[bkb]], wadd=[b_GateB])
        so = [load_w(w_out[:, 0:512], 512), load_w(w_out[:, 512:1024], 512)]
        def x_load(ti_):
            rows_, c0_ = (128, 128 * ti_) if ti_ < 16 else (NS, NP)
            src_ = xp[c0_:c0_ + 128, :] if ti_ < 16 else xs
            P.dma(SP, xsem2[ti_ % NSL], xt[ti_ % NSL][0:rows_, :], src_, writes=[b_xt[ti_ % NSL]])
        for ti_ in range(NSL):
            x_load(ti_)
        for tt_i in range(17):
            rows, c0 = (128, 128 * tt_i) if tt_i < 16 else (NS, NP)
            sl = tt_i % NSL
            gate_ap = GateB if tt_i < 16 else gateS
            gate_b = b_GateB if tt_i < 16 else b_gateS
            for fb in range(2):
                bko = bank()
                for k in range(8):
                    P.op(PE, lambda e, bko=bko, k=k, fb=fb, rows=rows, c0=c0: e.matmul(
                        ps[bko][0:rows, 0:512], lhsT=mT[:, k, c0:c0 + rows], rhs=wslot[so[fb]][:, k, 0:512], start=(k == 0), stop=(k == 7)),
                        reads=[wslot_b[so[fb]]] + mT_b if k == 0 else [], writes=[psb[bko]] if k == 0 else [], sig=(k == 7))
                if not P.dead:
                    P.attach(Dep(P.esem[PE], P.cnt[PE]), reads=[wslot_b[so[fb]]] + mT_b, writes=[psb[bko]])
                P.op(DVE, lambda e, bko=bko, fb=fb, rows=rows, sl=sl, gate_ap=gate_ap: e.tensor_tensor(
                    out=rt[sl][0:rows, 512 * fb:512 * fb + 512], in0=ps[bko][0:rows, 0:512], in1=gate_ap[0:rows, 512 * fb:512 * fb + 512], op=ALU.mult),
                    reads=[psb[bko], gate_b], writes=[b_rt[sl]] if fb == 0 else [], wadd=[b_rt[sl]] if fb == 1 else [])
            P.op(DVE, lambda e, rows=rows, sl=sl: e.scalar_tensor_tensor(out=rt[sl][0:rows, :], in0=xt[sl][0:rows, :], scalar=float(ALPHA),
                                                                         in1=rt[sl][0:rows, :], op0=ALU.mult, op1=ALU.add),
                 reads=[b_xt[sl], b_rt[sl]], writes=[b_rt[sl]])
            for hf in range(2):
                P.op(DVE, lambda e, rows=rows, sl=sl, hf=hf: e.bn_stats(out=stt[0:rows, sl, hf, :], in_=rt[sl][0:rows, 512 * hf:512 * hf + 512]),
                     reads=[b_rt[sl]], writes=[b_stt[sl]] if hf == 0 else [], wadd=[b_stt[sl]] if hf == 1 else [])
            P.op(DVE, lambda e, rows=rows, sl=sl: e.bn_aggr(out=mvt[0:rows, sl, :], in_=stt[0:rows, sl].rearrange("p a b -> p (a b)")),
                 reads=[b_stt[sl]], writes=[b_stt[sl]])
            P.op(POOL, lambda e, rows=rows, sl=sl: e.tensor_scalar(out=rsd[0:rows, sl, 0:1], in0=mvt[0:rows, sl, 1:2], scalar1=float(LN_EPS), scalar2=0.0,
                                                                   op0=ALU.add, op1=ALU.add), reads=[b_stt[sl]], writes=[b_stt[sl]])
            P.op(POOL, lambda e, rows=rows, sl=sl: e.tensor_tensor(out=rsd[0:rows, sl, 0:1], in0=rsd[0:rows, sl, 0:1], in1=mhalf[0:rows, :], op=ALU.pow),
                 reads=[b_stt[sl], b_mh], writes=[b_stt[sl]])
            P.op(POOL, lambda e, rows=rows, sl=sl: e.tensor_tensor(out=rsd[0:rows, sl, 1:2], in0=mvt[0:rows, sl, 0:1], in1=rsd[0:rows, sl, 0:1], op=ALU.mult),
                 reads=[b_stt[sl]], writes=[b_stt[sl]])
            P.op(POOL, lambda e, rows=rows, sl=sl: e.tensor_scalar(out=rsd[0:rows, sl, 1:2], in0=rsd[0:rows, sl, 1:2], scalar1=-1.0, scalar2=0.0,
                                                                   op0=ALU.mult, op1=ALU.add), reads=[b_stt[sl]], writes=[b_stt[sl]])
            P.op(ACT, lambda e, rows=rows, sl=sl: e.activation(out=xt[sl][0:rows, :], in_=rt[sl][0:rows, :], func=AF.Identity,
                                                               scale=rsd[0:rows, sl, 0:1], bias=rsd[0:rows, sl, 1:2]),
                 reads=[b_rt[sl], b_stt[sl]], writes=[b_xt[sl]])
            P.op(DVE, lambda e, rows=rows, sl=sl: e.tensor_tensor(out=xt[sl][0:rows, :], in0=xt[sl][0:rows, :], in1=LnG[0:rows, :], op=ALU.mult),
                 reads=[b_xt[sl], b_LnG], writes=[b_xt[sl]])
            P.op(POOL, lambda e, rows=rows, sl=sl: e.tensor_tensor(out=xt[sl][0:rows, :], in0=xt[sl][0:rows, :], in1=LnB[0:rows, :], op=ALU.add),
                 reads=[b_xt[sl], b_LnB], writes=[b_xt[sl]])
            dst = yp[c0:c0 + 128, :] if tt_i < 16 else ys
            P.dma(SP, osem[sl], dst, xt[sl][0:rows, :], reads=[b_xt[sl]])
            if tt_i + NSL < 17:
                x_load(tt_i + NSL)
        final_deps = [Dep(o_.h, o_.cnt) for o_ in osem]

        P.dead = False
        if DEBUG:
            pass
        P.wait(SP, [Dep(dout_sem.h, dout_sem.cnt)] + final_deps + [Dep(d_.h, d_.cnt) for d_ in out_sems])
        P.emit()
        nc.all_engine_barrier()
        nc.clear_and_free_semaphores(P.allsems)
        nc.all_engine_barrier()
        print("instruction counts:", P.ninst)
    return nc


def _bucket_np(dist):
    max_exact = 16
    df = np.maximum(dist, 1).astype(np.float32)
    large = max_exact + (np.log(df / np.float32(max_exact)) / np.float32(math.log(128 / max_exact)) * np.float32(16)).astype(np.int32)
    large = np.minimum(large, 31)
    return np.where(dist < max_exact, dist, large)


def _host_consts():
    R = np.zeros((32, 384), np.float32)
    for i in range(384):
        dist = 255 - i
        if 0 <= dist <= 128:
            R[int(_bucket_np(np.array([dist]))[0]), i] = 1.0
    j = np.arange(128)[:, None]
    q = np.arange(128)[None, :]
    mask = np.concatenate([(j >= q), (j <= q)], axis=1).astype(np.float32)
    r = np.arange(128)
    bmask = (r[:, None] // 32 == r[None, :] // 32).astype(np.float32)
    diag = np.zeros((16, 16, 4), np.float32)
    for s_ in range(16):
        diag[s_, s_, :] = 1.0
    return R, mask, bmask, diag.reshape(16, 64)


_NC_CACHE = {}


def kernel(x_prompt, x_sample, c_prompt, c_sample, state_ssm_re, state_ssm_im, cache_swa_k, cache_swa_v,
           w_ada, b_ada, w_in, ssm_lambda_re, ssm_lambda_im, ssm_log_delta, ssm_b_re, ssm_b_im,
           ssm_c_re, ssm_c_im, ssm_d, w_glu, b_glu, attn_sinks, rel_bias, w_branch_s, w_branch_a,
           w_out, ln_g, ln_b):
    f = lambda a: np.ascontiguousarray(np.asarray(a, dtype=np.float32))
    x_prompt = f(x_prompt); x_sample = f(x_sample); c_prompt = f(c_prompt); c_sample = f(c_sample)
    R, mask, bmask, diag = _host_consts()
    shared = {
        "w_ada": f(w_ada)[0], "b_ada": f(b_ada)[0], "w_in": f(w_in)[0],
        "lam_re": f(ssm_lambda_re)[0], "lam_im": f(ssm_lambda_im)[0], "log_delta": f(ssm_log_delta)[0],
        "b_re": f(ssm_b_re)[0].reshape(4096, 16), "b_im": f(ssm_b_im)[0].reshape(4096, 16),
        "c_re": f(ssm_c_re)[0].reshape(1024, 64), "c_im": f(ssm_c_im)[0].reshape(1024, 64),
        "ssm_d": f(ssm_d)[0], "w_glu": f(w_glu)[0], "b_glu": f(b_glu)[0], "sinks": f(attn_sinks)[0],
        "rel_bias": f(rel_bias), "w_bs": f(w_branch_s)[0], "w_ba": f(w_branch_a)[0], "w_out": f(w_out)[0],
        "ln_g": f(ln_g)[0], "ln_b": f(ln_b)[0],
        "rtab": R, "maskc": mask, "bmaskc": bmask, "diagc": diag,
    }
    sre = f(state_ssm_re)[0].reshape(128, 4096); sim = f(state_ssm_im)[0].reshape(128, 4096)
    ckk = f(cache_swa_k)[0].reshape(128, 128, 256); cvv = f(cache_swa_v)[0].reshape(128, 128, 256)
    in_maps = []
    for c in range(NCORES):
        b, qr = c // 4, c % 4
        t0 = NP * qr
        xh = x_prompt[b, t0 - 128:t0] if qr > 0 else np.zeros((128, D), np.float32)
        flags = np.zeros(32, np.float32)
        flags[0] = 1.0 if qr > 0 else 0.0
        xprev = np.zeros((3, NP, D), np.float32)
        for j in range(3):
            qq = qr - 1 - j
            if qq >= 0:
                flags[1 + j] = 1.0
                xprev[j] = x_prompt[b, NP * qq:NP * qq + NP]
        m = dict(shared)
        m.update({
            "xprev": xprev, "xp": np.ascontiguousarray(x_prompt[b, t0:t0 + NP]), "xh": np.ascontiguousarray(xh),
            "xs": np.ascontiguousarray(x_sample[NS * c:NS * c + NS, 0]),
            "cc": np.ascontiguousarray(np.concatenate([c_prompt[b:b + 1], c_sample[NS * c:NS * c + NS]], 0)),
            "st_re": np.ascontiguousarray(sre[NS * c:NS * c + NS]), "st_im": np.ascontiguousarray(sim[NS * c:NS * c + NS]),
            "ck": np.ascontiguousarray(ckk[NS * c:NS * c + NS]), "cv": np.ascontiguousarray(cvv[NS * c:NS * c + NS]),
            "flags": flags,
        })
        in_maps.append(m)
    nc = build()
    res = run_bass_kernel_spmd(nc, in_maps, core_ids=list(range(NCORES)))
    R_ = res.results
    kernel.last_results = R_
    y_prompt = np.stack([np.concatenate([R_[4 * b + q]["yp"] for q in range(4)], 0) for b in range(2)], 0)
    y_sample = np.concatenate([R_[c]["ys"] for c in range(NCORES)], 0).reshape(128, 1, D)
    p_hr = np.stack([R_[4 * b + 3]["pst_re"].reshape(64, 64) for b in range(2)], 0)[None]
    p_hi = np.stack([R_[4 * b + 3]["pst_im"].reshape(64, 64) for b in range(2)], 0)[None]
    p_k = np.stack([R_[4 * b + 3]["pck"].reshape(128, 4, 64) for b in range(2)], 0)[None]
    p_v = np.stack([R_[4 * b + 3]["pcv"].reshape(128, 4, 64) for b in range(2)], 0)[None]
    s_hr = np.concatenate([R_[c]["sst_re"] for c in range(NCORES)], 0).reshape(1, 128, 64, 64)
    s_hi = np.concatenate([R_[c]["sst_im"] for c in range(NCORES)], 0).reshape(1, 128, 64, 64)
    s_k = np.concatenate([R_[c]["sck"] for c in range(NCORES)], 0).reshape(1, 128, 128, 4, 64)
    s_v = np.concatenate([R_[c]["scv"] for c in range(NCORES)], 0).reshape(1, 128, 128, 4, 64)
    return (y_prompt.astype(np.float32), y_sample.astype(np.float32), p_hr.astype(np.float32), p_hi.astype(np.float32),
            p_k.astype(np.float32), p_v.astype(np.float32), s_hr.astype(np.float32), s_hi.astype(np.float32),
            s_k.astype(np.float32), s_v.astype(np.float32))
```

```python
import math
import os
from contextlib import ExitStack
import numpy as np
import ml_dtypes
import concourse.bass as bass
import concourse.mybir as mybir
from concourse.bass_utils import run_bass_kernel_spmd

F32 = mybir.dt.float32
BF16 = mybir.dt.bfloat16
U8 = mybir.dt.uint8
ALU = mybir.AluOpType
AF = mybir.ActivationFunctionType
AX = mybir.AxisListType

PE, ACT, DVE, POOL, SP = "tensor", "scalar", "vector", "gpsimd", "sync"
ENGS = [PE, ACT, DVE, POOL, SP]

NCORES = 8
D = 1024
NP = 2048
NS = 16
NT = NP + NS
TBS = [(0, 512), (512, 512), (1024, 512), (1536, 512), (2048, 16)]
DIN = 6656
OFF_U, OFF_ZS, OFF_Q, OFF_K, OFF_V, OFF_ZA, OFF_GS, OFF_GA = 0, 1024, 2048, 3072, 3328, 3584, 4608, 5632
ALPHA = 2.0 ** 0.25
LN_EPS = 1e-5
DEBUG = False


class Dep:
    __slots__ = ("sem", "val")

    def __init__(self, sem, val):
        self.sem = sem
        self.val = val


class Buf:
    __slots__ = ("w", "r", "name")

    def __init__(self, name=""):
        self.w = []
        self.r = []
        self.name = name


class _RProxy:
    def __init__(self, bufs):
        self.bufs = bufs

    def append(self, h):
        for b in self.bufs:
            b.r.append(h)

    def __len__(self):
        return 0


class BufGroup:
    def __init__(self, bufs):
        self.bufs = list(bufs)
        self.r = _RProxy(self.bufs)

    @property
    def w(self):
        return [h for b in self.bufs for h in b.w]


def handoff(new_bufs, old_bufs):
    deps = []
    for b in old_bufs:
        deps.extend(b.w)
        deps.extend(b.r)
    for nb in new_bufs:
        nb.r = list(nb.r) + deps


class DSem:
    def __init__(self, h):
        self.h = h
        self.cnt = 0


class Prog:
    def __init__(self, nc, stack):
        self.nc = nc
        self.q = {e: [] for e in ENGS}
        self.esem = {}
        self.cnt = {e: 0 for e in ENGS}
        self.allsems = []
        for e in [PE, ACT, DVE, POOL]:
            self.esem[e] = nc.alloc_semaphore("s_" + e)
            self.allsems.append(self.esem[e])
        self.seen = {}
        self.stack = stack
        self.nd = 0
        self.ninst = {e: 0 for e in ENGS}
        self.dead = False
        self.stop = int(os.environ.get("KSTOP", "99"))

    def phase(self, n):
        self.dead = n > self.stop

    def dsem(self, name=None):
        self.nd += 1
        h = self.nc.alloc_semaphore(f"d{self.nd}_{name or 'm'}")
        self.allsems.append(h)
        return DSem(h)

    def _waits(self, eng, deps):
        best = {}
        for d in deps:
            if d is None:
                continue
            k = id(d.sem)
            if k not in best or best[k].val < d.val:
                best[k] = d
        ws = []
        for d in best.values():
            k = (eng, id(d.sem))
            if self.seen.get(k, 0) >= d.val:
                continue
            self.seen[k] = d.val
            ws.append((d.sem, d.val))
        return ws

    @staticmethod
    def _compact(lst):
        best = {}
        for d in lst:
            k = id(d.sem)
            if k not in best or best[k].val < d.val:
                best[k] = d
        return list(best.values())

    @staticmethod
    def _bufdeps(reads, writes, wadd=()):
        deps = []
        for b in reads:
            deps.extend(b.w)
        for b in writes:
            deps.extend(b.w)
            deps.extend(b.r)
        for b in wadd:
            deps.extend(b.r)
        return deps

    @classmethod
    def _update(cls, h, reads, writes, wadd=()):
        for b in reads:
            b.r.append(h)
            if len(b.r) > 32:
                b.r = cls._compact(b.r)
        for b in writes:
            b.w = [h]
            b.r = []
        for b in wadd:
            b.w.append(h)
            if len(b.w) > 32:
                b.w = cls._compact(b.w)

    def op(self, eng, fn, reads=(), writes=(), deps=(), sig=True, wadd=()):
        if self.dead:
            return None
        alld = list(deps) + self._bufdeps(reads, writes, wadd)
        ws = self._waits(eng, alld)
        h = None
        if sig:
            self.cnt[eng] += 1
            h = Dep(self.esem[eng], self.cnt[eng])
        sem = self.esem[eng] if sig else None
        self.ninst[eng] += 1 + len(ws)

        def run(e, ws=ws, fn=fn, sem=sem):
            for (s, v) in ws:
                e.wait_ge(s, v)
            ins = fn(e)
            if sem is not None:
                ins.then_inc(sem, 1)
        self.q[eng].append(run)
        if h is not None:
            self._update(h, reads, writes, wadd)
        return h

    def attach(self, h, reads=(), writes=(), wadd=()):
        if self.dead or h is None:
            return
        self._update(h, reads, writes, wadd)

    def dma(self, eng, ds, out, in_, reads=(), writes=(), deps=(), wadd=(), **kw):
        if self.dead:
            return None
        alld = list(deps) + self._bufdeps(reads, writes, wadd)
        ws = self._waits(eng, alld)
        ds.cnt += 16
        h = Dep(ds.h, ds.cnt)
        self.ninst[eng] += 1 + len(ws)

        def run(e, ws=ws, out=out, in_=in_, kw=kw, sh=ds.h):
            for (s, v) in ws:
                e.wait_ge(s, v)
            e.dma_start(out=out, in_=in_, **kw).then_inc(sh, 16)
        self.q[eng].append(run)
        self._update(h, reads, writes, wadd)
        return h

    def raw(self, eng, fn, ds, inc, reads=(), writes=(), deps=()):
        if self.dead:
            return None
        alld = list(deps) + self._bufdeps(reads, writes)
        ws = self._waits(eng, alld)
        ds.cnt += inc
        h = Dep(ds.h, ds.cnt)

        def run(e, ws=ws, fn=fn, sh=ds.h, inc=inc):
            for (s, v) in ws:
                e.wait_ge(s, v)
            fn(e).then_inc(sh, inc)
        self.q[eng].append(run)
        self._update(h, reads, writes)
        return h

    def wait(self, eng, deps):
        ws = self._waits(eng, deps)

        def run(e, ws=ws):
            for (s, v) in ws:
                e.wait_ge(s, v)
        self.q[eng].append(run)

    def emit(self):
        nc = self.nc
        with nc.Block() as block:
            @block.tensor
            def _(e):
                for f in self.q[PE]:
                    f(e)

            @block.scalar
            def _(e):
                for f in self.q[ACT]:
                    f(e)

            @block.vector
            def _(e):
                for f in self.q[DVE]:
                    f(e)

            @block.gpsimd
            def _(e):
                for f in self.q[POOL]:
                    f(e)

            @block.sync
            def _(e):
                for f in self.q[SP]:
                    f(e)


def _dsize(dt):
    return {F32: 4, BF16: 2, U8: 1}[dt]


class Arena:
    def __init__(self, nc, stack, nbytes):
        self.t = stack.enter_context(nc.sbuf_tensor("arena", [128, nbytes], U8))
        self.nbytes = nbytes

    def carve(self, off, shape, dt):
        n = int(np.prod(shape)) * _dsize(dt)
        assert off % 4 == 0 and off + n <= self.nbytes, (off, n, self.nbytes)
        v = self.t[:, off:off + n]
        if dt != U8:
            v = v.bitcast(dt)
        if len(shape) > 1:
            names = [f"a{i}" for i in range(len(shape))]
            pat = "p (" + " ".join(names) + ") -> p " + " ".join(names)
            v = v.rearrange(pat, **{names[i]: shape[i] for i in range(len(shape))})
        return v


O_HT = 0
O_HTH = 33024
O_CONST = 35072
O_W = 45312
O_A = 61696
O_B = 94720
O_C = 127744
ARENA = 212000
C_SIZE = ARENA - O_C


def build():
    nc = bass.Bass("TRN2", target_bir_lowering=False)

    def din(name, shape, dt=F32):
        return nc.dram_tensor(name, list(shape), dt, kind="ExternalInput").ap()

    def dout(name, shape, dt=F32):
        return nc.dram_tensor(name, list(shape), dt, kind="ExternalOutput").ap()

    xprev = din("xprev", [3, NP, D]); xp = din("xp", [NP, D]); xh = din("xh", [128, D]); xs = din("xs", [NS, D]); ccin = din("cc", [17, D])
    st_re = din("st_re", [NS, 4096]); st_im = din("st_im", [NS, 4096])
    ck = din("ck", [NS, 128, 256]); cv = din("cv", [NS, 128, 256])
    w_ada = din("w_ada", [D, 3072]); b_ada = din("b_ada", [3072]); w_in = din("w_in", [D, DIN])
    lam_re = din("lam_re", [64, 64]); lam_im = din("lam_im", [64, 64]); log_delta = din("log_delta", [64])
    b_re = din("b_re", [4096, 16]); b_im = din("b_im", [4096, 16])
    c_re = din("c_re", [1024, 64]); c_im = din("c_im", [1024, 64])
    ssm_d = din("ssm_d", [1024]); w_glu = din("w_glu", [D, D]); b_glu = din("b_glu", [D])
    sinks = din("sinks", [16]); rel_bias = din("rel_bias", [32, 16])
    w_bs = din("w_bs", [D, D]); w_ba = din("w_ba", [D, D]); w_out = din("w_out", [D, D])
    ln_g = din("ln_g", [D]); ln_b = din("ln_b", [D])
    rtab = din("rtab", [32, 384]); maskc = din("maskc", [128, 256]); bmaskc = din("bmaskc", [128, 128])
    diagc = din("diagc", [16, 64]); flagsc = din("flags", [32])

    yp = dout("yp", [NP, D]); ys = dout("ys", [NS, D])
    pst_re = dout("pst_re", [32, 128]); pst_im = dout("pst_im", [32, 128])
    pck = dout("pck", [128, 256]); pcv = dout("pcv", [128, 256])
    sst_re = dout("sst_re", [NS, 4096]); sst_im = dout("sst_im", [NS, 4096])
    sck = dout("sck", [NS, 128, 256]); scv = dout("scv", [NS, 128, 256])
    dbg = {}
    if DEBUG:
        dbg["hT"] = dout("dbg_hT", [128, 8, NT], BF16)
        dbg["uT"] = dout("dbg_uT", [128, 8, NT], BF16)
        dbg["pw"] = dout("dbg_pw", [128, 9 * 2 * 32])
        dbg["hend"] = dout("dbg_hend", [128, 2 * 32 * 16])
        dbg["bb"] = dout("dbg_bb", [128, 2 * 32 * 16]); dbg["ccm"] = dout("dbg_ccm", [128, 2 * 32 * 16])
        dbg["R8"] = dout("dbg_R8", [128, 16 * 2 * 32]); dbg["R128"] = dout("dbg_R128", [128, 16 * 2 * 32])
        dbg["A2k"] = dout("dbg_A2k", [128, 3 * 2 * 32])
        dbg["WinL"] = dout("dbg_WinL", [128, 8 * 8 * 2 * 128], BF16)
        dbg["X0"] = dout("dbg_X0", [128, 512])
        dbg["KL"] = dout("dbg_KL", [128, 2 * 8 * 128], BF16); dbg["Ca"] = dout("dbg_Ca", [128, 2 * 4 * 9 * 2 * 32], BF16)
        dbg["Hb"] = dout("dbg_Hb", [128, 2 * 2048], BF16); dbg["Xs"] = dout("dbg_Xs", [128, 2 * 2048])
        dbg["carry"] = dout("dbg_carry", [128, 17 * 2 * 32])
        dbg["yT"] = dout("dbg_yT", [128, 8, NT], BF16)
        dbg["gbs"] = dout("dbg_gbs", [128, 8, NT], BF16)
        dbg["oT"] = dout("dbg_oT", [128, 8, NT], BF16)
        dbg["mT"] = dout("dbg_mT", [128, 8, NT], BF16)
        dbg["modT"] = dout("dbg_modT", [128, 24 * 17])

    ib = nc.dram_tensor("cc_ib", [128, 64], F32, kind="Internal")
    ob = nc.dram_tensor("cc_ob", [NCORES * 128, 64], F32, kind="Internal")

    st = ExitStack()
    with st:
        P = Prog(nc, st)
        AR = Arena(nc, st, ARENA)
        cv_ = AR.carve
        ps = [st.enter_context(nc.psum_tensor(f"ps{i}", [128, 512], F32)) for i in range(8)]
        psb = [Buf(f"ps{i}") for i in range(8)]
        dout_sem = P.dsem("dout")
        out_sems = []

        def osem_new(name):
            d_ = P.dsem(name)
            out_sems.append(d_)
            return d_
        misc_sem = P.dsem("misc")

        def misc_load(eng, out, in_, buf, wadd=False, **kw):
            if P.dead:
                return None
            if wadd:
                return P.dma(eng, P.dsem(), out, in_, wadd=[buf], **kw)
            return P.dma(eng, P.dsem(), out, in_, writes=[buf], **kw)

        hT = cv_(O_HT, [8, NT], BF16)
        hTh = cv_(O_HTH, [8, 128], BF16)
        hT_b = [Buf(f"hT{i}") for i in range(len(TBS))]
        hTh_b = Buf("hTh")
        o = O_CONST
        ident = cv_(o, [128], F32); o += 512
        modT = cv_(o, [24, 17], F32); o += 1664
        op1p = cv_(o, [8, 17], F32); o += 576
        flags = cv_(o, [32], F32); o += 128
        Dm = cv_(o, [8], F32); o += 32
        bglu = cv_(o, [8], F32); o += 32
        ES = cv_(o, [4, 4, 128], BF16); o += 4096
        onesd = cv_(o, [128], BF16); o += 256
        EBself = cv_(o, [16], F32); o += 64
        bmask = cv_(o, [128], F32); o += 512
        ones1 = cv_(o, [128], F32); o += 512
        assert o <= O_CONST + 10240
        b_ident = Buf(); b_modT = Buf(); b_flags = Buf(); b_Dm = Buf(); b_bglu = Buf(); b_ES = Buf()
        b_ones = Buf(); b_EBself = Buf(); b_bmask = Buf()
        wslot = [cv_(O_W + 8192 * i, [8, 512], BF16) for i in range(2)]
        wslot_b = [Buf("w0"), Buf("w1")]
        wsem = [P.dsem("w0"), P.dsem("w1")]
        wctr = [0]
        RA = cv_(O_A, [8, NT], BF16)
        RB = cv_(O_B, [8, NT], BF16)

        rr = {"i": 0}

        def bank():
            i = rr["i"] % 8
            rr["i"] += 1
            return i

        def load_w(src2d, ncols):
            s = wctr[0] % 2
            wctr[0] += 1
            P.dma(POOL, wsem[s], wslot[s][:, :, 0:ncols], src2d.rearrange("(k p) f -> p k f", p=128),
                  writes=[wslot_b[s]])
            return s

        def evac_copy(i, out_ap, in_ap, reads, writes, wadd=()):
            if i % 2 == 0:
                return P.op(ACT, lambda e: e.activation(out=out_ap, in_=in_ap, func=AF.Copy), reads=reads, writes=writes, wadd=wadd)
            return P.op(DVE, lambda e: e.tensor_copy(out=out_ap, in_=in_ap), reads=reads, writes=writes, wadd=wadd)

        def proj_fm(src2d, ncols, rhs_of, rhs_bufs, evac, tbs=TBS):
            s = load_w(src2d, ncols)
            for oc in range(ncols // 128):
                for tbi, (t0, n) in enumerate(tbs):
                    b = bank()
                    for k in range(8):
                        last = (k == 7)
                        P.op(PE, lambda e, b=b, k=k, oc=oc, tbi=tbi, t0=t0, n=n, s=s: e.matmul(
                            ps[b][:, 0:n], lhsT=wslot[s][:, k, oc * 128:(oc + 1) * 128], rhs=rhs_of(k, t0, n),
                            start=(k == 0), stop=(k == 7)),
                            reads=[wslot_b[s], rhs_bufs[tbi]] if k == 0 else [], writes=[psb[b]] if k == 0 else [],
                            sig=last)
                        if last and not P.dead:
                            h = Dep(P.esem[PE], P.cnt[PE])
                            P.attach(h, reads=[wslot_b[s], rhs_bufs[tbi]], writes=[psb[b]])
                    evac(oc, tbi, ps[b][:, 0:n], psb[b])

        def cmul(eng, dst_r, dst_i, xr, xi, yr, yi, t1, t2, bufs_r, bufs_w, tb):
            P.op(eng, lambda e: e.tensor_tensor(out=t1, in0=xr, in1=yr, op=ALU.mult), reads=bufs_r, writes=[tb])
            P.op(eng, lambda e: e.tensor_tensor(out=t2, in0=xi, in1=yi, op=ALU.mult), reads=bufs_r, writes=[tb])
            P.op(eng, lambda e: e.tensor_tensor(out=dst_r, in0=t1, in1=t2, op=ALU.subtract), reads=[tb], writes=bufs_w)
            P.op(eng, lambda e: e.tensor_tensor(out=t1, in0=xr, in1=yi, op=ALU.mult), reads=bufs_r + bufs_w, writes=[tb])
            P.op(eng, lambda e: e.tensor_tensor(out=t2, in0=xi, in1=yr, op=ALU.mult), reads=bufs_r + bufs_w, writes=[tb])
            P.op(eng, lambda e: e.tensor_tensor(out=dst_i, in0=t1, in1=t2, op=ALU.add), reads=[tb], writes=bufs_w)

        P.phase(0)
        P.op(POOL, lambda e: e.memset(ident, 0.0), writes=[b_ident])
        P.op(POOL, lambda e: e.affine_select(out=ident, in_=ident, pattern=[[-1, 128]], compare_op=ALU.not_equal,
                                             fill=1.0, base=0, channel_multiplier=1), writes=[b_ident])
        misc_load(SP, flags, flagsc.rearrange("(o n) -> o n", o=1).to_broadcast([128, 32]), b_flags)
        misc_load(SP, bmask, bmaskc, b_bmask)
        P.op(POOL, lambda e: e.memset(ones1[0:1, :], 1.0), writes=[b_ones])
        P.op(POOL, lambda e: e.memset(onesd[0:1, 0:64], 0.0), wadd=[b_ones])
        P.op(POOL, lambda e: e.memset(onesd[0:1, 64:128], 1.0), wadd=[b_ones])

        P.phase(1)
        c_t = cv_(O_C + 62208, [1024], F32); c_sg = cv_(O_C + 66304, [1024], F32)
        ccT = cv_(O_C + 70400, [8, 17], BF16); badain = cv_(O_C + 70912, [128], F32); badaT = cv_(O_C + 71424, [24], F32)
        b_ct = Buf(); b_csg = Buf(); b_ccT = Buf(); b_bin = Buf(); b_baT = Buf()
        misc_load(SP, c_t[0:17, :], ccin, b_ct)
        misc_load(SP, badain[0:24, :], b_ada.rearrange("(c p) -> c p", p=128), b_bin)
        P.op(ACT, lambda e: e.activation(out=c_sg[0:17, :], in_=c_t[0:17, :], func=AF.Sigmoid), reads=[b_ct], writes=[b_csg])
        P.op(DVE, lambda e: e.tensor_tensor(out=c_sg[0:17, :], in0=c_sg[0:17, :], in1=c_t[0:17, :], op=ALU.mult),
             reads=[b_ct], writes=[b_csg])
        bk = bank()
        for k in range(8):
            P.op(PE, lambda e, k=k: e.transpose(out=ps[bk][:, 17 * k:17 * k + 17], in_=c_sg[0:17, 128 * k:128 * k + 128],
                                                identity=ident[0:17, 0:17]),
                 reads=[b_csg, b_ident], writes=[psb[bk]] if k == 0 else [], wadd=[psb[bk]] if k > 0 else [])
        P.op(DVE, lambda e: e.tensor_copy(out=ccT.rearrange("p k s -> p (k s)"), in_=ps[bk][:, 0:136]), reads=[psb[bk]], writes=[b_ccT])
        bk2 = bank()
        P.op(PE, lambda e: e.transpose(out=ps[bk2][:, 0:24], in_=badain[0:24, :], identity=ident[0:24, 0:24]),
             reads=[b_bin, b_ident], writes=[psb[bk2]])
        P.op(DVE, lambda e: e.tensor_copy(out=badaT, in_=ps[bk2][:, 0:24]), reads=[psb[bk2]], writes=[b_baT])
        bkm = bank()
        hlast = None
        for blk in range(6):
            s = load_w(w_ada[:, 512 * blk:512 * blk + 512], 512)
            for oc in range(4):
                f = 4 * blk + oc
                for k in range(8):
                    first = (blk == 0 and oc == 0 and k == 0)
                    lastk = (k == 7)
                    hlast = P.op(PE, lambda e, f=f, k=k, oc=oc, s=s: e.matmul(
                        ps[bkm][:, 17 * f:17 * f + 17], lhsT=wslot[s][:, k, oc * 128:(oc + 1) * 128], rhs=ccT[:, k, :],
                        start=(k == 0), stop=(k == 7)),
                        reads=[wslot_b[s], b_ccT] if k == 0 else [], writes=[psb[bkm]] if first else [], sig=lastk and oc == 3)
            P.attach(hlast, reads=[wslot_b[s]], wadd=[psb[bkm]])
        P.op(DVE, lambda e: e.tensor_tensor(out=modT, in0=ps[bkm][:, 0:408].rearrange("p (f s) -> p f s", f=24),
                                            in1=badaT.unsqueeze(2).to_broadcast([128, 24, 17]), op=ALU.add),
             reads=[psb[bkm], b_baT], writes=[b_modT])
        P.op(DVE, lambda e: e.tensor_scalar(out=op1p, in0=modT[:, 8:16, :], scalar1=1.0, scalar2=None, op0=ALU.add),
             reads=[b_modT], wadd=[b_modT])

        xst = [cv_(O_C + 71552 + 4096 * i, [1024], F32) for i in range(2)]
        xst_b = [Buf(), Buf()]
        xsem = [P.dsem("x0"), P.dsem("x1")]
        hTs_b = hT_b[4]
        tmpS = cv_(O_C + 79744, [8, 16], F32)
        b_tmpS = Buf()

        def phase_a(xsrc, full):
            tiles = [("p", i) for i in range(16)] + ([("h", 0), ("s", 0)] if full else [])
            for ti, (kind, i) in enumerate(tiles):
                sl = ti % 2
                if kind == "p":
                    src, rows, dst, dbuf = xsrc[128 * i:128 * i + 128, :], 128, (lambda k, i=i: hT[:, k, 128 * i:128 * i + 128]), hT_b[i // 4]
                elif kind == "h":
                    src, rows, dst, dbuf = xh, 128, (lambda k: hTh[:, k, :]), hTh_b
                else:
                    src, rows, dst, dbuf = xs, NS, None, hTs_b
                P.dma(SP, xsem[sl], xst[sl][0:rows, :], src, writes=[xst_b[sl]])
                b0, b1 = bank(), bank()
                for k in range(8):
                    bb_ = b0 if k < 4 else b1
                    j = k % 4
                    P.op(PE, lambda e, k=k, bb_=bb_, j=j, sl=sl, rows=rows: e.transpose(
                        out=ps[bb_][:, rows * j:rows * j + rows], in_=xst[sl][0:rows, 128 * k:128 * k + 128],
                        identity=ident[0:rows, 0:rows]),
                        reads=[xst_b[sl], b_ident], writes=[psb[bb_]] if j == 0 else [], wadd=[psb[bb_]] if j > 0 else [])
                if kind != "s":
                    for k in range(8):
                        bb_ = b0 if k < 4 else b1
                        j = k % 4
                        src_ps = ps[bb_][:, 128 * j:128 * j + 128]
                        if k < 4:
                            P.op(DVE, lambda e, k=k, src_ps=src_ps, dst=dst: e.tensor_scalar(
                                out=dst(k), in0=src_ps, scalar1=op1p[:, k, 0:1], scalar2=modT[:, k, 0:1], op0=ALU.mult, op1=ALU.add),
                                reads=[psb[bb_], b_modT], wadd=[dbuf])
                        else:
                            P.op(ACT, lambda e, k=k, src_ps=src_ps, dst=dst: e.activation(
                                out=dst(k), in_=src_ps, func=AF.Identity, scale=op1p[:, k, 0:1], bias=modT[:, k, 0:1]),
                                reads=[psb[bb_], b_modT], wadd=[dbuf])
                else:
                    for half, bb_ in enumerate([b0, b1]):
                        P.op(DVE, lambda e, half=half, bb_=bb_: e.tensor_tensor(
                            out=tmpS[:, 4 * half:4 * half + 4, :], in0=ps[bb_][:, 0:64].rearrange("p (k s) -> p k s", k=4),
                            in1=op1p[:, 4 * half:4 * half + 4, 1:17], op=ALU.mult), reads=[psb[bb_], b_modT], wadd=[b_tmpS])
                    P.op(DVE, lambda e: e.tensor_tensor(out=hT[:, :, NP:NT], in0=tmpS, in1=modT[:, 0:8, 1:17], op=ALU.add),
                         reads=[b_tmpS, b_modT], wadd=[dbuf])

        uT = RA
        uT_b = [Buf(f"uT{g}") for g in range(8)]
        rhs_h = lambda k, t0, n: hT[:, k, t0:t0 + n]
        cnt = {"i": 0}

        def phase_b(full):
            for blk in range(2):
                def ev(oc, tbi, pap, pb, blk=blk):
                    t0, n = TBS[tbi]
                    g = 4 * blk + oc
                    evac_copy(0, uT[:, g, t0:t0 + n], pap, [pb], [], wadd=[uT_b[g]])
                proj_fm(w_in[:, OFF_U + 512 * blk:OFF_U + 512 * blk + 512], 512, rhs_h, hT_b, ev, tbs=TBS if full else TBS[:4])

        P.phase(3)
        phase_a(xprev[0], False)
        phase_b(False)
        P.phase(2)
        oc_ = O_C
        pw = cv_(oc_ + 0, [9, 2, 32], F32); bb = cv_(oc_ + 2304, [2, 32, 16], F32); ccm = cv_(oc_ + 6400, [2, 32, 16], F32)
        R8 = cv_(oc_ + 10496, [16, 2, 32], F32); R128 = cv_(oc_ + 14592, [16, 2, 32], F32); A2k = cv_(oc_ + 18688, [3, 2, 32], F32)
        Hend = cv_(oc_ + 19456, [2, 32, 16], F32); carry = cv_(oc_ + 23552, [17, 2, 32], F32)
        Sb = cv_(oc_ + 27904, [8, 2, 128], BF16); coef = cv_(oc_ + 32000, [12, 32], F32)
        misc = cv_(oc_ + 33536, [32, 32], F32)
        t1 = cv_(oc_ + 37632, [1024], F32); t2 = cv_(oc_ + 41728, [1024], F32)
        O_S = oc_ + 45824
        Sslot = [cv_(O_S + 8192 * i, [8, 2, 128], F32) for i in range(2)]
        O_CA = oc_ + 62208; O_KL = oc_ + 71424; O_HB = oc_ + 75520
        b_pw = Buf("pw"); b_bb = Buf("bb"); b_ccm = Buf("ccm"); b_R8 = Buf(); b_R128 = Buf(); b_A2k = Buf(); b_coef = Buf()
        b_misc = Buf("misc"); b_t = Buf("t12"); b_Sslot = [Buf("S0"), Buf("S1")]
        craw = [cv_(O_S + 2048 * i, [8, 64], F32) for i in range(2)]
        lamraw = cv_(O_S + 4096, [128], F32); ldraw = cv_(O_S + 4608, [64], F32)
        draw = cv_(O_S + 4864, [128], F32); bgraw = cv_(O_S + 5376, [128], F32)
        braw = [cv_(O_S + 8192 + 2048 * i, [32, 16], F32) for i in range(2)]
        b_craw = Buf(); b_lam = Buf(); b_ld = Buf(); b_draw = Buf(); b_braw = Buf()
        misc_load(SP, craw[0], c_re.rearrange("(t r) p -> r t p", r=128), b_craw, wadd=True)
        misc_load(SP, craw[1], c_im.rearrange("(t r) p -> r t p", r=128), b_craw, wadd=True)
        misc_load(SP, lamraw[0:64, 0:64], lam_re, b_lam, wadd=True)
        misc_load(SP, lamraw[0:64, 64:128], lam_im, b_lam, wadd=True)
        misc_load(SP, ldraw, log_delta.rearrange("(o n) -> o n", o=1).to_broadcast([128, 64]), b_ld)
        misc_load(SP, draw[0:8, :], ssm_d.rearrange("(c p) -> c p", p=128), b_draw, wadd=True)
        misc_load(SP, bgraw[0:8, :], b_glu.rearrange("(c p) -> c p", p=128), b_draw, wadd=True)
        for i, src in enumerate([b_re, b_im]):
            for q4 in range(4):
                misc_load(SP, braw[i][:, 8 * q4:8 * q4 + 8, :],
                          src[1024 * q4:1024 * q4 + 1024, :].rearrange("(gp q) c -> q gp c", q=128), b_braw, wadd=True)
        M_ = lambda i: misc[:, i, :]
        LR, LI, DT, TH, FR, FC, MAG, SN, CS, NR, DEN, CR, CI, G1, KF, TMPA = [M_(i) for i in range(16)]
        KI = misc[:, 16, :].bitcast(mybir.dt.int32)
        bkl = bank()
        P.op(PE, lambda e: e.transpose(out=ps[bkl][:, 0:64], in_=lamraw[0:64, :], identity=ident[0:64, 0:64]),
             reads=[b_lam, b_ident], writes=[psb[bkl]])
        P.op(DVE, lambda e: e.tensor_copy(out=LR[0:64, :], in_=ps[bkl][0:64, 0:64:2]), reads=[psb[bkl]], wadd=[b_misc])
        P.op(DVE, lambda e: e.tensor_copy(out=LR[64:128, :], in_=ps[bkl][0:64, 1:64:2]), reads=[psb[bkl]], wadd=[b_misc])
        P.op(DVE, lambda e: e.tensor_copy(out=LI[0:64, :], in_=ps[bkl][64:128, 0:64:2]), reads=[psb[bkl]], wadd=[b_misc])
        P.op(DVE, lambda e: e.tensor_copy(out=LI[64:128, :], in_=ps[bkl][64:128, 1:64:2]), reads=[psb[bkl]], wadd=[b_misc])
        bkd = bank()
        P.op(PE, lambda e: e.transpose(out=ps[bkd][:, 0:8], in_=draw[0:8, :], identity=ident[0:8, 0:8]),
             reads=[b_draw, b_ident], writes=[psb[bkd]])
        P.op(PE, lambda e: e.transpose(out=ps[bkd][:, 8:16], in_=bgraw[0:8, :], identity=ident[0:8, 0:8]),
             reads=[b_draw, b_ident], wadd=[psb[bkd]])
        P.op(DVE, lambda e: e.tensor_copy(out=Dm, in_=ps[bkd][:, 0:8]), reads=[psb[bkd]], writes=[b_Dm])
        P.op(DVE, lambda e: e.tensor_copy(out=bglu, in_=ps[bkd][:, 8:16]), reads=[psb[bkd]], writes=[b_bglu])
        for ri in range(2):
            for hb in range(2):
                bkc = bank()
                for tt in range(4):
                    t_ = 4 * hb + tt
                    P.op(PE, lambda e, ri=ri, t_=t_, tt=tt, bkc=bkc: e.transpose(
                        out=ps[bkc][0:64, 128 * tt:128 * tt + 128], in_=craw[ri][:, t_, :], identity=ident),
                        reads=[b_craw, b_ident], writes=[psb[bkc]] if tt == 0 else [], wadd=[psb[bkc]] if tt > 0 else [])
                for g2 in range(2):
                    src = ps[bkc][0:64, :].rearrange("p (tg g2 c) -> p tg g2 c", g2=2, c=16)[:, :, g2, :]
                    P.op(DVE, lambda e, ri=ri, hb=hb, g2=g2, src=src: e.tensor_copy(
                        out=ccm[64 * g2:64 * g2 + 64, ri, 16 * hb:16 * hb + 16, :], in_=src),
                        reads=[psb[bkc]], wadd=[b_ccm])
        P.op(ACT, lambda e: e.activation(out=DT[0:64, :], in_=ldraw[0:64, 0:64:2], func=AF.Exp), reads=[b_ld], wadd=[b_misc])
        P.op(ACT, lambda e: e.activation(out=DT[64:128, :], in_=ldraw[64:128, 1:64:2], func=AF.Exp), reads=[b_ld], wadd=[b_misc])
        G = DVE
        tt_ = lambda out, a, b_, op, **kw: P.op(G, lambda e: e.tensor_tensor(out=out, in0=a, in1=b_, op=op), reads=[b_misc], wadd=[b_misc], **kw)
        ts_ = lambda out, a, s1, s2, o0, o1: P.op(G, lambda e: e.tensor_scalar(out=out, in0=a, scalar1=s1, scalar2=s2, op0=o0, op1=o1), reads=[b_misc], wadd=[b_misc])
        tt_(TH, LI, DT, ALU.mult)
        ts_(FR, TH, 1.0 / (2 * math.pi), 0.0, ALU.mult, ALU.add)
        P.op(DVE, lambda e: e.tensor_copy(out=KI, in_=FR), reads=[b_misc], wadd=[b_misc])
        P.op(DVE, lambda e: e.tensor_copy(out=KF, in_=KI), reads=[b_misc], wadd=[b_misc])
        tt_(FR, FR, KF, ALU.subtract)
        ts_(FC, FR, 1.0, 0.25, ALU.mult, ALU.add)
        P.op(DVE, lambda e: e.tensor_single_scalar(out=G1, in_=FC, scalar=0.5, op=ALU.is_gt), reads=[b_misc], wadd=[b_misc])
        tt_(FC, FC, G1, ALU.subtract)
        TWO_PI = 6.283185
        P.op(ACT, lambda e: e.activation(out=SN, in_=FR, func=AF.Sin, scale=TWO_PI), reads=[b_misc], wadd=[b_misc])
        P.op(ACT, lambda e: e.activation(out=CS, in_=FC, func=AF.Sin, scale=TWO_PI), reads=[b_misc], wadd=[b_misc])
        tt_(TMPA, LR, DT, ALU.mult)
        P.op(ACT, lambda e: e.activation(out=MAG, in_=TMPA, func=AF.Exp), reads=[b_misc], wadd=[b_misc])
        P.op(G, lambda e: e.memset(pw[:, 0, 0, :], 1.0), wadd=[b_pw])
        P.op(G, lambda e: e.memset(pw[:, 0, 1, :], 0.0), wadd=[b_pw])
        P.op(G, lambda e: e.tensor_tensor(out=pw[:, 1, 0, :], in0=MAG, in1=CS, op=ALU.mult), reads=[b_misc], wadd=[b_pw])
        P.op(G, lambda e: e.tensor_tensor(out=pw[:, 1, 1, :], in0=MAG, in1=SN, op=ALU.mult), reads=[b_misc], wadd=[b_pw])

        def cm(dst, x, y, n, rb, wb):
            T1 = t1[:, 0:n * 32].rearrange("p (n g) -> p n g", n=n)
            T2 = t2[:, 0:n * 32].rearrange("p (n g) -> p n g", n=n)
            cmul(G, dst[:, :, 0, :], dst[:, :, 1, :], x[:, :, 0, :], x[:, :, 1, :], y[:, :, 0, :], y[:, :, 1, :], T1, T2, rb, wb, b_t)

        def bc(ap1, n):
            return ap1.to_broadcast([128, n, 2, 32])

        cm(pw[:, 2:3], pw[:, 1:2], pw[:, 1:2], 1, [b_pw], [b_pw])
        cm(pw[:, 3:5], pw[:, 1:3], bc(pw[:, 2:3], 2), 2, [b_pw], [b_pw])
        cm(pw[:, 5:9], pw[:, 1:5], bc(pw[:, 4:5], 4), 4, [b_pw], [b_pw])
        tt_(NR, pw[:, 1, 0, :], pw[:, 0, 0, :], ALU.subtract, deps=b_pw.w)
        tt_(DEN, LR, LR, ALU.mult)
        tt_(TMPA, LI, LI, ALU.mult)
        tt_(DEN, DEN, TMPA, ALU.add)
        P.op(DVE, lambda e: e.reciprocal(out=DEN, in_=DEN), reads=[b_misc], wadd=[b_misc])
        tt_(CR, NR, LR, ALU.mult)
        tt_(TMPA, pw[:, 1, 1, :], LI, ALU.mult)
        tt_(CR, CR, TMPA, ALU.add)
        tt_(CR, CR, DEN, ALU.mult)
        tt_(CI, pw[:, 1, 1, :], LR, ALU.mult)
        tt_(TMPA, NR, LI, ALU.mult)
        tt_(CI, CI, TMPA, ALU.subtract)
        tt_(CI, CI, DEN, ALU.mult)
        CRb = CR.unsqueeze(2).to_broadcast([128, 32, 16]); CIb = CI.unsqueeze(2).to_broadcast([128, 32, 16])
        T1b = t1[:, 0:512].rearrange("p (g c) -> p g c", g=32); T2b = t2[:, 0:512].rearrange("p (g c) -> p g c", g=32)
        P.op(G, lambda e: e.tensor_tensor(out=T1b, in0=braw[0], in1=CRb, op=ALU.mult), reads=[b_braw, b_misc, b_pw], writes=[b_t])
        P.op(G, lambda e: e.tensor_tensor(out=T2b, in0=braw[1], in1=CIb, op=ALU.mult), reads=[b_braw, b_misc], wadd=[b_t])
        P.op(G, lambda e: e.tensor_tensor(out=bb[:, 0], in0=T1b, in1=T2b, op=ALU.subtract), reads=[b_t], wadd=[b_bb])
        P.op(G, lambda e: e.tensor_tensor(out=T1b, in0=braw[1], in1=CRb, op=ALU.mult), reads=[b_braw, b_misc, b_bb], writes=[b_t])
        P.op(G, lambda e: e.tensor_tensor(out=T2b, in0=braw[0], in1=CIb, op=ALU.mult), reads=[b_braw, b_misc], wadd=[b_t])
        P.op(G, lambda e: e.tensor_tensor(out=bb[:, 1], in0=T1b, in1=T2b, op=ALU.add), reads=[b_t], wadd=[b_bb])

        def rev_table(R, A0, bufR, out_last):
            AW = misc[:, 20:22, :].rearrange("p (o r) g -> p o r g", o=1)
            AW2 = misc[:, 22:24, :].rearrange("p (o r) g -> p o r g", o=1)
            P.op(G, lambda e: e.memset(R[:, 15, 0, :], 1.0), wadd=[bufR])
            P.op(G, lambda e: e.memset(R[:, 15, 1, :], 0.0), wadd=[bufR])
            P.op(G, lambda e: e.tensor_copy(out=AW, in_=A0), reads=[b_pw, b_coef, b_misc], wadd=[b_misc])
            w = 1
            cur, nxt = AW, AW2
            while w <= 8:
                cm(R[:, 16 - 2 * w:16 - w], R[:, 16 - w:16], bc(cur, w), w, [bufR, b_misc], [bufR])
                cm(nxt, cur, cur, 1, [b_misc], [b_misc])
                cur, nxt = nxt, cur
                w *= 2
            P.op(G, lambda e: e.tensor_copy(out=out_last, in_=cur), reads=[b_misc], wadd=[b_coef])

        A128 = coef[:, 0:2, :].rearrange("p (o r) g -> p o r g", o=1)
        A2048 = coef[:, 2:4, :].rearrange("p (o r) g -> p o r g", o=1)
        rev_table(R8, pw[:, 8:9], b_R8, A128)
        rev_table(R128, A128, b_R128, A2048)
        P.op(G, lambda e: e.memset(A2k[:, 0, 0, :], 1.0), wadd=[b_A2k])
        P.op(G, lambda e: e.memset(A2k[:, 0, 1, :], 0.0), wadd=[b_A2k])
        P.op(G, lambda e: e.tensor_copy(out=A2k[:, 1:2], in_=A2048), reads=[b_coef], wadd=[b_A2k])
        cm(A2k[:, 2:3], A2048, A2048, 1, [b_coef], [b_A2k])
        P.op(G, lambda e: e.tensor_scalar(out=coef[:, 4, :], in0=pw[:, 8, 1, :], scalar1=-1.0, scalar2=0.0, op0=ALU.mult, op1=ALU.add),
             reads=[b_pw], wadd=[b_coef])
        P.op(G, lambda e: e.tensor_scalar(out=coef[:, 5, :], in0=coef[:, 1, :], scalar1=-1.0, scalar2=0.0, op0=ALU.mult, op1=ALU.add),
             reads=[b_coef], wadd=[b_coef])

        P.phase(5)
        WinL = cv_(O_B, [8, 8, 2, 128], BF16)
        b_WinL = [Buf(f"WinL{g}") for g in range(8)]
        b_Sb = Buf("Sb")
        handoff(b_Sslot, [b_craw, b_lam, b_ld, b_draw, b_braw])
        b_SInit = [Buf("SInit0"), Buf("SInit1")]
        for i in range(2):
            P.op(POOL, lambda e, i=i: e.memset(Sslot[i].rearrange("p k r c -> p (k r c)"), 0.0), writes=[b_Sslot[i], b_SInit[i]])
        T1e = [t1[:, 0:512].rearrange("p (k m c) -> p k m c", k=8, m=4), t1[:, 512:1024].rearrange("p (k m c) -> p k m c", k=8, m=4)]
        T2e = [t2[:, 0:512].rearrange("p (k m c) -> p k m c", k=8, m=4), t2[:, 512:1024].rearrange("p (k m c) -> p k m c", k=8, m=4)]
        b_t5 = [b_t, Buf("t5pool")]
        handoff([b_t5[1]], [b_t])
        for gc in range(8):
            sl = gc % 2
            EG = DVE if gc % 2 == 0 else POOL
            T1s, T2s, b_tt = T1e[gc % 2], T2e[gc % 2], b_t5[gc % 2]
            S = Sslot[sl]
            Sv = S.rearrange("p k r (m g c) -> p k r m g c", m=4, g=2)
            prk = pw[:, 0:8, 0, 4 * gc:4 * gc + 4].unsqueeze(3).to_broadcast([128, 8, 4, 16])
            pik = pw[:, 0:8, 1, 4 * gc:4 * gc + 4].unsqueeze(3).to_broadcast([128, 8, 4, 16])
            bbr = bb[:, 0, 4 * gc:4 * gc + 4, :].unsqueeze(1).to_broadcast([128, 8, 4, 16])
            bbi = bb[:, 1, 4 * gc:4 * gc + 4, :].unsqueeze(1).to_broadcast([128, 8, 4, 16])
            for ri in range(2):
                x1, x2 = (bbr, bbi) if ri == 0 else (bbi, bbr)
                op = ALU.subtract if ri == 0 else ALU.add
                P.op(EG, lambda e, x1=x1, prk=prk, T1s=T1s: e.tensor_tensor(out=T1s, in0=prk, in1=x1, op=ALU.mult), reads=[b_pw, b_bb], writes=[b_tt])
                P.op(EG, lambda e, x2=x2, pik=pik, T2s=T2s: e.tensor_tensor(out=T2s, in0=pik, in1=x2, op=ALU.mult), reads=[b_pw, b_bb], wadd=[b_tt])
                for g2 in range(2):
                    lo, hi = 64 * g2, 64 * g2 + 64
                    P.op(EG, lambda e, ri=ri, g2=g2, lo=lo, hi=hi, op=op, Sv=Sv, T1s=T1s, T2s=T2s: e.tensor_tensor(
                        out=Sv[lo:hi, :, ri, :, g2, :], in0=T1s[lo:hi], in1=T2s[lo:hi], op=op),
                        reads=[b_tt, b_SInit[sl]], wadd=[b_Sslot[sl]])
            P.op(EG, lambda e, gc=gc, S=S: e.tensor_copy(out=Sb[:, gc], in_=S[:, 0]), reads=[b_Sslot[sl]], wadd=[b_Sb])
            for q4 in range(4):
                bkw = bank()
                for j in range(4):
                    k_, ri_ = (4 * q4 + j) // 2, (4 * q4 + j) % 2
                    P.op(PE, lambda e, bkw=bkw, j=j, k_=k_, ri_=ri_, S=S: e.transpose(
                        out=ps[bkw][:, 128 * j:128 * j + 128], in_=S[:, k_, ri_, :], identity=ident),
                        reads=[b_Sslot[sl], b_ident], writes=[psb[bkw]] if j == 0 else [], wadd=[psb[bkw]] if j > 0 else [])
                dstw = WinL[:, gc, 2 * q4:2 * q4 + 2].rearrange("p k r c -> p (k r c)")
                evac_copy(q4, dstw, ps[bkw][:, 0:512], [psb[bkw]], [], wadd=[b_WinL[gc]])

        t1p = cv_(O_S, [2, 16, 16], F32); t2p = cv_(O_S + 2048, [2, 16, 16], F32); cbp = cv_(O_S + 4096, [2, 16, 16], F32)
        b_p1 = Buf("p1tmp")
        handoff([b_p1], b_Sslot)
        b_Hend = Buf("Hend")

        def x_matmuls(gc, banks):
            for ri in range(2):
                for s_ in range(8):
                    for m in range(4):
                        first = (ri == 0 and s_ == 0)
                        last = (ri == 1 and s_ == 7)
                        bkx = banks[m]
                        P.op(PE, lambda e, m=m, ri=ri, s_=s_, bkx=bkx, gc=gc: e.matmul(
                            ps[bkx][:, 256 * ri:256 * ri + 256], lhsT=WinL[32 * m:32 * m + 32, gc, 7 - s_, ri, :],
                            rhs=uT[32 * m:32 * m + 32, gc, s_:NP:8], start=(s_ == 0), stop=(s_ == 7),
                            tile_position=(32 * m, 0)),
                            reads=[b_WinL[gc], uT_b[gc]] if first else [], writes=[psb[bkx]] if first else [], sig=last)
                        if last and not P.dead:
                            P.attach(Dep(P.esem[PE], P.cnt[PE]), reads=[b_WinL[gc], uT_b[gc]], writes=[psb[bkx]])

        def seg_reduce(src_ap, Rtab, gp0, ngp, out_ap, rbufs, wbuf):
            raise NotImplementedError

        def pass1():
            for gc in range(8):
                banks = [bank() for _ in range(4)]
                x_matmuls(gc, banks)
                if DEBUG and gc == 0 and os.environ.get('KX0'):
                    xdbg = cv_(O_S + 6144, [512], F32); b_xdbg = Buf()
                    P.op(DVE, lambda e: e.tensor_copy(out=xdbg, in_=ps[banks[1]][:, :]), reads=[psb[banks[1]]], writes=[b_xdbg])
                    P.dma(SP, dout_sem, dbg["X0"], xdbg, reads=[b_xdbg])
                for m in range(4):
                    gp = 4 * gc + m
                    X4 = ps[banks[m]][:, :].rearrange("p (r s i) -> p r s i", r=2, s=16)
                    Pr = R8[:, :, 0, gp].unsqueeze(1).unsqueeze(1).to_broadcast([128, 2, 16, 16])
                    Pi = R8[:, :, 1, gp].unsqueeze(1).unsqueeze(1).to_broadcast([128, 2, 16, 16])
                    P.op(DVE, lambda e, X4=X4, Pr=Pr: e.tensor_tensor(out=t1p, in0=X4, in1=Pr, op=ALU.mult),
                         reads=[psb[banks[m]], b_R8], writes=[b_p1])
                    P.op(DVE, lambda e, X4=X4, Pi=Pi: e.tensor_tensor(out=t2p, in0=X4, in1=Pi, op=ALU.mult),
                         reads=[psb[banks[m]], b_R8], wadd=[b_p1])
                    P.op(DVE, lambda e: e.tensor_tensor(out=cbp[:, 0], in0=t1p[:, 0], in1=t2p[:, 1], op=ALU.subtract), reads=[b_p1], wadd=[b_p1])
                    P.op(DVE, lambda e: e.tensor_tensor(out=cbp[:, 1], in0=t2p[:, 0], in1=t1p[:, 1], op=ALU.add), reads=[b_p1], wadd=[b_p1])
                    P.op(DVE, lambda e, gp=gp: e.tensor_reduce(out=Hend[:, :, gp, :], in_=cbp, axis=AX.X, op=ALU.add),
                         reads=[b_p1], wadd=[b_Hend])


        Ecore = cv_(O_S + 6144, [2, 32], F32); Eall = cv_(O_S + 6400, [8, 64], F32)
        te1 = cv_(O_S + 0, [2, 32, 16], F32); te2 = cv_(O_S + 8448, [2, 32, 16], F32)
        b_E = Buf("Ecore"); b_Eall = Buf("Eall"); b_te = b_p1
        handoff([b_E, b_Eall], b_Sslot)
        def ecore(j):
            Qr = R128[:, :, 0, :].rearrange("p i g -> p g i").unsqueeze(1).to_broadcast([128, 2, 32, 16])
            Qi = R128[:, :, 1, :].rearrange("p i g -> p g i").unsqueeze(1).to_broadcast([128, 2, 32, 16])
            P.op(DVE, lambda e: e.tensor_tensor(out=te1, in0=Hend, in1=Qr, op=ALU.mult), reads=[b_Hend, b_R128], writes=[b_te])
            P.op(DVE, lambda e: e.tensor_tensor(out=te2, in0=Hend, in1=Qi, op=ALU.mult), reads=[b_Hend, b_R128], wadd=[b_te])
            P.op(DVE, lambda e: e.tensor_tensor(out=te1[:, 0], in0=te1[:, 0], in1=te2[:, 1], op=ALU.subtract), reads=[b_te], writes=[b_te])
            P.op(DVE, lambda e: e.tensor_tensor(out=te2[:, 0], in0=te2[:, 0], in1=te1[:, 1], op=ALU.add), reads=[b_te], writes=[b_te])
            P.op(DVE, lambda e: e.tensor_reduce(out=Ecore[:, 0, :], in_=te1[:, 0], axis=AX.X, op=ALU.add), reads=[b_te], wadd=[b_E])
            P.op(DVE, lambda e: e.tensor_reduce(out=Ecore[:, 1, :], in_=te2[:, 0], axis=AX.X, op=ALU.add), reads=[b_te], wadd=[b_E])

            if j is not None:
                P.op(DVE, lambda e, j=j: e.tensor_copy(out=Eall[:, j, :], in_=Ecore.rearrange("p r g -> p (r g)")), reads=[b_E], wadd=[b_Eall])

        P.phase(3)
        srcs = [(xprev[1], False), (xprev[2], False), (xp, True)]
        phase_a(*srcs[0])
        for j in range(3):
            pass1()
            ecore(j)
            phase_b(srcs[j][1])
            if j < 2:
                phase_a(*srcs[j + 1])
        P.phase(6)
        pass1()
        P.phase(7)
        Sn = [cv_(O_S + 12544 + 256 * n, [2, 32], F32) for n in range(3)]
        b_Sn = Buf("Sn")
        handoff([b_Sn], b_Sslot)
        for n in range(3):
            Snf = Sn[n].rearrange("p r g -> p (r g)")
            P.op(DVE, lambda e, n=n, Snf=Snf: e.tensor_scalar(out=Snf, in0=Eall[:, n, :], scalar1=flags[:, 1 + n:2 + n], scalar2=None,
                                                            op0=ALU.mult), reads=[b_Eall, b_flags], wadd=[b_Sn])
        b_carry = Buf("carry")
        tq1 = cv_(O_S + 13312, [2, 32], F32); tq2 = cv_(O_S + 13568, [2, 32], F32)
        b_tq = Buf("tq")
        handoff([b_tq], b_Sslot)

        def cmul_small(dst, x, y, rb, wb):
            cmul(DVE, dst[:, 0, :], dst[:, 1, :], x[:, 0, :], x[:, 1, :], y[:, 0, :], y[:, 1, :], tq1[:, 0, :], tq1[:, 1, :], rb, wb, b_tq)

        cmul_small(carry[:, 1], Sn[1], A2k[:, 1], [b_Sn, b_A2k], [b_carry])
        cmul_small(carry[:, 2], Sn[2], A2k[:, 2], [b_Sn, b_A2k, b_carry], [b_carry])
        P.op(DVE, lambda e: e.tensor_tensor(out=Sn[0], in0=Sn[0], in1=carry[:, 1], op=ALU.add), reads=[b_Sn, b_carry], writes=[b_Sn])
        P.op(DVE, lambda e: e.tensor_tensor(out=carry[:, 0], in0=Sn[0], in1=carry[:, 2], op=ALU.add), reads=[b_Sn, b_carry], writes=[b_carry])
        A128v = coef[:, 0:2, :]
        for sg in range(16):
            cmul_small(carry[:, sg + 1], carry[:, sg], A128v, [b_carry, b_coef], [b_carry])
            P.op(DVE, lambda e, sg=sg: e.tensor_tensor(out=carry[:, sg + 1], in0=carry[:, sg + 1], in1=Hend[:, :, :, sg], op=ALU.add),
                 reads=[b_carry, b_Hend], writes=[b_carry])
        pstT = cv_(O_S + 13824, [2, 128], F32)
        b_pst = Buf()
        handoff([b_pst], b_Sslot)
        bkp = bank()
        for ri in range(2):
            P.op(PE, lambda e, ri=ri: e.transpose(out=ps[bkp][0:32, 128 * ri:128 * ri + 128], in_=carry[:, 16, ri, :], identity=ident),
                 reads=[b_carry, b_ident], writes=[psb[bkp]] if ri == 0 else [], wadd=[psb[bkp]] if ri == 1 else [])
        P.op(DVE, lambda e: e.tensor_copy(out=pstT[0:32].rearrange("p r c -> p (r c)"), in_=ps[bkp][0:32, 0:256]), reads=[psb[bkp]], writes=[b_pst])
        P.dma(SP, osem_new("pre"), pst_re, pstT[0:32, 0, :], reads=[b_pst])
        P.dma(SP, osem_new("pim"), pst_im, pstT[0:32, 1, :], reads=[b_pst])

        P.phase(8)
        CaBD = [cv_(O_CA + 4608 * i, [4, 9, 2, 32], BF16) for i in range(2)]
        KLs = [cv_(O_KL + 2048 * i, [8, 128], BF16) for i in range(2)]
        Hb = [cv_(O_HB + 4096 * i, [2, 4, 256], BF16) for i in range(2)]
        Xs = [cv_(O_S + 8192 * i, [2, 4, 256], F32) for i in range(2)]
        b_CaBD = [Buf("Ca0"), Buf("Ca1")]; b_KL = [Buf("KL0"), Buf("KL1")]; b_Hb = [Buf("Hb0"), Buf("Hb1")]; b_Xs = [Buf("Xs0"), Buf("Xs1")]
        old_c = [b_ct, b_csg, b_ccT, b_bin, b_baT, xst_b[0], xst_b[1]]
        handoff(b_CaBD + b_KL + b_Hb, old_c)
        handoff(b_Xs, [b_p1, b_E, b_Eall, b_te, b_Sn, b_tq, b_pst] + b_Sslot)
        KL0all = cv_(O_C + 10496, [8, 128], BF16); Ca1all = cv_(O_C + 14592, [32, 2, 32], BF16)
        b_KL0 = Buf("KL0all"); b_Ca1 = Buf("Ca1all")
        handoff([b_KL0], [b_R8]); handoff([b_Ca1], [b_R128])
        Q1e = [misc[:, 24:28, :].rearrange("p a g -> p (a g)").rearrange("p (r m s) -> p r m s", r=2, m=4),
               misc[:, 0:4, :].rearrange("p a g -> p (a g)").rearrange("p (r m s) -> p r m s", r=2, m=4)]
        Q2e = [misc[:, 28:32, :].rearrange("p a g -> p (a g)").rearrange("p (r m s) -> p r m s", r=2, m=4),
               misc[:, 4:8, :].rearrange("p a g -> p (a g)").rearrange("p (r m s) -> p r m s", r=2, m=4)]
        tmpK = misc[:, 16:20, :].rearrange("p a g -> p (a g)")
        b_Q = [Buf("Qdve"), Buf("Qpool")]
        b_tmpK = Buf("tmpK")
        handoff([b_tmpK] + b_Q, [b_misc])
        b_CaInit = [Buf("CaInit0"), Buf("CaInit1")]
        for i in range(2):
            P.op(POOL, lambda e, i=i: e.memset(CaBD[i].rearrange("p m n r c -> p (m n r c)"), 0.0), writes=[b_CaBD[i], b_CaInit[i]])
        U1 = t1[:, 0:576].rearrange("p (m n c) -> p m n c", m=4, n=9)
        U2 = t2[:, 0:576].rearrange("p (m n c) -> p m n c", m=4, n=9)
        XB = [2, 3, 4, 5]
        YB = [6, 7]

        def emit_consts(gc):
            sl = gc % 2
            Ca = CaBD[sl]
            cre = ccm[:, 0, 4 * gc:4 * gc + 4, :].unsqueeze(2).to_broadcast([128, 4, 9, 16])
            cim = ccm[:, 1, 4 * gc:4 * gc + 4, :].unsqueeze(2).to_broadcast([128, 4, 9, 16])
            pr = pw[:, :, 0, 4 * gc:4 * gc + 4].rearrange("p n m -> p m n").unsqueeze(3).to_broadcast([128, 4, 9, 16])
            pi = pw[:, :, 1, 4 * gc:4 * gc + 4].rearrange("p n m -> p m n").unsqueeze(3).to_broadcast([128, 4, 9, 16])
            P.op(DVE, lambda e: e.tensor_tensor(out=U1, in0=cre, in1=pr, op=ALU.mult), reads=[b_ccm, b_pw], writes=[b_t])
            P.op(DVE, lambda e: e.tensor_tensor(out=U2, in0=cim, in1=pi, op=ALU.mult), reads=[b_ccm, b_pw], wadd=[b_t])
            for g2 in range(2):
                lo, hi = 64 * g2, 64 * g2 + 64
                P.op(DVE, lambda e, lo=lo, hi=hi, g2=g2: e.tensor_tensor(
                    out=Ca[lo:hi, :, :, 0, 16 * g2:16 * g2 + 16], in0=U1[lo:hi], in1=U2[lo:hi], op=ALU.subtract),
                    reads=[b_t, b_CaInit[sl]], wadd=[b_CaBD[sl]])
            P.op(DVE, lambda e: e.tensor_tensor(out=U1, in0=cre, in1=pi, op=ALU.mult), reads=[b_ccm, b_pw], writes=[b_t])
            P.op(DVE, lambda e: e.tensor_tensor(out=U2, in0=cim, in1=pr, op=ALU.mult), reads=[b_ccm, b_pw], wadd=[b_t])
            P.op(DVE, lambda e: e.tensor_tensor(out=U1, in0=U1, in1=U2, op=ALU.add), reads=[b_t], writes=[b_t])
            for g2 in range(2):
                lo, hi = 64 * g2, 64 * g2 + 64
                P.op(DVE, lambda e, lo=lo, hi=hi, g2=g2: e.tensor_scalar(
                    out=Ca[lo:hi, :, :, 1, 16 * g2:16 * g2 + 16], in0=U1[lo:hi], scalar1=-1.0, scalar2=0.0, op0=ALU.mult, op1=ALU.add),
                    reads=[b_t, b_CaInit[sl]], wadd=[b_CaBD[sl]])
            P.op(DVE, lambda e: e.tensor_copy(out=Ca1all[:, 4 * gc:4 * gc + 4], in_=Ca[:, :, 1, :, :]), reads=[b_CaBD[sl]], wadd=[b_Ca1])
            for hb in range(2):
                for tt in range(4):
                    tau = 4 * hb + tt
                    for ri in range(2):
                        P.op(PE, lambda e, hb=hb, tt=tt, tau=tau, ri=ri: e.matmul(
                            ps[hb][:, 128 * tt:128 * tt + 128], lhsT=Sb[:, gc, ri, :], rhs=Ca[:, :, tau, ri, :],
                            start=(ri == 0), stop=(ri == 1)),
                            reads=[b_Sb, b_CaBD[sl]] if (tt == 0 and ri == 0) else [],
                            writes=[psb[hb]] if (tt == 0 and ri == 0) else [], sig=(tt == 3 and ri == 1))
                if not P.dead:
                    P.attach(Dep(P.esem[PE], P.cnt[PE]), reads=[b_Sb, b_CaBD[sl]], writes=[psb[hb]])
            KL = KLs[sl]
            bmb3 = bmask.unsqueeze(1).to_broadcast([128, 3, 128]); bmb4 = bmask.unsqueeze(1).to_broadcast([128, 4, 128])
            P.op(DVE, lambda e: e.tensor_tensor(out=KL[:, 1:4, :], in0=ps[0][:, 128:512].rearrange("p (t c) -> p t c", t=3), in1=bmb3, op=ALU.mult),
                 reads=[psb[0], b_bmask], wadd=[b_KL[sl]])
            P.op(DVE, lambda e: e.tensor_tensor(out=tmpK, in0=ps[0][:, 0:128], in1=bmask, op=ALU.mult), reads=[psb[0], b_bmask], writes=[b_tmpK])
            P.op(DVE, lambda e: e.tensor_tensor(out=KL[:, 4:8, :], in0=ps[1][:, 0:512].rearrange("p (t c) -> p t c", t=4), in1=bmb4, op=ALU.mult),
                 reads=[psb[1], b_bmask], wadd=[b_KL[sl]])
            P.op(DVE, lambda e: e.scalar_tensor_tensor(out=KL[:, 0, :], in0=ident, scalar=Dm[:, gc:gc + 1], in1=tmpK, op0=ALU.mult, op1=ALU.add),
                 reads=[b_tmpK, b_ident, b_Dm], wadd=[b_KL[sl]])
            P.op(DVE, lambda e: e.tensor_copy(out=KL0all[:, gc, :], in_=KL[:, 0, :]), reads=[b_KL[sl]], wadd=[b_KL0])

        def emit_x_scan(gc):
            sl = gc % 2
            x_matmuls(gc, XB)
            X = Xs[sl]
            for m in range(4):
                P.op(ACT, lambda e, m=m: e.activation(out=X[:, :, m, :], in_=ps[XB[m]][:, :].rearrange("p (r j) -> p r j", r=2), func=AF.Copy),
                     reads=[psb[XB[m]]], wadd=[b_Xs[sl]])
            E, qi = (POOL, 1) if gc in (1, 4, 6) else (DVE, 0)
            Q1 = Q1e[qi]; Q2 = Q2e[qi]
            X5 = X.rearrange("p r m (s i) -> p r m s i", i=16)
            Ar = pw[:, 8, 0, 4 * gc:4 * gc + 4].unsqueeze(1).unsqueeze(3).to_broadcast([128, 2, 4, 16])
            Ai = pw[:, 8, 1, 4 * gc:4 * gc + 4].unsqueeze(2).to_broadcast([128, 4, 16])
            AiN = coef[:, 4, 4 * gc:4 * gc + 4].unsqueeze(2).to_broadcast([128, 4, 16])
            cview = carry[:, 0:16, :, 4 * gc:4 * gc + 4].rearrange("p s r m -> p r m s")
            for i in range(16):
                prev = cview if i == 0 else X5[:, :, :, :, i - 1]
                cur = X5[:, :, :, :, i]
                rb = [b_carry, b_pw, b_coef, b_Xs[sl]]
                P.op(E, lambda e, prev=prev: e.tensor_tensor(out=Q1, in0=prev, in1=Ar, op=ALU.mult), reads=rb, writes=[b_Q[qi]])
                P.op(E, lambda e, prev=prev: e.tensor_tensor(out=Q2[:, 0], in0=prev[:, 1], in1=AiN, op=ALU.mult), reads=rb, wadd=[b_Q[qi]])
                P.op(E, lambda e, prev=prev: e.tensor_tensor(out=Q2[:, 1], in0=prev[:, 0], in1=Ai, op=ALU.mult), reads=rb, wadd=[b_Q[qi]])
                P.op(E, lambda e, cur=cur: e.tensor_tensor(out=cur, in0=cur, in1=Q1, op=ALU.add), reads=[b_Q[qi]], writes=[b_Xs[sl]])
                P.op(E, lambda e, cur=cur: e.tensor_tensor(out=cur, in0=cur, in1=Q2, op=ALU.add), reads=[b_Q[qi]], writes=[b_Xs[sl]])
            H5 = Hb[sl].rearrange("p r m (s i) -> p r m s i", i=16)
            P.op(ACT, lambda e: e.activation(out=H5[:, :, :, :, 1:16].rearrange("p r m s i -> p (r m) s i"),
                                             in_=X5[:, :, :, :, 0:15].rearrange("p r m s i -> p (r m) s i"), func=AF.Copy),
                 reads=[b_Xs[sl]], writes=[b_Hb[sl]])
            P.op(ACT, lambda e: e.activation(out=H5[:, :, :, :, 0], in_=cview, func=AF.Copy), reads=[b_carry], wadd=[b_Hb[sl]])

        def emit_y(gc):
            sl = gc % 2
            KL = KLs[sl]; Ca = CaBD[sl]
            uview = uT[:, gc, 0:NP].rearrange("p (j s) -> p s j", s=8)
            for half in (1, 0):
                for tl in range(4):
                    t_lo = 4 * half + tl
                    bk_ = YB[tl // 2]
                    reg = ps[bk_][:, 256 * (tl % 2):256 * (tl % 2) + 256]
                    n_mm = (t_lo + 1) + 8
                    idx = 0
                    for s_ in range(t_lo + 1):
                        P.op(PE, lambda e, reg=reg, s_=s_, t_lo=t_lo: e.matmul(
                            reg, lhsT=KL[:, t_lo - s_, :], rhs=uview[:, s_, :], start=(s_ == 0), stop=False),
                            reads=[b_KL[sl], uT_b[gc], b_CaBD[sl], b_Hb[sl]] if idx == 0 else [],
                            writes=[psb[bk_]] if (idx == 0 and tl % 2 == 0) else [], sig=False)
                        idx += 1
                    for m in range(4):
                        for ri in range(2):
                            lastmm = (m == 3 and ri == 1)
                            P.op(PE, lambda e, reg=reg, m=m, ri=ri, t_lo=t_lo, lastmm=lastmm: e.matmul(
                                reg[32 * m:32 * m + 32, :], lhsT=Ca[:, m, t_lo + 1, ri, :], rhs=Hb[sl][:, ri, m, :],
                                start=False, stop=(ri == 1), tile_position=(0, 32 * m)), sig=lastmm)
                    if not P.dead:
                        P.attach(Dep(P.esem[PE], P.cnt[PE]), reads=[b_KL[sl], uT_b[gc], b_CaBD[sl], b_Hb[sl]],
                                 writes=[psb[bk_]] if tl % 2 == 1 else [], wadd=[psb[bk_]] if tl % 2 == 0 else [])
                for bi in range(2):
                    t0_ = 4 * half + 2 * bi
                    P.op(ACT, lambda e, bi=bi, t0_=t0_: e.activation(
                        out=uview[:, t0_:t0_ + 2, :], in_=ps[YB[bi]][:, :].rearrange("p (t j) -> p t j", t=2), func=AF.Gelu_apprx_tanh),
                        reads=[psb[YB[bi]]], writes=[uT_b[gc]])

        for step in range(9):
            if step < 8:
                emit_consts(step)
                emit_x_scan(step)
            if step >= 1:
                emit_y(step - 1)
        yT = uT
        yT_b = uT_b

        P.phase(9)
        all_ssm_tmp = b_Xs + b_Hb + b_Q + [b_tmpK, b_t, b_p1, b_E, b_Eall, b_te, b_Sn, b_tq, b_pst] + b_Sslot
        stile = [cv_(O_S + 2048 * i, [512], F32) for i in range(2)]
        Hsp = cv_(O_S + 4096, [2, 32, 16], F32); Hn = cv_(O_S + 8192, [2, 32, 16], F32)
        HbS = cv_(O_S + 12288, [2, 32, 16], BF16); Q1s = cv_(O_HB, [2, 32, 16], F32); Q2s = cv_(O_HB + 4096, [2, 32, 16], F32)
        b_stile = Buf(); b_Hsp = Buf(); b_Hn = Buf(); b_HbS = Buf(); b_Qs = Buf()
        handoff([b_stile, b_Hsp, b_Hn, b_HbS, b_Qs], all_ssm_tmp)
        misc_load(SP, stile[0], st_re.rearrange("s (gh f) -> (s gh) f", gh=8), b_stile, wadd=True)
        misc_load(SP, stile[1], st_im.rearrange("s (gh f) -> (s gh) f", gh=8), b_stile, wadd=True)
        for ri in range(2):
            bks = bank()
            for q4 in range(4):
                P.op(PE, lambda e, ri=ri, q4=q4, bks=bks: e.transpose(out=ps[bks][:, 128 * q4:128 * q4 + 128],
                                                                   in_=stile[ri][:, 128 * q4:128 * q4 + 128], identity=ident),
                     reads=[b_stile, b_ident], writes=[psb[bks]] if q4 == 0 else [], wadd=[psb[bks]] if q4 > 0 else [])
            for q4 in range(4):
                P.op(DVE, lambda e, ri=ri, q4=q4, bks=bks: e.tensor_copy(
                    out=Hsp[:, ri, q4:32:4, :], in_=ps[bks][:, 128 * q4:128 * q4 + 128].rearrange("p (s gh) -> p gh s", gh=8)),
                    reads=[psb[bks]], wadd=[b_Hsp])
        P.op(POOL, lambda e: e.tensor_copy(out=HbS, in_=Hsp), reads=[b_Hsp], writes=[b_HbS])
        xsb = [bank() for _ in range(4)]
        for m in range(4):
            for gc in range(8):
                for ri in range(2):
                    first = (gc == 0 and ri == 0); last = (gc == 7 and ri == 1)
                    P.op(PE, lambda e, m=m, gc=gc, ri=ri: e.matmul(
                        ps[xsb[m]][:, 32 * gc + 16 * ri:32 * gc + 16 * ri + 16], lhsT=WinL[32 * m:32 * m + 32, gc, 0, ri, :],
                        rhs=uT[32 * m:32 * m + 32, gc, NP:NT], start=True, stop=True, tile_position=(32 * m, 0)),
                        reads=b_WinL + uT_b if first else [], writes=[psb[xsb[m]]] if first else [], sig=last)
            if not P.dead:
                P.attach(Dep(P.esem[PE], P.cnt[PE]), reads=b_WinL + uT_b, writes=[psb[xsb[m]]])
        Ar1 = pw[:, 1, 0, :].unsqueeze(1).unsqueeze(3).to_broadcast([128, 2, 32, 16])
        Ai1 = pw[:, 1, 1, :].unsqueeze(2).to_broadcast([128, 32, 16])
        P.op(POOL, lambda e: e.tensor_scalar(out=coef[:, 6, :], in0=pw[:, 1, 1, :], scalar1=-1.0, scalar2=0.0, op0=ALU.mult, op1=ALU.add),
             reads=[b_pw], wadd=[b_coef])
        AiN1 = coef[:, 6, :].unsqueeze(2).to_broadcast([128, 32, 16])
        P.op(DVE, lambda e: e.tensor_tensor(out=Q1s, in0=Hsp, in1=Ar1, op=ALU.mult), reads=[b_Hsp, b_pw], writes=[b_Qs])
        P.op(DVE, lambda e: e.tensor_tensor(out=Q2s[:, 0], in0=Hsp[:, 1], in1=AiN1, op=ALU.mult), reads=[b_Hsp, b_coef], wadd=[b_Qs])
        P.op(DVE, lambda e: e.tensor_tensor(out=Q2s[:, 1], in0=Hsp[:, 0], in1=Ai1, op=ALU.mult), reads=[b_Hsp, b_pw], wadd=[b_Qs])
        P.op(DVE, lambda e: e.tensor_tensor(out=Hn, in0=Q1s, in1=Q2s, op=ALU.add), reads=[b_Qs], writes=[b_Hn])
        for m in range(4):
            P.op(DVE, lambda e, m=m: e.tensor_tensor(
                out=Hn[:, :, m:32:4, :], in0=ps[xsb[m]][:, 0:256].rearrange("p (gc r s) -> p r gc s", gc=8, r=2),
                in1=Hn[:, :, m:32:4, :], op=ALU.add), reads=[psb[xsb[m]], b_Hn], writes=[b_Hn])
        stg = cv_(O_HB + 8192 - 8192, [4, 128], F32)
        sout = [cv_(O_S + 2048 * i, [512], F32) for i in range(2)]
        b_stg = Buf(); b_sout = Buf()
        handoff([b_stg], [b_Qs]); handoff([b_sout], [b_stile])
        for ri in range(2):
            for q4 in range(4):
                P.op(POOL, lambda e, ri=ri, q4=q4: e.tensor_copy(out=stg[:, q4, :].rearrange("p (s gh) -> p gh s", gh=8),
                                                               in_=Hn[:, ri, q4:32:4, :]), reads=[b_Hn], writes=[b_stg] if q4 == 0 else [],
                     wadd=[b_stg] if q4 > 0 else [])
            bks = bank()
            for q4 in range(4):
                P.op(PE, lambda e, q4=q4, bks=bks: e.transpose(out=ps[bks][:, 128 * q4:128 * q4 + 128], in_=stg[:, q4, :], identity=ident),
                     reads=[b_stg, b_ident], writes=[psb[bks]] if q4 == 0 else [], wadd=[psb[bks]] if q4 > 0 else [])
            P.op(DVE, lambda e, ri=ri, bks=bks: e.tensor_copy(out=sout[ri], in_=ps[bks][:, 0:512]), reads=[psb[bks]], wadd=[b_sout])
            P.dma(SP, osem_new(f"sst{ri}"), (sst_re if ri == 0 else sst_im).rearrange("s (gh f) -> (s gh) f", gh=8), sout[ri], reads=[b_sout])
        bky = bank()
        for gc in range(8):
            reg = ps[bky][:, 16 * gc:16 * gc + 16]
            P.op(PE, lambda e, gc=gc, reg=reg: e.matmul(reg, lhsT=KL0all[:, gc, :], rhs=uT[:, gc, NP:NT], start=True, stop=False),
                 reads=[b_KL0, b_Ca1, b_HbS] + uT_b if gc == 0 else [], writes=[psb[bky]] if gc == 0 else [], sig=False)
            for m in range(4):
                for ri in range(2):
                    lastmm = (m == 3 and ri == 1)
                    P.op(PE, lambda e, gc=gc, reg=reg, m=m, ri=ri, lastmm=lastmm: e.matmul(
                        reg[32 * m:32 * m + 32, :], lhsT=Ca1all[:, 4 * gc + m, ri, :], rhs=HbS[:, ri, 4 * gc + m, :],
                        start=False, stop=(ri == 1), tile_position=(0, 32 * m)), sig=(lastmm and gc == 7))
        if not P.dead:
            P.attach(Dep(P.esem[PE], P.cnt[PE]), reads=[b_KL0, b_Ca1, b_HbS] + uT_b, writes=[psb[bky]])
        P.op(ACT, lambda e: e.activation(out=uT[:, :, NP:NT], in_=ps[bky][:, 0:128].rearrange("p (g s) -> p g s", g=8),
                                         func=AF.Gelu_apprx_tanh), reads=[psb[bky]], writes=uT_b)

        P.phase(10)
        s2T = RB
        s2_b = [Buf(f"s2_{g}") for g in range(8)]
        handoff(s2_b, b_WinL)
        gtmp = [cv_(O_C + 1024 * i, [512], BF16) for i in range(4)]
        ftmp = [cv_(O_C + 4096 + 2048 * i, [512], F32) for i in range(2)]
        b_gtmp = [Buf() for _ in range(4)]; b_ftmp = [Buf(), Buf()]
        handoff(b_gtmp + b_ftmp, [b_pw, b_bb, b_ccm])
        rhs_y = lambda k, t0, n: yT[:, k, t0:t0 + n]
        yall_b = [yT_b] * 5
        ctr = {"i": 0}

        class AllOf:
            pass
        for blk in range(2):
            def ev_glu(oc, tbi, pap, pb, blk=blk):
                t0, n = TBS[tbi]
                g = 4 * blk + oc
                ctr["i"] += 1
                gi = ctr["i"] % 4
                P.op(ACT, lambda e: e.activation(out=gtmp[gi][:, 0:n], in_=pap, func=AF.Sigmoid, bias=bglu[:, g:g + 1], scale=1.0),
                     reads=[pb, b_bglu], writes=[b_gtmp[gi]])
                P.op(DVE, lambda e: e.tensor_tensor(out=s2T[:, g, t0:t0 + n], in0=yT[:, g, t0:t0 + n], in1=gtmp[gi][:, 0:n], op=ALU.mult),
                     reads=[b_gtmp[gi], yT_b[g]], wadd=[s2_b[g]])
            proj_fm(w_glu[:, 512 * blk:512 * blk + 512], 512, rhs_y, [BufGroup(yT_b)] * 5, ev_glu)

            def ev_zs(oc, tbi, pap, pb, blk=blk):
                t0, n = TBS[tbi]
                g = 4 * blk + oc
                ctr["i"] += 1
                gi = ctr["i"] % 4
                fi = ctr["i"] % 2
                P.op(ACT, lambda e: e.activation(out=gtmp[gi][:, 0:n], in_=pap, func=AF.Sigmoid), reads=[pb], writes=[b_gtmp[gi]])
                P.op(DVE, lambda e: e.tensor_tensor(out=ftmp[fi][:, 0:n], in0=pap, in1=gtmp[gi][:, 0:n], op=ALU.mult),
                     reads=[pb, b_gtmp[gi]], writes=[b_ftmp[fi]])
                P.op(POOL, lambda e: e.tensor_tensor(out=s2T[:, g, t0:t0 + n], in0=s2T[:, g, t0:t0 + n], in1=ftmp[fi][:, 0:n], op=ALU.mult),
                     reads=[b_ftmp[fi], s2_b[g]], wadd=[s2_b[g]])
            proj_fm(w_in[:, OFF_ZS + 512 * blk:OFF_ZS + 512 * blk + 512], 512, rhs_h, hT_b, ev_zs)

        P.phase(11)
        gbs = RA
        gbs_b = [Buf(f"gbs{g}") for g in range(8)]
        handoff(gbs_b, yT_b)
        rhs_s2 = lambda k, t0, n: s2T[:, k, t0:t0 + n]
        for blk in range(2):
            def ev_gs(oc, tbi, pap, pb, blk=blk):
                t0, n = TBS[tbi]
                g = 4 * blk + oc
                P.op(ACT, lambda e: e.activation(out=gbs[:, g, t0:t0 + n], in_=pap, func=AF.Sigmoid), reads=[pb], wadd=[gbs_b[g]])
            proj_fm(w_in[:, OFF_GS + 512 * blk:OFF_GS + 512 * blk + 512], 512, rhs_h, hT_b, ev_gs)

            def ev_bs(oc, tbi, pap, pb, blk=blk):
                t0, n = TBS[tbi]
                g = 4 * blk + oc
                P.op(DVE, lambda e: e.tensor_tensor(out=gbs[:, g, t0:t0 + n], in0=pap, in1=gbs[:, g, t0:t0 + n], op=ALU.mult),
                     reads=[pb, gbs_b[g]], wadd=[gbs_b[g]])
            proj_fm(w_bs[:, 512 * blk:512 * blk + 512], 512, rhs_s2, [BufGroup(s2_b)] * 5, ev_bs)

        P.phase(12)
        oT = RB
        oT_b = [Buf(f"oT{g}") for g in range(8)]
        handoff(oT_b, s2_b)
        oc_ = O_C
        qT = cv_(oc_ + 0, [2, NT], BF16); kT2 = cv_(oc_ + 8256, [128 + NT], BF16); Vaug = cv_(oc_ + 12640, [18, 128], BF16)
        EB = cv_(oc_ + 17248, [2, 16, 128], BF16); EB0 = cv_(oc_ + 25440, [16, 128], BF16)
        Et = [cv_(oc_ + 29536 + 1024 * i, [512], BF16) for i in range(4)]
        PT = [cv_(oc_ + 33632 + 1024 * i, [512], BF16) for i in range(4)]
        rc = [cv_(oc_ + 37728 + 2048 * i, [512], F32) for i in range(2)]
        maskt = cv_(oc_ + 41824, [2, 128], F32); RT = cv_(oc_ + 42848, [384], F32); relb = cv_(oc_ + 44384, [16], F32)
        es16 = cv_(oc_ + 44448, [16], F32); klast = cv_(oc_ + 44512, [256], F32); vlast = cv_(oc_ + 45536, [256], F32)
        knew = cv_(oc_ + 46560, [256], F32); vnew = cv_(oc_ + 47584, [256], F32)
        Kc = cv_(oc_ + 48608, [16, 256], F32)
        KcT = cv_(oc_ + 64992, [16, 2, 128], BF16)
        Vcs = cv_(oc_ + 73184, [16, 256], BF16)
        QsT = cv_(oc_ + 81376, [2, 4, 16], BF16)
        dgt = cv_(oc_ + 81632, [64], F32); vnb = cv_(oc_ + 81888, [256], BF16); pdg = cv_(oc_ + 82400, [64], BF16)
        esr = cv_(oc_ + 82528, [64], BF16); ebs = cv_(oc_ + 82656, [16], F32); rcs = cv_(oc_ + 82720, [128], F32)
        ptS = cv_(oc_ + 83232, [128], BF16); ones_k = cv_(oc_ + 83488, [128], BF16)
        attn_bufs = {n: Buf(n) for n in ["qT", "kT2", "Vaug", "EB", "EB0", "mask", "RT", "relb", "es16", "klast", "vlast", "knew", "vnew",
                                         "Kc", "KcT", "Vcs", "QsT", "dgt", "vnb", "pdg", "esr", "ebs", "rcs", "ptS", "ones_k"]}
        A = attn_bufs
        b_Et = [Buf() for _ in range(4)]; b_PT = [Buf() for _ in range(4)]; b_rc = [Buf(), Buf()]
        prev_c = [b_pw, b_bb, b_ccm, b_R8, b_R128, b_A2k, b_Hend, b_carry, b_Sb, b_coef, b_misc, b_t, b_KL0, b_Ca1,
                  b_stile, b_Hsp, b_Hn, b_HbS, b_Qs, b_stg, b_sout] + b_gtmp + b_ftmp + all_ssm_tmp + b_CaBD + b_KL
        handoff(list(A.values()) + b_Et + b_PT + b_rc, prev_c)
        misc_load(SP, RT[0:32, :], rtab, A["RT"]); misc_load(SP, relb[0:32, :], rel_bias, A["relb"])
        misc_load(SP, maskt.rearrange("p h q -> p (h q)"), maskc, A["mask"])
        misc_load(SP, es16[0:1, :], sinks.rearrange("(o n) -> o n", o=1), A["es16"])
        misc_load(SP, ebs[0:16, :], rel_bias[0:1, :].to_broadcast([16, 16]), A["ebs"])
        misc_load(SP, dgt[0:16, :], diagc, A["dgt"])
        P.op(ACT, lambda e: e.activation(out=es16[0:1, :], in_=es16[0:1, :], func=AF.Exp), reads=[A["es16"]], writes=[A["es16"]])
        P.op(ACT, lambda e: e.activation(out=ebs[0:16, :], in_=ebs[0:16, :], func=AF.Exp), reads=[A["ebs"]], writes=[A["ebs"]])
        for kv in range(4):
            for sl_, i in enumerate([0, 2, 1, 3]):
                h = 4 * kv + i
                P.op(DVE, lambda e, kv=kv, sl_=sl_, h=h: e.tensor_copy(out=ES[0:1, kv, sl_, :], in_=es16[0:1, h:h + 1].to_broadcast([1, 128])),
                     reads=[A["es16"]], wadd=[b_ES])
        P.op(POOL, lambda e: e.memset(ones_k, 1.0), writes=[A["ones_k"]])
        for half in range(2):
            for qb in range(4):
                bke = bank()
                for qq in range(32):
                    q = 32 * qb + qq
                    st_ = (127 - q) if half == 0 else (255 - q)
                    P.op(PE, lambda e, bke=bke, qq=qq, st_=st_: e.matmul(ps[bke][:, 16 * qq:16 * qq + 16], lhsT=RT[0:32, st_:st_ + 128],
                                                                       rhs=relb[0:32, :], start=True, stop=True),
                         reads=[A["RT"], A["relb"]] if qq == 0 else [], writes=[psb[bke]] if qq == 0 else [], sig=(qq == 31))
                if not P.dead:
                    P.attach(Dep(P.esem[PE], P.cnt[PE]), reads=[A["RT"], A["relb"]], writes=[psb[bke]])
                P.op(ACT, lambda e, bke=bke, half=half, qb=qb: e.activation(
                    out=EB[:, half, :, 32 * qb:32 * qb + 32], in_=ps[bke][:, 0:512].rearrange("p (q h) -> p h q", h=16), func=AF.Exp),
                    reads=[psb[bke]], wadd=[A["EB"]])
        P.op(DVE, lambda e: e.tensor_tensor(out=EB, in0=EB, in1=maskt.unsqueeze(2).to_broadcast([128, 2, 16, 128]), op=ALU.mult),
             reads=[A["EB"], A["mask"]], writes=[A["EB"]])
        P.op(DVE, lambda e: e.tensor_scalar(out=EB0, in0=EB[:, 0], scalar1=flags[:, 0:1], scalar2=None, op0=ALU.mult),
             reads=[A["EB"], b_flags], writes=[A["EB0"]])
        P.dma(SP, dout_sem, sck[:, 0:127, :], ck[:, 1:128, :])
        P.dma(SP, dout_sem, scv[:, 0:127, :], cv[:, 1:128, :])
        kc_sem = P.dsem("kc")
        P.dma(SP, kc_sem, Kc, ck.rearrange("s t f -> t s f"), writes=[A["Kc"]])
        for s_ in range(NS):
            bkt = bank()
            for kvp in range(2):
                P.op(PE, lambda e, s_=s_, kvp=kvp, bkt=bkt: e.transpose(out=ps[bkt][:, 128 * kvp:128 * kvp + 128],
                                                                     in_=Kc[:, s_, 128 * kvp:128 * kvp + 128], identity=ident),
                     reads=[A["Kc"], b_ident], writes=[psb[bkt]] if kvp == 0 else [], wadd=[psb[bkt]] if kvp == 1 else [])
            evac_copy(s_, KcT[:, s_].rearrange("p a t -> p (a t)"), ps[bkt][:, 0:256], [psb[bkt]], [], wadd=[A["KcT"]])
        P.dma(SP, kc_sem, Kc, cv.rearrange("s t f -> t s f"), reads=[A["KcT"]], writes=[A["Kc"]])
        P.op(POOL, lambda e: e.tensor_copy(out=Vcs, in_=Kc), reads=[A["Kc"]], writes=[A["Vcs"]])
        P.op(POOL, lambda e: e.memset(Vaug[:, :, 64:128], 1.0), writes=[A["Vaug"]])

        rhs_hh = lambda k, t0, n: hTh[:, k, 0:n]
        TB5 = TBS
        def attn_kv(kv):
            def ev_q(oc, tbi, pap, pb):
                t0, n = TBS[tbi]
                ctr["i"] += 1
                evac_copy(ctr["i"], qT[:, oc, t0:t0 + n], pap, [pb], [], wadd=[A["qT"]])
            A["qT"].r = list(A["qT"].r) + list(A["qT"].w); A["qT"].w = []
            proj_fm(w_in[:, OFF_Q + 256 * kv:OFF_Q + 256 * kv + 256], 256, rhs_h, hT_b, ev_q)
            s = wctr[0] % 2
            wctr[0] += 1
            for dup in range(2):
                P.dma(POOL, wsem[s], wslot[s][:, :, 64 * dup:64 * dup + 64],
                      w_in[:, OFF_K + 64 * kv:OFF_K + 64 * kv + 64].rearrange("(k p) f -> p k f", p=128),
                      writes=[wslot_b[s]] if dup == 0 else [], wadd=[wslot_b[s]] if dup == 1 else [])
            A["kT2"].r = list(A["kT2"].r) + list(A["kT2"].w); A["kT2"].w = []
            kblocks = [(hTh, hTh_b, 0, 128, 0)] + [(hT, hT_b[i], t0, n, 128 + t0) for i, (t0, n) in enumerate(TBS)]
            for bi_, (src, sb_, t0, n, c0) in enumerate(kblocks):
                bkk = bank()
                for k in range(8):
                    P.op(PE, lambda e, bkk=bkk, k=k, src=src, t0=t0, n=n, s=s: e.matmul(
                        ps[bkk][:, 0:n], lhsT=wslot[s][:, k, 0:128], rhs=src[:, k, t0:t0 + n], start=(k == 0), stop=(k == 7)),
                        reads=[wslot_b[s], sb_] if k == 0 else [], writes=[psb[bkk]] if k == 0 else [], sig=(k == 7))
                if not P.dead:
                    P.attach(Dep(P.esem[PE], P.cnt[PE]), reads=[wslot_b[s], sb_], writes=[psb[bkk]])
                evac_copy(bi_, kT2[:, c0:c0 + n], ps[bkk][:, 0:n], [psb[bkk]], [], wadd=[A["kT2"]])
            bkl_ = bank()
            for j_, (c0_, m_) in enumerate([(NP - 128, 128), (NP, NS)]):
                for k in range(8):
                    P.op(PE, lambda e, j_=j_, c0_=c0_, m_=m_, k=k, s=s: e.matmul(
                        ps[bkl_][0:m_, 64 * j_:64 * j_ + 64], lhsT=hT[:, k, c0_:c0_ + m_], rhs=wslot[s][:, k, 0:64],
                        start=(k == 0), stop=(k == 7)),
                        reads=[wslot_b[s], hT_b[3], hT_b[4]] if (k == 0 and j_ == 0) else [],
                        writes=[psb[bkl_]] if (k == 0 and j_ == 0) else [], sig=(k == 7 and j_ == 1))
            if not P.dead:
                P.attach(Dep(P.esem[PE], P.cnt[PE]), reads=[wslot_b[s], hT_b[3], hT_b[4]], writes=[psb[bkl_]])
            P.op(DVE, lambda e, kv=kv: e.tensor_copy(out=klast[:, 64 * kv:64 * kv + 64], in_=ps[bkl_][:, 0:64]), reads=[psb[bkl_]], wadd=[A["klast"]])
            P.op(DVE, lambda e, kv=kv: e.tensor_copy(out=knew[0:NS, 64 * kv:64 * kv + 64], in_=ps[bkl_][0:NS, 64:128]), reads=[psb[bkl_]], wadd=[A["knew"]])
            s = wctr[0] % 2
            wctr[0] += 1
            P.dma(POOL, wsem[s], wslot[s][:, :, 0:64], w_in[:, OFF_V + 64 * kv:OFF_V + 64 * kv + 64].rearrange("(k p) f -> p k f", p=128),
                  writes=[wslot_b[s]])
            A["Vaug"].r = list(A["Vaug"].r) + list(A["Vaug"].w); A["Vaug"].w = []
            vtiles = [(hTh, hTh_b, 0, 128)] + [(hT, hT_b[i // 4], 128 * i, 128) for i in range(16)] + [(hT, hT_b[4], NP, NS)]
            for grp in range(3):
                bkv = bank()
                tl_ = vtiles[8 * grp:8 * grp + 8]
                for j_, (src, sb_, c0_, m_) in enumerate(tl_):
                    for k in range(8):
                        firstg = (j_ == 0 and k == 0)
                        P.op(PE, lambda e, bkv=bkv, j_=j_, src=src, c0_=c0_, m_=m_, k=k, s=s: e.matmul(
                            ps[bkv][0:m_, 64 * j_:64 * j_ + 64], lhsT=src[:, k, c0_:c0_ + m_], rhs=wslot[s][:, k, 0:64],
                            start=(k == 0), stop=(k == 7)),
                            reads=[wslot_b[s], sb_, hTh_b] + hT_b if firstg else [], writes=[psb[bkv]] if firstg else [],
                            sig=(j_ == len(tl_) - 1 and k == 7))
                if not P.dead:
                    P.attach(Dep(P.esem[PE], P.cnt[PE]), reads=[wslot_b[s], hTh_b] + hT_b, writes=[psb[bkv]])
                nt_ = len(tl_)
                if grp < 2:
                    P.op(ACT, lambda e, bkv=bkv, grp=grp: e.activation(out=Vaug[:, 8 * grp:8 * grp + 8, 0:64],
                                                                     in_=ps[bkv][:, 0:512].rearrange("p (t d) -> p t d", d=64), func=AF.Copy),
                         reads=[psb[bkv]], wadd=[A["Vaug"]])
                    if grp == 1:
                        pass
                else:
                    P.op(ACT, lambda e, bkv=bkv: e.activation(out=Vaug[:, 16, 0:64], in_=ps[bkv][:, 0:64], func=AF.Copy),
                         reads=[psb[bkv]], wadd=[A["Vaug"]])
                    P.op(ACT, lambda e, bkv=bkv: e.activation(out=Vaug[0:NS, 17, 0:64], in_=ps[bkv][0:NS, 64:128], func=AF.Copy),
                         reads=[psb[bkv]], wadd=[A["Vaug"]])
                    P.op(ACT, lambda e, bkv=bkv, kv=kv: e.activation(out=vlast[:, 64 * kv:64 * kv + 64], in_=ps[bkv][:, 0:64], func=AF.Copy),
                         reads=[psb[bkv]], wadd=[A["vlast"]])
                    P.op(ACT, lambda e, bkv=bkv, kv=kv: e.activation(out=vnew[0:NS, 64 * kv:64 * kv + 64], in_=ps[bkv][0:NS, 64:128], func=AF.Copy),
                         reads=[psb[bkv]], wadd=[A["vnew"]])
            qv = lambda base, b_: qT[base:base + 64, 0:2, 128 * b_:128 * b_ + 128]
            def attn_s1(b_):
                ia = (2 * b_) % 4; ib = (2 * b_ + 1) % 4
                bA, bB = bank(), bank()
                kprev = slice(128 * b_, 128 * b_ + 128); kcur = slice(128 * b_ + 128, 128 * b_ + 256)
                seq = [(bA, 0, 0, kprev), (bB, 64, 0, kprev), (bA, 0, 1, kcur), (bB, 64, 1, kcur)]
                for (bk_, base, half, ks) in seq:
                    first = (half == 0)
                    P.op(PE, lambda e, bk_=bk_, base=base, half=half, ks=ks, b_=b_: e.matmul(
                        ps[bk_][:, 256 * half:256 * half + 256], lhsT=kT2[base:base + 64, ks], rhs=qv(base, b_), start=True, stop=True),
                        reads=[A["kT2"], A["qT"]] if first else [], writes=[psb[bk_]] if first else [], sig=(half == 1))
                    if half == 1 and not P.dead:
                        P.attach(Dep(P.esem[PE], P.cnt[PE]), reads=[A["kT2"], A["qT"]], writes=[psb[bk_]])
                for (bk_, ie, base_h) in [(bA, ia, 0), (bB, ib, 1)]:
                    P.op(ACT, lambda e, bk_=bk_, ie=ie: e.activation(out=Et[ie], in_=ps[bk_][:, :], func=AF.Exp, scale=0.125),
                         reads=[psb[bk_]], writes=[b_Et[ie]])
                    Ev = Et[ie].rearrange("p (h i q) -> p h i q", h=2, i=2)
                    Pv = PT[ie].rearrange("p (h i q) -> p h i q", h=2, i=2)
                    h0 = 4 * kv + base_h
                    eng = DVE if base_h == 0 else POOL
                    if b_ > 0:
                        P.op(eng, lambda e, Ev=Ev, Pv=Pv, h0=h0: e.tensor_tensor(out=Pv, in0=Ev, in1=EB[:, :, h0:h0 + 3:2, :], op=ALU.mult),
                             reads=[b_Et[ie], A["EB"]], writes=[b_PT[ie]])
                    else:
                        P.op(eng, lambda e, Ev=Ev, Pv=Pv, h0=h0: e.tensor_tensor(out=Pv[:, 0], in0=Ev[:, 0], in1=EB0[:, h0:h0 + 3:2, :], op=ALU.mult),
                             reads=[b_Et[ie], A["EB0"]], writes=[b_PT[ie]])
                        P.op(eng, lambda e, Ev=Ev, Pv=Pv, h0=h0: e.tensor_tensor(out=Pv[:, 1], in0=Ev[:, 1], in1=EB[:, 1, h0:h0 + 3:2, :], op=ALU.mult),
                             reads=[b_Et[ie], A["EB"]], wadd=[b_PT[ie]])

            def attn_s2(b_):
                ia = (2 * b_) % 4; ib = (2 * b_ + 1) % 4
                bO = bank()
                mm = [(ia, 0, b_, True), (ia, 1, b_ + 1, False), (ib, 0, b_, False), (ib, 1, b_ + 1, False)]
                for j_, (ip, half, tile, st_) in enumerate(mm):
                    cols = slice(0, 256) if ip == ia else slice(256, 512)
                    P.op(PE, lambda e, ip=ip, half=half, tile=tile, st_=st_, cols=cols: e.matmul(
                        ps[bO][:, cols], lhsT=Vaug[:, tile, :], rhs=PT[ip][:, 256 * half:256 * half + 256], start=st_, stop=False),
                        reads=[A["Vaug"], b_PT[ia], b_PT[ib], b_ES, b_ones] if j_ == 0 else [], writes=[psb[bO]] if j_ == 0 else [], sig=False)
                P.op(PE, lambda e, kv=kv: e.matmul(ps[bO][:, 0:512], lhsT=onesd[0:1, :], rhs=ES[0:1, kv].rearrange("p i q -> p (i q)"),
                                                   start=False, stop=True), sig=True)
                if not P.dead:
                    P.attach(Dep(P.esem[PE], P.cnt[PE]), reads=[A["Vaug"], b_PT[ia], b_PT[ib], b_ES, b_ones], writes=[psb[bO]])
                ir = b_ % 2
                P.op(ACT, lambda e, ir=ir: e.activation(out=rc[ir][64:128, :], in_=ps[bO][64:128, :], func=AF.Ln), reads=[psb[bO]], writes=[b_rc[ir]])
                P.op(ACT, lambda e, ir=ir: e.activation(out=rc[ir][64:128, :], in_=rc[ir][64:128, :], func=AF.Exp, scale=-1.0),
                     reads=[b_rc[ir]], writes=[b_rc[ir]])
                for par in range(2):
                    P.op(DVE, lambda e, par=par, ir=ir, b_=b_, kv=kv: e.tensor_tensor(
                        out=oT[64 * par:64 * par + 64, 2 * kv:2 * kv + 2, 128 * b_:128 * b_ + 128],
                        in0=ps[bO][0:64, 256 * par:256 * par + 256].rearrange("p (c q) -> p c q", c=2),
                        in1=rc[ir][64:128, 256 * par:256 * par + 256].rearrange("p (c q) -> p c q", c=2), op=ALU.mult),
                        reads=[psb[bO], b_rc[ir]], wadd=[oT_b[2 * kv], oT_b[2 * kv + 1]])

            attn_s1(0)
            for b_ in range(1, 16):
                attn_s1(b_)
                attn_s2(b_ - 1)
            attn_s2(15)
            base = 64 * (kv % 2)
            for i in range(4):
                hsrc = 64 * (i % 2)
                P.op(POOL, lambda e, i=i, hsrc=hsrc, base=base: e.tensor_copy(out=QsT[base:base + 64, 0, i, :], in_=qT[hsrc:hsrc + 64, i // 2, NP:NT]),
                     reads=[A["qT"]], writes=[A["QsT"]] if i == 0 else [], wadd=[A["QsT"]] if i > 0 else [])
            for sl_, i in enumerate([0, 1, 2, 3]):
                P.op(DVE, lambda e, i=i, kv=kv: e.tensor_copy(out=esr[0:1, :].rearrange("p (s i) -> p s i", i=4)[:, :, i],
                                                            in_=es16[0:1, 4 * kv + i:4 * kv + i + 1].to_broadcast([1, 16])),
                     reads=[A["es16"]], writes=[A["esr"]] if i == 0 else [], wadd=[A["esr"]] if i > 0 else [])
            bS, bD, bN = bank(), bank(), bank()
            Qsi = QsT[base:base + 64, 0].rearrange("p i s -> p s i")
            for s_ in range(NS):
                P.op(PE, lambda e, s_=s_, base=base, kv=kv: e.matmul(ps[bS][:, 4 * s_:4 * s_ + 4], lhsT=KcT[base:base + 64, s_, kv // 2, :],
                                                                  rhs=QsT[base:base + 64, 0, :, s_], start=True, stop=True),
                     reads=[A["KcT"], A["QsT"], A["kT2"]] if s_ == 0 else [], writes=[psb[bS]] if s_ == 0 else [], sig=False)
            P.op(PE, lambda e, base=base: e.matmul(ps[bS][0:NS, 64:128], lhsT=kT2[base:base + 64, 128 + NP:128 + NT], rhs=Qsi, start=True, stop=True), sig=True)
            if not P.dead:
                P.attach(Dep(P.esem[PE], P.cnt[PE]), reads=[A["KcT"], A["QsT"], A["kT2"]], writes=[psb[bS]])
            P.op(ACT, lambda e: e.activation(out=ptS[:, 0:64], in_=ps[bS][:, 0:64], func=AF.Exp, scale=0.125), reads=[psb[bS]], writes=[A["ptS"]])
            P.op(ACT, lambda e: e.activation(out=pdg[0:NS, :], in_=ps[bS][0:NS, 64:128], func=AF.Exp, scale=0.125), reads=[psb[bS]], writes=[A["pdg"]])
            P.op(DVE, lambda e, kv=kv: e.tensor_tensor(out=ptS[:, 0:64].rearrange("p (s i) -> p s i", i=4), in0=ptS[:, 0:64].rearrange("p (s i) -> p s i", i=4),
                                                in1=EB[:, 0, 4 * kv:4 * kv + 4, 0].unsqueeze(1).to_broadcast([128, 16, 4]), op=ALU.mult),
                 reads=[A["ptS"], A["EB"]], writes=[A["ptS"]])
            P.op(DVE, lambda e, kv=kv: e.tensor_tensor(out=pdg[0:NS, :].rearrange("p (s i) -> p s i", i=4), in0=pdg[0:NS, :].rearrange("p (s i) -> p s i", i=4),
                                                in1=ebs[0:NS, 4 * kv:4 * kv + 4].unsqueeze(1).to_broadcast([NS, 16, 4]), op=ALU.mult),
                 reads=[A["pdg"], A["ebs"]], writes=[A["pdg"]])
            P.op(DVE, lambda e: e.tensor_tensor(out=pdg[0:NS, :], in0=pdg[0:NS, :], in1=dgt[0:NS, :], op=ALU.mult),
                 reads=[A["pdg"], A["dgt"]], writes=[A["pdg"]])
            P.op(POOL, lambda e, kv=kv: e.tensor_copy(out=vnb[0:NS, 64 * kv:64 * kv + 64], in_=vnew[0:NS, 64 * kv:64 * kv + 64]),
                 reads=[A["vnew"]], writes=[A["vnb"]])
            P.op(PE, lambda e: e.matmul(ps[bD][:, 0:64], lhsT=ones_k, rhs=ptS[:, 0:64], start=True, stop=False),
                 reads=[A["ones_k"], A["ptS"], A["pdg"], A["esr"]], writes=[psb[bD]], sig=False)
            P.op(PE, lambda e: e.matmul(ps[bD][:, 0:64], lhsT=ones_k[0:NS, :], rhs=pdg[0:NS, :], start=False, stop=False), sig=False)
            P.op(PE, lambda e: e.matmul(ps[bD][:, 0:64], lhsT=ones_k[0:1, :], rhs=esr[0:1, :], start=False, stop=True), sig=True)
            if not P.dead:
                P.attach(Dep(P.esem[PE], P.cnt[PE]), reads=[A["ones_k"], A["ptS"], A["pdg"], A["esr"]], writes=[psb[bD]])
            ptv = ptS[:, 0:64].rearrange("p (s i) -> p s i", i=4)
            pdv = pdg[0:NS, :].rearrange("p (s i) -> p s i", i=4)
            psn = ps[bN][:, 0:64].rearrange("p (s i) -> p s i", i=4)
            for par in range(2):
                for s_ in range(NS):
                    P.op(PE, lambda e, par=par, s_=s_, kv=kv: e.matmul(
                        psn[64 * par:64 * par + 64, s_, par:4:2], lhsT=Vcs[:, s_, 64 * kv:64 * kv + 64], rhs=ptv[:, s_, par:4:2],
                        start=(s_ == 0), stop=False, tile_position=(0, 64 * par)),
                        reads=[A["Vcs"], A["ptS"], A["pdg"], A["vnb"]] if (par == 0 and s_ == 0) else [],
                        writes=[psb[bN]] if (par == 0 and s_ == 0) else [], sig=False)
                P.op(PE, lambda e, par=par, kv=kv: e.matmul(
                    psn[64 * par:64 * par + 64, :, par:4:2], lhsT=vnb[0:NS, 64 * kv:64 * kv + 64], rhs=pdv[:, :, par:4:2],
                    start=False, stop=True, tile_position=(0, 64 * par)), sig=(par == 1))
            if not P.dead:
                P.attach(Dep(P.esem[PE], P.cnt[PE]), reads=[A["Vcs"], A["ptS"], A["pdg"], A["vnb"]], writes=[psb[bN]])
            P.op(DVE, lambda e: e.reciprocal(out=rcs[:, 0:64], in_=ps[bD][:, 0:64]), reads=[psb[bD]], writes=[A["rcs"]])
            rcv = rcs[:, 0:64].rearrange("p (s i) -> p s i", i=4)
            for par in range(2):
                P.op(DVE, lambda e, par=par, kv=kv: e.tensor_tensor(
                    out=oT[64 * par:64 * par + 64, 2 * kv:2 * kv + 2, NP:NT],
                    in0=psn[64 * par:64 * par + 64, :, par:4:2].rearrange("p s c -> p c s"),
                    in1=rcv[64 * par:64 * par + 64, :, par:4:2].rearrange("p s c -> p c s"), op=ALU.mult),
                    reads=[psb[bN], A["rcs"]], wadd=[oT_b[2 * kv], oT_b[2 * kv + 1]])
        for kv in range(4):
            attn_kv(kv)
        P.dma(SP, osem_new("pck"), pck, klast, reads=[A["klast"]])
        P.dma(SP, osem_new("pcv"), pcv, vlast, reads=[A["vlast"]])
        P.dma(SP, osem_new("sck"), sck[:, 127, :], knew[0:NS, :], reads=[A["knew"]])
        P.dma(SP, osem_new("scv"), scv[:, 127, :], vnew[0:NS, :], reads=[A["vnew"]])

        P.phase(13)
        sgaT = cv_(O_C + 0, [8, NT], BF16)
        sga_b = [Buf(f"sga{g}") for g in range(8)]
        gt2 = [cv_(O_C + 33024 + 1024 * i, [512], BF16) for i in range(4)]
        ft2 = [cv_(O_C + 37120 + 2048 * i, [512], F32) for i in range(2)]
        b_gt2 = [Buf() for _ in range(4)]; b_ft2 = [Buf(), Buf()]
        handoff(sga_b + b_gt2 + b_ft2, list(A.values()) + b_Et + b_PT + b_rc)
        for blk in range(2):
            def ev_za(oc, tbi, pap, pb, blk=blk):
                t0, n = TBS[tbi]
                g = 4 * blk + oc
                ctr["i"] += 1
                gi = ctr["i"] % 4; fi = ctr["i"] % 2
                P.op(ACT, lambda e: e.activation(out=gt2[gi][:, 0:n], in_=pap, func=AF.Sigmoid), reads=[pb], writes=[b_gt2[gi]])
                P.op(DVE, lambda e: e.tensor_tensor(out=ft2[fi][:, 0:n], in0=pap, in1=gt2[gi][:, 0:n], op=ALU.mult),
                     reads=[pb, b_gt2[gi]], writes=[b_ft2[fi]])
                P.op(POOL, lambda e: e.tensor_tensor(out=oT[:, g, t0:t0 + n], in0=oT[:, g, t0:t0 + n], in1=ft2[fi][:, 0:n], op=ALU.mult),
                     reads=[b_ft2[fi], oT_b[g]], wadd=[oT_b[g]])
            proj_fm(w_in[:, OFF_ZA + 512 * blk:OFF_ZA + 512 * blk + 512], 512, rhs_h, hT_b, ev_za)
        for blk in range(2):
            def ev_ga(oc, tbi, pap, pb, blk=blk):
                t0, n = TBS[tbi]
                g = 4 * blk + oc
                P.op(ACT, lambda e: e.activation(out=sgaT[:, g, t0:t0 + n], in_=pap, func=AF.Sigmoid), reads=[pb], wadd=[sga_b[g]])
            proj_fm(w_in[:, OFF_GA + 512 * blk:OFF_GA + 512 * blk + 512], 512, rhs_h, hT_b, ev_ga)
        mT = RA
        mT_b = gbs_b
        rhs_o = lambda k, t0, n: oT[:, k, t0:t0 + n]
        for blk in range(2):
            def ev_ba(oc, tbi, pap, pb, blk=blk):
                t0, n = TBS[tbi]
                g = 4 * blk + oc
                ctr["i"] += 1
                fi = ctr["i"] % 2
                P.op(DVE, lambda e: e.tensor_tensor(out=ft2[fi][:, 0:n], in0=pap, in1=sgaT[:, g, t0:t0 + n], op=ALU.mult),
                     reads=[pb, sga_b[g]], writes=[b_ft2[fi]])
                P.op(POOL, lambda e: e.tensor_tensor(out=mT[:, g, t0:t0 + n], in0=mT[:, g, t0:t0 + n], in1=ft2[fi][:, 0:n], op=ALU.add),
                     reads=[b_ft2[fi], mT_b[g]], wadd=[mT_b[g]])
            proj_fm(w_ba[:, 512 * blk:512 * blk + 512], 512, rhs_o, [BufGroup(oT_b)] * 5, ev_ba)

        P.phase(14)
        o2 = O_C + 8000
        NSL = 4
        GateB = cv_(o2 + 0, [1024], F32); LnG = cv_(o2 + 4096, [1024], F32); LnB = cv_(o2 + 8192, [1024], F32)
        gateS = cv_(o2 + 12288, [1024], F32); grow = cv_(o2 + 16384, [1024], F32)
        xt = [cv_(o2 + 20480 + 4096 * i, [1024], F32) for i in range(NSL)]
        rt = [cv_(o2 + 36864 + 4096 * i, [1024], F32) for i in range(NSL)]
        stt = cv_(o2 + 53248, [NSL, 2, 6], F32); mvt = cv_(o2 + 53504, [NSL, 2], F32); rsd = cv_(o2 + 53568, [NSL, 2], F32)
        mhalf = cv_(o2 + 53632, [1], F32)
        b_GateB = Buf(); b_LnG = Buf(); b_LnB = Buf(); b_gateS = Buf(); b_grow = Buf(); b_xt = [Buf() for _ in range(NSL)]; b_rt = [Buf() for _ in range(NSL)]
        b_stt = [Buf() for _ in range(NSL)]; b_mh = Buf()
        xsem2 = [P.dsem(f"xt{i}") for i in range(NSL)]; osem = [P.dsem(f"o{i}") for i in range(NSL)]
        handoff([b_GateB, b_LnG, b_LnB, b_gateS, b_grow] + b_xt + b_rt + b_stt + [b_mh], list(A.values()) + b_Et + b_PT + b_rc + sga_b + b_gt2 + b_ft2)
        misc_load(SP, LnG, ln_g.rearrange("(o n) -> o n", o=1).to_broadcast([128, 1024]), b_LnG)
        misc_load(SP, LnB, ln_b.rearrange("(o n) -> o n", o=1).to_broadcast([128, 1024]), b_LnB)
        P.op(POOL, lambda e: e.memset(mhalf, -0.5), writes=[b_mh])
        for hb in range(2):
            bkg = bank(); bkg2 = bank()
            for kk in range(4):
                k = 4 * hb + kk
                P.op(PE, lambda e, k=k, kk=kk, bkg=bkg: e.transpose(out=ps[bkg][0:1, 128 * kk:128 * kk + 128], in_=modT[:, 16 + k, 0:1], identity=ident),
                     reads=[b_modT, b_ident], writes=[psb[bkg]] if kk == 0 else [], wadd=[psb[bkg]] if kk > 0 else [])
                P.op(PE, lambda e, k=k, kk=kk, bkg2=bkg2: e.transpose(out=ps[bkg2][0:NS, 128 * kk:128 * kk + 128], in_=modT[:, 16 + k, 1:17], identity=ident),
                     reads=[b_modT, b_ident], writes=[psb[bkg2]] if kk == 0 else [], wadd=[psb[bkg2]] if kk > 0 else [])
            P.op(DVE, lambda e, hb=hb, bkg=bkg: e.tensor_copy(out=grow[0:1, 512 * hb:512 * hb + 512], in_=ps[bkg][0:1, 0:512]), reads=[psb[bkg]], wadd=[b_grow])
            P.op(DVE, lambda e, hb=hb, bkg2=bkg2: e.tensor_copy(out=gateS[0:NS, 512 * hb:512 * hb + 512], in_=ps[bkg2][0:NS, 0:512]), reads=[psb[bkg2]], wadd=[b_gateS])
        for hb in range(2):
            bkb = bank()
            P.op(PE, lambda e, hb=hb, bkb=bkb: e.matmul(ps[bkb][:, 0:512], lhsT=ones1[0:1, :], rhs=grow[0:1, 512 * hb:512 * hb + 512], start=True, stop=True),
                 reads=[b_grow, b_ones], writes=[psb[bkb]])
            P.op(DVE, lambda e, hb=hb, bkb=bkb: e.tensor_copy(out=GateB[:, 512 * hb:512 * hb + 512], in_=ps[bkb][:, 0:512]), reads=[psb[bkb]], wadd=[b_GateB])
        so = [load_w(w_out[:, 0:512], 512), load_w(w_out[:, 512:1024], 512)]
        def x_load(ti_):
            rows_, c0_ = (128, 128 * ti_) if ti_ < 16 else (NS, NP)
            src_ = xp[c0_:c0_ + 128, :] if ti_ < 16 else xs
            P.dma(SP, xsem2[ti_ % NSL], xt[ti_ % NSL][0:rows_, :], src_, writes=[b_xt[ti_ % NSL]])
        for ti_ in range(NSL):
            x_load(ti_)
        for tt_i in range(17):
            rows, c0 = (128, 128 * tt_i) if tt_i < 16 else (NS, NP)
            sl = tt_i % NSL
            gate_ap = GateB if tt_i < 16 else gateS
            gate_b = b_GateB if tt_i < 16 else b_gateS
            for fb in range(2):
                bko = bank()
                for k in range(8):
                    P.op(PE, lambda e, bko=bko, k=k, fb=fb, rows=rows, c0=c0: e.matmul(
                        ps[bko][0:rows, 0:512], lhsT=mT[:, k, c0:c0 + rows], rhs=wslot[so[fb]][:, k, 0:512], start=(k == 0), stop=(k == 7)),
                        reads=[wslot_b[so[fb]]] + mT_b if k == 0 else [], writes=[psb[bko]] if k == 0 else [], sig=(k == 7))
                if not P.dead:
                    P.attach(Dep(P.esem[PE], P.cnt[PE]), reads=[wslot_b[so[fb]]] + mT_b, writes=[psb[bko]])
                P.op(DVE, lambda e, bko=bko, fb=fb, rows=rows, sl=sl, gate_ap=gate_ap: e.tensor_tensor(
                    out=rt[sl][0:rows, 512 * fb:512 * fb + 512], in0=ps[bko][0:rows, 0:512], in1=gate_ap[0:rows, 512 * fb:512 * fb + 512], op=ALU.mult),
                    reads=[psb[bko], gate_b], writes=[b_rt[sl]] if fb == 0 else [], wadd=[b_rt[sl]] if fb == 1 else [])
            P.op(DVE, lambda e, rows=rows, sl=sl: e.scalar_tensor_tensor(out=rt[sl][0:rows, :], in0=xt[sl][0:rows, :], scalar=float(ALPHA),
                                                                         in1=rt[sl][0:rows, :], op0=ALU.mult, op1=ALU.add),
                 reads=[b_xt[sl], b_rt[sl]], writes=[b_rt[sl]])
            for hf in range(2):
                P.op(DVE, lambda e, rows=rows, sl=sl, hf=hf: e.bn_stats(out=stt[0:rows, sl, hf, :], in_=rt[sl][0:rows, 512 * hf:512 * hf + 512]),
                     reads=[b_rt[sl]], writes=[b_stt[sl]] if hf == 0 else [], wadd=[b_stt[sl]] if hf == 1 else [])
            P.op(DVE, lambda e, rows=rows, sl=sl: e.bn_aggr(out=mvt[0:rows, sl, :], in_=stt[0:rows, sl].rearrange("p a b -> p (a b)")),
                 reads=[b_stt[sl]], writes=[b_stt[sl]])
            P.op(POOL, lambda e, rows=rows, sl=sl: e.tensor_scalar(out=rsd[0:rows, sl, 0:1], in0=mvt[0:rows, sl, 1:2], scalar1=float(LN_EPS), scalar2=0.0,
                                                                   op0=ALU.add, op1=ALU.add), reads=[b_stt[sl]], writes=[b_stt[sl]])
            P.op(POOL, lambda e, rows=rows, sl=sl: e.tensor_tensor(out=rsd[0:rows, sl, 0:1], in0=rsd[0:rows, sl, 0:1], in1=mhalf[0:rows, :], op=ALU.pow),
                 reads=[b_stt[sl], b_mh], writes=[b_stt[sl]])
            P.op(POOL, lambda e, rows=rows, sl=sl: e.tensor_tensor(out=rsd[0:rows, sl, 1:2], in0=mvt[0:rows, sl, 0:1], in1=rsd[0:rows, sl, 0:1], op=ALU.mult),
                 reads=[b_stt[sl]], writes=[b_stt[sl]])
            P.op(POOL, lambda e, rows=rows, sl=sl: e.tensor_scalar(out=rsd[0:rows, sl, 1:2], in0=rsd[0:rows, sl, 1:2], scalar1=-1.0, scalar2=0.0,
                                                                   op0=ALU.mult, op1=ALU.add), reads=[b_stt[sl]], writes=[b_stt[sl]])
            P.op(ACT, lambda e, rows=rows, sl=sl: e.activation(out=xt[sl][0:rows, :], in_=rt[sl][0:rows, :], func=AF.Identity,
                                                               scale=rsd[0:rows, sl, 0:1], bias=rsd[0:rows, sl, 1:2]),
                 reads=[b_rt[sl], b_stt[sl]], writes=[b_xt[sl]])
            P.op(DVE, lambda e, rows=rows, sl=sl: e.tensor_tensor(out=xt[sl][0:rows, :], in0=xt[sl][0:rows, :], in1=LnG[0:rows, :], op=ALU.mult),
                 reads=[b_xt[sl], b_LnG], writes=[b_xt[sl]])
            P.op(POOL, lambda e, rows=rows, sl=sl: e.tensor_tensor(out=xt[sl][0:rows, :], in0=xt[sl][0:rows, :], in1=LnB[0:rows, :], op=ALU.add),
                 reads=[b_xt[sl], b_LnB], writes=[b_xt[sl]])
            dst = yp[c0:c0 + 128, :] if tt_i < 16 else ys
            P.dma(SP, osem[sl], dst, xt[sl][0:rows, :], reads=[b_xt[sl]])
            if tt_i + NSL < 17:
                x_load(tt_i + NSL)
        final_deps = [Dep(o_.h, o_.cnt) for o_ in osem]

        P.dead = False
        if DEBUG:
            pass
        P.wait(SP, [Dep(dout_sem.h, dout_sem.cnt)] + final_deps + [Dep(d_.h, d_.cnt) for d_ in out_sems])
        P.emit()
        print("instruction counts:", P.ninst)
    return nc


def _bucket_np(dist):
    max_exact = 16
    df = np.maximum(dist, 1).astype(np.float32)
    large = max_exact + (np.log(df / np.float32(max_exact)) / np.float32(math.log(128 / max_exact)) * np.float32(16)).astype(np.int32)
    large = np.minimum(large, 31)
    return np.where(dist < max_exact, dist, large)


def _host_consts():
    R = np.zeros((32, 384), np.float32)
    for i in range(384):
        dist = 255 - i
        if 0 <= dist <= 128:
            R[int(_bucket_np(np.array([dist]))[0]), i] = 1.0
    j = np.arange(128)[:, None]
    q = np.arange(128)[None, :]
    mask = np.concatenate([(j >= q), (j <= q)], axis=1).astype(np.float32)
    r = np.arange(128)
    bmask = (r[:, None] // 32 == r[None, :] // 32).astype(np.float32)
    diag = np.zeros((16, 16, 4), np.float32)
    for s_ in range(16):
        diag[s_, s_, :] = 1.0
    return R, mask, bmask, diag.reshape(16, 64)


_NC_CACHE = {}


def kernel(x_prompt, x_sample, c_prompt, c_sample, state_ssm_re, state_ssm_im, cache_swa_k, cache_swa_v,
           w_ada, b_ada, w_in, ssm_lambda_re, ssm_lambda_im, ssm_log_delta, ssm_b_re, ssm_b_im,
           ssm_c_re, ssm_c_im, ssm_d, w_glu, b_glu, attn_sinks, rel_bias, w_branch_s, w_branch_a,
           w_out, ln_g, ln_b):
    f = lambda a: np.ascontiguousarray(np.asarray(a, dtype=np.float32))
    x_prompt = f(x_prompt); x_sample = f(x_sample); c_prompt = f(c_prompt); c_sample = f(c_sample)
    R, mask, bmask, diag = _host_consts()
    shared = {
        "w_ada": f(w_ada)[0], "b_ada": f(b_ada)[0], "w_in": f(w_in)[0],
        "lam_re": f(ssm_lambda_re)[0], "lam_im": f(ssm_lambda_im)[0], "log_delta": f(ssm_log_delta)[0],
        "b_re": f(ssm_b_re)[0].reshape(4096, 16), "b_im": f(ssm_b_im)[0].reshape(4096, 16),
        "c_re": f(ssm_c_re)[0].reshape(1024, 64), "c_im": f(ssm_c_im)[0].reshape(1024, 64),
        "ssm_d": f(ssm_d)[0], "w_glu": f(w_glu)[0], "b_glu": f(b_glu)[0], "sinks": f(attn_sinks)[0],
        "rel_bias": f(rel_bias), "w_bs": f(w_branch_s)[0], "w_ba": f(w_branch_a)[0], "w_out": f(w_out)[0],
        "ln_g": f(ln_g)[0], "ln_b": f(ln_b)[0],
        "rtab": R, "maskc": mask, "bmaskc": bmask, "diagc": diag,
    }
    sre = f(state_ssm_re)[0].reshape(128, 4096); sim = f(state_ssm_im)[0].reshape(128, 4096)
    ckk = f(cache_swa_k)[0].reshape(128, 128, 256); cvv = f(cache_swa_v)[0].reshape(128, 128, 256)
    in_maps = []
    for c in range(NCORES):
        b, qr = c // 4, c % 4
        t0 = NP * qr
        xh = x_prompt[b, t0 - 128:t0] if qr > 0 else np.zeros((128, D), np.float32)
        flags = np.zeros(32, np.float32)
        flags[0] = 1.0 if qr > 0 else 0.0
        xprev = np.zeros((3, NP, D), np.float32)
        for j in range(3):
            qq = qr - 1 - j
            if qq >= 0:
                flags[1 + j] = 1.0
                xprev[j] = x_prompt[b, NP * qq:NP * qq + NP]
        m = dict(shared)
        m.update({
            "xprev": xprev, "xp": np.ascontiguousarray(x_prompt[b, t0:t0 + NP]), "xh": np.ascontiguousarray(xh),
            "xs": np.ascontiguousarray(x_sample[NS * c:NS * c + NS, 0]),
            "cc": np.ascontiguousarray(np.concatenate([c_prompt[b:b + 1], c_sample[NS * c:NS * c + NS]], 0)),
            "st_re": np.ascontiguousarray(sre[NS * c:NS * c + NS]), "st_im": np.ascontiguousarray(sim[NS * c:NS * c + NS]),
            "ck": np.ascontiguousarray(ckk[NS * c:NS * c + NS]), "cv": np.ascontiguousarray(cvv[NS * c:NS * c + NS]),
            "flags": flags,
        })
        in_maps.append(m)
    nc = build()
    res = run_bass_kernel_spmd(nc, in_maps, core_ids=list(range(NCORES)))
    R_ = res.results
    kernel.last_results = R_
    y_prompt = np.stack([np.concatenate([R_[4 * b + q]["yp"] for q in range(4)], 0) for b in range(2)], 0)
    y_sample = np.concatenate([R_[c]["ys"] for c in range(NCORES)], 0).reshape(128, 1, D)
    p_hr = np.stack([R_[4 * b + 3]["pst_re"].reshape(64, 64) for b in range(2)], 0)[None]
    p_hi = np.stack([R_[4 * b + 3]["pst_im"].reshape(64, 64) for b in range(2)], 0)[None]
    p_k = np.stack([R_[4 * b + 3]["pck"].reshape(128, 4, 64) for b in range(2)], 0)[None]
    p_v = np.stack([R_[4 * b + 3]["pcv"].reshape(128, 4, 64) for b in range(2)], 0)[None]
    s_hr = np.concatenate([R_[c]["sst_re"] for c in range(NCORES)], 0).reshape(1, 128, 64, 64)
    s_hi = np.concatenate([R_[c]["sst_im"] for c in range(NCORES)], 0).reshape(1, 128, 64, 64)
    s_k = np.concatenate([R_[c]["sck"] for c in range(NCORES)], 0).reshape(1, 128, 128, 4, 64)
    s_v = np.concatenate([R_[c]["scv"] for c in range(NCORES)], 0).reshape(1, 128, 128, 4, 64)
    return (y_prompt.astype(np.float32), y_sample.astype(np.float32), p_hr.astype(np.float32), p_hi.astype(np.float32),
            p_k.astype(np.float32), p_v.astype(np.float32), s_hr.astype(np.float32), s_hi.astype(np.float32),
            s_k.astype(np.float32), s_v.astype(np.float32))
```

```python
import math
import os
from contextlib import ExitStack
import numpy as np
import ml_dtypes
import concourse.bass as bass
import concourse.mybir as mybir
from concourse.bass_utils import run_bass_kernel_spmd

F32 = mybir.dt.float32
BF16 = mybir.dt.bfloat16
U8 = mybir.dt.uint8
ALU = mybir.AluOpType
AF = mybir.ActivationFunctionType
AX = mybir.AxisListType

PE, ACT, DVE, POOL, SP = "tensor", "scalar", "vector", "gpsimd", "sync"
ENGS = [PE, ACT, DVE, POOL, SP]

NCORES = 8
D = 1024
NP = 2048
NS = 16
NT = NP + NS
TBS = [(0, 512), (512, 512), (1024, 512), (1536, 512), (2048, 16)]
DIN = 6656
OFF_U, OFF_ZS, OFF_Q, OFF_K, OFF_V, OFF_ZA, OFF_GS, OFF_GA = 0, 1024, 2048, 3072, 3328, 3584, 4608, 5632
ALPHA = 2.0 ** 0.25
LN_EPS = 1e-5
DEBUG = False


class Dep:
    __slots__ = ("sem", "val")

    def __init__(self, sem, val):
        self.sem = sem
        self.val = val


class Buf:
    __slots__ = ("w", "r", "name")

    def __init__(self, name=""):
        self.w = []
        self.r = []
        self.name = name


class _RProxy:
    def __init__(self, bufs):
        self.bufs = bufs

    def append(self, h):
        for b in self.bufs:
            b.r.append(h)

    def __len__(self):
        return 0


class BufGroup:
    def __init__(self, bufs):
        self.bufs = list(bufs)
        self.r = _RProxy(self.bufs)

    @property
    def w(self):
        return [h for b in self.bufs for h in b.w]


def handoff(new_bufs, old_bufs):
    deps = []
    for b in old_bufs:
        deps.extend(b.w)
        deps.extend(b.r)
    for nb in new_bufs:
        nb.r = list(nb.r) + deps


class DSem:
    def __init__(self, h):
        self.h = h
        self.cnt = 0


class Prog:
    def __init__(self, nc, stack):
        self.nc = nc
        self.q = {e: [] for e in ENGS}
        self.esem = {}
        self.cnt = {e: 0 for e in ENGS}
        self.allsems = []
        for e in [PE, ACT, DVE, POOL]:
            self.esem[e] = nc.alloc_semaphore("s_" + e)
            self.allsems.append(self.esem[e])
        self.seen = {}
        self.stack = stack
        self.nd = 0
        self.ninst = {e: 0 for e in ENGS}
        self.dead = False
        self.stop = int(os.environ.get("KSTOP", "99"))

    def phase(self, n):
        self.dead = n > self.stop

    def dsem(self, name=None):
        self.nd += 1
        h = self.nc.alloc_semaphore(f"d{self.nd}_{name or 'm'}")
        self.allsems.append(h)
        return DSem(h)

    def _waits(self, eng, deps):
        best = {}
        for d in deps:
            if d is None:
                continue
            k = id(d.sem)
            if k not in best or best[k].val < d.val:
                best[k] = d
        ws = []
        for d in best.values():
            k = (eng, id(d.sem))
            if self.seen.get(k, 0) >= d.val:
                continue
            self.seen[k] = d.val
            ws.append((d.sem, d.val))
        return ws

    @staticmethod
    def _compact(lst):
        best = {}
        for d in lst:
            k = id(d.sem)
            if k not in best or best[k].val < d.val:
                best[k] = d
        return list(best.values())

    @staticmethod
    def _bufdeps(reads, writes, wadd=()):
        deps = []
        for b in reads:
            deps.extend(b.w)
        for b in writes:
            deps.extend(b.w)
            deps.extend(b.r)
        for b in wadd:
            deps.extend(b.r)
        return deps

    @classmethod
    def _update(cls, h, reads, writes, wadd=()):
        for b in reads:
            b.r.append(h)
            if len(b.r) > 32:
                b.r = cls._compact(b.r)
        for b in writes:
            b.w = [h]
            b.r = []
        for b in wadd:
            b.w.append(h)
            if len(b.w) > 32:
                b.w = cls._compact(b.w)

    def op(self, eng, fn, reads=(), writes=(), deps=(), sig=True, wadd=()):
        if self.dead:
            return None
        alld = list(deps) + self._bufdeps(reads, writes, wadd)
        ws = self._waits(eng, alld)
        h = None
        if sig:
            self.cnt[eng] += 1
            h = Dep(self.esem[eng], self.cnt[eng])
        sem = self.esem[eng] if sig else None
        self.ninst[eng] += 1 + len(ws)

        def run(e, ws=ws, fn=fn, sem=sem):
            for (s, v) in ws:
                e.wait_ge(s, v)
            ins = fn(e)
            if sem is not None:
                ins.then_inc(sem, 1)
        self.q[eng].append(run)
        if h is not None:
            self._update(h, reads, writes, wadd)
        return h

    def attach(self, h, reads=(), writes=(), wadd=()):
        if self.dead or h is None:
            return
        self._update(h, reads, writes, wadd)

    def dma(self, eng, ds, out, in_, reads=(), writes=(), deps=(), wadd=(), **kw):
        if self.dead:
            return None
        alld = list(deps) + self._bufdeps(reads, writes, wadd)
        ws = self._waits(eng, alld)
        ds.cnt += 16
        h = Dep(ds.h, ds.cnt)
        self.ninst[eng] += 1 + len(ws)

        def run(e, ws=ws, out=out, in_=in_, kw=kw, sh=ds.h):
            for (s, v) in ws:
                e.wait_ge(s, v)
            e.dma_start(out=out, in_=in_, **kw).then_inc(sh, 16)
        self.q[eng].append(run)
        self._update(h, reads, writes, wadd)
        return h

    def raw(self, eng, fn, ds, inc, reads=(), writes=(), deps=()):
        if self.dead:
            return None
        alld = list(deps) + self._bufdeps(reads, writes)
        ws = self._waits(eng, alld)
        ds.cnt += inc
        h = Dep(ds.h, ds.cnt)

        def run(e, ws=ws, fn=fn, sh=ds.h, inc=inc):
            for (s, v) in ws:
                e.wait_ge(s, v)
            fn(e).then_inc(sh, inc)
        self.q[eng].append(run)
        self._update(h, reads, writes)
        return h

    def wait(self, eng, deps):
        ws = self._waits(eng, deps)

        def run(e, ws=ws):
            for (s, v) in ws:
                e.wait_ge(s, v)
        self.q[eng].append(run)

    def emit(self):
        nc = self.nc
        with nc.Block() as block:
            @block.tensor
            def _(e):
                for f in self.q[PE]:
                    f(e)

            @block.scalar
            def _(e):
                for f in self.q[ACT]:
                    f(e)

            @block.vector
            def _(e):
                for f in self.q[DVE]:
                    f(e)

            @block.gpsimd
            def _(e):
                for f in self.q[POOL]:
                    f(e)

            @block.sync
            def _(e):
                for f in self.q[SP]:
                    f(e)


def _dsize(dt):
    return {F32: 4, BF16: 2, U8: 1}[dt]


class Arena:
    def __init__(self, nc, stack, nbytes):
        self.t = stack.enter_context(nc.sbuf_tensor("arena", [128, nbytes], U8))
        self.nbytes = nbytes

    def carve(self, off, shape, dt):
        n = int(np.prod(shape)) * _dsize(dt)
        assert off % 4 == 0 and off + n <= self.nbytes, (off, n, self.nbytes)
        v = self.t[:, off:off + n]
        if dt != U8:
            v = v.bitcast(dt)
        if len(shape) > 1:
            names = [f"a{i}" for i in range(len(shape))]
            pat = "p (" + " ".join(names) + ") -> p " + " ".join(names)
            v = v.rearrange(pat, **{names[i]: shape[i] for i in range(len(shape))})
        return v


O_HT = 0
O_HTH = 33024
O_CONST = 35072
O_W = 45312
O_A = 61696
O_B = 94720
O_C = 127744
ARENA = 212000
C_SIZE = ARENA - O_C


def build():
    nc = bass.Bass("TRN2", target_bir_lowering=False)

    def din(name, shape, dt=F32):
        return nc.dram_tensor(name, list(shape), dt, kind="ExternalInput").ap()

    def dout(name, shape, dt=F32):
        return nc.dram_tensor(name, list(shape), dt, kind="ExternalOutput").ap()

    xprev = din("xprev", [3, NP, D]); xp = din("xp", [NP, D]); xh = din("xh", [128, D]); xs = din("xs", [NS, D]); ccin = din("cc", [17, D])
    st_re = din("st_re", [NS, 4096]); st_im = din("st_im", [NS, 4096])
    ck = din("ck", [NS, 128, 256]); cv = din("cv", [NS, 128, 256])
    w_ada = din("w_ada", [D, 3072]); b_ada = din("b_ada", [3072]); w_in = din("w_in", [D, DIN])
    lam_re = din("lam_re", [64, 64]); lam_im = din("lam_im", [64, 64]); log_delta = din("log_delta", [64])
    b_re = din("b_re", [4096, 16]); b_im = din("b_im", [4096, 16])
    c_re = din("c_re", [1024, 64]); c_im = din("c_im", [1024, 64])
    ssm_d = din("ssm_d", [1024]); w_glu = din("w_glu", [D, D]); b_glu = din("b_glu", [D])
    sinks = din("sinks", [16]); rel_bias = din("rel_bias", [32, 16])
    w_bs = din("w_bs", [D, D]); w_ba = din("w_ba", [D, D]); w_out = din("w_out", [D, D])
    ln_g = din("ln_g", [D]); ln_b = din("ln_b", [D])
    rtab = din("rtab", [32, 384]); maskc = din("maskc", [128, 256]); bmaskc = din("bmaskc", [128, 128])
    diagc = din("diagc", [16, 64]); flagsc = din("flags", [32])

    yp = dout("yp", [NP, D]); ys = dout("ys", [NS, D])
    pst_re = dout("pst_re", [32, 128]); pst_im = dout("pst_im", [32, 128])
    pck = dout("pck", [128, 256]); pcv = dout("pcv", [128, 256])
    sst_re = dout("sst_re", [NS, 4096]); sst_im = dout("sst_im", [NS, 4096])
    sck = dout("sck", [NS, 128, 256]); scv = dout("scv", [NS, 128, 256])
    dbg = {}
    if DEBUG:
        dbg["hT"] = dout("dbg_hT", [128, 8, NT], BF16)
        dbg["uT"] = dout("dbg_uT", [128, 8, NT], BF16)
        dbg["pw"] = dout("dbg_pw", [128, 9 * 2 * 32])
        dbg["hend"] = dout("dbg_hend", [128, 2 * 32 * 16])
        dbg["bb"] = dout("dbg_bb", [128, 2 * 32 * 16]); dbg["ccm"] = dout("dbg_ccm", [128, 2 * 32 * 16])
        dbg["R8"] = dout("dbg_R8", [128, 16 * 2 * 32]); dbg["R128"] = dout("dbg_R128", [128, 16 * 2 * 32])
        dbg["A2k"] = dout("dbg_A2k", [128, 3 * 2 * 32])
        dbg["WinL"] = dout("dbg_WinL", [128, 8 * 8 * 2 * 128], BF16)
        dbg["X0"] = dout("dbg_X0", [128, 512])
        dbg["KL"] = dout("dbg_KL", [128, 2 * 8 * 128], BF16); dbg["Ca"] = dout("dbg_Ca", [128, 2 * 4 * 9 * 2 * 32], BF16)
        dbg["Hb"] = dout("dbg_Hb", [128, 2 * 2048], BF16); dbg["Xs"] = dout("dbg_Xs", [128, 2 * 2048])
        dbg["carry"] = dout("dbg_carry", [128, 17 * 2 * 32])
        dbg["yT"] = dout("dbg_yT", [128, 8, NT], BF16)
        dbg["gbs"] = dout("dbg_gbs", [128, 8, NT], BF16)
        dbg["oT"] = dout("dbg_oT", [128, 8, NT], BF16)
        dbg["mT"] = dout("dbg_mT", [128, 8, NT], BF16)
        dbg["modT"] = dout("dbg_modT", [128, 24 * 17])

    ib = nc.dram_tensor("cc_ib", [128, 64], F32, kind="Internal")
    ob = nc.dram_tensor("cc_ob", [NCORES * 128, 64], F32, kind="Internal")

    st = ExitStack()
    with st:
        P = Prog(nc, st)
        AR = Arena(nc, st, ARENA)
        cv_ = AR.carve
        ps = [st.enter_context(nc.psum_tensor(f"ps{i}", [128, 512], F32)) for i in range(8)]
        psb = [Buf(f"ps{i}") for i in range(8)]
        dout_sem = P.dsem("dout")
        out_sems = []

        def osem_new(name):
            d_ = P.dsem(name)
            out_sems.append(d_)
            return d_
        misc_sem = P.dsem("misc")

        def misc_load(eng, out, in_, buf, wadd=False, **kw):
            if P.dead:
                return None
            if wadd:
                return P.dma(eng, P.dsem(), out, in_, wadd=[buf], **kw)
            return P.dma(eng, P.dsem(), out, in_, writes=[buf], **kw)

        hT = cv_(O_HT, [8, NT], BF16)
        hTh = cv_(O_HTH, [8, 128], BF16)
        hT_b = [Buf(f"hT{i}") for i in range(len(TBS))]
        hTh_b = Buf("hTh")
        o = O_CONST
        ident = cv_(o, [128], F32); o += 512
        modT = cv_(o, [24, 17], F32); o += 1664
        op1p = cv_(o, [8, 17], F32); o += 576
        flags = cv_(o, [32], F32); o += 128
        Dm = cv_(o, [8], F32); o += 32
        bglu = cv_(o, [8], F32); o += 32
        ES = cv_(o, [4, 4, 128], BF16); o += 4096
        onesd = cv_(o, [128], BF16); o += 256
        EBself = cv_(o, [16], F32); o += 64
        bmask = cv_(o, [128], F32); o += 512
        ones1 = cv_(o, [128], F32); o += 512
        assert o <= O_CONST + 10240
        b_ident = Buf(); b_modT = Buf(); b_flags = Buf(); b_Dm = Buf(); b_bglu = Buf(); b_ES = Buf()
        b_ones = Buf(); b_EBself = Buf(); b_bmask = Buf()
        wslot = [cv_(O_W + 8192 * i, [8, 512], BF16) for i in range(2)]
        wslot_b = [Buf("w0"), Buf("w1")]
        wsem = [P.dsem("w0"), P.dsem("w1")]
        wctr = [0]
        RA = cv_(O_A, [8, NT], BF16)
        RB = cv_(O_B, [8, NT], BF16)

        rr = {"i": 0}

        def bank():
            i = rr["i"] % 8
            rr["i"] += 1
            return i

        def load_w(src2d, ncols):
            s = wctr[0] % 2
            wctr[0] += 1
            P.dma(POOL, wsem[s], wslot[s][:, :, 0:ncols], src2d.rearrange("(k p) f -> p k f", p=128),
                  writes=[wslot_b[s]])
            return s

        def evac_copy(i, out_ap, in_ap, reads, writes, wadd=()):
            if i % 2 == 0:
                return P.op(ACT, lambda e: e.activation(out=out_ap, in_=in_ap, func=AF.Copy), reads=reads, writes=writes, wadd=wadd)
            return P.op(DVE, lambda e: e.tensor_copy(out=out_ap, in_=in_ap), reads=reads, writes=writes, wadd=wadd)

        def proj_fm(src2d, ncols, rhs_of, rhs_bufs, evac, tbs=TBS):
            s = load_w(src2d, ncols)
            for oc in range(ncols // 128):
                for tbi, (t0, n) in enumerate(tbs):
                    b = bank()
                    for k in range(8):
                        last = (k == 7)
                        P.op(PE, lambda e, b=b, k=k, oc=oc, tbi=tbi, t0=t0, n=n, s=s: e.matmul(
                            ps[b][:, 0:n], lhsT=wslot[s][:, k, oc * 128:(oc + 1) * 128], rhs=rhs_of(k, t0, n),
                            start=(k == 0), stop=(k == 7)),
                            reads=[wslot_b[s], rhs_bufs[tbi]] if k == 0 else [], writes=[psb[b]] if k == 0 else [],
                            sig=last)
                        if last and not P.dead:
                            h = Dep(P.esem[PE], P.cnt[PE])
                            P.attach(h, reads=[wslot_b[s], rhs_bufs[tbi]], writes=[psb[b]])
                    evac(oc, tbi, ps[b][:, 0:n], psb[b])

        def cmul(eng, dst_r, dst_i, xr, xi, yr, yi, t1, t2, bufs_r, bufs_w, tb):
            P.op(eng, lambda e: e.tensor_tensor(out=t1, in0=xr, in1=yr, op=ALU.mult), reads=bufs_r, writes=[tb])
            P.op(eng, lambda e: e.tensor_tensor(out=t2, in0=xi, in1=yi, op=ALU.mult), reads=bufs_r, writes=[tb])
            P.op(eng, lambda e: e.tensor_tensor(out=dst_r, in0=t1, in1=t2, op=ALU.subtract), reads=[tb], writes=bufs_w)
            P.op(eng, lambda e: e.tensor_tensor(out=t1, in0=xr, in1=yi, op=ALU.mult), reads=bufs_r + bufs_w, writes=[tb])
            P.op(eng, lambda e: e.tensor_tensor(out=t2, in0=xi, in1=yr, op=ALU.mult), reads=bufs_r + bufs_w, writes=[tb])
            P.op(eng, lambda e: e.tensor_tensor(out=dst_i, in0=t1, in1=t2, op=ALU.add), reads=[tb], writes=bufs_w)

        P.phase(0)
        P.op(POOL, lambda e: e.memset(ident, 0.0), writes=[b_ident])
        P.op(POOL, lambda e: e.affine_select(out=ident, in_=ident, pattern=[[-1, 128]], compare_op=ALU.not_equal,
                                             fill=1.0, base=0, channel_multiplier=1), writes=[b_ident])
        misc_load(SP, flags, flagsc.rearrange("(o n) -> o n", o=1).to_broadcast([128, 32]), b_flags)
        misc_load(SP, bmask, bmaskc, b_bmask)
        P.op(POOL, lambda e: e.memset(ones1[0:1, :], 1.0), writes=[b_ones])
        P.op(POOL, lambda e: e.memset(onesd[0:1, 0:64], 0.0), wadd=[b_ones])
        P.op(POOL, lambda e: e.memset(onesd[0:1, 64:128], 1.0), wadd=[b_ones])

        P.phase(1)
        c_t = cv_(O_C + 62208, [1024], F32); c_sg = cv_(O_C + 66304, [1024], F32)
        ccT = cv_(O_C + 70400, [8, 17], BF16); badain = cv_(O_C + 70912, [128], F32); badaT = cv_(O_C + 71424, [24], F32)
        b_ct = Buf(); b_csg = Buf(); b_ccT = Buf(); b_bin = Buf(); b_baT = Buf()
        misc_load(SP, c_t[0:17, :], ccin, b_ct)
        misc_load(SP, badain[0:24, :], b_ada.rearrange("(c p) -> c p", p=128), b_bin)
        P.op(ACT, lambda e: e.activation(out=c_sg[0:17, :], in_=c_t[0:17, :], func=AF.Sigmoid), reads=[b_ct], writes=[b_csg])
        P.op(DVE, lambda e: e.tensor_tensor(out=c_sg[0:17, :], in0=c_sg[0:17, :], in1=c_t[0:17, :], op=ALU.mult),
             reads=[b_ct], writes=[b_csg])
        bk = bank()
        for k in range(8):
            P.op(PE, lambda e, k=k: e.transpose(out=ps[bk][:, 17 * k:17 * k + 17], in_=c_sg[0:17, 128 * k:128 * k + 128],
                                                identity=ident[0:17, 0:17]),
                 reads=[b_csg, b_ident], writes=[psb[bk]] if k == 0 else [], wadd=[psb[bk]] if k > 0 else [])
        P.op(DVE, lambda e: e.tensor_copy(out=ccT.rearrange("p k s -> p (k s)"), in_=ps[bk][:, 0:136]), reads=[psb[bk]], writes=[b_ccT])
        bk2 = bank()
        P.op(PE, lambda e: e.transpose(out=ps[bk2][:, 0:24], in_=badain[0:24, :], identity=ident[0:24, 0:24]),
             reads=[b_bin, b_ident], writes=[psb[bk2]])
        P.op(DVE, lambda e: e.tensor_copy(out=badaT, in_=ps[bk2][:, 0:24]), reads=[psb[bk2]], writes=[b_baT])
        bkm = bank()
        hlast = None
        for blk in range(6):
            s = load_w(w_ada[:, 512 * blk:512 * blk + 512], 512)
            for oc in range(4):
                f = 4 * blk + oc
                for k in range(8):
                    first = (blk == 0 and oc == 0 and k == 0)
                    lastk = (k == 7)
                    hlast = P.op(PE, lambda e, f=f, k=k, oc=oc, s=s: e.matmul(
                        ps[bkm][:, 17 * f:17 * f + 17], lhsT=wslot[s][:, k, oc * 128:(oc + 1) * 128], rhs=ccT[:, k, :],
                        start=(k == 0), stop=(k == 7)),
                        reads=[wslot_b[s], b_ccT] if k == 0 else [], writes=[psb[bkm]] if first else [], sig=lastk and oc == 3)
            P.attach(hlast, reads=[wslot_b[s]], wadd=[psb[bkm]])
        P.op(DVE, lambda e: e.tensor_tensor(out=modT, in0=ps[bkm][:, 0:408].rearrange("p (f s) -> p f s", f=24),
                                            in1=badaT.unsqueeze(2).to_broadcast([128, 24, 17]), op=ALU.add),
             reads=[psb[bkm], b_baT], writes=[b_modT])
        P.op(DVE, lambda e: e.tensor_scalar(out=op1p, in0=modT[:, 8:16, :], scalar1=1.0, scalar2=None, op0=ALU.add),
             reads=[b_modT], wadd=[b_modT])

        xst = [cv_(O_C + 71552 + 4096 * i, [1024], F32) for i in range(2)]
        xst_b = [Buf(), Buf()]
        xsem = [P.dsem("x0"), P.dsem("x1")]
        hTs_b = hT_b[4]
        tmpS = cv_(O_C + 79744, [8, 16], F32)
        b_tmpS = Buf()

        def phase_a(xsrc, full):
            tiles = [("p", i) for i in range(16)] + ([("h", 0), ("s", 0)] if full else [])
            for ti, (kind, i) in enumerate(tiles):
                sl = ti % 2
                if kind == "p":
                    src, rows, dst, dbuf = xsrc[128 * i:128 * i + 128, :], 128, (lambda k, i=i: hT[:, k, 128 * i:128 * i + 128]), hT_b[i // 4]
                elif kind == "h":
                    src, rows, dst, dbuf = xh, 128, (lambda k: hTh[:, k, :]), hTh_b
                else:
                    src, rows, dst, dbuf = xs, NS, None, hTs_b
                P.dma(SP, xsem[sl], xst[sl][0:rows, :], src, writes=[xst_b[sl]])
                b0, b1 = bank(), bank()
                for k in range(8):
                    bb_ = b0 if k < 4 else b1
                    j = k % 4
                    P.op(PE, lambda e, k=k, bb_=bb_, j=j, sl=sl, rows=rows: e.transpose(
                        out=ps[bb_][:, rows * j:rows * j + rows], in_=xst[sl][0:rows, 128 * k:128 * k + 128],
                        identity=ident[0:rows, 0:rows]),
                        reads=[xst_b[sl], b_ident], writes=[psb[bb_]] if j == 0 else [], wadd=[psb[bb_]] if j > 0 else [])
                if kind != "s":
                    for k in range(8):
                        bb_ = b0 if k < 4 else b1
                        j = k % 4
                        src_ps = ps[bb_][:, 128 * j:128 * j + 128]
                        if k < 4:
                            P.op(DVE, lambda e, k=k, src_ps=src_ps, dst=dst: e.tensor_scalar(
                                out=dst(k), in0=src_ps, scalar1=op1p[:, k, 0:1], scalar2=modT[:, k, 0:1], op0=ALU.mult, op1=ALU.add),
                                reads=[psb[bb_], b_modT], wadd=[dbuf])
                        else:
                            P.op(ACT, lambda e, k=k, src_ps=src_ps, dst=dst: e.activation(
                                out=dst(k), in_=src_ps, func=AF.Identity, scale=op1p[:, k, 0:1], bias=modT[:, k, 0:1]),
                                reads=[psb[bb_], b_modT], wadd=[dbuf])
                else:
                    for half, bb_ in enumerate([b0, b1]):
                        P.op(DVE, lambda e, half=half, bb_=bb_: e.tensor_tensor(
                            out=tmpS[:, 4 * half:4 * half + 4, :], in0=ps[bb_][:, 0:64].rearrange("p (k s) -> p k s", k=4),
                            in1=op1p[:, 4 * half:4 * half + 4, 1:17], op=ALU.mult), reads=[psb[bb_], b_modT], wadd=[b_tmpS])
                    P.op(DVE, lambda e: e.tensor_tensor(out=hT[:, :, NP:NT], in0=tmpS, in1=modT[:, 0:8, 1:17], op=ALU.add),
                         reads=[b_tmpS, b_modT], wadd=[dbuf])

        uT = RA
        uT_b = [Buf(f"uT{g}") for g in range(8)]
        rhs_h = lambda k, t0, n: hT[:, k, t0:t0 + n]
        cnt = {"i": 0}

        def phase_b(full):
            for blk in range(2):
                def ev(oc, tbi, pap, pb, blk=blk):
                    t0, n = TBS[tbi]
                    g = 4 * blk + oc
                    evac_copy(0, uT[:, g, t0:t0 + n], pap, [pb], [], wadd=[uT_b[g]])
                proj_fm(w_in[:, OFF_U + 512 * blk:OFF_U + 512 * blk + 512], 512, rhs_h, hT_b, ev, tbs=TBS if full else TBS[:4])

        P.phase(3)
        phase_a(xprev[0], False)
        phase_b(False)
        P.phase(2)
        oc_ = O_C
        pw = cv_(oc_ + 0, [9, 2, 32], F32); bb = cv_(oc_ + 2304, [2, 32, 16], F32); ccm = cv_(oc_ + 6400, [2, 32, 16], F32)
        R8 = cv_(oc_ + 10496, [16, 2, 32], F32); R128 = cv_(oc_ + 14592, [16, 2, 32], F32); A2k = cv_(oc_ + 18688, [3, 2, 32], F32)
        Hend = cv_(oc_ + 19456, [2, 32, 16], F32); carry = cv_(oc_ + 23552, [17, 2, 32], F32)
        Sb = cv_(oc_ + 27904, [8, 2, 128], BF16); coef = cv_(oc_ + 32000, [12, 32], F32)
        misc = cv_(oc_ + 33536, [32, 32], F32)
        t1 = cv_(oc_ + 37632, [1024], F32); t2 = cv_(oc_ + 41728, [1024], F32)
        O_S = oc_ + 45824
        Sslot = [cv_(O_S + 8192 * i, [8, 2, 128], F32) for i in range(2)]
        O_CA = oc_ + 62208; O_KL = oc_ + 71424; O_HB = oc_ + 75520
        b_pw = Buf("pw"); b_bb = Buf("bb"); b_ccm = Buf("ccm"); b_R8 = Buf(); b_R128 = Buf(); b_A2k = Buf(); b_coef = Buf()
        b_misc = Buf("misc"); b_t = Buf("t12"); b_Sslot = [Buf("S0"), Buf("S1")]
        craw = [cv_(O_S + 2048 * i, [8, 64], F32) for i in range(2)]
        lamraw = cv_(O_S + 4096, [128], F32); ldraw = cv_(O_S + 4608, [64], F32)
        draw = cv_(O_S + 4864, [128], F32); bgraw = cv_(O_S + 5376, [128], F32)
        braw = [cv_(O_S + 8192 + 2048 * i, [32, 16], F32) for i in range(2)]
        b_craw = Buf(); b_lam = Buf(); b_ld = Buf(); b_draw = Buf(); b_braw = Buf()
        misc_load(SP, craw[0], c_re.rearrange("(t r) p -> r t p", r=128), b_craw, wadd=True)
        misc_load(SP, craw[1], c_im.rearrange("(t r) p -> r t p", r=128), b_craw, wadd=True)
        misc_load(SP, lamraw[0:64, 0:64], lam_re, b_lam, wadd=True)
        misc_load(SP, lamraw[0:64, 64:128], lam_im, b_lam, wadd=True)
        misc_load(SP, ldraw, log_delta.rearrange("(o n) -> o n", o=1).to_broadcast([128, 64]), b_ld)
        misc_load(SP, draw[0:8, :], ssm_d.rearrange("(c p) -> c p", p=128), b_draw, wadd=True)
        misc_load(SP, bgraw[0:8, :], b_glu.rearrange("(c p) -> c p", p=128), b_draw, wadd=True)
        for i, src in enumerate([b_re, b_im]):
            for q4 in range(4):
                misc_load(SP, braw[i][:, 8 * q4:8 * q4 + 8, :],
                          src[1024 * q4:1024 * q4 + 1024, :].rearrange("(gp q) c -> q gp c", q=128), b_braw, wadd=True)
        M_ = lambda i: misc[:, i, :]
        LR, LI, DT, TH, FR, FC, MAG, SN, CS, NR, DEN, CR, CI, G1, KF, TMPA = [M_(i) for i in range(16)]
        KI = misc[:, 16, :].bitcast(mybir.dt.int32)
        bkl = bank()
        P.op(PE, lambda e: e.transpose(out=ps[bkl][:, 0:64], in_=lamraw[0:64, :], identity=ident[0:64, 0:64]),
             reads=[b_lam, b_ident], writes=[psb[bkl]])
        P.op(DVE, lambda e: e.tensor_copy(out=LR[0:64, :], in_=ps[bkl][0:64, 0:64:2]), reads=[psb[bkl]], wadd=[b_misc])
        P.op(DVE, lambda e: e.tensor_copy(out=LR[64:128, :], in_=ps[bkl][0:64, 1:64:2]), reads=[psb[bkl]], wadd=[b_misc])
        P.op(DVE, lambda e: e.tensor_copy(out=LI[0:64, :], in_=ps[bkl][64:128, 0:64:2]), reads=[psb[bkl]], wadd=[b_misc])
        P.op(DVE, lambda e: e.tensor_copy(out=LI[64:128, :], in_=ps[bkl][64:128, 1:64:2]), reads=[psb[bkl]], wadd=[b_misc])
        bkd = bank()
        P.op(PE, lambda e: e.transpose(out=ps[bkd][:, 0:8], in_=draw[0:8, :], identity=ident[0:8, 0:8]),
             reads=[b_draw, b_ident], writes=[psb[bkd]])
        P.op(PE, lambda e: e.transpose(out=ps[bkd][:, 8:16], in_=bgraw[0:8, :], identity=ident[0:8, 0:8]),
             reads=[b_draw, b_ident], wadd=[psb[bkd]])
        P.op(DVE, lambda e: e.tensor_copy(out=Dm, in_=ps[bkd][:, 0:8]), reads=[psb[bkd]], writes=[b_Dm])
        P.op(DVE, lambda e: e.tensor_copy(out=bglu, in_=ps[bkd][:, 8:16]), reads=[psb[bkd]], writes=[b_bglu])
        for ri in range(2):
            for hb in range(2):
                bkc = bank()
                for tt in range(4):
                    t_ = 4 * hb + tt
                    P.op(PE, lambda e, ri=ri, t_=t_, tt=tt, bkc=bkc: e.transpose(
                        out=ps[bkc][0:64, 128 * tt:128 * tt + 128], in_=craw[ri][:, t_, :], identity=ident),
                        reads=[b_craw, b_ident], writes=[psb[bkc]] if tt == 0 else [], wadd=[psb[bkc]] if tt > 0 else [])
                for g2 in range(2):
                    src = ps[bkc][0:64, :].rearrange("p (tg g2 c) -> p tg g2 c", g2=2, c=16)[:, :, g2, :]
                    P.op(DVE, lambda e, ri=ri, hb=hb, g2=g2, src=src: e.tensor_copy(
                        out=ccm[64 * g2:64 * g2 + 64, ri, 16 * hb:16 * hb + 16, :], in_=src),
                        reads=[psb[bkc]], wadd=[b_ccm])
        P.op(ACT, lambda e: e.activation(out=DT[0:64, :], in_=ldraw[0:64, 0:64:2], func=AF.Exp), reads=[b_ld], wadd=[b_misc])
        P.op(ACT, lambda e: e.activation(out=DT[64:128, :], in_=ldraw[64:128, 1:64:2], func=AF.Exp), reads=[b_ld], wadd=[b_misc])
        G = DVE
        tt_ = lambda out, a, b_, op, **kw: P.op(G, lambda e: e.tensor_tensor(out=out, in0=a, in1=b_, op=op), reads=[b_misc], wadd=[b_misc], **kw)
        ts_ = lambda out, a, s1, s2, o0, o1: P.op(G, lambda e: e.tensor_scalar(out=out, in0=a, scalar1=s1, scalar2=s2, op0=o0, op1=o1), reads=[b_misc], wadd=[b_misc])
        tt_(TH, LI, DT, ALU.mult)
        ts_(FR, TH, 1.0 / (2 * math.pi), 0.0, ALU.mult, ALU.add)
        P.op(DVE, lambda e: e.tensor_copy(out=KI, in_=FR), reads=[b_misc], wadd=[b_misc])
        P.op(DVE, lambda e: e.tensor_copy(out=KF, in_=KI), reads=[b_misc], wadd=[b_misc])
        tt_(FR, FR, KF, ALU.subtract)
        ts_(FC, FR, 1.0, 0.25, ALU.mult, ALU.add)
        P.op(DVE, lambda e: e.tensor_single_scalar(out=G1, in_=FC, scalar=0.5, op=ALU.is_gt), reads=[b_misc], wadd=[b_misc])
        tt_(FC, FC, G1, ALU.subtract)
        TWO_PI = 6.283185
        P.op(ACT, lambda e: e.activation(out=SN, in_=FR, func=AF.Sin, scale=TWO_PI), reads=[b_misc], wadd=[b_misc])
        P.op(ACT, lambda e: e.activation(out=CS, in_=FC, func=AF.Sin, scale=TWO_PI), reads=[b_misc], wadd=[b_misc])
        tt_(TMPA, LR, DT, ALU.mult)
        P.op(ACT, lambda e: e.activation(out=MAG, in_=TMPA, func=AF.Exp), reads=[b_misc], wadd=[b_misc])
        P.op(G, lambda e: e.memset(pw[:, 0, 0, :], 1.0), wadd=[b_pw])
        P.op(G, lambda e: e.memset(pw[:, 0, 1, :], 0.0), wadd=[b_pw])
        P.op(G, lambda e: e.tensor_tensor(out=pw[:, 1, 0, :], in0=MAG, in1=CS, op=ALU.mult), reads=[b_misc], wadd=[b_pw])
        P.op(G, lambda e: e.tensor_tensor(out=pw[:, 1, 1, :], in0=MAG, in1=SN, op=ALU.mult), reads=[b_misc], wadd=[b_pw])

        def cm(dst, x, y, n, rb, wb):
            T1 = t1[:, 0:n * 32].rearrange("p (n g) -> p n g", n=n)
            T2 = t2[:, 0:n * 32].rearrange("p (n g) -> p n g", n=n)
            cmul(G, dst[:, :, 0, :], dst[:, :, 1, :], x[:, :, 0, :], x[:, :, 1, :], y[:, :, 0, :], y[:, :, 1, :], T1, T2, rb, wb, b_t)

        def bc(ap1, n):
            return ap1.to_broadcast([128, n, 2, 32])

        cm(pw[:, 2:3], pw[:, 1:2], pw[:, 1:2], 1, [b_pw], [b_pw])
        cm(pw[:, 3:5], pw[:, 1:3], bc(pw[:, 2:3], 2), 2, [b_pw], [b_pw])
        cm(pw[:, 5:9], pw[:, 1:5], bc(pw[:, 4:5], 4), 4, [b_pw], [b_pw])
        tt_(NR, pw[:, 1, 0, :], pw[:, 0, 0, :], ALU.subtract, deps=b_pw.w)
        tt_(DEN, LR, LR, ALU.mult)
        tt_(TMPA, LI, LI, ALU.mult)
        tt_(DEN, DEN, TMPA, ALU.add)
        P.op(DVE, lambda e: e.reciprocal(out=DEN, in_=DEN), reads=[b_misc], wadd=[b_misc])
        tt_(CR, NR, LR, ALU.mult)
        tt_(TMPA, pw[:, 1, 1, :], LI, ALU.mult)
        tt_(CR, CR, TMPA, ALU.add)
        tt_(CR, CR, DEN, ALU.mult)
        tt_(CI, pw[:, 1, 1, :], LR, ALU.mult)
        tt_(TMPA, NR, LI, ALU.mult)
        tt_(CI, CI, TMPA, ALU.subtract)
        tt_(CI, CI, DEN, ALU.mult)
        CRb = CR.unsqueeze(2).to_broadcast([128, 32, 16]); CIb = CI.unsqueeze(2).to_broadcast([128, 32, 16])
        T1b = t1[:, 0:512].rearrange("p (g c) -> p g c", g=32); T2b = t2[:, 0:512].rearrange("p (g c) -> p g c", g=32)
        P.op(G, lambda e: e.tensor_tensor(out=T1b, in0=braw[0], in1=CRb, op=ALU.mult), reads=[b_braw, b_misc, b_pw], writes=[b_t])
        P.op(G, lambda e: e.tensor_tensor(out=T2b, in0=braw[1], in1=CIb, op=ALU.mult), reads=[b_braw, b_misc], wadd=[b_t])
        P.op(G, lambda e: e.tensor_tensor(out=bb[:, 0], in0=T1b, in1=T2b, op=ALU.subtract), reads=[b_t], wadd=[b_bb])
        P.op(G, lambda e: e.tensor_tensor(out=T1b, in0=braw[1], in1=CRb, op=ALU.mult), reads=[b_braw, b_misc, b_bb], writes=[b_t])
        P.op(G, lambda e: e.tensor_tensor(out=T2b, in0=braw[0], in1=CIb, op=ALU.mult), reads=[b_braw, b_misc], wadd=[b_t])
        P.op(G, lambda e: e.tensor_tensor(out=bb[:, 1], in0=T1b, in1=T2b, op=ALU.add), reads=[b_t], wadd=[b_bb])

        def rev_table(R, A0, bufR, out_last):
            AW = misc[:, 20:22, :].rearrange("p (o r) g -> p o r g", o=1)
            AW2 = misc[:, 22:24, :].rearrange("p (o r) g -> p o r g", o=1)
            P.op(G, lambda e: e.memset(R[:, 15, 0, :], 1.0), wadd=[bufR])
            P.op(G, lambda e: e.memset(R[:, 15, 1, :], 0.0), wadd=[bufR])
            P.op(G, lambda e: e.tensor_copy(out=AW, in_=A0), reads=[b_pw, b_coef, b_misc], wadd=[b_misc])
            w = 1
            cur, nxt = AW, AW2
            while w <= 8:
                cm(R[:, 16 - 2 * w:16 - w], R[:, 16 - w:16], bc(cur, w), w, [bufR, b_misc], [bufR])
                cm(nxt, cur, cur, 1, [b_misc], [b_misc])
                cur, nxt = nxt, cur
                w *= 2
            P.op(G, lambda e: e.tensor_copy(out=out_last, in_=cur), reads=[b_misc], wadd=[b_coef])

        A128 = coef[:, 0:2, :].rearrange("p (o r) g -> p o r g", o=1)
        A2048 = coef[:, 2:4, :].rearrange("p (o r) g -> p o r g", o=1)
        rev_table(R8, pw[:, 8:9], b_R8, A128)
        rev_table(R128, A128, b_R128, A2048)
        P.op(G, lambda e: e.memset(A2k[:, 0, 0, :], 1.0), wadd=[b_A2k])
        P.op(G, lambda e: e.memset(A2k[:, 0, 1, :], 0.0), wadd=[b_A2k])
        P.op(G, lambda e: e.tensor_copy(out=A2k[:, 1:2], in_=A2048), reads=[b_coef], wadd=[b_A2k])
        cm(A2k[:, 2:3], A2048, A2048, 1, [b_coef], [b_A2k])
        P.op(G, lambda e: e.tensor_scalar(out=coef[:, 4, :], in0=pw[:, 8, 1, :], scalar1=-1.0, scalar2=0.0, op0=ALU.mult, op1=ALU.add),
             reads=[b_pw], wadd=[b_coef])
        P.op(G, lambda e: e.tensor_scalar(out=coef[:, 5, :], in0=coef[:, 1, :], scalar1=-1.0, scalar2=0.0, op0=ALU.mult, op1=ALU.add),
             reads=[b_coef], wadd=[b_coef])

        P.phase(5)
        WinL = cv_(O_B, [8, 8, 2, 128], BF16)
        b_WinL = [Buf(f"WinL{g}") for g in range(8)]
        b_Sb = Buf("Sb")
        handoff(b_Sslot, [b_craw, b_lam, b_ld, b_draw, b_braw])
        b_SInit = [Buf("SInit0"), Buf("SInit1")]
        for i in range(2):
            P.op(POOL, lambda e, i=i: e.memset(Sslot[i].rearrange("p k r c -> p (k r c)"), 0.0), writes=[b_Sslot[i], b_SInit[i]])
        T1e = [t1[:, 0:512].rearrange("p (k m c) -> p k m c", k=8, m=4), t1[:, 512:1024].rearrange("p (k m c) -> p k m c", k=8, m=4)]
        T2e = [t2[:, 0:512].rearrange("p (k m c) -> p k m c", k=8, m=4), t2[:, 512:1024].rearrange("p (k m c) -> p k m c", k=8, m=4)]
        b_t5 = [b_t, Buf("t5pool")]
        handoff([b_t5[1]], [b_t])
        for gc in range(8):
            sl = gc % 2
            EG = DVE if gc % 2 == 0 else POOL
            T1s, T2s, b_tt = T1e[gc % 2], T2e[gc % 2], b_t5[gc % 2]
            S = Sslot[sl]
            Sv = S.rearrange("p k r (m g c) -> p k r m g c", m=4, g=2)
            prk = pw[:, 0:8, 0, 4 * gc:4 * gc + 4].unsqueeze(3).to_broadcast([128, 8, 4, 16])
            pik = pw[:, 0:8, 1, 4 * gc:4 * gc + 4].unsqueeze(3).to_broadcast([128, 8, 4, 16])
            bbr = bb[:, 0, 4 * gc:4 * gc + 4, :].unsqueeze(1).to_broadcast([128, 8, 4, 16])
            bbi = bb[:, 1, 4 * gc:4 * gc + 4, :].unsqueeze(1).to_broadcast([128, 8, 4, 16])
            for ri in range(2):
                x1, x2 = (bbr, bbi) if ri == 0 else (bbi, bbr)
                op = ALU.subtract if ri == 0 else ALU.add
                P.op(EG, lambda e, x1=x1, prk=prk, T1s=T1s: e.tensor_tensor(out=T1s, in0=prk, in1=x1, op=ALU.mult), reads=[b_pw, b_bb], writes=[b_tt])
                P.op(EG, lambda e, x2=x2, pik=pik, T2s=T2s: e.tensor_tensor(out=T2s, in0=pik, in1=x2, op=ALU.mult), reads=[b_pw, b_bb], wadd=[b_tt])
                for g2 in range(2):
                    lo, hi = 64 * g2, 64 * g2 + 64
                    P.op(EG, lambda e, ri=ri, g2=g2, lo=lo, hi=hi, op=op, Sv=Sv, T1s=T1s, T2s=T2s: e.tensor_tensor(
                        out=Sv[lo:hi, :, ri, :, g2, :], in0=T1s[lo:hi], in1=T2s[lo:hi], op=op),
                        reads=[b_tt, b_SInit[sl]], wadd=[b_Sslot[sl]])
            P.op(EG, lambda e, gc=gc, S=S: e.tensor_copy(out=Sb[:, gc], in_=S[:, 0]), reads=[b_Sslot[sl]], wadd=[b_Sb])
            for q4 in range(4):
                bkw = bank()
                for j in range(4):
                    k_, ri_ = (4 * q4 + j) // 2, (4 * q4 + j) % 2
                    P.op(PE, lambda e, bkw=bkw, j=j, k_=k_, ri_=ri_, S=S: e.transpose(
                        out=ps[bkw][:, 128 * j:128 * j + 128], in_=S[:, k_, ri_, :], identity=ident),
                        reads=[b_Sslot[sl], b_ident], writes=[psb[bkw]] if j == 0 else [], wadd=[psb[bkw]] if j > 0 else [])
                dstw = WinL[:, gc, 2 * q4:2 * q4 + 2].rearrange("p k r c -> p (k r c)")
                evac_copy(q4, dstw, ps[bkw][:, 0:512], [psb[bkw]], [], wadd=[b_WinL[gc]])

        t1p = cv_(O_S, [2, 16, 16], F32); t2p = cv_(O_S + 2048, [2, 16, 16], F32); cbp = cv_(O_S + 4096, [2, 16, 16], F32)
        b_p1 = Buf("p1tmp")
        handoff([b_p1], b_Sslot)
        b_Hend = Buf("Hend")

        def x_matmuls(gc, banks):
            for ri in range(2):
                for s_ in range(8):
                    for m in range(4):
                        first = (ri == 0 and s_ == 0)
                        last = (ri == 1 and s_ == 7)
                        bkx = banks[m]
                        P.op(PE, lambda e, m=m, ri=ri, s_=s_, bkx=bkx, gc=gc: e.matmul(
                            ps[bkx][:, 256 * ri:256 * ri + 256], lhsT=WinL[32 * m:32 * m + 32, gc, 7 - s_, ri, :],
                            rhs=uT[32 * m:32 * m + 32, gc, s_:NP:8], start=(s_ == 0), stop=(s_ == 7),
                            tile_position=(32 * m, 0)),
                            reads=[b_WinL[gc], uT_b[gc]] if first else [], writes=[psb[bkx]] if first else [], sig=last)
                        if last and not P.dead:
                            P.attach(Dep(P.esem[PE], P.cnt[PE]), reads=[b_WinL[gc], uT_b[gc]], writes=[psb[bkx]])

        def seg_reduce(src_ap, Rtab, gp0, ngp, out_ap, rbufs, wbuf):
            raise NotImplementedError

        def pass1():
            for gc in range(8):
                banks = [bank() for _ in range(4)]
                x_matmuls(gc, banks)
                if DEBUG and gc == 0 and os.environ.get('KX0'):
                    xdbg = cv_(O_S + 6144, [512], F32); b_xdbg = Buf()
                    P.op(DVE, lambda e: e.tensor_copy(out=xdbg, in_=ps[banks[1]][:, :]), reads=[psb[banks[1]]], writes=[b_xdbg])
                    P.dma(SP, dout_sem, dbg["X0"], xdbg, reads=[b_xdbg])
                for m in range(4):
                    gp = 4 * gc + m
                    X4 = ps[banks[m]][:, :].rearrange("p (r s i) -> p r s i", r=2, s=16)
                    Pr = R8[:, :, 0, gp].unsqueeze(1).unsqueeze(1).to_broadcast([128, 2, 16, 16])
                    Pi = R8[:, :, 1, gp].unsqueeze(1).unsqueeze(1).to_broadcast([128, 2, 16, 16])
                    P.op(DVE, lambda e, X4=X4, Pr=Pr: e.tensor_tensor(out=t1p, in0=X4, in1=Pr, op=ALU.mult),
                         reads=[psb[banks[m]], b_R8], writes=[b_p1])
                    P.op(DVE, lambda e, X4=X4, Pi=Pi: e.tensor_tensor(out=t2p, in0=X4, in1=Pi, op=ALU.mult),
                         reads=[psb[banks[m]], b_R8], wadd=[b_p1])
                    P.op(DVE, lambda e: e.tensor_tensor(out=cbp[:, 0], in0=t1p[:, 0], in1=t2p[:, 1], op=ALU.subtract), reads=[b_p1], wadd=[b_p1])
                    P.op(DVE, lambda e: e.tensor_tensor(out=cbp[:, 1], in0=t2p[:, 0], in1=t1p[:, 1], op=ALU.add), reads=[b_p1], wadd=[b_p1])
                    P.op(DVE, lambda e, gp=gp: e.tensor_reduce(out=Hend[:, :, gp, :], in_=cbp, axis=AX.X, op=ALU.add),
                         reads=[b_p1], wadd=[b_Hend])


        Ecore = cv_(O_S + 6144, [2, 32], F32); Eall = cv_(O_S + 6400, [8, 64], F32)
        te1 = cv_(O_S + 0, [2, 32, 16], F32); te2 = cv_(O_S + 8448, [2, 32, 16], F32)
        b_E = Buf("Ecore"); b_Eall = Buf("Eall"); b_te = b_p1
        handoff([b_E, b_Eall], b_Sslot)
        def ecore(j):
            Qr = R128[:, :, 0, :].rearrange("p i g -> p g i").unsqueeze(1).to_broadcast([128, 2, 32, 16])
            Qi = R128[:, :, 1, :].rearrange("p i g -> p g i").unsqueeze(1).to_broadcast([128, 2, 32, 16])
            P.op(DVE, lambda e: e.tensor_tensor(out=te1, in0=Hend, in1=Qr, op=ALU.mult), reads=[b_Hend, b_R128], writes=[b_te])
            P.op(DVE, lambda e: e.tensor_tensor(out=te2, in0=Hend, in1=Qi, op=ALU.mult), reads=[b_Hend, b_R128], wadd=[b_te])
            P.op(DVE, lambda e: e.tensor_tensor(out=te1[:, 0], in0=te1[:, 0], in1=te2[:, 1], op=ALU.subtract), reads=[b_te], writes=[b_te])
            P.op(DVE, lambda e: e.tensor_tensor(out=te2[:, 0], in0=te2[:, 0], in1=te1[:, 1], op=ALU.add), reads=[b_te], writes=[b_te])
            P.op(DVE, lambda e: e.tensor_reduce(out=Ecore[:, 0, :], in_=te1[:, 0], axis=AX.X, op=ALU.add), reads=[b_te], wadd=[b_E])
            P.op(DVE, lambda e: e.tensor_reduce(out=Ecore[:, 1, :], in_=te2[:, 0], axis=AX.X, op=ALU.add), reads=[b_te], wadd=[b_E])

            if j is not None:
                P.op(DVE, lambda e, j=j: e.tensor_copy(out=Eall[:, j, :], in_=Ecore.rearrange("p r g -> p (r g)")), reads=[b_E], wadd=[b_Eall])

        P.phase(3)
        srcs = [(xprev[1], False), (xprev[2], False), (xp, True)]
        phase_a(*srcs[0])
        for j in range(3):
            pass1()
            ecore(j)
            phase_b(srcs[j][1])
            if j < 2:
                phase_a(*srcs[j + 1])
        P.phase(6)
        pass1()
        P.phase(7)
        Sn = [cv_(O_S + 12544 + 256 * n, [2, 32], F32) for n in range(3)]
        b_Sn = Buf("Sn")
        handoff([b_Sn], b_Sslot)
        for n in range(3):
            Snf = Sn[n].rearrange("p r g -> p (r g)")
            P.op(DVE, lambda e, n=n, Snf=Snf: e.tensor_scalar(out=Snf, in0=Eall[:, n, :], scalar1=flags[:, 1 + n:2 + n], scalar2=None,
                                                            op0=ALU.mult), reads=[b_Eall, b_flags], wadd=[b_Sn])
        b_carry = Buf("carry")
        tq1 = cv_(O_S + 13312, [2, 32], F32); tq2 = cv_(O_S + 13568, [2, 32], F32)
        b_tq = Buf("tq")
        handoff([b_tq], b_Sslot)

        def cmul_small(dst, x, y, rb, wb):
            cmul(DVE, dst[:, 0, :], dst[:, 1, :], x[:, 0, :], x[:, 1, :], y[:, 0, :], y[:, 1, :], tq1[:, 0, :], tq1[:, 1, :], rb, wb, b_tq)

        cmul_small(carry[:, 1], Sn[1], A2k[:, 1], [b_Sn, b_A2k], [b_carry])
        cmul_small(carry[:, 2], Sn[2], A2k[:, 2], [b_Sn, b_A2k, b_carry], [b_carry])
        P.op(DVE, lambda e: e.tensor_tensor(out=Sn[0], in0=Sn[0], in1=carry[:, 1], op=ALU.add), reads=[b_Sn, b_carry], writes=[b_Sn])
        P.op(DVE, lambda e: e.tensor_tensor(out=carry[:, 0], in0=Sn[0], in1=carry[:, 2], op=ALU.add), reads=[b_Sn, b_carry], writes=[b_carry])
        A128v = coef[:, 0:2, :]
        for sg in range(16):
            cmul_small(carry[:, sg + 1], carry[:, sg], A128v, [b_carry, b_coef], [b_carry])
            P.op(DVE, lambda e, sg=sg: e.tensor_tensor(out=carry[:, sg + 1], in0=carry[:, sg + 1], in1=Hend[:, :, :, sg], op=ALU.add),
                 reads=[b_carry, b_Hend], writes=[b_carry])
        pstT = cv_(O_S + 13824, [2, 128], F32)
        b_pst = Buf()
        handoff([b_pst], b_Sslot)
        bkp = bank()
        for ri in range(2):
            P.op(PE, lambda e, ri=ri: e.transpose(out=ps[bkp][0:32, 128 * ri:128 * ri + 128], in_=carry[:, 16, ri, :], identity=ident),
                 reads=[b_carry, b_ident], writes=[psb[bkp]] if ri == 0 else [], wadd=[psb[bkp]] if ri == 1 else [])
        P.op(DVE, lambda e: e.tensor_copy(out=pstT[0:32].rearrange("p r c -> p (r c)"), in_=ps[bkp][0:32, 0:256]), reads=[psb[bkp]], writes=[b_pst])
        P.dma(SP, osem_new("pre"), pst_re, pstT[0:32, 0, :], reads=[b_pst])
        P.dma(SP, osem_new("pim"), pst_im, pstT[0:32, 1, :], reads=[b_pst])

        P.phase(8)
        CaBD = [cv_(O_CA + 4608 * i, [4, 9, 2, 32], BF16) for i in range(2)]
        KLs = [cv_(O_KL + 2048 * i, [8, 128], BF16) for i in range(2)]
        Hb = [cv_(O_HB + 4096 * i, [2, 4, 256], BF16) for i in range(2)]
        Xs = [cv_(O_S + 8192 * i, [2, 4, 256], F32) for i in range(2)]
        b_CaBD = [Buf("Ca0"), Buf("Ca1")]; b_KL = [Buf("KL0"), Buf("KL1")]; b_Hb = [Buf("Hb0"), Buf("Hb1")]; b_Xs = [Buf("Xs0"), Buf("Xs1")]
        old_c = [b_ct, b_csg, b_ccT, b_bin, b_baT, xst_b[0], xst_b[1]]
        handoff(b_CaBD + b_KL + b_Hb, old_c)
        handoff(b_Xs, [b_p1, b_E, b_Eall, b_te, b_Sn, b_tq, b_pst] + b_Sslot)
        KL0all = cv_(O_C + 10496, [8, 128], BF16); Ca1all = cv_(O_C + 14592, [32, 2, 32], BF16)
        b_KL0 = Buf("KL0all"); b_Ca1 = Buf("Ca1all")
        handoff([b_KL0], [b_R8]); handoff([b_Ca1], [b_R128])
        Q1e = [misc[:, 24:28, :].rearrange("p a g -> p (a g)").rearrange("p (r m s) -> p r m s", r=2, m=4),
               misc[:, 0:4, :].rearrange("p a g -> p (a g)").rearrange("p (r m s) -> p r m s", r=2, m=4)]
        Q2e = [misc[:, 28:32, :].rearrange("p a g -> p (a g)").rearrange("p (r m s) -> p r m s", r=2, m=4),
               misc[:, 4:8, :].rearrange("p a g -> p (a g)").rearrange("p (r m s) -> p r m s", r=2, m=4)]
        tmpK = misc[:, 16:20, :].rearrange("p a g -> p (a g)")
        b_Q = [Buf("Qdve"), Buf("Qpool")]
        b_tmpK = Buf("tmpK")
        handoff([b_tmpK] + b_Q, [b_misc])
        b_CaInit = [Buf("CaInit0"), Buf("CaInit1")]
        for i in range(2):
            P.op(POOL, lambda e, i=i: e.memset(CaBD[i].rearrange("p m n r c -> p (m n r c)"), 0.0), writes=[b_CaBD[i], b_CaInit[i]])
        U1 = t1[:, 0:576].rearrange("p (m n c) -> p m n c", m=4, n=9)
        U2 = t2[:, 0:576].rearrange("p (m n c) -> p m n c", m=4, n=9)
        XB = [2, 3, 4, 5]
        YB = [6, 7]

        def emit_consts(gc):
            sl = gc % 2
            Ca = CaBD[sl]
            cre = ccm[:, 0, 4 * gc:4 * gc + 4, :].unsqueeze(2).to_broadcast([128, 4, 9, 16])
            cim = ccm[:, 1, 4 * gc:4 * gc + 4, :].unsqueeze(2).to_broadcast([128, 4, 9, 16])
            pr = pw[:, :, 0, 4 * gc:4 * gc + 4].rearrange("p n m -> p m n").unsqueeze(3).to_broadcast([128, 4, 9, 16])
            pi = pw[:, :, 1, 4 * gc:4 * gc + 4].rearrange("p n m -> p m n").unsqueeze(3).to_broadcast([128, 4, 9, 16])
            P.op(DVE, lambda e: e.tensor_tensor(out=U1, in0=cre, in1=pr, op=ALU.mult), reads=[b_ccm, b_pw], writes=[b_t])
            P.op(DVE, lambda e: e.tensor_tensor(out=U2, in0=cim, in1=pi, op=ALU.mult), reads=[b_ccm, b_pw], wadd=[b_t])
            for g2 in range(2):
                lo, hi = 64 * g2, 64 * g2 + 64
                P.op(DVE, lambda e, lo=lo, hi=hi, g2=g2: e.tensor_tensor(
                    out=Ca[lo:hi, :, :, 0, 16 * g2:16 * g2 + 16], in0=U1[lo:hi], in1=U2[lo:hi], op=ALU.subtract),
                    reads=[b_t, b_CaInit[sl]], wadd=[b_CaBD[sl]])
            P.op(DVE, lambda e: e.tensor_tensor(out=U1, in0=cre, in1=pi, op=ALU.mult), reads=[b_ccm, b_pw], writes=[b_t])
            P.op(DVE, lambda e: e.tensor_tensor(out=U2, in0=cim, in1=pr, op=ALU.mult), reads=[b_ccm, b_pw], wadd=[b_t])
            P.op(DVE, lambda e: e.tensor_tensor(out=U1, in0=U1, in1=U2, op=ALU.add), reads=[b_t], writes=[b_t])
            for g2 in range(2):
                lo, hi = 64 * g2, 64 * g2 + 64
                P.op(DVE, lambda e, lo=lo, hi=hi, g2=g2: e.tensor_scalar(
                    out=Ca[lo:hi, :, :, 1, 16 * g2:16 * g2 + 16], in0=U1[lo:hi], scalar1=-1.0, scalar2=0.0, op0=ALU.mult, op1=ALU.add),
                    reads=[b_t, b_CaInit[sl]], wadd=[b_CaBD[sl]])
            P.op(DVE, lambda e: e.tensor_copy(out=Ca1all[:, 4 * gc:4 * gc + 4], in_=Ca[:, :, 1, :, :]), reads=[b_CaBD[sl]], wadd=[b_Ca1])
            for hb in range(2):
                for tt in range(4):
                    tau = 4 * hb + tt
                    for ri in range(2):
                        P.op(PE, lambda e, hb=hb, tt=tt, tau=tau, ri=ri: e.matmul(
                            ps[hb][:, 128 * tt:128 * tt + 128], lhsT=Sb[:, gc, ri, :], rhs=Ca[:, :, tau, ri, :],
                            start=(ri == 0), stop=(ri == 1)),
                            reads=[b_Sb, b_CaBD[sl]] if (tt == 0 and ri == 0) else [],
                            writes=[psb[hb]] if (tt == 0 and ri == 0) else [], sig=(tt == 3 and ri == 1))
                if not P.dead:
                    P.attach(Dep(P.esem[PE], P.cnt[PE]), reads=[b_Sb, b_CaBD[sl]], writes=[psb[hb]])
            KL = KLs[sl]
            bmb3 = bmask.unsqueeze(1).to_broadcast([128, 3, 128]); bmb4 = bmask.unsqueeze(1).to_broadcast([128, 4, 128])
            P.op(DVE, lambda e: e.tensor_tensor(out=KL[:, 1:4, :], in0=ps[0][:, 128:512].rearrange("p (t c) -> p t c", t=3), in1=bmb3, op=ALU.mult),
                 reads=[psb[0], b_bmask], wadd=[b_KL[sl]])
            P.op(DVE, lambda e: e.tensor_tensor(out=tmpK, in0=ps[0][:, 0:128], in1=bmask, op=ALU.mult), reads=[psb[0], b_bmask], writes=[b_tmpK])
            P.op(DVE, lambda e: e.tensor_tensor(out=KL[:, 4:8, :], in0=ps[1][:, 0:512].rearrange("p (t c) -> p t c", t=4), in1=bmb4, op=ALU.mult),
                 reads=[psb[1], b_bmask], wadd=[b_KL[sl]])
            P.op(DVE, lambda e: e.scalar_tensor_tensor(out=KL[:, 0, :], in0=ident, scalar=Dm[:, gc:gc + 1], in1=tmpK, op0=ALU.mult, op1=ALU.add),
                 reads=[b_tmpK, b_ident, b_Dm], wadd=[b_KL[sl]])
            P.op(DVE, lambda e: e.tensor_copy(out=KL0all[:, gc, :], in_=KL[:, 0, :]), reads=[b_KL[sl]], wadd=[b_KL0])

        def emit_x_scan(gc):
            sl = gc % 2
            x_matmuls(gc, XB)
            X = Xs[sl]
            for m in range(4):
                P.op(ACT, lambda e, m=m: e.activation(out=X[:, :, m, :], in_=ps[XB[m]][:, :].rearrange("p (r j) -> p r j", r=2), func=AF.Copy),
                     reads=[psb[XB[m]]], wadd=[b_Xs[sl]])
            E, qi = (POOL, 1) if gc in (1, 4, 6) else (DVE, 0)
            Q1 = Q1e[qi]; Q2 = Q2e[qi]
            X5 = X.rearrange("p r m (s i) -> p r m s i", i=16)
            Ar = pw[:, 8, 0, 4 * gc:4 * gc + 4].unsqueeze(1).unsqueeze(3).to_broadcast([128, 2, 4, 16])
            Ai = pw[:, 8, 1, 4 * gc:4 * gc + 4].unsqueeze(2).to_broadcast([128, 4, 16])
            AiN = coef[:, 4, 4 * gc:4 * gc + 4].unsqueeze(2).to_broadcast([128, 4, 16])
            cview = carry[:, 0:16, :, 4 * gc:4 * gc + 4].rearrange("p s r m -> p r m s")
            for i in range(16):
                prev = cview if i == 0 else X5[:, :, :, :, i - 1]
                cur = X5[:, :, :, :, i]
                rb = [b_carry, b_pw, b_coef, b_Xs[sl]]
                P.op(E, lambda e, prev=prev: e.tensor_tensor(out=Q1, in0=prev, in1=Ar, op=ALU.mult), reads=rb, writes=[b_Q[qi]])
                P.op(E, lambda e, prev=prev: e.tensor_tensor(out=Q2[:, 0], in0=prev[:, 1], in1=AiN, op=ALU.mult), reads=rb, wadd=[b_Q[qi]])
                P.op(E, lambda e, prev=prev: e.tensor_tensor(out=Q2[:, 1], in0=prev[:, 0], in1=Ai, op=ALU.mult), reads=rb, wadd=[b_Q[qi]])
                P.op(E, lambda e, cur=cur: e.tensor_tensor(out=cur, in0=cur, in1=Q1, op=ALU.add), reads=[b_Q[qi]], writes=[b_Xs[sl]])
                P.op(E, lambda e, cur=cur: e.tensor_tensor(out=cur, in0=cur, in1=Q2, op=ALU.add), reads=[b_Q[qi]], writes=[b_Xs[sl]])

        def emit_hb(gc):
            sl = gc % 2
            X = Xs[sl]
            X5 = X.rearrange("p r m (s i) -> p r m s i", i=16)
            cview = carry[:, 0:16, :, 4 * gc:4 * gc + 4].rearrange("p s r m -> p r m s")
            H5 = Hb[sl].rearrange("p r m (s i) -> p r m s i", i=16)
            P.op(ACT, lambda e: e.activation(out=H5[:, :, :, :, 1:16].rearrange("p r m s i -> p (r m) s i"),
                                             in_=X5[:, :, :, :, 0:15].rearrange("p r m s i -> p (r m) s i"), func=AF.Copy),
                 reads=[b_Xs[sl]], writes=[b_Hb[sl]])
            P.op(ACT, lambda e: e.activation(out=H5[:, :, :, :, 0], in_=cview, func=AF.Copy), reads=[b_carry], wadd=[b_Hb[sl]])

        def emit_y(gc):
            sl = gc % 2
            KL = KLs[sl]; Ca = CaBD[sl]
            uview = uT[:, gc, 0:NP].rearrange("p (j s) -> p s j", s=8)
            for half in (1, 0):
                for tl in range(4):
                    t_lo = 4 * half + tl
                    bk_ = YB[tl // 2]
                    reg = ps[bk_][:, 256 * (tl % 2):256 * (tl % 2) + 256]
                    n_mm = (t_lo + 1) + 8
                    idx = 0
                    for s_ in range(t_lo + 1):
                        P.op(PE, lambda e, reg=reg, s_=s_, t_lo=t_lo: e.matmul(
                            reg, lhsT=KL[:, t_lo - s_, :], rhs=uview[:, s_, :], start=(s_ == 0), stop=False),
                            reads=[b_KL[sl], uT_b[gc], b_CaBD[sl], b_Hb[sl]] if idx == 0 else [],
                            writes=[psb[bk_]] if (idx == 0 and tl % 2 == 0) else [], sig=False)
                        idx += 1
                    for m in range(4):
                        for ri in range(2):
                            lastmm = (m == 3 and ri == 1)
                            P.op(PE, lambda e, reg=reg, m=m, ri=ri, t_lo=t_lo, lastmm=lastmm: e.matmul(
                                reg[32 * m:32 * m + 32, :], lhsT=Ca[:, m, t_lo + 1, ri, :], rhs=Hb[sl][:, ri, m, :],
                                start=False, stop=(ri == 1), tile_position=(0, 32 * m)), sig=lastmm)
                    if not P.dead:
                        P.attach(Dep(P.esem[PE], P.cnt[PE]), reads=[b_KL[sl], uT_b[gc], b_CaBD[sl], b_Hb[sl]],
                                 writes=[psb[bk_]] if tl % 2 == 1 else [], wadd=[psb[bk_]] if tl % 2 == 0 else [])
                for bi in range(2):
                    t0_ = 4 * half + 2 * bi
                    P.op(ACT, lambda e, bi=bi, t0_=t0_: e.activation(
                        out=uview[:, t0_:t0_ + 2, :], in_=ps[YB[bi]][:, :].rearrange("p (t j) -> p t j", t=2), func=AF.Gelu_apprx_tanh),
                        reads=[psb[YB[bi]]], writes=[uT_b[gc]])

        for g0 in range(2):
            emit_consts(g0)
            emit_x_scan(g0)
            emit_hb(g0)
        for gc in range(8):
            if gc + 2 < 8:
                emit_x_scan(gc + 2)
            emit_y(gc)
            if gc + 2 < 8:
                emit_consts(gc + 2)
                emit_hb(gc + 2)
        yT = uT
        yT_b = uT_b

        P.phase(9)
        all_ssm_tmp = b_Xs + b_Hb + b_Q + [b_tmpK, b_t, b_p1, b_E, b_Eall, b_te, b_Sn, b_tq, b_pst] + b_Sslot
        stile = [cv_(O_S + 2048 * i, [512], F32) for i in range(2)]
        Hsp = cv_(O_S + 4096, [2, 32, 16], F32); Hn = cv_(O_S + 8192, [2, 32, 16], F32)
        HbS = cv_(O_S + 12288, [2, 32, 16], BF16); Q1s = cv_(O_HB, [2, 32, 16], F32); Q2s = cv_(O_HB + 4096, [2, 32, 16], F32)
        b_stile = Buf(); b_Hsp = Buf(); b_Hn = Buf(); b_HbS = Buf(); b_Qs = Buf()
        handoff([b_stile, b_Hsp, b_Hn, b_HbS, b_Qs], all_ssm_tmp)
        misc_load(SP, stile[0], st_re.rearrange("s (gh f) -> (s gh) f", gh=8), b_stile, wadd=True)
        misc_load(SP, stile[1], st_im.rearrange("s (gh f) -> (s gh) f", gh=8), b_stile, wadd=True)
        for ri in range(2):
            bks = bank()
            for q4 in range(4):
                P.op(PE, lambda e, ri=ri, q4=q4, bks=bks: e.transpose(out=ps[bks][:, 128 * q4:128 * q4 + 128],
                                                                   in_=stile[ri][:, 128 * q4:128 * q4 + 128], identity=ident),
                     reads=[b_stile, b_ident], writes=[psb[bks]] if q4 == 0 else [], wadd=[psb[bks]] if q4 > 0 else [])
            for q4 in range(4):
                P.op(DVE, lambda e, ri=ri, q4=q4, bks=bks: e.tensor_copy(
                    out=Hsp[:, ri, q4:32:4, :], in_=ps[bks][:, 128 * q4:128 * q4 + 128].rearrange("p (s gh) -> p gh s", gh=8)),
                    reads=[psb[bks]], wadd=[b_Hsp])
        P.op(POOL, lambda e: e.tensor_copy(out=HbS, in_=Hsp), reads=[b_Hsp], writes=[b_HbS])
        xsb = [bank() for _ in range(4)]
        for m in range(4):
            for gc in range(8):
                for ri in range(2):
                    first = (gc == 0 and ri == 0); last = (gc == 7 and ri == 1)
                    P.op(PE, lambda e, m=m, gc=gc, ri=ri: e.matmul(
                        ps[xsb[m]][:, 32 * gc + 16 * ri:32 * gc + 16 * ri + 16], lhsT=WinL[32 * m:32 * m + 32, gc, 0, ri, :],
                        rhs=uT[32 * m:32 * m + 32, gc, NP:NT], start=True, stop=True, tile_position=(32 * m, 0)),
                        reads=b_WinL + uT_b if first else [], writes=[psb[xsb[m]]] if first else [], sig=last)
            if not P.dead:
                P.attach(Dep(P.esem[PE], P.cnt[PE]), reads=b_WinL + uT_b, writes=[psb[xsb[m]]])
        Ar1 = pw[:, 1, 0, :].unsqueeze(1).unsqueeze(3).to_broadcast([128, 2, 32, 16])
        Ai1 = pw[:, 1, 1, :].unsqueeze(2).to_broadcast([128, 32, 16])
        P.op(POOL, lambda e: e.tensor_scalar(out=coef[:, 6, :], in0=pw[:, 1, 1, :], scalar1=-1.0, scalar2=0.0, op0=ALU.mult, op1=ALU.add),
             reads=[b_pw], wadd=[b_coef])
        AiN1 = coef[:, 6, :].unsqueeze(2).to_broadcast([128, 32, 16])
        P.op(DVE, lambda e: e.tensor_tensor(out=Q1s, in0=Hsp, in1=Ar1, op=ALU.mult), reads=[b_Hsp, b_pw], writes=[b_Qs])
        P.op(DVE, lambda e: e.tensor_tensor(out=Q2s[:, 0], in0=Hsp[:, 1], in1=AiN1, op=ALU.mult), reads=[b_Hsp, b_coef], wadd=[b_Qs])
        P.op(DVE, lambda e: e.tensor_tensor(out=Q2s[:, 1], in0=Hsp[:, 0], in1=Ai1, op=ALU.mult), reads=[b_Hsp, b_pw], wadd=[b_Qs])
        P.op(DVE, lambda e: e.tensor_tensor(out=Hn, in0=Q1s, in1=Q2s, op=ALU.add), reads=[b_Qs], writes=[b_Hn])
        for m in range(4):
            P.op(DVE, lambda e, m=m: e.tensor_tensor(
                out=Hn[:, :, m:32:4, :], in0=ps[xsb[m]][:, 0:256].rearrange("p (gc r s) -> p r gc s", gc=8, r=2),
                in1=Hn[:, :, m:32:4, :], op=ALU.add), reads=[psb[xsb[m]], b_Hn], writes=[b_Hn])
        stg = cv_(O_HB + 8192 - 8192, [4, 128], F32)
        sout = [cv_(O_S + 2048 * i, [512], F32) for i in range(2)]
        b_stg = Buf(); b_sout = Buf()
        handoff([b_stg], [b_Qs]); handoff([b_sout], [b_stile])
        for ri in range(2):
            for q4 in range(4):
                P.op(POOL, lambda e, ri=ri, q4=q4: e.tensor_copy(out=stg[:, q4, :].rearrange("p (s gh) -> p gh s", gh=8),
                                                               in_=Hn[:, ri, q4:32:4, :]), reads=[b_Hn], writes=[b_stg] if q4 == 0 else [],
                     wadd=[b_stg] if q4 > 0 else [])
            bks = bank()
            for q4 in range(4):
                P.op(PE, lambda e, q4=q4, bks=bks: e.transpose(out=ps[bks][:, 128 * q4:128 * q4 + 128], in_=stg[:, q4, :], identity=ident),
                     reads=[b_stg, b_ident], writes=[psb[bks]] if q4 == 0 else [], wadd=[psb[bks]] if q4 > 0 else [])
            P.op(DVE, lambda e, ri=ri, bks=bks: e.tensor_copy(out=sout[ri], in_=ps[bks][:, 0:512]), reads=[psb[bks]], wadd=[b_sout])
            P.dma(SP, osem_new(f"sst{ri}"), (sst_re if ri == 0 else sst_im).rearrange("s (gh f) -> (s gh) f", gh=8), sout[ri], reads=[b_sout])
        bky = bank()
        for gc in range(8):
            reg = ps[bky][:, 16 * gc:16 * gc + 16]
            P.op(PE, lambda e, gc=gc, reg=reg: e.matmul(reg, lhsT=KL0all[:, gc, :], rhs=uT[:, gc, NP:NT], start=True, stop=False),
                 reads=[b_KL0, b_Ca1, b_HbS] + uT_b if gc == 0 else [], writes=[psb[bky]] if gc == 0 else [], sig=False)
            for m in range(4):
                for ri in range(2):
                    lastmm = (m == 3 and ri == 1)
                    P.op(PE, lambda e, gc=gc, reg=reg, m=m, ri=ri, lastmm=lastmm: e.matmul(
                        reg[32 * m:32 * m + 32, :], lhsT=Ca1all[:, 4 * gc + m, ri, :], rhs=HbS[:, ri, 4 * gc + m, :],
                        start=False, stop=(ri == 1), tile_position=(0, 32 * m)), sig=(lastmm and gc == 7))
        if not P.dead:
            P.attach(Dep(P.esem[PE], P.cnt[PE]), reads=[b_KL0, b_Ca1, b_HbS] + uT_b, writes=[psb[bky]])
        P.op(ACT, lambda e: e.activation(out=uT[:, :, NP:NT], in_=ps[bky][:, 0:128].rearrange("p (g s) -> p g s", g=8),
                                         func=AF.Gelu_apprx_tanh), reads=[psb[bky]], writes=uT_b)

        P.phase(10)
        s2T = RB
        s2_b = [Buf(f"s2_{g}") for g in range(8)]
        handoff(s2_b, b_WinL)
        gtmp = [cv_(O_C + 1024 * i, [512], BF16) for i in range(4)]
        ftmp = [cv_(O_C + 4096 + 2048 * i, [512], F32) for i in range(2)]
        b_gtmp = [Buf() for _ in range(4)]; b_ftmp = [Buf(), Buf()]
        handoff(b_gtmp + b_ftmp, [b_pw, b_bb, b_ccm])
        rhs_y = lambda k, t0, n: yT[:, k, t0:t0 + n]
        yall_b = [yT_b] * 5
        ctr = {"i": 0}

        class AllOf:
            pass
        for blk in range(2):
            def ev_glu(oc, tbi, pap, pb, blk=blk):
                t0, n = TBS[tbi]
                g = 4 * blk + oc
                ctr["i"] += 1
                gi = ctr["i"] % 4
                P.op(ACT, lambda e: e.activation(out=gtmp[gi][:, 0:n], in_=pap, func=AF.Sigmoid, bias=bglu[:, g:g + 1], scale=1.0),
                     reads=[pb, b_bglu], writes=[b_gtmp[gi]])
                P.op(DVE, lambda e: e.tensor_tensor(out=s2T[:, g, t0:t0 + n], in0=yT[:, g, t0:t0 + n], in1=gtmp[gi][:, 0:n], op=ALU.mult),
                     reads=[b_gtmp[gi], yT_b[g]], wadd=[s2_b[g]])
            proj_fm(w_glu[:, 512 * blk:512 * blk + 512], 512, rhs_y, [BufGroup(yT_b)] * 5, ev_glu)

            def ev_zs(oc, tbi, pap, pb, blk=blk):
                t0, n = TBS[tbi]
                g = 4 * blk + oc
                ctr["i"] += 1
                gi = ctr["i"] % 4
                fi = ctr["i"] % 2
                P.op(ACT, lambda e: e.activation(out=gtmp[gi][:, 0:n], in_=pap, func=AF.Sigmoid), reads=[pb], writes=[b_gtmp[gi]])
                P.op(DVE, lambda e: e.tensor_tensor(out=ftmp[fi][:, 0:n], in0=pap, in1=gtmp[gi][:, 0:n], op=ALU.mult),
                     reads=[pb, b_gtmp[gi]], writes=[b_ftmp[fi]])
                P.op(POOL, lambda e: e.tensor_tensor(out=s2T[:, g, t0:t0 + n], in0=s2T[:, g, t0:t0 + n], in1=ftmp[fi][:, 0:n], op=ALU.mult),
                     reads=[b_ftmp[fi], s2_b[g]], wadd=[s2_b[g]])
            proj_fm(w_in[:, OFF_ZS + 512 * blk:OFF_ZS + 512 * blk + 512], 512, rhs_h, hT_b, ev_zs)

        P.phase(11)
        gbs = RA
        gbs_b = [Buf(f"gbs{g}") for g in range(8)]
        handoff(gbs_b, yT_b)
        rhs_s2 = lambda k, t0, n: s2T[:, k, t0:t0 + n]
        for blk in range(2):
            def ev_gs(oc, tbi, pap, pb, blk=blk):
                t0, n = TBS[tbi]
                g = 4 * blk + oc
                P.op(ACT, lambda e: e.activation(out=gbs[:, g, t0:t0 + n], in_=pap, func=AF.Sigmoid), reads=[pb], wadd=[gbs_b[g]])
            proj_fm(w_in[:, OFF_GS + 512 * blk:OFF_GS + 512 * blk + 512], 512, rhs_h, hT_b, ev_gs)

            def ev_bs(oc, tbi, pap, pb, blk=blk):
                t0, n = TBS[tbi]
                g = 4 * blk + oc
                P.op(DVE, lambda e: e.tensor_tensor(out=gbs[:, g, t0:t0 + n], in0=pap, in1=gbs[:, g, t0:t0 + n], op=ALU.mult),
                     reads=[pb, gbs_b[g]], wadd=[gbs_b[g]])
            proj_fm(w_bs[:, 512 * blk:512 * blk + 512], 512, rhs_s2, [BufGroup(s2_b)] * 5, ev_bs)

        P.phase(12)
        oT = RB
        oT_b = [Buf(f"oT{g}") for g in range(8)]
        handoff(oT_b, s2_b)
        oc_ = O_C
        qT = cv_(oc_ + 0, [2, NT], BF16); kT2 = cv_(oc_ + 8256, [128 + NT], BF16); Vaug = cv_(oc_ + 12640, [18, 128], BF16)
        EB = cv_(oc_ + 17248, [2, 16, 128], BF16); EB0 = cv_(oc_ + 25440, [16, 128], BF16)
        Et = [cv_(oc_ + 29536 + 1024 * i, [512], BF16) for i in range(4)]
        PT = [cv_(oc_ + 33632 + 1024 * i, [512], BF16) for i in range(4)]
        rc = [cv_(oc_ + 37728 + 2048 * i, [512], F32) for i in range(2)]
        maskt = cv_(oc_ + 41824, [2, 128], F32); RT = cv_(oc_ + 42848, [384], F32); relb = cv_(oc_ + 44384, [16], F32)
        es16 = cv_(oc_ + 44448, [16], F32); klast = cv_(oc_ + 44512, [256], F32); vlast = cv_(oc_ + 45536, [256], F32)
        knew = cv_(oc_ + 46560, [256], F32); vnew = cv_(oc_ + 47584, [256], F32)
        Kc = cv_(oc_ + 48608, [16, 256], F32)
        KcT = cv_(oc_ + 64992, [16, 2, 128], BF16)
        Vcs = cv_(oc_ + 73184, [16, 256], BF16)
        QsT = cv_(oc_ + 81376, [2, 4, 16], BF16)
        dgt = cv_(oc_ + 81632, [64], F32); vnb = cv_(oc_ + 81888, [256], BF16); pdg = cv_(oc_ + 82400, [64], BF16)
        esr = cv_(oc_ + 82528, [64], BF16); ebs = cv_(oc_ + 82656, [16], F32); rcs = cv_(oc_ + 82720, [128], F32)
        ptS = cv_(oc_ + 83232, [128], BF16); ones_k = cv_(oc_ + 83488, [128], BF16)
        attn_bufs = {n: Buf(n) for n in ["qT", "kT2", "Vaug", "EB", "EB0", "mask", "RT", "relb", "es16", "klast", "vlast", "knew", "vnew",
                                         "Kc", "KcT", "Vcs", "QsT", "dgt", "vnb", "pdg", "esr", "ebs", "rcs", "ptS", "ones_k"]}
        A = attn_bufs
        b_Et = [Buf() for _ in range(4)]; b_PT = [Buf() for _ in range(4)]; b_rc = [Buf(), Buf()]
        prev_c = [b_pw, b_bb, b_ccm, b_R8, b_R128, b_A2k, b_Hend, b_carry, b_Sb, b_coef, b_misc, b_t, b_KL0, b_Ca1,
                  b_stile, b_Hsp, b_Hn, b_HbS, b_Qs, b_stg, b_sout] + b_gtmp + b_ftmp + all_ssm_tmp + b_CaBD + b_KL
        handoff(list(A.values()) + b_Et + b_PT + b_rc, prev_c)
        misc_load(SP, RT[0:32, :], rtab, A["RT"]); misc_load(SP, relb[0:32, :], rel_bias, A["relb"])
        misc_load(SP, maskt.rearrange("p h q -> p (h q)"), maskc, A["mask"])
        misc_load(SP, es16[0:1, :], sinks.rearrange("(o n) -> o n", o=1), A["es16"])
        misc_load(SP, ebs[0:16, :], rel_bias[0:1, :].to_broadcast([16, 16]), A["ebs"])
        misc_load(SP, dgt[0:16, :], diagc, A["dgt"])
        P.op(ACT, lambda e: e.activation(out=es16[0:1, :], in_=es16[0:1, :], func=AF.Exp), reads=[A["es16"]], writes=[A["es16"]])
        P.op(ACT, lambda e: e.activation(out=ebs[0:16, :], in_=ebs[0:16, :], func=AF.Exp), reads=[A["ebs"]], writes=[A["ebs"]])
        for kv in range(4):
            for sl_, i in enumerate([0, 2, 1, 3]):
                h = 4 * kv + i
                P.op(DVE, lambda e, kv=kv, sl_=sl_, h=h: e.tensor_copy(out=ES[0:1, kv, sl_, :], in_=es16[0:1, h:h + 1].to_broadcast([1, 128])),
                     reads=[A["es16"]], wadd=[b_ES])
        P.op(POOL, lambda e: e.memset(ones_k, 1.0), writes=[A["ones_k"]])
        for half in range(2):
            for qb in range(4):
                bke = bank()
                for qq in range(32):
                    q = 32 * qb + qq
                    st_ = (127 - q) if half == 0 else (255 - q)
                    P.op(PE, lambda e, bke=bke, qq=qq, st_=st_: e.matmul(ps[bke][:, 16 * qq:16 * qq + 16], lhsT=RT[0:32, st_:st_ + 128],
                                                                       rhs=relb[0:32, :], start=True, stop=True),
                         reads=[A["RT"], A["relb"]] if qq == 0 else [], writes=[psb[bke]] if qq == 0 else [], sig=(qq == 31))
                if not P.dead:
                    P.attach(Dep(P.esem[PE], P.cnt[PE]), reads=[A["RT"], A["relb"]], writes=[psb[bke]])
                P.op(ACT, lambda e, bke=bke, half=half, qb=qb: e.activation(
                    out=EB[:, half, :, 32 * qb:32 * qb + 32], in_=ps[bke][:, 0:512].rearrange("p (q h) -> p h q", h=16), func=AF.Exp),
                    reads=[psb[bke]], wadd=[A["EB"]])
        P.op(DVE, lambda e: e.tensor_tensor(out=EB, in0=EB, in1=maskt.unsqueeze(2).to_broadcast([128, 2, 16, 128]), op=ALU.mult),
             reads=[A["EB"], A["mask"]], writes=[A["EB"]])
        P.op(DVE, lambda e: e.tensor_scalar(out=EB0, in0=EB[:, 0], scalar1=flags[:, 0:1], scalar2=None, op0=ALU.mult),
             reads=[A["EB"], b_flags], writes=[A["EB0"]])
        P.dma(SP, dout_sem, sck[:, 0:127, :], ck[:, 1:128, :])
        P.dma(SP, dout_sem, scv[:, 0:127, :], cv[:, 1:128, :])
        kc_sem = P.dsem("kc")
        P.dma(SP, kc_sem, Kc, ck.rearrange("s t f -> t s f"), writes=[A["Kc"]])
        for s_ in range(NS):
            bkt = bank()
            for kvp in range(2):
                P.op(PE, lambda e, s_=s_, kvp=kvp, bkt=bkt: e.transpose(out=ps[bkt][:, 128 * kvp:128 * kvp + 128],
                                                                     in_=Kc[:, s_, 128 * kvp:128 * kvp + 128], identity=ident),
                     reads=[A["Kc"], b_ident], writes=[psb[bkt]] if kvp == 0 else [], wadd=[psb[bkt]] if kvp == 1 else [])
            evac_copy(s_, KcT[:, s_].rearrange("p a t -> p (a t)"), ps[bkt][:, 0:256], [psb[bkt]], [], wadd=[A["KcT"]])
        P.dma(SP, kc_sem, Kc, cv.rearrange("s t f -> t s f"), reads=[A["KcT"]], writes=[A["Kc"]])
        P.op(POOL, lambda e: e.tensor_copy(out=Vcs, in_=Kc), reads=[A["Kc"]], writes=[A["Vcs"]])
        P.op(POOL, lambda e: e.memset(Vaug[:, :, 64:128], 1.0), writes=[A["Vaug"]])

        rhs_hh = lambda k, t0, n: hTh[:, k, 0:n]
        TB5 = TBS
        def attn_kv(kv):
            def ev_q(oc, tbi, pap, pb):
                t0, n = TBS[tbi]
                ctr["i"] += 1
                evac_copy(ctr["i"], qT[:, oc, t0:t0 + n], pap, [pb], [], wadd=[A["qT"]])
            A["qT"].r = list(A["qT"].r) + list(A["qT"].w); A["qT"].w = []
            proj_fm(w_in[:, OFF_Q + 256 * kv:OFF_Q + 256 * kv + 256], 256, rhs_h, hT_b, ev_q)
            s = wctr[0] % 2
            wctr[0] += 1
            for dup in range(2):
                P.dma(POOL, wsem[s], wslot[s][:, :, 64 * dup:64 * dup + 64],
                      w_in[:, OFF_K + 64 * kv:OFF_K + 64 * kv + 64].rearrange("(k p) f -> p k f", p=128),
                      writes=[wslot_b[s]] if dup == 0 else [], wadd=[wslot_b[s]] if dup == 1 else [])
            A["kT2"].r = list(A["kT2"].r) + list(A["kT2"].w); A["kT2"].w = []
            kblocks = [(hTh, hTh_b, 0, 128, 0)] + [(hT, hT_b[i], t0, n, 128 + t0) for i, (t0, n) in enumerate(TBS)]
            for bi_, (src, sb_, t0, n, c0) in enumerate(kblocks):
                bkk = bank()
                for k in range(8):
                    P.op(PE, lambda e, bkk=bkk, k=k, src=src, t0=t0, n=n, s=s: e.matmul(
                        ps[bkk][:, 0:n], lhsT=wslot[s][:, k, 0:128], rhs=src[:, k, t0:t0 + n], start=(k == 0), stop=(k == 7)),
                        reads=[wslot_b[s], sb_] if k == 0 else [], writes=[psb[bkk]] if k == 0 else [], sig=(k == 7))
                if not P.dead:
                    P.attach(Dep(P.esem[PE], P.cnt[PE]), reads=[wslot_b[s], sb_], writes=[psb[bkk]])
                evac_copy(bi_, kT2[:, c0:c0 + n], ps[bkk][:, 0:n], [psb[bkk]], [], wadd=[A["kT2"]])
            bkl_ = bank()
            for j_, (c0_, m_) in enumerate([(NP - 128, 128), (NP, NS)]):
                for k in range(8):
                    P.op(PE, lambda e, j_=j_, c0_=c0_, m_=m_, k=k, s=s: e.matmul(
                        ps[bkl_][0:m_, 64 * j_:64 * j_ + 64], lhsT=hT[:, k, c0_:c0_ + m_], rhs=wslot[s][:, k, 0:64],
                        start=(k == 0), stop=(k == 7)),
                        reads=[wslot_b[s], hT_b[3], hT_b[4]] if (k == 0 and j_ == 0) else [],
                        writes=[psb[bkl_]] if (k == 0 and j_ == 0) else [], sig=(k == 7 and j_ == 1))
            if not P.dead:
                P.attach(Dep(P.esem[PE], P.cnt[PE]), reads=[wslot_b[s], hT_b[3], hT_b[4]], writes=[psb[bkl_]])
            P.op(DVE, lambda e, kv=kv: e.tensor_copy(out=klast[:, 64 * kv:64 * kv + 64], in_=ps[bkl_][:, 0:64]), reads=[psb[bkl_]], wadd=[A["klast"]])
            P.op(DVE, lambda e, kv=kv: e.tensor_copy(out=knew[0:NS, 64 * kv:64 * kv + 64], in_=ps[bkl_][0:NS, 64:128]), reads=[psb[bkl_]], wadd=[A["knew"]])
            s = wctr[0] % 2
            wctr[0] += 1
            P.dma(POOL, wsem[s], wslot[s][:, :, 0:64], w_in[:, OFF_V + 64 * kv:OFF_V + 64 * kv + 64].rearrange("(k p) f -> p k f", p=128),
                  writes=[wslot_b[s]])
            A["Vaug"].r = list(A["Vaug"].r) + list(A["Vaug"].w); A["Vaug"].w = []
            vtiles = [(hTh, hTh_b, 0, 128)] + [(hT, hT_b[i // 4], 128 * i, 128) for i in range(16)] + [(hT, hT_b[4], NP, NS)]
            for grp in range(3):
                bkv = bank()
                tl_ = vtiles[8 * grp:8 * grp + 8]
                for j_, (src, sb_, c0_, m_) in enumerate(tl_):
                    for k in range(8):
                        firstg = (j_ == 0 and k == 0)
                        P.op(PE, lambda e, bkv=bkv, j_=j_, src=src, c0_=c0_, m_=m_, k=k, s=s: e.matmul(
                            ps[bkv][0:m_, 64 * j_:64 * j_ + 64], lhsT=src[:, k, c0_:c0_ + m_], rhs=wslot[s][:, k, 0:64],
                            start=(k == 0), stop=(k == 7)),
                            reads=[wslot_b[s], sb_, hTh_b] + hT_b if firstg else [], writes=[psb[bkv]] if firstg else [],
                            sig=(j_ == len(tl_) - 1 and k == 7))
                if not P.dead:
                    P.attach(Dep(P.esem[PE], P.cnt[PE]), reads=[wslot_b[s], hTh_b] + hT_b, writes=[psb[bkv]])
                nt_ = len(tl_)
                if grp < 2:
                    P.op(ACT, lambda e, bkv=bkv, grp=grp: e.activation(out=Vaug[:, 8 * grp:8 * grp + 8, 0:64],
                                                                     in_=ps[bkv][:, 0:512].rearrange("p (t d) -> p t d", d=64), func=AF.Copy),
                         reads=[psb[bkv]], wadd=[A["Vaug"]])
                    if grp == 1:
                        pass
                else:
                    P.op(ACT, lambda e, bkv=bkv: e.activation(out=Vaug[:, 16, 0:64], in_=ps[bkv][:, 0:64], func=AF.Copy),
                         reads=[psb[bkv]], wadd=[A["Vaug"]])
                    P.op(ACT, lambda e, bkv=bkv: e.activation(out=Vaug[0:NS, 17, 0:64], in_=ps[bkv][0:NS, 64:128], func=AF.Copy),
                         reads=[psb[bkv]], wadd=[A["Vaug"]])
                    P.op(ACT, lambda e, bkv=bkv, kv=kv: e.activation(out=vlast[:, 64 * kv:64 * kv + 64], in_=ps[bkv][:, 0:64], func=AF.Copy),
                         reads=[psb[bkv]], wadd=[A["vlast"]])
                    P.op(ACT, lambda e, bkv=bkv, kv=kv: e.activation(out=vnew[0:NS, 64 * kv:64 * kv + 64], in_=ps[bkv][0:NS, 64:128], func=AF.Copy),
                         reads=[psb[bkv]], wadd=[A["vnew"]])
            qv = lambda base, b_: qT[base:base + 64, 0:2, 128 * b_:128 * b_ + 128]
            def attn_s1(b_):
                ia = (2 * b_) % 4; ib = (2 * b_ + 1) % 4
                bA, bB = bank(), bank()
                kprev = slice(128 * b_, 128 * b_ + 128); kcur = slice(128 * b_ + 128, 128 * b_ + 256)
                seq = [(bA, 0, 0, kprev), (bB, 64, 0, kprev), (bA, 0, 1, kcur), (bB, 64, 1, kcur)]
                for (bk_, base, half, ks) in seq:
                    first = (half == 0)
                    P.op(PE, lambda e, bk_=bk_, base=base, half=half, ks=ks, b_=b_: e.matmul(
                        ps[bk_][:, 256 * half:256 * half + 256], lhsT=kT2[base:base + 64, ks], rhs=qv(base, b_), start=True, stop=True),
                        reads=[A["kT2"], A["qT"]] if first else [], writes=[psb[bk_]] if first else [], sig=(half == 1))
                    if half == 1 and not P.dead:
                        P.attach(Dep(P.esem[PE], P.cnt[PE]), reads=[A["kT2"], A["qT"]], writes=[psb[bk_]])
                for (bk_, ie, base_h) in [(bA, ia, 0), (bB, ib, 1)]:
                    P.op(ACT, lambda e, bk_=bk_, ie=ie: e.activation(out=Et[ie], in_=ps[bk_][:, :], func=AF.Exp, scale=0.125),
                         reads=[psb[bk_]], writes=[b_Et[ie]])
                    Ev = Et[ie].rearrange("p (h i q) -> p h i q", h=2, i=2)
                    Pv = PT[ie].rearrange("p (h i q) -> p h i q", h=2, i=2)
                    h0 = 4 * kv + base_h
                    eng = DVE if base_h == 0 else POOL
                    if b_ > 0:
                        P.op(eng, lambda e, Ev=Ev, Pv=Pv, h0=h0: e.tensor_tensor(out=Pv, in0=Ev, in1=EB[:, :, h0:h0 + 3:2, :], op=ALU.mult),
                             reads=[b_Et[ie], A["EB"]], writes=[b_PT[ie]])
                    else:
                        P.op(eng, lambda e, Ev=Ev, Pv=Pv, h0=h0: e.tensor_tensor(out=Pv[:, 0], in0=Ev[:, 0], in1=EB0[:, h0:h0 + 3:2, :], op=ALU.mult),
                             reads=[b_Et[ie], A["EB0"]], writes=[b_PT[ie]])
                        P.op(eng, lambda e, Ev=Ev, Pv=Pv, h0=h0: e.tensor_tensor(out=Pv[:, 1], in0=Ev[:, 1], in1=EB[:, 1, h0:h0 + 3:2, :], op=ALU.mult),
                             reads=[b_Et[ie], A["EB"]], wadd=[b_PT[ie]])

            def attn_s2(b_):
                ia = (2 * b_) % 4; ib = (2 * b_ + 1) % 4
                bO = bank()
                mm = [(ia, 0, b_, True), (ia, 1, b_ + 1, False), (ib, 0, b_, False), (ib, 1, b_ + 1, False)]
                for j_, (ip, half, tile, st_) in enumerate(mm):
                    cols = slice(0, 256) if ip == ia else slice(256, 512)
                    P.op(PE, lambda e, ip=ip, half=half, tile=tile, st_=st_, cols=cols: e.matmul(
                        ps[bO][:, cols], lhsT=Vaug[:, tile, :], rhs=PT[ip][:, 256 * half:256 * half + 256], start=st_, stop=False),
                        reads=[A["Vaug"], b_PT[ia], b_PT[ib], b_ES, b_ones] if j_ == 0 else [], writes=[psb[bO]] if j_ == 0 else [], sig=False)
                P.op(PE, lambda e, kv=kv: e.matmul(ps[bO][:, 0:512], lhsT=onesd[0:1, :], rhs=ES[0:1, kv].rearrange("p i q -> p (i q)"),
                                                   start=False, stop=True), sig=True)
                if not P.dead:
                    P.attach(Dep(P.esem[PE], P.cnt[PE]), reads=[A["Vaug"], b_PT[ia], b_PT[ib], b_ES, b_ones], writes=[psb[bO]])
                ir = b_ % 2
                P.op(ACT, lambda e, ir=ir: e.activation(out=rc[ir][64:128, :], in_=ps[bO][64:128, :], func=AF.Ln), reads=[psb[bO]], writes=[b_rc[ir]])
                P.op(ACT, lambda e, ir=ir: e.activation(out=rc[ir][64:128, :], in_=rc[ir][64:128, :], func=AF.Exp, scale=-1.0),
                     reads=[b_rc[ir]], writes=[b_rc[ir]])
                for par in range(2):
                    P.op(DVE, lambda e, par=par, ir=ir, b_=b_, kv=kv: e.tensor_tensor(
                        out=oT[64 * par:64 * par + 64, 2 * kv:2 * kv + 2, 128 * b_:128 * b_ + 128],
                        in0=ps[bO][0:64, 256 * par:256 * par + 256].rearrange("p (c q) -> p c q", c=2),
                        in1=rc[ir][64:128, 256 * par:256 * par + 256].rearrange("p (c q) -> p c q", c=2), op=ALU.mult),
                        reads=[psb[bO], b_rc[ir]], wadd=[oT_b[2 * kv], oT_b[2 * kv + 1]])

            attn_s1(0)
            for b_ in range(1, 16):
                attn_s1(b_)
                attn_s2(b_ - 1)
            attn_s2(15)
            base = 64 * (kv % 2)
            for i in range(4):
                hsrc = 64 * (i % 2)
                P.op(POOL, lambda e, i=i, hsrc=hsrc, base=base: e.tensor_copy(out=QsT[base:base + 64, 0, i, :], in_=qT[hsrc:hsrc + 64, i // 2, NP:NT]),
                     reads=[A["qT"]], writes=[A["QsT"]] if i == 0 else [], wadd=[A["QsT"]] if i > 0 else [])
            for sl_, i in enumerate([0, 1, 2, 3]):
                P.op(DVE, lambda e, i=i, kv=kv: e.tensor_copy(out=esr[0:1, :].rearrange("p (s i) -> p s i", i=4)[:, :, i],
                                                            in_=es16[0:1, 4 * kv + i:4 * kv + i + 1].to_broadcast([1, 16])),
                     reads=[A["es16"]], writes=[A["esr"]] if i == 0 else [], wadd=[A["esr"]] if i > 0 else [])
            bS, bD, bN = bank(), bank(), bank()
            Qsi = QsT[base:base + 64, 0].rearrange("p i s -> p s i")
            for s_ in range(NS):
                P.op(PE, lambda e, s_=s_, base=base, kv=kv: e.matmul(ps[bS][:, 4 * s_:4 * s_ + 4], lhsT=KcT[base:base + 64, s_, kv // 2, :],
                                                                  rhs=QsT[base:base + 64, 0, :, s_], start=True, stop=True),
                     reads=[A["KcT"], A["QsT"], A["kT2"]] if s_ == 0 else [], writes=[psb[bS]] if s_ == 0 else [], sig=False)
            P.op(PE, lambda e, base=base: e.matmul(ps[bS][0:NS, 64:128], lhsT=kT2[base:base + 64, 128 + NP:128 + NT], rhs=Qsi, start=True, stop=True), sig=True)
            if not P.dead:
                P.attach(Dep(P.esem[PE], P.cnt[PE]), reads=[A["KcT"], A["QsT"], A["kT2"]], writes=[psb[bS]])
            P.op(ACT, lambda e: e.activation(out=ptS[:, 0:64], in_=ps[bS][:, 0:64], func=AF.Exp, scale=0.125), reads=[psb[bS]], writes=[A["ptS"]])
            P.op(ACT, lambda e: e.activation(out=pdg[0:NS, :], in_=ps[bS][0:NS, 64:128], func=AF.Exp, scale=0.125), reads=[psb[bS]], writes=[A["pdg"]])
            P.op(DVE, lambda e, kv=kv: e.tensor_tensor(out=ptS[:, 0:64].rearrange("p (s i) -> p s i", i=4), in0=ptS[:, 0:64].rearrange("p (s i) -> p s i", i=4),
                                                in1=EB[:, 0, 4 * kv:4 * kv + 4, 0].unsqueeze(1).to_broadcast([128, 16, 4]), op=ALU.mult),
                 reads=[A["ptS"], A["EB"]], writes=[A["ptS"]])
            P.op(DVE, lambda e, kv=kv: e.tensor_tensor(out=pdg[0:NS, :].rearrange("p (s i) -> p s i", i=4), in0=pdg[0:NS, :].rearrange("p (s i) -> p s i", i=4),
                                                in1=ebs[0:NS, 4 * kv:4 * kv + 4].unsqueeze(1).to_broadcast([NS, 16, 4]), op=ALU.mult),
                 reads=[A["pdg"], A["ebs"]], writes=[A["pdg"]])
            P.op(DVE, lambda e: e.tensor_tensor(out=pdg[0:NS, :], in0=pdg[0:NS, :], in1=dgt[0:NS, :], op=ALU.mult),
                 reads=[A["pdg"], A["dgt"]], writes=[A["pdg"]])
            P.op(POOL, lambda e, kv=kv: e.tensor_copy(out=vnb[0:NS, 64 * kv:64 * kv + 64], in_=vnew[0:NS, 64 * kv:64 * kv + 64]),
                 reads=[A["vnew"]], writes=[A["vnb"]])
            P.op(PE, lambda e: e.matmul(ps[bD][:, 0:64], lhsT=ones_k, rhs=ptS[:, 0:64], start=True, stop=False),
                 reads=[A["ones_k"], A["ptS"], A["pdg"], A["esr"]], writes=[psb[bD]], sig=False)
            P.op(PE, lambda e: e.matmul(ps[bD][:, 0:64], lhsT=ones_k[0:NS, :], rhs=pdg[0:NS, :], start=False, stop=False), sig=False)
            P.op(PE, lambda e: e.matmul(ps[bD][:, 0:64], lhsT=ones_k[0:1, :], rhs=esr[0:1, :], start=False, stop=True), sig=True)
            if not P.dead:
                P.attach(Dep(P.esem[PE], P.cnt[PE]), reads=[A["ones_k"], A["ptS"], A["pdg"], A["esr"]], writes=[psb[bD]])
            ptv = ptS[:, 0:64].rearrange("p (s i) -> p s i", i=4)
            pdv = pdg[0:NS, :].rearrange("p (s i) -> p s i", i=4)
            psn = ps[bN][:, 0:64].rearrange("p (s i) -> p s i", i=4)
            for par in range(2):
                for s_ in range(NS):
                    P.op(PE, lambda e, par=par, s_=s_, kv=kv: e.matmul(
                        psn[64 * par:64 * par + 64, s_, par:4:2], lhsT=Vcs[:, s_, 64 * kv:64 * kv + 64], rhs=ptv[:, s_, par:4:2],
                        start=(s_ == 0), stop=False, tile_position=(0, 64 * par)),
                        reads=[A["Vcs"], A["ptS"], A["pdg"], A["vnb"]] if (par == 0 and s_ == 0) else [],
                        writes=[psb[bN]] if (par == 0 and s_ == 0) else [], sig=False)
                P.op(PE, lambda e, par=par, kv=kv: e.matmul(
                    psn[64 * par:64 * par + 64, :, par:4:2], lhsT=vnb[0:NS, 64 * kv:64 * kv + 64], rhs=pdv[:, :, par:4:2],
                    start=False, stop=True, tile_position=(0, 64 * par)), sig=(par == 1))
            if not P.dead:
                P.attach(Dep(P.esem[PE], P.cnt[PE]), reads=[A["Vcs"], A["ptS"], A["pdg"], A["vnb"]], writes=[psb[bN]])
            P.op(DVE, lambda e: e.reciprocal(out=rcs[:, 0:64], in_=ps[bD][:, 0:64]), reads=[psb[bD]], writes=[A["rcs"]])
            rcv = rcs[:, 0:64].rearrange("p (s i) -> p s i", i=4)
            for par in range(2):
                P.op(DVE, lambda e, par=par, kv=kv: e.tensor_tensor(
                    out=oT[64 * par:64 * par + 64, 2 * kv:2 * kv + 2, NP:NT],
                    in0=psn[64 * par:64 * par + 64, :, par:4:2].rearrange("p s c -> p c s"),
                    in1=rcv[64 * par:64 * par + 64, :, par:4:2].rearrange("p s c -> p c s"), op=ALU.mult),
                    reads=[psb[bN], A["rcs"]], wadd=[oT_b[2 * kv], oT_b[2 * kv + 1]])
        for kv in range(4):
            attn_kv(kv)
        P.dma(SP, osem_new("pck"), pck, klast, reads=[A["klast"]])
        P.dma(SP, osem_new("pcv"), pcv, vlast, reads=[A["vlast"]])
        P.dma(SP, osem_new("sck"), sck[:, 127, :], knew[0:NS, :], reads=[A["knew"]])
        P.dma(SP, osem_new("scv"), scv[:, 127, :], vnew[0:NS, :], reads=[A["vnew"]])

        P.phase(13)
        sgaT = cv_(O_C + 0, [8, NT], BF16)
        sga_b = [Buf(f"sga{g}") for g in range(8)]
        gt2 = [cv_(O_C + 33024 + 1024 * i, [512], BF16) for i in range(4)]
        ft2 = [cv_(O_C + 37120 + 2048 * i, [512], F32) for i in range(2)]
        b_gt2 = [Buf() for _ in range(4)]; b_ft2 = [Buf(), Buf()]
        handoff(sga_b + b_gt2 + b_ft2, list(A.values()) + b_Et + b_PT + b_rc)
        for blk in range(2):
            def ev_za(oc, tbi, pap, pb, blk=blk):
                t0, n = TBS[tbi]
                g = 4 * blk + oc
                ctr["i"] += 1
                gi = ctr["i"] % 4; fi = ctr["i"] % 2
                P.op(ACT, lambda e: e.activation(out=gt2[gi][:, 0:n], in_=pap, func=AF.Sigmoid), reads=[pb], writes=[b_gt2[gi]])
                P.op(DVE, lambda e: e.tensor_tensor(out=ft2[fi][:, 0:n], in0=pap, in1=gt2[gi][:, 0:n], op=ALU.mult),
                     reads=[pb, b_gt2[gi]], writes=[b_ft2[fi]])
                P.op(POOL, lambda e: e.tensor_tensor(out=oT[:, g, t0:t0 + n], in0=oT[:, g, t0:t0 + n], in1=ft2[fi][:, 0:n], op=ALU.mult),
                     reads=[b_ft2[fi], oT_b[g]], wadd=[oT_b[g]])
            proj_fm(w_in[:, OFF_ZA + 512 * blk:OFF_ZA + 512 * blk + 512], 512, rhs_h, hT_b, ev_za)
        for blk in range(2):
            def ev_ga(oc, tbi, pap, pb, blk=blk):
                t0, n = TBS[tbi]
                g = 4 * blk + oc
                P.op(ACT, lambda e: e.activation(out=sgaT[:, g, t0:t0 + n], in_=pap, func=AF.Sigmoid), reads=[pb], wadd=[sga_b[g]])
            proj_fm(w_in[:, OFF_GA + 512 * blk:OFF_GA + 512 * blk + 512], 512, rhs_h, hT_b, ev_ga)
        mT = RA
        mT_b = gbs_b
        rhs_o = lambda k, t0, n: oT[:, k, t0:t0 + n]
        for blk in range(2):
            def ev_ba(oc, tbi, pap, pb, blk=blk):
                t0, n = TBS[tbi]
                g = 4 * blk + oc
                ctr["i"] += 1
                fi = ctr["i"] % 2
                P.op(DVE, lambda e: e.tensor_tensor(out=ft2[fi][:, 0:n], in0=pap, in1=sgaT[:, g, t0:t0 + n], op=ALU.mult),
                     reads=[pb, sga_b[g]], writes=[b_ft2[fi]])
                P.op(POOL, lambda e: e.tensor_tensor(out=mT[:, g, t0:t0 + n], in0=mT[:, g, t0:t0 + n], in1=ft2[fi][:, 0:n], op=ALU.add),
                     reads=[b_ft2[fi], mT_b[g]], wadd=[mT_b[g]])
            proj_fm(w_ba[:, 512 * blk:512 * blk + 512], 512, rhs_o, [BufGroup(oT_b)] * 5, ev_ba)

        P.phase(14)
        o2 = O_C + 8000
        NSL = 4
        GateB = cv_(o2 + 0, [1024], F32); LnG = cv_(o2 + 4096, [1024], F32); LnB = cv_(o2 + 8192, [1024], F32)
        gateS = cv_(o2 + 12288, [1024], F32); grow = cv_(o2 + 16384, [1024], F32)
        xt = [cv_(o2 + 20480 + 4096 * i, [1024], F32) for i in range(NSL)]
        rt = [cv_(o2 + 36864 + 4096 * i, [1024], F32) for i in range(NSL)]
        stt = cv_(o2 + 53248, [NSL, 2, 6], F32); mvt = cv_(o2 + 53504, [NSL, 2], F32); rsd = cv_(o2 + 53568, [NSL, 2], F32)
        mhalf = cv_(o2 + 53632, [1], F32)
        b_GateB = Buf(); b_LnG = Buf(); b_LnB = Buf(); b_gateS = Buf(); b_grow = Buf(); b_xt = [Buf() for _ in range(NSL)]; b_rt = [Buf() for _ in range(NSL)]
        b_stt = [Buf() for _ in range(NSL)]; b_mh = Buf()
        xsem2 = [P.dsem(f"xt{i}") for i in range(NSL)]; osem = [P.dsem(f"o{i}") for i in range(NSL)]
        handoff([b_GateB, b_LnG, b_LnB, b_gateS, b_grow] + b_xt + b_rt + b_stt + [b_mh], list(A.values()) + b_Et + b_PT + b_rc + sga_b + b_gt2 + b_ft2)
        misc_load(SP, LnG, ln_g.rearrange("(o n) -> o n", o=1).to_broadcast([128, 1024]), b_LnG)
        misc_load(SP, LnB, ln_b.rearrange("(o n) -> o n", o=1).to_broadcast([128, 1024]), b_LnB)
        P.op(POOL, lambda e: e.memset(mhalf, -0.5), writes=[b_mh])
        for hb in range(2):
            bkg = bank(); bkg2 = bank()
            for kk in range(4):
                k = 4 * hb + kk
                P.op(PE, lambda e, k=k, kk=kk, bkg=bkg: e.transpose(out=ps[bkg][0:1, 128 * kk:128 * kk + 128], in_=modT[:, 16 + k, 0:1], identity=ident),
                     reads=[b_modT, b_ident], writes=[psb[bkg]] if kk == 0 else [], wadd=[psb[bkg]] if kk > 0 else [])
                P.op(PE, lambda e, k=k, kk=kk, bkg2=bkg2: e.transpose(out=ps[bkg2][0:NS, 128 * kk:128 * kk + 128], in_=modT[:, 16 + k, 1:17], identity=ident),
                     reads=[b_modT, b_ident], writes=[psb[bkg2]] if kk == 0 else [], wadd=[psb[bkg2]] if kk > 0 else [])
            P.op(DVE, lambda e, hb=hb, bkg=bkg: e.tensor_copy(out=grow[0:1, 512 * hb:512 * hb + 512], in_=ps[bkg][0:1, 0:512]), reads=[psb[bkg]], wadd=[b_grow])
            P.op(DVE, lambda e, hb=hb, bkg2=bkg2: e.tensor_copy(out=gateS[0:NS, 512 * hb:512 * hb + 512], in_=ps[bkg2][0:NS, 0:512]), reads=[psb[bkg2]], wadd=[b_gateS])
        for hb in range(2):
            bkb = bank()
            P.op(PE, lambda e, hb=hb, bkb=bkb: e.matmul(ps[bkb][:, 0:512], lhsT=ones1[0:1, :], rhs=grow[0:1, 512 * hb:512 * hb + 512], start=True, stop=True),
                 reads=[b_grow, b_ones], writes=[psb[bkb]])
            P.op(DVE, lambda e, hb=hb, bkb=bkb: e.tensor_copy(out=GateB[:, 512 * hb:512 * hb + 512], in_=ps[bkb][:, 0:512]), reads=[psb[bkb]], wadd=[b_GateB])
        so = [load_w(w_out[:, 0:512], 512), load_w(w_out[:, 512:1024], 512)]
        def x_load(ti_):
            rows_, c0_ = (128, 128 * ti_) if ti_ < 16 else (NS, NP)
            src_ = xp[c0_:c0_ + 128, :] if ti_ < 16 else xs
            P.dma(SP, xsem2[ti_ % NSL], xt[ti_ % NSL][0:rows_, :], src_, writes=[b_xt[ti_ % NSL]])
        for ti_ in range(NSL):
            x_load(ti_)
        for tt_i in range(17):
            rows, c0 = (128, 128 * tt_i) if tt_i < 16 else (NS, NP)
            sl = tt_i % NSL
            gate_ap = GateB if tt_i < 16 else gateS
            gate_b = b_GateB if tt_i < 16 else b_gateS
            for fb in range(2):
                bko = bank()
                for k in range(8):
                    P.op(PE, lambda e, bko=bko, k=k, fb=fb, rows=rows, c0=c0: e.matmul(
                        ps[bko][0:rows, 0:512], lhsT=mT[:, k, c0:c0 + rows], rhs=wslot[so[fb]][:, k, 0:512], start=(k == 0), stop=(k == 7)),
                        reads=[wslot_b[so[fb]]] + mT_b if k == 0 else [], writes=[psb[bko]] if k == 0 else [], sig=(k == 7))
                if not P.dead:
                    P.attach(Dep(P.esem[PE], P.cnt[PE]), reads=[wslot_b[so[fb]]] + mT_b, writes=[psb[bko]])
                P.op(DVE, lambda e, bko=bko, fb=fb, rows=rows, sl=sl, gate_ap=gate_ap: e.tensor_tensor(
                    out=rt[sl][0:rows, 512 * fb:512 * fb + 512], in0=ps[bko][0:rows, 0:512], in1=gate_ap[0:rows, 512 * fb:512 * fb + 512], op=ALU.mult),
                    reads=[psb[bko], gate_b], writes=[b_rt[sl]] if fb == 0 else [], wadd=[b_rt[sl]] if fb == 1 else [])
            P.op(DVE, lambda e, rows=rows, sl=sl: e.scalar_tensor_tensor(out=rt[sl][0:rows, :], in0=xt[sl][0:rows, :], scalar=float(ALPHA),
                                                                         in1=rt[sl][0:rows, :], op0=ALU.mult, op1=ALU.add),
                 reads=[b_xt[sl], b_rt[sl]], writes=[b_rt[sl]])
            for hf in range(2):
                P.op(DVE, lambda e, rows=rows, sl=sl, hf=hf: e.bn_stats(out=stt[0:rows, sl, hf, :], in_=rt[sl][0:rows, 512 * hf:512 * hf + 512]),
                     reads=[b_rt[sl]], writes=[b_stt[sl]] if hf == 0 else [], wadd=[b_stt[sl]] if hf == 1 else [])
            P.op(DVE, lambda e, rows=rows, sl=sl: e.bn_aggr(out=mvt[0:rows, sl, :], in_=stt[0:rows, sl].rearrange("p a b -> p (a b)")),
                 reads=[b_stt[sl]], writes=[b_stt[sl]])
            P.op(POOL, lambda e, rows=rows, sl=sl: e.tensor_scalar(out=rsd[0:rows, sl, 0:1], in0=mvt[0:rows, sl, 1:2], scalar1=float(LN_EPS), scalar2=0.0,
                                                                   op0=ALU.add, op1=ALU.add), reads=[b_stt[sl]], writes=[b_stt[sl]])
            P.op(POOL, lambda e, rows=rows, sl=sl: e.tensor_tensor(out=rsd[0:rows, sl, 0:1], in0=rsd[0:rows, sl, 0:1], in1=mhalf[0:rows, :], op=ALU.pow),
                 reads=[b_stt[sl], b_mh], writes=[b_stt[sl]])
            P.op(POOL, lambda e, rows=rows, sl=sl: e.tensor_tensor(out=rsd[0:rows, sl, 1:2], in0=mvt[0:rows, sl, 0:1], in1=rsd[0:rows, sl, 0:1], op=ALU.mult),
                 reads=[b_stt[sl]], writes=[b_stt[sl]])
            P.op(POOL, lambda e, rows=rows, sl=sl: e.tensor_scalar(out=rsd[0:rows, sl, 1:2], in0=rsd[0:rows, sl, 1:2], scalar1=-1.0, scalar2=0.0,
                                                                   op0=ALU.mult, op1=ALU.add), reads=[b_stt[sl]], writes=[b_stt[sl]])
            P.op(ACT, lambda e, rows=rows, sl=sl: e.activation(out=xt[sl][0:rows, :], in_=rt[sl][0:rows, :], func=AF.Identity,
                                                               scale=rsd[0:rows, sl, 0:1], bias=rsd[0:rows, sl, 1:2]),
                 reads=[b_rt[sl], b_stt[sl]], writes=[b_xt[sl]])
            P.op(DVE, lambda e, rows=rows, sl=sl: e.tensor_tensor(out=xt[sl][0:rows, :], in0=xt[sl][0:rows, :], in1=LnG[0:rows, :], op=ALU.mult),
                 reads=[b_xt[sl], b_LnG], writes=[b_xt[sl]])
            P.op(POOL, lambda e, rows=rows, sl=sl: e.tensor_tensor(out=xt[sl][0:rows, :], in0=xt[sl][0:rows, :], in1=LnB[0:rows, :], op=ALU.add),
                 reads=[b_xt[sl], b_LnB], writes=[b_xt[sl]])
            dst = yp[c0:c0 + 128, :] if tt_i < 16 else ys
            P.dma(SP, osem[sl], dst, xt[sl][0:rows, :], reads=[b_xt[sl]])
            if tt_i + NSL < 17:
                x_load(tt_i + NSL)
        final_deps = [Dep(o_.h, o_.cnt) for o_ in osem]

        P.dead = False
        if DEBUG:
            pass
        P.wait(SP, [Dep(dout_sem.h, dout_sem.cnt)] + final_deps + [Dep(d_.h, d_.cnt) for d_ in out_sems])
        P.emit()
        print("instruction counts:", P.ninst)
    return nc


def _bucket_np(dist):
    max_exact = 16
    df = np.maximum(dist, 1).astype(np.float32)
    large = max_exact + (np.log(df / np.float32(max_exact)) / np.float32(math.log(128 / max_exact)) * np.float32(16)).astype(np.int32)
    large = np.minimum(large, 31)
    return np.where(dist < max_exact, dist, large)


def _host_consts():
    R = np.zeros((32, 384), np.float32)
    for i in range(384):
        dist = 255 - i
        if 0 <= dist <= 128:
            R[int(_bucket_np(np.array([dist]))[0]), i] = 1.0
    j = np.arange(128)[:, None]
    q = np.arange(128)[None, :]
    mask = np.concatenate([(j >= q), (j <= q)], axis=1).astype(np.float32)
    r = np.arange(128)
    bmask = (r[:, None] // 32 == r[None, :] // 32).astype(np.float32)
    diag = np.zeros((16, 16, 4), np.float32)
    for s_ in range(16):
        diag[s_, s_, :] = 1.0
    return R, mask, bmask, diag.reshape(16, 64)


_NC_CACHE = {}


def kernel(x_prompt, x_sample, c_prompt, c_sample, state_ssm_re, state_ssm_im, cache_swa_k, cache_swa_v,
           w_ada, b_ada, w_in, ssm_lambda_re, ssm_lambda_im, ssm_log_delta, ssm_b_re, ssm_b_im,
           ssm_c_re, ssm_c_im, ssm_d, w_glu, b_glu, attn_sinks, rel_bias, w_branch_s, w_branch_a,
           w_out, ln_g, ln_b):
    f = lambda a: np.ascontiguousarray(np.asarray(a, dtype=np.float32))
    x_prompt = f(x_prompt); x_sample = f(x_sample); c_prompt = f(c_prompt); c_sample = f(c_sample)
    R, mask, bmask, diag = _host_consts()
    shared = {
        "w_ada": f(w_ada)[0], "b_ada": f(b_ada)[0], "w_in": f(w_in)[0],
        "lam_re": f(ssm_lambda_re)[0], "lam_im": f(ssm_lambda_im)[0], "log_delta": f(ssm_log_delta)[0],
        "b_re": f(ssm_b_re)[0].reshape(4096, 16), "b_im": f(ssm_b_im)[0].reshape(4096, 16),
        "c_re": f(ssm_c_re)[0].reshape(1024, 64), "c_im": f(ssm_c_im)[0].reshape(1024, 64),
        "ssm_d": f(ssm_d)[0], "w_glu": f(w_glu)[0], "b_glu": f(b_glu)[0], "sinks": f(attn_sinks)[0],
        "rel_bias": f(rel_bias), "w_bs": f(w_branch_s)[0], "w_ba": f(w_branch_a)[0], "w_out": f(w_out)[0],
        "ln_g": f(ln_g)[0], "ln_b": f(ln_b)[0],
        "rtab": R, "maskc": mask, "bmaskc": bmask, "diagc": diag,
    }
    sre = f(state_ssm_re)[0].reshape(128, 4096); sim = f(state_ssm_im)[0].reshape(128, 4096)
    ckk = f(cache_swa_k)[0].reshape(128, 128, 256); cvv = f(cache_swa_v)[0].reshape(128, 128, 256)
    in_maps = []
    for c in range(NCORES):
        b, qr = c // 4, c % 4
        t0 = NP * qr
        xh = x_prompt[b, t0 - 128:t0] if qr > 0 else np.zeros((128, D), np.float32)
        flags = np.zeros(32, np.float32)
        flags[0] = 1.0 if qr > 0 else 0.0
        xprev = np.zeros((3, NP, D), np.float32)
        for j in range(3):
            qq = qr - 1 - j
            if qq >= 0:
                flags[1 + j] = 1.0
                xprev[j] = x_prompt[b, NP * qq:NP * qq + NP]
        m = dict(shared)
        m.update({
            "xprev": xprev, "xp": np.ascontiguousarray(x_prompt[b, t0:t0 + NP]), "xh": np.ascontiguousarray(xh),
            "xs": np.ascontiguousarray(x_sample[NS * c:NS * c + NS, 0]),
            "cc": np.ascontiguousarray(np.concatenate([c_prompt[b:b + 1], c_sample[NS * c:NS * c + NS]], 0)),
            "st_re": np.ascontiguousarray(sre[NS * c:NS * c + NS]), "st_im": np.ascontiguousarray(sim[NS * c:NS * c + NS]),
            "ck": np.ascontiguousarray(ckk[NS * c:NS * c + NS]), "cv": np.ascontiguousarray(cvv[NS * c:NS * c + NS]),
            "flags": flags,
        })
        in_maps.append(m)
    nc = build()
    res = run_bass_kernel_spmd(nc, in_maps, core_ids=list(range(NCORES)))
    R_ = res.results
    kernel.last_results = R_
    y_prompt = np.stack([np.concatenate([R_[4 * b + q]["yp"] for q in range(4)], 0) for b in range(2)], 0)
    y_sample = np.concatenate([R_[c]["ys"] for c in range(NCORES)], 0).reshape(128, 1, D)
    p_hr = np.stack([R_[4 * b + 3]["pst_re"].reshape(64, 64) for b in range(2)], 0)[None]
    p_hi = np.stack([R_[4 * b + 3]["pst_im"].reshape(64, 64) for b in range(2)], 0)[None]
    p_k = np.stack([R_[4 * b + 3]["pck"].reshape(128, 4, 64) for b in range(2)], 0)[None]
    p_v = np.stack([R_[4 * b + 3]["pcv"].reshape(128, 4, 64) for b in range(2)], 0)[None]
    s_hr = np.concatenate([R_[c]["sst_re"] for c in range(NCORES)], 0).reshape(1, 128, 64, 64)
    s_hi = np.concatenate([R_[c]["sst_im"] for c in range(NCORES)], 0).reshape(1, 128, 64, 64)
    s_k = np.concatenate([R_[c]["sck"] for c in range(NCORES)], 0).reshape(1, 128, 128, 4, 64)
    s_v = np.concatenate([R_[c]["scv"] for c in range(NCORES)], 0).reshape(1, 128, 128, 4, 64)
    return (y_prompt.astype(np.float32), y_sample.astype(np.float32), p_hr.astype(np.float32), p_hi.astype(np.float32),
            p_k.astype(np.float32), p_v.astype(np.float32), s_hr.astype(np.float32), s_hi.astype(np.float32),
            s_k.astype(np.float32), s_v.astype(np.float32))
```

```python
import math
import os
from contextlib import ExitStack
import numpy as np
import ml_dtypes
import concourse.bass as bass
import concourse.mybir as mybir
from concourse.bass_utils import run_bass_kernel_spmd

F32 = mybir.dt.float32
BF16 = mybir.dt.bfloat16
U8 = mybir.dt.uint8
ALU = mybir.AluOpType
AF = mybir.ActivationFunctionType
AX = mybir.AxisListType

PE, ACT, DVE, POOL, SP = "tensor", "scalar", "vector", "gpsimd", "sync"
ENGS = [PE, ACT, DVE, POOL, SP]

NCORES = 8
D = 1024
NP = 2048
NS = 16
NT = NP + NS
TBS = [(0, 512), (512, 512), (1024, 512), (1536, 512), (2048, 16)]
DIN = 6656
OFF_U, OFF_ZS, OFF_Q, OFF_K, OFF_V, OFF_ZA, OFF_GS, OFF_GA = 0, 1024, 2048, 3072, 3328, 3584, 4608, 5632
ALPHA = 2.0 ** 0.25
LN_EPS = 1e-5
DEBUG = False


class Dep:
    __slots__ = ("sem", "val")

    def __init__(self, sem, val):
        self.sem = sem
        self.val = val


class Buf:
    __slots__ = ("w", "r", "name")

    def __init__(self, name=""):
        self.w = []
        self.r = []
        self.name = name


class _RProxy:
    def __init__(self, bufs):
        self.bufs = bufs

    def append(self, h):
        for b in self.bufs:
            b.r.append(h)

    def __len__(self):
        return 0


class BufGroup:
    def __init__(self, bufs):
        self.bufs = list(bufs)
        self.r = _RProxy(self.bufs)

    @property
    def w(self):
        return [h for b in self.bufs for h in b.w]


def handoff(new_bufs, old_bufs):
    deps = []
    for b in old_bufs:
        deps.extend(b.w)
        deps.extend(b.r)
    for nb in new_bufs:
        nb.r = list(nb.r) + deps


class DSem:
    def __init__(self, h):
        self.h = h
        self.cnt = 0


class Prog:
    def __init__(self, nc, stack):
        self.nc = nc
        self.q = {e: [] for e in ENGS}
        self.esem = {}
        self.cnt = {e: 0 for e in ENGS}
        self.allsems = []
        for e in [PE, ACT, DVE, POOL]:
            self.esem[e] = nc.alloc_semaphore("s_" + e)
            self.allsems.append(self.esem[e])
        self.seen = {}
        self.stack = stack
        self.nd = 0
        self.ninst = {e: 0 for e in ENGS}
        self.dead = False
        self.stop = int(os.environ.get("KSTOP", "99"))

    def phase(self, n):
        self.dead = n > self.stop

    def dsem(self, name=None):
        self.nd += 1
        h = self.nc.alloc_semaphore(f"d{self.nd}_{name or 'm'}")
        self.allsems.append(h)
        return DSem(h)

    def _waits(self, eng, deps):
        best = {}
        for d in deps:
            if d is None:
                continue
            k = id(d.sem)
            if k not in best or best[k].val < d.val:
                best[k] = d
        ws = []
        for d in best.values():
            k = (eng, id(d.sem))
            if self.seen.get(k, 0) >= d.val:
                continue
            self.seen[k] = d.val
            ws.append((d.sem, d.val))
        return ws

    @staticmethod
    def _compact(lst):
        best = {}
        for d in lst:
            k = id(d.sem)
            if k not in best or best[k].val < d.val:
                best[k] = d
        return list(best.values())

    @staticmethod
    def _bufdeps(reads, writes, wadd=()):
        deps = []
        for b in reads:
            deps.extend(b.w)
        for b in writes:
            deps.extend(b.w)
            deps.extend(b.r)
        for b in wadd:
            deps.extend(b.r)
        return deps

    @classmethod
    def _update(cls, h, reads, writes, wadd=()):
        for b in reads:
            b.r.append(h)
            if len(b.r) > 32:
                b.r = cls._compact(b.r)
        for b in writes:
            b.w = [h]
            b.r = []
        for b in wadd:
            b.w.append(h)
            if len(b.w) > 32:
                b.w = cls._compact(b.w)

    def op(self, eng, fn, reads=(), writes=(), deps=(), sig=True, wadd=()):
        if self.dead:
            return None
        alld = list(deps) + self._bufdeps(reads, writes, wadd)
        ws = self._waits(eng, alld)
        h = None
        if sig:
            self.cnt[eng] += 1
            h = Dep(self.esem[eng], self.cnt[eng])
        sem = self.esem[eng] if sig else None
        self.ninst[eng] += 1 + len(ws)

        def run(e, ws=ws, fn=fn, sem=sem):
            for (s, v) in ws:
                e.wait_ge(s, v)
            ins = fn(e)
            if sem is not None:
                ins.then_inc(sem, 1)
        self.q[eng].append(run)
        if h is not None:
            self._update(h, reads, writes, wadd)
        return h

    def attach(self, h, reads=(), writes=(), wadd=()):
        if self.dead or h is None:
            return
        self._update(h, reads, writes, wadd)

    def dma(self, eng, ds, out, in_, reads=(), writes=(), deps=(), wadd=(), **kw):
        if self.dead:
            return None
        alld = list(deps) + self._bufdeps(reads, writes, wadd)
        ws = self._waits(eng, alld)
        ds.cnt += 16
        h = Dep(ds.h, ds.cnt)
        self.ninst[eng] += 1 + len(ws)

        def run(e, ws=ws, out=out, in_=in_, kw=kw, sh=ds.h):
            for (s, v) in ws:
                e.wait_ge(s, v)
            e.dma_start(out=out, in_=in_, **kw).then_inc(sh, 16)
        self.q[eng].append(run)
        self._update(h, reads, writes, wadd)
        return h

    def raw(self, eng, fn, ds, inc, reads=(), writes=(), deps=()):
        if self.dead:
            return None
        alld = list(deps) + self._bufdeps(reads, writes)
        ws = self._waits(eng, alld)
        ds.cnt += inc
        h = Dep(ds.h, ds.cnt)

        def run(e, ws=ws, fn=fn, sh=ds.h, inc=inc):
            for (s, v) in ws:
                e.wait_ge(s, v)
            fn(e).then_inc(sh, inc)
        self.q[eng].append(run)
        self._update(h, reads, writes)
        return h

    def wait(self, eng, deps):
        ws = self._waits(eng, deps)

        def run(e, ws=ws):
            for (s, v) in ws:
                e.wait_ge(s, v)
        self.q[eng].append(run)

    def emit(self):
        nc = self.nc
        with nc.Block() as block:
            @block.tensor
            def _(e):
                for f in self.q[PE]:
                    f(e)

            @block.scalar
            def _(e):
                for f in self.q[ACT]:
                    f(e)

            @block.vector
            def _(e):
                for f in self.q[DVE]:
                    f(e)

            @block.gpsimd
            def _(e):
                for f in self.q[POOL]:
                    f(e)

            @block.sync
            def _(e):
                for f in self.q[SP]:
                    f(e)


def _dsize(dt):
    return {F32: 4, BF16: 2, U8: 1}[dt]


class Arena:
    def __init__(self, nc, stack, nbytes):
        self.t = stack.enter_context(nc.sbuf_tensor("arena", [128, nbytes], U8))
        self.nbytes = nbytes

    def carve(self, off, shape, dt):
        n = int(np.prod(shape)) * _dsize(dt)
        assert off % 4 == 0 and off + n <= self.nbytes, (off, n, self.nbytes)
        v = self.t[:, off:off + n]
        if dt != U8:
            v = v.bitcast(dt)
        if len(shape) > 1:
            names = [f"a{i}" for i in range(len(shape))]
            pat = "p (" + " ".join(names) + ") -> p " + " ".join(names)
            v = v.rearrange(pat, **{names[i]: shape[i] for i in range(len(shape))})
        return v


O_HT = 0
O_HTH = 33024
O_CONST = 35072
O_W = 45312
O_A = 61696
O_B = 94720
O_C = 127744
ARENA = 212000
C_SIZE = ARENA - O_C


def build():
    nc = bass.Bass("TRN2", target_bir_lowering=False)

    def din(name, shape, dt=F32):
        return nc.dram_tensor(name, list(shape), dt, kind="ExternalInput").ap()

    def dout(name, shape, dt=F32):
        return nc.dram_tensor(name, list(shape), dt, kind="ExternalOutput").ap()

    xprev = din("xprev", [3, NP, D]); xp = din("xp", [NP, D]); xh = din("xh", [128, D]); xs = din("xs", [NS, D]); ccin = din("cc", [17, D])
    st_re = din("st_re", [NS, 4096]); st_im = din("st_im", [NS, 4096])
    ck = din("ck", [NS, 128, 256]); cv = din("cv", [NS, 128, 256])
    w_ada = din("w_ada", [D, 3072]); b_ada = din("b_ada", [3072]); w_in = din("w_in", [D, DIN])
    lam_re = din("lam_re", [64, 64]); lam_im = din("lam_im", [64, 64]); log_delta = din("log_delta", [64])
    b_re = din("b_re", [4096, 16]); b_im = din("b_im", [4096, 16])
    c_re = din("c_re", [1024, 64]); c_im = din("c_im", [1024, 64])
    ssm_d = din("ssm_d", [1024]); w_glu = din("w_glu", [D, D]); b_glu = din("b_glu", [D])
    sinks = din("sinks", [16]); rel_bias = din("rel_bias", [32, 16])
    w_bs = din("w_bs", [D, D]); w_ba = din("w_ba", [D, D]); w_out = din("w_out", [D, D])
    ln_g = din("ln_g", [D]); ln_b = din("ln_b", [D])
    rtab = din("rtab", [32, 384]); maskc = din("maskc", [128, 256]); bmaskc = din("bmaskc", [128, 128])
    diagc = din("diagc", [16, 64]); flagsc = din("flags", [32])

    yp = dout("yp", [NP, D]); ys = dout("ys", [NS, D])
    pst_re = dout("pst_re", [32, 128]); pst_im = dout("pst_im", [32, 128])
    pck = dout("pck", [128, 256]); pcv = dout("pcv", [128, 256])
    sst_re = dout("sst_re", [NS, 4096]); sst_im = dout("sst_im", [NS, 4096])
    sck = dout("sck", [NS, 128, 256]); scv = dout("scv", [NS, 128, 256])
    dbg = {}
    if DEBUG:
        dbg["hT"] = dout("dbg_hT", [128, 8, NT], BF16)
        dbg["uT"] = dout("dbg_uT", [128, 8, NT], BF16)
        dbg["pw"] = dout("dbg_pw", [128, 9 * 2 * 32])
        dbg["hend"] = dout("dbg_hend", [128, 2 * 32 * 16])
        dbg["bb"] = dout("dbg_bb", [128, 2 * 32 * 16]); dbg["ccm"] = dout("dbg_ccm", [128, 2 * 32 * 16])
        dbg["R8"] = dout("dbg_R8", [128, 16 * 2 * 32]); dbg["R128"] = dout("dbg_R128", [128, 16 * 2 * 32])
        dbg["A2k"] = dout("dbg_A2k", [128, 3 * 2 * 32])
        dbg["WinL"] = dout("dbg_WinL", [128, 8 * 8 * 2 * 128], BF16)
        dbg["X0"] = dout("dbg_X0", [128, 512])
        dbg["KL"] = dout("dbg_KL", [128, 2 * 8 * 128], BF16); dbg["Ca"] = dout("dbg_Ca", [128, 2 * 4 * 9 * 2 * 32], BF16)
        dbg["Hb"] = dout("dbg_Hb", [128, 2 * 2048], BF16); dbg["Xs"] = dout("dbg_Xs", [128, 2 * 2048])
        dbg["carry"] = dout("dbg_carry", [128, 17 * 2 * 32])
        dbg["yT"] = dout("dbg_yT", [128, 8, NT], BF16)
        dbg["gbs"] = dout("dbg_gbs", [128, 8, NT], BF16)
        dbg["oT"] = dout("dbg_oT", [128, 8, NT], BF16)
        dbg["mT"] = dout("dbg_mT", [128, 8, NT], BF16)
        dbg["modT"] = dout("dbg_modT", [128, 24 * 17])

    ib = nc.dram_tensor("cc_ib", [128, 64], F32, kind="Internal")
    ob = nc.dram_tensor("cc_ob", [NCORES * 128, 64], F32, kind="Internal")

    st = ExitStack()
    with st:
        P = Prog(nc, st)
        AR = Arena(nc, st, ARENA)
        cv_ = AR.carve
        ps = [st.enter_context(nc.psum_tensor(f"ps{i}", [128, 512], F32)) for i in range(8)]
        psb = [Buf(f"ps{i}") for i in range(8)]
        dout_sem = P.dsem("dout")
        out_sems = []

        def osem_new(name):
            d_ = P.dsem(name)
            out_sems.append(d_)
            return d_
        misc_sem = P.dsem("misc")

        def misc_load(eng, out, in_, buf, wadd=False, **kw):
            if P.dead:
                return None
            if wadd:
                return P.dma(eng, P.dsem(), out, in_, wadd=[buf], **kw)
            return P.dma(eng, P.dsem(), out, in_, writes=[buf], **kw)

        hT = cv_(O_HT, [8, NT], BF16)
        hTh = cv_(O_HTH, [8, 128], BF16)
        hT_b = [Buf(f"hT{i}") for i in range(len(TBS))]
        hTh_b = Buf("hTh")
        o = O_CONST
        ident = cv_(o, [128], F32); o += 512
        modT = cv_(o, [24, 17], F32); o += 1664
        op1p = cv_(o, [8, 17], F32); o += 576
        flags = cv_(o, [32], F32); o += 128
        Dm = cv_(o, [8], F32); o += 32
        bglu = cv_(o, [8], F32); o += 32
        ES = cv_(o, [4, 4, 128], BF16); o += 4096
        onesd = cv_(o, [128], BF16); o += 256
        EBself = cv_(o, [16], F32); o += 64
        bmask = cv_(o, [128], F32); o += 512
        ones1 = cv_(o, [128], F32); o += 512
        assert o <= O_CONST + 10240
        b_ident = Buf(); b_modT = Buf(); b_flags = Buf(); b_Dm = Buf(); b_bglu = Buf(); b_ES = Buf()
        b_ones = Buf(); b_EBself = Buf(); b_bmask = Buf()
        wslot = [cv_(O_W + 8192 * i, [8, 512], BF16) for i in range(2)]
        wslot_b = [Buf("w0"), Buf("w1")]
        wsem = [P.dsem("w0"), P.dsem("w1")]
        wctr = [0]
        RA = cv_(O_A, [8, NT], BF16)
        RB = cv_(O_B, [8, NT], BF16)

        rr = {"i": 0}

        def bank():
            i = rr["i"] % 8
            rr["i"] += 1
            return i

        def load_w(src2d, ncols):
            s = wctr[0] % 2
            wctr[0] += 1
            P.dma(POOL, wsem[s], wslot[s][:, :, 0:ncols], src2d.rearrange("(k p) f -> p k f", p=128),
                  writes=[wslot_b[s]])
            return s

        def evac_copy(i, out_ap, in_ap, reads, writes, wadd=()):
            if i % 2 == 0:
                return P.op(ACT, lambda e: e.activation(out=out_ap, in_=in_ap, func=AF.Copy), reads=reads, writes=writes, wadd=wadd)
            return P.op(DVE, lambda e: e.tensor_copy(out=out_ap, in_=in_ap), reads=reads, writes=writes, wadd=wadd)

        def proj_fm(src2d, ncols, rhs_of, rhs_bufs, evac, tbs=TBS):
            s = load_w(src2d, ncols)
            for oc in range(ncols // 128):
                for tbi, (t0, n) in enumerate(tbs):
                    b = bank()
                    for k in range(8):
                        last = (k == 7)
                        P.op(PE, lambda e, b=b, k=k, oc=oc, tbi=tbi, t0=t0, n=n, s=s: e.matmul(
                            ps[b][:, 0:n], lhsT=wslot[s][:, k, oc * 128:(oc + 1) * 128], rhs=rhs_of(k, t0, n),
                            start=(k == 0), stop=(k == 7)),
                            reads=[wslot_b[s], rhs_bufs[tbi]] if k == 0 else [], writes=[psb[b]] if k == 0 else [],
                            sig=last)
                        if last and not P.dead:
                            h = Dep(P.esem[PE], P.cnt[PE])
                            P.attach(h, reads=[wslot_b[s], rhs_bufs[tbi]], writes=[psb[b]])
                    evac(oc, tbi, ps[b][:, 0:n], psb[b])

        def cmul(eng, dst_r, dst_i, xr, xi, yr, yi, t1, t2, bufs_r, bufs_w, tb):
            P.op(eng, lambda e: e.tensor_tensor(out=t1, in0=xr, in1=yr, op=ALU.mult), reads=bufs_r, writes=[tb])
            P.op(eng, lambda e: e.tensor_tensor(out=t2, in0=xi, in1=yi, op=ALU.mult), reads=bufs_r, writes=[tb])
            P.op(eng, lambda e: e.tensor_tensor(out=dst_r, in0=t1, in1=t2, op=ALU.subtract), reads=[tb], writes=bufs_w)
            P.op(eng, lambda e: e.tensor_tensor(out=t1, in0=xr, in1=yi, op=ALU.mult), reads=bufs_r + bufs_w, writes=[tb])
            P.op(eng, lambda e: e.tensor_tensor(out=t2, in0=xi, in1=yr, op=ALU.mult), reads=bufs_r + bufs_w, writes=[tb])
            P.op(eng, lambda e: e.tensor_tensor(out=dst_i, in0=t1, in1=t2, op=ALU.add), reads=[tb], writes=bufs_w)

        P.phase(0)
        P.op(POOL, lambda e: e.memset(ident, 0.0), writes=[b_ident])
        P.op(POOL, lambda e: e.affine_select(out=ident, in_=ident, pattern=[[-1, 128]], compare_op=ALU.not_equal,
                                             fill=1.0, base=0, channel_multiplier=1), writes=[b_ident])
        misc_load(SP, flags, flagsc.rearrange("(o n) -> o n", o=1).to_broadcast([128, 32]), b_flags)
        misc_load(SP, bmask, bmaskc, b_bmask)
        P.op(POOL, lambda e: e.memset(ones1[0:1, :], 1.0), writes=[b_ones])
        P.op(POOL, lambda e: e.memset(onesd[0:1, 0:64], 0.0), wadd=[b_ones])
        P.op(POOL, lambda e: e.memset(onesd[0:1, 64:128], 1.0), wadd=[b_ones])

        P.phase(1)
        c_t = cv_(O_C + 62208, [1024], F32); c_sg = cv_(O_C + 66304, [1024], F32)
        ccT = cv_(O_C + 70400, [8, 17], BF16); badain = cv_(O_C + 70912, [128], F32); badaT = cv_(O_C + 71424, [24], F32)
        b_ct = Buf(); b_csg = Buf(); b_ccT = Buf(); b_bin = Buf(); b_baT = Buf()
        misc_load(SP, c_t[0:17, :], ccin, b_ct)
        misc_load(SP, badain[0:24, :], b_ada.rearrange("(c p) -> c p", p=128), b_bin)
        P.op(ACT, lambda e: e.activation(out=c_sg[0:17, :], in_=c_t[0:17, :], func=AF.Sigmoid), reads=[b_ct], writes=[b_csg])
        P.op(DVE, lambda e: e.tensor_tensor(out=c_sg[0:17, :], in0=c_sg[0:17, :], in1=c_t[0:17, :], op=ALU.mult),
             reads=[b_ct], writes=[b_csg])
        bk = bank()
        for k in range(8):
            P.op(PE, lambda e, k=k: e.transpose(out=ps[bk][:, 17 * k:17 * k + 17], in_=c_sg[0:17, 128 * k:128 * k + 128],
                                                identity=ident[0:17, 0:17]),
                 reads=[b_csg, b_ident], writes=[psb[bk]] if k == 0 else [], wadd=[psb[bk]] if k > 0 else [])
        P.op(DVE, lambda e: e.tensor_copy(out=ccT.rearrange("p k s -> p (k s)"), in_=ps[bk][:, 0:136]), reads=[psb[bk]], writes=[b_ccT])
        bk2 = bank()
        P.op(PE, lambda e: e.transpose(out=ps[bk2][:, 0:24], in_=badain[0:24, :], identity=ident[0:24, 0:24]),
             reads=[b_bin, b_ident], writes=[psb[bk2]])
        P.op(DVE, lambda e: e.tensor_copy(out=badaT, in_=ps[bk2][:, 0:24]), reads=[psb[bk2]], writes=[b_baT])
        bkm = bank()
        hlast = None
        for blk in range(6):
            s = load_w(w_ada[:, 512 * blk:512 * blk + 512], 512)
            for oc in range(4):
                f = 4 * blk + oc
                for k in range(8):
                    first = (blk == 0 and oc == 0 and k == 0)
                    lastk = (k == 7)
                    hlast = P.op(PE, lambda e, f=f, k=k, oc=oc, s=s: e.matmul(
                        ps[bkm][:, 17 * f:17 * f + 17], lhsT=wslot[s][:, k, oc * 128:(oc + 1) * 128], rhs=ccT[:, k, :],
                        start=(k == 0), stop=(k == 7)),
                        reads=[wslot_b[s], b_ccT] if k == 0 else [], writes=[psb[bkm]] if first else [], sig=lastk and oc == 3)
            P.attach(hlast, reads=[wslot_b[s]], wadd=[psb[bkm]])
        P.op(DVE, lambda e: e.tensor_tensor(out=modT, in0=ps[bkm][:, 0:408].rearrange("p (f s) -> p f s", f=24),
                                            in1=badaT.unsqueeze(2).to_broadcast([128, 24, 17]), op=ALU.add),
             reads=[psb[bkm], b_baT], writes=[b_modT])
        P.op(DVE, lambda e: e.tensor_scalar(out=op1p, in0=modT[:, 8:16, :], scalar1=1.0, scalar2=None, op0=ALU.add),
             reads=[b_modT], wadd=[b_modT])

        xst = [cv_(O_C + 71552 + 4096 * i, [1024], F32) for i in range(2)]
        xst_b = [Buf(), Buf()]
        xsem = [P.dsem("x0"), P.dsem("x1")]
        hTs_b = hT_b[4]
        tmpS = cv_(O_C + 79744, [8, 16], F32)
        b_tmpS = Buf()

        def phase_a(xsrc, full):
            tiles = [("p", i) for i in range(16)] + ([("h", 0), ("s", 0)] if full else [])
            for ti, (kind, i) in enumerate(tiles):
                sl = ti % 2
                if kind == "p":
                    src, rows, dst, dbuf = xsrc[128 * i:128 * i + 128, :], 128, (lambda k, i=i: hT[:, k, 128 * i:128 * i + 128]), hT_b[i // 4]
                elif kind == "h":
                    src, rows, dst, dbuf = xh, 128, (lambda k: hTh[:, k, :]), hTh_b
                else:
                    src, rows, dst, dbuf = xs, NS, None, hTs_b
                P.dma(SP, xsem[sl], xst[sl][0:rows, :], src, writes=[xst_b[sl]])
                b0, b1 = bank(), bank()
                for k in range(8):
                    bb_ = b0 if k < 4 else b1
                    j = k % 4
                    P.op(PE, lambda e, k=k, bb_=bb_, j=j, sl=sl, rows=rows: e.transpose(
                        out=ps[bb_][:, rows * j:rows * j + rows], in_=xst[sl][0:rows, 128 * k:128 * k + 128],
                        identity=ident[0:rows, 0:rows]),
                        reads=[xst_b[sl], b_ident], writes=[psb[bb_]] if j == 0 else [], wadd=[psb[bb_]] if j > 0 else [])
                if kind != "s":
                    for k in range(8):
                        bb_ = b0 if k < 4 else b1
                        j = k % 4
                        src_ps = ps[bb_][:, 128 * j:128 * j + 128]
                        if k < 4:
                            P.op(DVE, lambda e, k=k, src_ps=src_ps, dst=dst: e.tensor_scalar(
                                out=dst(k), in0=src_ps, scalar1=op1p[:, k, 0:1], scalar2=modT[:, k, 0:1], op0=ALU.mult, op1=ALU.add),
                                reads=[psb[bb_], b_modT], wadd=[dbuf])
                        else:
                            P.op(ACT, lambda e, k=k, src_ps=src_ps, dst=dst: e.activation(
                                out=dst(k), in_=src_ps, func=AF.Identity, scale=op1p[:, k, 0:1], bias=modT[:, k, 0:1]),
                                reads=[psb[bb_], b_modT], wadd=[dbuf])
                else:
                    for half, bb_ in enumerate([b0, b1]):
                        P.op(DVE, lambda e, half=half, bb_=bb_: e.tensor_tensor(
                            out=tmpS[:, 4 * half:4 * half + 4, :], in0=ps[bb_][:, 0:64].rearrange("p (k s) -> p k s", k=4),
                            in1=op1p[:, 4 * half:4 * half + 4, 1:17], op=ALU.mult), reads=[psb[bb_], b_modT], wadd=[b_tmpS])
                    P.op(DVE, lambda e: e.tensor_tensor(out=hT[:, :, NP:NT], in0=tmpS, in1=modT[:, 0:8, 1:17], op=ALU.add),
                         reads=[b_tmpS, b_modT], wadd=[dbuf])

        uT = RA
        uT_b = [Buf(f"uT{g}") for g in range(8)]
        rhs_h = lambda k, t0, n: hT[:, k, t0:t0 + n]
        cnt = {"i": 0}

        def phase_b(full):
            for blk in range(2):
                def ev(oc, tbi, pap, pb, blk=blk):
                    t0, n = TBS[tbi]
                    g = 4 * blk + oc
                    evac_copy(0, uT[:, g, t0:t0 + n], pap, [pb], [], wadd=[uT_b[g]])
                proj_fm(w_in[:, OFF_U + 512 * blk:OFF_U + 512 * blk + 512], 512, rhs_h, hT_b, ev, tbs=TBS if full else TBS[:4])

        P.phase(3)
        phase_a(xprev[0], False)
        phase_b(False)
        P.phase(2)
        oc_ = O_C
        pw = cv_(oc_ + 0, [9, 2, 32], F32); bb = cv_(oc_ + 2304, [2, 32, 16], F32); ccm = cv_(oc_ + 6400, [2, 32, 16], F32)
        R8 = cv_(oc_ + 10496, [16, 2, 32], F32); R128 = cv_(oc_ + 14592, [16, 2, 32], F32); A2k = cv_(oc_ + 18688, [3, 2, 32], F32)
        Hend = cv_(oc_ + 19456, [2, 32, 16], F32); carry = cv_(oc_ + 23552, [17, 2, 32], F32)
        Sb = cv_(oc_ + 27904, [8, 2, 128], BF16); coef = cv_(oc_ + 32000, [12, 32], F32)
        misc = cv_(oc_ + 33536, [32, 32], F32)
        t1 = cv_(oc_ + 37632, [1024], F32); t2 = cv_(oc_ + 41728, [1024], F32)
        O_S = oc_ + 45824
        Sslot = [cv_(O_S + 8192 * i, [8, 2, 128], F32) for i in range(2)]
        O_CA = oc_ + 62208; O_KL = oc_ + 71424; O_HB = oc_ + 75520
        b_pw = Buf("pw"); b_bb = Buf("bb"); b_ccm = Buf("ccm"); b_R8 = Buf(); b_R128 = Buf(); b_A2k = Buf(); b_coef = Buf()
        b_misc = Buf("misc"); b_t = Buf("t12"); b_Sslot = [Buf("S0"), Buf("S1")]
        craw = [cv_(O_S + 2048 * i, [8, 64], F32) for i in range(2)]
        lamraw = cv_(O_S + 4096, [128], F32); ldraw = cv_(O_S + 4608, [64], F32)
        draw = cv_(O_S + 4864, [128], F32); bgraw = cv_(O_S + 5376, [128], F32)
        braw = [cv_(O_S + 8192 + 2048 * i, [32, 16], F32) for i in range(2)]
        b_craw = Buf(); b_lam = Buf(); b_ld = Buf(); b_draw = Buf(); b_braw = Buf()
        misc_load(SP, craw[0], c_re.rearrange("(t r) p -> r t p", r=128), b_craw, wadd=True)
        misc_load(SP, craw[1], c_im.rearrange("(t r) p -> r t p", r=128), b_craw, wadd=True)
        misc_load(SP, lamraw[0:64, 0:64], lam_re, b_lam, wadd=True)
        misc_load(SP, lamraw[0:64, 64:128], lam_im, b_lam, wadd=True)
        misc_load(SP, ldraw, log_delta.rearrange("(o n) -> o n", o=1).to_broadcast([128, 64]), b_ld)
        misc_load(SP, draw[0:8, :], ssm_d.rearrange("(c p) -> c p", p=128), b_draw, wadd=True)
        misc_load(SP, bgraw[0:8, :], b_glu.rearrange("(c p) -> c p", p=128), b_draw, wadd=True)
        for i, src in enumerate([b_re, b_im]):
            for q4 in range(4):
                misc_load(SP, braw[i][:, 8 * q4:8 * q4 + 8, :],
                          src[1024 * q4:1024 * q4 + 1024, :].rearrange("(gp q) c -> q gp c", q=128), b_braw, wadd=True)
        M_ = lambda i: misc[:, i, :]
        LR, LI, DT, TH, FR, FC, MAG, SN, CS, NR, DEN, CR, CI, G1, KF, TMPA = [M_(i) for i in range(16)]
        KI = misc[:, 16, :].bitcast(mybir.dt.int32)
        bkl = bank()
        P.op(PE, lambda e: e.transpose(out=ps[bkl][:, 0:64], in_=lamraw[0:64, :], identity=ident[0:64, 0:64]),
             reads=[b_lam, b_ident], writes=[psb[bkl]])
        P.op(DVE, lambda e: e.tensor_copy(out=LR[0:64, :], in_=ps[bkl][0:64, 0:64:2]), reads=[psb[bkl]], wadd=[b_misc])
        P.op(DVE, lambda e: e.tensor_copy(out=LR[64:128, :], in_=ps[bkl][0:64, 1:64:2]), reads=[psb[bkl]], wadd=[b_misc])
        P.op(DVE, lambda e: e.tensor_copy(out=LI[0:64, :], in_=ps[bkl][64:128, 0:64:2]), reads=[psb[bkl]], wadd=[b_misc])
        P.op(DVE, lambda e: e.tensor_copy(out=LI[64:128, :], in_=ps[bkl][64:128, 1:64:2]), reads=[psb[bkl]], wadd=[b_misc])
        bkd = bank()
        P.op(PE, lambda e: e.transpose(out=ps[bkd][:, 0:8], in_=draw[0:8, :], identity=ident[0:8, 0:8]),
             reads=[b_draw, b_ident], writes=[psb[bkd]])
        P.op(PE, lambda e: e.transpose(out=ps[bkd][:, 8:16], in_=bgraw[0:8, :], identity=ident[0:8, 0:8]),
             reads=[b_draw, b_ident], wadd=[psb[bkd]])
        P.op(DVE, lambda e: e.tensor_copy(out=Dm, in_=ps[bkd][:, 0:8]), reads=[psb[bkd]], writes=[b_Dm])
        P.op(DVE, lambda e: e.tensor_copy(out=bglu, in_=ps[bkd][:, 8:16]), reads=[psb[bkd]], writes=[b_bglu])
        for ri in range(2):
            for hb in range(2):
                bkc = bank()
                for tt in range(4):
                    t_ = 4 * hb + tt
                    P.op(PE, lambda e, ri=ri, t_=t_, tt=tt, bkc=bkc: e.transpose(
                        out=ps[bkc][0:64, 128 * tt:128 * tt + 128], in_=craw[ri][:, t_, :], identity=ident),
                        reads=[b_craw, b_ident], writes=[psb[bkc]] if tt == 0 else [], wadd=[psb[bkc]] if tt > 0 else [])
                for g2 in range(2):
                    src = ps[bkc][0:64, :].rearrange("p (tg g2 c) -> p tg g2 c", g2=2, c=16)[:, :, g2, :]
                    P.op(DVE, lambda e, ri=ri, hb=hb, g2=g2, src=src: e.tensor_copy(
                        out=ccm[64 * g2:64 * g2 + 64, ri, 16 * hb:16 * hb + 16, :], in_=src),
                        reads=[psb[bkc]], wadd=[b_ccm])
        P.op(ACT, lambda e: e.activation(out=DT[0:64, :], in_=ldraw[0:64, 0:64:2], func=AF.Exp), reads=[b_ld], wadd=[b_misc])
        P.op(ACT, lambda e: e.activation(out=DT[64:128, :], in_=ldraw[64:128, 1:64:2], func=AF.Exp), reads=[b_ld], wadd=[b_misc])
        G = DVE
        tt_ = lambda out, a, b_, op, **kw: P.op(G, lambda e: e.tensor_tensor(out=out, in0=a, in1=b_, op=op), reads=[b_misc], wadd=[b_misc], **kw)
        ts_ = lambda out, a, s1, s2, o0, o1: P.op(G, lambda e: e.tensor_scalar(out=out, in0=a, scalar1=s1, scalar2=s2, op0=o0, op1=o1), reads=[b_misc], wadd=[b_misc])
        tt_(TH, LI, DT, ALU.mult)
        ts_(FR, TH, 1.0 / (2 * math.pi), 0.0, ALU.mult, ALU.add)
        P.op(DVE, lambda e: e.tensor_copy(out=KI, in_=FR), reads=[b_misc], wadd=[b_misc])
        P.op(DVE, lambda e: e.tensor_copy(out=KF, in_=KI), reads=[b_misc], wadd=[b_misc])
        tt_(FR, FR, KF, ALU.subtract)
        ts_(FC, FR, 1.0, 0.25, ALU.mult, ALU.add)
        P.op(DVE, lambda e: e.tensor_single_scalar(out=G1, in_=FC, scalar=0.5, op=ALU.is_gt), reads=[b_misc], wadd=[b_misc])
        tt_(FC, FC, G1, ALU.subtract)
        TWO_PI = 2.0 * math.pi
        P.op(ACT, lambda e: e.activation(out=SN, in_=FR, func=AF.Sin, scale=TWO_PI), reads=[b_misc], wadd=[b_misc])
        P.op(ACT, lambda e: e.activation(out=CS, in_=FC, func=AF.Sin, scale=TWO_PI), reads=[b_misc], wadd=[b_misc])
        tt_(TMPA, LR, DT, ALU.mult)
        P.op(ACT, lambda e: e.activation(out=MAG, in_=TMPA, func=AF.Exp), reads=[b_misc], wadd=[b_misc])
        P.op(G, lambda e: e.memset(pw[:, 0, 0, :], 1.0), wadd=[b_pw])
        P.op(G, lambda e: e.memset(pw[:, 0, 1, :], 0.0), wadd=[b_pw])
        P.op(G, lambda e: e.tensor_tensor(out=pw[:, 1, 0, :], in0=MAG, in1=CS, op=ALU.mult), reads=[b_misc], wadd=[b_pw])
        P.op(G, lambda e: e.tensor_tensor(out=pw[:, 1, 1, :], in0=MAG, in1=SN, op=ALU.mult), reads=[b_misc], wadd=[b_pw])

        def cm(dst, x, y, n, rb, wb):
            T1 = t1[:, 0:n * 32].rearrange("p (n g) -> p n g", n=n)
            T2 = t2[:, 0:n * 32].rearrange("p (n g) -> p n g", n=n)
            cmul(G, dst[:, :, 0, :], dst[:, :, 1, :], x[:, :, 0, :], x[:, :, 1, :], y[:, :, 0, :], y[:, :, 1, :], T1, T2, rb, wb, b_t)

        def bc(ap1, n):
            return ap1.to_broadcast([128, n, 2, 32])

        cm(pw[:, 2:3], pw[:, 1:2], pw[:, 1:2], 1, [b_pw], [b_pw])
        cm(pw[:, 3:5], pw[:, 1:3], bc(pw[:, 2:3], 2), 2, [b_pw], [b_pw])
        cm(pw[:, 5:9], pw[:, 1:5], bc(pw[:, 4:5], 4), 4, [b_pw], [b_pw])
        tt_(NR, pw[:, 1, 0, :], pw[:, 0, 0, :], ALU.subtract, deps=b_pw.w)
        tt_(DEN, LR, LR, ALU.mult)
        tt_(TMPA, LI, LI, ALU.mult)
        tt_(DEN, DEN, TMPA, ALU.add)
        P.op(DVE, lambda e: e.reciprocal(out=DEN, in_=DEN), reads=[b_misc], wadd=[b_misc])
        tt_(CR, NR, LR, ALU.mult)
        tt_(TMPA, pw[:, 1, 1, :], LI, ALU.mult)
        tt_(CR, CR, TMPA, ALU.add)
        tt_(CR, CR, DEN, ALU.mult)
        tt_(CI, pw[:, 1, 1, :], LR, ALU.mult)
        tt_(TMPA, NR, LI, ALU.mult)
        tt_(CI, CI, TMPA, ALU.subtract)
        tt_(CI, CI, DEN, ALU.mult)
        CRb = CR.unsqueeze(2).to_broadcast([128, 32, 16]); CIb = CI.unsqueeze(2).to_broadcast([128, 32, 16])
        T1b = t1[:, 0:512].rearrange("p (g c) -> p g c", g=32); T2b = t2[:, 0:512].rearrange("p (g c) -> p g c", g=32)
        P.op(G, lambda e: e.tensor_tensor(out=T1b, in0=braw[0], in1=CRb, op=ALU.mult), reads=[b_braw, b_misc, b_pw], writes=[b_t])
        P.op(G, lambda e: e.tensor_tensor(out=T2b, in0=braw[1], in1=CIb, op=ALU.mult), reads=[b_braw, b_misc], wadd=[b_t])
        P.op(G, lambda e: e.tensor_tensor(out=bb[:, 0], in0=T1b, in1=T2b, op=ALU.subtract), reads=[b_t], wadd=[b_bb])
        P.op(G, lambda e: e.tensor_tensor(out=T1b, in0=braw[1], in1=CRb, op=ALU.mult), reads=[b_braw, b_misc, b_bb], writes=[b_t])
        P.op(G, lambda e: e.tensor_tensor(out=T2b, in0=braw[0], in1=CIb, op=ALU.mult), reads=[b_braw, b_misc], wadd=[b_t])
        P.op(G, lambda e: e.tensor_tensor(out=bb[:, 1], in0=T1b, in1=T2b, op=ALU.add), reads=[b_t], wadd=[b_bb])

        def rev_table(R, A0, bufR, out_last):
            AW = misc[:, 20:22, :].rearrange("p (o r) g -> p o r g", o=1)
            AW2 = misc[:, 22:24, :].rearrange("p (o r) g -> p o r g", o=1)
            P.op(G, lambda e: e.memset(R[:, 15, 0, :], 1.0), wadd=[bufR])
            P.op(G, lambda e: e.memset(R[:, 15, 1, :], 0.0), wadd=[bufR])
            P.op(G, lambda e: e.tensor_copy(out=AW, in_=A0), reads=[b_pw, b_coef, b_misc], wadd=[b_misc])
            w = 1
            cur, nxt = AW, AW2
            while w <= 8:
                cm(R[:, 16 - 2 * w:16 - w], R[:, 16 - w:16], bc(cur, w), w, [bufR, b_misc], [bufR])
                cm(nxt, cur, cur, 1, [b_misc], [b_misc])
                cur, nxt = nxt, cur
                w *= 2
            P.op(G, lambda e: e.tensor_copy(out=out_last, in_=cur), reads=[b_misc], wadd=[b_coef])

        A128 = coef[:, 0:2, :].rearrange("p (o r) g -> p o r g", o=1)
        A2048 = coef[:, 2:4, :].rearrange("p (o r) g -> p o r g", o=1)
        rev_table(R8, pw[:, 8:9], b_R8, A128)
        rev_table(R128, A128, b_R128, A2048)
        P.op(G, lambda e: e.memset(A2k[:, 0, 0, :], 1.0), wadd=[b_A2k])
        P.op(G, lambda e: e.memset(A2k[:, 0, 1, :], 0.0), wadd=[b_A2k])
        P.op(G, lambda e: e.tensor_copy(out=A2k[:, 1:2], in_=A2048), reads=[b_coef], wadd=[b_A2k])
        cm(A2k[:, 2:3], A2048, A2048, 1, [b_coef], [b_A2k])
        P.op(G, lambda e: e.tensor_scalar(out=coef[:, 4, :], in0=pw[:, 8, 1, :], scalar1=-1.0, scalar2=0.0, op0=ALU.mult, op1=ALU.add),
             reads=[b_pw], wadd=[b_coef])
        P.op(G, lambda e: e.tensor_scalar(out=coef[:, 5, :], in0=coef[:, 1, :], scalar1=-1.0, scalar2=0.0, op0=ALU.mult, op1=ALU.add),
             reads=[b_coef], wadd=[b_coef])

        P.phase(5)
        WinL = cv_(O_B, [8, 8, 2, 128], BF16)
        b_WinL = [Buf(f"WinL{g}") for g in range(8)]
        b_Sb = Buf("Sb")
        handoff(b_Sslot, [b_craw, b_lam, b_ld, b_draw, b_braw])
        b_SInit = [Buf("SInit0"), Buf("SInit1")]
        for i in range(2):
            P.op(POOL, lambda e, i=i: e.memset(Sslot[i].rearrange("p k r c -> p (k r c)"), 0.0), writes=[b_Sslot[i], b_SInit[i]])
        T1e = [t1[:, 0:512].rearrange("p (k m c) -> p k m c", k=8, m=4), t1[:, 512:1024].rearrange("p (k m c) -> p k m c", k=8, m=4)]
        T2e = [t2[:, 0:512].rearrange("p (k m c) -> p k m c", k=8, m=4), t2[:, 512:1024].rearrange("p (k m c) -> p k m c", k=8, m=4)]
        b_t5 = [b_t, Buf("t5pool")]
        handoff([b_t5[1]], [b_t])
        for gc in range(8):
            sl = gc % 2
            EG = DVE if gc % 2 == 0 else POOL
            T1s, T2s, b_tt = T1e[gc % 2], T2e[gc % 2], b_t5[gc % 2]
            S = Sslot[sl]
            Sv = S.rearrange("p k r (m g c) -> p k r m g c", m=4, g=2)
            prk = pw[:, 0:8, 0, 4 * gc:4 * gc + 4].unsqueeze(3).to_broadcast([128, 8, 4, 16])
            pik = pw[:, 0:8, 1, 4 * gc:4 * gc + 4].unsqueeze(3).to_broadcast([128, 8, 4, 16])
            bbr = bb[:, 0, 4 * gc:4 * gc + 4, :].unsqueeze(1).to_broadcast([128, 8, 4, 16])
            bbi = bb[:, 1, 4 * gc:4 * gc + 4, :].unsqueeze(1).to_broadcast([128, 8, 4, 16])
            for ri in range(2):
                x1, x2 = (bbr, bbi) if ri == 0 else (bbi, bbr)
                op = ALU.subtract if ri == 0 else ALU.add
                P.op(EG, lambda e, x1=x1, prk=prk, T1s=T1s: e.tensor_tensor(out=T1s, in0=prk, in1=x1, op=ALU.mult), reads=[b_pw, b_bb], writes=[b_tt])
                P.op(EG, lambda e, x2=x2, pik=pik, T2s=T2s: e.tensor_tensor(out=T2s, in0=pik, in1=x2, op=ALU.mult), reads=[b_pw, b_bb], wadd=[b_tt])
                for g2 in range(2):
                    lo, hi = 64 * g2, 64 * g2 + 64
                    P.op(EG, lambda e, ri=ri, g2=g2, lo=lo, hi=hi, op=op, Sv=Sv, T1s=T1s, T2s=T2s: e.tensor_tensor(
                        out=Sv[lo:hi, :, ri, :, g2, :], in0=T1s[lo:hi], in1=T2s[lo:hi], op=op),
                        reads=[b_tt, b_SInit[sl]], wadd=[b_Sslot[sl]])
            P.op(EG, lambda e, gc=gc, S=S: e.tensor_copy(out=Sb[:, gc], in_=S[:, 0]), reads=[b_Sslot[sl]], wadd=[b_Sb])
            for q4 in range(4):
                bkw = bank()
                for j in range(4):
                    k_, ri_ = (4 * q4 + j) // 2, (4 * q4 + j) % 2
                    P.op(PE, lambda e, bkw=bkw, j=j, k_=k_, ri_=ri_, S=S: e.transpose(
                        out=ps[bkw][:, 128 * j:128 * j + 128], in_=S[:, k_, ri_, :], identity=ident),
                        reads=[b_Sslot[sl], b_ident], writes=[psb[bkw]] if j == 0 else [], wadd=[psb[bkw]] if j > 0 else [])
                dstw = WinL[:, gc, 2 * q4:2 * q4 + 2].rearrange("p k r c -> p (k r c)")
                evac_copy(q4, dstw, ps[bkw][:, 0:512], [psb[bkw]], [], wadd=[b_WinL[gc]])

        t1p = cv_(O_S, [2, 16, 16], F32); t2p = cv_(O_S + 2048, [2, 16, 16], F32); cbp = cv_(O_S + 4096, [2, 16, 16], F32)
        b_p1 = Buf("p1tmp")
        handoff([b_p1], b_Sslot)
        b_Hend = Buf("Hend")

        def x_matmuls(gc, banks):
            for ri in range(2):
                for s_ in range(8):
                    for m in range(4):
                        first = (ri == 0 and s_ == 0)
                        last = (ri == 1 and s_ == 7)
                        bkx = banks[m]
                        P.op(PE, lambda e, m=m, ri=ri, s_=s_, bkx=bkx, gc=gc: e.matmul(
                            ps[bkx][:, 256 * ri:256 * ri + 256], lhsT=WinL[32 * m:32 * m + 32, gc, 7 - s_, ri, :],
                            rhs=uT[32 * m:32 * m + 32, gc, s_:NP:8], start=(s_ == 0), stop=(s_ == 7),
                            tile_position=(32 * m, 0)),
                            reads=[b_WinL[gc], uT_b[gc]] if first else [], writes=[psb[bkx]] if first else [], sig=last)
                        if last and not P.dead:
                            P.attach(Dep(P.esem[PE], P.cnt[PE]), reads=[b_WinL[gc], uT_b[gc]], writes=[psb[bkx]])

        def seg_reduce(src_ap, Rtab, gp0, ngp, out_ap, rbufs, wbuf):
            raise NotImplementedError

        def pass1():
            for gc in range(8):
                banks = [bank() for _ in range(4)]
                x_matmuls(gc, banks)
                if DEBUG and gc == 0 and os.environ.get('KX0'):
                    xdbg = cv_(O_S + 6144, [512], F32); b_xdbg = Buf()
                    P.op(DVE, lambda e: e.tensor_copy(out=xdbg, in_=ps[banks[1]][:, :]), reads=[psb[banks[1]]], writes=[b_xdbg])
                    P.dma(SP, dout_sem, dbg["X0"], xdbg, reads=[b_xdbg])
                for m in range(4):
                    gp = 4 * gc + m
                    X4 = ps[banks[m]][:, :].rearrange("p (r s i) -> p r s i", r=2, s=16)
                    Pr = R8[:, :, 0, gp].unsqueeze(1).unsqueeze(1).to_broadcast([128, 2, 16, 16])
                    Pi = R8[:, :, 1, gp].unsqueeze(1).unsqueeze(1).to_broadcast([128, 2, 16, 16])
                    P.op(DVE, lambda e, X4=X4, Pr=Pr: e.tensor_tensor(out=t1p, in0=X4, in1=Pr, op=ALU.mult),
                         reads=[psb[banks[m]], b_R8], writes=[b_p1])
                    P.op(DVE, lambda e, X4=X4, Pi=Pi: e.tensor_tensor(out=t2p, in0=X4, in1=Pi, op=ALU.mult),
                         reads=[psb[banks[m]], b_R8], wadd=[b_p1])
                    P.op(DVE, lambda e: e.tensor_tensor(out=cbp[:, 0], in0=t1p[:, 0], in1=t2p[:, 1], op=ALU.subtract), reads=[b_p1], wadd=[b_p1])
                    P.op(DVE, lambda e: e.tensor_tensor(out=cbp[:, 1], in0=t2p[:, 0], in1=t1p[:, 1], op=ALU.add), reads=[b_p1], wadd=[b_p1])
                    P.op(DVE, lambda e, gp=gp: e.tensor_reduce(out=Hend[:, :, gp, :], in_=cbp, axis=AX.X, op=ALU.add),
                         reads=[b_p1], wadd=[b_Hend])


        Ecore = cv_(O_S + 6144, [2, 32], F32); Eall = cv_(O_S + 6400, [8, 64], F32)
        te1 = cv_(O_S + 0, [2, 32, 16], F32); te2 = cv_(O_S + 8448, [2, 32, 16], F32)
        b_E = Buf("Ecore"); b_Eall = Buf("Eall"); b_te = b_p1
        handoff([b_E, b_Eall], b_Sslot)
        def ecore(j):
            Qr = R128[:, :, 0, :].rearrange("p i g -> p g i").unsqueeze(1).to_broadcast([128, 2, 32, 16])
            Qi = R128[:, :, 1, :].rearrange("p i g -> p g i").unsqueeze(1).to_broadcast([128, 2, 32, 16])
            P.op(DVE, lambda e: e.tensor_tensor(out=te1, in0=Hend, in1=Qr, op=ALU.mult), reads=[b_Hend, b_R128], writes=[b_te])
            P.op(DVE, lambda e: e.tensor_tensor(out=te2, in0=Hend, in1=Qi, op=ALU.mult), reads=[b_Hend, b_R128], wadd=[b_te])
            P.op(DVE, lambda e: e.tensor_tensor(out=te1[:, 0], in0=te1[:, 0], in1=te2[:, 1], op=ALU.subtract), reads=[b_te], writes=[b_te])
            P.op(DVE, lambda e: e.tensor_tensor(out=te2[:, 0], in0=te2[:, 0], in1=te1[:, 1], op=ALU.add), reads=[b_te], writes=[b_te])
            P.op(DVE, lambda e: e.tensor_reduce(out=Ecore[:, 0, :], in_=te1[:, 0], axis=AX.X, op=ALU.add), reads=[b_te], wadd=[b_E])
            P.op(DVE, lambda e: e.tensor_reduce(out=Ecore[:, 1, :], in_=te2[:, 0], axis=AX.X, op=ALU.add), reads=[b_te], wadd=[b_E])

            if j is not None:
                P.op(DVE, lambda e, j=j: e.tensor_copy(out=Eall[:, j, :], in_=Ecore.rearrange("p r g -> p (r g)")), reads=[b_E], wadd=[b_Eall])

        P.phase(3)
        srcs = [(xprev[1], False), (xprev[2], False), (xp, True)]
        phase_a(*srcs[0])
        for j in range(3):
            pass1()
            ecore(j)
            phase_b(srcs[j][1])
            if j < 2:
                phase_a(*srcs[j + 1])
        P.phase(6)
        pass1()
        P.phase(7)
        Sn = [cv_(O_S + 12544 + 256 * n, [2, 32], F32) for n in range(3)]
        b_Sn = Buf("Sn")
        handoff([b_Sn], b_Sslot)
        for n in range(3):
            Snf = Sn[n].rearrange("p r g -> p (r g)")
            P.op(DVE, lambda e, n=n, Snf=Snf: e.tensor_scalar(out=Snf, in0=Eall[:, n, :], scalar1=flags[:, 1 + n:2 + n], scalar2=None,
                                                            op0=ALU.mult), reads=[b_Eall, b_flags], wadd=[b_Sn])
        b_carry = Buf("carry")
        tq1 = cv_(O_S + 13312, [2, 32], F32); tq2 = cv_(O_S + 13568, [2, 32], F32)
        b_tq = Buf("tq")
        handoff([b_tq], b_Sslot)

        def cmul_small(dst, x, y, rb, wb):
            cmul(DVE, dst[:, 0, :], dst[:, 1, :], x[:, 0, :], x[:, 1, :], y[:, 0, :], y[:, 1, :], tq1[:, 0, :], tq1[:, 1, :], rb, wb, b_tq)

        cmul_small(carry[:, 1], Sn[1], A2k[:, 1], [b_Sn, b_A2k], [b_carry])
        cmul_small(carry[:, 2], Sn[2], A2k[:, 2], [b_Sn, b_A2k, b_carry], [b_carry])
        P.op(DVE, lambda e: e.tensor_tensor(out=Sn[0], in0=Sn[0], in1=carry[:, 1], op=ALU.add), reads=[b_Sn, b_carry], writes=[b_Sn])
        P.op(DVE, lambda e: e.tensor_tensor(out=carry[:, 0], in0=Sn[0], in1=carry[:, 2], op=ALU.add), reads=[b_Sn, b_carry], writes=[b_carry])
        A128v = coef[:, 0:2, :]
        for sg in range(16):
            cmul_small(carry[:, sg + 1], carry[:, sg], A128v, [b_carry, b_coef], [b_carry])
            P.op(DVE, lambda e, sg=sg: e.tensor_tensor(out=carry[:, sg + 1], in0=carry[:, sg + 1], in1=Hend[:, :, :, sg], op=ALU.add),
                 reads=[b_carry, b_Hend], writes=[b_carry])
        pstT = cv_(O_S + 13824, [2, 128], F32)
        b_pst = Buf()
        handoff([b_pst], b_Sslot)
        bkp = bank()
        for ri in range(2):
            P.op(PE, lambda e, ri=ri: e.transpose(out=ps[bkp][0:32, 128 * ri:128 * ri + 128], in_=carry[:, 16, ri, :], identity=ident),
                 reads=[b_carry, b_ident], writes=[psb[bkp]] if ri == 0 else [], wadd=[psb[bkp]] if ri == 1 else [])
        P.op(DVE, lambda e: e.tensor_copy(out=pstT[0:32].rearrange("p r c -> p (r c)"), in_=ps[bkp][0:32, 0:256]), reads=[psb[bkp]], writes=[b_pst])
        P.dma(SP, osem_new("pre"), pst_re, pstT[0:32, 0, :], reads=[b_pst])
        P.dma(SP, osem_new("pim"), pst_im, pstT[0:32, 1, :], reads=[b_pst])

        P.phase(8)
        CaBD = [cv_(O_CA + 4608 * i, [4, 9, 2, 32], BF16) for i in range(2)]
        KLs = [cv_(O_KL + 2048 * i, [8, 128], BF16) for i in range(2)]
        Hb = [cv_(O_HB + 4096 * i, [2, 4, 256], BF16) for i in range(2)]
        Xs = [cv_(O_S + 8192 * i, [2, 4, 256], F32) for i in range(2)]
        b_CaBD = [Buf("Ca0"), Buf("Ca1")]; b_KL = [Buf("KL0"), Buf("KL1")]; b_Hb = [Buf("Hb0"), Buf("Hb1")]; b_Xs = [Buf("Xs0"), Buf("Xs1")]
        old_c = [b_ct, b_csg, b_ccT, b_bin, b_baT, xst_b[0], xst_b[1]]
        handoff(b_CaBD + b_KL + b_Hb, old_c)
        handoff(b_Xs, [b_p1, b_E, b_Eall, b_te, b_Sn, b_tq, b_pst] + b_Sslot)
        KL0all = cv_(O_C + 10496, [8, 128], BF16); Ca1all = cv_(O_C + 14592, [32, 2, 32], BF16)
        b_KL0 = Buf("KL0all"); b_Ca1 = Buf("Ca1all")
        handoff([b_KL0], [b_R8]); handoff([b_Ca1], [b_R128])
        Q1e = [misc[:, 24:28, :].rearrange("p a g -> p (a g)").rearrange("p (r m s) -> p r m s", r=2, m=4),
               misc[:, 0:4, :].rearrange("p a g -> p (a g)").rearrange("p (r m s) -> p r m s", r=2, m=4)]
        Q2e = [misc[:, 28:32, :].rearrange("p a g -> p (a g)").rearrange("p (r m s) -> p r m s", r=2, m=4),
               misc[:, 4:8, :].rearrange("p a g -> p (a g)").rearrange("p (r m s) -> p r m s", r=2, m=4)]
        tmpK = misc[:, 16:20, :].rearrange("p a g -> p (a g)")
        b_Q = [Buf("Qdve"), Buf("Qpool")]
        b_tmpK = Buf("tmpK")
        handoff([b_tmpK] + b_Q, [b_misc])
        b_CaInit = [Buf("CaInit0"), Buf("CaInit1")]
        for i in range(2):
            P.op(POOL, lambda e, i=i: e.memset(CaBD[i].rearrange("p m n r c -> p (m n r c)"), 0.0), writes=[b_CaBD[i], b_CaInit[i]])
        U1 = t1[:, 0:576].rearrange("p (m n c) -> p m n c", m=4, n=9)
        U2 = t2[:, 0:576].rearrange("p (m n c) -> p m n c", m=4, n=9)
        XB = [2, 3, 4, 5]
        YB = [6, 7]

        def emit_consts(gc):
            sl = gc % 2
            Ca = CaBD[sl]
            cre = ccm[:, 0, 4 * gc:4 * gc + 4, :].unsqueeze(2).to_broadcast([128, 4, 9, 16])
            cim = ccm[:, 1, 4 * gc:4 * gc + 4, :].unsqueeze(2).to_broadcast([128, 4, 9, 16])
            pr = pw[:, :, 0, 4 * gc:4 * gc + 4].rearrange("p n m -> p m n").unsqueeze(3).to_broadcast([128, 4, 9, 16])
            pi = pw[:, :, 1, 4 * gc:4 * gc + 4].rearrange("p n m -> p m n").unsqueeze(3).to_broadcast([128, 4, 9, 16])
            P.op(DVE, lambda e: e.tensor_tensor(out=U1, in0=cre, in1=pr, op=ALU.mult), reads=[b_ccm, b_pw], writes=[b_t])
            P.op(DVE, lambda e: e.tensor_tensor(out=U2, in0=cim, in1=pi, op=ALU.mult), reads=[b_ccm, b_pw], wadd=[b_t])
            for g2 in range(2):
                lo, hi = 64 * g2, 64 * g2 + 64
                P.op(DVE, lambda e, lo=lo, hi=hi, g2=g2: e.tensor_tensor(
                    out=Ca[lo:hi, :, :, 0, 16 * g2:16 * g2 + 16], in0=U1[lo:hi], in1=U2[lo:hi], op=ALU.subtract),
                    reads=[b_t, b_CaInit[sl]], wadd=[b_CaBD[sl]])
            P.op(DVE, lambda e: e.tensor_tensor(out=U1, in0=cre, in1=pi, op=ALU.mult), reads=[b_ccm, b_pw], writes=[b_t])
            P.op(DVE, lambda e: e.tensor_tensor(out=U2, in0=cim, in1=pr, op=ALU.mult), reads=[b_ccm, b_pw], wadd=[b_t])
            P.op(DVE, lambda e: e.tensor_tensor(out=U1, in0=U1, in1=U2, op=ALU.add), reads=[b_t], writes=[b_t])
            for g2 in range(2):
                lo, hi = 64 * g2, 64 * g2 + 64
                P.op(DVE, lambda e, lo=lo, hi=hi, g2=g2: e.tensor_scalar(
                    out=Ca[lo:hi, :, :, 1, 16 * g2:16 * g2 + 16], in0=U1[lo:hi], scalar1=-1.0, scalar2=0.0, op0=ALU.mult, op1=ALU.add),
                    reads=[b_t, b_CaInit[sl]], wadd=[b_CaBD[sl]])
            P.op(DVE, lambda e: e.tensor_copy(out=Ca1all[:, 4 * gc:4 * gc + 4], in_=Ca[:, :, 1, :, :]), reads=[b_CaBD[sl]], wadd=[b_Ca1])
            for hb in range(2):
                for tt in range(4):
                    tau = 4 * hb + tt
                    for ri in range(2):
                        P.op(PE, lambda e, hb=hb, tt=tt, tau=tau, ri=ri: e.matmul(
                            ps[hb][:, 128 * tt:128 * tt + 128], lhsT=Sb[:, gc, ri, :], rhs=Ca[:, :, tau, ri, :],
                            start=(ri == 0), stop=(ri == 1)),
                            reads=[b_Sb, b_CaBD[sl]] if (tt == 0 and ri == 0) else [],
                            writes=[psb[hb]] if (tt == 0 and ri == 0) else [], sig=(tt == 3 and ri == 1))
                if not P.dead:
                    P.attach(Dep(P.esem[PE], P.cnt[PE]), reads=[b_Sb, b_CaBD[sl]], writes=[psb[hb]])
            KL = KLs[sl]
            bmb3 = bmask.unsqueeze(1).to_broadcast([128, 3, 128]); bmb4 = bmask.unsqueeze(1).to_broadcast([128, 4, 128])
            P.op(DVE, lambda e: e.tensor_tensor(out=KL[:, 1:4, :], in0=ps[0][:, 128:512].rearrange("p (t c) -> p t c", t=3), in1=bmb3, op=ALU.mult),
                 reads=[psb[0], b_bmask], wadd=[b_KL[sl]])
            P.op(DVE, lambda e: e.tensor_tensor(out=tmpK, in0=ps[0][:, 0:128], in1=bmask, op=ALU.mult), reads=[psb[0], b_bmask], writes=[b_tmpK])
            P.op(DVE, lambda e: e.tensor_tensor(out=KL[:, 4:8, :], in0=ps[1][:, 0:512].rearrange("p (t c) -> p t c", t=4), in1=bmb4, op=ALU.mult),
                 reads=[psb[1], b_bmask], wadd=[b_KL[sl]])
            P.op(DVE, lambda e: e.scalar_tensor_tensor(out=KL[:, 0, :], in0=ident, scalar=Dm[:, gc:gc + 1], in1=tmpK, op0=ALU.mult, op1=ALU.add),
                 reads=[b_tmpK, b_ident, b_Dm], wadd=[b_KL[sl]])
            P.op(DVE, lambda e: e.tensor_copy(out=KL0all[:, gc, :], in_=KL[:, 0, :]), reads=[b_KL[sl]], wadd=[b_KL0])

        def emit_x_scan(gc):
            sl = gc % 2
            x_matmuls(gc, XB)
            X = Xs[sl]
            for m in range(4):
                P.op(ACT, lambda e, m=m: e.activation(out=X[:, :, m, :], in_=ps[XB[m]][:, :].rearrange("p (r j) -> p r j", r=2), func=AF.Copy),
                     reads=[psb[XB[m]]], wadd=[b_Xs[sl]])
            E, qi = (POOL, 1) if gc in (1, 4, 6) else (DVE, 0)
            Q1 = Q1e[qi]; Q2 = Q2e[qi]
            X5 = X.rearrange("p r m (s i) -> p r m s i", i=16)
            Ar = pw[:, 8, 0, 4 * gc:4 * gc + 4].unsqueeze(1).unsqueeze(3).to_broadcast([128, 2, 4, 16])
            Ai = pw[:, 8, 1, 4 * gc:4 * gc + 4].unsqueeze(2).to_broadcast([128, 4, 16])
            AiN = coef[:, 4, 4 * gc:4 * gc + 4].unsqueeze(2).to_broadcast([128, 4, 16])
            cview = carry[:, 0:16, :, 4 * gc:4 * gc + 4].rearrange("p s r m -> p r m s")
            for i in range(16):
                prev = cview if i == 0 else X5[:, :, :, :, i - 1]
                cur = X5[:, :, :, :, i]
                rb = [b_carry, b_pw, b_coef, b_Xs[sl]]
                P.op(E, lambda e, prev=prev: e.tensor_tensor(out=Q1, in0=prev, in1=Ar, op=ALU.mult), reads=rb, writes=[b_Q[qi]])
                P.op(E, lambda e, prev=prev: e.tensor_tensor(out=Q2[:, 0], in0=prev[:, 1], in1=AiN, op=ALU.mult), reads=rb, wadd=[b_Q[qi]])
                P.op(E, lambda e, prev=prev: e.tensor_tensor(out=Q2[:, 1], in0=prev[:, 0], in1=Ai, op=ALU.mult), reads=rb, wadd=[b_Q[qi]])
                P.op(E, lambda e, cur=cur: e.tensor_tensor(out=cur, in0=cur, in1=Q1, op=ALU.add), reads=[b_Q[qi]], writes=[b_Xs[sl]])
                P.op(E, lambda e, cur=cur: e.tensor_tensor(out=cur, in0=cur, in1=Q2, op=ALU.add), reads=[b_Q[qi]], writes=[b_Xs[sl]])

        def emit_hb(gc):
            sl = gc % 2
            X = Xs[sl]
            X5 = X.rearrange("p r m (s i) -> p r m s i", i=16)
            cview = carry[:, 0:16, :, 4 * gc:4 * gc + 4].rearrange("p s r m -> p r m s")
            H5 = Hb[sl].rearrange("p r m (s i) -> p r m s i", i=16)
            P.op(ACT, lambda e: e.activation(out=H5[:, :, :, :, 1:16].rearrange("p r m s i -> p (r m) s i"),
                                             in_=X5[:, :, :, :, 0:15].rearrange("p r m s i -> p (r m) s i"), func=AF.Copy),
                 reads=[b_Xs[sl]], writes=[b_Hb[sl]])
            P.op(ACT, lambda e: e.activation(out=H5[:, :, :, :, 0], in_=cview, func=AF.Copy), reads=[b_carry], wadd=[b_Hb[sl]])

        def emit_y(gc):
            sl = gc % 2
            KL = KLs[sl]; Ca = CaBD[sl]
            uview = uT[:, gc, 0:NP].rearrange("p (j s) -> p s j", s=8)
            for half in (1, 0):
                for tl in range(4):
                    t_lo = 4 * half + tl
                    bk_ = YB[tl // 2]
                    reg = ps[bk_][:, 256 * (tl % 2):256 * (tl % 2) + 256]
                    n_mm = (t_lo + 1) + 8
                    idx = 0
                    for s_ in range(t_lo + 1):
                        P.op(PE, lambda e, reg=reg, s_=s_, t_lo=t_lo: e.matmul(
                            reg, lhsT=KL[:, t_lo - s_, :], rhs=uview[:, s_, :], start=(s_ == 0), stop=False),
                            reads=[b_KL[sl], uT_b[gc], b_CaBD[sl], b_Hb[sl]] if idx == 0 else [],
                            writes=[psb[bk_]] if (idx == 0 and tl % 2 == 0) else [], sig=False)
                        idx += 1
                    for m in range(4):
                        for ri in range(2):
                            lastmm = (m == 3 and ri == 1)
                            P.op(PE, lambda e, reg=reg, m=m, ri=ri, t_lo=t_lo, lastmm=lastmm: e.matmul(
                                reg[32 * m:32 * m + 32, :], lhsT=Ca[:, m, t_lo + 1, ri, :], rhs=Hb[sl][:, ri, m, :],
                                start=False, stop=(ri == 1), tile_position=(0, 32 * m)), sig=lastmm)
                    if not P.dead:
                        P.attach(Dep(P.esem[PE], P.cnt[PE]), reads=[b_KL[sl], uT_b[gc], b_CaBD[sl], b_Hb[sl]],
                                 writes=[psb[bk_]] if tl % 2 == 1 else [], wadd=[psb[bk_]] if tl % 2 == 0 else [])
                for bi in range(2):
                    t0_ = 4 * half + 2 * bi
                    P.op(ACT, lambda e, bi=bi, t0_=t0_: e.activation(
                        out=uview[:, t0_:t0_ + 2, :], in_=ps[YB[bi]][:, :].rearrange("p (t j) -> p t j", t=2), func=AF.Gelu_apprx_tanh),
                        reads=[psb[YB[bi]]], writes=[uT_b[gc]])

        for g0 in range(2):
            emit_consts(g0)
            emit_x_scan(g0)
            emit_hb(g0)
        for gc in range(8):
            if gc + 2 < 8:
                emit_x_scan(gc + 2)
            emit_y(gc)
            if gc + 2 < 8:
                emit_consts(gc + 2)
                emit_hb(gc + 2)
        yT = uT
        yT_b = uT_b

        P.phase(9)
        all_ssm_tmp = b_Xs + b_Hb + b_Q + [b_tmpK, b_t, b_p1, b_E, b_Eall, b_te, b_Sn, b_tq, b_pst] + b_Sslot
        stile = [cv_(O_S + 2048 * i, [512], F32) for i in range(2)]
        Hsp = cv_(O_S + 4096, [2, 32, 16], F32); Hn = cv_(O_S + 8192, [2, 32, 16], F32)
        HbS = cv_(O_S + 12288, [2, 32, 16], BF16); Q1s = cv_(O_HB, [2, 32, 16], F32); Q2s = cv_(O_HB + 4096, [2, 32, 16], F32)
        b_stile = Buf(); b_Hsp = Buf(); b_Hn = Buf(); b_HbS = Buf(); b_Qs = Buf()
        handoff([b_stile, b_Hsp, b_Hn, b_HbS, b_Qs], all_ssm_tmp)
        misc_load(SP, stile[0], st_re.rearrange("s (gh f) -> (s gh) f", gh=8), b_stile, wadd=True)
        misc_load(SP, stile[1], st_im.rearrange("s (gh f) -> (s gh) f", gh=8), b_stile, wadd=True)
        for ri in range(2):
            bks = bank()
            for q4 in range(4):
                P.op(PE, lambda e, ri=ri, q4=q4, bks=bks: e.transpose(out=ps[bks][:, 128 * q4:128 * q4 + 128],
                                                                   in_=stile[ri][:, 128 * q4:128 * q4 + 128], identity=ident),
                     reads=[b_stile, b_ident], writes=[psb[bks]] if q4 == 0 else [], wadd=[psb[bks]] if q4 > 0 else [])
            for q4 in range(4):
                P.op(DVE, lambda e, ri=ri, q4=q4, bks=bks: e.tensor_copy(
                    out=Hsp[:, ri, q4:32:4, :], in_=ps[bks][:, 128 * q4:128 * q4 + 128].rearrange("p (s gh) -> p gh s", gh=8)),
                    reads=[psb[bks]], wadd=[b_Hsp])
        P.op(POOL, lambda e: e.tensor_copy(out=HbS, in_=Hsp), reads=[b_Hsp], writes=[b_HbS])
        xsb = [bank() for _ in range(4)]
        for m in range(4):
            for gc in range(8):
                for ri in range(2):
                    first = (gc == 0 and ri == 0); last = (gc == 7 and ri == 1)
                    P.op(PE, lambda e, m=m, gc=gc, ri=ri: e.matmul(
                        ps[xsb[m]][:, 32 * gc + 16 * ri:32 * gc + 16 * ri + 16], lhsT=WinL[32 * m:32 * m + 32, gc, 0, ri, :],
                        rhs=uT[32 * m:32 * m + 32, gc, NP:NT], start=True, stop=True, tile_position=(32 * m, 0)),
                        reads=b_WinL + uT_b if first else [], writes=[psb[xsb[m]]] if first else [], sig=last)
            if not P.dead:
                P.attach(Dep(P.esem[PE], P.cnt[PE]), reads=b_WinL + uT_b, writes=[psb[xsb[m]]])
        Ar1 = pw[:, 1, 0, :].unsqueeze(1).unsqueeze(3).to_broadcast([128, 2, 32, 16])
        Ai1 = pw[:, 1, 1, :].unsqueeze(2).to_broadcast([128, 32, 16])
        P.op(POOL, lambda e: e.tensor_scalar(out=coef[:, 6, :], in0=pw[:, 1, 1, :], scalar1=-1.0, scalar2=0.0, op0=ALU.mult, op1=ALU.add),
             reads=[b_pw], wadd=[b_coef])
        AiN1 = coef[:, 6, :].unsqueeze(2).to_broadcast([128, 32, 16])
        P.op(DVE, lambda e: e.tensor_tensor(out=Q1s, in0=Hsp, in1=Ar1, op=ALU.mult), reads=[b_Hsp, b_pw], writes=[b_Qs])
        P.op(DVE, lambda e: e.tensor_tensor(out=Q2s[:, 0], in0=Hsp[:, 1], in1=AiN1, op=ALU.mult), reads=[b_Hsp, b_coef], wadd=[b_Qs])
        P.op(DVE, lambda e: e.tensor_tensor(out=Q2s[:, 1], in0=Hsp[:, 0], in1=Ai1, op=ALU.mult), reads=[b_Hsp, b_pw], wadd=[b_Qs])
        P.op(DVE, lambda e: e.tensor_tensor(out=Hn, in0=Q1s, in1=Q2s, op=ALU.add), reads=[b_Qs], writes=[b_Hn])
        for m in range(4):
            P.op(DVE, lambda e, m=m: e.tensor_tensor(
                out=Hn[:, :, m:32:4, :], in0=ps[xsb[m]][:, 0:256].rearrange("p (gc r s) -> p r gc s", gc=8, r=2),
                in1=Hn[:, :, m:32:4, :], op=ALU.add), reads=[psb[xsb[m]], b_Hn], writes=[b_Hn])
        stg = cv_(O_HB + 8192 - 8192, [4, 128], F32)
        sout = [cv_(O_S + 2048 * i, [512], F32) for i in range(2)]
        b_stg = Buf(); b_sout = Buf()
        handoff([b_stg], [b_Qs]); handoff([b_sout], [b_stile])
        for ri in range(2):
            for q4 in range(4):
                P.op(POOL, lambda e, ri=ri, q4=q4: e.tensor_copy(out=stg[:, q4, :].rearrange("p (s gh) -> p gh s", gh=8),
                                                               in_=Hn[:, ri, q4:32:4, :]), reads=[b_Hn], writes=[b_stg] if q4 == 0 else [],
                     wadd=[b_stg] if q4 > 0 else [])
            bks = bank()
            for q4 in range(4):
                P.op(PE, lambda e, q4=q4, bks=bks: e.transpose(out=ps[bks][:, 128 * q4:128 * q4 + 128], in_=stg[:, q4, :], identity=ident),
                     reads=[b_stg, b_ident], writes=[psb[bks]] if q4 == 0 else [], wadd=[psb[bks]] if q4 > 0 else [])
            P.op(DVE, lambda e, ri=ri, bks=bks: e.tensor_copy(out=sout[ri], in_=ps[bks][:, 0:512]), reads=[psb[bks]], wadd=[b_sout])
            P.dma(SP, osem_new(f"sst{ri}"), (sst_re if ri == 0 else sst_im).rearrange("s (gh f) -> (s gh) f", gh=8), sout[ri], reads=[b_sout])
        bky = bank()
        for gc in range(8):
            reg = ps[bky][:, 16 * gc:16 * gc + 16]
            P.op(PE, lambda e, gc=gc, reg=reg: e.matmul(reg, lhsT=KL0all[:, gc, :], rhs=uT[:, gc, NP:NT], start=True, stop=False),
                 reads=[b_KL0, b_Ca1, b_HbS] + uT_b if gc == 0 else [], writes=[psb[bky]] if gc == 0 else [], sig=False)
            for m in range(4):
                for ri in range(2):
                    lastmm = (m == 3 and ri == 1)
                    P.op(PE, lambda e, gc=gc, reg=reg, m=m, ri=ri, lastmm=lastmm: e.matmul(
                        reg[32 * m:32 * m + 32, :], lhsT=Ca1all[:, 4 * gc + m, ri, :], rhs=HbS[:, ri, 4 * gc + m, :],
                        start=False, stop=(ri == 1), tile_position=(0, 32 * m)), sig=(lastmm and gc == 7))
        if not P.dead:
            P.attach(Dep(P.esem[PE], P.cnt[PE]), reads=[b_KL0, b_Ca1, b_HbS] + uT_b, writes=[psb[bky]])
        P.op(ACT, lambda e: e.activation(out=uT[:, :, NP:NT], in_=ps[bky][:, 0:128].rearrange("p (g s) -> p g s", g=8),
                                         func=AF.Gelu_apprx_tanh), reads=[psb[bky]], writes=uT_b)

        P.phase(10)
        s2T = RB
        s2_b = [Buf(f"s2_{g}") for g in range(8)]
        handoff(s2_b, b_WinL)
        gtmp = [cv_(O_C + 1024 * i, [512], BF16) for i in range(4)]
        ftmp = [cv_(O_C + 4096 + 2048 * i, [512], F32) for i in range(2)]
        b_gtmp = [Buf() for _ in range(4)]; b_ftmp = [Buf(), Buf()]
        handoff(b_gtmp + b_ftmp, [b_pw, b_bb, b_ccm])
        rhs_y = lambda k, t0, n: yT[:, k, t0:t0 + n]
        yall_b = [yT_b] * 5
        ctr = {"i": 0}

        class AllOf:
            pass
        for blk in range(2):
            def ev_glu(oc, tbi, pap, pb, blk=blk):
                t0, n = TBS[tbi]
                g = 4 * blk + oc
                ctr["i"] += 1
                gi = ctr["i"] % 4
                P.op(ACT, lambda e: e.activation(out=gtmp[gi][:, 0:n], in_=pap, func=AF.Sigmoid, bias=bglu[:, g:g + 1], scale=1.0),
                     reads=[pb, b_bglu], writes=[b_gtmp[gi]])
                P.op(DVE, lambda e: e.tensor_tensor(out=s2T[:, g, t0:t0 + n], in0=yT[:, g, t0:t0 + n], in1=gtmp[gi][:, 0:n], op=ALU.mult),
                     reads=[b_gtmp[gi], yT_b[g]], wadd=[s2_b[g]])
            proj_fm(w_glu[:, 512 * blk:512 * blk + 512], 512, rhs_y, [BufGroup(yT_b)] * 5, ev_glu)

            def ev_zs(oc, tbi, pap, pb, blk=blk):
                t0, n = TBS[tbi]
                g = 4 * blk + oc
                ctr["i"] += 1
                gi = ctr["i"] % 4
                fi = ctr["i"] % 2
                P.op(ACT, lambda e: e.activation(out=gtmp[gi][:, 0:n], in_=pap, func=AF.Sigmoid), reads=[pb], writes=[b_gtmp[gi]])
                P.op(DVE, lambda e: e.tensor_tensor(out=ftmp[fi][:, 0:n], in0=pap, in1=gtmp[gi][:, 0:n], op=ALU.mult),
                     reads=[pb, b_gtmp[gi]], writes=[b_ftmp[fi]])
                P.op(POOL, lambda e: e.tensor_tensor(out=s2T[:, g, t0:t0 + n], in0=s2T[:, g, t0:t0 + n], in1=ftmp[fi][:, 0:n], op=ALU.mult),
                     reads=[b_ftmp[fi], s2_b[g]], wadd=[s2_b[g]])
            proj_fm(w_in[:, OFF_ZS + 512 * blk:OFF_ZS + 512 * blk + 512], 512, rhs_h, hT_b, ev_zs)

        P.phase(11)
        gbs = RA
        gbs_b = [Buf(f"gbs{g}") for g in range(8)]
        handoff(gbs_b, yT_b)
        rhs_s2 = lambda k, t0, n: s2T[:, k, t0:t0 + n]
        for blk in range(2):
            def ev_gs(oc, tbi, pap, pb, blk=blk):
                t0, n = TBS[tbi]
                g = 4 * blk + oc
                P.op(ACT, lambda e: e.activation(out=gbs[:, g, t0:t0 + n], in_=pap, func=AF.Sigmoid), reads=[pb], wadd=[gbs_b[g]])
            proj_fm(w_in[:, OFF_GS + 512 * blk:OFF_GS + 512 * blk + 512], 512, rhs_h, hT_b, ev_gs)

            def ev_bs(oc, tbi, pap, pb, blk=blk):
                t0, n = TBS[tbi]
                g = 4 * blk + oc
                P.op(DVE, lambda e: e.tensor_tensor(out=gbs[:, g, t0:t0 + n], in0=pap, in1=gbs[:, g, t0:t0 + n], op=ALU.mult),
                     reads=[pb, gbs_b[g]], wadd=[gbs_b[g]])
            proj_fm(w_bs[:, 512 * blk:512 * blk + 512], 512, rhs_s2, [BufGroup(s2_b)] * 5, ev_bs)

        P.phase(12)
        oT = RB
        oT_b = [Buf(f"oT{g}") for g in range(8)]
        handoff(oT_b, s2_b)
        oc_ = O_C
        qT = cv_(oc_ + 0, [2, NT], BF16); kT2 = cv_(oc_ + 8256, [128 + NT], BF16); Vaug = cv_(oc_ + 12640, [18, 128], BF16)
        EB = cv_(oc_ + 17248, [2, 16, 128], BF16); EB0 = cv_(oc_ + 25440, [16, 128], BF16)
        Et = [cv_(oc_ + 29536 + 1024 * i, [512], BF16) for i in range(4)]
        PT = [cv_(oc_ + 33632 + 1024 * i, [512], BF16) for i in range(4)]
        rc = [cv_(oc_ + 37728 + 2048 * i, [512], F32) for i in range(2)]
        maskt = cv_(oc_ + 41824, [2, 128], F32); RT = cv_(oc_ + 42848, [384], F32); relb = cv_(oc_ + 44384, [16], F32)
        es16 = cv_(oc_ + 44448, [16], F32); klast = cv_(oc_ + 44512, [256], F32); vlast = cv_(oc_ + 45536, [256], F32)
        knew = cv_(oc_ + 46560, [256], F32); vnew = cv_(oc_ + 47584, [256], F32)
        Kc = cv_(oc_ + 48608, [16, 256], F32)
        KcT = cv_(oc_ + 64992, [16, 2, 128], BF16)
        Vcs = cv_(oc_ + 73184, [16, 256], BF16)
        QsT = cv_(oc_ + 81376, [2, 4, 16], BF16)
        dgt = cv_(oc_ + 81632, [64], F32); vnb = cv_(oc_ + 81888, [256], BF16); pdg = cv_(oc_ + 82400, [64], BF16)
        esr = cv_(oc_ + 82528, [64], BF16); ebs = cv_(oc_ + 82656, [16], F32); rcs = cv_(oc_ + 82720, [128], F32)
        ptS = cv_(oc_ + 83232, [128], BF16); ones_k = cv_(oc_ + 83488, [128], BF16)
        attn_bufs = {n: Buf(n) for n in ["qT", "kT2", "Vaug", "EB", "EB0", "mask", "RT", "relb", "es16", "klast", "vlast", "knew", "vnew",
                                         "Kc", "KcT", "Vcs", "QsT", "dgt", "vnb", "pdg", "esr", "ebs", "rcs", "ptS", "ones_k"]}
        A = attn_bufs
        b_Et = [Buf() for _ in range(4)]; b_PT = [Buf() for _ in range(4)]; b_rc = [Buf(), Buf()]
        prev_c = [b_pw, b_bb, b_ccm, b_R8, b_R128, b_A2k, b_Hend, b_carry, b_Sb, b_coef, b_misc, b_t, b_KL0, b_Ca1,
                  b_stile, b_Hsp, b_Hn, b_HbS, b_Qs, b_stg, b_sout] + b_gtmp + b_ftmp + all_ssm_tmp + b_CaBD + b_KL
        handoff(list(A.values()) + b_Et + b_PT + b_rc, prev_c)
        misc_load(SP, RT[0:32, :], rtab, A["RT"]); misc_load(SP, relb[0:32, :], rel_bias, A["relb"])
        misc_load(SP, maskt.rearrange("p h q -> p (h q)"), maskc, A["mask"])
        misc_load(SP, es16[0:1, :], sinks.rearrange("(o n) -> o n", o=1), A["es16"])
        misc_load(SP, ebs[0:16, :], rel_bias[0:1, :].to_broadcast([16, 16]), A["ebs"])
        misc_load(SP, dgt[0:16, :], diagc, A["dgt"])
        P.op(ACT, lambda e: e.activation(out=es16[0:1, :], in_=es16[0:1, :], func=AF.Exp), reads=[A["es16"]], writes=[A["es16"]])
        P.op(ACT, lambda e: e.activation(out=ebs[0:16, :], in_=ebs[0:16, :], func=AF.Exp), reads=[A["ebs"]], writes=[A["ebs"]])
        for kv in range(4):
            for sl_, i in enumerate([0, 2, 1, 3]):
                h = 4 * kv + i
                P.op(DVE, lambda e, kv=kv, sl_=sl_, h=h: e.tensor_copy(out=ES[0:1, kv, sl_, :], in_=es16[0:1, h:h + 1].to_broadcast([1, 128])),
                     reads=[A["es16"]], wadd=[b_ES])
        P.op(POOL, lambda e: e.memset(ones_k, 1.0), writes=[A["ones_k"]])
        for half in range(2):
            for qb in range(4):
                bke = bank()
                for qq in range(32):
                    q = 32 * qb + qq
                    st_ = (127 - q) if half == 0 else (255 - q)
                    P.op(PE, lambda e, bke=bke, qq=qq, st_=st_: e.matmul(ps[bke][:, 16 * qq:16 * qq + 16], lhsT=RT[0:32, st_:st_ + 128],
                                                                       rhs=relb[0:32, :], start=True, stop=True),
                         reads=[A["RT"], A["relb"]] if qq == 0 else [], writes=[psb[bke]] if qq == 0 else [], sig=(qq == 31))
                if not P.dead:
                    P.attach(Dep(P.esem[PE], P.cnt[PE]), reads=[A["RT"], A["relb"]], writes=[psb[bke]])
                P.op(ACT, lambda e, bke=bke, half=half, qb=qb: e.activation(
                    out=EB[:, half, :, 32 * qb:32 * qb + 32], in_=ps[bke][:, 0:512].rearrange("p (q h) -> p h q", h=16), func=AF.Exp),
                    reads=[psb[bke]], wadd=[A["EB"]])
        P.op(DVE, lambda e: e.tensor_tensor(out=EB, in0=EB, in1=maskt.unsqueeze(2).to_broadcast([128, 2, 16, 128]), op=ALU.mult),
             reads=[A["EB"], A["mask"]], writes=[A["EB"]])
        P.op(DVE, lambda e: e.tensor_scalar(out=EB0, in0=EB[:, 0], scalar1=flags[:, 0:1], scalar2=None, op0=ALU.mult),
             reads=[A["EB"], b_flags], writes=[A["EB0"]])
        P.dma(SP, dout_sem, sck[:, 0:127, :], ck[:, 1:128, :])
        P.dma(SP, dout_sem, scv[:, 0:127, :], cv[:, 1:128, :])
        kc_sem = P.dsem("kc")
        P.dma(SP, kc_sem, Kc, ck.rearrange("s t f -> t s f"), writes=[A["Kc"]])
        for s_ in range(NS):
            bkt = bank()
            for kvp in range(2):
                P.op(PE, lambda e, s_=s_, kvp=kvp, bkt=bkt: e.transpose(out=ps[bkt][:, 128 * kvp:128 * kvp + 128],
                                                                     in_=Kc[:, s_, 128 * kvp:128 * kvp + 128], identity=ident),
                     reads=[A["Kc"], b_ident], writes=[psb[bkt]] if kvp == 0 else [], wadd=[psb[bkt]] if kvp == 1 else [])
            evac_copy(s_, KcT[:, s_].rearrange("p a t -> p (a t)"), ps[bkt][:, 0:256], [psb[bkt]], [], wadd=[A["KcT"]])
        P.dma(SP, kc_sem, Kc, cv.rearrange("s t f -> t s f"), reads=[A["KcT"]], writes=[A["Kc"]])
        P.op(POOL, lambda e: e.tensor_copy(out=Vcs, in_=Kc), reads=[A["Kc"]], writes=[A["Vcs"]])
        P.op(POOL, lambda e: e.memset(Vaug[:, :, 64:128], 1.0), writes=[A["Vaug"]])

        rhs_hh = lambda k, t0, n: hTh[:, k, 0:n]
        TB5 = TBS
        def attn_kv(kv):
            def ev_q(oc, tbi, pap, pb):
                t0, n = TBS[tbi]
                ctr["i"] += 1
                evac_copy(ctr["i"], qT[:, oc, t0:t0 + n], pap, [pb], [], wadd=[A["qT"]])
            A["qT"].r = list(A["qT"].r) + list(A["qT"].w); A["qT"].w = []
            proj_fm(w_in[:, OFF_Q + 256 * kv:OFF_Q + 256 * kv + 256], 256, rhs_h, hT_b, ev_q)
            s = wctr[0] % 2
            wctr[0] += 1
            for dup in range(2):
                P.dma(POOL, wsem[s], wslot[s][:, :, 64 * dup:64 * dup + 64],
                      w_in[:, OFF_K + 64 * kv:OFF_K + 64 * kv + 64].rearrange("(k p) f -> p k f", p=128),
                      writes=[wslot_b[s]] if dup == 0 else [], wadd=[wslot_b[s]] if dup == 1 else [])
            A["kT2"].r = list(A["kT2"].r) + list(A["kT2"].w); A["kT2"].w = []
            kblocks = [(hTh, hTh_b, 0, 128, 0)] + [(hT, hT_b[i], t0, n, 128 + t0) for i, (t0, n) in enumerate(TBS)]
            for bi_, (src, sb_, t0, n, c0) in enumerate(kblocks):
                bkk = bank()
                for k in range(8):
                    P.op(PE, lambda e, bkk=bkk, k=k, src=src, t0=t0, n=n, s=s: e.matmul(
                        ps[bkk][:, 0:n], lhsT=wslot[s][:, k, 0:128], rhs=src[:, k, t0:t0 + n], start=(k == 0), stop=(k == 7)),
                        reads=[wslot_b[s], sb_] if k == 0 else [], writes=[psb[bkk]] if k == 0 else [], sig=(k == 7))
                if not P.dead:
                    P.attach(Dep(P.esem[PE], P.cnt[PE]), reads=[wslot_b[s], sb_], writes=[psb[bkk]])
                evac_copy(bi_, kT2[:, c0:c0 + n], ps[bkk][:, 0:n], [psb[bkk]], [], wadd=[A["kT2"]])
            bkl_ = bank()
            for j_, (c0_, m_) in enumerate([(NP - 128, 128), (NP, NS)]):
                for k in range(8):
                    P.op(PE, lambda e, j_=j_, c0_=c0_, m_=m_, k=k, s=s: e.matmul(
                        ps[bkl_][0:m_, 64 * j_:64 * j_ + 64], lhsT=hT[:, k, c0_:c0_ + m_], rhs=wslot[s][:, k, 0:64],
                        start=(k == 0), stop=(k == 7)),
                        reads=[wslot_b[s], hT_b[3], hT_b[4]] if (k == 0 and j_ == 0) else [],
                        writes=[psb[bkl_]] if (k == 0 and j_ == 0) else [], sig=(k == 7 and j_ == 1))
            if not P.dead:
                P.attach(Dep(P.esem[PE], P.cnt[PE]), reads=[wslot_b[s], hT_b[3], hT_b[4]], writes=[psb[bkl_]])
            P.op(DVE, lambda e, kv=kv: e.tensor_copy(out=klast[:, 64 * kv:64 * kv + 64], in_=ps[bkl_][:, 0:64]), reads=[psb[bkl_]], wadd=[A["klast"]])
            P.op(DVE, lambda e, kv=kv: e.tensor_copy(out=knew[0:NS, 64 * kv:64 * kv + 64], in_=ps[bkl_][0:NS, 64:128]), reads=[psb[bkl_]], wadd=[A["knew"]])
            s = wctr[0] % 2
            wctr[0] += 1
            P.dma(POOL, wsem[s], wslot[s][:, :, 0:64], w_in[:, OFF_V + 64 * kv:OFF_V + 64 * kv + 64].rearrange("(k p) f -> p k f", p=128),
                  writes=[wslot_b[s]])
            A["Vaug"].r = list(A["Vaug"].r) + list(A["Vaug"].w); A["Vaug"].w = []
            vtiles = [(hTh, hTh_b, 0, 128)] + [(hT, hT_b[i // 4], 128 * i, 128) for i in range(16)] + [(hT, hT_b[4], NP, NS)]
            for grp in range(3):
                bkv = bank()
                tl_ = vtiles[8 * grp:8 * grp + 8]
                for j_, (src, sb_, c0_, m_) in enumerate(tl_):
                    for k in range(8):
                        firstg = (j_ == 0 and k == 0)
                        P.op(PE, lambda e, bkv=bkv, j_=j_, src=src, c0_=c0_, m_=m_, k=k, s=s: e.matmul(
                            ps[bkv][0:m_, 64 * j_:64 * j_ + 64], lhsT=src[:, k, c0_:c0_ + m_], rhs=wslot[s][:, k, 0:64],
                            start=(k == 0), stop=(k == 7)),
                            reads=[wslot_b[s], sb_, hTh_b] + hT_b if firstg else [], writes=[psb[bkv]] if firstg else [],
                            sig=(j_ == len(tl_) - 1 and k == 7))
                if not P.dead:
                    P.attach(Dep(P.esem[PE], P.cnt[PE]), reads=[wslot_b[s], hTh_b] + hT_b, writes=[psb[bkv]])
                nt_ = len(tl_)
                if grp < 2:
                    P.op(ACT, lambda e, bkv=bkv, grp=grp: e.activation(out=Vaug[:, 8 * grp:8 * grp + 8, 0:64],
                                                                     in_=ps[bkv][:, 0:512].rearrange("p (t d) -> p t d", d=64), func=AF.Copy),
                         reads=[psb[bkv]], wadd=[A["Vaug"]])
                    if grp == 1:
                        pass
                else:
                    P.op(ACT, lambda e, bkv=bkv: e.activation(out=Vaug[:, 16, 0:64], in_=ps[bkv][:, 0:64], func=AF.Copy),
                         reads=[psb[bkv]], wadd=[A["Vaug"]])
                    P.op(ACT, lambda e, bkv=bkv: e.activation(out=Vaug[0:NS, 17, 0:64], in_=ps[bkv][0:NS, 64:128], func=AF.Copy),
                         reads=[psb[bkv]], wadd=[A["Vaug"]])
                    P.op(ACT, lambda e, bkv=bkv, kv=kv: e.activation(out=vlast[:, 64 * kv:64 * kv + 64], in_=ps[bkv][:, 0:64], func=AF.Copy),
                         reads=[psb[bkv]], wadd=[A["vlast"]])
                    P.op(ACT, lambda e, bkv=bkv, kv=kv: e.activation(out=vnew[0:NS, 64 * kv:64 * kv + 64], in_=ps[bkv][0:NS, 64:128], func=AF.Copy),
                         reads=[psb[bkv]], wadd=[A["vnew"]])
            qv = lambda base, b_: qT[base:base + 64, 0:2, 128 * b_:128 * b_ + 128]
            def attn_s1(b_):
                ia = (2 * b_) % 4; ib = (2 * b_ + 1) % 4
                bA, bB = bank(), bank()
                kprev = slice(128 * b_, 128 * b_ + 128); kcur = slice(128 * b_ + 128, 128 * b_ + 256)
                seq = [(bA, 0, 0, kprev), (bB, 64, 0, kprev), (bA, 0, 1, kcur), (bB, 64, 1, kcur)]
                for (bk_, base, half, ks) in seq:
                    first = (half == 0)
                    P.op(PE, lambda e, bk_=bk_, base=base, half=half, ks=ks, b_=b_: e.matmul(
                        ps[bk_][:, 256 * half:256 * half + 256], lhsT=kT2[base:base + 64, ks], rhs=qv(base, b_), start=True, stop=True),
                        reads=[A["kT2"], A["qT"]] if first else [], writes=[psb[bk_]] if first else [], sig=(half == 1))
                    if half == 1 and not P.dead:
                        P.attach(Dep(P.esem[PE], P.cnt[PE]), reads=[A["kT2"], A["qT"]], writes=[psb[bk_]])
                for (bk_, ie, base_h) in [(bA, ia, 0), (bB, ib, 1)]:
                    P.op(ACT, lambda e, bk_=bk_, ie=ie: e.activation(out=Et[ie], in_=ps[bk_][:, :], func=AF.Exp, scale=0.125),
                         reads=[psb[bk_]], writes=[b_Et[ie]])
                    Ev = Et[ie].rearrange("p (h i q) -> p h i q", h=2, i=2)
                    Pv = PT[ie].rearrange("p (h i q) -> p h i q", h=2, i=2)
                    h0 = 4 * kv + base_h
                    eng = DVE if base_h == 0 else POOL
                    if b_ > 0:
                        P.op(eng, lambda e, Ev=Ev, Pv=Pv, h0=h0: e.tensor_tensor(out=Pv, in0=Ev, in1=EB[:, :, h0:h0 + 3:2, :], op=ALU.mult),
                             reads=[b_Et[ie], A["EB"]], writes=[b_PT[ie]])
                    else:
                        P.op(eng, lambda e, Ev=Ev, Pv=Pv, h0=h0: e.tensor_tensor(out=Pv[:, 0], in0=Ev[:, 0], in1=EB0[:, h0:h0 + 3:2, :], op=ALU.mult),
                             reads=[b_Et[ie], A["EB0"]], writes=[b_PT[ie]])
                        P.op(eng, lambda e, Ev=Ev, Pv=Pv, h0=h0: e.tensor_tensor(out=Pv[:, 1], in0=Ev[:, 1], in1=EB[:, 1, h0:h0 + 3:2, :], op=ALU.mult),
                             reads=[b_Et[ie], A["EB"]], wadd=[b_PT[ie]])

            def attn_s2(b_):
                ia = (2 * b_) % 4; ib = (2 * b_ + 1) % 4
                bO = bank()
                mm = [(ia, 0, b_, True), (ia, 1, b_ + 1, False), (ib, 0, b_, False), (ib, 1, b_ + 1, False)]
                for j_, (ip, half, tile, st_) in enumerate(mm):
                    cols = slice(0, 256) if ip == ia else slice(256, 512)
                    P.op(PE, lambda e, ip=ip, half=half, tile=tile, st_=st_, cols=cols: e.matmul(
                        ps[bO][:, cols], lhsT=Vaug[:, tile, :], rhs=PT[ip][:, 256 * half:256 * half + 256], start=st_, stop=False),
                        reads=[A["Vaug"], b_PT[ia], b_PT[ib], b_ES, b_ones] if j_ == 0 else [], writes=[psb[bO]] if j_ == 0 else [], sig=False)
                P.op(PE, lambda e, kv=kv: e.matmul(ps[bO][:, 0:512], lhsT=onesd[0:1, :], rhs=ES[0:1, kv].rearrange("p i q -> p (i q)"),
                                                   start=False, stop=True), sig=True)
                if not P.dead:
                    P.attach(Dep(P.esem[PE], P.cnt[PE]), reads=[A["Vaug"], b_PT[ia], b_PT[ib], b_ES, b_ones], writes=[psb[bO]])
                ir = b_ % 2
                P.op(ACT, lambda e, ir=ir: e.activation(out=rc[ir][64:128, :], in_=ps[bO][64:128, :], func=AF.Ln), reads=[psb[bO]], writes=[b_rc[ir]])
                P.op(ACT, lambda e, ir=ir: e.activation(out=rc[ir][64:128, :], in_=rc[ir][64:128, :], func=AF.Exp, scale=-1.0),
                     reads=[b_rc[ir]], writes=[b_rc[ir]])
                for par in range(2):
                    P.op(DVE, lambda e, par=par, ir=ir, b_=b_, kv=kv: e.tensor_tensor(
                        out=oT[64 * par:64 * par + 64, 2 * kv:2 * kv + 2, 128 * b_:128 * b_ + 128],
                        in0=ps[bO][0:64, 256 * par:256 * par + 256].rearrange("p (c q) -> p c q", c=2),
                        in1=rc[ir][64:128, 256 * par:256 * par + 256].rearrange("p (c q) -> p c q", c=2), op=ALU.mult),
                        reads=[psb[bO], b_rc[ir]], wadd=[oT_b[2 * kv], oT_b[2 * kv + 1]])

            attn_s1(0)
            for b_ in range(1, 16):
                attn_s1(b_)
                attn_s2(b_ - 1)
            attn_s2(15)
            base = 64 * (kv % 2)
            for i in range(4):
                hsrc = 64 * (i % 2)
                P.op(POOL, lambda e, i=i, hsrc=hsrc, base=base: e.tensor_copy(out=QsT[base:base + 64, 0, i, :], in_=qT[hsrc:hsrc + 64, i // 2, NP:NT]),
                     reads=[A["qT"]], writes=[A["QsT"]] if i == 0 else [], wadd=[A["QsT"]] if i > 0 else [])
            for sl_, i in enumerate([0, 1, 2, 3]):
                P.op(DVE, lambda e, i=i, kv=kv: e.tensor_copy(out=esr[0:1, :].rearrange("p (s i) -> p s i", i=4)[:, :, i],
                                                            in_=es16[0:1, 4 * kv + i:4 * kv + i + 1].to_broadcast([1, 16])),
                     reads=[A["es16"]], writes=[A["esr"]] if i == 0 else [], wadd=[A["esr"]] if i > 0 else [])
            bS, bD, bN = bank(), bank(), bank()
            Qsi = QsT[base:base + 64, 0].rearrange("p i s -> p s i")
            for s_ in range(NS):
                P.op(PE, lambda e, s_=s_, base=base, kv=kv: e.matmul(ps[bS][:, 4 * s_:4 * s_ + 4], lhsT=KcT[base:base + 64, s_, kv // 2, :],
                                                                  rhs=QsT[base:base + 64, 0, :, s_], start=True, stop=True),
                     reads=[A["KcT"], A["QsT"], A["kT2"]] if s_ == 0 else [], writes=[psb[bS]] if s_ == 0 else [], sig=False)
            P.op(PE, lambda e, base=base: e.matmul(ps[bS][0:NS, 64:128], lhsT=kT2[base:base + 64, 128 + NP:128 + NT], rhs=Qsi, start=True, stop=True), sig=True)
            if not P.dead:
                P.attach(Dep(P.esem[PE], P.cnt[PE]), reads=[A["KcT"], A["QsT"], A["kT2"]], writes=[psb[bS]])
            P.op(ACT, lambda e: e.activation(out=ptS[:, 0:64], in_=ps[bS][:, 0:64], func=AF.Exp, scale=0.125), reads=[psb[bS]], writes=[A["ptS"]])
            P.op(ACT, lambda e: e.activation(out=pdg[0:NS, :], in_=ps[bS][0:NS, 64:128], func=AF.Exp, scale=0.125), reads=[psb[bS]], writes=[A["pdg"]])
            P.op(DVE, lambda e, kv=kv: e.tensor_tensor(out=ptS[:, 0:64].rearrange("p (s i) -> p s i", i=4), in0=ptS[:, 0:64].rearrange("p (s i) -> p s i", i=4),
                                                in1=EB[:, 0, 4 * kv:4 * kv + 4, 0].unsqueeze(1).to_broadcast([128, 16, 4]), op=ALU.mult),
                 reads=[A["ptS"], A["EB"]], writes=[A["ptS"]])
            P.op(DVE, lambda e, kv=kv: e.tensor_tensor(out=pdg[0:NS, :].rearrange("p (s i) -> p s i", i=4), in0=pdg[0:NS, :].rearrange("p (s i) -> p s i", i=4),
                                                in1=ebs[0:NS, 4 * kv:4 * kv + 4].unsqueeze(1).to_broadcast([NS, 16, 4]), op=ALU.mult),
                 reads=[A["pdg"], A["ebs"]], writes=[A["pdg"]])
            P.op(DVE, lambda e: e.tensor_tensor(out=pdg[0:NS, :], in0=pdg[0:NS, :], in1=dgt[0:NS, :], op=ALU.mult),
                 reads=[A["pdg"], A["dgt"]], writes=[A["pdg"]])
            P.op(POOL, lambda e, kv=kv: e.tensor_copy(out=vnb[0:NS, 64 * kv:64 * kv + 64], in_=vnew[0:NS, 64 * kv:64 * kv + 64]),
                 reads=[A["vnew"]], writes=[A["vnb"]])
            P.op(PE, lambda e: e.matmul(ps[bD][:, 0:64], lhsT=ones_k, rhs=ptS[:, 0:64], start=True, stop=False),
                 reads=[A["ones_k"], A["ptS"], A["pdg"], A["esr"]], writes=[psb[bD]], sig=False)
            P.op(PE, lambda e: e.matmul(ps[bD][:, 0:64], lhsT=ones_k[0:NS, :], rhs=pdg[0:NS, :], start=False, stop=False), sig=False)
            P.op(PE, lambda e: e.matmul(ps[bD][:, 0:64], lhsT=ones_k[0:1, :], rhs=esr[0:1, :], start=False, stop=True), sig=True)
            if not P.dead:
                P.attach(Dep(P.esem[PE], P.cnt[PE]), reads=[A["ones_k"], A["ptS"], A["pdg"], A["esr"]], writes=[psb[bD]])
            ptv = ptS[:, 0:64].rearrange("p (s i) -> p s i", i=4)
            pdv = pdg[0:NS, :].rearrange("p (s i) -> p s i", i=4)
            psn = ps[bN][:, 0:64].rearrange("p (s i) -> p s i", i=4)
            for par in range(2):
                for s_ in range(NS):
                    P.op(PE, lambda e, par=par, s_=s_, kv=kv: e.matmul(
                        psn[64 * par:64 * par + 64, s_, par:4:2], lhsT=Vcs[:, s_, 64 * kv:64 * kv + 64], rhs=ptv[:, s_, par:4:2],
                        start=(s_ == 0), stop=False, tile_position=(0, 64 * par)),
                        reads=[A["Vcs"], A["ptS"], A["pdg"], A["vnb"]] if (par == 0 and s_ == 0) else [],
                        writes=[psb[bN]] if (par == 0 and s_ == 0) else [], sig=False)
                P.op(PE, lambda e, par=par, kv=kv: e.matmul(
                    psn[64 * par:64 * par + 64, :, par:4:2], lhsT=vnb[0:NS, 64 * kv:64 * kv + 64], rhs=pdv[:, :, par:4:2],
                    start=False, stop=True, tile_position=(0, 64 * par)), sig=(par == 1))
            if not P.dead:
                P.attach(Dep(P.esem[PE], P.cnt[PE]), reads=[A["Vcs"], A["ptS"], A["pdg"], A["vnb"]], writes=[psb[bN]])
            P.op(DVE, lambda e: e.reciprocal(out=rcs[:, 0:64], in_=ps[bD][:, 0:64]), reads=[psb[bD]], writes=[A["rcs"]])
            rcv = rcs[:, 0:64].rearrange("p (s i) -> p s i", i=4)
            for par in range(2):
                P.op(DVE, lambda e, par=par, kv=kv: e.tensor_tensor(
                    out=oT[64 * par:64 * par + 64, 2 * kv:2 * kv + 2, NP:NT],
                    in0=psn[64 * par:64 * par + 64, :, par:4:2].rearrange("p s c -> p c s"),
                    in1=rcv[64 * par:64 * par + 64, :, par:4:2].rearrange("p s c -> p c s"), op=ALU.mult),
                    reads=[psb[bN], A["rcs"]], wadd=[oT_b[2 * kv], oT_b[2 * kv + 1]])
        for kv in range(4):
            attn_kv(kv)
        P.dma(SP, osem_new("pck"), pck, klast, reads=[A["klast"]])
        P.dma(SP, osem_new("pcv"), pcv, vlast, reads=[A["vlast"]])
        P.dma(SP, osem_new("sck"), sck[:, 127, :], knew[0:NS, :], reads=[A["knew"]])
        P.dma(SP, osem_new("scv"), scv[:, 127, :], vnew[0:NS, :], reads=[A["vnew"]])

        P.phase(13)
        sgaT = cv_(O_C + 0, [8, NT], BF16)
        sga_b = [Buf(f"sga{g}") for g in range(8)]
        gt2 = [cv_(O_C + 33024 + 1024 * i, [512], BF16) for i in range(4)]
        ft2 = [cv_(O_C + 37120 + 2048 * i, [512], F32) for i in range(2)]
        b_gt2 = [Buf() for _ in range(4)]; b_ft2 = [Buf(), Buf()]
        handoff(sga_b + b_gt2 + b_ft2, list(A.values()) + b_Et + b_PT + b_rc)
        for blk in range(2):
            def ev_za(oc, tbi, pap, pb, blk=blk):
                t0, n = TBS[tbi]
                g = 4 * blk + oc
                ctr["i"] += 1
                gi = ctr["i"] % 4; fi = ctr["i"] % 2
                P.op(ACT, lambda e: e.activation(out=gt2[gi][:, 0:n], in_=pap, func=AF.Sigmoid), reads=[pb], writes=[b_gt2[gi]])
                P.op(DVE, lambda e: e.tensor_tensor(out=ft2[fi][:, 0:n], in0=pap, in1=gt2[gi][:, 0:n], op=ALU.mult),
                     reads=[pb, b_gt2[gi]], writes=[b_ft2[fi]])
                P.op(POOL, lambda e: e.tensor_tensor(out=oT[:, g, t0:t0 + n], in0=oT[:, g, t0:t0 + n], in1=ft2[fi][:, 0:n], op=ALU.mult),
                     reads=[b_ft2[fi], oT_b[g]], wadd=[oT_b[g]])
            proj_fm(w_in[:, OFF_ZA + 512 * blk:OFF_ZA + 512 * blk + 512], 512, rhs_h, hT_b, ev_za)
        for blk in range(2):
            def ev_ga(oc, tbi, pap, pb, blk=blk):
                t0, n = TBS[tbi]
                g = 4 * blk + oc
                P.op(ACT, lambda e: e.activation(out=sgaT[:, g, t0:t0 + n], in_=pap, func=AF.Sigmoid), reads=[pb], wadd=[sga_b[g]])
            proj_fm(w_in[:, OFF_GA + 512 * blk:OFF_GA + 512 * blk + 512], 512, rhs_h, hT_b, ev_ga)
        mT = RA
        mT_b = gbs_b
        rhs_o = lambda k, t0, n: oT[:, k, t0:t0 + n]
        for blk in range(2):
            def ev_ba(oc, tbi, pap, pb, blk=blk):
                t0, n = TBS[tbi]
                g = 4 * blk + oc
                ctr["i"] += 1
                fi = ctr["i"] % 2
                P.op(DVE, lambda e: e.tensor_tensor(out=ft2[fi][:, 0:n], in0=pap, in1=sgaT[:, g, t0:t0 + n], op=ALU.mult),
                     reads=[pb, sga_b[g]], writes=[b_ft2[fi]])
                P.op(POOL, lambda e: e.tensor_tensor(out=mT[:, g, t0:t0 + n], in0=mT[:, g, t0:t0 + n], in1=ft2[fi][:, 0:n], op=ALU.add),
                     reads=[b_ft2[fi], mT_b[g]], wadd=[mT_b[g]])
            proj_fm(w_ba[:, 512 * blk:512 * blk + 512], 512, rhs_o, [BufGroup(oT_b)] * 5, ev_ba)

        P.phase(14)
        o2 = O_C + 8000
        NSL = 4
        GateB = cv_(o2 + 0, [1024], F32); LnG = cv_(o2 + 4096, [1024], F32); LnB = cv_(o2 + 8192, [1024], F32)
        gateS = cv_(o2 + 12288, [1024], F32); grow = cv_(o2 + 16384, [1024], F32)
        xt = [cv_(o2 + 20480 + 4096 * i, [1024], F32) for i in range(NSL)]
        rt = [cv_(o2 + 36864 + 4096 * i, [1024], F32) for i in range(NSL)]
        stt = cv_(o2 + 53248, [NSL, 2, 6], F32); mvt = cv_(o2 + 53504, [NSL, 2], F32); rsd = cv_(o2 + 53568, [NSL, 2], F32)
        mhalf = cv_(o2 + 53632, [1], F32)
        b_GateB = Buf(); b_LnG = Buf(); b_LnB = Buf(); b_gateS = Buf(); b_grow = Buf(); b_xt = [Buf() for _ in range(NSL)]; b_rt = [Buf() for _ in range(NSL)]
        b_stt = [Buf() for _ in range(NSL)]; b_mh = Buf()
        xsem2 = [P.dsem(f"xt{i}") for i in range(NSL)]; osem = [P.dsem(f"o{i}") for i in range(NSL)]
        handoff([b_GateB, b_LnG, b_LnB, b_gateS, b_grow] + b_xt + b_rt + b_stt + [b_mh], list(A.values()) + b_Et + b_PT + b_rc + sga_b + b_gt2 + b_ft2)
        misc_load(SP, LnG, ln_g.rearrange("(o n) -> o n", o=1).to_broadcast([128, 1024]), b_LnG)
        misc_load(SP, LnB, ln_b.rearrange("(o n) -> o n", o=1).to_broadcast([128, 1024]), b_LnB)
        P.op(POOL, lambda e: e.memset(mhalf, -0.5), writes=[b_mh])
        for hb in range(2):
            bkg = bank(); bkg2 = bank()
            for kk in range(4):
                k = 4 * hb + kk
                P.op(PE, lambda e, k=k, kk=kk, bkg=bkg: e.transpose(out=ps[bkg][0:1, 128 * kk:128 * kk + 128], in_=modT[:, 16 + k, 0:1], identity=ident),
                     reads=[b_modT, b_ident], writes=[psb[bkg]] if kk == 0 else [], wadd=[psb[bkg]] if kk > 0 else [])
                P.op(PE, lambda e, k=k, kk=kk, bkg2=bkg2: e.transpose(out=ps[bkg2][0:NS, 128 * kk:128 * kk + 128], in_=modT[:, 16 + k, 1:17], identity=ident),
                     reads=[b_modT, b_ident], writes=[psb[bkg2]] if kk == 0 else [], wadd=[psb[bkg2]] if kk > 0 else [])
            P.op(DVE, lambda e, hb=hb, bkg=bkg: e.tensor_copy(out=grow[0:1, 512 * hb:512 * hb + 512], in_=ps[bkg][0:1, 0:512]), reads=[psb[bkg]], wadd=[b_grow])
            P.op(DVE, lambda e, hb=hb, bkg2=bkg2: e.tensor_copy(out=gateS[0:NS, 512 * hb:512 * hb + 512], in_=ps[bkg2][0:NS, 0:512]), reads=[psb[bkg2]], wadd=[b_gateS])
        for hb in range(2):
            bkb = bank()
            P.op(PE, lambda e, hb=hb, bkb=bkb: e.matmul(ps[bkb][:, 0:512], lhsT=ones1[0:1, :], rhs=grow[0:1, 512 * hb:512 * hb + 512], start=True, stop=True),
                 reads=[b_grow, b_ones], writes=[psb[bkb]])
            P.op(DVE, lambda e, hb=hb, bkb=bkb: e.tensor_copy(out=GateB[:, 512 * hb:512 * hb + 512], in_=ps[bkb][:, 0:512]), reads=[psb[bkb]], wadd=[b_GateB])
        so = [load_w(w_out[:, 0:512], 512), load_w(w_out[:, 512:1024], 512)]
        def x_load(ti_):
            rows_, c0_ = (128, 128 * ti_) if ti_ < 16 else (NS, NP)
            src_ = xp[c0_:c0_ + 128, :] if ti_ < 16 else xs
            P.dma(SP, xsem2[ti_ % NSL], xt[ti_ % NSL][0:rows_, :], src_, writes=[b_xt[ti_ % NSL]])
        for ti_ in range(NSL):
            x_load(ti_)
        for tt_i in range(17):
            rows, c0 = (128, 128 * tt_i) if tt_i < 16 else (NS, NP)
            sl = tt_i % NSL
            gate_ap = GateB if tt_i < 16 else gateS
            gate_b = b_GateB if tt_i < 16 else b_gateS
            for fb in range(2):
                bko = bank()
                for k in range(8):
                    P.op(PE, lambda e, bko=bko, k=k, fb=fb, rows=rows, c0=c0: e.matmul(
                        ps[bko][0:rows, 0:512], lhsT=mT[:, k, c0:c0 + rows], rhs=wslot[so[fb]][:, k, 0:512], start=(k == 0), stop=(k == 7)),
                        reads=[wslot_b[so[fb]]] + mT_b if k == 0 else [], writes=[psb[bko]] if k == 0 else [], sig=(k == 7))
                if not P.dead:
                    P.attach(Dep(P.esem[PE], P.cnt[PE]), reads=[wslot_b[so[fb]]] + mT_b, writes=[psb[bko]])
                P.op(DVE, lambda e, bko=bko, fb=fb, rows=rows, sl=sl, gate_ap=gate_ap: e.tensor_tensor(
                    out=rt[sl][0:rows, 512 * fb:512 * fb + 512], in0=ps[bko][0:rows, 0:512], in1=gate_ap[0:rows, 512 * fb:512 * fb + 512], op=ALU.mult),
                    reads=[psb[bko], gate_b], writes=[b_rt[sl]] if fb == 0 else [], wadd=[b_rt[sl]] if fb == 1 else [])
            P.op(DVE, lambda e, rows=rows, sl=sl: e.scalar_tensor_tensor(out=rt[sl][0:rows, :], in0=xt[sl][0:rows, :], scalar=float(ALPHA),
                                                                         in1=rt[sl][0:rows, :], op0=ALU.mult, op1=ALU.add),
                 reads=[b_xt[sl], b_rt[sl]], writes=[b_rt[sl]])
            for hf in range(2):
                P.op(DVE, lambda e, rows=rows, sl=sl, hf=hf: e.bn_stats(out=stt[0:rows, sl, hf, :], in_=rt[sl][0:rows, 512 * hf:512 * hf + 512]),
                     reads=[b_rt[sl]], writes=[b_stt[sl]] if hf == 0 else [], wadd=[b_stt[sl]] if hf == 1 else [])
            P.op(DVE, lambda e, rows=rows, sl=sl: e.bn_aggr(out=mvt[0:rows, sl, :], in_=stt[0:rows, sl].rearrange("p a b -> p (a b)")),
                 reads=[b_stt[sl]], writes=[b_stt[sl]])
            P.op(POOL, lambda e, rows=rows, sl=sl: e.tensor_scalar(out=rsd[0:rows, sl, 0:1], in0=mvt[0:rows, sl, 1:2], scalar1=float(LN_EPS), scalar2=0.0,
                                                                   op0=ALU.add, op1=ALU.add), reads=[b_stt[sl]], writes=[b_stt[sl]])
            P.op(POOL, lambda e, rows=rows, sl=sl: e.tensor_tensor(out=rsd[0:rows, sl, 0:1], in0=rsd[0:rows, sl, 0:1], in1=mhalf[0:rows, :], op=ALU.pow),
                 reads=[b_stt[sl], b_mh], writes=[b_stt[sl]])
            P.op(POOL, lambda e, rows=rows, sl=sl: e.tensor_tensor(out=rsd[0:rows, sl, 1:2], in0=mvt[0:rows, sl, 0:1], in1=rsd[0:rows, sl, 0:1], op=ALU.mult),
                 reads=[b_stt[sl]], writes=[b_stt[sl]])
            P.op(POOL, lambda e, rows=rows, sl=sl: e.tensor_scalar(out=rsd[0:rows, sl, 1:2], in0=rsd[0:rows, sl, 1:2], scalar1=-1.0, scalar2=0.0,
                                                                   op0=ALU.mult, op1=ALU.add), reads=[b_stt[sl]], writes=[b_stt[sl]])
            P.op(ACT, lambda e, rows=rows, sl=sl: e.activation(out=xt[sl][0:rows, :], in_=rt[sl][0:rows, :], func=AF.Identity,
                                                               scale=rsd[0:rows, sl, 0:1], bias=rsd[0:rows, sl, 1:2]),
                 reads=[b_rt[sl], b_stt[sl]], writes=[b_xt[sl]])
            P.op(DVE, lambda e, rows=rows, sl=sl: e.tensor_tensor(out=xt[sl][0:rows, :], in0=xt[sl][0:rows, :], in1=LnG[0:rows, :], op=ALU.mult),
                 reads=[b_xt[sl], b_LnG], writes=[b_xt[sl]])
            P.op(POOL, lambda e, rows=rows, sl=sl: e.tensor_tensor(out=xt[sl][0:rows, :], in0=xt[sl][0:rows, :], in1=LnB[0:rows, :], op=ALU.add),
                 reads=[b_xt[sl], b_LnB], writes=[b_xt[sl]])
            dst = yp[c0:c0 + 128, :] if tt_i < 16 else ys
            P.dma(SP, osem[sl], dst, xt[sl][0:rows, :], reads=[b_xt[sl]])
            if tt_i + NSL < 17:
                x_load(tt_i + NSL)
        final_deps = [Dep(o_.h, o_.cnt) for o_ in osem]

        P.dead = False
        if DEBUG:
            pass
        P.wait(SP, [Dep(dout_sem.h, dout_sem.cnt)] + final_deps + [Dep(d_.h, d_.cnt) for d_ in out_sems])
        P.emit()
        print("instruction counts:", P.ninst)
    return nc


def _bucket_np(dist):
    max_exact = 16
    df = np.maximum(dist, 1).astype(np.float32)
    large = max_exact + (np.log(df / np.float32(max_exact)) / np.float32(math.log(128 / max_exact)) * np.float32(16)).astype(np.int32)
    large = np.minimum(large, 31)
    return np.where(dist < max_exact, dist, large)


def _host_consts():
    R = np.zeros((32, 384), np.float32)
    for i in range(384):
        dist = 255 - i
        if 0 <= dist <= 128:
            R[int(_bucket_np(np.array([dist]))[0]), i] = 1.0
    j = np.arange(128)[:, None]
    q = np.arange(128)[None, :]
    mask = np.concatenate([(j >= q), (j <= q)], axis=1).astype(np.float32)
    r = np.arange(128)
    bmask = (r[:, None] // 32 == r[None, :] // 32).astype(np.float32)
    diag = np.zeros((16, 16, 4), np.float32)
    for s_ in range(16):
        diag[s_, s_, :] = 1.0
    return R, mask, bmask, diag.reshape(16, 64)


_NC_CACHE = {}


def kernel(x_prompt, x_sample, c_prompt, c_sample, state_ssm_re, state_ssm_im, cache_swa_k, cache_swa_v,
           w_ada, b_ada, w_in, ssm_lambda_re, ssm_lambda_im, ssm_log_delta, ssm_b_re, ssm_b_im,
           ssm_c_re, ssm_c_im, ssm_d, w_glu, b_glu, attn_sinks, rel_bias, w_branch_s, w_branch_a,
           w_out, ln_g, ln_b):
    f = lambda a: np.ascontiguousarray(np.asarray(a, dtype=np.float32))
    x_prompt = f(x_prompt); x_sample = f(x_sample); c_prompt = f(c_prompt); c_sample = f(c_sample)
    R, mask, bmask, diag = _host_consts()
    shared = {
        "w_ada": f(w_ada)[0], "b_ada": f(b_ada)[0], "w_in": f(w_in)[0],
        "lam_re": f(ssm_lambda_re)[0], "lam_im": f(ssm_lambda_im)[0], "log_delta": f(ssm_log_delta)[0],
        "b_re": f(ssm_b_re)[0].reshape(4096, 16), "b_im": f(ssm_b_im)[0].reshape(4096, 16),
        "c_re": f(ssm_c_re)[0].reshape(1024, 64), "c_im": f(ssm_c_im)[0].reshape(1024, 64),
        "ssm_d": f(ssm_d)[0], "w_glu": f(w_glu)[0], "b_glu": f(b_glu)[0], "sinks": f(attn_sinks)[0],
        "rel_bias": f(rel_bias), "w_bs": f(w_branch_s)[0], "w_ba": f(w_branch_a)[0], "w_out": f(w_out)[0],
        "ln_g": f(ln_g)[0], "ln_b": f(ln_b)[0],
        "rtab": R, "maskc": mask, "bmaskc": bmask, "diagc": diag,
    }
    sre = f(state_ssm_re)[0].reshape(128, 4096); sim = f(state_ssm_im)[0].reshape(128, 4096)
    ckk = f(cache_swa_k)[0].reshape(128, 128, 256); cvv = f(cache_swa_v)[0].reshape(128, 128, 256)
    in_maps = []
    for c in range(NCORES):
        b, qr = c // 4, c % 4
        t0 = NP * qr
        xh = x_prompt[b, t0 - 128:t0] if qr > 0 else np.zeros((128, D), np.float32)
        flags = np.zeros(32, np.float32)
        flags[0] = 1.0 if qr > 0 else 0.0
        xprev = np.zeros((3, NP, D), np.float32)
        for j in range(3):
            qq = qr - 1 - j
            if qq >= 0:
                flags[1 + j] = 1.0
                xprev[j] = x_prompt[b, NP * qq:NP * qq + NP]
        m = dict(shared)
        m.update({
            "xprev": xprev, "xp": np.ascontiguousarray(x_prompt[b, t0:t0 + NP]), "xh": np.ascontiguousarray(xh),
            "xs": np.ascontiguousarray(x_sample[NS * c:NS * c + NS, 0]),
            "cc": np.ascontiguousarray(np.concatenate([c_prompt[b:b + 1], c_sample[NS * c:NS * c + NS]], 0)),
            "st_re": np.ascontiguousarray(sre[NS * c:NS * c + NS]), "st_im": np.ascontiguousarray(sim[NS * c:NS * c + NS]),
            "ck": np.ascontiguousarray(ckk[NS * c:NS * c + NS]), "cv": np.ascontiguousarray(cvv[NS * c:NS * c + NS]),
            "flags": flags,
        })
        in_maps.append(m)
    nc = build()
    res = run_bass_kernel_spmd(nc, in_maps, core_ids=list(range(NCORES)))
    R_ = res.results
    kernel.last_results = R_
    y_prompt = np.stack([np.concatenate([R_[4 * b + q]["yp"] for q in range(4)], 0) for b in range(2)], 0)
    y_sample = np.concatenate([R_[c]["ys"] for c in range(NCORES)], 0).reshape(128, 1, D)
    p_hr = np.stack([R_[4 * b + 3]["pst_re"].reshape(64, 64) for b in range(2)], 0)[None]
    p_hi = np.stack([R_[4 * b + 3]["pst_im"].reshape(64, 64) for b in range(2)], 0)[None]
    p_k = np.stack([R_[4 * b + 3]["pck"].reshape(128, 4, 64) for b in range(2)], 0)[None]
    p_v = np.stack([R_[4 * b + 3]["pcv"].reshape(128, 4, 64) for b in range(2)], 0)[None]
    s_hr = np.concatenate([R_[c]["sst_re"] for c in range(NCORES)], 0).reshape(1, 128, 64, 64)
    s_hi = np.concatenate([R_[c]["sst_im"] for c in range(NCORES)], 0).reshape(1, 128, 64, 64)
    s_k = np.concatenate([R_[c]["sck"] for c in range(NCORES)], 0).reshape(1, 128, 128, 4, 64)
    s_v = np.concatenate([R_[c]["scv"] for c in range(NCORES)], 0).reshape(1, 128, 128, 4, 64)
    return (y_prompt.astype(np.float32), y_sample.astype(np.float32), p_hr.astype(np.float32), p_hi.astype(np.float32),
            p_k.astype(np.float32), p_v.astype(np.float32), s_hr.astype(np.float32), s_hi.astype(np.float32),
            s_k.astype(np.float32), s_v.astype(np.float32))
```

```python
import math
import os
from contextlib import ExitStack
import numpy as np
import ml_dtypes
import concourse.bass as bass
import concourse.mybir as mybir
from concourse.bass_utils import run_bass_kernel_spmd

F32 = mybir.dt.float32
BF16 = mybir.dt.bfloat16
U8 = mybir.dt.uint8
ALU = mybir.AluOpType
AF = mybir.ActivationFunctionType
AX = mybir.AxisListType

PE, ACT, DVE, POOL, SP = "tensor", "scalar", "vector", "gpsimd", "sync"
ENGS = [PE, ACT, DVE, POOL, SP]

NCORES = 8
D = 1024
NP = 2048
NS = 16
NT = NP + NS
TBS = [(0, 512), (512, 512), (1024, 512), (1536, 512), (2048, 16)]
DIN = 6656
OFF_U, OFF_ZS, OFF_Q, OFF_K, OFF_V, OFF_ZA, OFF_GS, OFF_GA = 0, 1024, 2048, 3072, 3328, 3584, 4608, 5632
ALPHA = 2.0 ** 0.25
LN_EPS = 1e-5
DEBUG = False


class Dep:
    __slots__ = ("sem", "val")

    def __init__(self, sem, val):
        self.sem = sem
        self.val = val


class Buf:
    __slots__ = ("w", "r", "name")

    def __init__(self, name=""):
        self.w = []
        self.r = []
        self.name = name


class _RProxy:
    def __init__(self, bufs):
        self.bufs = bufs

    def append(self, h):
        for b in self.bufs:
            b.r.append(h)

    def __len__(self):
        return 0


class BufGroup:
    def __init__(self, bufs):
        self.bufs = list(bufs)
        self.r = _RProxy(self.bufs)

    @property
    def w(self):
        return [h for b in self.bufs for h in b.w]


def handoff(new_bufs, old_bufs):
    deps = []
    for b in old_bufs:
        deps.extend(b.w)
        deps.extend(b.r)
    for nb in new_bufs:
        nb.r = list(nb.r) + deps


class DSem:
    def __init__(self, h):
        self.h = h
        self.cnt = 0


class Prog:
    def __init__(self, nc, stack):
        self.nc = nc
        self.q = {e: [] for e in ENGS}
        self.esem = {}
        self.cnt = {e: 0 for e in ENGS}
        self.allsems = []
        for e in [PE, ACT, DVE, POOL]:
            self.esem[e] = nc.alloc_semaphore("s_" + e)
            self.allsems.append(self.esem[e])
        self.seen = {}
        self.stack = stack
        self.nd = 0
        self.ninst = {e: 0 for e in ENGS}
        self.dead = False
        self.stop = int(os.environ.get("KSTOP", "99"))

    def phase(self, n):
        self.dead = n > self.stop

    def dsem(self, name=None):
        self.nd += 1
        h = self.nc.alloc_semaphore(f"d{self.nd}_{name or 'm'}")
        self.allsems.append(h)
        return DSem(h)

    def _waits(self, eng, deps):
        best = {}
        for d in deps:
            if d is None:
                continue
            k = id(d.sem)
            if k not in best or best[k].val < d.val:
                best[k] = d
        ws = []
        for d in best.values():
            k = (eng, id(d.sem))
            if self.seen.get(k, 0) >= d.val:
                continue
            self.seen[k] = d.val
            ws.append((d.sem, d.val))
        return ws

    @staticmethod
    def _compact(lst):
        best = {}
        for d in lst:
            k = id(d.sem)
            if k not in best or best[k].val < d.val:
                best[k] = d
        return list(best.values())

    @staticmethod
    def _bufdeps(reads, writes, wadd=()):
        deps = []
        for b in reads:
            deps.extend(b.w)
        for b in writes:
            deps.extend(b.w)
            deps.extend(b.r)
        for b in wadd:
            deps.extend(b.r)
        return deps

    @classmethod
    def _update(cls, h, reads, writes, wadd=()):
        for b in reads:
            b.r.append(h)
            if len(b.r) > 32:
                b.r = cls._compact(b.r)
        for b in writes:
            b.w = [h]
            b.r = []
        for b in wadd:
            b.w.append(h)
            if len(b.w) > 32:
                b.w = cls._compact(b.w)

    def op(self, eng, fn, reads=(), writes=(), deps=(), sig=True, wadd=()):
        if self.dead:
            return None
        alld = list(deps) + self._bufdeps(reads, writes, wadd)
        ws = self._waits(eng, alld)
        h = None
        if sig:
            self.cnt[eng] += 1
            h = Dep(self.esem[eng], self.cnt[eng])
        sem = self.esem[eng] if sig else None
        self.ninst[eng] += 1 + len(ws)

        def run(e, ws=ws, fn=fn, sem=sem):
            for (s, v) in ws:
                e.wait_ge(s, v)
            ins = fn(e)
            if sem is not None:
                ins.then_inc(sem, 1)
        self.q[eng].append(run)
        if h is not None:
            self._update(h, reads, writes, wadd)
        return h

    def attach(self, h, reads=(), writes=(), wadd=()):
        if self.dead or h is None:
            return
        self._update(h, reads, writes, wadd)

    def dma(self, eng, ds, out, in_, reads=(), writes=(), deps=(), wadd=(), **kw):
        if self.dead:
            return None
        alld = list(deps) + self._bufdeps(reads, writes, wadd)
        ws = self._waits(eng, alld)
        ds.cnt += 16
        h = Dep(ds.h, ds.cnt)
        self.ninst[eng] += 1 + len(ws)

        def run(e, ws=ws, out=out, in_=in_, kw=kw, sh=ds.h):
            for (s, v) in ws:
                e.wait_ge(s, v)
            e.dma_start(out=out, in_=in_, **kw).then_inc(sh, 16)
        self.q[eng].append(run)
        self._update(h, reads, writes, wadd)
        return h

    def raw(self, eng, fn, ds, inc, reads=(), writes=(), deps=()):
        if self.dead:
            return None
        alld = list(deps) + self._bufdeps(reads, writes)
        ws = self._waits(eng, alld)
        ds.cnt += inc
        h = Dep(ds.h, ds.cnt)

        def run(e, ws=ws, fn=fn, sh=ds.h, inc=inc):
            for (s, v) in ws:
                e.wait_ge(s, v)
            fn(e).then_inc(sh, inc)
        self.q[eng].append(run)
        self._update(h, reads, writes)
        return h

    def wait(self, eng, deps):
        ws = self._waits(eng, deps)

        def run(e, ws=ws):
            for (s, v) in ws:
                e.wait_ge(s, v)
        self.q[eng].append(run)

    def emit(self):
        nc = self.nc
        with nc.Block() as block:
            @block.tensor
            def _(e):
                for f in self.q[PE]:
                    f(e)

            @block.scalar
            def _(e):
                for f in self.q[ACT]:
                    f(e)

            @block.vector
            def _(e):
                for f in self.q[DVE]:
                    f(e)

            @block.gpsimd
            def _(e):
                for f in self.q[POOL]:
                    f(e)

            @block.sync
            def _(e):
                for f in self.q[SP]:
                    f(e)


def _dsize(dt):
    return {F32: 4, BF16: 2, U8: 1}[dt]


class Arena:
    def __init__(self, nc, stack, nbytes):
        self.t = stack.enter_context(nc.sbuf_tensor("arena", [128, nbytes], U8))
        self.nbytes = nbytes

    def carve(self, off, shape, dt):
        n = int(np.prod(shape)) * _dsize(dt)
        assert off % 4 == 0 and off + n <= self.nbytes, (off, n, self.nbytes)
        v = self.t[:, off:off + n]
        if dt != U8:
            v = v.bitcast(dt)
        if len(shape) > 1:
            names = [f"a{i}" for i in range(len(shape))]
            pat = "p (" + " ".join(names) + ") -> p " + " ".join(names)
            v = v.rearrange(pat, **{names[i]: shape[i] for i in range(len(shape))})
        return v


O_HT = 0
O_HTH = 33024
O_CONST = 35072
O_W = 45312
O_A = 61696
O_B = 94720
O_C = 127744
ARENA = 212000
C_SIZE = ARENA - O_C


def build():
    nc = bass.Bass("TRN2", target_bir_lowering=False)

    def din(name, shape, dt=F32):
        return nc.dram_tensor(name, list(shape), dt, kind="ExternalInput").ap()

    def dout(name, shape, dt=F32):
        return nc.dram_tensor(name, list(shape), dt, kind="ExternalOutput").ap()

    xprev = din("xprev", [3, NP, D]); xp = din("xp", [NP, D]); xh = din("xh", [128, D]); xs = din("xs", [NS, D]); ccin = din("cc", [17, D])
    st_re = din("st_re", [NS, 4096]); st_im = din("st_im", [NS, 4096])
    ck = din("ck", [NS, 128, 256]); cv = din("cv", [NS, 128, 256])
    w_ada = din("w_ada", [D, 3072]); b_ada = din("b_ada", [3072]); w_in = din("w_in", [D, DIN])
    lam_re = din("lam_re", [64, 64]); lam_im = din("lam_im", [64, 64]); log_delta = din("log_delta", [64])
    b_re = din("b_re", [4096, 16]); b_im = din("b_im", [4096, 16])
    c_re = din("c_re", [1024, 64]); c_im = din("c_im", [1024, 64])
    ssm_d = din("ssm_d", [1024]); w_glu = din("w_glu", [D, D]); b_glu = din("b_glu", [D])
    sinks = din("sinks", [16]); rel_bias = din("rel_bias", [32, 16])
    w_bs = din("w_bs", [D, D]); w_ba = din("w_ba", [D, D]); w_out = din("w_out", [D, D])
    ln_g = din("ln_g", [D]); ln_b = din("ln_b", [D])
    rtab = din("rtab", [32, 384]); maskc = din("maskc", [128, 256]); bmaskc = din("bmaskc", [128, 128])
    diagc = din("diagc", [16, 64]); flagsc = din("flags", [32])

    yp = dout("yp", [NP, D]); ys = dout("ys", [NS, D])
    pst_re = dout("pst_re", [32, 128]); pst_im = dout("pst_im", [32, 128])
    pck = dout("pck", [128, 256]); pcv = dout("pcv", [128, 256])
    sst_re = dout("sst_re", [NS, 4096]); sst_im = dout("sst_im", [NS, 4096])
    sck = dout("sck", [NS, 128, 256]); scv = dout("scv", [NS, 128, 256])
    dbg = {}
    if DEBUG:
        dbg["hT"] = dout("dbg_hT", [128, 8, NT], BF16)
        dbg["uT"] = dout("dbg_uT", [128, 8, NT], BF16)
        dbg["pw"] = dout("dbg_pw", [128, 9 * 2 * 32])
        dbg["hend"] = dout("dbg_hend", [128, 2 * 32 * 16])
        dbg["bb"] = dout("dbg_bb", [128, 2 * 32 * 16]); dbg["ccm"] = dout("dbg_ccm", [128, 2 * 32 * 16])
        dbg["R8"] = dout("dbg_R8", [128, 16 * 2 * 32]); dbg["R128"] = dout("dbg_R128", [128, 16 * 2 * 32])
        dbg["A2k"] = dout("dbg_A2k", [128, 3 * 2 * 32])
        dbg["WinL"] = dout("dbg_WinL", [128, 8 * 8 * 2 * 128], BF16)
        dbg["X0"] = dout("dbg_X0", [128, 512])
        dbg["KL"] = dout("dbg_KL", [128, 2 * 8 * 128], BF16); dbg["Ca"] = dout("dbg_Ca", [128, 2 * 4 * 9 * 2 * 32], BF16)
        dbg["Hb"] = dout("dbg_Hb", [128, 2 * 2048], BF16); dbg["Xs"] = dout("dbg_Xs", [128, 2 * 2048])
        dbg["carry"] = dout("dbg_carry", [128, 17 * 2 * 32])
        dbg["yT"] = dout("dbg_yT", [128, 8, NT], BF16)
        dbg["gbs"] = dout("dbg_gbs", [128, 8, NT], BF16)
        dbg["oT"] = dout("dbg_oT", [128, 8, NT], BF16)
        dbg["mT"] = dout("dbg_mT", [128, 8, NT], BF16)
        dbg["modT"] = dout("dbg_modT", [128, 24 * 17])

    ib = nc.dram_tensor("cc_ib", [128, 64], F32, kind="Internal")
    ob = nc.dram_tensor("cc_ob", [NCORES * 128, 64], F32, kind="Internal")

    st = ExitStack()
    with st:
        P = Prog(nc, st)
        AR = Arena(nc, st, ARENA)
        cv_ = AR.carve
        ps = [st.enter_context(nc.psum_tensor(f"ps{i}", [128, 512], F32)) for i in range(8)]
        psb = [Buf(f"ps{i}") for i in range(8)]
        dout_sem = P.dsem("dout")
        out_sems = []

        def osem_new(name):
            d_ = P.dsem(name)
            out_sems.append(d_)
            return d_
        misc_sem = P.dsem("misc")

        def misc_load(eng, out, in_, buf, wadd=False, **kw):
            if P.dead:
                return None
            if wadd:
                return P.dma(eng, P.dsem(), out, in_, wadd=[buf], **kw)
            return P.dma(eng, P.dsem(), out, in_, writes=[buf], **kw)

        hT = cv_(O_HT, [8, NT], BF16)
        hTh = cv_(O_HTH, [8, 128], BF16)
        hT_b = [Buf(f"hT{i}") for i in range(len(TBS))]
        hTh_b = Buf("hTh")
        o = O_CONST
        ident = cv_(o, [128], F32); o += 512
        modT = cv_(o, [24, 17], F32); o += 1664
        op1p = cv_(o, [8, 17], F32); o += 576
        flags = cv_(o, [32], F32); o += 128
        Dm = cv_(o, [8], F32); o += 32
        bglu = cv_(o, [8], F32); o += 32
        ES = cv_(o, [4, 4, 128], BF16); o += 4096
        onesd = cv_(o, [128], BF16); o += 256
        EBself = cv_(o, [16], F32); o += 64
        bmask = cv_(o, [128], F32); o += 512
        ones1 = cv_(o, [128], F32); o += 512
        assert o <= O_CONST + 10240
        b_ident = Buf(); b_modT = Buf(); b_flags = Buf(); b_Dm = Buf(); b_bglu = Buf(); b_ES = Buf()
        b_ones = Buf(); b_EBself = Buf(); b_bmask = Buf()
        wslot = [cv_(O_W + 8192 * i, [8, 512], BF16) for i in range(2)]
        wslot_b = [Buf("w0"), Buf("w1")]
        wsem = [P.dsem("w0"), P.dsem("w1")]
        wctr = [0]
        RA = cv_(O_A, [8, NT], BF16)
        RB = cv_(O_B, [8, NT], BF16)

        rr = {"i": 0}

        def bank():
            i = rr["i"] % 8
            rr["i"] += 1
            return i

        def load_w(src2d, ncols):
            s = wctr[0] % 2
            wctr[0] += 1
            P.dma(POOL, wsem[s], wslot[s][:, :, 0:ncols], src2d.rearrange("(k p) f -> p k f", p=128),
                  writes=[wslot_b[s]])
            return s

        def evac_copy(i, out_ap, in_ap, reads, writes, wadd=()):
            if i % 2 == 0:
                return P.op(ACT, lambda e: e.activation(out=out_ap, in_=in_ap, func=AF.Copy), reads=reads, writes=writes, wadd=wadd)
            return P.op(DVE, lambda e: e.tensor_copy(out=out_ap, in_=in_ap), reads=reads, writes=writes, wadd=wadd)

        def proj_fm(src2d, ncols, rhs_of, rhs_bufs, evac, tbs=TBS):
            s = load_w(src2d, ncols)
            for oc in range(ncols // 128):
                for tbi, (t0, n) in enumerate(tbs):
                    b = bank()
                    for k in range(8):
                        last = (k == 7)
                        P.op(PE, lambda e, b=b, k=k, oc=oc, tbi=tbi, t0=t0, n=n, s=s: e.matmul(
                            ps[b][:, 0:n], lhsT=wslot[s][:, k, oc * 128:(oc + 1) * 128], rhs=rhs_of(k, t0, n),
                            start=(k == 0), stop=(k == 7)),
                            reads=[wslot_b[s], rhs_bufs[tbi]] if k == 0 else [], writes=[psb[b]] if k == 0 else [],
                            sig=last)
                        if last and not P.dead:
                            h = Dep(P.esem[PE], P.cnt[PE])
                            P.attach(h, reads=[wslot_b[s], rhs_bufs[tbi]], writes=[psb[b]])
                    evac(oc, tbi, ps[b][:, 0:n], psb[b])

        def cmul(eng, dst_r, dst_i, xr, xi, yr, yi, t1, t2, bufs_r, bufs_w, tb):
            P.op(eng, lambda e: e.tensor_tensor(out=t1, in0=xr, in1=yr, op=ALU.mult), reads=bufs_r, writes=[tb])
            P.op(eng, lambda e: e.tensor_tensor(out=t2, in0=xi, in1=yi, op=ALU.mult), reads=bufs_r, writes=[tb])
            P.op(eng, lambda e: e.tensor_tensor(out=dst_r, in0=t1, in1=t2, op=ALU.subtract), reads=[tb], writes=bufs_w)
            P.op(eng, lambda e: e.tensor_tensor(out=t1, in0=xr, in1=yi, op=ALU.mult), reads=bufs_r + bufs_w, writes=[tb])
            P.op(eng, lambda e: e.tensor_tensor(out=t2, in0=xi, in1=yr, op=ALU.mult), reads=bufs_r + bufs_w, writes=[tb])
            P.op(eng, lambda e: e.tensor_tensor(out=dst_i, in0=t1, in1=t2, op=ALU.add), reads=[tb], writes=bufs_w)

        P.phase(0)
        P.op(POOL, lambda e: e.memset(ident, 0.0), writes=[b_ident])
        P.op(POOL, lambda e: e.affine_select(out=ident, in_=ident, pattern=[[-1, 128]], compare_op=ALU.not_equal,
                                             fill=1.0, base=0, channel_multiplier=1), writes=[b_ident])
        misc_load(SP, flags, flagsc.rearrange("(o n) -> o n", o=1).to_broadcast([128, 32]), b_flags)
        misc_load(SP, bmask, bmaskc, b_bmask)
        P.op(POOL, lambda e: e.memset(ones1[0:1, :], 1.0), writes=[b_ones])
        P.op(POOL, lambda e: e.memset(onesd[0:1, 0:64], 0.0), wadd=[b_ones])
        P.op(POOL, lambda e: e.memset(onesd[0:1, 64:128], 1.0), wadd=[b_ones])

        P.phase(1)
        c_t = cv_(O_C + 62208, [1024], F32); c_sg = cv_(O_C + 66304, [1024], F32)
        ccT = cv_(O_C + 70400, [8, 17], BF16); badain = cv_(O_C + 70912, [128], F32); badaT = cv_(O_C + 71424, [24], F32)
        b_ct = Buf(); b_csg = Buf(); b_ccT = Buf(); b_bin = Buf(); b_baT = Buf()
        misc_load(SP, c_t[0:17, :], ccin, b_ct)
        misc_load(SP, badain[0:24, :], b_ada.rearrange("(c p) -> c p", p=128), b_bin)
        P.op(ACT, lambda e: e.activation(out=c_sg[0:17, :], in_=c_t[0:17, :], func=AF.Sigmoid), reads=[b_ct], writes=[b_csg])
        P.op(DVE, lambda e: e.tensor_tensor(out=c_sg[0:17, :], in0=c_sg[0:17, :], in1=c_t[0:17, :], op=ALU.mult),
             reads=[b_ct], writes=[b_csg])
        bk = bank()
        for k in range(8):
            P.op(PE, lambda e, k=k: e.transpose(out=ps[bk][:, 17 * k:17 * k + 17], in_=c_sg[0:17, 128 * k:128 * k + 128],
                                                identity=ident[0:17, 0:17]),
                 reads=[b_csg, b_ident], writes=[psb[bk]] if k == 0 else [], wadd=[psb[bk]] if k > 0 else [])
        P.op(DVE, lambda e: e.tensor_copy(out=ccT.rearrange("p k s -> p (k s)"), in_=ps[bk][:, 0:136]), reads=[psb[bk]], writes=[b_ccT])
        bk2 = bank()
        P.op(PE, lambda e: e.transpose(out=ps[bk2][:, 0:24], in_=badain[0:24, :], identity=ident[0:24, 0:24]),
             reads=[b_bin, b_ident], writes=[psb[bk2]])
        P.op(DVE, lambda e: e.tensor_copy(out=badaT, in_=ps[bk2][:, 0:24]), reads=[psb[bk2]], writes=[b_baT])
        bkm = bank()
        hlast = None
        for blk in range(6):
            s = load_w(w_ada[:, 512 * blk:512 * blk + 512], 512)
            for oc in range(4):
                f = 4 * blk + oc
                for k in range(8):
                    first = (blk == 0 and oc == 0 and k == 0)
                    lastk = (k == 7)
                    hlast = P.op(PE, lambda e, f=f, k=k, oc=oc, s=s: e.matmul(
                        ps[bkm][:, 17 * f:17 * f + 17], lhsT=wslot[s][:, k, oc * 128:(oc + 1) * 128], rhs=ccT[:, k, :],
                        start=(k == 0), stop=(k == 7)),
                        reads=[wslot_b[s], b_ccT] if k == 0 else [], writes=[psb[bkm]] if first else [], sig=lastk and oc == 3)
            P.attach(hlast, reads=[wslot_b[s]], wadd=[psb[bkm]])
        P.op(DVE, lambda e: e.tensor_tensor(out=modT, in0=ps[bkm][:, 0:408].rearrange("p (f s) -> p f s", f=24),
                                            in1=badaT.unsqueeze(2).to_broadcast([128, 24, 17]), op=ALU.add),
             reads=[psb[bkm], b_baT], writes=[b_modT])
        P.op(DVE, lambda e: e.tensor_scalar(out=op1p, in0=modT[:, 8:16, :], scalar1=1.0, scalar2=None, op0=ALU.add),
             reads=[b_modT], wadd=[b_modT])

        xst = [cv_(O_C + 71552 + 4096 * i, [1024], F32) for i in range(2)]
        xst_b = [Buf(), Buf()]
        xsem = [P.dsem("x0"), P.dsem("x1")]
        hTs_b = hT_b[4]
        tmpS = cv_(O_C + 79744, [8, 16], F32)
        b_tmpS = Buf()

        def phase_a(xsrc, full):
            tiles = [("p", i) for i in range(16)] + ([("h", 0), ("s", 0)] if full else [])
            for ti, (kind, i) in enumerate(tiles):
                sl = ti % 2
                if kind == "p":
                    src, rows, dst, dbuf = xsrc[128 * i:128 * i + 128, :], 128, (lambda k, i=i: hT[:, k, 128 * i:128 * i + 128]), hT_b[i // 4]
                elif kind == "h":
                    src, rows, dst, dbuf = xh, 128, (lambda k: hTh[:, k, :]), hTh_b
                else:
                    src, rows, dst, dbuf = xs, NS, None, hTs_b
                P.dma(SP, xsem[sl], xst[sl][0:rows, :], src, writes=[xst_b[sl]])
                b0, b1 = bank(), bank()
                for k in range(8):
                    bb_ = b0 if k < 4 else b1
                    j = k % 4
                    P.op(PE, lambda e, k=k, bb_=bb_, j=j, sl=sl, rows=rows: e.transpose(
                        out=ps[bb_][:, rows * j:rows * j + rows], in_=xst[sl][0:rows, 128 * k:128 * k + 128],
                        identity=ident[0:rows, 0:rows]),
                        reads=[xst_b[sl], b_ident], writes=[psb[bb_]] if j == 0 else [], wadd=[psb[bb_]] if j > 0 else [])
                if kind != "s":
                    for k in range(8):
                        bb_ = b0 if k < 4 else b1
                        j = k % 4
                        src_ps = ps[bb_][:, 128 * j:128 * j + 128]
                        if False:
                            P.op(DVE, lambda e, k=k, src_ps=src_ps, dst=dst: e.tensor_scalar(
                                out=dst(k), in0=src_ps, scalar1=op1p[:, k, 0:1], scalar2=modT[:, k, 0:1], op0=ALU.mult, op1=ALU.add),
                                reads=[psb[bb_], b_modT], wadd=[dbuf])
                        else:
                            P.op(ACT, lambda e, k=k, src_ps=src_ps, dst=dst: e.activation(
                                out=dst(k), in_=src_ps, func=AF.Identity, scale=op1p[:, k, 0:1], bias=modT[:, k, 0:1]),
                                reads=[psb[bb_], b_modT], wadd=[dbuf])
                else:
                    for half, bb_ in enumerate([b0, b1]):
                        P.op(DVE, lambda e, half=half, bb_=bb_: e.tensor_tensor(
                            out=tmpS[:, 4 * half:4 * half + 4, :], in0=ps[bb_][:, 0:64].rearrange("p (k s) -> p k s", k=4),
                            in1=op1p[:, 4 * half:4 * half + 4, 1:17], op=ALU.mult), reads=[psb[bb_], b_modT], wadd=[b_tmpS])
                    P.op(DVE, lambda e: e.tensor_tensor(out=hT[:, :, NP:NT], in0=tmpS, in1=modT[:, 0:8, 1:17], op=ALU.add),
                         reads=[b_tmpS, b_modT], wadd=[dbuf])

        uT = RA
        uT_b = [Buf(f"uT{g}") for g in range(8)]
        rhs_h = lambda k, t0, n: hT[:, k, t0:t0 + n]
        cnt = {"i": 0}

        def phase_b(full):
            for blk in range(2):
                def ev(oc, tbi, pap, pb, blk=blk):
                    t0, n = TBS[tbi]
                    g = 4 * blk + oc
                    evac_copy(0, uT[:, g, t0:t0 + n], pap, [pb], [], wadd=[uT_b[g]])
                proj_fm(w_in[:, OFF_U + 512 * blk:OFF_U + 512 * blk + 512], 512, rhs_h, hT_b, ev, tbs=TBS if full else TBS[:4])

        P.phase(3)
        phase_a(xprev[0], False)
        phase_b(False)
        P.phase(2)
        oc_ = O_C
        pw = cv_(oc_ + 0, [9, 2, 32], F32); bb = cv_(oc_ + 2304, [2, 32, 16], F32); ccm = cv_(oc_ + 6400, [2, 32, 16], F32)
        R8 = cv_(oc_ + 10496, [16, 2, 32], F32); R128 = cv_(oc_ + 14592, [16, 2, 32], F32); A2k = cv_(oc_ + 18688, [3, 2, 32], F32)
        Hend = cv_(oc_ + 19456, [2, 32, 16], F32); carry = cv_(oc_ + 23552, [17, 2, 32], F32)
        Sb = cv_(oc_ + 27904, [8, 2, 128], BF16); coef = cv_(oc_ + 32000, [12, 32], F32)
        misc = cv_(oc_ + 33536, [32, 32], F32)
        t1 = cv_(oc_ + 37632, [1024], F32); t2 = cv_(oc_ + 41728, [1024], F32)
        O_S = oc_ + 45824
        Sslot = [cv_(O_S + 8192 * i, [8, 2, 128], F32) for i in range(2)]
        O_CA = oc_ + 62208; O_KL = oc_ + 71424; O_HB = oc_ + 75520
        b_pw = Buf("pw"); b_bb = Buf("bb"); b_ccm = Buf("ccm"); b_R8 = Buf(); b_R128 = Buf(); b_A2k = Buf(); b_coef = Buf()
        b_misc = Buf("misc"); b_t = Buf("t12"); b_Sslot = [Buf("S0"), Buf("S1")]
        craw = [cv_(O_S + 2048 * i, [8, 64], F32) for i in range(2)]
        lamraw = cv_(O_S + 4096, [128], F32); ldraw = cv_(O_S + 4608, [64], F32)
        draw = cv_(O_S + 4864, [128], F32); bgraw = cv_(O_S + 5376, [128], F32)
        braw = [cv_(O_S + 8192 + 2048 * i, [32, 16], F32) for i in range(2)]
        b_craw = Buf(); b_lam = Buf(); b_ld = Buf(); b_draw = Buf(); b_braw = Buf()
        misc_load(SP, craw[0], c_re.rearrange("(t r) p -> r t p", r=128), b_craw, wadd=True)
        misc_load(SP, craw[1], c_im.rearrange("(t r) p -> r t p", r=128), b_craw, wadd=True)
        misc_load(SP, lamraw[0:64, 0:64], lam_re, b_lam, wadd=True)
        misc_load(SP, lamraw[0:64, 64:128], lam_im, b_lam, wadd=True)
        misc_load(SP, ldraw, log_delta.rearrange("(o n) -> o n", o=1).to_broadcast([128, 64]), b_ld)
        misc_load(SP, draw[0:8, :], ssm_d.rearrange("(c p) -> c p", p=128), b_draw, wadd=True)
        misc_load(SP, bgraw[0:8, :], b_glu.rearrange("(c p) -> c p", p=128), b_draw, wadd=True)
        for i, src in enumerate([b_re, b_im]):
            for q4 in range(4):
                misc_load(SP, braw[i][:, 8 * q4:8 * q4 + 8, :],
                          src[1024 * q4:1024 * q4 + 1024, :].rearrange("(gp q) c -> q gp c", q=128), b_braw, wadd=True)
        M_ = lambda i: misc[:, i, :]
        LR, LI, DT, TH, FR, FC, MAG, SN, CS, NR, DEN, CR, CI, G1, KF, TMPA = [M_(i) for i in range(16)]
        KI = misc[:, 16, :].bitcast(mybir.dt.int32)
        bkl = bank()
        P.op(PE, lambda e: e.transpose(out=ps[bkl][:, 0:64], in_=lamraw[0:64, :], identity=ident[0:64, 0:64]),
             reads=[b_lam, b_ident], writes=[psb[bkl]])
        P.op(DVE, lambda e: e.tensor_copy(out=LR[0:64, :], in_=ps[bkl][0:64, 0:64:2]), reads=[psb[bkl]], wadd=[b_misc])
        P.op(DVE, lambda e: e.tensor_copy(out=LR[64:128, :], in_=ps[bkl][0:64, 1:64:2]), reads=[psb[bkl]], wadd=[b_misc])
        P.op(DVE, lambda e: e.tensor_copy(out=LI[0:64, :], in_=ps[bkl][64:128, 0:64:2]), reads=[psb[bkl]], wadd=[b_misc])
        P.op(DVE, lambda e: e.tensor_copy(out=LI[64:128, :], in_=ps[bkl][64:128, 1:64:2]), reads=[psb[bkl]], wadd=[b_misc])
        bkd = bank()
        P.op(PE, lambda e: e.transpose(out=ps[bkd][:, 0:8], in_=draw[0:8, :], identity=ident[0:8, 0:8]),
             reads=[b_draw, b_ident], writes=[psb[bkd]])
        P.op(PE, lambda e: e.transpose(out=ps[bkd][:, 8:16], in_=bgraw[0:8, :], identity=ident[0:8, 0:8]),
             reads=[b_draw, b_ident], wadd=[psb[bkd]])
        P.op(DVE, lambda e: e.tensor_copy(out=Dm, in_=ps[bkd][:, 0:8]), reads=[psb[bkd]], writes=[b_Dm])
        P.op(DVE, lambda e: e.tensor_copy(out=bglu, in_=ps[bkd][:, 8:16]), reads=[psb[bkd]], writes=[b_bglu])
        for ri in range(2):
            for hb in range(2):
                bkc = bank()
                for tt in range(4):
                    t_ = 4 * hb + tt
                    P.op(PE, lambda e, ri=ri, t_=t_, tt=tt, bkc=bkc: e.transpose(
                        out=ps[bkc][0:64, 128 * tt:128 * tt + 128], in_=craw[ri][:, t_, :], identity=ident),
                        reads=[b_craw, b_ident], writes=[psb[bkc]] if tt == 0 else [], wadd=[psb[bkc]] if tt > 0 else [])
                for g2 in range(2):
                    src = ps[bkc][0:64, :].rearrange("p (tg g2 c) -> p tg g2 c", g2=2, c=16)[:, :, g2, :]
                    P.op(DVE, lambda e, ri=ri, hb=hb, g2=g2, src=src: e.tensor_copy(
                        out=ccm[64 * g2:64 * g2 + 64, ri, 16 * hb:16 * hb + 16, :], in_=src),
                        reads=[psb[bkc]], wadd=[b_ccm])
        P.op(ACT, lambda e: e.activation(out=DT[0:64, :], in_=ldraw[0:64, 0:64:2], func=AF.Exp), reads=[b_ld], wadd=[b_misc])
        P.op(ACT, lambda e: e.activation(out=DT[64:128, :], in_=ldraw[64:128, 1:64:2], func=AF.Exp), reads=[b_ld], wadd=[b_misc])
        G = DVE
        tt_ = lambda out, a, b_, op, **kw: P.op(G, lambda e: e.tensor_tensor(out=out, in0=a, in1=b_, op=op), reads=[b_misc], wadd=[b_misc], **kw)
        ts_ = lambda out, a, s1, s2, o0, o1: P.op(G, lambda e: e.tensor_scalar(out=out, in0=a, scalar1=s1, scalar2=s2, op0=o0, op1=o1), reads=[b_misc], wadd=[b_misc])
        tt_(TH, LI, DT, ALU.mult)
        ts_(FR, TH, 1.0 / (2 * math.pi), 0.0, ALU.mult, ALU.add)
        P.op(DVE, lambda e: e.tensor_copy(out=KI, in_=FR), reads=[b_misc], wadd=[b_misc])
        P.op(DVE, lambda e: e.tensor_copy(out=KF, in_=KI), reads=[b_misc], wadd=[b_misc])
        tt_(FR, FR, KF, ALU.subtract)
        ts_(FC, FR, 1.0, 0.25, ALU.mult, ALU.add)
        P.op(DVE, lambda e: e.tensor_single_scalar(out=G1, in_=FC, scalar=0.5, op=ALU.is_gt), reads=[b_misc], wadd=[b_misc])
        tt_(FC, FC, G1, ALU.subtract)
        TWO_PI = 2.0 * math.pi
        P.op(ACT, lambda e: e.activation(out=SN, in_=FR, func=AF.Sin, scale=TWO_PI), reads=[b_misc], wadd=[b_misc])
        P.op(ACT, lambda e: e.activation(out=CS, in_=FC, func=AF.Sin, scale=TWO_PI), reads=[b_misc], wadd=[b_misc])
        tt_(TMPA, LR, DT, ALU.mult)
        P.op(ACT, lambda e: e.activation(out=MAG, in_=TMPA, func=AF.Exp), reads=[b_misc], wadd=[b_misc])
        P.op(G, lambda e: e.memset(pw[:, 0, 0, :], 1.0), wadd=[b_pw])
        P.op(G, lambda e: e.memset(pw[:, 0, 1, :], 0.0), wadd=[b_pw])
        P.op(G, lambda e: e.tensor_tensor(out=pw[:, 1, 0, :], in0=MAG, in1=CS, op=ALU.mult), reads=[b_misc], wadd=[b_pw])
        P.op(G, lambda e: e.tensor_tensor(out=pw[:, 1, 1, :], in0=MAG, in1=SN, op=ALU.mult), reads=[b_misc], wadd=[b_pw])

        def cm(dst, x, y, n, rb, wb):
            T1 = t1[:, 0:n * 32].rearrange("p (n g) -> p n g", n=n)
            T2 = t2[:, 0:n * 32].rearrange("p (n g) -> p n g", n=n)
            cmul(G, dst[:, :, 0, :], dst[:, :, 1, :], x[:, :, 0, :], x[:, :, 1, :], y[:, :, 0, :], y[:, :, 1, :], T1, T2, rb, wb, b_t)

        def bc(ap1, n):
            return ap1.to_broadcast([128, n, 2, 32])

        cm(pw[:, 2:3], pw[:, 1:2], pw[:, 1:2], 1, [b_pw], [b_pw])
        cm(pw[:, 3:5], pw[:, 1:3], bc(pw[:, 2:3], 2), 2, [b_pw], [b_pw])
        cm(pw[:, 5:9], pw[:, 1:5], bc(pw[:, 4:5], 4), 4, [b_pw], [b_pw])
        tt_(NR, pw[:, 1, 0, :], pw[:, 0, 0, :], ALU.subtract, deps=b_pw.w)
        tt_(DEN, LR, LR, ALU.mult)
        tt_(TMPA, LI, LI, ALU.mult)
        tt_(DEN, DEN, TMPA, ALU.add)
        P.op(DVE, lambda e: e.reciprocal(out=DEN, in_=DEN), reads=[b_misc], wadd=[b_misc])
        tt_(CR, NR, LR, ALU.mult)
        tt_(TMPA, pw[:, 1, 1, :], LI, ALU.mult)
        tt_(CR, CR, TMPA, ALU.add)
        tt_(CR, CR, DEN, ALU.mult)
        tt_(CI, pw[:, 1, 1, :], LR, ALU.mult)
        tt_(TMPA, NR, LI, ALU.mult)
        tt_(CI, CI, TMPA, ALU.subtract)
        tt_(CI, CI, DEN, ALU.mult)
        CRb = CR.unsqueeze(2).to_broadcast([128, 32, 16]); CIb = CI.unsqueeze(2).to_broadcast([128, 32, 16])
        T1b = t1[:, 0:512].rearrange("p (g c) -> p g c", g=32); T2b = t2[:, 0:512].rearrange("p (g c) -> p g c", g=32)
        P.op(G, lambda e: e.tensor_tensor(out=T1b, in0=braw[0], in1=CRb, op=ALU.mult), reads=[b_braw, b_misc, b_pw], writes=[b_t])
        P.op(G, lambda e: e.tensor_tensor(out=T2b, in0=braw[1], in1=CIb, op=ALU.mult), reads=[b_braw, b_misc], wadd=[b_t])
        P.op(G, lambda e: e.tensor_tensor(out=bb[:, 0], in0=T1b, in1=T2b, op=ALU.subtract), reads=[b_t], wadd=[b_bb])
        P.op(G, lambda e: e.tensor_tensor(out=T1b, in0=braw[1], in1=CRb, op=ALU.mult), reads=[b_braw, b_misc, b_bb], writes=[b_t])
        P.op(G, lambda e: e.tensor_tensor(out=T2b, in0=braw[0], in1=CIb, op=ALU.mult), reads=[b_braw, b_misc], wadd=[b_t])
        P.op(G, lambda e: e.tensor_tensor(out=bb[:, 1], in0=T1b, in1=T2b, op=ALU.add), reads=[b_t], wadd=[b_bb])

        def rev_table(R, A0, bufR, out_last):
            AW = misc[:, 20:22, :].rearrange("p (o r) g -> p o r g", o=1)
            AW2 = misc[:, 22:24, :].rearrange("p (o r) g -> p o r g", o=1)
            P.op(G, lambda e: e.memset(R[:, 15, 0, :], 1.0), wadd=[bufR])
            P.op(G, lambda e: e.memset(R[:, 15, 1, :], 0.0), wadd=[bufR])
            P.op(G, lambda e: e.tensor_copy(out=AW, in_=A0), reads=[b_pw, b_coef, b_misc], wadd=[b_misc])
            w = 1
            cur, nxt = AW, AW2
            while w <= 8:
                cm(R[:, 16 - 2 * w:16 - w], R[:, 16 - w:16], bc(cur, w), w, [bufR, b_misc], [bufR])
                cm(nxt, cur, cur, 1, [b_misc], [b_misc])
                cur, nxt = nxt, cur
                w *= 2
            P.op(G, lambda e: e.tensor_copy(out=out_last, in_=cur), reads=[b_misc], wadd=[b_coef])

        A128 = coef[:, 0:2, :].rearrange("p (o r) g -> p o r g", o=1)
        A2048 = coef[:, 2:4, :].rearrange("p (o r) g -> p o r g", o=1)
        rev_table(R8, pw[:, 8:9], b_R8, A128)
        rev_table(R128, A128, b_R128, A2048)
        P.op(G, lambda e: e.memset(A2k[:, 0, 0, :], 1.0), wadd=[b_A2k])
        P.op(G, lambda e: e.memset(A2k[:, 0, 1, :], 0.0), wadd=[b_A2k])
        P.op(G, lambda e: e.tensor_copy(out=A2k[:, 1:2], in_=A2048), reads=[b_coef], wadd=[b_A2k])
        cm(A2k[:, 2:3], A2048, A2048, 1, [b_coef], [b_A2k])
        P.op(G, lambda e: e.tensor_scalar(out=coef[:, 4, :], in0=pw[:, 8, 1, :], scalar1=-1.0, scalar2=0.0, op0=ALU.mult, op1=ALU.add),
             reads=[b_pw], wadd=[b_coef])
        P.op(G, lambda e: e.tensor_scalar(out=coef[:, 5, :], in0=coef[:, 1, :], scalar1=-1.0, scalar2=0.0, op0=ALU.mult, op1=ALU.add),
             reads=[b_coef], wadd=[b_coef])

        P.phase(5)
        WinL = cv_(O_B, [8, 8, 2, 128], BF16)
        b_WinL = [Buf(f"WinL{g}") for g in range(8)]
        b_Sb = Buf("Sb")
        handoff(b_Sslot, [b_craw, b_lam, b_ld, b_draw, b_braw])
        b_SInit = [Buf("SInit0"), Buf("SInit1")]
        for i in range(2):
            P.op(POOL, lambda e, i=i: e.memset(Sslot[i].rearrange("p k r c -> p (k r c)"), 0.0), writes=[b_Sslot[i], b_SInit[i]])
        T1e = [t1[:, 0:512].rearrange("p (k m c) -> p k m c", k=8, m=4), t1[:, 512:1024].rearrange("p (k m c) -> p k m c", k=8, m=4)]
        T2e = [t2[:, 0:512].rearrange("p (k m c) -> p k m c", k=8, m=4), t2[:, 512:1024].rearrange("p (k m c) -> p k m c", k=8, m=4)]
        b_t5 = [b_t, Buf("t5pool")]
        handoff([b_t5[1]], [b_t])
        for gc in range(8):
            sl = gc % 2
            EG = DVE if gc % 2 == 0 else POOL
            T1s, T2s, b_tt = T1e[gc % 2], T2e[gc % 2], b_t5[gc % 2]
            S = Sslot[sl]
            Sv = S.rearrange("p k r (m g c) -> p k r m g c", m=4, g=2)
            prk = pw[:, 0:8, 0, 4 * gc:4 * gc + 4].unsqueeze(3).to_broadcast([128, 8, 4, 16])
            pik = pw[:, 0:8, 1, 4 * gc:4 * gc + 4].unsqueeze(3).to_broadcast([128, 8, 4, 16])
            bbr = bb[:, 0, 4 * gc:4 * gc + 4, :].unsqueeze(1).to_broadcast([128, 8, 4, 16])
            bbi = bb[:, 1, 4 * gc:4 * gc + 4, :].unsqueeze(1).to_broadcast([128, 8, 4, 16])
            for ri in range(2):
                x1, x2 = (bbr, bbi) if ri == 0 else (bbi, bbr)
                op = ALU.subtract if ri == 0 else ALU.add
                P.op(EG, lambda e, x1=x1, prk=prk, T1s=T1s: e.tensor_tensor(out=T1s, in0=prk, in1=x1, op=ALU.mult), reads=[b_pw, b_bb], writes=[b_tt])
                P.op(EG, lambda e, x2=x2, pik=pik, T2s=T2s: e.tensor_tensor(out=T2s, in0=pik, in1=x2, op=ALU.mult), reads=[b_pw, b_bb], wadd=[b_tt])
                for g2 in range(2):
                    lo, hi = 64 * g2, 64 * g2 + 64
                    P.op(EG, lambda e, ri=ri, g2=g2, lo=lo, hi=hi, op=op, Sv=Sv, T1s=T1s, T2s=T2s: e.tensor_tensor(
                        out=Sv[lo:hi, :, ri, :, g2, :], in0=T1s[lo:hi], in1=T2s[lo:hi], op=op),
                        reads=[b_tt, b_SInit[sl]], wadd=[b_Sslot[sl]])
            P.op(EG, lambda e, gc=gc, S=S: e.tensor_copy(out=Sb[:, gc], in_=S[:, 0]), reads=[b_Sslot[sl]], wadd=[b_Sb])
            for q4 in range(4):
                bkw = bank()
                for j in range(4):
                    k_, ri_ = (4 * q4 + j) // 2, (4 * q4 + j) % 2
                    P.op(PE, lambda e, bkw=bkw, j=j, k_=k_, ri_=ri_, S=S: e.transpose(
                        out=ps[bkw][:, 128 * j:128 * j + 128], in_=S[:, k_, ri_, :], identity=ident),
                        reads=[b_Sslot[sl], b_ident], writes=[psb[bkw]] if j == 0 else [], wadd=[psb[bkw]] if j > 0 else [])
                dstw = WinL[:, gc, 2 * q4:2 * q4 + 2].rearrange("p k r c -> p (k r c)")
                evac_copy(q4, dstw, ps[bkw][:, 0:512], [psb[bkw]], [], wadd=[b_WinL[gc]])

        t1p = cv_(O_S, [2, 16, 16], F32); t2p = cv_(O_S + 2048, [2, 16, 16], F32); cbp = cv_(O_S + 4096, [2, 16, 16], F32)
        b_p1 = Buf("p1tmp")
        handoff([b_p1], b_Sslot)
        b_Hend = Buf("Hend")

        def x_matmuls(gc, banks):
            for ri in range(2):
                for s_ in range(8):
                    for m in range(4):
                        first = (ri == 0 and s_ == 0)
                        last = (ri == 1 and s_ == 7)
                        bkx = banks[m]
                        P.op(PE, lambda e, m=m, ri=ri, s_=s_, bkx=bkx, gc=gc: e.matmul(
                            ps[bkx][:, 256 * ri:256 * ri + 256], lhsT=WinL[32 * m:32 * m + 32, gc, 7 - s_, ri, :],
                            rhs=uT[32 * m:32 * m + 32, gc, s_:NP:8], start=(s_ == 0), stop=(s_ == 7),
                            tile_position=(32 * m, 0)),
                            reads=[b_WinL[gc], uT_b[gc]] if first else [], writes=[psb[bkx]] if first else [], sig=last)
                        if last and not P.dead:
                            P.attach(Dep(P.esem[PE], P.cnt[PE]), reads=[b_WinL[gc], uT_b[gc]], writes=[psb[bkx]])

        def seg_reduce(src_ap, Rtab, gp0, ngp, out_ap, rbufs, wbuf):
            raise NotImplementedError

        def pass1():
            for gc in range(8):
                banks = [bank() for _ in range(4)]
                x_matmuls(gc, banks)
                if DEBUG and gc == 0 and os.environ.get('KX0'):
                    xdbg = cv_(O_S + 6144, [512], F32); b_xdbg = Buf()
                    P.op(DVE, lambda e: e.tensor_copy(out=xdbg, in_=ps[banks[1]][:, :]), reads=[psb[banks[1]]], writes=[b_xdbg])
                    P.dma(SP, dout_sem, dbg["X0"], xdbg, reads=[b_xdbg])
                for m in range(4):
                    gp = 4 * gc + m
                    X4 = ps[banks[m]][:, :].rearrange("p (r s i) -> p r s i", r=2, s=16)
                    Pr = R8[:, :, 0, gp].unsqueeze(1).unsqueeze(1).to_broadcast([128, 2, 16, 16])
                    Pi = R8[:, :, 1, gp].unsqueeze(1).unsqueeze(1).to_broadcast([128, 2, 16, 16])
                    P.op(DVE, lambda e, X4=X4, Pr=Pr: e.tensor_tensor(out=t1p, in0=X4, in1=Pr, op=ALU.mult),
                         reads=[psb[banks[m]], b_R8], writes=[b_p1])
                    P.op(DVE, lambda e, X4=X4, Pi=Pi: e.tensor_tensor(out=t2p, in0=X4, in1=Pi, op=ALU.mult),
                         reads=[psb[banks[m]], b_R8], wadd=[b_p1])
                    P.op(DVE, lambda e: e.tensor_tensor(out=cbp[:, 0], in0=t1p[:, 0], in1=t2p[:, 1], op=ALU.subtract), reads=[b_p1], wadd=[b_p1])
                    P.op(DVE, lambda e: e.tensor_tensor(out=cbp[:, 1], in0=t2p[:, 0], in1=t1p[:, 1], op=ALU.add), reads=[b_p1], wadd=[b_p1])
                    P.op(DVE, lambda e, gp=gp: e.tensor_reduce(out=Hend[:, :, gp, :], in_=cbp, axis=AX.X, op=ALU.add),
                         reads=[b_p1], wadd=[b_Hend])


        Ecore = cv_(O_S + 6144, [2, 32], F32); Eall = cv_(O_S + 6400, [8, 64], F32)
        te1 = cv_(O_S + 0, [2, 32, 16], F32); te2 = cv_(O_S + 8448, [2, 32, 16], F32)
        b_E = Buf("Ecore"); b_Eall = Buf("Eall"); b_te = b_p1
        handoff([b_E, b_Eall], b_Sslot)
        def ecore(j):
            Qr = R128[:, :, 0, :].rearrange("p i g -> p g i").unsqueeze(1).to_broadcast([128, 2, 32, 16])
            Qi = R128[:, :, 1, :].rearrange("p i g -> p g i").unsqueeze(1).to_broadcast([128, 2, 32, 16])
            P.op(DVE, lambda e: e.tensor_tensor(out=te1, in0=Hend, in1=Qr, op=ALU.mult), reads=[b_Hend, b_R128], writes=[b_te])
            P.op(DVE, lambda e: e.tensor_tensor(out=te2, in0=Hend, in1=Qi, op=ALU.mult), reads=[b_Hend, b_R128], wadd=[b_te])
            P.op(DVE, lambda e: e.tensor_tensor(out=te1[:, 0], in0=te1[:, 0], in1=te2[:, 1], op=ALU.subtract), reads=[b_te], writes=[b_te])
            P.op(DVE, lambda e: e.tensor_tensor(out=te2[:, 0], in0=te2[:, 0], in1=te1[:, 1], op=ALU.add), reads=[b_te], writes=[b_te])
            P.op(DVE, lambda e: e.tensor_reduce(out=Ecore[:, 0, :], in_=te1[:, 0], axis=AX.X, op=ALU.add), reads=[b_te], wadd=[b_E])
            P.op(DVE, lambda e: e.tensor_reduce(out=Ecore[:, 1, :], in_=te2[:, 0], axis=AX.X, op=ALU.add), reads=[b_te], wadd=[b_E])

            if j is not None:
                P.op(DVE, lambda e, j=j: e.tensor_copy(out=Eall[:, j, :], in_=Ecore.rearrange("p r g -> p (r g)")), reads=[b_E], wadd=[b_Eall])

        P.phase(3)
        srcs = [(xprev[1], False), (xprev[2], False), (xp, True)]
        phase_a(*srcs[0])
        for j in range(3):
            pass1()
            ecore(j)
            phase_b(srcs[j][1])
            if j < 2:
                phase_a(*srcs[j + 1])
        P.phase(6)
        pass1()
        P.phase(7)
        Sn = [cv_(O_S + 12544 + 256 * n, [2, 32], F32) for n in range(3)]
        b_Sn = Buf("Sn")
        handoff([b_Sn], b_Sslot)
        for n in range(3):
            Snf = Sn[n].rearrange("p r g -> p (r g)")
            P.op(DVE, lambda e, n=n, Snf=Snf: e.tensor_scalar(out=Snf, in0=Eall[:, n, :], scalar1=flags[:, 1 + n:2 + n], scalar2=None,
                                                            op0=ALU.mult), reads=[b_Eall, b_flags], wadd=[b_Sn])
        b_carry = Buf("carry")
        tq1 = cv_(O_S + 13312, [2, 32], F32); tq2 = cv_(O_S + 13568, [2, 32], F32)
        b_tq = Buf("tq")
        handoff([b_tq], b_Sslot)

        def cmul_small(dst, x, y, rb, wb):
            cmul(DVE, dst[:, 0, :], dst[:, 1, :], x[:, 0, :], x[:, 1, :], y[:, 0, :], y[:, 1, :], tq1[:, 0, :], tq1[:, 1, :], rb, wb, b_tq)

        cmul_small(carry[:, 1], Sn[1], A2k[:, 1], [b_Sn, b_A2k], [b_carry])
        cmul_small(carry[:, 2], Sn[2], A2k[:, 2], [b_Sn, b_A2k, b_carry], [b_carry])
        P.op(DVE, lambda e: e.tensor_tensor(out=Sn[0], in0=Sn[0], in1=carry[:, 1], op=ALU.add), reads=[b_Sn, b_carry], writes=[b_Sn])
        P.op(DVE, lambda e: e.tensor_tensor(out=carry[:, 0], in0=Sn[0], in1=carry[:, 2], op=ALU.add), reads=[b_Sn, b_carry], writes=[b_carry])
        A128v = coef[:, 0:2, :]
        for sg in range(16):
            cmul_small(carry[:, sg + 1], carry[:, sg], A128v, [b_carry, b_coef], [b_carry])
            P.op(DVE, lambda e, sg=sg: e.tensor_tensor(out=carry[:, sg + 1], in0=carry[:, sg + 1], in1=Hend[:, :, :, sg], op=ALU.add),
                 reads=[b_carry, b_Hend], writes=[b_carry])
        pstT = cv_(O_S + 13824, [2, 128], F32)
        b_pst = Buf()
        handoff([b_pst], b_Sslot)
        bkp = bank()
        for ri in range(2):
            P.op(PE, lambda e, ri=ri: e.transpose(out=ps[bkp][0:32, 128 * ri:128 * ri + 128], in_=carry[:, 16, ri, :], identity=ident),
                 reads=[b_carry, b_ident], writes=[psb[bkp]] if ri == 0 else [], wadd=[psb[bkp]] if ri == 1 else [])
        P.op(DVE, lambda e: e.tensor_copy(out=pstT[0:32].rearrange("p r c -> p (r c)"), in_=ps[bkp][0:32, 0:256]), reads=[psb[bkp]], writes=[b_pst])
        P.dma(SP, osem_new("pre"), pst_re, pstT[0:32, 0, :], reads=[b_pst])
        P.dma(SP, osem_new("pim"), pst_im, pstT[0:32, 1, :], reads=[b_pst])

        P.phase(8)
        CaBD = [cv_(O_CA + 4608 * i, [4, 9, 2, 32], BF16) for i in range(2)]
        KLs = [cv_(O_KL + 2048 * i, [8, 128], BF16) for i in range(2)]
        Hb = [cv_(O_HB + 4096 * i, [2, 4, 256], BF16) for i in range(2)]
        Xs = [cv_(O_S + 8192 * i, [2, 4, 256], F32) for i in range(2)]
        b_CaBD = [Buf("Ca0"), Buf("Ca1")]; b_KL = [Buf("KL0"), Buf("KL1")]; b_Hb = [Buf("Hb0"), Buf("Hb1")]; b_Xs = [Buf("Xs0"), Buf("Xs1")]
        old_c = [b_ct, b_csg, b_ccT, b_bin, b_baT, xst_b[0], xst_b[1]]
        handoff(b_CaBD + b_KL + b_Hb, old_c)
        handoff(b_Xs, [b_p1, b_E, b_Eall, b_te, b_Sn, b_tq, b_pst] + b_Sslot)
        KL0all = cv_(O_C + 10496, [8, 128], BF16); Ca1all = cv_(O_C + 14592, [32, 2, 32], BF16)
        b_KL0 = Buf("KL0all"); b_Ca1 = Buf("Ca1all")
        handoff([b_KL0], [b_R8]); handoff([b_Ca1], [b_R128])
        Q1e = [misc[:, 24:28, :].rearrange("p a g -> p (a g)").rearrange("p (r m s) -> p r m s", r=2, m=4),
               misc[:, 0:4, :].rearrange("p a g -> p (a g)").rearrange("p (r m s) -> p r m s", r=2, m=4)]
        Q2e = [misc[:, 28:32, :].rearrange("p a g -> p (a g)").rearrange("p (r m s) -> p r m s", r=2, m=4),
               misc[:, 4:8, :].rearrange("p a g -> p (a g)").rearrange("p (r m s) -> p r m s", r=2, m=4)]
        tmpK = misc[:, 16:20, :].rearrange("p a g -> p (a g)")
        b_Q = [Buf("Qdve"), Buf("Qpool")]
        b_tmpK = Buf("tmpK")
        handoff([b_tmpK] + b_Q, [b_misc])
        b_CaInit = [Buf("CaInit0"), Buf("CaInit1")]
        for i in range(2):
            P.op(POOL, lambda e, i=i: e.memset(CaBD[i].rearrange("p m n r c -> p (m n r c)"), 0.0), writes=[b_CaBD[i], b_CaInit[i]])
        U1 = t1[:, 0:576].rearrange("p (m n c) -> p m n c", m=4, n=9)
        U2 = t2[:, 0:576].rearrange("p (m n c) -> p m n c", m=4, n=9)
        XB = [2, 3, 4, 5]
        YB = [6, 7]

        def emit_consts(gc):
            sl = gc % 2
            Ca = CaBD[sl]
            cre = ccm[:, 0, 4 * gc:4 * gc + 4, :].unsqueeze(2).to_broadcast([128, 4, 9, 16])
            cim = ccm[:, 1, 4 * gc:4 * gc + 4, :].unsqueeze(2).to_broadcast([128, 4, 9, 16])
            pr = pw[:, :, 0, 4 * gc:4 * gc + 4].rearrange("p n m -> p m n").unsqueeze(3).to_broadcast([128, 4, 9, 16])
            pi = pw[:, :, 1, 4 * gc:4 * gc + 4].rearrange("p n m -> p m n").unsqueeze(3).to_broadcast([128, 4, 9, 16])
            P.op(DVE, lambda e: e.tensor_tensor(out=U1, in0=cre, in1=pr, op=ALU.mult), reads=[b_ccm, b_pw], writes=[b_t])
            P.op(DVE, lambda e: e.tensor_tensor(out=U2, in0=cim, in1=pi, op=ALU.mult), reads=[b_ccm, b_pw], wadd=[b_t])
            for g2 in range(2):
                lo, hi = 64 * g2, 64 * g2 + 64
                P.op(DVE, lambda e, lo=lo, hi=hi, g2=g2: e.tensor_tensor(
                    out=Ca[lo:hi, :, :, 0, 16 * g2:16 * g2 + 16], in0=U1[lo:hi], in1=U2[lo:hi], op=ALU.subtract),
                    reads=[b_t, b_CaInit[sl]], wadd=[b_CaBD[sl]])
            P.op(DVE, lambda e: e.tensor_tensor(out=U1, in0=cre, in1=pi, op=ALU.mult), reads=[b_ccm, b_pw], writes=[b_t])
            P.op(DVE, lambda e: e.tensor_tensor(out=U2, in0=cim, in1=pr, op=ALU.mult), reads=[b_ccm, b_pw], wadd=[b_t])
            P.op(DVE, lambda e: e.tensor_tensor(out=U1, in0=U1, in1=U2, op=ALU.add), reads=[b_t], writes=[b_t])
            for g2 in range(2):
                lo, hi = 64 * g2, 64 * g2 + 64
                P.op(DVE, lambda e, lo=lo, hi=hi, g2=g2: e.tensor_scalar(
                    out=Ca[lo:hi, :, :, 1, 16 * g2:16 * g2 + 16], in0=U1[lo:hi], scalar1=-1.0, scalar2=0.0, op0=ALU.mult, op1=ALU.add),
                    reads=[b_t, b_CaInit[sl]], wadd=[b_CaBD[sl]])
            P.op(DVE, lambda e: e.tensor_copy(out=Ca1all[:, 4 * gc:4 * gc + 4], in_=Ca[:, :, 1, :, :]), reads=[b_CaBD[sl]], wadd=[b_Ca1])
            for hb in range(2):
                for tt in range(4):
                    tau = 4 * hb + tt
                    for ri in range(2):
                        P.op(PE, lambda e, hb=hb, tt=tt, tau=tau, ri=ri: e.matmul(
                            ps[hb][:, 128 * tt:128 * tt + 128], lhsT=Sb[:, gc, ri, :], rhs=Ca[:, :, tau, ri, :],
                            start=(ri == 0), stop=(ri == 1)),
                            reads=[b_Sb, b_CaBD[sl]] if (tt == 0 and ri == 0) else [],
                            writes=[psb[hb]] if (tt == 0 and ri == 0) else [], sig=(tt == 3 and ri == 1))
                if not P.dead:
                    P.attach(Dep(P.esem[PE], P.cnt[PE]), reads=[b_Sb, b_CaBD[sl]], writes=[psb[hb]])
            KL = KLs[sl]
            bmb3 = bmask.unsqueeze(1).to_broadcast([128, 3, 128]); bmb4 = bmask.unsqueeze(1).to_broadcast([128, 4, 128])
            P.op(DVE, lambda e: e.tensor_tensor(out=KL[:, 1:4, :], in0=ps[0][:, 128:512].rearrange("p (t c) -> p t c", t=3), in1=bmb3, op=ALU.mult),
                 reads=[psb[0], b_bmask], wadd=[b_KL[sl]])
            P.op(DVE, lambda e: e.tensor_tensor(out=tmpK, in0=ps[0][:, 0:128], in1=bmask, op=ALU.mult), reads=[psb[0], b_bmask], writes=[b_tmpK])
            P.op(DVE, lambda e: e.tensor_tensor(out=KL[:, 4:8, :], in0=ps[1][:, 0:512].rearrange("p (t c) -> p t c", t=4), in1=bmb4, op=ALU.mult),
                 reads=[psb[1], b_bmask], wadd=[b_KL[sl]])
            P.op(DVE, lambda e: e.scalar_tensor_tensor(out=KL[:, 0, :], in0=ident, scalar=Dm[:, gc:gc + 1], in1=tmpK, op0=ALU.mult, op1=ALU.add),
                 reads=[b_tmpK, b_ident, b_Dm], wadd=[b_KL[sl]])
            P.op(DVE, lambda e: e.tensor_copy(out=KL0all[:, gc, :], in_=KL[:, 0, :]), reads=[b_KL[sl]], wadd=[b_KL0])

        def emit_x_scan(gc):
            sl = gc % 2
            x_matmuls(gc, XB)
            X = Xs[sl]
            for m in range(4):
                P.op(ACT, lambda e, m=m: e.activation(out=X[:, :, m, :], in_=ps[XB[m]][:, :].rearrange("p (r j) -> p r j", r=2), func=AF.Copy),
                     reads=[psb[XB[m]]], wadd=[b_Xs[sl]])
            E, qi = (POOL, 1) if gc in (1, 4, 6) else (DVE, 0)
            Q1 = Q1e[qi]; Q2 = Q2e[qi]
            X5 = X.rearrange("p r m (s i) -> p r m s i", i=16)
            Ar = pw[:, 8, 0, 4 * gc:4 * gc + 4].unsqueeze(1).unsqueeze(3).to_broadcast([128, 2, 4, 16])
            Ai = pw[:, 8, 1, 4 * gc:4 * gc + 4].unsqueeze(2).to_broadcast([128, 4, 16])
            AiN = coef[:, 4, 4 * gc:4 * gc + 4].unsqueeze(2).to_broadcast([128, 4, 16])
            cview = carry[:, 0:16, :, 4 * gc:4 * gc + 4].rearrange("p s r m -> p r m s")
            for i in range(16):
                prev = cview if i == 0 else X5[:, :, :, :, i - 1]
                cur = X5[:, :, :, :, i]
                rb = [b_carry, b_pw, b_coef, b_Xs[sl]]
                P.op(E, lambda e, prev=prev: e.tensor_tensor(out=Q1, in0=prev, in1=Ar, op=ALU.mult), reads=rb, writes=[b_Q[qi]])
                P.op(E, lambda e, prev=prev: e.tensor_tensor(out=Q2[:, 0], in0=prev[:, 1], in1=AiN, op=ALU.mult), reads=rb, wadd=[b_Q[qi]])
                P.op(E, lambda e, prev=prev: e.tensor_tensor(out=Q2[:, 1], in0=prev[:, 0], in1=Ai, op=ALU.mult), reads=rb, wadd=[b_Q[qi]])
                P.op(E, lambda e, cur=cur: e.tensor_tensor(out=cur, in0=cur, in1=Q1, op=ALU.add), reads=[b_Q[qi]], writes=[b_Xs[sl]])
                P.op(E, lambda e, cur=cur: e.tensor_tensor(out=cur, in0=cur, in1=Q2, op=ALU.add), reads=[b_Q[qi]], writes=[b_Xs[sl]])

        def emit_hb(gc):
            sl = gc % 2
            X = Xs[sl]
            X5 = X.rearrange("p r m (s i) -> p r m s i", i=16)
            cview = carry[:, 0:16, :, 4 * gc:4 * gc + 4].rearrange("p s r m -> p r m s")
            H5 = Hb[sl].rearrange("p r m (s i) -> p r m s i", i=16)
            P.op(ACT, lambda e: e.activation(out=H5[:, :, :, :, 1:16].rearrange("p r m s i -> p (r m) s i"),
                                             in_=X5[:, :, :, :, 0:15].rearrange("p r m s i -> p (r m) s i"), func=AF.Copy),
                 reads=[b_Xs[sl]], writes=[b_Hb[sl]])
            P.op(ACT, lambda e: e.activation(out=H5[:, :, :, :, 0], in_=cview, func=AF.Copy), reads=[b_carry], wadd=[b_Hb[sl]])

        def emit_y(gc):
            sl = gc % 2
            KL = KLs[sl]; Ca = CaBD[sl]
            uview = uT[:, gc, 0:NP].rearrange("p (j s) -> p s j", s=8)
            for half in (1, 0):
                for tl in range(4):
                    t_lo = 4 * half + tl
                    bk_ = YB[tl // 2]
                    reg = ps[bk_][:, 256 * (tl % 2):256 * (tl % 2) + 256]
                    n_mm = (t_lo + 1) + 8
                    idx = 0
                    for s_ in range(t_lo + 1):
                        P.op(PE, lambda e, reg=reg, s_=s_, t_lo=t_lo: e.matmul(
                            reg, lhsT=KL[:, t_lo - s_, :], rhs=uview[:, s_, :], start=(s_ == 0), stop=False),
                            reads=[b_KL[sl], uT_b[gc], b_CaBD[sl], b_Hb[sl]] if idx == 0 else [],
                            writes=[psb[bk_]] if (idx == 0 and tl % 2 == 0) else [], sig=False)
                        idx += 1
                    for m in range(4):
                        for ri in range(2):
                            lastmm = (m == 3 and ri == 1)
                            P.op(PE, lambda e, reg=reg, m=m, ri=ri, t_lo=t_lo, lastmm=lastmm: e.matmul(
                                reg[32 * m:32 * m + 32, :], lhsT=Ca[:, m, t_lo + 1, ri, :], rhs=Hb[sl][:, ri, m, :],
                                start=False, stop=(ri == 1), tile_position=(0, 32 * m)), sig=lastmm)
                    if not P.dead:
                        P.attach(Dep(P.esem[PE], P.cnt[PE]), reads=[b_KL[sl], uT_b[gc], b_CaBD[sl], b_Hb[sl]],
                                 writes=[psb[bk_]] if tl % 2 == 1 else [], wadd=[psb[bk_]] if tl % 2 == 0 else [])
                for bi in range(2):
                    t0_ = 4 * half + 2 * bi
                    P.op(ACT, lambda e, bi=bi, t0_=t0_: e.activation(
                        out=uview[:, t0_:t0_ + 2, :], in_=ps[YB[bi]][:, :].rearrange("p (t j) -> p t j", t=2), func=AF.Gelu_apprx_tanh),
                        reads=[psb[YB[bi]]], writes=[uT_b[gc]])

        for g0 in range(2):
            emit_consts(g0)
            emit_x_scan(g0)
            emit_hb(g0)
        for gc in range(8):
            if gc + 2 < 8:
                emit_x_scan(gc + 2)
            emit_y(gc)
            if gc + 2 < 8:
                emit_consts(gc + 2)
                emit_hb(gc + 2)
        yT = uT
        yT_b = uT_b

        P.phase(9)
        all_ssm_tmp = b_Xs + b_Hb + b_Q + [b_tmpK, b_t, b_p1, b_E, b_Eall, b_te, b_Sn, b_tq, b_pst] + b_Sslot
        stile = [cv_(O_S + 2048 * i, [512], F32) for i in range(2)]
        Hsp = cv_(O_S + 4096, [2, 32, 16], F32); Hn = cv_(O_S + 8192, [2, 32, 16], F32)
        HbS = cv_(O_S + 12288, [2, 32, 16], BF16); Q1s = cv_(O_HB, [2, 32, 16], F32); Q2s = cv_(O_HB + 4096, [2, 32, 16], F32)
        b_stile = Buf(); b_Hsp = Buf(); b_Hn = Buf(); b_HbS = Buf(); b_Qs = Buf()
        handoff([b_stile, b_Hsp, b_Hn, b_HbS, b_Qs], all_ssm_tmp)
        misc_load(SP, stile[0], st_re.rearrange("s (gh f) -> (s gh) f", gh=8), b_stile, wadd=True)
        misc_load(SP, stile[1], st_im.rearrange("s (gh f) -> (s gh) f", gh=8), b_stile, wadd=True)
        for ri in range(2):
            bks = bank()
            for q4 in range(4):
                P.op(PE, lambda e, ri=ri, q4=q4, bks=bks: e.transpose(out=ps[bks][:, 128 * q4:128 * q4 + 128],
                                                                   in_=stile[ri][:, 128 * q4:128 * q4 + 128], identity=ident),
                     reads=[b_stile, b_ident], writes=[psb[bks]] if q4 == 0 else [], wadd=[psb[bks]] if q4 > 0 else [])
            for q4 in range(4):
                P.op(DVE, lambda e, ri=ri, q4=q4, bks=bks: e.tensor_copy(
                    out=Hsp[:, ri, q4:32:4, :], in_=ps[bks][:, 128 * q4:128 * q4 + 128].rearrange("p (s gh) -> p gh s", gh=8)),
                    reads=[psb[bks]], wadd=[b_Hsp])
        P.op(POOL, lambda e: e.tensor_copy(out=HbS, in_=Hsp), reads=[b_Hsp], writes=[b_HbS])
        xsb = [bank() for _ in range(4)]
        for m in range(4):
            for gc in range(8):
                for ri in range(2):
                    first = (gc == 0 and ri == 0); last = (gc == 7 and ri == 1)
                    P.op(PE, lambda e, m=m, gc=gc, ri=ri: e.matmul(
                        ps[xsb[m]][:, 32 * gc + 16 * ri:32 * gc + 16 * ri + 16], lhsT=WinL[32 * m:32 * m + 32, gc, 0, ri, :],
                        rhs=uT[32 * m:32 * m + 32, gc, NP:NT], start=True, stop=True, tile_position=(32 * m, 0)),
                        reads=b_WinL + uT_b if first else [], writes=[psb[xsb[m]]] if first else [], sig=last)
            if not P.dead:
                P.attach(Dep(P.esem[PE], P.cnt[PE]), reads=b_WinL + uT_b, writes=[psb[xsb[m]]])
        Ar1 = pw[:, 1, 0, :].unsqueeze(1).unsqueeze(3).to_broadcast([128, 2, 32, 16])
        Ai1 = pw[:, 1, 1, :].unsqueeze(2).to_broadcast([128, 32, 16])
        P.op(POOL, lambda e: e.tensor_scalar(out=coef[:, 6, :], in0=pw[:, 1, 1, :], scalar1=-1.0, scalar2=0.0, op0=ALU.mult, op1=ALU.add),
             reads=[b_pw], wadd=[b_coef])
        AiN1 = coef[:, 6, :].unsqueeze(2).to_broadcast([128, 32, 16])
        P.op(DVE, lambda e: e.tensor_tensor(out=Q1s, in0=Hsp, in1=Ar1, op=ALU.mult), reads=[b_Hsp, b_pw], writes=[b_Qs])
        P.op(DVE, lambda e: e.tensor_tensor(out=Q2s[:, 0], in0=Hsp[:, 1], in1=AiN1, op=ALU.mult), reads=[b_Hsp, b_coef], wadd=[b_Qs])
        P.op(DVE, lambda e: e.tensor_tensor(out=Q2s[:, 1], in0=Hsp[:, 0], in1=Ai1, op=ALU.mult), reads=[b_Hsp, b_pw], wadd=[b_Qs])
        P.op(DVE, lambda e: e.tensor_tensor(out=Hn, in0=Q1s, in1=Q2s, op=ALU.add), reads=[b_Qs], writes=[b_Hn])
        for m in range(4):
            P.op(DVE, lambda e, m=m: e.tensor_tensor(
                out=Hn[:, :, m:32:4, :], in0=ps[xsb[m]][:, 0:256].rearrange("p (gc r s) -> p r gc s", gc=8, r=2),
                in1=Hn[:, :, m:32:4, :], op=ALU.add), reads=[psb[xsb[m]], b_Hn], writes=[b_Hn])
        stg = cv_(O_HB + 8192 - 8192, [4, 128], F32)
        sout = [cv_(O_S + 2048 * i, [512], F32) for i in range(2)]
        b_stg = Buf(); b_sout = Buf()
        handoff([b_stg], [b_Qs]); handoff([b_sout], [b_stile])
        for ri in range(2):
            for q4 in range(4):
                P.op(POOL, lambda e, ri=ri, q4=q4: e.tensor_copy(out=stg[:, q4, :].rearrange("p (s gh) -> p gh s", gh=8),
                                                               in_=Hn[:, ri, q4:32:4, :]), reads=[b_Hn], writes=[b_stg] if q4 == 0 else [],
                     wadd=[b_stg] if q4 > 0 else [])
            bks = bank()
            for q4 in range(4):
                P.op(PE, lambda e, q4=q4, bks=bks: e.transpose(out=ps[bks][:, 128 * q4:128 * q4 + 128], in_=stg[:, q4, :], identity=ident),
                     reads=[b_stg, b_ident], writes=[psb[bks]] if q4 == 0 else [], wadd=[psb[bks]] if q4 > 0 else [])
            P.op(DVE, lambda e, ri=ri, bks=bks: e.tensor_copy(out=sout[ri], in_=ps[bks][:, 0:512]), reads=[psb[bks]], wadd=[b_sout])
            P.dma(SP, osem_new(f"sst{ri}"), (sst_re if ri == 0 else sst_im).rearrange("s (gh f) -> (s gh) f", gh=8), sout[ri], reads=[b_sout])
        bky = bank()
        for gc in range(8):
            reg = ps[bky][:, 16 * gc:16 * gc + 16]
            P.op(PE, lambda e, gc=gc, reg=reg: e.matmul(reg, lhsT=KL0all[:, gc, :], rhs=uT[:, gc, NP:NT], start=True, stop=False),
                 reads=[b_KL0, b_Ca1, b_HbS] + uT_b if gc == 0 else [], writes=[psb[bky]] if gc == 0 else [], sig=False)
            for m in range(4):
                for ri in range(2):
                    lastmm = (m == 3 and ri == 1)
                    P.op(PE, lambda e, gc=gc, reg=reg, m=m, ri=ri, lastmm=lastmm: e.matmul(
                        reg[32 * m:32 * m + 32, :], lhsT=Ca1all[:, 4 * gc + m, ri, :], rhs=HbS[:, ri, 4 * gc + m, :],
                        start=False, stop=(ri == 1), tile_position=(0, 32 * m)), sig=(lastmm and gc == 7))
        if not P.dead:
            P.attach(Dep(P.esem[PE], P.cnt[PE]), reads=[b_KL0, b_Ca1, b_HbS] + uT_b, writes=[psb[bky]])
        P.op(ACT, lambda e: e.activation(out=uT[:, :, NP:NT], in_=ps[bky][:, 0:128].rearrange("p (g s) -> p g s", g=8),
                                         func=AF.Gelu_apprx_tanh), reads=[psb[bky]], writes=uT_b)

        P.phase(10)
        s2T = RB
        s2_b = [Buf(f"s2_{g}") for g in range(8)]
        handoff(s2_b, b_WinL)
        gtmp = [cv_(O_C + 1024 * i, [512], BF16) for i in range(4)]
        ftmp = [cv_(O_C + 4096 + 2048 * i, [512], F32) for i in range(2)]
        b_gtmp = [Buf() for _ in range(4)]; b_ftmp = [Buf(), Buf()]
        handoff(b_gtmp + b_ftmp, [b_pw, b_bb, b_ccm])
        rhs_y = lambda k, t0, n: yT[:, k, t0:t0 + n]
        yall_b = [yT_b] * 5
        ctr = {"i": 0}

        class AllOf:
            pass
        for blk in range(2):
            def ev_glu(oc, tbi, pap, pb, blk=blk):
                t0, n = TBS[tbi]
                g = 4 * blk + oc
                ctr["i"] += 1
                gi = ctr["i"] % 4
                P.op(ACT, lambda e: e.activation(out=gtmp[gi][:, 0:n], in_=pap, func=AF.Sigmoid, bias=bglu[:, g:g + 1], scale=1.0),
                     reads=[pb, b_bglu], writes=[b_gtmp[gi]])
                P.op(DVE, lambda e: e.tensor_tensor(out=s2T[:, g, t0:t0 + n], in0=yT[:, g, t0:t0 + n], in1=gtmp[gi][:, 0:n], op=ALU.mult),
                     reads=[b_gtmp[gi], yT_b[g]], wadd=[s2_b[g]])
            proj_fm(w_glu[:, 512 * blk:512 * blk + 512], 512, rhs_y, [BufGroup(yT_b)] * 5, ev_glu)

            def ev_zs(oc, tbi, pap, pb, blk=blk):
                t0, n = TBS[tbi]
                g = 4 * blk + oc
                ctr["i"] += 1
                gi = ctr["i"] % 4
                fi = ctr["i"] % 2
                P.op(ACT, lambda e: e.activation(out=gtmp[gi][:, 0:n], in_=pap, func=AF.Sigmoid), reads=[pb], writes=[b_gtmp[gi]])
                P.op(DVE, lambda e: e.tensor_tensor(out=ftmp[fi][:, 0:n], in0=pap, in1=gtmp[gi][:, 0:n], op=ALU.mult),
                     reads=[pb, b_gtmp[gi]], writes=[b_ftmp[fi]])
                P.op(POOL, lambda e: e.tensor_tensor(out=s2T[:, g, t0:t0 + n], in0=s2T[:, g, t0:t0 + n], in1=ftmp[fi][:, 0:n], op=ALU.mult),
                     reads=[b_ftmp[fi], s2_b[g]], wadd=[s2_b[g]])
            proj_fm(w_in[:, OFF_ZS + 512 * blk:OFF_ZS + 512 * blk + 512], 512, rhs_h, hT_b, ev_zs)

        P.phase(11)
        gbs = RA
        gbs_b = [Buf(f"gbs{g}") for g in range(8)]
        handoff(gbs_b, yT_b)
        rhs_s2 = lambda k, t0, n: s2T[:, k, t0:t0 + n]
        for blk in range(2):
            def ev_gs(oc, tbi, pap, pb, blk=blk):
                t0, n = TBS[tbi]
                g = 4 * blk + oc
                P.op(ACT, lambda e: e.activation(out=gbs[:, g, t0:t0 + n], in_=pap, func=AF.Sigmoid), reads=[pb], wadd=[gbs_b[g]])
            proj_fm(w_in[:, OFF_GS + 512 * blk:OFF_GS + 512 * blk + 512], 512, rhs_h, hT_b, ev_gs)

            def ev_bs(oc, tbi, pap, pb, blk=blk):
                t0, n = TBS[tbi]
                g = 4 * blk + oc
                P.op(DVE, lambda e: e.tensor_tensor(out=gbs[:, g, t0:t0 + n], in0=pap, in1=gbs[:, g, t0:t0 + n], op=ALU.mult),
                     reads=[pb, gbs_b[g]], wadd=[gbs_b[g]])
            proj_fm(w_bs[:, 512 * blk:512 * blk + 512], 512, rhs_s2, [BufGroup(s2_b)] * 5, ev_bs)

        P.phase(12)
        oT = RB
        oT_b = [Buf(f"oT{g}") for g in range(8)]
        handoff(oT_b, s2_b)
        oc_ = O_C
        qT = cv_(oc_ + 0, [2, NT], BF16); kT2 = cv_(oc_ + 8256, [128 + NT], BF16); Vaug = cv_(oc_ + 12640, [18, 128], BF16)
        EB = cv_(oc_ + 17248, [2, 16, 128], BF16); EB0 = cv_(oc_ + 25440, [16, 128], BF16)
        Et = [cv_(oc_ + 29536 + 1024 * i, [512], BF16) for i in range(4)]
        PT = [cv_(oc_ + 33632 + 1024 * i, [512], BF16) for i in range(4)]
        rc = [cv_(oc_ + 37728 + 2048 * i, [512], F32) for i in range(2)]
        maskt = cv_(oc_ + 41824, [2, 128], F32); RT = cv_(oc_ + 42848, [384], F32); relb = cv_(oc_ + 44384, [16], F32)
        es16 = cv_(oc_ + 44448, [16], F32); klast = cv_(oc_ + 44512, [256], F32); vlast = cv_(oc_ + 45536, [256], F32)
        knew = cv_(oc_ + 46560, [256], F32); vnew = cv_(oc_ + 47584, [256], F32)
        Kc = cv_(oc_ + 48608, [16, 256], F32)
        KcT = cv_(oc_ + 64992, [16, 2, 128], BF16)
        Vcs = cv_(oc_ + 73184, [16, 256], BF16)
        QsT = cv_(oc_ + 81376, [2, 4, 16], BF16)
        dgt = cv_(oc_ + 81632, [64], F32); vnb = cv_(oc_ + 81888, [256], BF16); pdg = cv_(oc_ + 82400, [64], BF16)
        esr = cv_(oc_ + 82528, [64], BF16); ebs = cv_(oc_ + 82656, [16], F32); rcs = cv_(oc_ + 82720, [128], F32)
        ptS = cv_(oc_ + 83232, [128], BF16); ones_k = cv_(oc_ + 83488, [128], BF16)
        attn_bufs = {n: Buf(n) for n in ["qT", "kT2", "Vaug", "EB", "EB0", "mask", "RT", "relb", "es16", "klast", "vlast", "knew", "vnew",
                                         "Kc", "KcT", "Vcs", "QsT", "dgt", "vnb", "pdg", "esr", "ebs", "rcs", "ptS", "ones_k"]}
        A = attn_bufs
        b_Et = [Buf() for _ in range(4)]; b_PT = [Buf() for _ in range(4)]; b_rc = [Buf(), Buf()]
        prev_c = [b_pw, b_bb, b_ccm, b_R8, b_R128, b_A2k, b_Hend, b_carry, b_Sb, b_coef, b_misc, b_t, b_KL0, b_Ca1,
                  b_stile, b_Hsp, b_Hn, b_HbS, b_Qs, b_stg, b_sout] + b_gtmp + b_ftmp + all_ssm_tmp + b_CaBD + b_KL
        handoff(list(A.values()) + b_Et + b_PT + b_rc, prev_c)
        misc_load(SP, RT[0:32, :], rtab, A["RT"]); misc_load(SP, relb[0:32, :], rel_bias, A["relb"])
        misc_load(SP, maskt.rearrange("p h q -> p (h q)"), maskc, A["mask"])
        misc_load(SP, es16[0:1, :], sinks.rearrange("(o n) -> o n", o=1), A["es16"])
        misc_load(SP, ebs[0:16, :], rel_bias[0:1, :].to_broadcast([16, 16]), A["ebs"])
        misc_load(SP, dgt[0:16, :], diagc, A["dgt"])
        P.op(ACT, lambda e: e.activation(out=es16[0:1, :], in_=es16[0:1, :], func=AF.Exp), reads=[A["es16"]], writes=[A["es16"]])
        P.op(ACT, lambda e: e.activation(out=ebs[0:16, :], in_=ebs[0:16, :], func=AF.Exp), reads=[A["ebs"]], writes=[A["ebs"]])
        for kv in range(4):
            for sl_, i in enumerate([0, 2, 1, 3]):
                h = 4 * kv + i
                P.op(DVE, lambda e, kv=kv, sl_=sl_, h=h: e.tensor_copy(out=ES[0:1, kv, sl_, :], in_=es16[0:1, h:h + 1].to_broadcast([1, 128])),
                     reads=[A["es16"]], wadd=[b_ES])
        P.op(POOL, lambda e: e.memset(ones_k, 1.0), writes=[A["ones_k"]])
        for half in range(2):
            for qb in range(4):
                bke = bank()
                for qq in range(32):
                    q = 32 * qb + qq
                    st_ = (127 - q) if half == 0 else (255 - q)
                    P.op(PE, lambda e, bke=bke, qq=qq, st_=st_: e.matmul(ps[bke][:, 16 * qq:16 * qq + 16], lhsT=RT[0:32, st_:st_ + 128],
                                                                       rhs=relb[0:32, :], start=True, stop=True),
                         reads=[A["RT"], A["relb"]] if qq == 0 else [], writes=[psb[bke]] if qq == 0 else [], sig=(qq == 31))
                if not P.dead:
                    P.attach(Dep(P.esem[PE], P.cnt[PE]), reads=[A["RT"], A["relb"]], writes=[psb[bke]])
                P.op(ACT, lambda e, bke=bke, half=half, qb=qb: e.activation(
                    out=EB[:, half, :, 32 * qb:32 * qb + 32], in_=ps[bke][:, 0:512].rearrange("p (q h) -> p h q", h=16), func=AF.Exp),
                    reads=[psb[bke]], wadd=[A["EB"]])
        P.op(DVE, lambda e: e.tensor_tensor(out=EB, in0=EB, in1=maskt.unsqueeze(2).to_broadcast([128, 2, 16, 128]), op=ALU.mult),
             reads=[A["EB"], A["mask"]], writes=[A["EB"]])
        P.op(DVE, lambda e: e.tensor_scalar(out=EB0, in0=EB[:, 0], scalar1=flags[:, 0:1], scalar2=None, op0=ALU.mult),
             reads=[A["EB"], b_flags], writes=[A["EB0"]])
        P.dma(SP, dout_sem, sck[:, 0:127, :], ck[:, 1:128, :])
        P.dma(SP, dout_sem, scv[:, 0:127, :], cv[:, 1:128, :])
        kc_sem = P.dsem("kc")
        P.dma(SP, kc_sem, Kc, ck.rearrange("s t f -> t s f"), writes=[A["Kc"]])
        for s_ in range(NS):
            bkt = bank()
            for kvp in range(2):
                P.op(PE, lambda e, s_=s_, kvp=kvp, bkt=bkt: e.transpose(out=ps[bkt][:, 128 * kvp:128 * kvp + 128],
                                                                     in_=Kc[:, s_, 128 * kvp:128 * kvp + 128], identity=ident),
                     reads=[A["Kc"], b_ident], writes=[psb[bkt]] if kvp == 0 else [], wadd=[psb[bkt]] if kvp == 1 else [])
            evac_copy(s_, KcT[:, s_].rearrange("p a t -> p (a t)"), ps[bkt][:, 0:256], [psb[bkt]], [], wadd=[A["KcT"]])
        P.dma(SP, kc_sem, Kc, cv.rearrange("s t f -> t s f"), reads=[A["KcT"]], writes=[A["Kc"]])
        P.op(POOL, lambda e: e.tensor_copy(out=Vcs, in_=Kc), reads=[A["Kc"]], writes=[A["Vcs"]])
        P.op(POOL, lambda e: e.memset(Vaug[:, :, 64:128], 1.0), writes=[A["Vaug"]])

        rhs_hh = lambda k, t0, n: hTh[:, k, 0:n]
        TB5 = TBS
        def attn_kv(kv):
            def ev_q(oc, tbi, pap, pb):
                t0, n = TBS[tbi]
                ctr["i"] += 1
                evac_copy(ctr["i"], qT[:, oc, t0:t0 + n], pap, [pb], [], wadd=[A["qT"]])
            A["qT"].r = list(A["qT"].r) + list(A["qT"].w); A["qT"].w = []
            proj_fm(w_in[:, OFF_Q + 256 * kv:OFF_Q + 256 * kv + 256], 256, rhs_h, hT_b, ev_q)
            s = wctr[0] % 2
            wctr[0] += 1
            for dup in range(2):
                P.dma(POOL, wsem[s], wslot[s][:, :, 64 * dup:64 * dup + 64],
                      w_in[:, OFF_K + 64 * kv:OFF_K + 64 * kv + 64].rearrange("(k p) f -> p k f", p=128),
                      writes=[wslot_b[s]] if dup == 0 else [], wadd=[wslot_b[s]] if dup == 1 else [])
            A["kT2"].r = list(A["kT2"].r) + list(A["kT2"].w); A["kT2"].w = []
            kblocks = [(hTh, hTh_b, 0, 128, 0)] + [(hT, hT_b[i], t0, n, 128 + t0) for i, (t0, n) in enumerate(TBS)]
            for bi_, (src, sb_, t0, n, c0) in enumerate(kblocks):
                bkk = bank()
                for k in range(8):
                    P.op(PE, lambda e, bkk=bkk, k=k, src=src, t0=t0, n=n, s=s: e.matmul(
                        ps[bkk][:, 0:n], lhsT=wslot[s][:, k, 0:128], rhs=src[:, k, t0:t0 + n], start=(k == 0), stop=(k == 7)),
                        reads=[wslot_b[s], sb_] if k == 0 else [], writes=[psb[bkk]] if k == 0 else [], sig=(k == 7))
                if not P.dead:
                    P.attach(Dep(P.esem[PE], P.cnt[PE]), reads=[wslot_b[s], sb_], writes=[psb[bkk]])
                evac_copy(bi_, kT2[:, c0:c0 + n], ps[bkk][:, 0:n], [psb[bkk]], [], wadd=[A["kT2"]])
            bkl_ = bank()
            for j_, (c0_, m_) in enumerate([(NP - 128, 128), (NP, NS)]):
                for k in range(8):
                    P.op(PE, lambda e, j_=j_, c0_=c0_, m_=m_, k=k, s=s: e.matmul(
                        ps[bkl_][0:m_, 64 * j_:64 * j_ + 64], lhsT=hT[:, k, c0_:c0_ + m_], rhs=wslot[s][:, k, 0:64],
                        start=(k == 0), stop=(k == 7)),
                        reads=[wslot_b[s], hT_b[3], hT_b[4]] if (k == 0 and j_ == 0) else [],
                        writes=[psb[bkl_]] if (k == 0 and j_ == 0) else [], sig=(k == 7 and j_ == 1))
            if not P.dead:
                P.attach(Dep(P.esem[PE], P.cnt[PE]), reads=[wslot_b[s], hT_b[3], hT_b[4]], writes=[psb[bkl_]])
            P.op(DVE, lambda e, kv=kv: e.tensor_copy(out=klast[:, 64 * kv:64 * kv + 64], in_=ps[bkl_][:, 0:64]), reads=[psb[bkl_]], wadd=[A["klast"]])
            P.op(DVE, lambda e, kv=kv: e.tensor_copy(out=knew[0:NS, 64 * kv:64 * kv + 64], in_=ps[bkl_][0:NS, 64:128]), reads=[psb[bkl_]], wadd=[A["knew"]])
            s = wctr[0] % 2
            wctr[0] += 1
            P.dma(POOL, wsem[s], wslot[s][:, :, 0:64], w_in[:, OFF_V + 64 * kv:OFF_V + 64 * kv + 64].rearrange("(k p) f -> p k f", p=128),
                  writes=[wslot_b[s]])
            A["Vaug"].r = list(A["Vaug"].r) + list(A["Vaug"].w); A["Vaug"].w = []
            vtiles = [(hTh, hTh_b, 0, 128)] + [(hT, hT_b[i // 4], 128 * i, 128) for i in range(16)] + [(hT, hT_b[4], NP, NS)]
            for grp in range(3):
                bkv = bank()
                tl_ = vtiles[8 * grp:8 * grp + 8]
                for j_, (src, sb_, c0_, m_) in enumerate(tl_):
                    for k in range(8):
                        firstg = (j_ == 0 and k == 0)
                        P.op(PE, lambda e, bkv=bkv, j_=j_, src=src, c0_=c0_, m_=m_, k=k, s=s: e.matmul(
                            ps[bkv][0:m_, 64 * j_:64 * j_ + 64], lhsT=src[:, k, c0_:c0_ + m_], rhs=wslot[s][:, k, 0:64],
                            start=(k == 0), stop=(k == 7)),
                            reads=[wslot_b[s], sb_, hTh_b] + hT_b if firstg else [], writes=[psb[bkv]] if firstg else [],
                            sig=(j_ == len(tl_) - 1 and k == 7))
                if not P.dead:
                    P.attach(Dep(P.esem[PE], P.cnt[PE]), reads=[wslot_b[s], hTh_b] + hT_b, writes=[psb[bkv]])
                nt_ = len(tl_)
                if grp < 2:
                    P.op(ACT, lambda e, bkv=bkv, grp=grp: e.activation(out=Vaug[:, 8 * grp:8 * grp + 8, 0:64],
                                                                     in_=ps[bkv][:, 0:512].rearrange("p (t d) -> p t d", d=64), func=AF.Copy),
                         reads=[psb[bkv]], wadd=[A["Vaug"]])
                    if grp == 1:
                        pass
                else:
                    P.op(ACT, lambda e, bkv=bkv: e.activation(out=Vaug[:, 16, 0:64], in_=ps[bkv][:, 0:64], func=AF.Copy),
                         reads=[psb[bkv]], wadd=[A["Vaug"]])
                    P.op(ACT, lambda e, bkv=bkv: e.activation(out=Vaug[0:NS, 17, 0:64], in_=ps[bkv][0:NS, 64:128], func=AF.Copy),
                         reads=[psb[bkv]], wadd=[A["Vaug"]])
                    P.op(ACT, lambda e, bkv=bkv, kv=kv: e.activation(out=vlast[:, 64 * kv:64 * kv + 64], in_=ps[bkv][:, 0:64], func=AF.Copy),
                         reads=[psb[bkv]], wadd=[A["vlast"]])
                    P.op(ACT, lambda e, bkv=bkv, kv=kv: e.activation(out=vnew[0:NS, 64 * kv:64 * kv + 64], in_=ps[bkv][0:NS, 64:128], func=AF.Copy),
                         reads=[psb[bkv]], wadd=[A["vnew"]])
            qv = lambda base, b_: qT[base:base + 64, 0:2, 128 * b_:128 * b_ + 128]
            def attn_s1(b_):
                ia = (2 * b_) % 4; ib = (2 * b_ + 1) % 4
                bA, bB = bank(), bank()
                kprev = slice(128 * b_, 128 * b_ + 128); kcur = slice(128 * b_ + 128, 128 * b_ + 256)
                seq = [(bA, 0, 0, kprev), (bB, 64, 0, kprev), (bA, 0, 1, kcur), (bB, 64, 1, kcur)]
                for (bk_, base, half, ks) in seq:
                    first = (half == 0)
                    P.op(PE, lambda e, bk_=bk_, base=base, half=half, ks=ks, b_=b_: e.matmul(
                        ps[bk_][:, 256 * half:256 * half + 256], lhsT=kT2[base:base + 64, ks], rhs=qv(base, b_), start=True, stop=True),
                        reads=[A["kT2"], A["qT"]] if first else [], writes=[psb[bk_]] if first else [], sig=(half == 1))
                    if half == 1 and not P.dead:
                        P.attach(Dep(P.esem[PE], P.cnt[PE]), reads=[A["kT2"], A["qT"]], writes=[psb[bk_]])
                for (bk_, ie, base_h) in [(bA, ia, 0), (bB, ib, 1)]:
                    P.op(ACT, lambda e, bk_=bk_, ie=ie: e.activation(out=Et[ie], in_=ps[bk_][:, :], func=AF.Exp, scale=0.125),
                         reads=[psb[bk_]], writes=[b_Et[ie]])
                    Ev = Et[ie].rearrange("p (h i q) -> p h i q", h=2, i=2)
                    Pv = PT[ie].rearrange("p (h i q) -> p h i q", h=2, i=2)
                    h0 = 4 * kv + base_h
                    eng = DVE if base_h == 0 else POOL
                    if b_ > 0:
                        P.op(eng, lambda e, Ev=Ev, Pv=Pv, h0=h0: e.tensor_tensor(out=Pv, in0=Ev, in1=EB[:, :, h0:h0 + 3:2, :], op=ALU.mult),
                             reads=[b_Et[ie], A["EB"]], writes=[b_PT[ie]])
                    else:
                        P.op(eng, lambda e, Ev=Ev, Pv=Pv, h0=h0: e.tensor_tensor(out=Pv[:, 0], in0=Ev[:, 0], in1=EB0[:, h0:h0 + 3:2, :], op=ALU.mult),
                             reads=[b_Et[ie], A["EB0"]], writes=[b_PT[ie]])
                        P.op(eng, lambda e, Ev=Ev, Pv=Pv, h0=h0: e.tensor_tensor(out=Pv[:, 1], in0=Ev[:, 1], in1=EB[:, 1, h0:h0 + 3:2, :], op=ALU.mult),
                             reads=[b_Et[ie], A["EB"]], wadd=[b_PT[ie]])

            def attn_s2(b_):
                ia = (2 * b_) % 4; ib = (2 * b_ + 1) % 4
                bO = bank()
                mm = [(ia, 0, b_, True), (ia, 1, b_ + 1, False), (ib, 0, b_, False), (ib, 1, b_ + 1, False)]
                for j_, (ip, half, tile, st_) in enumerate(mm):
                    cols = slice(0, 256) if ip == ia else slice(256, 512)
                    P.op(PE, lambda e, ip=ip, half=half, tile=tile, st_=st_, cols=cols: e.matmul(
                        ps[bO][:, cols], lhsT=Vaug[:, tile, :], rhs=PT[ip][:, 256 * half:256 * half + 256], start=st_, stop=False),
                        reads=[A["Vaug"], b_PT[ia], b_PT[ib], b_ES, b_ones] if j_ == 0 else [], writes=[psb[bO]] if j_ == 0 else [], sig=False)
                P.op(PE, lambda e, kv=kv: e.matmul(ps[bO][:, 0:512], lhsT=onesd[0:1, :], rhs=ES[0:1, kv].rearrange("p i q -> p (i q)"),
                                                   start=False, stop=True), sig=True)
                if not P.dead:
                    P.attach(Dep(P.esem[PE], P.cnt[PE]), reads=[A["Vaug"], b_PT[ia], b_PT[ib], b_ES, b_ones], writes=[psb[bO]])
                ir = b_ % 2
                P.op(ACT, lambda e, ir=ir: e.activation(out=rc[ir][64:128, :], in_=ps[bO][64:128, :], func=AF.Ln), reads=[psb[bO]], writes=[b_rc[ir]])
                P.op(ACT, lambda e, ir=ir: e.activation(out=rc[ir][64:128, :], in_=rc[ir][64:128, :], func=AF.Exp, scale=-1.0),
                     reads=[b_rc[ir]], writes=[b_rc[ir]])
                for par in range(2):
                    P.op(DVE, lambda e, par=par, ir=ir, b_=b_, kv=kv: e.tensor_tensor(
                        out=oT[64 * par:64 * par + 64, 2 * kv:2 * kv + 2, 128 * b_:128 * b_ + 128],
                        in0=ps[bO][0:64, 256 * par:256 * par + 256].rearrange("p (c q) -> p c q", c=2),
                        in1=rc[ir][64:128, 256 * par:256 * par + 256].rearrange("p (c q) -> p c q", c=2), op=ALU.mult),
                        reads=[psb[bO], b_rc[ir]], wadd=[oT_b[2 * kv], oT_b[2 * kv + 1]])

            attn_s1(0)
            for b_ in range(1, 16):
                attn_s1(b_)
                attn_s2(b_ - 1)
            attn_s2(15)
            base = 64 * (kv % 2)
            for i in range(4):
                hsrc = 64 * (i % 2)
                P.op(POOL, lambda e, i=i, hsrc=hsrc, base=base: e.tensor_copy(out=QsT[base:base + 64, 0, i, :], in_=qT[hsrc:hsrc + 64, i // 2, NP:NT]),
                     reads=[A["qT"]], writes=[A["QsT"]] if i == 0 else [], wadd=[A["QsT"]] if i > 0 else [])
            for sl_, i in enumerate([0, 1, 2, 3]):
                P.op(DVE, lambda e, i=i, kv=kv: e.tensor_copy(out=esr[0:1, :].rearrange("p (s i) -> p s i", i=4)[:, :, i],
                                                            in_=es16[0:1, 4 * kv + i:4 * kv + i + 1].to_broadcast([1, 16])),
                     reads=[A["es16"]], writes=[A["esr"]] if i == 0 else [], wadd=[A["esr"]] if i > 0 else [])
            bS, bD, bN = bank(), bank(), bank()
            Qsi = QsT[base:base + 64, 0].rearrange("p i s -> p s i")
            for s_ in range(NS):
                P.op(PE, lambda e, s_=s_, base=base, kv=kv: e.matmul(ps[bS][:, 4 * s_:4 * s_ + 4], lhsT=KcT[base:base + 64, s_, kv // 2, :],
                                                                  rhs=QsT[base:base + 64, 0, :, s_], start=True, stop=True),
                     reads=[A["KcT"], A["QsT"], A["kT2"]] if s_ == 0 else [], writes=[psb[bS]] if s_ == 0 else [], sig=False)
            P.op(PE, lambda e, base=base: e.matmul(ps[bS][0:NS, 64:128], lhsT=kT2[base:base + 64, 128 + NP:128 + NT], rhs=Qsi, start=True, stop=True), sig=True)
            if not P.dead:
                P.attach(Dep(P.esem[PE], P.cnt[PE]), reads=[A["KcT"], A["QsT"], A["kT2"]], writes=[psb[bS]])
            P.op(ACT, lambda e: e.activation(out=ptS[:, 0:64], in_=ps[bS][:, 0:64], func=AF.Exp, scale=0.125), reads=[psb[bS]], writes=[A["ptS"]])
            P.op(ACT, lambda e: e.activation(out=pdg[0:NS, :], in_=ps[bS][0:NS, 64:128], func=AF.Exp, scale=0.125), reads=[psb[bS]], writes=[A["pdg"]])
            P.op(DVE, lambda e, kv=kv: e.tensor_tensor(out=ptS[:, 0:64].rearrange("p (s i) -> p s i", i=4), in0=ptS[:, 0:64].rearrange("p (s i) -> p s i", i=4),
                                                in1=EB[:, 0, 4 * kv:4 * kv + 4, 0].unsqueeze(1).to_broadcast([128, 16, 4]), op=ALU.mult),
                 reads=[A["ptS"], A["EB"]], writes=[A["ptS"]])
            P.op(DVE, lambda e, kv=kv: e.tensor_tensor(out=pdg[0:NS, :].rearrange("p (s i) -> p s i", i=4), in0=pdg[0:NS, :].rearrange("p (s i) -> p s i", i=4),
                                                in1=ebs[0:NS, 4 * kv:4 * kv + 4].unsqueeze(1).to_broadcast([NS, 16, 4]), op=ALU.mult),
                 reads=[A["pdg"], A["ebs"]], writes=[A["pdg"]])
            P.op(DVE, lambda e: e.tensor_tensor(out=pdg[0:NS, :], in0=pdg[0:NS, :], in1=dgt[0:NS, :], op=ALU.mult),
                 reads=[A["pdg"], A["dgt"]], writes=[A["pdg"]])
            P.op(POOL, lambda e, kv=kv: e.tensor_copy(out=vnb[0:NS, 64 * kv:64 * kv + 64], in_=vnew[0:NS, 64 * kv:64 * kv + 64]),
                 reads=[A["vnew"]], writes=[A["vnb"]])
            P.op(PE, lambda e: e.matmul(ps[bD][:, 0:64], lhsT=ones_k, rhs=ptS[:, 0:64], start=True, stop=False),
                 reads=[A["ones_k"], A["ptS"], A["pdg"], A["esr"]], writes=[psb[bD]], sig=False)
            P.op(PE, lambda e: e.matmul(ps[bD][:, 0:64], lhsT=ones_k[0:NS, :], rhs=pdg[0:NS, :], start=False, stop=False), sig=False)
            P.op(PE, lambda e: e.matmul(ps[bD][:, 0:64], lhsT=ones_k[0:1, :], rhs=esr[0:1, :], start=False, stop=True), sig=True)
            if not P.dead:
                P.attach(Dep(P.esem[PE], P.cnt[PE]), reads=[A["ones_k"], A["ptS"], A["pdg"], A["esr"]], writes=[psb[bD]])
            ptv = ptS[:, 0:64].rearrange("p (s i) -> p s i", i=4)
            pdv = pdg[0:NS, :].rearrange("p (s i) -> p s i", i=4)
            psn = ps[bN][:, 0:64].rearrange("p (s i) -> p s i", i=4)
            for par in range(2):
                for s_ in range(NS):
                    P.op(PE, lambda e, par=par, s_=s_, kv=kv: e.matmul(
                        psn[64 * par:64 * par + 64, s_, par:4:2], lhsT=Vcs[:, s_, 64 * kv:64 * kv + 64], rhs=ptv[:, s_, par:4:2],
                        start=(s_ == 0), stop=False, tile_position=(0, 64 * par)),
                        reads=[A["Vcs"], A["ptS"], A["pdg"], A["vnb"]] if (par == 0 and s_ == 0) else [],
                        writes=[psb[bN]] if (par == 0 and s_ == 0) else [], sig=False)
                P.op(PE, lambda e, par=par, kv=kv: e.matmul(
                    psn[64 * par:64 * par + 64, :, par:4:2], lhsT=vnb[0:NS, 64 * kv:64 * kv + 64], rhs=pdv[:, :, par:4:2],
                    start=False, stop=True, tile_position=(0, 64 * par)), sig=(par == 1))
            if not P.dead:
                P.attach(Dep(P.esem[PE], P.cnt[PE]), reads=[A["Vcs"], A["ptS"], A["pdg"], A["vnb"]], writes=[psb[bN]])
            P.op(DVE, lambda e: e.reciprocal(out=rcs[:, 0:64], in_=ps[bD][:, 0:64]), reads=[psb[bD]], writes=[A["rcs"]])
            rcv = rcs[:, 0:64].rearrange("p (s i) -> p s i", i=4)
            for par in range(2):
                P.op(DVE, lambda e, par=par, kv=kv: e.tensor_tensor(
                    out=oT[64 * par:64 * par + 64, 2 * kv:2 * kv + 2, NP:NT],
                    in0=psn[64 * par:64 * par + 64, :, par:4:2].rearrange("p s c -> p c s"),
                    in1=rcv[64 * par:64 * par + 64, :, par:4:2].rearrange("p s c -> p c s"), op=ALU.mult),
                    reads=[psb[bN], A["rcs"]], wadd=[oT_b[2 * kv], oT_b[2 * kv + 1]])
        for kv in range(4):
            attn_kv(kv)
        P.dma(SP, osem_new("pck"), pck, klast, reads=[A["klast"]])
        P.dma(SP, osem_new("pcv"), pcv, vlast, reads=[A["vlast"]])
        P.dma(SP, osem_new("sck"), sck[:, 127, :], knew[0:NS, :], reads=[A["knew"]])
        P.dma(SP, osem_new("scv"), scv[:, 127, :], vnew[0:NS, :], reads=[A["vnew"]])

        P.phase(13)
        sgaT = cv_(O_C + 0, [8, NT], BF16)
        sga_b = [Buf(f"sga{g}") for g in range(8)]
        gt2 = [cv_(O_C + 33024 + 1024 * i, [512], BF16) for i in range(4)]
        ft2 = [cv_(O_C + 37120 + 2048 * i, [512], F32) for i in range(2)]
        b_gt2 = [Buf() for _ in range(4)]; b_ft2 = [Buf(), Buf()]
        handoff(sga_b + b_gt2 + b_ft2, list(A.values()) + b_Et + b_PT + b_rc)
        for blk in range(2):
            def ev_za(oc, tbi, pap, pb, blk=blk):
                t0, n = TBS[tbi]
                g = 4 * blk + oc
                ctr["i"] += 1
                gi = ctr["i"] % 4; fi = ctr["i"] % 2
                P.op(ACT, lambda e: e.activation(out=gt2[gi][:, 0:n], in_=pap, func=AF.Sigmoid), reads=[pb], writes=[b_gt2[gi]])
                P.op(DVE, lambda e: e.tensor_tensor(out=ft2[fi][:, 0:n], in0=pap, in1=gt2[gi][:, 0:n], op=ALU.mult),
                     reads=[pb, b_gt2[gi]], writes=[b_ft2[fi]])
                P.op(POOL, lambda e: e.tensor_tensor(out=oT[:, g, t0:t0 + n], in0=oT[:, g, t0:t0 + n], in1=ft2[fi][:, 0:n], op=ALU.mult),
                     reads=[b_ft2[fi], oT_b[g]], wadd=[oT_b[g]])
            proj_fm(w_in[:, OFF_ZA + 512 * blk:OFF_ZA + 512 * blk + 512], 512, rhs_h, hT_b, ev_za)
        for blk in range(2):
            def ev_ga(oc, tbi, pap, pb, blk=blk):
                t0, n = TBS[tbi]
                g = 4 * blk + oc
                P.op(ACT, lambda e: e.activation(out=sgaT[:, g, t0:t0 + n], in_=pap, func=AF.Sigmoid), reads=[pb], wadd=[sga_b[g]])
            proj_fm(w_in[:, OFF_GA + 512 * blk:OFF_GA + 512 * blk + 512], 512, rhs_h, hT_b, ev_ga)
        mT = RA
        mT_b = gbs_b
        rhs_o = lambda k, t0, n: oT[:, k, t0:t0 + n]
        for blk in range(2):
            def ev_ba(oc, tbi, pap, pb, blk=blk):
                t0, n = TBS[tbi]
                g = 4 * blk + oc
                ctr["i"] += 1
                fi = ctr["i"] % 2
                P.op(DVE, lambda e: e.tensor_tensor(out=ft2[fi][:, 0:n], in0=pap, in1=sgaT[:, g, t0:t0 + n], op=ALU.mult),
                     reads=[pb, sga_b[g]], writes=[b_ft2[fi]])
                P.op(POOL, lambda e: e.tensor_tensor(out=mT[:, g, t0:t0 + n], in0=mT[:, g, t0:t0 + n], in1=ft2[fi][:, 0:n], op=ALU.add),
                     reads=[b_ft2[fi], mT_b[g]], wadd=[mT_b[g]])
            proj_fm(w_ba[:, 512 * blk:512 * blk + 512], 512, rhs_o, [BufGroup(oT_b)] * 5, ev_ba)

        P.phase(14)
        o2 = O_C + 8000
        NSL = 4
        GateB = cv_(o2 + 0, [1024], F32); LnG = cv_(o2 + 4096, [1024], F32); LnB = cv_(o2 + 8192, [1024], F32)
        gateS = cv_(o2 + 12288, [1024], F32); grow = cv_(o2 + 16384, [1024], F32)
        xt = [cv_(o2 + 20480 + 4096 * i, [1024], F32) for i in range(NSL)]
        rt = [cv_(o2 + 36864 + 4096 * i, [1024], F32) for i in range(NSL)]
        stt = cv_(o2 + 53248, [NSL, 2, 6], F32); mvt = cv_(o2 + 53504, [NSL, 2], F32); rsd = cv_(o2 + 53568, [NSL, 2], F32)
        mhalf = cv_(o2 + 53632, [1], F32)
        b_GateB = Buf(); b_LnG = Buf(); b_LnB = Buf(); b_gateS = Buf(); b_grow = Buf(); b_xt = [Buf() for _ in range(NSL)]; b_rt = [Buf() for _ in range(NSL)]
        b_stt = [Buf() for _ in range(NSL)]; b_mh = Buf()
        xsem2 = [P.dsem(f"xt{i}") for i in range(NSL)]; osem = [P.dsem(f"o{i}") for i in range(NSL)]
        handoff([b_GateB, b_LnG, b_LnB, b_gateS, b_grow] + b_xt + b_rt + b_stt + [b_mh], list(A.values()) + b_Et + b_PT + b_rc + sga_b + b_gt2 + b_ft2)
        misc_load(SP, LnG, ln_g.rearrange("(o n) -> o n", o=1).to_broadcast([128, 1024]), b_LnG)
        misc_load(SP, LnB, ln_b.rearrange("(o n) -> o n", o=1).to_broadcast([128, 1024]), b_LnB)
        P.op(POOL, lambda e: e.memset(mhalf, -0.5), writes=[b_mh])
        for hb in range(2):
            bkg = bank(); bkg2 = bank()
            for kk in range(4):
                k = 4 * hb + kk
                P.op(PE, lambda e, k=k, kk=kk, bkg=bkg: e.transpose(out=ps[bkg][0:1, 128 * kk:128 * kk + 128], in_=modT[:, 16 + k, 0:1], identity=ident),
                     reads=[b_modT, b_ident], writes=[psb[bkg]] if kk == 0 else [], wadd=[psb[bkg]] if kk > 0 else [])
                P.op(PE, lambda e, k=k, kk=kk, bkg2=bkg2: e.transpose(out=ps[bkg2][0:NS, 128 * kk:128 * kk + 128], in_=modT[:, 16 + k, 1:17], identity=ident),
                     reads=[b_modT, b_ident], writes=[psb[bkg2]] if kk == 0 else [], wadd=[psb[bkg2]] if kk > 0 else [])
            P.op(DVE, lambda e, hb=hb, bkg=bkg: e.tensor_copy(out=grow[0:1, 512 * hb:512 * hb + 512], in_=ps[bkg][0:1, 0:512]), reads=[psb[bkg]], wadd=[b_grow])
            P.op(DVE, lambda e, hb=hb, bkg2=bkg2: e.tensor_copy(out=gateS[0:NS, 512 * hb:512 * hb + 512], in_=ps[bkg2][0:NS, 0:512]), reads=[psb[bkg2]], wadd=[b_gateS])
        for hb in range(2):
            bkb = bank()
            P.op(PE, lambda e, hb=hb, bkb=bkb: e.matmul(ps[bkb][:, 0:512], lhsT=ones1[0:1, :], rhs=grow[0:1, 512 * hb:512 * hb + 512], start=True, stop=True),
                 reads=[b_grow, b_ones], writes=[psb[bkb]])
            P.op(DVE, lambda e, hb=hb, bkb=bkb: e.tensor_copy(out=GateB[:, 512 * hb:512 * hb + 512], in_=ps[bkb][:, 0:512]), reads=[psb[bkb]], wadd=[b_GateB])
        so = [load_w(w_out[:, 0:512], 512), load_w(w_out[:, 512:1024], 512)]
        def x_load(ti_):
            rows_, c0_ = (128, 128 * ti_) if ti_ < 16 else (NS, NP)
            src_ = xp[c0_:c0_ + 128, :] if ti_ < 16 else xs
            P.dma(SP, xsem2[ti_ % NSL], xt[ti_ % NSL][0:rows_, :], src_, writes=[b_xt[ti_ % NSL]])
        for ti_ in range(NSL):
            x_load(ti_)
        for tt_i in range(17):
            rows, c0 = (128, 128 * tt_i) if tt_i < 16 else (NS, NP)
            sl = tt_i % NSL
            gate_ap = GateB if tt_i < 16 else gateS
            gate_b = b_GateB if tt_i < 16 else b_gateS
            for fb in range(2):
                bko = bank()
                for k in range(8):
                    P.op(PE, lambda e, bko=bko, k=k, fb=fb, rows=rows, c0=c0: e.matmul(
                        ps[bko][0:rows, 0:512], lhsT=mT[:, k, c0:c0 + rows], rhs=wslot[so[fb]][:, k, 0:512], start=(k == 0), stop=(k == 7)),
                        reads=[wslot_b[so[fb]]] + mT_b if k == 0 else [], writes=[psb[bko]] if k == 0 else [], sig=(k == 7))
                if not P.dead:
                    P.attach(Dep(P.esem[PE], P.cnt[PE]), reads=[wslot_b[so[fb]]] + mT_b, writes=[psb[bko]])
                P.op(DVE, lambda e, bko=bko, fb=fb, rows=rows, sl=sl, gate_ap=gate_ap: e.tensor_tensor(
                    out=rt[sl][0:rows, 512 * fb:512 * fb + 512], in0=ps[bko][0:rows, 0:512], in1=gate_ap[0:rows, 512 * fb:512 * fb + 512], op=ALU.mult),
                    reads=[psb[bko], gate_b], writes=[b_rt[sl]] if fb == 0 else [], wadd=[b_rt[sl]] if fb == 1 else [])
            P.op(DVE, lambda e, rows=rows, sl=sl: e.scalar_tensor_tensor(out=rt[sl][0:rows, :], in0=xt[sl][0:rows, :], scalar=float(ALPHA),
                                                                         in1=rt[sl][0:rows, :], op0=ALU.mult, op1=ALU.add),
                 reads=[b_xt[sl], b_rt[sl]], writes=[b_rt[sl]])
            for hf in range(2):
                P.op(DVE, lambda e, rows=rows, sl=sl, hf=hf: e.bn_stats(out=stt[0:rows, sl, hf, :], in_=rt[sl][0:rows, 512 * hf:512 * hf + 512]),
                     reads=[b_rt[sl]], writes=[b_stt[sl]] if hf == 0 else [], wadd=[b_stt[sl]] if hf == 1 else [])
            P.op(DVE, lambda e, rows=rows, sl=sl: e.bn_aggr(out=mvt[0:rows, sl, :], in_=stt[0:rows, sl].rearrange("p a b -> p (a b)")),
                 reads=[b_stt[sl]], writes=[b_stt[sl]])
            P.op(POOL, lambda e, rows=rows, sl=sl: e.tensor_scalar(out=rsd[0:rows, sl, 0:1], in0=mvt[0:rows, sl, 1:2], scalar1=float(LN_EPS), scalar2=0.0,
                                                                   op0=ALU.add, op1=ALU.add), reads=[b_stt[sl]], writes=[b_stt[sl]])
            P.op(POOL, lambda e, rows=rows, sl=sl: e.tensor_tensor(out=rsd[0:rows, sl, 0:1], in0=rsd[0:rows, sl, 0:1], in1=mhalf[0:rows, :], op=ALU.pow),
                 reads=[b_stt[sl], b_mh], writes=[b_stt[sl]])
            P.op(POOL, lambda e, rows=rows, sl=sl: e.tensor_tensor(out=rsd[0:rows, sl, 1:2], in0=mvt[0:rows, sl, 0:1], in1=rsd[0:rows, sl, 0:1], op=ALU.mult),
                 reads=[b_stt[sl]], writes=[b_stt[sl]])
            P.op(POOL, lambda e, rows=rows, sl=sl: e.tensor_scalar(out=rsd[0:rows, sl, 1:2], in0=rsd[0:rows, sl, 1:2], scalar1=-1.0, scalar2=0.0,
                                                                   op0=ALU.mult, op1=ALU.add), reads=[b_stt[sl]], writes=[b_stt[sl]])
            P.op(ACT, lambda e, rows=rows, sl=sl: e.activation(out=xt[sl][0:rows, :], in_=rt[sl][0:rows, :], func=AF.Identity,
                                                               scale=rsd[0:rows, sl, 0:1], bias=rsd[0:rows, sl, 1:2]),
                 reads=[b_rt[sl], b_stt[sl]], writes=[b_xt[sl]])
            P.op(DVE, lambda e, rows=rows, sl=sl: e.tensor_tensor(out=xt[sl][0:rows, :], in0=xt[sl][0:rows, :], in1=LnG[0:rows, :], op=ALU.mult),
                 reads=[b_xt[sl], b_LnG], writes=[b_xt[sl]])
            P.op(POOL, lambda e, rows=rows, sl=sl: e.tensor_tensor(out=xt[sl][0:rows, :], in0=xt[sl][0:rows, :], in1=LnB[0:rows, :], op=ALU.add),
                 reads=[b_xt[sl], b_LnB], writes=[b_xt[sl]])
            dst = yp[c0:c0 + 128, :] if tt_i < 16 else ys
            P.dma(SP, osem[sl], dst, xt[sl][0:rows, :], reads=[b_xt[sl]])
            if tt_i + NSL < 17:
                x_load(tt_i + NSL)
        final_deps = [Dep(o_.h, o_.cnt) for o_ in osem]

        P.dead = False
        if DEBUG:
            pass
        P.wait(SP, [Dep(dout_sem.h, dout_sem.cnt)] + final_deps + [Dep(d_.h, d_.cnt) for d_ in out_sems])
        P.emit()
        print("instruction counts:", P.ninst)
    return nc


def _bucket_np(dist):
    max_exact = 16
    df = np.maximum(dist, 1).astype(np.float32)
    large = max_exact + (np.log(df / np.float32(max_exact)) / np.float32(math.log(128 / max_exact)) * np.float32(16)).astype(np.int32)
    large = np.minimum(large, 31)
    return np.where(dist < max_exact, dist, large)


def _host_consts():
    R = np.zeros((32, 384), np.float32)
    for i in range(384):
        dist = 255 - i
        if 0 <= dist <= 128:
            R[int(_bucket_np(np.array([dist]))[0]), i] = 1.0
    j = np.arange(128)[:, None]
    q = np.arange(128)[None, :]
    mask = np.concatenate([(j >= q), (j <= q)], axis=1).astype(np.float32)
    r = np.arange(128)
    bmask = (r[:, None] // 32 == r[None, :] // 32).astype(np.float32)
    diag = np.zeros((16, 16, 4), np.float32)
    for s_ in range(16):
        diag[s_, s_, :] = 1.0
    return R, mask, bmask, diag.reshape(16, 64)


_NC_CACHE = {}


def kernel(x_prompt, x_sample, c_prompt, c_sample, state_ssm_re, state_ssm_im, cache_swa_k, cache_swa_v,
           w_ada, b_ada, w_in, ssm_lambda_re, ssm_lambda_im, ssm_log_delta, ssm_b_re, ssm_b_im,
           ssm_c_re, ssm_c_im, ssm_d, w_glu, b_glu, attn_sinks, rel_bias, w_branch_s, w_branch_a,
           w_out, ln_g, ln_b):
    f = lambda a: np.ascontiguousarray(np.asarray(a, dtype=np.float32))
    x_prompt = f(x_prompt); x_sample = f(x_sample); c_prompt = f(c_prompt); c_sample = f(c_sample)
    R, mask, bmask, diag = _host_consts()
    shared = {
        "w_ada": f(w_ada)[0], "b_ada": f(b_ada)[0], "w_in": f(w_in)[0],
        "lam_re": f(ssm_lambda_re)[0], "lam_im": f(ssm_lambda_im)[0], "log_delta": f(ssm_log_delta)[0],
        "b_re": f(ssm_b_re)[0].reshape(4096, 16), "b_im": f(ssm_b_im)[0].reshape(4096, 16),
        "c_re": f(ssm_c_re)[0].reshape(1024, 64), "c_im": f(ssm_c_im)[0].reshape(1024, 64),
        "ssm_d": f(ssm_d)[0], "w_glu": f(w_glu)[0], "b_glu": f(b_glu)[0], "sinks": f(attn_sinks)[0],
        "rel_bias": f(rel_bias), "w_bs": f(w_branch_s)[0], "w_ba": f(w_branch_a)[0], "w_out": f(w_out)[0],
        "ln_g": f(ln_g)[0], "ln_b": f(ln_b)[0],
        "rtab": R, "maskc": mask, "bmaskc": bmask, "diagc": diag,
    }
    sre = f(state_ssm_re)[0].reshape(128, 4096); sim = f(state_ssm_im)[0].reshape(128, 4096)
    ckk = f(cache_swa_k)[0].reshape(128, 128, 256); cvv = f(cache_swa_v)[0].reshape(128, 128, 256)
    in_maps = []
    for c in range(NCORES):
        b, qr = c // 4, c % 4
        t0 = NP * qr
        xh = x_prompt[b, t0 - 128:t0] if qr > 0 else np.zeros((128, D), np.float32)
        flags = np.zeros(32, np.float32)
        flags[0] = 1.0 if qr > 0 else 0.0
        xprev = np.zeros((3, NP, D), np.float32)
        for j in range(3):
            qq = qr - 1 - j
            if qq >= 0:
                flags[1 + j] = 1.0
                xprev[j] = x_prompt[b, NP * qq:NP * qq + NP]
        m = dict(shared)
        m.update({
            "xprev": xprev, "xp": np.ascontiguousarray(x_prompt[b, t0:t0 + NP]), "xh": np.ascontiguousarray(xh),
            "xs": np.ascontiguousarray(x_sample[NS * c:NS * c + NS, 0]),
            "cc": np.ascontiguousarray(np.concatenate([c_prompt[b:b + 1], c_sample[NS * c:NS * c + NS]], 0)),
            "st_re": np.ascontiguousarray(sre[NS * c:NS * c + NS]), "st_im": np.ascontiguousarray(sim[NS * c:NS * c + NS]),
            "ck": np.ascontiguousarray(ckk[NS * c:NS * c + NS]), "cv": np.ascontiguousarray(cvv[NS * c:NS * c + NS]),
            "flags": flags,
        })
        in_maps.append(m)
    nc = build()
    res = run_bass_kernel_spmd(nc, in_maps, core_ids=list(range(NCORES)))
    R_ = res.results
    kernel.last_results = R_
    y_prompt = np.stack([np.concatenate([R_[4 * b + q]["yp"] for q in range(4)], 0) for b in range(2)], 0)
    y_sample = np.concatenate([R_[c]["ys"] for c in range(NCORES)], 0).reshape(128, 1, D)
    p_hr = np.stack([R_[4 * b + 3]["pst_re"].reshape(64, 64) for b in range(2)], 0)[None]
    p_hi = np.stack([R_[4 * b + 3]["pst_im"].reshape(64, 64) for b in range(2)], 0)[None]
    p_k = np.stack([R_[4 * b + 3]["pck"].reshape(128, 4, 64) for b in range(2)], 0)[None]
    p_v = np.stack([R_[4 * b + 3]["pcv"].reshape(128, 4, 64) for b in range(2)], 0)[None]
    s_hr = np.concatenate([R_[c]["sst_re"] for c in range(NCORES)], 0).reshape(1, 128, 64, 64)
    s_hi = np.concatenate([R_[c]["sst_im"] for c in range(NCORES)], 0).reshape(1, 128, 64, 64)
    s_k = np.concatenate([R_[c]["sck"] for c in range(NCORES)], 0).reshape(1, 128, 128, 4, 64)
    s_v = np.concatenate([R_[c]["scv"] for c in range(NCORES)], 0).reshape(1, 128, 128, 4, 64)
    return (y_prompt.astype(np.float32), y_sample.astype(np.float32), p_hr.astype(np.float32), p_hi.astype(np.float32),
            p_k.astype(np.float32), p_v.astype(np.float32), s_hr.astype(np.float32), s_hi.astype(np.float32),
            s_k.astype(np.float32), s_v.astype(np.float32))
```

```python
import math
import os
from contextlib import ExitStack
import numpy as np
import ml_dtypes
import concourse.bass as bass
import concourse.mybir as mybir
from concourse.bass_utils import run_bass_kernel_spmd

F32 = mybir.dt.float32
BF16 = mybir.dt.bfloat16
U8 = mybir.dt.uint8
ALU = mybir.AluOpType
AF = mybir.ActivationFunctionType
AX = mybir.AxisListType

PE, ACT, DVE, POOL, SP = "tensor", "scalar", "vector", "gpsimd", "sync"
ENGS = [PE, ACT, DVE, POOL, SP]

NCORES = 8
D = 1024
NP = 2048
NS = 16
NT = NP + NS
TBS = [(0, 512), (512, 512), (1024, 512), (1536, 512), (2048, 16)]
DIN = 6656
OFF_U, OFF_ZS, OFF_Q, OFF_K, OFF_V, OFF_ZA, OFF_GS, OFF_GA = 0, 1024, 2048, 3072, 3328, 3584, 4608, 5632
ALPHA = 2.0 ** 0.25
LN_EPS = 1e-5
DEBUG = False


class Dep:
    __slots__ = ("sem", "val")

    def __init__(self, sem, val):
        self.sem = sem
        self.val = val


class Buf:
    __slots__ = ("w", "r", "name")

    def __init__(self, name=""):
        self.w = []
        self.r = []
        self.name = name


class _RProxy:
    def __init__(self, bufs):
        self.bufs = bufs

    def append(self, h):
        for b in self.bufs:
            b.r.append(h)

    def __len__(self):
        return 0


class BufGroup:
    def __init__(self, bufs):
        self.bufs = list(bufs)
        self.r = _RProxy(self.bufs)

    @property
    def w(self):
        return [h for b in self.bufs for h in b.w]


def handoff(new_bufs, old_bufs):
    deps = []
    for b in old_bufs:
        deps.extend(b.w)
        deps.extend(b.r)
    for nb in new_bufs:
        nb.r = list(nb.r) + deps


class DSem:
    def __init__(self, h):
        self.h = h
        self.cnt = 0


class Prog:
    def __init__(self, nc, stack):
        self.nc = nc
        self.q = {e: [] for e in ENGS}
        self.esem = {}
        self.cnt = {e: 0 for e in ENGS}
        self.allsems = []
        for e in [PE, ACT, DVE, POOL]:
            self.esem[e] = nc.alloc_semaphore("s_" + e)
            self.allsems.append(self.esem[e])
        self.seen = {}
        self.stack = stack
        self.nd = 0
        self.ninst = {e: 0 for e in ENGS}
        self.dead = False
        self.stop = int(os.environ.get("KSTOP", "99"))

    def phase(self, n):
        self.dead = n > self.stop

    def dsem(self, name=None):
        self.nd += 1
        h = self.nc.alloc_semaphore(f"d{self.nd}_{name or 'm'}")
        self.allsems.append(h)
        return DSem(h)

    def _waits(self, eng, deps):
        best = {}
        for d in deps:
            if d is None:
                continue
            k = id(d.sem)
            if k not in best or best[k].val < d.val:
                best[k] = d
        ws = []
        for d in best.values():
            k = (eng, id(d.sem))
            if self.seen.get(k, 0) >= d.val:
                continue
            self.seen[k] = d.val
            ws.append((d.sem, d.val))
        return ws

    @staticmethod
    def _compact(lst):
        best = {}
        for d in lst:
            k = id(d.sem)
            if k not in best or best[k].val < d.val:
                best[k] = d
        return list(best.values())

    @staticmethod
    def _bufdeps(reads, writes, wadd=()):
        deps = []
        for b in reads:
            deps.extend(b.w)
        for b in writes:
            deps.extend(b.w)
            deps.extend(b.r)
        for b in wadd:
            deps.extend(b.r)
        return deps

    @classmethod
    def _update(cls, h, reads, writes, wadd=()):
        for b in reads:
            b.r.append(h)
            if len(b.r) > 32:
                b.r = cls._compact(b.r)
        for b in writes:
            b.w = [h]
            b.r = []
        for b in wadd:
            b.w.append(h)
            if len(b.w) > 32:
                b.w = cls._compact(b.w)

    def op(self, eng, fn, reads=(), writes=(), deps=(), sig=True, wadd=()):
        if self.dead:
            return None
        alld = list(deps) + self._bufdeps(reads, writes, wadd)
        ws = self._waits(eng, alld)
        h = None
        if sig:
            self.cnt[eng] += 1
            h = Dep(self.esem[eng], self.cnt[eng])
        sem = self.esem[eng] if sig else None
        self.ninst[eng] += 1 + len(ws)

        def run(e, ws=ws, fn=fn, sem=sem):
            for (s, v) in ws:
                e.wait_ge(s, v)
            ins = fn(e)
            if sem is not None:
                ins.then_inc(sem, 1)
        self.q[eng].append(run)
        if h is not None:
            self._update(h, reads, writes, wadd)
        return h

    def attach(self, h, reads=(), writes=(), wadd=()):
        if self.dead or h is None:
            return
        self._update(h, reads, writes, wadd)

    def dma(self, eng, ds, out, in_, reads=(), writes=(), deps=(), wadd=(), **kw):
        if self.dead:
            return None
        alld = list(deps) + self._bufdeps(reads, writes, wadd)
        ws = self._waits(eng, alld)
        ds.cnt += 16
        h = Dep(ds.h, ds.cnt)
        self.ninst[eng] += 1 + len(ws)

        def run(e, ws=ws, out=out, in_=in_, kw=kw, sh=ds.h):
            for (s, v) in ws:
                e.wait_ge(s, v)
            e.dma_start(out=out, in_=in_, **kw).then_inc(sh, 16)
        self.q[eng].append(run)
        self._update(h, reads, writes, wadd)
        return h

    def raw(self, eng, fn, ds, inc, reads=(), writes=(), deps=()):
        if self.dead:
            return None
        alld = list(deps) + self._bufdeps(reads, writes)
        ws = self._waits(eng, alld)
        ds.cnt += inc
        h = Dep(ds.h, ds.cnt)

        def run(e, ws=ws, fn=fn, sh=ds.h, inc=inc):
            for (s, v) in ws:
                e.wait_ge(s, v)
            fn(e).then_inc(sh, inc)
        self.q[eng].append(run)
        self._update(h, reads, writes)
        return h

    def wait(self, eng, deps):
        ws = self._waits(eng, deps)

        def run(e, ws=ws):
            for (s, v) in ws:
                e.wait_ge(s, v)
        self.q[eng].append(run)

    def emit(self):
        nc = self.nc
        with nc.Block() as block:
            @block.tensor
            def _(e):
                for f in self.q[PE]:
                    f(e)

            @block.scalar
            def _(e):
                for f in self.q[ACT]:
                    f(e)

            @block.vector
            def _(e):
                for f in self.q[DVE]:
                    f(e)

            @block.gpsimd
            def _(e):
                for f in self.q[POOL]:
                    f(e)

            @block.sync
            def _(e):
                for f in self.q[SP]:
                    f(e)


def _dsize(dt):
    return {F32: 4, BF16: 2, U8: 1}[dt]


class Arena:
    def __init__(self, nc, stack, nbytes):
        self.t = stack.enter_context(nc.sbuf_tensor("arena", [128, nbytes], U8))
        self.nbytes = nbytes

    def carve(self, off, shape, dt):
        n = int(np.prod(shape)) * _dsize(dt)
        assert off % 4 == 0 and off + n <= self.nbytes, (off, n, self.nbytes)
        v = self.t[:, off:off + n]
        if dt != U8:
            v = v.bitcast(dt)
        if len(shape) > 1:
            names = [f"a{i}" for i in range(len(shape))]
            pat = "p (" + " ".join(names) + ") -> p " + " ".join(names)
            v = v.rearrange(pat, **{names[i]: shape[i] for i in range(len(shape))})
        return v


O_HT = 0
O_HTH = 33024
O_CONST = 35072
O_W = 45312
O_A = 61696
O_B = 94720
O_C = 127744
ARENA = 212000
C_SIZE = ARENA - O_C


def build():
    nc = bass.Bass("TRN2", target_bir_lowering=False)

    def din(name, shape, dt=F32):
        return nc.dram_tensor(name, list(shape), dt, kind="ExternalInput").ap()

    def dout(name, shape, dt=F32):
        return nc.dram_tensor(name, list(shape), dt, kind="ExternalOutput").ap()

    xprev = din("xprev", [3, NP, D]); xp = din("xp", [NP, D]); xh = din("xh", [128, D]); xs = din("xs", [NS, D]); ccin = din("cc", [17, D])
    st_re = din("st_re", [NS, 4096]); st_im = din("st_im", [NS, 4096])
    ck = din("ck", [NS, 128, 256]); cv = din("cv", [NS, 128, 256])
    w_ada = din("w_ada", [D, 3072]); b_ada = din("b_ada", [3072]); w_in = din("w_in", [D, DIN])
    lam_re = din("lam_re", [64, 64]); lam_im = din("lam_im", [64, 64]); log_delta = din("log_delta", [64])
    b_re = din("b_re", [4096, 16]); b_im = din("b_im", [4096, 16])
    c_re = din("c_re", [1024, 64]); c_im = din("c_im", [1024, 64])
    ssm_d = din("ssm_d", [1024]); w_glu = din("w_glu", [D, D]); b_glu = din("b_glu", [D])
    sinks = din("sinks", [16]); rel_bias = din("rel_bias", [32, 16])
    w_bs = din("w_bs", [D, D]); w_ba = din("w_ba", [D, D]); w_out = din("w_out", [D, D])
    ln_g = din("ln_g", [D]); ln_b = din("ln_b", [D])
    rtab = din("rtab", [32, 384]); maskc = din("maskc", [128, 256]); bmaskc = din("bmaskc", [128, 128])
    diagc = din("diagc", [16, 64]); flagsc = din("flags", [32])

    yp = dout("yp", [NP, D]); ys = dout("ys", [NS, D])
    pst_re = dout("pst_re", [32, 128]); pst_im = dout("pst_im", [32, 128])
    pck = dout("pck", [128, 256]); pcv = dout("pcv", [128, 256])
    sst_re = dout("sst_re", [NS, 4096]); sst_im = dout("sst_im", [NS, 4096])
    sck = dout("sck", [NS, 128, 256]); scv = dout("scv", [NS, 128, 256])
    dbg = {}
    if DEBUG:
        dbg["hT"] = dout("dbg_hT", [128, 8, NT], BF16)
        dbg["uT"] = dout("dbg_uT", [128, 8, NT], BF16)
        dbg["pw"] = dout("dbg_pw", [128, 9 * 2 * 32])
        dbg["hend"] = dout("dbg_hend", [128, 2 * 32 * 16])
        dbg["bb"] = dout("dbg_bb", [128, 2 * 32 * 16]); dbg["ccm"] = dout("dbg_ccm", [128, 2 * 32 * 16])
        dbg["R8"] = dout("dbg_R8", [128, 16 * 2 * 32]); dbg["R128"] = dout("dbg_R128", [128, 16 * 2 * 32])
        dbg["A2k"] = dout("dbg_A2k", [128, 3 * 2 * 32])
        dbg["WinL"] = dout("dbg_WinL", [128, 8 * 8 * 2 * 128], BF16)
        dbg["X0"] = dout("dbg_X0", [128, 512])
        dbg["KL"] = dout("dbg_KL", [128, 2 * 8 * 128], BF16); dbg["Ca"] = dout("dbg_Ca", [128, 2 * 4 * 9 * 2 * 32], BF16)
        dbg["Hb"] = dout("dbg_Hb", [128, 2 * 2048], BF16); dbg["Xs"] = dout("dbg_Xs", [128, 2 * 2048])
        dbg["carry"] = dout("dbg_carry", [128, 17 * 2 * 32])
        dbg["yT"] = dout("dbg_yT", [128, 8, NT], BF16)
        dbg["gbs"] = dout("dbg_gbs", [128, 8, NT], BF16)
        dbg["oT"] = dout("dbg_oT", [128, 8, NT], BF16)
        dbg["mT"] = dout("dbg_mT", [128, 8, NT], BF16)
        dbg["modT"] = dout("dbg_modT", [128, 24 * 17])

    ib = nc.dram_tensor("cc_ib", [128, 64], F32, kind="Internal")
    ob = nc.dram_tensor("cc_ob", [NCORES * 128, 64], F32, kind="Internal")

    st = ExitStack()
    with st:
        P = Prog(nc, st)
        AR = Arena(nc, st, ARENA)
        cv_ = AR.carve
        ps = [st.enter_context(nc.psum_tensor(f"ps{i}", [128, 512], F32)) for i in range(8)]
        psb = [Buf(f"ps{i}") for i in range(8)]
        dout_sem = P.dsem("dout")
        out_sems = []

        def osem_new(name):
            d_ = P.dsem(name)
            out_sems.append(d_)
            return d_
        misc_sem = P.dsem("misc")

        def misc_load(eng, out, in_, buf, wadd=False, **kw):
            if P.dead:
                return None
            if wadd:
                return P.dma(eng, P.dsem(), out, in_, wadd=[buf], **kw)
            return P.dma(eng, P.dsem(), out, in_, writes=[buf], **kw)

        hT = cv_(O_HT, [8, NT], BF16)
        hTh = cv_(O_HTH, [8, 128], BF16)
        hT_b = [Buf(f"hT{i}") for i in range(len(TBS))]
        hTh_b = Buf("hTh")
        o = O_CONST
        ident = cv_(o, [128], F32); o += 512
        modT = cv_(o, [24, 17], F32); o += 1664
        op1p = cv_(o, [8, 17], F32); o += 576
        flags = cv_(o, [32], F32); o += 128
        Dm = cv_(o, [8], F32); o += 32
        bglu = cv_(o, [8], F32); o += 32
        ES = cv_(o, [4, 4, 128], BF16); o += 4096
        onesd = cv_(o, [128], BF16); o += 256
        EBself = cv_(o, [16], F32); o += 64
        bmask = cv_(o, [128], F32); o += 512
        ones1 = cv_(o, [128], F32); o += 512
        assert o <= O_CONST + 10240
        b_ident = Buf(); b_modT = Buf(); b_flags = Buf(); b_Dm = Buf(); b_bglu = Buf(); b_ES = Buf()
        b_ones = Buf(); b_EBself = Buf(); b_bmask = Buf()
        wslot = [cv_(O_W + 8192 * i, [8, 512], BF16) for i in range(2)]
        wslot_b = [Buf("w0"), Buf("w1")]
        wsem = [P.dsem("w0"), P.dsem("w1")]
        wctr = [0]
        RA = cv_(O_A, [8, NT], BF16)
        RB = cv_(O_B, [8, NT], BF16)

        rr = {"i": 0}

        def bank():
            i = rr["i"] % 8
            rr["i"] += 1
            return i

        def load_w(src2d, ncols):
            s = wctr[0] % 2
            wctr[0] += 1
            P.dma(POOL, wsem[s], wslot[s][:, :, 0:ncols], src2d.rearrange("(k p) f -> p k f", p=128),
                  writes=[wslot_b[s]])
            return s

        def evac_copy(i, out_ap, in_ap, reads, writes, wadd=()):
            if i % 2 == 0:
                return P.op(ACT, lambda e: e.activation(out=out_ap, in_=in_ap, func=AF.Copy), reads=reads, writes=writes, wadd=wadd)
            return P.op(DVE, lambda e: e.tensor_copy(out=out_ap, in_=in_ap), reads=reads, writes=writes, wadd=wadd)

        def proj_fm(src2d, ncols, rhs_of, rhs_bufs, evac, tbs=TBS):
            s = load_w(src2d, ncols)
            for oc in range(ncols // 128):
                for tbi, (t0, n) in enumerate(tbs):
                    b = bank()
                    for k in range(8):
                        last = (k == 7)
                        P.op(PE, lambda e, b=b, k=k, oc=oc, tbi=tbi, t0=t0, n=n, s=s: e.matmul(
                            ps[b][:, 0:n], lhsT=wslot[s][:, k, oc * 128:(oc + 1) * 128], rhs=rhs_of(k, t0, n),
                            start=(k == 0), stop=(k == 7)),
                            reads=[wslot_b[s], rhs_bufs[tbi]] if k == 0 else [], writes=[psb[b]] if k == 0 else [],
                            sig=last)
                        if last and not P.dead:
                            h = Dep(P.esem[PE], P.cnt[PE])
                            P.attach(h, reads=[wslot_b[s], rhs_bufs[tbi]], writes=[psb[b]])
                    evac(oc, tbi, ps[b][:, 0:n], psb[b])

        def cmul(eng, dst_r, dst_i, xr, xi, yr, yi, t1, t2, bufs_r, bufs_w, tb):
            P.op(eng, lambda e: e.tensor_tensor(out=t1, in0=xr, in1=yr, op=ALU.mult), reads=bufs_r, writes=[tb])
            P.op(eng, lambda e: e.tensor_tensor(out=t2, in0=xi, in1=yi, op=ALU.mult), reads=bufs_r, writes=[tb])
            P.op(eng, lambda e: e.tensor_tensor(out=dst_r, in0=t1, in1=t2, op=ALU.subtract), reads=[tb], writes=bufs_w)
            P.op(eng, lambda e: e.tensor_tensor(out=t1, in0=xr, in1=yi, op=ALU.mult), reads=bufs_r + bufs_w, writes=[tb])
            P.op(eng, lambda e: e.tensor_tensor(out=t2, in0=xi, in1=yr, op=ALU.mult), reads=bufs_r + bufs_w, writes=[tb])
            P.op(eng, lambda e: e.tensor_tensor(out=dst_i, in0=t1, in1=t2, op=ALU.add), reads=[tb], writes=bufs_w)

        P.phase(0)
        P.op(POOL, lambda e: e.memset(ident, 0.0), writes=[b_ident])
        P.op(POOL, lambda e: e.affine_select(out=ident, in_=ident, pattern=[[-1, 128]], compare_op=ALU.not_equal,
                                             fill=1.0, base=0, channel_multiplier=1), writes=[b_ident])
        misc_load(SP, flags, flagsc.rearrange("(o n) -> o n", o=1).to_broadcast([128, 32]), b_flags)
        misc_load(SP, bmask, bmaskc, b_bmask)
        P.op(POOL, lambda e: e.memset(ones1[0:1, :], 1.0), writes=[b_ones])
        P.op(POOL, lambda e: e.memset(onesd[0:1, 0:64], 0.0), wadd=[b_ones])
        P.op(POOL, lambda e: e.memset(onesd[0:1, 64:128], 1.0), wadd=[b_ones])

        P.phase(1)
        c_t = cv_(O_C + 62208, [1024], F32); c_sg = cv_(O_C + 66304, [1024], F32)
        ccT = cv_(O_C + 70400, [8, 17], BF16); badain = cv_(O_C + 70912, [128], F32); badaT = cv_(O_C + 71424, [24], F32)
        b_ct = Buf(); b_csg = Buf(); b_ccT = Buf(); b_bin = Buf(); b_baT = Buf()
        misc_load(SP, c_t[0:17, :], ccin, b_ct)
        misc_load(SP, badain[0:24, :], b_ada.rearrange("(c p) -> c p", p=128), b_bin)
        P.op(ACT, lambda e: e.activation(out=c_sg[0:17, :], in_=c_t[0:17, :], func=AF.Sigmoid), reads=[b_ct], writes=[b_csg])
        P.op(DVE, lambda e: e.tensor_tensor(out=c_sg[0:17, :], in0=c_sg[0:17, :], in1=c_t[0:17, :], op=ALU.mult),
             reads=[b_ct], writes=[b_csg])
        bk = bank()
        for k in range(8):
            P.op(PE, lambda e, k=k: e.transpose(out=ps[bk][:, 17 * k:17 * k + 17], in_=c_sg[0:17, 128 * k:128 * k + 128],
                                                identity=ident[0:17, 0:17]),
                 reads=[b_csg, b_ident], writes=[psb[bk]] if k == 0 else [], wadd=[psb[bk]] if k > 0 else [])
        P.op(DVE, lambda e: e.tensor_copy(out=ccT.rearrange("p k s -> p (k s)"), in_=ps[bk][:, 0:136]), reads=[psb[bk]], writes=[b_ccT])
        bk2 = bank()
        P.op(PE, lambda e: e.transpose(out=ps[bk2][:, 0:24], in_=badain[0:24, :], identity=ident[0:24, 0:24]),
             reads=[b_bin, b_ident], writes=[psb[bk2]])
        P.op(DVE, lambda e: e.tensor_copy(out=badaT, in_=ps[bk2][:, 0:24]), reads=[psb[bk2]], writes=[b_baT])
        bkm = bank()
        hlast = None
        for blk in range(6):
            s = load_w(w_ada[:, 512 * blk:512 * blk + 512], 512)
            for oc in range(4):
                f = 4 * blk + oc
                for k in range(8):
                    first = (blk == 0 and oc == 0 and k == 0)
                    lastk = (k == 7)
                    hlast = P.op(PE, lambda e, f=f, k=k, oc=oc, s=s: e.matmul(
                        ps[bkm][:, 17 * f:17 * f + 17], lhsT=wslot[s][:, k, oc * 128:(oc + 1) * 128], rhs=ccT[:, k, :],
                        start=(k == 0), stop=(k == 7)),
                        reads=[wslot_b[s], b_ccT] if k == 0 else [], writes=[psb[bkm]] if first else [], sig=lastk and oc == 3)
            P.attach(hlast, reads=[wslot_b[s]], wadd=[psb[bkm]])
        P.op(DVE, lambda e: e.tensor_tensor(out=modT, in0=ps[bkm][:, 0:408].rearrange("p (f s) -> p f s", f=24),
                                            in1=badaT.unsqueeze(2).to_broadcast([128, 24, 17]), op=ALU.add),
             reads=[psb[bkm], b_baT], writes=[b_modT])
        P.op(DVE, lambda e: e.tensor_scalar(out=op1p, in0=modT[:, 8:16, :], scalar1=1.0, scalar2=None, op0=ALU.add),
             reads=[b_modT], wadd=[b_modT])

        xst = [cv_(O_C + 71552 + 4096 * i, [1024], F32) for i in range(2)] + [cv_(O_C + 62208 + 4096 * i, [1024], F32) for i in range(2)]
        xst_b = [Buf(), Buf(), Buf(), Buf()]
        xsem = [P.dsem("x0"), P.dsem("x1"), P.dsem("x2"), P.dsem("x3")]
        handoff([xst_b[2]], [b_ct]); handoff([xst_b[3]], [b_csg])
        hTs_b = hT_b[4]
        tmpS = cv_(O_C + 79744, [8, 16], F32)
        b_tmpS = Buf()

        def phase_a(xsrc, full):
            tiles = [("p", i) for i in range(16)] + ([("h", 0), ("s", 0)] if full else [])
            for ti, (kind, i) in enumerate(tiles):
                sl = ti % 4
                if kind == "p":
                    src, rows, dst, dbuf = xsrc[128 * i:128 * i + 128, :], 128, (lambda k, i=i: hT[:, k, 128 * i:128 * i + 128]), hT_b[i // 4]
                elif kind == "h":
                    src, rows, dst, dbuf = xh, 128, (lambda k: hTh[:, k, :]), hTh_b
                else:
                    src, rows, dst, dbuf = xs, NS, None, hTs_b
                P.dma(SP, xsem[sl], xst[sl][0:rows, :], src, writes=[xst_b[sl]])
                b0, b1 = bank(), bank()
                for k in range(8):
                    bb_ = b0 if k < 4 else b1
                    j = k % 4
                    P.op(PE, lambda e, k=k, bb_=bb_, j=j, sl=sl, rows=rows: e.transpose(
                        out=ps[bb_][:, rows * j:rows * j + rows], in_=xst[sl][0:rows, 128 * k:128 * k + 128],
                        identity=ident[0:rows, 0:rows]),
                        reads=[xst_b[sl], b_ident], writes=[psb[bb_]] if j == 0 else [], wadd=[psb[bb_]] if j > 0 else [])
                if kind != "s":
                    for k in range(8):
                        bb_ = b0 if k < 4 else b1
                        j = k % 4
                        src_ps = ps[bb_][:, 128 * j:128 * j + 128]
                        if False:
                            P.op(DVE, lambda e, k=k, src_ps=src_ps, dst=dst: e.tensor_scalar(
                                out=dst(k), in0=src_ps, scalar1=op1p[:, k, 0:1], scalar2=modT[:, k, 0:1], op0=ALU.mult, op1=ALU.add),
                                reads=[psb[bb_], b_modT], wadd=[dbuf])
                        else:
                            P.op(ACT, lambda e, k=k, src_ps=src_ps, dst=dst: e.activation(
                                out=dst(k), in_=src_ps, func=AF.Identity, scale=op1p[:, k, 0:1], bias=modT[:, k, 0:1]),
                                reads=[psb[bb_], b_modT], wadd=[dbuf])
                else:
                    for half, bb_ in enumerate([b0, b1]):
                        P.op(DVE, lambda e, half=half, bb_=bb_: e.tensor_tensor(
                            out=tmpS[:, 4 * half:4 * half + 4, :], in0=ps[bb_][:, 0:64].rearrange("p (k s) -> p k s", k=4),
                            in1=op1p[:, 4 * half:4 * half + 4, 1:17], op=ALU.mult), reads=[psb[bb_], b_modT], wadd=[b_tmpS])
                    P.op(DVE, lambda e: e.tensor_tensor(out=hT[:, :, NP:NT], in0=tmpS, in1=modT[:, 0:8, 1:17], op=ALU.add),
                         reads=[b_tmpS, b_modT], wadd=[dbuf])

        uT = RA
        uT_b = [Buf(f"uT{g}") for g in range(8)]
        rhs_h = lambda k, t0, n: hT[:, k, t0:t0 + n]
        cnt = {"i": 0}

        def phase_b(full):
            for blk in range(2):
                def ev(oc, tbi, pap, pb, blk=blk):
                    t0, n = TBS[tbi]
                    g = 4 * blk + oc
                    evac_copy(0, uT[:, g, t0:t0 + n], pap, [pb], [], wadd=[uT_b[g]])
                proj_fm(w_in[:, OFF_U + 512 * blk:OFF_U + 512 * blk + 512], 512, rhs_h, hT_b, ev, tbs=TBS if full else TBS[:4])

        P.phase(3)
        phase_a(xprev[0], False)
        phase_b(False)
        P.phase(2)
        oc_ = O_C
        pw = cv_(oc_ + 0, [9, 2, 32], F32); bb = cv_(oc_ + 2304, [2, 32, 16], F32); ccm = cv_(oc_ + 6400, [2, 32, 16], F32)
        R8 = cv_(oc_ + 10496, [16, 2, 32], F32); R128 = cv_(oc_ + 14592, [16, 2, 32], F32); A2k = cv_(oc_ + 18688, [3, 2, 32], F32)
        Hend = cv_(oc_ + 19456, [2, 32, 16], F32); carry = cv_(oc_ + 23552, [17, 2, 32], F32)
        Sb = cv_(oc_ + 27904, [8, 2, 128], BF16); coef = cv_(oc_ + 32000, [12, 32], F32)
        misc = cv_(oc_ + 33536, [32, 32], F32)
        t1 = cv_(oc_ + 37632, [1024], F32); t2 = cv_(oc_ + 41728, [1024], F32)
        O_S = oc_ + 45824
        Sslot = [cv_(O_S + 8192 * i, [8, 2, 128], F32) for i in range(2)]
        O_CA = oc_ + 62208; O_KL = oc_ + 71424; O_HB = oc_ + 75520
        b_pw = Buf("pw"); b_bb = Buf("bb"); b_ccm = Buf("ccm"); b_R8 = Buf(); b_R128 = Buf(); b_A2k = Buf(); b_coef = Buf()
        b_misc = Buf("misc"); b_t = Buf("t12"); b_Sslot = [Buf("S0"), Buf("S1")]
        craw = [cv_(O_S + 2048 * i, [8, 64], F32) for i in range(2)]
        lamraw = cv_(O_S + 4096, [128], F32); ldraw = cv_(O_S + 4608, [64], F32)
        draw = cv_(O_S + 4864, [128], F32); bgraw = cv_(O_S + 5376, [128], F32)
        braw = [cv_(O_S + 8192 + 2048 * i, [32, 16], F32) for i in range(2)]
        b_craw = Buf(); b_lam = Buf(); b_ld = Buf(); b_draw = Buf(); b_braw = Buf()
        misc_load(SP, craw[0], c_re.rearrange("(t r) p -> r t p", r=128), b_craw, wadd=True)
        misc_load(SP, craw[1], c_im.rearrange("(t r) p -> r t p", r=128), b_craw, wadd=True)
        misc_load(SP, lamraw[0:64, 0:64], lam_re, b_lam, wadd=True)
        misc_load(SP, lamraw[0:64, 64:128], lam_im, b_lam, wadd=True)
        misc_load(SP, ldraw, log_delta.rearrange("(o n) -> o n", o=1).to_broadcast([128, 64]), b_ld)
        misc_load(SP, draw[0:8, :], ssm_d.rearrange("(c p) -> c p", p=128), b_draw, wadd=True)
        misc_load(SP, bgraw[0:8, :], b_glu.rearrange("(c p) -> c p", p=128), b_draw, wadd=True)
        for i, src in enumerate([b_re, b_im]):
            for q4 in range(4):
                misc_load(SP, braw[i][:, 8 * q4:8 * q4 + 8, :],
                          src[1024 * q4:1024 * q4 + 1024, :].rearrange("(gp q) c -> q gp c", q=128), b_braw, wadd=True)
        M_ = lambda i: misc[:, i, :]
        LR, LI, DT, TH, FR, FC, MAG, SN, CS, NR, DEN, CR, CI, G1, KF, TMPA = [M_(i) for i in range(16)]
        KI = misc[:, 16, :].bitcast(mybir.dt.int32)
        bkl = bank()
        P.op(PE, lambda e: e.transpose(out=ps[bkl][:, 0:64], in_=lamraw[0:64, :], identity=ident[0:64, 0:64]),
             reads=[b_lam, b_ident], writes=[psb[bkl]])
        P.op(DVE, lambda e: e.tensor_copy(out=LR[0:64, :], in_=ps[bkl][0:64, 0:64:2]), reads=[psb[bkl]], wadd=[b_misc])
        P.op(DVE, lambda e: e.tensor_copy(out=LR[64:128, :], in_=ps[bkl][0:64, 1:64:2]), reads=[psb[bkl]], wadd=[b_misc])
        P.op(DVE, lambda e: e.tensor_copy(out=LI[0:64, :], in_=ps[bkl][64:128, 0:64:2]), reads=[psb[bkl]], wadd=[b_misc])
        P.op(DVE, lambda e: e.tensor_copy(out=LI[64:128, :], in_=ps[bkl][64:128, 1:64:2]), reads=[psb[bkl]], wadd=[b_misc])
        bkd = bank()
        P.op(PE, lambda e: e.transpose(out=ps[bkd][:, 0:8], in_=draw[0:8, :], identity=ident[0:8, 0:8]),
             reads=[b_draw, b_ident], writes=[psb[bkd]])
        P.op(PE, lambda e: e.transpose(out=ps[bkd][:, 8:16], in_=bgraw[0:8, :], identity=ident[0:8, 0:8]),
             reads=[b_draw, b_ident], wadd=[psb[bkd]])
        P.op(DVE, lambda e: e.tensor_copy(out=Dm, in_=ps[bkd][:, 0:8]), reads=[psb[bkd]], writes=[b_Dm])
        P.op(DVE, lambda e: e.tensor_copy(out=bglu, in_=ps[bkd][:, 8:16]), reads=[psb[bkd]], writes=[b_bglu])
        for ri in range(2):
            for hb in range(2):
                bkc = bank()
                for tt in range(4):
                    t_ = 4 * hb + tt
                    P.op(PE, lambda e, ri=ri, t_=t_, tt=tt, bkc=bkc: e.transpose(
                        out=ps[bkc][0:64, 128 * tt:128 * tt + 128], in_=craw[ri][:, t_, :], identity=ident),
                        reads=[b_craw, b_ident], writes=[psb[bkc]] if tt == 0 else [], wadd=[psb[bkc]] if tt > 0 else [])
                for g2 in range(2):
                    src = ps[bkc][0:64, :].rearrange("p (tg g2 c) -> p tg g2 c", g2=2, c=16)[:, :, g2, :]
                    P.op(DVE, lambda e, ri=ri, hb=hb, g2=g2, src=src: e.tensor_copy(
                        out=ccm[64 * g2:64 * g2 + 64, ri, 16 * hb:16 * hb + 16, :], in_=src),
                        reads=[psb[bkc]], wadd=[b_ccm])
        P.op(ACT, lambda e: e.activation(out=DT[0:64, :], in_=ldraw[0:64, 0:64:2], func=AF.Exp), reads=[b_ld], wadd=[b_misc])
        P.op(ACT, lambda e: e.activation(out=DT[64:128, :], in_=ldraw[64:128, 1:64:2], func=AF.Exp), reads=[b_ld], wadd=[b_misc])
        G = DVE
        tt_ = lambda out, a, b_, op, **kw: P.op(G, lambda e: e.tensor_tensor(out=out, in0=a, in1=b_, op=op), reads=[b_misc], wadd=[b_misc], **kw)
        ts_ = lambda out, a, s1, s2, o0, o1: P.op(G, lambda e: e.tensor_scalar(out=out, in0=a, scalar1=s1, scalar2=s2, op0=o0, op1=o1), reads=[b_misc], wadd=[b_misc])
        tt_(TH, LI, DT, ALU.mult)
        ts_(FR, TH, 1.0 / (2 * math.pi), 0.0, ALU.mult, ALU.add)
        P.op(DVE, lambda e: e.tensor_copy(out=KI, in_=FR), reads=[b_misc], wadd=[b_misc])
        P.op(DVE, lambda e: e.tensor_copy(out=KF, in_=KI), reads=[b_misc], wadd=[b_misc])
        tt_(FR, FR, KF, ALU.subtract)
        ts_(FC, FR, 1.0, 0.25, ALU.mult, ALU.add)
        P.op(DVE, lambda e: e.tensor_single_scalar(out=G1, in_=FC, scalar=0.5, op=ALU.is_gt), reads=[b_misc], wadd=[b_misc])
        tt_(FC, FC, G1, ALU.subtract)
        TWO_PI = 2.0 * math.pi
        P.op(ACT, lambda e: e.activation(out=SN, in_=FR, func=AF.Sin, scale=TWO_PI), reads=[b_misc], wadd=[b_misc])
        P.op(ACT, lambda e: e.activation(out=CS, in_=FC, func=AF.Sin, scale=TWO_PI), reads=[b_misc], wadd=[b_misc])
        tt_(TMPA, LR, DT, ALU.mult)
        P.op(ACT, lambda e: e.activation(out=MAG, in_=TMPA, func=AF.Exp), reads=[b_misc], wadd=[b_misc])
        P.op(G, lambda e: e.memset(pw[:, 0, 0, :], 1.0), wadd=[b_pw])
        P.op(G, lambda e: e.memset(pw[:, 0, 1, :], 0.0), wadd=[b_pw])
        P.op(G, lambda e: e.tensor_tensor(out=pw[:, 1, 0, :], in0=MAG, in1=CS, op=ALU.mult), reads=[b_misc], wadd=[b_pw])
        P.op(G, lambda e: e.tensor_tensor(out=pw[:, 1, 1, :], in0=MAG, in1=SN, op=ALU.mult), reads=[b_misc], wadd=[b_pw])

        def cm(dst, x, y, n, rb, wb):
            T1 = t1[:, 0:n * 32].rearrange("p (n g) -> p n g", n=n)
            T2 = t2[:, 0:n * 32].rearrange("p (n g) -> p n g", n=n)
            cmul(G, dst[:, :, 0, :], dst[:, :, 1, :], x[:, :, 0, :], x[:, :, 1, :], y[:, :, 0, :], y[:, :, 1, :], T1, T2, rb, wb, b_t)

        def bc(ap1, n):
            return ap1.to_broadcast([128, n, 2, 32])

        cm(pw[:, 2:3], pw[:, 1:2], pw[:, 1:2], 1, [b_pw], [b_pw])
        cm(pw[:, 3:5], pw[:, 1:3], bc(pw[:, 2:3], 2), 2, [b_pw], [b_pw])
        cm(pw[:, 5:9], pw[:, 1:5], bc(pw[:, 4:5], 4), 4, [b_pw], [b_pw])
        tt_(NR, pw[:, 1, 0, :], pw[:, 0, 0, :], ALU.subtract, deps=b_pw.w)
        tt_(DEN, LR, LR, ALU.mult)
        tt_(TMPA, LI, LI, ALU.mult)
        tt_(DEN, DEN, TMPA, ALU.add)
        P.op(DVE, lambda e: e.reciprocal(out=DEN, in_=DEN), reads=[b_misc], wadd=[b_misc])
        tt_(CR, NR, LR, ALU.mult)
        tt_(TMPA, pw[:, 1, 1, :], LI, ALU.mult)
        tt_(CR, CR, TMPA, ALU.add)
        tt_(CR, CR, DEN, ALU.mult)
        tt_(CI, pw[:, 1, 1, :], LR, ALU.mult)
        tt_(TMPA, NR, LI, ALU.mult)
        tt_(CI, CI, TMPA, ALU.subtract)
        tt_(CI, CI, DEN, ALU.mult)
        CRb = CR.unsqueeze(2).to_broadcast([128, 32, 16]); CIb = CI.unsqueeze(2).to_broadcast([128, 32, 16])
        T1b = t1[:, 0:512].rearrange("p (g c) -> p g c", g=32); T2b = t2[:, 0:512].rearrange("p (g c) -> p g c", g=32)
        P.op(G, lambda e: e.tensor_tensor(out=T1b, in0=braw[0], in1=CRb, op=ALU.mult), reads=[b_braw, b_misc, b_pw], writes=[b_t])
        P.op(G, lambda e: e.tensor_tensor(out=T2b, in0=braw[1], in1=CIb, op=ALU.mult), reads=[b_braw, b_misc], wadd=[b_t])
        P.op(G, lambda e: e.tensor_tensor(out=bb[:, 0], in0=T1b, in1=T2b, op=ALU.subtract), reads=[b_t], wadd=[b_bb])
        P.op(G, lambda e: e.tensor_tensor(out=T1b, in0=braw[1], in1=CRb, op=ALU.mult), reads=[b_braw, b_misc, b_bb], writes=[b_t])
        P.op(G, lambda e: e.tensor_tensor(out=T2b, in0=braw[0], in1=CIb, op=ALU.mult), reads=[b_braw, b_misc], wadd=[b_t])
        P.op(G, lambda e: e.tensor_tensor(out=bb[:, 1], in0=T1b, in1=T2b, op=ALU.add), reads=[b_t], wadd=[b_bb])

        def rev_table(R, A0, bufR, out_last):
            AW = misc[:, 20:22, :].rearrange("p (o r) g -> p o r g", o=1)
            AW2 = misc[:, 22:24, :].rearrange("p (o r) g -> p o r g", o=1)
            P.op(G, lambda e: e.memset(R[:, 15, 0, :], 1.0), wadd=[bufR])
            P.op(G, lambda e: e.memset(R[:, 15, 1, :], 0.0), wadd=[bufR])
            P.op(G, lambda e: e.tensor_copy(out=AW, in_=A0), reads=[b_pw, b_coef, b_misc], wadd=[b_misc])
            w = 1
            cur, nxt = AW, AW2
            while w <= 8:
                cm(R[:, 16 - 2 * w:16 - w], R[:, 16 - w:16], bc(cur, w), w, [bufR, b_misc], [bufR])
                cm(nxt, cur, cur, 1, [b_misc], [b_misc])
                cur, nxt = nxt, cur
                w *= 2
            P.op(G, lambda e: e.tensor_copy(out=out_last, in_=cur), reads=[b_misc], wadd=[b_coef])

        A128 = coef[:, 0:2, :].rearrange("p (o r) g -> p o r g", o=1)
        A2048 = coef[:, 2:4, :].rearrange("p (o r) g -> p o r g", o=1)
        rev_table(R8, pw[:, 8:9], b_R8, A128)
        rev_table(R128, A128, b_R128, A2048)
        P.op(G, lambda e: e.memset(A2k[:, 0, 0, :], 1.0), wadd=[b_A2k])
        P.op(G, lambda e: e.memset(A2k[:, 0, 1, :], 0.0), wadd=[b_A2k])
        P.op(G, lambda e: e.tensor_copy(out=A2k[:, 1:2], in_=A2048), reads=[b_coef], wadd=[b_A2k])
        cm(A2k[:, 2:3], A2048, A2048, 1, [b_coef], [b_A2k])
        P.op(G, lambda e: e.tensor_scalar(out=coef[:, 4, :], in0=pw[:, 8, 1, :], scalar1=-1.0, scalar2=0.0, op0=ALU.mult, op1=ALU.add),
             reads=[b_pw], wadd=[b_coef])
        P.op(G, lambda e: e.tensor_scalar(out=coef[:, 5, :], in0=coef[:, 1, :], scalar1=-1.0, scalar2=0.0, op0=ALU.mult, op1=ALU.add),
             reads=[b_coef], wadd=[b_coef])

        P.phase(5)
        WinL = cv_(O_B, [8, 8, 2, 128], BF16)
        b_WinL = [Buf(f"WinL{g}") for g in range(8)]
        b_Sb = Buf("Sb")
        handoff(b_Sslot, [b_craw, b_lam, b_ld, b_draw, b_braw])
        b_SInit = [Buf("SInit0"), Buf("SInit1")]
        for i in range(2):
            P.op(POOL, lambda e, i=i: e.memset(Sslot[i].rearrange("p k r c -> p (k r c)"), 0.0), writes=[b_Sslot[i], b_SInit[i]])
        T1e = [t1[:, 0:512].rearrange("p (k m c) -> p k m c", k=8, m=4), t1[:, 512:1024].rearrange("p (k m c) -> p k m c", k=8, m=4)]
        T2e = [t2[:, 0:512].rearrange("p (k m c) -> p k m c", k=8, m=4), t2[:, 512:1024].rearrange("p (k m c) -> p k m c", k=8, m=4)]
        b_t5 = [b_t, Buf("t5pool")]
        handoff([b_t5[1]], [b_t])
        for gc in range(8):
            sl = gc % 2
            EG = DVE if gc % 2 == 0 else POOL
            T1s, T2s, b_tt = T1e[gc % 2], T2e[gc % 2], b_t5[gc % 2]
            S = Sslot[sl]
            Sv = S.rearrange("p k r (m g c) -> p k r m g c", m=4, g=2)
            prk = pw[:, 0:8, 0, 4 * gc:4 * gc + 4].unsqueeze(3).to_broadcast([128, 8, 4, 16])
            pik = pw[:, 0:8, 1, 4 * gc:4 * gc + 4].unsqueeze(3).to_broadcast([128, 8, 4, 16])
            bbr = bb[:, 0, 4 * gc:4 * gc + 4, :].unsqueeze(1).to_broadcast([128, 8, 4, 16])
            bbi = bb[:, 1, 4 * gc:4 * gc + 4, :].unsqueeze(1).to_broadcast([128, 8, 4, 16])
            for ri in range(2):
                x1, x2 = (bbr, bbi) if ri == 0 else (bbi, bbr)
                op = ALU.subtract if ri == 0 else ALU.add
                P.op(EG, lambda e, x1=x1, prk=prk, T1s=T1s: e.tensor_tensor(out=T1s, in0=prk, in1=x1, op=ALU.mult), reads=[b_pw, b_bb], writes=[b_tt])
                P.op(EG, lambda e, x2=x2, pik=pik, T2s=T2s: e.tensor_tensor(out=T2s, in0=pik, in1=x2, op=ALU.mult), reads=[b_pw, b_bb], wadd=[b_tt])
                for g2 in range(2):
                    lo, hi = 64 * g2, 64 * g2 + 64
                    P.op(EG, lambda e, ri=ri, g2=g2, lo=lo, hi=hi, op=op, Sv=Sv, T1s=T1s, T2s=T2s: e.tensor_tensor(
                        out=Sv[lo:hi, :, ri, :, g2, :], in0=T1s[lo:hi], in1=T2s[lo:hi], op=op),
                        reads=[b_tt, b_SInit[sl]], wadd=[b_Sslot[sl]])
            P.op(EG, lambda e, gc=gc, S=S: e.tensor_copy(out=Sb[:, gc], in_=S[:, 0]), reads=[b_Sslot[sl]], wadd=[b_Sb])
            for q4 in range(4):
                bkw = bank()
                for j in range(4):
                    k_, ri_ = (4 * q4 + j) // 2, (4 * q4 + j) % 2
                    P.op(PE, lambda e, bkw=bkw, j=j, k_=k_, ri_=ri_, S=S: e.transpose(
                        out=ps[bkw][:, 128 * j:128 * j + 128], in_=S[:, k_, ri_, :], identity=ident),
                        reads=[b_Sslot[sl], b_ident], writes=[psb[bkw]] if j == 0 else [], wadd=[psb[bkw]] if j > 0 else [])
                dstw = WinL[:, gc, 2 * q4:2 * q4 + 2].rearrange("p k r c -> p (k r c)")
                evac_copy(q4, dstw, ps[bkw][:, 0:512], [psb[bkw]], [], wadd=[b_WinL[gc]])

        t1p = cv_(O_S, [2, 16, 16], F32); t2p = cv_(O_S + 2048, [2, 16, 16], F32); cbp = cv_(O_S + 4096, [2, 16, 16], F32)
        b_p1 = Buf("p1tmp")
        handoff([b_p1], b_Sslot)
        b_Hend = Buf("Hend")

        def x_matmuls(gc, banks):
            for ri in range(2):
                for s_ in range(8):
                    for m in range(4):
                        first = (ri == 0 and s_ == 0)
                        last = (ri == 1 and s_ == 7)
                        bkx = banks[m]
                        P.op(PE, lambda e, m=m, ri=ri, s_=s_, bkx=bkx, gc=gc: e.matmul(
                            ps[bkx][:, 256 * ri:256 * ri + 256], lhsT=WinL[32 * m:32 * m + 32, gc, 7 - s_, ri, :],
                            rhs=uT[32 * m:32 * m + 32, gc, s_:NP:8], start=(s_ == 0), stop=(s_ == 7),
                            tile_position=(32 * m, 0)),
                            reads=[b_WinL[gc], uT_b[gc]] if first else [], writes=[psb[bkx]] if first else [], sig=last)
                        if last and not P.dead:
                            P.attach(Dep(P.esem[PE], P.cnt[PE]), reads=[b_WinL[gc], uT_b[gc]], writes=[psb[bkx]])

        def seg_reduce(src_ap, Rtab, gp0, ngp, out_ap, rbufs, wbuf):
            raise NotImplementedError

        def pass1():
            for gc in range(8):
                banks = [bank() for _ in range(4)]
                x_matmuls(gc, banks)
                if DEBUG and gc == 0 and os.environ.get('KX0'):
                    xdbg = cv_(O_S + 6144, [512], F32); b_xdbg = Buf()
                    P.op(DVE, lambda e: e.tensor_copy(out=xdbg, in_=ps[banks[1]][:, :]), reads=[psb[banks[1]]], writes=[b_xdbg])
                    P.dma(SP, dout_sem, dbg["X0"], xdbg, reads=[b_xdbg])
                for m in range(4):
                    gp = 4 * gc + m
                    X4 = ps[banks[m]][:, :].rearrange("p (r s i) -> p r s i", r=2, s=16)
                    Pr = R8[:, :, 0, gp].unsqueeze(1).unsqueeze(1).to_broadcast([128, 2, 16, 16])
                    Pi = R8[:, :, 1, gp].unsqueeze(1).unsqueeze(1).to_broadcast([128, 2, 16, 16])
                    P.op(DVE, lambda e, X4=X4, Pr=Pr: e.tensor_tensor(out=t1p, in0=X4, in1=Pr, op=ALU.mult),
                         reads=[psb[banks[m]], b_R8], writes=[b_p1])
                    P.op(DVE, lambda e, X4=X4, Pi=Pi: e.tensor_tensor(out=t2p, in0=X4, in1=Pi, op=ALU.mult),
                         reads=[psb[banks[m]], b_R8], wadd=[b_p1])
                    P.op(DVE, lambda e: e.tensor_tensor(out=cbp[:, 0], in0=t1p[:, 0], in1=t2p[:, 1], op=ALU.subtract), reads=[b_p1], wadd=[b_p1])
                    P.op(DVE, lambda e: e.tensor_tensor(out=cbp[:, 1], in0=t2p[:, 0], in1=t1p[:, 1], op=ALU.add), reads=[b_p1], wadd=[b_p1])
                    P.op(DVE, lambda e, gp=gp: e.tensor_reduce(out=Hend[:, :, gp, :], in_=cbp, axis=AX.X, op=ALU.add),
                         reads=[b_p1], wadd=[b_Hend])


        Ecore = cv_(O_S + 6144, [2, 32], F32); Eall = cv_(O_S + 6400, [8, 64], F32)
        te1 = cv_(O_S + 0, [2, 32, 16], F32); te2 = cv_(O_S + 8448, [2, 32, 16], F32)
        b_E = Buf("Ecore"); b_Eall = Buf("Eall"); b_te = b_p1
        handoff([b_E, b_Eall], b_Sslot)
        def ecore(j):
            Qr = R128[:, :, 0, :].rearrange("p i g -> p g i").unsqueeze(1).to_broadcast([128, 2, 32, 16])
            Qi = R128[:, :, 1, :].rearrange("p i g -> p g i").unsqueeze(1).to_broadcast([128, 2, 32, 16])
            P.op(DVE, lambda e: e.tensor_tensor(out=te1, in0=Hend, in1=Qr, op=ALU.mult), reads=[b_Hend, b_R128], writes=[b_te])
            P.op(DVE, lambda e: e.tensor_tensor(out=te2, in0=Hend, in1=Qi, op=ALU.mult), reads=[b_Hend, b_R128], wadd=[b_te])
            P.op(DVE, lambda e: e.tensor_tensor(out=te1[:, 0], in0=te1[:, 0], in1=te2[:, 1], op=ALU.subtract), reads=[b_te], writes=[b_te])
            P.op(DVE, lambda e: e.tensor_tensor(out=te2[:, 0], in0=te2[:, 0], in1=te1[:, 1], op=ALU.add), reads=[b_te], writes=[b_te])
            P.op(DVE, lambda e: e.tensor_reduce(out=Ecore[:, 0, :], in_=te1[:, 0], axis=AX.X, op=ALU.add), reads=[b_te], wadd=[b_E])
            P.op(DVE, lambda e: e.tensor_reduce(out=Ecore[:, 1, :], in_=te2[:, 0], axis=AX.X, op=ALU.add), reads=[b_te], wadd=[b_E])

            if j is not None:
                P.op(DVE, lambda e, j=j: e.tensor_copy(out=Eall[:, j, :], in_=Ecore.rearrange("p r g -> p (r g)")), reads=[b_E], wadd=[b_Eall])

        P.phase(3)
        srcs = [(xprev[1], False), (xprev[2], False), (xp, True)]
        phase_a(*srcs[0])
        for j in range(3):
            pass1()
            ecore(j)
            phase_b(srcs[j][1])
            if j < 2:
                phase_a(*srcs[j + 1])
        P.phase(6)
        pass1()
        P.phase(7)
        Sn = [cv_(O_S + 12544 + 256 * n, [2, 32], F32) for n in range(3)]
        b_Sn = Buf("Sn")
        handoff([b_Sn], b_Sslot)
        for n in range(3):
            Snf = Sn[n].rearrange("p r g -> p (r g)")
            P.op(DVE, lambda e, n=n, Snf=Snf: e.tensor_scalar(out=Snf, in0=Eall[:, n, :], scalar1=flags[:, 1 + n:2 + n], scalar2=None,
                                                            op0=ALU.mult), reads=[b_Eall, b_flags], wadd=[b_Sn])
        b_carry = Buf("carry")
        tq1 = cv_(O_S + 13312, [2, 32], F32); tq2 = cv_(O_S + 13568, [2, 32], F32)
        b_tq = Buf("tq")
        handoff([b_tq], b_Sslot)

        def cmul_small(dst, x, y, rb, wb):
            cmul(DVE, dst[:, 0, :], dst[:, 1, :], x[:, 0, :], x[:, 1, :], y[:, 0, :], y[:, 1, :], tq1[:, 0, :], tq1[:, 1, :], rb, wb, b_tq)

        cmul_small(carry[:, 1], Sn[1], A2k[:, 1], [b_Sn, b_A2k], [b_carry])
        cmul_small(carry[:, 2], Sn[2], A2k[:, 2], [b_Sn, b_A2k, b_carry], [b_carry])
        P.op(DVE, lambda e: e.tensor_tensor(out=Sn[0], in0=Sn[0], in1=carry[:, 1], op=ALU.add), reads=[b_Sn, b_carry], writes=[b_Sn])
        P.op(DVE, lambda e: e.tensor_tensor(out=carry[:, 0], in0=Sn[0], in1=carry[:, 2], op=ALU.add), reads=[b_Sn, b_carry], writes=[b_carry])
        A128v = coef[:, 0:2, :]
        for sg in range(16):
            cmul_small(carry[:, sg + 1], carry[:, sg], A128v, [b_carry, b_coef], [b_carry])
            P.op(DVE, lambda e, sg=sg: e.tensor_tensor(out=carry[:, sg + 1], in0=carry[:, sg + 1], in1=Hend[:, :, :, sg], op=ALU.add),
                 reads=[b_carry, b_Hend], writes=[b_carry])
        pstT = cv_(O_S + 13824, [2, 128], F32)
        b_pst = Buf()
        handoff([b_pst], b_Sslot)
        bkp = bank()
        for ri in range(2):
            P.op(PE, lambda e, ri=ri: e.transpose(out=ps[bkp][0:32, 128 * ri:128 * ri + 128], in_=carry[:, 16, ri, :], identity=ident),
                 reads=[b_carry, b_ident], writes=[psb[bkp]] if ri == 0 else [], wadd=[psb[bkp]] if ri == 1 else [])
        P.op(DVE, lambda e: e.tensor_copy(out=pstT[0:32].rearrange("p r c -> p (r c)"), in_=ps[bkp][0:32, 0:256]), reads=[psb[bkp]], writes=[b_pst])
        P.dma(SP, osem_new("pre"), pst_re, pstT[0:32, 0, :], reads=[b_pst])
        P.dma(SP, osem_new("pim"), pst_im, pstT[0:32, 1, :], reads=[b_pst])

        P.phase(8)
        CaBD = [cv_(O_CA + 4608 * i, [4, 9, 2, 32], BF16) for i in range(2)]
        KLs = [cv_(O_KL + 2048 * i, [8, 128], BF16) for i in range(2)]
        Hb = [cv_(O_HB + 4096 * i, [2, 4, 256], BF16) for i in range(2)]
        Xs = [cv_(O_S + 8192 * i, [2, 4, 256], F32) for i in range(2)]
        b_CaBD = [Buf("Ca0"), Buf("Ca1")]; b_KL = [Buf("KL0"), Buf("KL1")]; b_Hb = [Buf("Hb0"), Buf("Hb1")]; b_Xs = [Buf("Xs0"), Buf("Xs1")]
        old_c = [b_ct, b_csg, b_ccT, b_bin, b_baT] + xst_b
        handoff(b_CaBD + b_KL + b_Hb, old_c)
        handoff(b_Xs, [b_p1, b_E, b_Eall, b_te, b_Sn, b_tq, b_pst] + b_Sslot)
        KL0all = cv_(O_C + 10496, [8, 128], BF16); Ca1all = cv_(O_C + 14592, [32, 2, 32], BF16)
        b_KL0 = Buf("KL0all"); b_Ca1 = Buf("Ca1all")
        handoff([b_KL0], [b_R8]); handoff([b_Ca1], [b_R128])
        Q1e = [misc[:, 24:28, :].rearrange("p a g -> p (a g)").rearrange("p (r m s) -> p r m s", r=2, m=4),
               misc[:, 0:4, :].rearrange("p a g -> p (a g)").rearrange("p (r m s) -> p r m s", r=2, m=4)]
        Q2e = [misc[:, 28:32, :].rearrange("p a g -> p (a g)").rearrange("p (r m s) -> p r m s", r=2, m=4),
               misc[:, 4:8, :].rearrange("p a g -> p (a g)").rearrange("p (r m s) -> p r m s", r=2, m=4)]
        tmpK = misc[:, 16:20, :].rearrange("p a g -> p (a g)")
        b_Q = [Buf("Qdve"), Buf("Qpool")]
        b_tmpK = Buf("tmpK")
        handoff([b_tmpK] + b_Q, [b_misc])
        b_CaInit = [Buf("CaInit0"), Buf("CaInit1")]
        for i in range(2):
            P.op(POOL, lambda e, i=i: e.memset(CaBD[i].rearrange("p m n r c -> p (m n r c)"), 0.0), writes=[b_CaBD[i], b_CaInit[i]])
        U1 = t1[:, 0:576].rearrange("p (m n c) -> p m n c", m=4, n=9)
        U2 = t2[:, 0:576].rearrange("p (m n c) -> p m n c", m=4, n=9)
        XB = [2, 3, 4, 5]
        YB = [6, 7]

        def emit_consts(gc):
            sl = gc % 2
            Ca = CaBD[sl]
            cre = ccm[:, 0, 4 * gc:4 * gc + 4, :].unsqueeze(2).to_broadcast([128, 4, 9, 16])
            cim = ccm[:, 1, 4 * gc:4 * gc + 4, :].unsqueeze(2).to_broadcast([128, 4, 9, 16])
            pr = pw[:, :, 0, 4 * gc:4 * gc + 4].rearrange("p n m -> p m n").unsqueeze(3).to_broadcast([128, 4, 9, 16])
            pi = pw[:, :, 1, 4 * gc:4 * gc + 4].rearrange("p n m -> p m n").unsqueeze(3).to_broadcast([128, 4, 9, 16])
            P.op(DVE, lambda e: e.tensor_tensor(out=U1, in0=cre, in1=pr, op=ALU.mult), reads=[b_ccm, b_pw], writes=[b_t])
            P.op(DVE, lambda e: e.tensor_tensor(out=U2, in0=cim, in1=pi, op=ALU.mult), reads=[b_ccm, b_pw], wadd=[b_t])
            for g2 in range(2):
                lo, hi = 64 * g2, 64 * g2 + 64
                P.op(DVE, lambda e, lo=lo, hi=hi, g2=g2: e.tensor_tensor(
                    out=Ca[lo:hi, :, :, 0, 16 * g2:16 * g2 + 16], in0=U1[lo:hi], in1=U2[lo:hi], op=ALU.subtract),
                    reads=[b_t, b_CaInit[sl]], wadd=[b_CaBD[sl]])
            P.op(DVE, lambda e: e.tensor_tensor(out=U1, in0=cre, in1=pi, op=ALU.mult), reads=[b_ccm, b_pw], writes=[b_t])
            P.op(DVE, lambda e: e.tensor_tensor(out=U2, in0=cim, in1=pr, op=ALU.mult), reads=[b_ccm, b_pw], wadd=[b_t])
            P.op(DVE, lambda e: e.tensor_tensor(out=U1, in0=U1, in1=U2, op=ALU.add), reads=[b_t], writes=[b_t])
            for g2 in range(2):
                lo, hi = 64 * g2, 64 * g2 + 64
                P.op(DVE, lambda e, lo=lo, hi=hi, g2=g2: e.tensor_scalar(
                    out=Ca[lo:hi, :, :, 1, 16 * g2:16 * g2 + 16], in0=U1[lo:hi], scalar1=-1.0, scalar2=0.0, op0=ALU.mult, op1=ALU.add),
                    reads=[b_t, b_CaInit[sl]], wadd=[b_CaBD[sl]])
            P.op(DVE, lambda e: e.tensor_copy(out=Ca1all[:, 4 * gc:4 * gc + 4], in_=Ca[:, :, 1, :, :]), reads=[b_CaBD[sl]], wadd=[b_Ca1])
            for hb in range(2):
                for tt in range(4):
                    tau = 4 * hb + tt
                    for ri in range(2):
                        P.op(PE, lambda e, hb=hb, tt=tt, tau=tau, ri=ri: e.matmul(
                            ps[hb][:, 128 * tt:128 * tt + 128], lhsT=Sb[:, gc, ri, :], rhs=Ca[:, :, tau, ri, :],
                            start=(ri == 0), stop=(ri == 1)),
                            reads=[b_Sb, b_CaBD[sl]] if (tt == 0 and ri == 0) else [],
                            writes=[psb[hb]] if (tt == 0 and ri == 0) else [], sig=(tt == 3 and ri == 1))
                if not P.dead:
                    P.attach(Dep(P.esem[PE], P.cnt[PE]), reads=[b_Sb, b_CaBD[sl]], writes=[psb[hb]])
            KL = KLs[sl]
            bmb3 = bmask.unsqueeze(1).to_broadcast([128, 3, 128]); bmb4 = bmask.unsqueeze(1).to_broadcast([128, 4, 128])
            P.op(DVE, lambda e: e.tensor_tensor(out=KL[:, 1:4, :], in0=ps[0][:, 128:512].rearrange("p (t c) -> p t c", t=3), in1=bmb3, op=ALU.mult),
                 reads=[psb[0], b_bmask], wadd=[b_KL[sl]])
            P.op(DVE, lambda e: e.tensor_tensor(out=tmpK, in0=ps[0][:, 0:128], in1=bmask, op=ALU.mult), reads=[psb[0], b_bmask], writes=[b_tmpK])
            P.op(DVE, lambda e: e.tensor_tensor(out=KL[:, 4:8, :], in0=ps[1][:, 0:512].rearrange("p (t c) -> p t c", t=4), in1=bmb4, op=ALU.mult),
                 reads=[psb[1], b_bmask], wadd=[b_KL[sl]])
            P.op(DVE, lambda e: e.scalar_tensor_tensor(out=KL[:, 0, :], in0=ident, scalar=Dm[:, gc:gc + 1], in1=tmpK, op0=ALU.mult, op1=ALU.add),
                 reads=[b_tmpK, b_ident, b_Dm], wadd=[b_KL[sl]])
            P.op(DVE, lambda e: e.tensor_copy(out=KL0all[:, gc, :], in_=KL[:, 0, :]), reads=[b_KL[sl]], wadd=[b_KL0])

        def emit_x_scan(gc):
            sl = gc % 2
            x_matmuls(gc, XB)
            X = Xs[sl]
            for m in range(4):
                P.op(ACT, lambda e, m=m: e.activation(out=X[:, :, m, :], in_=ps[XB[m]][:, :].rearrange("p (r j) -> p r j", r=2), func=AF.Copy),
                     reads=[psb[XB[m]]], wadd=[b_Xs[sl]])
            E, qi = (POOL, 1) if gc in (1, 4, 6) else (DVE, 0)
            Q1 = Q1e[qi]; Q2 = Q2e[qi]
            X5 = X.rearrange("p r m (s i) -> p r m s i", i=16)
            Ar = pw[:, 8, 0, 4 * gc:4 * gc + 4].unsqueeze(1).unsqueeze(3).to_broadcast([128, 2, 4, 16])
            Ai = pw[:, 8, 1, 4 * gc:4 * gc + 4].unsqueeze(2).to_broadcast([128, 4, 16])
            AiN = coef[:, 4, 4 * gc:4 * gc + 4].unsqueeze(2).to_broadcast([128, 4, 16])
            cview = carry[:, 0:16, :, 4 * gc:4 * gc + 4].rearrange("p s r m -> p r m s")
            for i in range(16):
                prev = cview if i == 0 else X5[:, :, :, :, i - 1]
                cur = X5[:, :, :, :, i]
                rb = [b_carry, b_pw, b_coef, b_Xs[sl]]
                P.op(E, lambda e, prev=prev: e.tensor_tensor(out=Q1, in0=prev, in1=Ar, op=ALU.mult), reads=rb, writes=[b_Q[qi]])
                P.op(E, lambda e, prev=prev: e.tensor_tensor(out=Q2[:, 0], in0=prev[:, 1], in1=AiN, op=ALU.mult), reads=rb, wadd=[b_Q[qi]])
                P.op(E, lambda e, prev=prev: e.tensor_tensor(out=Q2[:, 1], in0=prev[:, 0], in1=Ai, op=ALU.mult), reads=rb, wadd=[b_Q[qi]])
                P.op(E, lambda e, cur=cur: e.tensor_tensor(out=cur, in0=cur, in1=Q1, op=ALU.add), reads=[b_Q[qi]], writes=[b_Xs[sl]])
                P.op(E, lambda e, cur=cur: e.tensor_tensor(out=cur, in0=cur, in1=Q2, op=ALU.add), reads=[b_Q[qi]], writes=[b_Xs[sl]])

        def emit_hb(gc):
            sl = gc % 2
            X = Xs[sl]
            X5 = X.rearrange("p r m (s i) -> p r m s i", i=16)
            cview = carry[:, 0:16, :, 4 * gc:4 * gc + 4].rearrange("p s r m -> p r m s")
            H5 = Hb[sl].rearrange("p r m (s i) -> p r m s i", i=16)
            P.op(ACT, lambda e: e.activation(out=H5[:, :, :, :, 1:16].rearrange("p r m s i -> p (r m) s i"),
                                             in_=X5[:, :, :, :, 0:15].rearrange("p r m s i -> p (r m) s i"), func=AF.Copy),
                 reads=[b_Xs[sl]], writes=[b_Hb[sl]])
            P.op(ACT, lambda e: e.activation(out=H5[:, :, :, :, 0], in_=cview, func=AF.Copy), reads=[b_carry], wadd=[b_Hb[sl]])

        def emit_y(gc):
            sl = gc % 2
            KL = KLs[sl]; Ca = CaBD[sl]
            uview = uT[:, gc, 0:NP].rearrange("p (j s) -> p s j", s=8)
            for half in (1, 0):
                for tl in range(4):
                    t_lo = 4 * half + tl
                    bk_ = YB[tl // 2]
                    reg = ps[bk_][:, 256 * (tl % 2):256 * (tl % 2) + 256]
                    n_mm = (t_lo + 1) + 8
                    idx = 0
                    for s_ in range(t_lo + 1):
                        P.op(PE, lambda e, reg=reg, s_=s_, t_lo=t_lo: e.matmul(
                            reg, lhsT=KL[:, t_lo - s_, :], rhs=uview[:, s_, :], start=(s_ == 0), stop=False),
                            reads=[b_KL[sl], uT_b[gc], b_CaBD[sl], b_Hb[sl]] if idx == 0 else [],
                            writes=[psb[bk_]] if (idx == 0 and tl % 2 == 0) else [], sig=False)
                        idx += 1
                    for m in range(4):
                        for ri in range(2):
                            lastmm = (m == 3 and ri == 1)
                            P.op(PE, lambda e, reg=reg, m=m, ri=ri, t_lo=t_lo, lastmm=lastmm: e.matmul(
                                reg[32 * m:32 * m + 32, :], lhsT=Ca[:, m, t_lo + 1, ri, :], rhs=Hb[sl][:, ri, m, :],
                                start=False, stop=(ri == 1), tile_position=(0, 32 * m)), sig=lastmm)
                    if not P.dead:
                        P.attach(Dep(P.esem[PE], P.cnt[PE]), reads=[b_KL[sl], uT_b[gc], b_CaBD[sl], b_Hb[sl]],
                                 writes=[psb[bk_]] if tl % 2 == 1 else [], wadd=[psb[bk_]] if tl % 2 == 0 else [])
                for bi in range(2):
                    t0_ = 4 * half + 2 * bi
                    P.op(ACT, lambda e, bi=bi, t0_=t0_: e.activation(
                        out=uview[:, t0_:t0_ + 2, :], in_=ps[YB[bi]][:, :].rearrange("p (t j) -> p t j", t=2), func=AF.Gelu_apprx_tanh),
                        reads=[psb[YB[bi]]], writes=[uT_b[gc]])

        for g0 in range(2):
            emit_consts(g0)
            emit_x_scan(g0)
            emit_hb(g0)
        for gc in range(8):
            if gc + 2 < 8:
                emit_x_scan(gc + 2)
            emit_y(gc)
            if gc + 2 < 8:
                emit_consts(gc + 2)
                emit_hb(gc + 2)
        yT = uT
        yT_b = uT_b

        P.phase(9)
        all_ssm_tmp = b_Xs + b_Hb + b_Q + [b_tmpK, b_t, b_p1, b_E, b_Eall, b_te, b_Sn, b_tq, b_pst] + b_Sslot
        stile = [cv_(O_S + 2048 * i, [512], F32) for i in range(2)]
        Hsp = cv_(O_S + 4096, [2, 32, 16], F32); Hn = cv_(O_S + 8192, [2, 32, 16], F32)
        HbS = cv_(O_S + 12288, [2, 32, 16], BF16); Q1s = cv_(O_HB, [2, 32, 16], F32); Q2s = cv_(O_HB + 4096, [2, 32, 16], F32)
        b_stile = Buf(); b_Hsp = Buf(); b_Hn = Buf(); b_HbS = Buf(); b_Qs = Buf()
        handoff([b_stile, b_Hsp, b_Hn, b_HbS, b_Qs], all_ssm_tmp)
        misc_load(SP, stile[0], st_re.rearrange("s (gh f) -> (s gh) f", gh=8), b_stile, wadd=True)
        misc_load(SP, stile[1], st_im.rearrange("s (gh f) -> (s gh) f", gh=8), b_stile, wadd=True)
        for ri in range(2):
            bks = bank()
            for q4 in range(4):
                P.op(PE, lambda e, ri=ri, q4=q4, bks=bks: e.transpose(out=ps[bks][:, 128 * q4:128 * q4 + 128],
                                                                   in_=stile[ri][:, 128 * q4:128 * q4 + 128], identity=ident),
                     reads=[b_stile, b_ident], writes=[psb[bks]] if q4 == 0 else [], wadd=[psb[bks]] if q4 > 0 else [])
            for q4 in range(4):
                P.op(DVE, lambda e, ri=ri, q4=q4, bks=bks: e.tensor_copy(
                    out=Hsp[:, ri, q4:32:4, :], in_=ps[bks][:, 128 * q4:128 * q4 + 128].rearrange("p (s gh) -> p gh s", gh=8)),
                    reads=[psb[bks]], wadd=[b_Hsp])
        P.op(POOL, lambda e: e.tensor_copy(out=HbS, in_=Hsp), reads=[b_Hsp], writes=[b_HbS])
        xsb = [bank() for _ in range(4)]
        for m in range(4):
            for gc in range(8):
                for ri in range(2):
                    first = (gc == 0 and ri == 0); last = (gc == 7 and ri == 1)
                    P.op(PE, lambda e, m=m, gc=gc, ri=ri: e.matmul(
                        ps[xsb[m]][:, 32 * gc + 16 * ri:32 * gc + 16 * ri + 16], lhsT=WinL[32 * m:32 * m + 32, gc, 0, ri, :],
                        rhs=uT[32 * m:32 * m + 32, gc, NP:NT], start=True, stop=True, tile_position=(32 * m, 0)),
                        reads=b_WinL + uT_b if first else [], writes=[psb[xsb[m]]] if first else [], sig=last)
            if not P.dead:
                P.attach(Dep(P.esem[PE], P.cnt[PE]), reads=b_WinL + uT_b, writes=[psb[xsb[m]]])
        Ar1 = pw[:, 1, 0, :].unsqueeze(1).unsqueeze(3).to_broadcast([128, 2, 32, 16])
        Ai1 = pw[:, 1, 1, :].unsqueeze(2).to_broadcast([128, 32, 16])
        P.op(POOL, lambda e: e.tensor_scalar(out=coef[:, 6, :], in0=pw[:, 1, 1, :], scalar1=-1.0, scalar2=0.0, op0=ALU.mult, op1=ALU.add),
             reads=[b_pw], wadd=[b_coef])
        AiN1 = coef[:, 6, :].unsqueeze(2).to_broadcast([128, 32, 16])
        P.op(DVE, lambda e: e.tensor_tensor(out=Q1s, in0=Hsp, in1=Ar1, op=ALU.mult), reads=[b_Hsp, b_pw], writes=[b_Qs])
        P.op(DVE, lambda e: e.tensor_tensor(out=Q2s[:, 0], in0=Hsp[:, 1], in1=AiN1, op=ALU.mult), reads=[b_Hsp, b_coef], wadd=[b_Qs])
        P.op(DVE, lambda e: e.tensor_tensor(out=Q2s[:, 1], in0=Hsp[:, 0], in1=Ai1, op=ALU.mult), reads=[b_Hsp, b_pw], wadd=[b_Qs])
        P.op(DVE, lambda e: e.tensor_tensor(out=Hn, in0=Q1s, in1=Q2s, op=ALU.add), reads=[b_Qs], writes=[b_Hn])
        for m in range(4):
            P.op(DVE, lambda e, m=m: e.tensor_tensor(
                out=Hn[:, :, m:32:4, :], in0=ps[xsb[m]][:, 0:256].rearrange("p (gc r s) -> p r gc s", gc=8, r=2),
                in1=Hn[:, :, m:32:4, :], op=ALU.add), reads=[psb[xsb[m]], b_Hn], writes=[b_Hn])
        stg = cv_(O_HB + 8192 - 8192, [4, 128], F32)
        sout = [cv_(O_S + 2048 * i, [512], F32) for i in range(2)]
        b_stg = Buf(); b_sout = Buf()
        handoff([b_stg], [b_Qs]); handoff([b_sout], [b_stile])
        for ri in range(2):
            for q4 in range(4):
                P.op(POOL, lambda e, ri=ri, q4=q4: e.tensor_copy(out=stg[:, q4, :].rearrange("p (s gh) -> p gh s", gh=8),
                                                               in_=Hn[:, ri, q4:32:4, :]), reads=[b_Hn], writes=[b_stg] if q4 == 0 else [],
                     wadd=[b_stg] if q4 > 0 else [])
            bks = bank()
            for q4 in range(4):
                P.op(PE, lambda e, q4=q4, bks=bks: e.transpose(out=ps[bks][:, 128 * q4:128 * q4 + 128], in_=stg[:, q4, :], identity=ident),
                     reads=[b_stg, b_ident], writes=[psb[bks]] if q4 == 0 else [], wadd=[psb[bks]] if q4 > 0 else [])
            P.op(DVE, lambda e, ri=ri, bks=bks: e.tensor_copy(out=sout[ri], in_=ps[bks][:, 0:512]), reads=[psb[bks]], wadd=[b_sout])
            P.dma(SP, osem_new(f"sst{ri}"), (sst_re if ri == 0 else sst_im).rearrange("s (gh f) -> (s gh) f", gh=8), sout[ri], reads=[b_sout])
        bky = bank()
        for gc in range(8):
            reg = ps[bky][:, 16 * gc:16 * gc + 16]
            P.op(PE, lambda e, gc=gc, reg=reg: e.matmul(reg, lhsT=KL0all[:, gc, :], rhs=uT[:, gc, NP:NT], start=True, stop=False),
                 reads=[b_KL0, b_Ca1, b_HbS] + uT_b if gc == 0 else [], writes=[psb[bky]] if gc == 0 else [], sig=False)
            for m in range(4):
                for ri in range(2):
                    lastmm = (m == 3 and ri == 1)
                    P.op(PE, lambda e, gc=gc, reg=reg, m=m, ri=ri, lastmm=lastmm: e.matmul(
                        reg[32 * m:32 * m + 32, :], lhsT=Ca1all[:, 4 * gc + m, ri, :], rhs=HbS[:, ri, 4 * gc + m, :],
                        start=False, stop=(ri == 1), tile_position=(0, 32 * m)), sig=(lastmm and gc == 7))
        if not P.dead:
            P.attach(Dep(P.esem[PE], P.cnt[PE]), reads=[b_KL0, b_Ca1, b_HbS] + uT_b, writes=[psb[bky]])
        P.op(ACT, lambda e: e.activation(out=uT[:, :, NP:NT], in_=ps[bky][:, 0:128].rearrange("p (g s) -> p g s", g=8),
                                         func=AF.Gelu_apprx_tanh), reads=[psb[bky]], writes=uT_b)

        P.phase(10)
        s2T = RB
        s2_b = [Buf(f"s2_{g}") for g in range(8)]
        handoff(s2_b, b_WinL)
        gtmp = [cv_(O_C + 1024 * i, [512], BF16) for i in range(4)]
        ftmp = [cv_(O_C + 4096 + 2048 * i, [512], F32) for i in range(2)]
        b_gtmp = [Buf() for _ in range(4)]; b_ftmp = [Buf(), Buf()]
        handoff(b_gtmp + b_ftmp, [b_pw, b_bb, b_ccm])
        rhs_y = lambda k, t0, n: yT[:, k, t0:t0 + n]
        yall_b = [yT_b] * 5
        ctr = {"i": 0}

        class AllOf:
            pass
        for blk in range(2):
            def ev_glu(oc, tbi, pap, pb, blk=blk):
                t0, n = TBS[tbi]
                g = 4 * blk + oc
                ctr["i"] += 1
                gi = ctr["i"] % 4
                P.op(ACT, lambda e: e.activation(out=gtmp[gi][:, 0:n], in_=pap, func=AF.Sigmoid, bias=bglu[:, g:g + 1], scale=1.0),
                     reads=[pb, b_bglu], writes=[b_gtmp[gi]])
                P.op(DVE, lambda e: e.tensor_tensor(out=s2T[:, g, t0:t0 + n], in0=yT[:, g, t0:t0 + n], in1=gtmp[gi][:, 0:n], op=ALU.mult),
                     reads=[b_gtmp[gi], yT_b[g]], wadd=[s2_b[g]])
            proj_fm(w_glu[:, 512 * blk:512 * blk + 512], 512, rhs_y, [BufGroup(yT_b)] * 5, ev_glu)

            def ev_zs(oc, tbi, pap, pb, blk=blk):
                t0, n = TBS[tbi]
                g = 4 * blk + oc
                ctr["i"] += 1
                gi = ctr["i"] % 4
                fi = ctr["i"] % 2
                P.op(ACT, lambda e: e.activation(out=gtmp[gi][:, 0:n], in_=pap, func=AF.Sigmoid), reads=[pb], writes=[b_gtmp[gi]])
                P.op(DVE, lambda e: e.tensor_tensor(out=ftmp[fi][:, 0:n], in0=pap, in1=gtmp[gi][:, 0:n], op=ALU.mult),
                     reads=[pb, b_gtmp[gi]], writes=[b_ftmp[fi]])
                P.op(DVE, lambda e: e.tensor_tensor(out=s2T[:, g, t0:t0 + n], in0=s2T[:, g, t0:t0 + n], in1=ftmp[fi][:, 0:n], op=ALU.mult),
                     reads=[b_ftmp[fi], s2_b[g]], wadd=[s2_b[g]])
            proj_fm(w_in[:, OFF_ZS + 512 * blk:OFF_ZS + 512 * blk + 512], 512, rhs_h, hT_b, ev_zs)

        P.phase(11)
        gbs = RA
        gbs_b = [Buf(f"gbs{g}") for g in range(8)]
        handoff(gbs_b, yT_b)
        rhs_s2 = lambda k, t0, n: s2T[:, k, t0:t0 + n]
        for blk in range(2):
            def ev_gs(oc, tbi, pap, pb, blk=blk):
                t0, n = TBS[tbi]
                g = 4 * blk + oc
                P.op(ACT, lambda e: e.activation(out=gbs[:, g, t0:t0 + n], in_=pap, func=AF.Sigmoid), reads=[pb], wadd=[gbs_b[g]])
            proj_fm(w_in[:, OFF_GS + 512 * blk:OFF_GS + 512 * blk + 512], 512, rhs_h, hT_b, ev_gs)

            def ev_bs(oc, tbi, pap, pb, blk=blk):
                t0, n = TBS[tbi]
                g = 4 * blk + oc
                P.op(DVE, lambda e: e.tensor_tensor(out=gbs[:, g, t0:t0 + n], in0=pap, in1=gbs[:, g, t0:t0 + n], op=ALU.mult),
                     reads=[pb, gbs_b[g]], wadd=[gbs_b[g]])
            proj_fm(w_bs[:, 512 * blk:512 * blk + 512], 512, rhs_s2, [BufGroup(s2_b)] * 5, ev_bs)

        P.phase(12)
        oT = RB
        oT_b = [Buf(f"oT{g}") for g in range(8)]
        handoff(oT_b, s2_b)
        oc_ = O_C
        qT = cv_(oc_ + 0, [2, NT], BF16); kT2 = cv_(oc_ + 8256, [128 + NT], BF16); Vaug = cv_(oc_ + 12640, [18, 128], BF16)
        EB = cv_(oc_ + 17248, [2, 16, 128], BF16); EB0 = cv_(oc_ + 25440, [16, 128], BF16)
        Et = [cv_(oc_ + 29536 + 1024 * i, [512], BF16) for i in range(4)]
        PT = [cv_(oc_ + 33632 + 1024 * i, [512], BF16) for i in range(4)]
        rc = [cv_(oc_ + 37728 + 2048 * i, [512], F32) for i in range(2)]
        maskt = cv_(oc_ + 41824, [2, 128], F32); RT = cv_(oc_ + 42848, [384], F32); relb = cv_(oc_ + 44384, [16], F32)
        es16 = cv_(oc_ + 44448, [16], F32); klast = cv_(oc_ + 44512, [256], F32); vlast = cv_(oc_ + 45536, [256], F32)
        knew = cv_(oc_ + 46560, [256], F32); vnew = cv_(oc_ + 47584, [256], F32)
        Kc = cv_(oc_ + 48608, [16, 256], F32)
        KcT = cv_(oc_ + 64992, [16, 2, 128], BF16)
        Vcs = cv_(oc_ + 73184, [16, 256], BF16)
        QsT = cv_(oc_ + 81376, [2, 4, 16], BF16)
        dgt = cv_(oc_ + 81632, [64], F32); vnb = cv_(oc_ + 81888, [256], BF16); pdg = cv_(oc_ + 82400, [64], BF16)
        esr = cv_(oc_ + 82528, [64], BF16); ebs = cv_(oc_ + 82656, [16], F32); rcs = cv_(oc_ + 82720, [128], F32)
        ptS = cv_(oc_ + 83232, [128], BF16); ones_k = cv_(oc_ + 83488, [128], BF16)
        attn_bufs = {n: Buf(n) for n in ["qT", "kT2", "Vaug", "EB", "EB0", "mask", "RT", "relb", "es16", "klast", "vlast", "knew", "vnew",
                                         "Kc", "KcT", "Vcs", "QsT", "dgt", "vnb", "pdg", "esr", "ebs", "rcs", "ptS", "ones_k"]}
        A = attn_bufs
        b_Et = [Buf() for _ in range(4)]; b_PT = [Buf() for _ in range(4)]; b_rc = [Buf(), Buf()]
        prev_c = [b_pw, b_bb, b_ccm, b_R8, b_R128, b_A2k, b_Hend, b_carry, b_Sb, b_coef, b_misc, b_t, b_KL0, b_Ca1,
                  b_stile, b_Hsp, b_Hn, b_HbS, b_Qs, b_stg, b_sout] + b_gtmp + b_ftmp + all_ssm_tmp + b_CaBD + b_KL
        handoff(list(A.values()) + b_Et + b_PT + b_rc, prev_c)
        misc_load(SP, RT[0:32, :], rtab, A["RT"]); misc_load(SP, relb[0:32, :], rel_bias, A["relb"])
        misc_load(SP, maskt.rearrange("p h q -> p (h q)"), maskc, A["mask"])
        misc_load(SP, es16[0:1, :], sinks.rearrange("(o n) -> o n", o=1), A["es16"])
        misc_load(SP, ebs[0:16, :], rel_bias[0:1, :].to_broadcast([16, 16]), A["ebs"])
        misc_load(SP, dgt[0:16, :], diagc, A["dgt"])
        P.op(ACT, lambda e: e.activation(out=es16[0:1, :], in_=es16[0:1, :], func=AF.Exp), reads=[A["es16"]], writes=[A["es16"]])
        P.op(ACT, lambda e: e.activation(out=ebs[0:16, :], in_=ebs[0:16, :], func=AF.Exp), reads=[A["ebs"]], writes=[A["ebs"]])
        for kv in range(4):
            for sl_, i in enumerate([0, 2, 1, 3]):
                h = 4 * kv + i
                P.op(DVE, lambda e, kv=kv, sl_=sl_, h=h: e.tensor_copy(out=ES[0:1, kv, sl_, :], in_=es16[0:1, h:h + 1].to_broadcast([1, 128])),
                     reads=[A["es16"]], wadd=[b_ES])
        P.op(POOL, lambda e: e.memset(ones_k, 1.0), writes=[A["ones_k"]])
        for half in range(2):
            for qb in range(4):
                bke = bank()
                for qq in range(32):
                    q = 32 * qb + qq
                    st_ = (127 - q) if half == 0 else (255 - q)
                    P.op(PE, lambda e, bke=bke, qq=qq, st_=st_: e.matmul(ps[bke][:, 16 * qq:16 * qq + 16], lhsT=RT[0:32, st_:st_ + 128],
                                                                       rhs=relb[0:32, :], start=True, stop=True),
                         reads=[A["RT"], A["relb"]] if qq == 0 else [], writes=[psb[bke]] if qq == 0 else [], sig=(qq == 31))
                if not P.dead:
                    P.attach(Dep(P.esem[PE], P.cnt[PE]), reads=[A["RT"], A["relb"]], writes=[psb[bke]])
                P.op(ACT, lambda e, bke=bke, half=half, qb=qb: e.activation(
                    out=EB[:, half, :, 32 * qb:32 * qb + 32], in_=ps[bke][:, 0:512].rearrange("p (q h) -> p h q", h=16), func=AF.Exp),
                    reads=[psb[bke]], wadd=[A["EB"]])
        P.op(DVE, lambda e: e.tensor_tensor(out=EB, in0=EB, in1=maskt.unsqueeze(2).to_broadcast([128, 2, 16, 128]), op=ALU.mult),
             reads=[A["EB"], A["mask"]], writes=[A["EB"]])
        P.op(DVE, lambda e: e.tensor_scalar(out=EB0, in0=EB[:, 0], scalar1=flags[:, 0:1], scalar2=None, op0=ALU.mult),
             reads=[A["EB"], b_flags], writes=[A["EB0"]])
        P.dma(SP, dout_sem, sck[:, 0:127, :], ck[:, 1:128, :])
        P.dma(SP, dout_sem, scv[:, 0:127, :], cv[:, 1:128, :])
        kc_sem = P.dsem("kc")
        P.dma(SP, kc_sem, Kc, ck.rearrange("s t f -> t s f"), writes=[A["Kc"]])
        for s_ in range(NS):
            bkt = bank()
            for kvp in range(2):
                P.op(PE, lambda e, s_=s_, kvp=kvp, bkt=bkt: e.transpose(out=ps[bkt][:, 128 * kvp:128 * kvp + 128],
                                                                     in_=Kc[:, s_, 128 * kvp:128 * kvp + 128], identity=ident),
                     reads=[A["Kc"], b_ident], writes=[psb[bkt]] if kvp == 0 else [], wadd=[psb[bkt]] if kvp == 1 else [])
            evac_copy(s_, KcT[:, s_].rearrange("p a t -> p (a t)"), ps[bkt][:, 0:256], [psb[bkt]], [], wadd=[A["KcT"]])
        P.dma(SP, kc_sem, Kc, cv.rearrange("s t f -> t s f"), reads=[A["KcT"]], writes=[A["Kc"]])
        P.op(POOL, lambda e: e.tensor_copy(out=Vcs, in_=Kc), reads=[A["Kc"]], writes=[A["Vcs"]])
        P.op(POOL, lambda e: e.memset(Vaug[:, :, 64:128], 1.0), writes=[A["Vaug"]])

        rhs_hh = lambda k, t0, n: hTh[:, k, 0:n]
        TB5 = TBS
        def attn_kv(kv):
            def ev_q(oc, tbi, pap, pb):
                t0, n = TBS[tbi]
                ctr["i"] += 1
                evac_copy(ctr["i"], qT[:, oc, t0:t0 + n], pap, [pb], [], wadd=[A["qT"]])
            A["qT"].r = list(A["qT"].r) + list(A["qT"].w); A["qT"].w = []
            proj_fm(w_in[:, OFF_Q + 256 * kv:OFF_Q + 256 * kv + 256], 256, rhs_h, hT_b, ev_q)
            s = wctr[0] % 2
            wctr[0] += 1
            for dup in range(2):
                P.dma(POOL, wsem[s], wslot[s][:, :, 64 * dup:64 * dup + 64],
                      w_in[:, OFF_K + 64 * kv:OFF_K + 64 * kv + 64].rearrange("(k p) f -> p k f", p=128),
                      writes=[wslot_b[s]] if dup == 0 else [], wadd=[wslot_b[s]] if dup == 1 else [])
            A["kT2"].r = list(A["kT2"].r) + list(A["kT2"].w); A["kT2"].w = []
            kblocks = [(hTh, hTh_b, 0, 128, 0)] + [(hT, hT_b[i], t0, n, 128 + t0) for i, (t0, n) in enumerate(TBS)]
            for bi_, (src, sb_, t0, n, c0) in enumerate(kblocks):
                bkk = bank()
                for k in range(8):
                    P.op(PE, lambda e, bkk=bkk, k=k, src=src, t0=t0, n=n, s=s: e.matmul(
                        ps[bkk][:, 0:n], lhsT=wslot[s][:, k, 0:128], rhs=src[:, k, t0:t0 + n], start=(k == 0), stop=(k == 7)),
                        reads=[wslot_b[s], sb_] if k == 0 else [], writes=[psb[bkk]] if k == 0 else [], sig=(k == 7))
                if not P.dead:
                    P.attach(Dep(P.esem[PE], P.cnt[PE]), reads=[wslot_b[s], sb_], writes=[psb[bkk]])
                evac_copy(bi_, kT2[:, c0:c0 + n], ps[bkk][:, 0:n], [psb[bkk]], [], wadd=[A["kT2"]])
            bkl_ = bank()
            for j_, (c0_, m_) in enumerate([(NP - 128, 128), (NP, NS)]):
                for k in range(8):
                    P.op(PE, lambda e, j_=j_, c0_=c0_, m_=m_, k=k, s=s: e.matmul(
                        ps[bkl_][0:m_, 64 * j_:64 * j_ + 64], lhsT=hT[:, k, c0_:c0_ + m_], rhs=wslot[s][:, k, 0:64],
                        start=(k == 0), stop=(k == 7)),
                        reads=[wslot_b[s], hT_b[3], hT_b[4]] if (k == 0 and j_ == 0) else [],
                        writes=[psb[bkl_]] if (k == 0 and j_ == 0) else [], sig=(k == 7 and j_ == 1))
            if not P.dead:
                P.attach(Dep(P.esem[PE], P.cnt[PE]), reads=[wslot_b[s], hT_b[3], hT_b[4]], writes=[psb[bkl_]])
            P.op(DVE, lambda e, kv=kv: e.tensor_copy(out=klast[:, 64 * kv:64 * kv + 64], in_=ps[bkl_][:, 0:64]), reads=[psb[bkl_]], wadd=[A["klast"]])
            P.op(DVE, lambda e, kv=kv: e.tensor_copy(out=knew[0:NS, 64 * kv:64 * kv + 64], in_=ps[bkl_][0:NS, 64:128]), reads=[psb[bkl_]], wadd=[A["knew"]])
            s = wctr[0] % 2
            wctr[0] += 1
            P.dma(POOL, wsem[s], wslot[s][:, :, 0:64], w_in[:, OFF_V + 64 * kv:OFF_V + 64 * kv + 64].rearrange("(k p) f -> p k f", p=128),
                  writes=[wslot_b[s]])
            A["Vaug"].r = list(A["Vaug"].r) + list(A["Vaug"].w); A["Vaug"].w = []
            vtiles = [(hTh, hTh_b, 0, 128)] + [(hT, hT_b[i // 4], 128 * i, 128) for i in range(16)] + [(hT, hT_b[4], NP, NS)]
            for grp in range(3):
                bkv = bank()
                tl_ = vtiles[8 * grp:8 * grp + 8]
                for j_, (src, sb_, c0_, m_) in enumerate(tl_):
                    for k in range(8):
                        firstg = (j_ == 0 and k == 0)
                        P.op(PE, lambda e, bkv=bkv, j_=j_, src=src, c0_=c0_, m_=m_, k=k, s=s: e.matmul(
                            ps[bkv][0:m_, 64 * j_:64 * j_ + 64], lhsT=src[:, k, c0_:c0_ + m_], rhs=wslot[s][:, k, 0:64],
                            start=(k == 0), stop=(k == 7)),
                            reads=[wslot_b[s], sb_, hTh_b] + hT_b if firstg else [], writes=[psb[bkv]] if firstg else [],
                            sig=(j_ == len(tl_) - 1 and k == 7))
                if not P.dead:
                    P.attach(Dep(P.esem[PE], P.cnt[PE]), reads=[wslot_b[s], hTh_b] + hT_b, writes=[psb[bkv]])
                nt_ = len(tl_)
                if grp < 2:
                    P.op(ACT, lambda e, bkv=bkv, grp=grp: e.activation(out=Vaug[:, 8 * grp:8 * grp + 8, 0:64],
                                                                     in_=ps[bkv][:, 0:512].rearrange("p (t d) -> p t d", d=64), func=AF.Copy),
                         reads=[psb[bkv]], wadd=[A["Vaug"]])
                    if grp == 1:
                        pass
                else:
                    P.op(ACT, lambda e, bkv=bkv: e.activation(out=Vaug[:, 16, 0:64], in_=ps[bkv][:, 0:64], func=AF.Copy),
                         reads=[psb[bkv]], wadd=[A["Vaug"]])
                    P.op(ACT, lambda e, bkv=bkv: e.activation(out=Vaug[0:NS, 17, 0:64], in_=ps[bkv][0:NS, 64:128], func=AF.Copy),
                         reads=[psb[bkv]], wadd=[A["Vaug"]])
                    P.op(ACT, lambda e, bkv=bkv, kv=kv: e.activation(out=vlast[:, 64 * kv:64 * kv + 64], in_=ps[bkv][:, 0:64], func=AF.Copy),
                         reads=[psb[bkv]], wadd=[A["vlast"]])
                    P.op(ACT, lambda e, bkv=bkv, kv=kv: e.activation(out=vnew[0:NS, 64 * kv:64 * kv + 64], in_=ps[bkv][0:NS, 64:128], func=AF.Copy),
                         reads=[psb[bkv]], wadd=[A["vnew"]])
            qv = lambda base, b_: qT[base:base + 64, 0:2, 128 * b_:128 * b_ + 128]
            def attn_s1(b_):
                ia = (2 * b_) % 4; ib = (2 * b_ + 1) % 4
                bA, bB = bank(), bank()
                kprev = slice(128 * b_, 128 * b_ + 128); kcur = slice(128 * b_ + 128, 128 * b_ + 256)
                seq = [(bA, 0, 0, kprev), (bB, 64, 0, kprev), (bA, 0, 1, kcur), (bB, 64, 1, kcur)]
                for (bk_, base, half, ks) in seq:
                    first = (half == 0)
                    P.op(PE, lambda e, bk_=bk_, base=base, half=half, ks=ks, b_=b_: e.matmul(
                        ps[bk_][:, 256 * half:256 * half + 256], lhsT=kT2[base:base + 64, ks], rhs=qv(base, b_), start=True, stop=True),
                        reads=[A["kT2"], A["qT"]] if first else [], writes=[psb[bk_]] if first else [], sig=(half == 1))
                    if half == 1 and not P.dead:
                        P.attach(Dep(P.esem[PE], P.cnt[PE]), reads=[A["kT2"], A["qT"]], writes=[psb[bk_]])
                for (bk_, ie, base_h) in [(bA, ia, 0), (bB, ib, 1)]:
                    P.op(ACT, lambda e, bk_=bk_, ie=ie: e.activation(out=Et[ie], in_=ps[bk_][:, :], func=AF.Exp, scale=0.125),
                         reads=[psb[bk_]], writes=[b_Et[ie]])
                    Ev = Et[ie].rearrange("p (h i q) -> p h i q", h=2, i=2)
                    Pv = PT[ie].rearrange("p (h i q) -> p h i q", h=2, i=2)
                    h0 = 4 * kv + base_h
                    eng = DVE if base_h == 0 else POOL
                    if b_ > 0:
                        P.op(eng, lambda e, Ev=Ev, Pv=Pv, h0=h0: e.tensor_tensor(out=Pv, in0=Ev, in1=EB[:, :, h0:h0 + 3:2, :], op=ALU.mult),
                             reads=[b_Et[ie], A["EB"]], writes=[b_PT[ie]])
                    else:
                        P.op(eng, lambda e, Ev=Ev, Pv=Pv, h0=h0: e.tensor_tensor(out=Pv[:, 0], in0=Ev[:, 0], in1=EB0[:, h0:h0 + 3:2, :], op=ALU.mult),
                             reads=[b_Et[ie], A["EB0"]], writes=[b_PT[ie]])
                        P.op(eng, lambda e, Ev=Ev, Pv=Pv, h0=h0: e.tensor_tensor(out=Pv[:, 1], in0=Ev[:, 1], in1=EB[:, 1, h0:h0 + 3:2, :], op=ALU.mult),
                             reads=[b_Et[ie], A["EB"]], wadd=[b_PT[ie]])

            def attn_s2(b_):
                ia = (2 * b_) % 4; ib = (2 * b_ + 1) % 4
                bO = bank()
                mm = [(ia, 0, b_, True), (ia, 1, b_ + 1, False), (ib, 0, b_, False), (ib, 1, b_ + 1, False)]
                for j_, (ip, half, tile, st_) in enumerate(mm):
                    cols = slice(0, 256) if ip == ia else slice(256, 512)
                    P.op(PE, lambda e, ip=ip, half=half, tile=tile, st_=st_, cols=cols: e.matmul(
                        ps[bO][:, cols], lhsT=Vaug[:, tile, :], rhs=PT[ip][:, 256 * half:256 * half + 256], start=st_, stop=False),
                        reads=[A["Vaug"], b_PT[ia], b_PT[ib], b_ES, b_ones] if j_ == 0 else [], writes=[psb[bO]] if j_ == 0 else [], sig=False)
                P.op(PE, lambda e, kv=kv: e.matmul(ps[bO][:, 0:512], lhsT=onesd[0:1, :], rhs=ES[0:1, kv].rearrange("p i q -> p (i q)"),
                                                   start=False, stop=True), sig=True)
                if not P.dead:
                    P.attach(Dep(P.esem[PE], P.cnt[PE]), reads=[A["Vaug"], b_PT[ia], b_PT[ib], b_ES, b_ones], writes=[psb[bO]])
                ir = b_ % 2
                P.op(ACT, lambda e, ir=ir: e.activation(out=rc[ir][64:128, :], in_=ps[bO][64:128, :], func=AF.Ln), reads=[psb[bO]], writes=[b_rc[ir]])
                P.op(ACT, lambda e, ir=ir: e.activation(out=rc[ir][64:128, :], in_=rc[ir][64:128, :], func=AF.Exp, scale=-1.0),
                     reads=[b_rc[ir]], writes=[b_rc[ir]])
                for par in range(2):
                    P.op(DVE, lambda e, par=par, ir=ir, b_=b_, kv=kv: e.tensor_tensor(
                        out=oT[64 * par:64 * par + 64, 2 * kv:2 * kv + 2, 128 * b_:128 * b_ + 128],
                        in0=ps[bO][0:64, 256 * par:256 * par + 256].rearrange("p (c q) -> p c q", c=2),
                        in1=rc[ir][64:128, 256 * par:256 * par + 256].rearrange("p (c q) -> p c q", c=2), op=ALU.mult),
                        reads=[psb[bO], b_rc[ir]], wadd=[oT_b[2 * kv], oT_b[2 * kv + 1]])

            attn_s1(0)
            for b_ in range(1, 16):
                attn_s1(b_)
                attn_s2(b_ - 1)
            attn_s2(15)
            base = 64 * (kv % 2)
            for i in range(4):
                hsrc = 64 * (i % 2)
                P.op(POOL, lambda e, i=i, hsrc=hsrc, base=base: e.tensor_copy(out=QsT[base:base + 64, 0, i, :], in_=qT[hsrc:hsrc + 64, i // 2, NP:NT]),
                     reads=[A["qT"]], writes=[A["QsT"]] if i == 0 else [], wadd=[A["QsT"]] if i > 0 else [])
            for sl_, i in enumerate([0, 1, 2, 3]):
                P.op(DVE, lambda e, i=i, kv=kv: e.tensor_copy(out=esr[0:1, :].rearrange("p (s i) -> p s i", i=4)[:, :, i],
                                                            in_=es16[0:1, 4 * kv + i:4 * kv + i + 1].to_broadcast([1, 16])),
                     reads=[A["es16"]], writes=[A["esr"]] if i == 0 else [], wadd=[A["esr"]] if i > 0 else [])
            bS, bD, bN = bank(), bank(), bank()
            Qsi = QsT[base:base + 64, 0].rearrange("p i s -> p s i")
            for s_ in range(NS):
                P.op(PE, lambda e, s_=s_, base=base, kv=kv: e.matmul(ps[bS][:, 4 * s_:4 * s_ + 4], lhsT=KcT[base:base + 64, s_, kv // 2, :],
                                                                  rhs=QsT[base:base + 64, 0, :, s_], start=True, stop=True),
                     reads=[A["KcT"], A["QsT"], A["kT2"]] if s_ == 0 else [], writes=[psb[bS]] if s_ == 0 else [], sig=False)
            P.op(PE, lambda e, base=base: e.matmul(ps[bS][0:NS, 64:128], lhsT=kT2[base:base + 64, 128 + NP:128 + NT], rhs=Qsi, start=True, stop=True), sig=True)
            if not P.dead:
                P.attach(Dep(P.esem[PE], P.cnt[PE]), reads=[A["KcT"], A["QsT"], A["kT2"]], writes=[psb[bS]])
            P.op(ACT, lambda e: e.activation(out=ptS[:, 0:64], in_=ps[bS][:, 0:64], func=AF.Exp, scale=0.125), reads=[psb[bS]], writes=[A["ptS"]])
            P.op(ACT, lambda e: e.activation(out=pdg[0:NS, :], in_=ps[bS][0:NS, 64:128], func=AF.Exp, scale=0.125), reads=[psb[bS]], writes=[A["pdg"]])
            P.op(DVE, lambda e, kv=kv: e.tensor_tensor(out=ptS[:, 0:64].rearrange("p (s i) -> p s i", i=4), in0=ptS[:, 0:64].rearrange("p (s i) -> p s i", i=4),
                                                in1=EB[:, 0, 4 * kv:4 * kv + 4, 0].unsqueeze(1).to_broadcast([128, 16, 4]), op=ALU.mult),
                 reads=[A["ptS"], A["EB"]], writes=[A["ptS"]])
            P.op(DVE, lambda e, kv=kv: e.tensor_tensor(out=pdg[0:NS, :].rearrange("p (s i) -> p s i", i=4), in0=pdg[0:NS, :].rearrange("p (s i) -> p s i", i=4),
                                                in1=ebs[0:NS, 4 * kv:4 * kv + 4].unsqueeze(1).to_broadcast([NS, 16, 4]), op=ALU.mult),
                 reads=[A["pdg"], A["ebs"]], writes=[A["pdg"]])
            P.op(DVE, lambda e: e.tensor_tensor(out=pdg[0:NS, :], in0=pdg[0:NS, :], in1=dgt[0:NS, :], op=ALU.mult),
                 reads=[A["pdg"], A["dgt"]], writes=[A["pdg"]])
            P.op(POOL, lambda e, kv=kv: e.tensor_copy(out=vnb[0:NS, 64 * kv:64 * kv + 64], in_=vnew[0:NS, 64 * kv:64 * kv + 64]),
                 reads=[A["vnew"]], writes=[A["vnb"]])
            P.op(PE, lambda e: e.matmul(ps[bD][:, 0:64], lhsT=ones_k, rhs=ptS[:, 0:64], start=True, stop=False),
                 reads=[A["ones_k"], A["ptS"], A["pdg"], A["esr"]], writes=[psb[bD]], sig=False)
            P.op(PE, lambda e: e.matmul(ps[bD][:, 0:64], lhsT=ones_k[0:NS, :], rhs=pdg[0:NS, :], start=False, stop=False), sig=False)
            P.op(PE, lambda e: e.matmul(ps[bD][:, 0:64], lhsT=ones_k[0:1, :], rhs=esr[0:1, :], start=False, stop=True), sig=True)
            if not P.dead:
                P.attach(Dep(P.esem[PE], P.cnt[PE]), reads=[A["ones_k"], A["ptS"], A["pdg"], A["esr"]], writes=[psb[bD]])
            ptv = ptS[:, 0:64].rearrange("p (s i) -> p s i", i=4)
            pdv = pdg[0:NS, :].rearrange("p (s i) -> p s i", i=4)
            psn = ps[bN][:, 0:64].rearrange("p (s i) -> p s i", i=4)
            for par in range(2):
                for s_ in range(NS):
                    P.op(PE, lambda e, par=par, s_=s_, kv=kv: e.matmul(
                        psn[64 * par:64 * par + 64, s_, par:4:2], lhsT=Vcs[:, s_, 64 * kv:64 * kv + 64], rhs=ptv[:, s_, par:4:2],
                        start=(s_ == 0), stop=False, tile_position=(0, 64 * par)),
                        reads=[A["Vcs"], A["ptS"], A["pdg"], A["vnb"]] if (par == 0 and s_ == 0) else [],
                        writes=[psb[bN]] if (par == 0 and s_ == 0) else [], sig=False)
                P.op(PE, lambda e, par=par, kv=kv: e.matmul(
                    psn[64 * par:64 * par + 64, :, par:4:2], lhsT=vnb[0:NS, 64 * kv:64 * kv + 64], rhs=pdv[:, :, par:4:2],
                    start=False, stop=True, tile_position=(0, 64 * par)), sig=(par == 1))
            if not P.dead:
                P.attach(Dep(P.esem[PE], P.cnt[PE]), reads=[A["Vcs"], A["ptS"], A["pdg"], A["vnb"]], writes=[psb[bN]])
            P.op(DVE, lambda e: e.reciprocal(out=rcs[:, 0:64], in_=ps[bD][:, 0:64]), reads=[psb[bD]], writes=[A["rcs"]])
            rcv = rcs[:, 0:64].rearrange("p (s i) -> p s i", i=4)
            for par in range(2):
                P.op(DVE, lambda e, par=par, kv=kv: e.tensor_tensor(
                    out=oT[64 * par:64 * par + 64, 2 * kv:2 * kv + 2, NP:NT],
                    in0=psn[64 * par:64 * par + 64, :, par:4:2].rearrange("p s c -> p c s"),
                    in1=rcv[64 * par:64 * par + 64, :, par:4:2].rearrange("p s c -> p c s"), op=ALU.mult),
                    reads=[psb[bN], A["rcs"]], wadd=[oT_b[2 * kv], oT_b[2 * kv + 1]])
        for kv in range(4):
            attn_kv(kv)
        P.dma(SP, osem_new("pck"), pck, klast, reads=[A["klast"]])
        P.dma(SP, osem_new("pcv"), pcv, vlast, reads=[A["vlast"]])
        P.dma(SP, osem_new("sck"), sck[:, 127, :], knew[0:NS, :], reads=[A["knew"]])
        P.dma(SP, osem_new("scv"), scv[:, 127, :], vnew[0:NS, :], reads=[A["vnew"]])

        P.phase(13)
        sgaT = cv_(O_C + 0, [8, NT], BF16)
        sga_b = [Buf(f"sga{g}") for g in range(8)]
        gt2 = [cv_(O_C + 33024 + 1024 * i, [512], BF16) for i in range(4)]
        ft2 = [cv_(O_C + 37120 + 2048 * i, [512], F32) for i in range(2)]
        b_gt2 = [Buf() for _ in range(4)]; b_ft2 = [Buf(), Buf()]
        handoff(sga_b + b_gt2 + b_ft2, list(A.values()) + b_Et + b_PT + b_rc)
        for blk in range(2):
            def ev_za(oc, tbi, pap, pb, blk=blk):
                t0, n = TBS[tbi]
                g = 4 * blk + oc
                ctr["i"] += 1
                gi = ctr["i"] % 4; fi = ctr["i"] % 2
                P.op(ACT, lambda e: e.activation(out=gt2[gi][:, 0:n], in_=pap, func=AF.Sigmoid), reads=[pb], writes=[b_gt2[gi]])
                P.op(DVE, lambda e: e.tensor_tensor(out=ft2[fi][:, 0:n], in0=pap, in1=gt2[gi][:, 0:n], op=ALU.mult),
                     reads=[pb, b_gt2[gi]], writes=[b_ft2[fi]])
                P.op(DVE, lambda e: e.tensor_tensor(out=oT[:, g, t0:t0 + n], in0=oT[:, g, t0:t0 + n], in1=ft2[fi][:, 0:n], op=ALU.mult),
                     reads=[b_ft2[fi], oT_b[g]], wadd=[oT_b[g]])
            proj_fm(w_in[:, OFF_ZA + 512 * blk:OFF_ZA + 512 * blk + 512], 512, rhs_h, hT_b, ev_za)
        for blk in range(2):
            def ev_ga(oc, tbi, pap, pb, blk=blk):
                t0, n = TBS[tbi]
                g = 4 * blk + oc
                P.op(ACT, lambda e: e.activation(out=sgaT[:, g, t0:t0 + n], in_=pap, func=AF.Sigmoid), reads=[pb], wadd=[sga_b[g]])
            proj_fm(w_in[:, OFF_GA + 512 * blk:OFF_GA + 512 * blk + 512], 512, rhs_h, hT_b, ev_ga)
        mT = RA
        mT_b = gbs_b
        rhs_o = lambda k, t0, n: oT[:, k, t0:t0 + n]
        for blk in range(2):
            def ev_ba(oc, tbi, pap, pb, blk=blk):
                t0, n = TBS[tbi]
                g = 4 * blk + oc
                ctr["i"] += 1
                fi = ctr["i"] % 2
                P.op(DVE, lambda e: e.tensor_tensor(out=ft2[fi][:, 0:n], in0=pap, in1=sgaT[:, g, t0:t0 + n], op=ALU.mult),
                     reads=[pb, sga_b[g]], writes=[b_ft2[fi]])
                P.op(DVE, lambda e: e.tensor_tensor(out=mT[:, g, t0:t0 + n], in0=mT[:, g, t0:t0 + n], in1=ft2[fi][:, 0:n], op=ALU.add),
                     reads=[b_ft2[fi], mT_b[g]], wadd=[mT_b[g]])
            proj_fm(w_ba[:, 512 * blk:512 * blk + 512], 512, rhs_o, [BufGroup(oT_b)] * 5, ev_ba)

        P.phase(14)
        o2 = O_C + 8000
        NSL = 4
        GateB = cv_(o2 + 0, [1024], F32); LnG = cv_(o2 + 4096, [1024], F32); LnB = cv_(o2 + 8192, [1024], F32)
        gateS = cv_(o2 + 12288, [1024], F32); grow = cv_(o2 + 16384, [1024], F32)
        xt = [cv_(o2 + 20480 + 4096 * i, [1024], F32) for i in range(NSL)]
        rt = [cv_(o2 + 36864 + 4096 * i, [1024], F32) for i in range(NSL)]
        stt = cv_(o2 + 53248, [NSL, 2, 6], F32); mvt = cv_(o2 + 53504, [NSL, 2], F32); rsd = cv_(o2 + 53568, [NSL, 2], F32)
        mhalf = cv_(o2 + 53632, [1], F32)
        b_GateB = Buf(); b_LnG = Buf(); b_LnB = Buf(); b_gateS = Buf(); b_grow = Buf(); b_xt = [Buf() for _ in range(NSL)]; b_rt = [Buf() for _ in range(NSL)]
        b_stt = [Buf() for _ in range(NSL)]; b_mh = Buf()
        xsem2 = [P.dsem(f"xt{i}") for i in range(NSL)]; osem = [P.dsem(f"o{i}") for i in range(NSL)]
        handoff([b_GateB, b_LnG, b_LnB, b_gateS, b_grow] + b_xt + b_rt + b_stt + [b_mh], list(A.values()) + b_Et + b_PT + b_rc + sga_b + b_gt2 + b_ft2)
        misc_load(SP, LnG, ln_g.rearrange("(o n) -> o n", o=1).to_broadcast([128, 1024]), b_LnG)
        misc_load(SP, LnB, ln_b.rearrange("(o n) -> o n", o=1).to_broadcast([128, 1024]), b_LnB)
        P.op(POOL, lambda e: e.memset(mhalf, -0.5), writes=[b_mh])
        for hb in range(2):
            bkg = bank(); bkg2 = bank()
            for kk in range(4):
                k = 4 * hb + kk
                P.op(PE, lambda e, k=k, kk=kk, bkg=bkg: e.transpose(out=ps[bkg][0:1, 128 * kk:128 * kk + 128], in_=modT[:, 16 + k, 0:1], identity=ident),
                     reads=[b_modT, b_ident], writes=[psb[bkg]] if kk == 0 else [], wadd=[psb[bkg]] if kk > 0 else [])
                P.op(PE, lambda e, k=k, kk=kk, bkg2=bkg2: e.transpose(out=ps[bkg2][0:NS, 128 * kk:128 * kk + 128], in_=modT[:, 16 + k, 1:17], identity=ident),
                     reads=[b_modT, b_ident], writes=[psb[bkg2]] if kk == 0 else [], wadd=[psb[bkg2]] if kk > 0 else [])
            P.op(DVE, lambda e, hb=hb, bkg=bkg: e.tensor_copy(out=grow[0:1, 512 * hb:512 * hb + 512], in_=ps[bkg][0:1, 0:512]), reads=[psb[bkg]], wadd=[b_grow])
            P.op(DVE, lambda e, hb=hb, bkg2=bkg2: e.tensor_copy(out=gateS[0:NS, 512 * hb:512 * hb + 512], in_=ps[bkg2][0:NS, 0:512]), reads=[psb[bkg2]], wadd=[b_gateS])
        for hb in range(2):
            bkb = bank()
            P.op(PE, lambda e, hb=hb, bkb=bkb: e.matmul(ps[bkb][:, 0:512], lhsT=ones1[0:1, :], rhs=grow[0:1, 512 * hb:512 * hb + 512], start=True, stop=True),
                 reads=[b_grow, b_ones], writes=[psb[bkb]])
            P.op(DVE, lambda e, hb=hb, bkb=bkb: e.tensor_copy(out=GateB[:, 512 * hb:512 * hb + 512], in_=ps[bkb][:, 0:512]), reads=[psb[bkb]], wadd=[b_GateB])
        so = [load_w(w_out[:, 0:512], 512), load_w(w_out[:, 512:1024], 512)]
        def x_load(ti_):
            rows_, c0_ = (128, 128 * ti_) if ti_ < 16 else (NS, NP)
            src_ = xp[c0_:c0_ + 128, :] if ti_ < 16 else xs
            P.dma(SP, xsem2[ti_ % NSL], xt[ti_ % NSL][0:rows_, :], src_, writes=[b_xt[ti_ % NSL]])
        for ti_ in range(NSL):
            x_load(ti_)
        for tt_i in range(17):
            rows, c0 = (128, 128 * tt_i) if tt_i < 16 else (NS, NP)
            sl = tt_i % NSL
            gate_ap = GateB if tt_i < 16 else gateS
            gate_b = b_GateB if tt_i < 16 else b_gateS
            for fb in range(2):
                bko = bank()
                for k in range(8):
                    P.op(PE, lambda e, bko=bko, k=k, fb=fb, rows=rows, c0=c0: e.matmul(
                        ps[bko][0:rows, 0:512], lhsT=mT[:, k, c0:c0 + rows], rhs=wslot[so[fb]][:, k, 0:512], start=(k == 0), stop=(k == 7)),
                        reads=[wslot_b[so[fb]]] + mT_b if k == 0 else [], writes=[psb[bko]] if k == 0 else [], sig=(k == 7))
                if not P.dead:
                    P.attach(Dep(P.esem[PE], P.cnt[PE]), reads=[wslot_b[so[fb]]] + mT_b, writes=[psb[bko]])
                P.op(DVE, lambda e, bko=bko, fb=fb, rows=rows, sl=sl, gate_ap=gate_ap: e.tensor_tensor(
                    out=rt[sl][0:rows, 512 * fb:512 * fb + 512], in0=ps[bko][0:rows, 0:512], in1=gate_ap[0:rows, 512 * fb:512 * fb + 512], op=ALU.mult),
                    reads=[psb[bko], gate_b], writes=[b_rt[sl]] if fb == 0 else [], wadd=[b_rt[sl]] if fb == 1 else [])
            P.op(DVE, lambda e, rows=rows, sl=sl: e.scalar_tensor_tensor(out=rt[sl][0:rows, :], in0=xt[sl][0:rows, :], scalar=float(ALPHA),
                                                                         in1=rt[sl][0:rows, :], op0=ALU.mult, op1=ALU.add),
                 reads=[b_xt[sl], b_rt[sl]], writes=[b_rt[sl]])
            for hf in range(2):
                P.op(DVE, lambda e, rows=rows, sl=sl, hf=hf: e.bn_stats(out=stt[0:rows, sl, hf, :], in_=rt[sl][0:rows, 512 * hf:512 * hf + 512]),
                     reads=[b_rt[sl]], writes=[b_stt[sl]] if hf == 0 else [], wadd=[b_stt[sl]] if hf == 1 else [])
            P.op(DVE, lambda e, rows=rows, sl=sl: e.bn_aggr(out=mvt[0:rows, sl, :], in_=stt[0:rows, sl].rearrange("p a b -> p (a b)")),
                 reads=[b_stt[sl]], writes=[b_stt[sl]])
            P.op(POOL, lambda e, rows=rows, sl=sl: e.tensor_scalar(out=rsd[0:rows, sl, 0:1], in0=mvt[0:rows, sl, 1:2], scalar1=float(LN_EPS), scalar2=0.0,
                                                                   op0=ALU.add, op1=ALU.add), reads=[b_stt[sl]], writes=[b_stt[sl]])
            P.op(POOL, lambda e, rows=rows, sl=sl: e.tensor_tensor(out=rsd[0:rows, sl, 0:1], in0=rsd[0:rows, sl, 0:1], in1=mhalf[0:rows, :], op=ALU.pow),
                 reads=[b_stt[sl], b_mh], writes=[b_stt[sl]])
            P.op(POOL, lambda e, rows=rows, sl=sl: e.tensor_tensor(out=rsd[0:rows, sl, 1:2], in0=mvt[0:rows, sl, 0:1], in1=rsd[0:rows, sl, 0:1], op=ALU.mult),
                 reads=[b_stt[sl]], writes=[b_stt[sl]])
            P.op(POOL, lambda e, rows=rows, sl=sl: e.tensor_scalar(out=rsd[0:rows, sl, 1:2], in0=rsd[0:rows, sl, 1:2], scalar1=-1.0, scalar2=0.0,
                                                                   op0=ALU.mult, op1=ALU.add), reads=[b_stt[sl]], writes=[b_stt[sl]])
            P.op(ACT, lambda e, rows=rows, sl=sl: e.activation(out=xt[sl][0:rows, :], in_=rt[sl][0:rows, :], func=AF.Identity,
                                                               scale=rsd[0:rows, sl, 0:1], bias=rsd[0:rows, sl, 1:2]),
                 reads=[b_rt[sl], b_stt[sl]], writes=[b_xt[sl]])
            P.op(DVE, lambda e, rows=rows, sl=sl: e.tensor_tensor(out=xt[sl][0:rows, :], in0=xt[sl][0:rows, :], in1=LnG[0:rows, :], op=ALU.mult),
                 reads=[b_xt[sl], b_LnG], writes=[b_xt[sl]])
            P.op(POOL, lambda e, rows=rows, sl=sl: e.tensor_tensor(out=xt[sl][0:rows, :], in0=xt[sl][0:rows, :], in1=LnB[0:rows, :], op=ALU.add),
                 reads=[b_xt[sl], b_LnB], writes=[b_xt[sl]])
            dst = yp[c0:c0 + 128, :] if tt_i < 16 else ys
            P.dma(SP, osem[sl], dst, xt[sl][0:rows, :], reads=[b_xt[sl]])
            if tt_i + NSL < 17:
                x_load(tt_i + NSL)
        final_deps = [Dep(o_.h, o_.cnt) for o_ in osem]

        P.dead = False
        if DEBUG:
            pass
        P.wait(SP, [Dep(dout_sem.h, dout_sem.cnt)] + final_deps + [Dep(d_.h, d_.cnt) for d_ in out_sems])
        P.emit()
        print("instruction counts:", P.ninst)
    return nc


def _bucket_np(dist):
    max_exact = 16
    df = np.maximum(dist, 1).astype(np.float32)
    large = max_exact + (np.log(df / np.float32(max_exact)) / np.float32(math.log(128 / max_exact)) * np.float32(16)).astype(np.int32)
    large = np.minimum(large, 31)
    return np.where(dist < max_exact, dist, large)


def _host_consts():
    R = np.zeros((32, 384), np.float32)
    for i in range(384):
        dist = 255 - i
        if 0 <= dist <= 128:
            R[int(_bucket_np(np.array([dist]))[0]), i] = 1.0
    j = np.arange(128)[:, None]
    q = np.arange(128)[None, :]
    mask = np.concatenate([(j >= q), (j <= q)], axis=1).astype(np.float32)
    r = np.arange(128)
    bmask = (r[:, None] // 32 == r[None, :] // 32).astype(np.float32)
    diag = np.zeros((16, 16, 4), np.float32)
    for s_ in range(16):
        diag[s_, s_, :] = 1.0
    return R, mask, bmask, diag.reshape(16, 64)


_NC_CACHE = {}


def kernel(x_prompt, x_sample, c_prompt, c_sample, state_ssm_re, state_ssm_im, cache_swa_k, cache_swa_v,
           w_ada, b_ada, w_in, ssm_lambda_re, ssm_lambda_im, ssm_log_delta, ssm_b_re, ssm_b_im,
           ssm_c_re, ssm_c_im, ssm_d, w_glu, b_glu, attn_sinks, rel_bias, w_branch_s, w_branch_a,
           w_out, ln_g, ln_b):
    f = lambda a: np.ascontiguousarray(np.asarray(a, dtype=np.float32))
    x_prompt = f(x_prompt); x_sample = f(x_sample); c_prompt = f(c_prompt); c_sample = f(c_sample)
    R, mask, bmask, diag = _host_consts()
    shared = {
        "w_ada": f(w_ada)[0], "b_ada": f(b_ada)[0], "w_in": f(w_in)[0],
        "lam_re": f(ssm_lambda_re)[0], "lam_im": f(ssm_lambda_im)[0], "log_delta": f(ssm_log_delta)[0],
        "b_re": f(ssm_b_re)[0].reshape(4096, 16), "b_im": f(ssm_b_im)[0].reshape(4096, 16),
        "c_re": f(ssm_c_re)[0].reshape(1024, 64), "c_im": f(ssm_c_im)[0].reshape(1024, 64),
        "ssm_d": f(ssm_d)[0], "w_glu": f(w_glu)[0], "b_glu": f(b_glu)[0], "sinks": f(attn_sinks)[0],
        "rel_bias": f(rel_bias), "w_bs": f(w_branch_s)[0], "w_ba": f(w_branch_a)[0], "w_out": f(w_out)[0],
        "ln_g": f(ln_g)[0], "ln_b": f(ln_b)[0],
        "rtab": R, "maskc": mask, "bmaskc": bmask, "diagc": diag,
    }
    sre = f(state_ssm_re)[0].reshape(128, 4096); sim = f(state_ssm_im)[0].reshape(128, 4096)
    ckk = f(cache_swa_k)[0].reshape(128, 128, 256); cvv = f(cache_swa_v)[0].reshape(128, 128, 256)
    in_maps = []
    for c in range(NCORES):
        b, qr = c // 4, c % 4
        t0 = NP * qr
        xh = x_prompt[b, t0 - 128:t0] if qr > 0 else np.zeros((128, D), np.float32)
        flags = np.zeros(32, np.float32)
        flags[0] = 1.0 if qr > 0 else 0.0
        xprev = np.zeros((3, NP, D), np.float32)
        for j in range(3):
            qq = qr - 1 - j
            if qq >= 0:
                flags[1 + j] = 1.0
                xprev[j] = x_prompt[b, NP * qq:NP * qq + NP]
        m = dict(shared)
        m.update({
            "xprev": xprev, "xp": np.ascontiguousarray(x_prompt[b, t0:t0 + NP]), "xh": np.ascontiguousarray(xh),
            "xs": np.ascontiguousarray(x_sample[NS * c:NS * c + NS, 0]),
            "cc": np.ascontiguousarray(np.concatenate([c_prompt[b:b + 1], c_sample[NS * c:NS * c + NS]], 0)),
            "st_re": np.ascontiguousarray(sre[NS * c:NS * c + NS]), "st_im": np.ascontiguousarray(sim[NS * c:NS * c + NS]),
            "ck": np.ascontiguousarray(ckk[NS * c:NS * c + NS]), "cv": np.ascontiguousarray(cvv[NS * c:NS * c + NS]),
            "flags": flags,
        })
        in_maps.append(m)
    nc = build()
    res = run_bass_kernel_spmd(nc, in_maps, core_ids=list(range(NCORES)))
    R_ = res.results
    kernel.last_results = R_
    y_prompt = np.stack([np.concatenate([R_[4 * b + q]["yp"] for q in range(4)], 0) for b in range(2)], 0)
    y_sample = np.concatenate([R_[c]["ys"] for c in range(NCORES)], 0).reshape(128, 1, D)
    p_hr = np.stack([R_[4 * b + 3]["pst_re"].reshape(64, 64) for b in range(2)], 0)[None]
    p_hi = np.stack([R_[4 * b + 3]["pst_im"].reshape(64, 64) for b in range(2)], 0)[None]
    p_k = np.stack([R_[4 * b + 3]["pck"].reshape(128, 4, 64) for b in range(2)], 0)[None]
    p_v = np.stack([R_[4 * b + 3]["pcv"].reshape(128, 4, 64) for b in range(2)], 0)[None]
    s_hr = np.concatenate([R_[c]["sst_re"] for c in range(NCORES)], 0).reshape(1, 128, 64, 64)
    s_hi = np.concatenate([R_[c]["sst_im"] for c in range(NCORES)], 0).reshape(1, 128, 64, 64)
    s_k = np.concatenate([R_[c]["sck"] for c in range(NCORES)], 0).reshape(1, 128, 128, 4, 64)
    s_v = np.concatenate([R_[c]["scv"] for c in range(NCORES)], 0).reshape(1, 128, 128, 4, 64)
    return (y_prompt.astype(np.float32), y_sample.astype(np.float32), p_hr.astype(np.float32), p_hi.astype(np.float32),
            p_k.astype(np.float32), p_v.astype(np.float32), s_hr.astype(np.float32), s_hi.astype(np.float32),
            s_k.astype(np.float32), s_v.astype(np.float32))
```

```python
import math
import os
from contextlib import ExitStack
import numpy as np
import ml_dtypes
import concourse.bass as bass
import concourse.mybir as mybir
from concourse.bass_utils import run_bass_kernel_spmd

F32 = mybir.dt.float32
BF16 = mybir.dt.bfloat16
U8 = mybir.dt.uint8
ALU = mybir.AluOpType
AF = mybir.ActivationFunctionType
AX = mybir.AxisListType

PE, ACT, DVE, POOL, SP = "tensor", "scalar", "vector", "gpsimd", "sync"
ENGS = [PE, ACT, DVE, POOL, SP]

NCORES = 8
D = 1024
NP = 2048
NS = 16
NT = NP + NS
TBS = [(0, 512), (512, 512), (1024, 512), (1536, 512), (2048, 16)]
DIN = 6656
OFF_U, OFF_ZS, OFF_Q, OFF_K, OFF_V, OFF_ZA, OFF_GS, OFF_GA = 0, 1024, 2048, 3072, 3328, 3584, 4608, 5632
ALPHA = 2.0 ** 0.25
LN_EPS = 1e-5
DEBUG = False


class Dep:
    __slots__ = ("sem", "val")

    def __init__(self, sem, val):
        self.sem = sem
        self.val = val


class Buf:
    __slots__ = ("w", "r", "name")

    def __init__(self, name=""):
        self.w = []
        self.r = []
        self.name = name


class _RProxy:
    def __init__(self, bufs):
        self.bufs = bufs

    def append(self, h):
        for b in self.bufs:
            b.r.append(h)

    def __len__(self):
        return 0


class BufGroup:
    def __init__(self, bufs):
        self.bufs = list(bufs)
        self.r = _RProxy(self.bufs)

    @property
    def w(self):
        return [h for b in self.bufs for h in b.w]


def handoff(new_bufs, old_bufs):
    deps = []
    for b in old_bufs:
        deps.extend(b.w)
        deps.extend(b.r)
    for nb in new_bufs:
        nb.r = list(nb.r) + deps


class DSem:
    def __init__(self, h):
        self.h = h
        self.cnt = 0


class Prog:
    def __init__(self, nc, stack):
        self.nc = nc
        self.q = {e: [] for e in ENGS}
        self.esem = {}
        self.cnt = {e: 0 for e in ENGS}
        self.allsems = []
        for e in [PE, ACT, DVE, POOL]:
            self.esem[e] = nc.alloc_semaphore("s_" + e)
            self.allsems.append(self.esem[e])
        self.seen = {}
        self.stack = stack
        self.nd = 0
        self.ninst = {e: 0 for e in ENGS}
        self.dead = False
        self.stop = int(os.environ.get("KSTOP", "99"))

    def phase(self, n):
        self.dead = n > self.stop

    def dsem(self, name=None):
        self.nd += 1
        h = self.nc.alloc_semaphore(f"d{self.nd}_{name or 'm'}")
        self.allsems.append(h)
        return DSem(h)

    def _waits(self, eng, deps):
        best = {}
        for d in deps:
            if d is None:
                continue
            k = id(d.sem)
            if k not in best or best[k].val < d.val:
                best[k] = d
        ws = []
        for d in best.values():
            k = (eng, id(d.sem))
            if self.seen.get(k, 0) >= d.val:
                continue
            self.seen[k] = d.val
            ws.append((d.sem, d.val))
        return ws

    @staticmethod
    def _compact(lst):
        best = {}
        for d in lst:
            k = id(d.sem)
            if k not in best or best[k].val < d.val:
                best[k] = d
        return list(best.values())

    @staticmethod
    def _bufdeps(reads, writes, wadd=()):
        deps = []
        for b in reads:
            deps.extend(b.w)
        for b in writes:
            deps.extend(b.w)
            deps.extend(b.r)
        for b in wadd:
            deps.extend(b.r)
        return deps

    @classmethod
    def _update(cls, h, reads, writes, wadd=()):
        for b in reads:
            b.r.append(h)
            if len(b.r) > 32:
                b.r = cls._compact(b.r)
        for b in writes:
            b.w = [h]
            b.r = []
        for b in wadd:
            b.w.append(h)
            if len(b.w) > 32:
                b.w = cls._compact(b.w)

    def op(self, eng, fn, reads=(), writes=(), deps=(), sig=True, wadd=()):
        if self.dead:
            return None
        alld = list(deps) + self._bufdeps(reads, writes, wadd)
        ws = self._waits(eng, alld)
        h = None
        if sig:
            self.cnt[eng] += 1
            h = Dep(self.esem[eng], self.cnt[eng])
        sem = self.esem[eng] if sig else None
        self.ninst[eng] += 1 + len(ws)

        def run(e, ws=ws, fn=fn, sem=sem):
            for (s, v) in ws:
                e.wait_ge(s, v)
            ins = fn(e)
            if sem is not None:
                ins.then_inc(sem, 1)
        self.q[eng].append(run)
        if h is not None:
            self._update(h, reads, writes, wadd)
        return h

    def attach(self, h, reads=(), writes=(), wadd=()):
        if self.dead or h is None:
            return
        self._update(h, reads, writes, wadd)

    def dma(self, eng, ds, out, in_, reads=(), writes=(), deps=(), wadd=(), **kw):
        if self.dead:
            return None
        alld = list(deps) + self._bufdeps(reads, writes, wadd)
        ws = self._waits(eng, alld)
        ds.cnt += 16
        h = Dep(ds.h, ds.cnt)
        self.ninst[eng] += 1 + len(ws)

        def run(e, ws=ws, out=out, in_=in_, kw=kw, sh=ds.h):
            for (s, v) in ws:
                e.wait_ge(s, v)
            e.dma_start(out=out, in_=in_, **kw).then_inc(sh, 16)
        self.q[eng].append(run)
        self._update(h, reads, writes, wadd)
        return h

    def raw(self, eng, fn, ds, inc, reads=(), writes=(), deps=()):
        if self.dead:
            return None
        alld = list(deps) + self._bufdeps(reads, writes)
        ws = self._waits(eng, alld)
        ds.cnt += inc
        h = Dep(ds.h, ds.cnt)

        def run(e, ws=ws, fn=fn, sh=ds.h, inc=inc):
            for (s, v) in ws:
                e.wait_ge(s, v)
            fn(e).then_inc(sh, inc)
        self.q[eng].append(run)
        self._update(h, reads, writes)
        return h

    def wait(self, eng, deps):
        ws = self._waits(eng, deps)

        def run(e, ws=ws):
            for (s, v) in ws:
                e.wait_ge(s, v)
        self.q[eng].append(run)

    def emit(self):
        nc = self.nc
        with nc.Block() as block:
            @block.tensor
            def _(e):
                for f in self.q[PE]:
                    f(e)

            @block.scalar
            def _(e):
                for f in self.q[ACT]:
                    f(e)

            @block.vector
            def _(e):
                for f in self.q[DVE]:
                    f(e)

            @block.gpsimd
            def _(e):
                for f in self.q[POOL]:
                    f(e)

            @block.sync
            def _(e):
                for f in self.q[SP]:
                    f(e)


def _dsize(dt):
    return {F32: 4, BF16: 2, U8: 1}[dt]


class Arena:
    def __init__(self, nc, stack, nbytes):
        self.t = stack.enter_context(nc.sbuf_tensor("arena", [128, nbytes], U8))
        self.nbytes = nbytes

    def carve(self, off, shape, dt):
        n = int(np.prod(shape)) * _dsize(dt)
        assert off % 4 == 0 and off + n <= self.nbytes, (off, n, self.nbytes)
        v = self.t[:, off:off + n]
        if dt != U8:
            v = v.bitcast(dt)
        if len(shape) > 1:
            names = [f"a{i}" for i in range(len(shape))]
            pat = "p (" + " ".join(names) + ") -> p " + " ".join(names)
            v = v.rearrange(pat, **{names[i]: shape[i] for i in range(len(shape))})
        return v


O_HT = 0
O_HTH = 33024
O_CONST = 35072
O_W = 45312
O_A = 61696
O_B = 94720
O_C = 127744
ARENA = 212000
C_SIZE = ARENA - O_C


def build():
    nc = bass.Bass("TRN2", target_bir_lowering=False)

    def din(name, shape, dt=F32):
        return nc.dram_tensor(name, list(shape), dt, kind="ExternalInput").ap()

    def dout(name, shape, dt=F32):
        return nc.dram_tensor(name, list(shape), dt, kind="ExternalOutput").ap()

    xprev = din("xprev", [3, NP, D]); xp = din("xp", [NP, D]); xh = din("xh", [128, D]); xs = din("xs", [NS, D]); ccin = din("cc", [17, D])
    st_re = din("st_re", [NS, 4096]); st_im = din("st_im", [NS, 4096])
    ck = din("ck", [NS, 128, 256]); cv = din("cv", [NS, 128, 256])
    w_ada = din("w_ada", [D, 3072]); b_ada = din("b_ada", [3072]); w_in = din("w_in", [D, DIN])
    lam_re = din("lam_re", [64, 64]); lam_im = din("lam_im", [64, 64]); log_delta = din("log_delta", [64])
    b_re = din("b_re", [4096, 16]); b_im = din("b_im", [4096, 16])
    c_re = din("c_re", [1024, 64]); c_im = din("c_im", [1024, 64])
    ssm_d = din("ssm_d", [1024]); w_glu = din("w_glu", [D, D]); b_glu = din("b_glu", [D])
    sinks = din("sinks", [16]); rel_bias = din("rel_bias", [32, 16])
    w_bs = din("w_bs", [D, D]); w_ba = din("w_ba", [D, D]); w_out = din("w_out", [D, D])
    ln_g = din("ln_g", [D]); ln_b = din("ln_b", [D])
    rtab = din("rtab", [32, 384]); maskc = din("maskc", [128, 256]); bmaskc = din("bmaskc", [128, 128])
    diagc = din("diagc", [16, 64]); flagsc = din("flags", [32])

    yp = dout("yp", [NP, D]); ys = dout("ys", [NS, D])
    pst_re = dout("pst_re", [32, 128]); pst_im = dout("pst_im", [32, 128])
    pck = dout("pck", [128, 256]); pcv = dout("pcv", [128, 256])
    sst_re = dout("sst_re", [NS, 4096]); sst_im = dout("sst_im", [NS, 4096])
    sck = dout("sck", [NS, 128, 256]); scv = dout("scv", [NS, 128, 256])
    dbg = {}
    if DEBUG:
        dbg["hT"] = dout("dbg_hT", [128, 8, NT], BF16)
        dbg["uT"] = dout("dbg_uT", [128, 8, NT], BF16)
        dbg["pw"] = dout("dbg_pw", [128, 9 * 2 * 32])
        dbg["hend"] = dout("dbg_hend", [128, 2 * 32 * 16])
        dbg["bb"] = dout("dbg_bb", [128, 2 * 32 * 16]); dbg["ccm"] = dout("dbg_ccm", [128, 2 * 32 * 16])
        dbg["R8"] = dout("dbg_R8", [128, 16 * 2 * 32]); dbg["R128"] = dout("dbg_R128", [128, 16 * 2 * 32])
        dbg["A2k"] = dout("dbg_A2k", [128, 3 * 2 * 32])
        dbg["WinL"] = dout("dbg_WinL", [128, 8 * 8 * 2 * 128], BF16)
        dbg["X0"] = dout("dbg_X0", [128, 512])
        dbg["KL"] = dout("dbg_KL", [128, 2 * 8 * 128], BF16); dbg["Ca"] = dout("dbg_Ca", [128, 2 * 4 * 9 * 2 * 32], BF16)
        dbg["Hb"] = dout("dbg_Hb", [128, 2 * 2048], BF16); dbg["Xs"] = dout("dbg_Xs", [128, 2 * 2048])
        dbg["carry"] = dout("dbg_carry", [128, 17 * 2 * 32])
        dbg["yT"] = dout("dbg_yT", [128, 8, NT], BF16)
        dbg["gbs"] = dout("dbg_gbs", [128, 8, NT], BF16)
        dbg["oT"] = dout("dbg_oT", [128, 8, NT], BF16)
        dbg["mT"] = dout("dbg_mT", [128, 8, NT], BF16)
        dbg["modT"] = dout("dbg_modT", [128, 24 * 17])

    ib = nc.dram_tensor("cc_ib", [128, 64], F32, kind="Internal")
    ob = nc.dram_tensor("cc_ob", [NCORES * 128, 64], F32, kind="Internal")

    st = ExitStack()
    with st:
        P = Prog(nc, st)
        AR = Arena(nc, st, ARENA)
        cv_ = AR.carve
        ps = [st.enter_context(nc.psum_tensor(f"ps{i}", [128, 512], F32)) for i in range(8)]
        psb = [Buf(f"ps{i}") for i in range(8)]
        dout_sem = P.dsem("dout")
        out_sems = []

        def osem_new(name):
            d_ = P.dsem(name)
            out_sems.append(d_)
            return d_
        misc_sem = P.dsem("misc")

        def misc_load(eng, out, in_, buf, wadd=False, **kw):
            if P.dead:
                return None
            if wadd:
                return P.dma(eng, P.dsem(), out, in_, wadd=[buf], **kw)
            return P.dma(eng, P.dsem(), out, in_, writes=[buf], **kw)

        hT = cv_(O_HT, [8, NT], BF16)
        hTh = cv_(O_HTH, [8, 128], BF16)
        hT_b = [Buf(f"hT{i}") for i in range(len(TBS))]
        hTh_b = Buf("hTh")
        o = O_CONST
        ident = cv_(o, [128], F32); o += 512
        modT = cv_(o, [24, 17], F32); o += 1664
        op1p = cv_(o, [8, 17], F32); o += 576
        flags = cv_(o, [32], F32); o += 128
        Dm = cv_(o, [8], F32); o += 32
        bglu = cv_(o, [8], F32); o += 32
        ES = cv_(o, [4, 4, 128], BF16); o += 4096
        onesd = cv_(o, [128], BF16); o += 256
        EBself = cv_(o, [16], F32); o += 64
        bmask = cv_(o, [128], F32); o += 512
        ones1 = cv_(o, [128], F32); o += 512
        assert o <= O_CONST + 10240
        b_ident = Buf(); b_modT = Buf(); b_flags = Buf(); b_Dm = Buf(); b_bglu = Buf(); b_ES = Buf()
        b_ones = Buf(); b_EBself = Buf(); b_bmask = Buf()
        wslot = [cv_(O_W + 8192 * i, [8, 512], BF16) for i in range(2)]
        wslot_b = [Buf("w0"), Buf("w1")]
        wsem = [P.dsem("w0"), P.dsem("w1")]
        wctr = [0]
        RA = cv_(O_A, [8, NT], BF16)
        RB = cv_(O_B, [8, NT], BF16)

        rr = {"i": 0}

        def bank():
            i = rr["i"] % 8
            rr["i"] += 1
            return i

        def load_w(src2d, ncols):
            s = wctr[0] % 2
            wctr[0] += 1
            P.dma(POOL, wsem[s], wslot[s][:, :, 0:ncols], src2d.rearrange("(k p) f -> p k f", p=128),
                  writes=[wslot_b[s]])
            return s

        def evac_copy(i, out_ap, in_ap, reads, writes, wadd=()):
            if i % 2 == 0:
                return P.op(ACT, lambda e: e.activation(out=out_ap, in_=in_ap, func=AF.Copy), reads=reads, writes=writes, wadd=wadd)
            return P.op(DVE, lambda e: e.tensor_copy(out=out_ap, in_=in_ap), reads=reads, writes=writes, wadd=wadd)

        def proj_fm(src2d, ncols, rhs_of, rhs_bufs, evac, tbs=TBS):
            s = load_w(src2d, ncols)
            for oc in range(ncols // 128):
                for tbi, (t0, n) in enumerate(tbs):
                    b = bank()
                    for k in range(8):
                        last = (k == 7)
                        P.op(PE, lambda e, b=b, k=k, oc=oc, tbi=tbi, t0=t0, n=n, s=s: e.matmul(
                            ps[b][:, 0:n], lhsT=wslot[s][:, k, oc * 128:(oc + 1) * 128], rhs=rhs_of(k, t0, n),
                            start=(k == 0), stop=(k == 7)),
                            reads=[wslot_b[s], rhs_bufs[tbi]] if k == 0 else [], writes=[psb[b]] if k == 0 else [],
                            sig=last)
                        if last and not P.dead:
                            h = Dep(P.esem[PE], P.cnt[PE])
                            P.attach(h, reads=[wslot_b[s], rhs_bufs[tbi]], writes=[psb[b]])
                    evac(oc, tbi, ps[b][:, 0:n], psb[b])

        def cmul(eng, dst_r, dst_i, xr, xi, yr, yi, t1, t2, bufs_r, bufs_w, tb):
            P.op(eng, lambda e: e.tensor_tensor(out=t1, in0=xr, in1=yr, op=ALU.mult), reads=bufs_r, writes=[tb])
            P.op(eng, lambda e: e.tensor_tensor(out=t2, in0=xi, in1=yi, op=ALU.mult), reads=bufs_r, writes=[tb])
            P.op(eng, lambda e: e.tensor_tensor(out=dst_r, in0=t1, in1=t2, op=ALU.subtract), reads=[tb], writes=bufs_w)
            P.op(eng, lambda e: e.tensor_tensor(out=t1, in0=xr, in1=yi, op=ALU.mult), reads=bufs_r + bufs_w, writes=[tb])
            P.op(eng, lambda e: e.tensor_tensor(out=t2, in0=xi, in1=yr, op=ALU.mult), reads=bufs_r + bufs_w, writes=[tb])
            P.op(eng, lambda e: e.tensor_tensor(out=dst_i, in0=t1, in1=t2, op=ALU.add), reads=[tb], writes=bufs_w)

        P.phase(0)
        P.op(POOL, lambda e: e.memset(ident, 0.0), writes=[b_ident])
        P.op(POOL, lambda e: e.affine_select(out=ident, in_=ident, pattern=[[-1, 128]], compare_op=ALU.not_equal,
                                             fill=1.0, base=0, channel_multiplier=1), writes=[b_ident])
        misc_load(SP, flags, flagsc.rearrange("(o n) -> o n", o=1).to_broadcast([128, 32]), b_flags)
        misc_load(SP, bmask, bmaskc, b_bmask)
        P.op(POOL, lambda e: e.memset(ones1[0:1, :], 1.0), writes=[b_ones])
        P.op(POOL, lambda e: e.memset(onesd[0:1, 0:64], 0.0), wadd=[b_ones])
        P.op(POOL, lambda e: e.memset(onesd[0:1, 64:128], 1.0), wadd=[b_ones])

        P.phase(1)
        c_t = cv_(O_C + 62208, [1024], F32); c_sg = cv_(O_C + 66304, [1024], F32)
        ccT = cv_(O_C + 70400, [8, 17], BF16); badain = cv_(O_C + 70912, [128], F32); badaT = cv_(O_C + 71424, [24], F32)
        b_ct = Buf(); b_csg = Buf(); b_ccT = Buf(); b_bin = Buf(); b_baT = Buf()
        misc_load(SP, c_t[0:17, :], ccin, b_ct)
        misc_load(SP, badain[0:24, :], b_ada.rearrange("(c p) -> c p", p=128), b_bin)
        P.op(ACT, lambda e: e.activation(out=c_sg[0:17, :], in_=c_t[0:17, :], func=AF.Sigmoid), reads=[b_ct], writes=[b_csg])
        P.op(DVE, lambda e: e.tensor_tensor(out=c_sg[0:17, :], in0=c_sg[0:17, :], in1=c_t[0:17, :], op=ALU.mult),
             reads=[b_ct], writes=[b_csg])
        bk = bank()
        for k in range(8):
            P.op(PE, lambda e, k=k: e.transpose(out=ps[bk][:, 17 * k:17 * k + 17], in_=c_sg[0:17, 128 * k:128 * k + 128],
                                                identity=ident[0:17, 0:17]),
                 reads=[b_csg, b_ident], writes=[psb[bk]] if k == 0 else [], wadd=[psb[bk]] if k > 0 else [])
        P.op(DVE, lambda e: e.tensor_copy(out=ccT.rearrange("p k s -> p (k s)"), in_=ps[bk][:, 0:136]), reads=[psb[bk]], writes=[b_ccT])
        bk2 = bank()
        P.op(PE, lambda e: e.transpose(out=ps[bk2][:, 0:24], in_=badain[0:24, :], identity=ident[0:24, 0:24]),
             reads=[b_bin, b_ident], writes=[psb[bk2]])
        P.op(DVE, lambda e: e.tensor_copy(out=badaT, in_=ps[bk2][:, 0:24]), reads=[psb[bk2]], writes=[b_baT])
        bkm = bank()
        hlast = None
        for blk in range(6):
            s = load_w(w_ada[:, 512 * blk:512 * blk + 512], 512)
            for oc in range(4):
                f = 4 * blk + oc
                for k in range(8):
                    first = (blk == 0 and oc == 0 and k == 0)
                    lastk = (k == 7)
                    hlast = P.op(PE, lambda e, f=f, k=k, oc=oc, s=s: e.matmul(
                        ps[bkm][:, 17 * f:17 * f + 17], lhsT=wslot[s][:, k, oc * 128:(oc + 1) * 128], rhs=ccT[:, k, :],
                        start=(k == 0), stop=(k == 7)),
                        reads=[wslot_b[s], b_ccT] if k == 0 else [], writes=[psb[bkm]] if first else [], sig=lastk and oc == 3)
            P.attach(hlast, reads=[wslot_b[s]], wadd=[psb[bkm]])
        P.op(DVE, lambda e: e.tensor_tensor(out=modT, in0=ps[bkm][:, 0:408].rearrange("p (f s) -> p f s", f=24),
                                            in1=badaT.unsqueeze(2).to_broadcast([128, 24, 17]), op=ALU.add),
             reads=[psb[bkm], b_baT], writes=[b_modT])
        P.op(DVE, lambda e: e.tensor_scalar(out=op1p, in0=modT[:, 8:16, :], scalar1=1.0, scalar2=None, op0=ALU.add),
             reads=[b_modT], wadd=[b_modT])

        xst = [cv_(O_C + 71552 + 4096 * i, [1024], F32) for i in range(2)] + [cv_(O_C + 62208 + 4096 * i, [1024], F32) for i in range(2)]
        xst_b = [Buf(), Buf(), Buf(), Buf()]
        xsem = [P.dsem("x0"), P.dsem("x1"), P.dsem("x2"), P.dsem("x3")]
        handoff([xst_b[2]], [b_ct]); handoff([xst_b[3]], [b_csg])
        hTs_b = hT_b[4]
        tmpS = cv_(O_C + 79744, [8, 16], F32)
        b_tmpS = Buf()

        def phase_a(xsrc, full):
            tiles = [("p", i) for i in range(16)] + ([("h", 0), ("s", 0)] if full else [])
            for ti, (kind, i) in enumerate(tiles):
                sl = ti % 4
                if kind == "p":
                    src, rows, dst, dbuf = xsrc[128 * i:128 * i + 128, :], 128, (lambda k, i=i: hT[:, k, 128 * i:128 * i + 128]), hT_b[i // 4]
                elif kind == "h":
                    src, rows, dst, dbuf = xh, 128, (lambda k: hTh[:, k, :]), hTh_b
                else:
                    src, rows, dst, dbuf = xs, NS, None, hTs_b
                P.dma(SP, xsem[sl], xst[sl][0:rows, :], src, writes=[xst_b[sl]])
                b0, b1 = bank(), bank()
                for k in range(8):
                    bb_ = b0 if k < 4 else b1
                    j = k % 4
                    P.op(PE, lambda e, k=k, bb_=bb_, j=j, sl=sl, rows=rows: e.transpose(
                        out=ps[bb_][:, rows * j:rows * j + rows], in_=xst[sl][0:rows, 128 * k:128 * k + 128],
                        identity=ident[0:rows, 0:rows]),
                        reads=[xst_b[sl], b_ident], writes=[psb[bb_]] if j == 0 else [], wadd=[psb[bb_]] if j > 0 else [])
                if kind != "s":
                    for k in range(8):
                        bb_ = b0 if k < 4 else b1
                        j = k % 4
                        src_ps = ps[bb_][:, 128 * j:128 * j + 128]
                        if False:
                            P.op(DVE, lambda e, k=k, src_ps=src_ps, dst=dst: e.tensor_scalar(
                                out=dst(k), in0=src_ps, scalar1=op1p[:, k, 0:1], scalar2=modT[:, k, 0:1], op0=ALU.mult, op1=ALU.add),
                                reads=[psb[bb_], b_modT], wadd=[dbuf])
                        else:
                            P.op(ACT, lambda e, k=k, src_ps=src_ps, dst=dst: e.activation(
                                out=dst(k), in_=src_ps, func=AF.Identity, scale=op1p[:, k, 0:1], bias=modT[:, k, 0:1]),
                                reads=[psb[bb_], b_modT], wadd=[dbuf])
                else:
                    for half, bb_ in enumerate([b0, b1]):
                        P.op(DVE, lambda e, half=half, bb_=bb_: e.tensor_tensor(
                            out=tmpS[:, 4 * half:4 * half + 4, :], in0=ps[bb_][:, 0:64].rearrange("p (k s) -> p k s", k=4),
                            in1=op1p[:, 4 * half:4 * half + 4, 1:17], op=ALU.mult), reads=[psb[bb_], b_modT], wadd=[b_tmpS])
                    P.op(DVE, lambda e: e.tensor_tensor(out=hT[:, :, NP:NT], in0=tmpS, in1=modT[:, 0:8, 1:17], op=ALU.add),
                         reads=[b_tmpS, b_modT], wadd=[dbuf])

        uT = RA
        uT_b = [Buf(f"uT{g}") for g in range(8)]
        rhs_h = lambda k, t0, n: hT[:, k, t0:t0 + n]
        cnt = {"i": 0}

        def phase_b(full):
            for blk in range(2):
                def ev(oc, tbi, pap, pb, blk=blk):
                    t0, n = TBS[tbi]
                    g = 4 * blk + oc
                    evac_copy(0, uT[:, g, t0:t0 + n], pap, [pb], [], wadd=[uT_b[g]])
                proj_fm(w_in[:, OFF_U + 512 * blk:OFF_U + 512 * blk + 512], 512, rhs_h, hT_b, ev, tbs=TBS if full else TBS[:4])

        P.phase(3)
        phase_a(xprev[0], False)
        phase_b(False)
        P.phase(2)
        oc_ = O_C
        pw = cv_(oc_ + 0, [9, 2, 32], F32); bb = cv_(oc_ + 2304, [2, 32, 16], F32); ccm = cv_(oc_ + 6400, [2, 32, 16], F32)
        R8 = cv_(oc_ + 10496, [16, 2, 32], F32); R128 = cv_(oc_ + 14592, [16, 2, 32], F32); A2k = cv_(oc_ + 18688, [3, 2, 32], F32)
        Hend = cv_(oc_ + 19456, [2, 32, 16], F32); carry = cv_(oc_ + 23552, [17, 2, 32], F32)
        Sb = cv_(oc_ + 27904, [8, 2, 128], BF16); coef = cv_(oc_ + 32000, [12, 32], F32)
        misc = cv_(oc_ + 33536, [32, 32], F32)
        t1 = cv_(oc_ + 37632, [1024], F32); t2 = cv_(oc_ + 41728, [1024], F32)
        O_S = oc_ + 45824
        Sslot = [cv_(O_S + 8192 * i, [8, 2, 128], F32) for i in range(2)]
        O_CA = oc_ + 62208; O_KL = oc_ + 71424; O_HB = oc_ + 75520
        b_pw = Buf("pw"); b_bb = Buf("bb"); b_ccm = Buf("ccm"); b_R8 = Buf(); b_R128 = Buf(); b_A2k = Buf(); b_coef = Buf()
        b_misc = Buf("misc"); b_t = Buf("t12"); b_Sslot = [Buf("S0"), Buf("S1")]
        craw = [cv_(O_S + 2048 * i, [8, 64], F32) for i in range(2)]
        lamraw = cv_(O_S + 4096, [128], F32); ldraw = cv_(O_S + 4608, [64], F32)
        draw = cv_(O_S + 4864, [128], F32); bgraw = cv_(O_S + 5376, [128], F32)
        braw = [cv_(O_S + 8192 + 2048 * i, [32, 16], F32) for i in range(2)]
        b_craw = Buf(); b_lam = Buf(); b_ld = Buf(); b_draw = Buf(); b_braw = Buf()
        misc_load(SP, craw[0], c_re.rearrange("(t r) p -> r t p", r=128), b_craw, wadd=True)
        misc_load(SP, craw[1], c_im.rearrange("(t r) p -> r t p", r=128), b_craw, wadd=True)
        misc_load(SP, lamraw[0:64, 0:64], lam_re, b_lam, wadd=True)
        misc_load(SP, lamraw[0:64, 64:128], lam_im, b_lam, wadd=True)
        misc_load(SP, ldraw, log_delta.rearrange("(o n) -> o n", o=1).to_broadcast([128, 64]), b_ld)
        misc_load(SP, draw[0:8, :], ssm_d.rearrange("(c p) -> c p", p=128), b_draw, wadd=True)
        misc_load(SP, bgraw[0:8, :], b_glu.rearrange("(c p) -> c p", p=128), b_draw, wadd=True)
        for i, src in enumerate([b_re, b_im]):
            for q4 in range(4):
                misc_load(SP, braw[i][:, 8 * q4:8 * q4 + 8, :],
                          src[1024 * q4:1024 * q4 + 1024, :].rearrange("(gp q) c -> q gp c", q=128), b_braw, wadd=True)
        M_ = lambda i: misc[:, i, :]
        LR, LI, DT, TH, FR, FC, MAG, SN, CS, NR, DEN, CR, CI, G1, KF, TMPA = [M_(i) for i in range(16)]
        KI = misc[:, 16, :].bitcast(mybir.dt.int32)
        bkl = bank()
        P.op(PE, lambda e: e.transpose(out=ps[bkl][:, 0:64], in_=lamraw[0:64, :], identity=ident[0:64, 0:64]),
             reads=[b_lam, b_ident], writes=[psb[bkl]])
        P.op(DVE, lambda e: e.tensor_copy(out=LR[0:64, :], in_=ps[bkl][0:64, 0:64:2]), reads=[psb[bkl]], wadd=[b_misc])
        P.op(DVE, lambda e: e.tensor_copy(out=LR[64:128, :], in_=ps[bkl][0:64, 1:64:2]), reads=[psb[bkl]], wadd=[b_misc])
        P.op(DVE, lambda e: e.tensor_copy(out=LI[0:64, :], in_=ps[bkl][64:128, 0:64:2]), reads=[psb[bkl]], wadd=[b_misc])
        P.op(DVE, lambda e: e.tensor_copy(out=LI[64:128, :], in_=ps[bkl][64:128, 1:64:2]), reads=[psb[bkl]], wadd=[b_misc])
        bkd = bank()
        P.op(PE, lambda e: e.transpose(out=ps[bkd][:, 0:8], in_=draw[0:8, :], identity=ident[0:8, 0:8]),
             reads=[b_draw, b_ident], writes=[psb[bkd]])
        P.op(PE, lambda e: e.transpose(out=ps[bkd][:, 8:16], in_=bgraw[0:8, :], identity=ident[0:8, 0:8]),
             reads=[b_draw, b_ident], wadd=[psb[bkd]])
        P.op(DVE, lambda e: e.tensor_copy(out=Dm, in_=ps[bkd][:, 0:8]), reads=[psb[bkd]], writes=[b_Dm])
        P.op(DVE, lambda e: e.tensor_copy(out=bglu, in_=ps[bkd][:, 8:16]), reads=[psb[bkd]], writes=[b_bglu])
        for ri in range(2):
            for hb in range(2):
                bkc = bank()
                for tt in range(4):
                    t_ = 4 * hb + tt
                    P.op(PE, lambda e, ri=ri, t_=t_, tt=tt, bkc=bkc: e.transpose(
                        out=ps[bkc][0:64, 128 * tt:128 * tt + 128], in_=craw[ri][:, t_, :], identity=ident),
                        reads=[b_craw, b_ident], writes=[psb[bkc]] if tt == 0 else [], wadd=[psb[bkc]] if tt > 0 else [])
                for g2 in range(2):
                    src = ps[bkc][0:64, :].rearrange("p (tg g2 c) -> p tg g2 c", g2=2, c=16)[:, :, g2, :]
                    P.op(DVE, lambda e, ri=ri, hb=hb, g2=g2, src=src: e.tensor_copy(
                        out=ccm[64 * g2:64 * g2 + 64, ri, 16 * hb:16 * hb + 16, :], in_=src),
                        reads=[psb[bkc]], wadd=[b_ccm])
        P.op(ACT, lambda e: e.activation(out=DT[0:64, :], in_=ldraw[0:64, 0:64:2], func=AF.Exp), reads=[b_ld], wadd=[b_misc])
        P.op(ACT, lambda e: e.activation(out=DT[64:128, :], in_=ldraw[64:128, 1:64:2], func=AF.Exp), reads=[b_ld], wadd=[b_misc])
        G = DVE
        tt_ = lambda out, a, b_, op, **kw: P.op(G, lambda e: e.tensor_tensor(out=out, in0=a, in1=b_, op=op), reads=[b_misc], wadd=[b_misc], **kw)
        ts_ = lambda out, a, s1, s2, o0, o1: P.op(G, lambda e: e.tensor_scalar(out=out, in0=a, scalar1=s1, scalar2=s2, op0=o0, op1=o1), reads=[b_misc], wadd=[b_misc])
        tt_(TH, LI, DT, ALU.mult)
        ts_(FR, TH, 1.0 / (2 * math.pi), 0.0, ALU.mult, ALU.add)
        P.op(DVE, lambda e: e.tensor_copy(out=KI, in_=FR), reads=[b_misc], wadd=[b_misc])
        P.op(DVE, lambda e: e.tensor_copy(out=KF, in_=KI), reads=[b_misc], wadd=[b_misc])
        tt_(FR, FR, KF, ALU.subtract)
        ts_(FC, FR, 1.0, 0.25, ALU.mult, ALU.add)
        P.op(DVE, lambda e: e.tensor_single_scalar(out=G1, in_=FC, scalar=0.5, op=ALU.is_gt), reads=[b_misc], wadd=[b_misc])
        tt_(FC, FC, G1, ALU.subtract)
        TWO_PI = 2.0 * math.pi
        P.op(ACT, lambda e: e.activation(out=SN, in_=FR, func=AF.Sin, scale=TWO_PI), reads=[b_misc], wadd=[b_misc])
        P.op(ACT, lambda e: e.activation(out=CS, in_=FC, func=AF.Sin, scale=TWO_PI), reads=[b_misc], wadd=[b_misc])
        tt_(TMPA, LR, DT, ALU.mult)
        P.op(ACT, lambda e: e.activation(out=MAG, in_=TMPA, func=AF.Exp), reads=[b_misc], wadd=[b_misc])
        P.op(G, lambda e: e.memset(pw[:, 0, 0, :], 1.0), wadd=[b_pw])
        P.op(G, lambda e: e.memset(pw[:, 0, 1, :], 0.0), wadd=[b_pw])
        P.op(G, lambda e: e.tensor_tensor(out=pw[:, 1, 0, :], in0=MAG, in1=CS, op=ALU.mult), reads=[b_misc], wadd=[b_pw])
        P.op(G, lambda e: e.tensor_tensor(out=pw[:, 1, 1, :], in0=MAG, in1=SN, op=ALU.mult), reads=[b_misc], wadd=[b_pw])

        def cm(dst, x, y, n, rb, wb):
            T1 = t1[:, 0:n * 32].rearrange("p (n g) -> p n g", n=n)
            T2 = t2[:, 0:n * 32].rearrange("p (n g) -> p n g", n=n)
            cmul(G, dst[:, :, 0, :], dst[:, :, 1, :], x[:, :, 0, :], x[:, :, 1, :], y[:, :, 0, :], y[:, :, 1, :], T1, T2, rb, wb, b_t)

        def bc(ap1, n):
            return ap1.to_broadcast([128, n, 2, 32])

        cm(pw[:, 2:3], pw[:, 1:2], pw[:, 1:2], 1, [b_pw], [b_pw])
        cm(pw[:, 3:5], pw[:, 1:3], bc(pw[:, 2:3], 2), 2, [b_pw], [b_pw])
        cm(pw[:, 5:9], pw[:, 1:5], bc(pw[:, 4:5], 4), 4, [b_pw], [b_pw])
        tt_(NR, pw[:, 1, 0, :], pw[:, 0, 0, :], ALU.subtract, deps=b_pw.w)
        tt_(DEN, LR, LR, ALU.mult)
        tt_(TMPA, LI, LI, ALU.mult)
        tt_(DEN, DEN, TMPA, ALU.add)
        P.op(DVE, lambda e: e.reciprocal(out=DEN, in_=DEN), reads=[b_misc], wadd=[b_misc])
        tt_(CR, NR, LR, ALU.mult)
        tt_(TMPA, pw[:, 1, 1, :], LI, ALU.mult)
        tt_(CR, CR, TMPA, ALU.add)
        tt_(CR, CR, DEN, ALU.mult)
        tt_(CI, pw[:, 1, 1, :], LR, ALU.mult)
        tt_(TMPA, NR, LI, ALU.mult)
        tt_(CI, CI, TMPA, ALU.subtract)
        tt_(CI, CI, DEN, ALU.mult)
        CRb = CR.unsqueeze(2).to_broadcast([128, 32, 16]); CIb = CI.unsqueeze(2).to_broadcast([128, 32, 16])
        T1b = t1[:, 0:512].rearrange("p (g c) -> p g c", g=32); T2b = t2[:, 0:512].rearrange("p (g c) -> p g c", g=32)
        P.op(G, lambda e: e.tensor_tensor(out=T1b, in0=braw[0], in1=CRb, op=ALU.mult), reads=[b_braw, b_misc, b_pw], writes=[b_t])
        P.op(G, lambda e: e.tensor_tensor(out=T2b, in0=braw[1], in1=CIb, op=ALU.mult), reads=[b_braw, b_misc], wadd=[b_t])
        P.op(G, lambda e: e.tensor_tensor(out=bb[:, 0], in0=T1b, in1=T2b, op=ALU.subtract), reads=[b_t], wadd=[b_bb])
        P.op(G, lambda e: e.tensor_tensor(out=T1b, in0=braw[1], in1=CRb, op=ALU.mult), reads=[b_braw, b_misc, b_bb], writes=[b_t])
        P.op(G, lambda e: e.tensor_tensor(out=T2b, in0=braw[0], in1=CIb, op=ALU.mult), reads=[b_braw, b_misc], wadd=[b_t])
        P.op(G, lambda e: e.tensor_tensor(out=bb[:, 1], in0=T1b, in1=T2b, op=ALU.add), reads=[b_t], wadd=[b_bb])

        def rev_table(R, A0, bufR, out_last):
            AW = misc[:, 20:22, :].rearrange("p (o r) g -> p o r g", o=1)
            AW2 = misc[:, 22:24, :].rearrange("p (o r) g -> p o r g", o=1)
            P.op(G, lambda e: e.memset(R[:, 15, 0, :], 1.0), wadd=[bufR])
            P.op(G, lambda e: e.memset(R[:, 15, 1, :], 0.0), wadd=[bufR])
            P.op(G, lambda e: e.tensor_copy(out=AW, in_=A0), reads=[b_pw, b_coef, b_misc], wadd=[b_misc])
            w = 1
            cur, nxt = AW, AW2
            while w <= 8:
                cm(R[:, 16 - 2 * w:16 - w], R[:, 16 - w:16], bc(cur, w), w, [bufR, b_misc], [bufR])
                cm(nxt, cur, cur, 1, [b_misc], [b_misc])
                cur, nxt = nxt, cur
                w *= 2
            P.op(G, lambda e: e.tensor_copy(out=out_last, in_=cur), reads=[b_misc], wadd=[b_coef])

        A128 = coef[:, 0:2, :].rearrange("p (o r) g -> p o r g", o=1)
        A2048 = coef[:, 2:4, :].rearrange("p (o r) g -> p o r g", o=1)
        rev_table(R8, pw[:, 8:9], b_R8, A128)
        rev_table(R128, A128, b_R128, A2048)
        P.op(G, lambda e: e.memset(A2k[:, 0, 0, :], 1.0), wadd=[b_A2k])
        P.op(G, lambda e: e.memset(A2k[:, 0, 1, :], 0.0), wadd=[b_A2k])
        P.op(G, lambda e: e.tensor_copy(out=A2k[:, 1:2], in_=A2048), reads=[b_coef], wadd=[b_A2k])
        cm(A2k[:, 2:3], A2048, A2048, 1, [b_coef], [b_A2k])
        P.op(G, lambda e: e.tensor_scalar(out=coef[:, 4, :], in0=pw[:, 8, 1, :], scalar1=-1.0, scalar2=0.0, op0=ALU.mult, op1=ALU.add),
             reads=[b_pw], wadd=[b_coef])
        P.op(G, lambda e: e.tensor_scalar(out=coef[:, 5, :], in0=coef[:, 1, :], scalar1=-1.0, scalar2=0.0, op0=ALU.mult, op1=ALU.add),
             reads=[b_coef], wadd=[b_coef])

        P.phase(5)
        WinL = cv_(O_B, [8, 8, 2, 128], BF16)
        b_WinL = [Buf(f"WinL{g}") for g in range(8)]
        b_Sb = Buf("Sb")
        handoff(b_Sslot, [b_craw, b_lam, b_ld, b_draw, b_braw])
        b_SInit = [Buf("SInit0"), Buf("SInit1")]
        for i in range(2):
            P.op(POOL, lambda e, i=i: e.memset(Sslot[i].rearrange("p k r c -> p (k r c)"), 0.0), writes=[b_Sslot[i], b_SInit[i]])
        T1e = [t1[:, 0:512].rearrange("p (k m c) -> p k m c", k=8, m=4), t1[:, 512:1024].rearrange("p (k m c) -> p k m c", k=8, m=4)]
        T2e = [t2[:, 0:512].rearrange("p (k m c) -> p k m c", k=8, m=4), t2[:, 512:1024].rearrange("p (k m c) -> p k m c", k=8, m=4)]
        b_t5 = [b_t, Buf("t5pool")]
        handoff([b_t5[1]], [b_t])
        for gc in range(8):
            sl = gc % 2
            EG = DVE if gc % 2 == 0 else POOL
            T1s, T2s, b_tt = T1e[gc % 2], T2e[gc % 2], b_t5[gc % 2]
            S = Sslot[sl]
            Sv = S.rearrange("p k r (m g c) -> p k r m g c", m=4, g=2)
            prk = pw[:, 0:8, 0, 4 * gc:4 * gc + 4].unsqueeze(3).to_broadcast([128, 8, 4, 16])
            pik = pw[:, 0:8, 1, 4 * gc:4 * gc + 4].unsqueeze(3).to_broadcast([128, 8, 4, 16])
            bbr = bb[:, 0, 4 * gc:4 * gc + 4, :].unsqueeze(1).to_broadcast([128, 8, 4, 16])
            bbi = bb[:, 1, 4 * gc:4 * gc + 4, :].unsqueeze(1).to_broadcast([128, 8, 4, 16])
            for ri in range(2):
                x1, x2 = (bbr, bbi) if ri == 0 else (bbi, bbr)
                op = ALU.subtract if ri == 0 else ALU.add
                P.op(EG, lambda e, x1=x1, prk=prk, T1s=T1s: e.tensor_tensor(out=T1s, in0=prk, in1=x1, op=ALU.mult), reads=[b_pw, b_bb], writes=[b_tt])
                P.op(EG, lambda e, x2=x2, pik=pik, T2s=T2s: e.tensor_tensor(out=T2s, in0=pik, in1=x2, op=ALU.mult), reads=[b_pw, b_bb], wadd=[b_tt])
                for g2 in range(2):
                    lo, hi = 64 * g2, 64 * g2 + 64
                    P.op(EG, lambda e, ri=ri, g2=g2, lo=lo, hi=hi, op=op, Sv=Sv, T1s=T1s, T2s=T2s: e.tensor_tensor(
                        out=Sv[lo:hi, :, ri, :, g2, :], in0=T1s[lo:hi], in1=T2s[lo:hi], op=op),
                        reads=[b_tt, b_SInit[sl]], wadd=[b_Sslot[sl]])
            P.op(EG, lambda e, gc=gc, S=S: e.tensor_copy(out=Sb[:, gc], in_=S[:, 0]), reads=[b_Sslot[sl]], wadd=[b_Sb])
            for q4 in range(4):
                bkw = bank()
                for j in range(4):
                    k_, ri_ = (4 * q4 + j) // 2, (4 * q4 + j) % 2
                    P.op(PE, lambda e, bkw=bkw, j=j, k_=k_, ri_=ri_, S=S: e.transpose(
                        out=ps[bkw][:, 128 * j:128 * j + 128], in_=S[:, k_, ri_, :], identity=ident),
                        reads=[b_Sslot[sl], b_ident], writes=[psb[bkw]] if j == 0 else [], wadd=[psb[bkw]] if j > 0 else [])
                dstw = WinL[:, gc, 2 * q4:2 * q4 + 2].rearrange("p k r c -> p (k r c)")
                evac_copy(q4, dstw, ps[bkw][:, 0:512], [psb[bkw]], [], wadd=[b_WinL[gc]])

        t1p = cv_(O_S, [2, 16, 16], F32); t2p = cv_(O_S + 2048, [2, 16, 16], F32); cbp = cv_(O_S + 4096, [2, 16, 16], F32)
        b_p1 = Buf("p1tmp")
        handoff([b_p1], b_Sslot)
        b_Hend = Buf("Hend")

        def x_matmuls(gc, banks):
            for ri in range(2):
                for s_ in range(8):
                    for m in range(4):
                        first = (ri == 0 and s_ == 0)
                        last = (ri == 1 and s_ == 7)
                        bkx = banks[m]
                        P.op(PE, lambda e, m=m, ri=ri, s_=s_, bkx=bkx, gc=gc: e.matmul(
                            ps[bkx][:, 256 * ri:256 * ri + 256], lhsT=WinL[32 * m:32 * m + 32, gc, 7 - s_, ri, :],
                            rhs=uT[32 * m:32 * m + 32, gc, s_:NP:8], start=(s_ == 0), stop=(s_ == 7),
                            tile_position=(32 * m, 0)),
                            reads=[b_WinL[gc], uT_b[gc]] if first else [], writes=[psb[bkx]] if first else [], sig=last)
                        if last and not P.dead:
                            P.attach(Dep(P.esem[PE], P.cnt[PE]), reads=[b_WinL[gc], uT_b[gc]], writes=[psb[bkx]])

        def seg_reduce(src_ap, Rtab, gp0, ngp, out_ap, rbufs, wbuf):
            raise NotImplementedError

        def pass1():
            for gc in range(8):
                banks = [bank() for _ in range(4)]
                x_matmuls(gc, banks)
                if DEBUG and gc == 0 and os.environ.get('KX0'):
                    xdbg = cv_(O_S + 6144, [512], F32); b_xdbg = Buf()
                    P.op(DVE, lambda e: e.tensor_copy(out=xdbg, in_=ps[banks[1]][:, :]), reads=[psb[banks[1]]], writes=[b_xdbg])
                    P.dma(SP, dout_sem, dbg["X0"], xdbg, reads=[b_xdbg])
                for m in range(4):
                    gp = 4 * gc + m
                    X4 = ps[banks[m]][:, :].rearrange("p (r s i) -> p r s i", r=2, s=16)
                    Pr = R8[:, :, 0, gp].unsqueeze(1).unsqueeze(1).to_broadcast([128, 2, 16, 16])
                    Pi = R8[:, :, 1, gp].unsqueeze(1).unsqueeze(1).to_broadcast([128, 2, 16, 16])
                    P.op(DVE, lambda e, X4=X4, Pr=Pr: e.tensor_tensor(out=t1p, in0=X4, in1=Pr, op=ALU.mult),
                         reads=[psb[banks[m]], b_R8], writes=[b_p1])
                    P.op(DVE, lambda e, X4=X4, Pi=Pi: e.tensor_tensor(out=t2p, in0=X4, in1=Pi, op=ALU.mult),
                         reads=[psb[banks[m]], b_R8], wadd=[b_p1])
                    P.op(DVE, lambda e: e.tensor_tensor(out=cbp[:, 0], in0=t1p[:, 0], in1=t2p[:, 1], op=ALU.subtract), reads=[b_p1], wadd=[b_p1])
                    P.op(DVE, lambda e: e.tensor_tensor(out=cbp[:, 1], in0=t2p[:, 0], in1=t1p[:, 1], op=ALU.add), reads=[b_p1], wadd=[b_p1])
                    P.op(DVE, lambda e, gp=gp: e.tensor_reduce(out=Hend[:, :, gp, :], in_=cbp, axis=AX.X, op=ALU.add),
                         reads=[b_p1], wadd=[b_Hend])


        Ecore = cv_(O_S + 6144, [2, 32], F32); Eall = cv_(O_S + 6400, [8, 64], F32)
        te1 = cv_(O_S + 0, [2, 32, 16], F32); te2 = cv_(O_S + 8448, [2, 32, 16], F32)
        b_E = Buf("Ecore"); b_Eall = Buf("Eall"); b_te = b_p1
        handoff([b_E, b_Eall], b_Sslot)
        def ecore(j):
            Qr = R128[:, :, 0, :].rearrange("p i g -> p g i").unsqueeze(1).to_broadcast([128, 2, 32, 16])
            Qi = R128[:, :, 1, :].rearrange("p i g -> p g i").unsqueeze(1).to_broadcast([128, 2, 32, 16])
            P.op(DVE, lambda e: e.tensor_tensor(out=te1, in0=Hend, in1=Qr, op=ALU.mult), reads=[b_Hend, b_R128], writes=[b_te])
            P.op(DVE, lambda e: e.tensor_tensor(out=te2, in0=Hend, in1=Qi, op=ALU.mult), reads=[b_Hend, b_R128], wadd=[b_te])
            P.op(DVE, lambda e: e.tensor_tensor(out=te1[:, 0], in0=te1[:, 0], in1=te2[:, 1], op=ALU.subtract), reads=[b_te], writes=[b_te])
            P.op(DVE, lambda e: e.tensor_tensor(out=te2[:, 0], in0=te2[:, 0], in1=te1[:, 1], op=ALU.add), reads=[b_te], writes=[b_te])
            P.op(DVE, lambda e: e.tensor_reduce(out=Ecore[:, 0, :], in_=te1[:, 0], axis=AX.X, op=ALU.add), reads=[b_te], wadd=[b_E])
            P.op(DVE, lambda e: e.tensor_reduce(out=Ecore[:, 1, :], in_=te2[:, 0], axis=AX.X, op=ALU.add), reads=[b_te], wadd=[b_E])

            if j is not None:
                P.op(DVE, lambda e, j=j: e.tensor_copy(out=Eall[:, j, :], in_=Ecore.rearrange("p r g -> p (r g)")), reads=[b_E], wadd=[b_Eall])

        P.phase(3)
        srcs = [(xprev[1], False), (xprev[2], False), (xp, True)]
        phase_a(*srcs[0])
        for j in range(3):
            pass1()
            ecore(j)
            phase_b(srcs[j][1])
            if j < 2:
                phase_a(*srcs[j + 1])
        P.phase(6)
        pass1()
        P.phase(7)
        Sn = [cv_(O_S + 12544 + 256 * n, [2, 32], F32) for n in range(3)]
        b_Sn = Buf("Sn")
        handoff([b_Sn], b_Sslot)
        for n in range(3):
            Snf = Sn[n].rearrange("p r g -> p (r g)")
            P.op(DVE, lambda e, n=n, Snf=Snf: e.tensor_scalar(out=Snf, in0=Eall[:, n, :], scalar1=flags[:, 1 + n:2 + n], scalar2=None,
                                                            op0=ALU.mult), reads=[b_Eall, b_flags], wadd=[b_Sn])
        b_carry = Buf("carry")
        tq1 = cv_(O_S + 13312, [2, 32], F32); tq2 = cv_(O_S + 13568, [2, 32], F32)
        b_tq = Buf("tq")
        handoff([b_tq], b_Sslot)

        def cmul_small(dst, x, y, rb, wb):
            cmul(DVE, dst[:, 0, :], dst[:, 1, :], x[:, 0, :], x[:, 1, :], y[:, 0, :], y[:, 1, :], tq1[:, 0, :], tq1[:, 1, :], rb, wb, b_tq)

        cmul_small(carry[:, 1], Sn[1], A2k[:, 1], [b_Sn, b_A2k], [b_carry])
        cmul_small(carry[:, 2], Sn[2], A2k[:, 2], [b_Sn, b_A2k, b_carry], [b_carry])
        P.op(DVE, lambda e: e.tensor_tensor(out=Sn[0], in0=Sn[0], in1=carry[:, 1], op=ALU.add), reads=[b_Sn, b_carry], writes=[b_Sn])
        P.op(DVE, lambda e: e.tensor_tensor(out=carry[:, 0], in0=Sn[0], in1=carry[:, 2], op=ALU.add), reads=[b_Sn, b_carry], writes=[b_carry])
        A128v = coef[:, 0:2, :]
        for sg in range(16):
            cmul_small(carry[:, sg + 1], carry[:, sg], A128v, [b_carry, b_coef], [b_carry])
            P.op(DVE, lambda e, sg=sg: e.tensor_tensor(out=carry[:, sg + 1], in0=carry[:, sg + 1], in1=Hend[:, :, :, sg], op=ALU.add),
                 reads=[b_carry, b_Hend], writes=[b_carry])
        pstT = cv_(O_S + 13824, [2, 128], F32)
        b_pst = Buf()
        handoff([b_pst], b_Sslot)
        bkp = bank()
        for ri in range(2):
            P.op(PE, lambda e, ri=ri: e.transpose(out=ps[bkp][0:32, 128 * ri:128 * ri + 128], in_=carry[:, 16, ri, :], identity=ident),
                 reads=[b_carry, b_ident], writes=[psb[bkp]] if ri == 0 else [], wadd=[psb[bkp]] if ri == 1 else [])
        P.op(DVE, lambda e: e.tensor_copy(out=pstT[0:32].rearrange("p r c -> p (r c)"), in_=ps[bkp][0:32, 0:256]), reads=[psb[bkp]], writes=[b_pst])
        P.dma(SP, osem_new("pre"), pst_re, pstT[0:32, 0, :], reads=[b_pst])
        P.dma(SP, osem_new("pim"), pst_im, pstT[0:32, 1, :], reads=[b_pst])

        P.phase(8)
        CaBD = [cv_(O_CA + 4608 * i, [4, 9, 2, 32], BF16) for i in range(2)]
        KLs = [cv_(O_KL + 2048 * i, [8, 128], BF16) for i in range(2)]
        Hb = [cv_(O_HB + 4096 * i, [2, 4, 256], BF16) for i in range(2)]
        Xs = [cv_(O_S + 8192 * i, [2, 4, 256], F32) for i in range(2)]
        b_CaBD = [Buf("Ca0"), Buf("Ca1")]; b_KL = [Buf("KL0"), Buf("KL1")]; b_Hb = [Buf("Hb0"), Buf("Hb1")]; b_Xs = [Buf("Xs0"), Buf("Xs1")]
        old_c = [b_ct, b_csg, b_ccT, b_bin, b_baT] + xst_b
        handoff(b_CaBD + b_KL + b_Hb, old_c)
        handoff(b_Xs, [b_p1, b_E, b_Eall, b_te, b_Sn, b_tq, b_pst] + b_Sslot)
        KL0all = cv_(O_C + 10496, [8, 128], BF16); Ca1all = cv_(O_C + 14592, [32, 2, 32], BF16)
        b_KL0 = Buf("KL0all"); b_Ca1 = Buf("Ca1all")
        handoff([b_KL0], [b_R8]); handoff([b_Ca1], [b_R128])
        Q1e = [misc[:, 24:28, :].rearrange("p a g -> p (a g)").rearrange("p (r m s) -> p r m s", r=2, m=4),
               misc[:, 0:4, :].rearrange("p a g -> p (a g)").rearrange("p (r m s) -> p r m s", r=2, m=4)]
        Q2e = [misc[:, 28:32, :].rearrange("p a g -> p (a g)").rearrange("p (r m s) -> p r m s", r=2, m=4),
               misc[:, 4:8, :].rearrange("p a g -> p (a g)").rearrange("p (r m s) -> p r m s", r=2, m=4)]
        tmpK = misc[:, 16:20, :].rearrange("p a g -> p (a g)")
        b_Q = [Buf("Qdve"), Buf("Qpool")]
        b_tmpK = Buf("tmpK")
        handoff([b_tmpK] + b_Q, [b_misc])
        b_CaInit = [Buf("CaInit0"), Buf("CaInit1")]
        for i in range(2):
            P.op(POOL, lambda e, i=i: e.memset(CaBD[i].rearrange("p m n r c -> p (m n r c)"), 0.0), writes=[b_CaBD[i], b_CaInit[i]])
        U1 = t1[:, 0:576].rearrange("p (m n c) -> p m n c", m=4, n=9)
        U2 = t2[:, 0:576].rearrange("p (m n c) -> p m n c", m=4, n=9)
        XB = [2, 3, 4, 5]
        YB = [6, 7]

        def emit_consts(gc):
            sl = gc % 2
            Ca = CaBD[sl]
            cre = ccm[:, 0, 4 * gc:4 * gc + 4, :].unsqueeze(2).to_broadcast([128, 4, 9, 16])
            cim = ccm[:, 1, 4 * gc:4 * gc + 4, :].unsqueeze(2).to_broadcast([128, 4, 9, 16])
            pr = pw[:, :, 0, 4 * gc:4 * gc + 4].rearrange("p n m -> p m n").unsqueeze(3).to_broadcast([128, 4, 9, 16])
            pi = pw[:, :, 1, 4 * gc:4 * gc + 4].rearrange("p n m -> p m n").unsqueeze(3).to_broadcast([128, 4, 9, 16])
            P.op(DVE, lambda e: e.tensor_tensor(out=U1, in0=cre, in1=pr, op=ALU.mult), reads=[b_ccm, b_pw], writes=[b_t])
            P.op(DVE, lambda e: e.tensor_tensor(out=U2, in0=cim, in1=pi, op=ALU.mult), reads=[b_ccm, b_pw], wadd=[b_t])
            for g2 in range(2):
                lo, hi = 64 * g2, 64 * g2 + 64
                P.op(DVE, lambda e, lo=lo, hi=hi, g2=g2: e.tensor_tensor(
                    out=Ca[lo:hi, :, :, 0, 16 * g2:16 * g2 + 16], in0=U1[lo:hi], in1=U2[lo:hi], op=ALU.subtract),
                    reads=[b_t, b_CaInit[sl]], wadd=[b_CaBD[sl]])
            P.op(DVE, lambda e: e.tensor_tensor(out=U1, in0=cre, in1=pi, op=ALU.mult), reads=[b_ccm, b_pw], writes=[b_t])
            P.op(DVE, lambda e: e.tensor_tensor(out=U2, in0=cim, in1=pr, op=ALU.mult), reads=[b_ccm, b_pw], wadd=[b_t])
            P.op(DVE, lambda e: e.tensor_tensor(out=U1, in0=U1, in1=U2, op=ALU.add), reads=[b_t], writes=[b_t])
            for g2 in range(2):
                lo, hi = 64 * g2, 64 * g2 + 64
                P.op(DVE, lambda e, lo=lo, hi=hi, g2=g2: e.tensor_scalar(
                    out=Ca[lo:hi, :, :, 1, 16 * g2:16 * g2 + 16], in0=U1[lo:hi], scalar1=-1.0, scalar2=0.0, op0=ALU.mult, op1=ALU.add),
                    reads=[b_t, b_CaInit[sl]], wadd=[b_CaBD[sl]])
            P.op(DVE, lambda e: e.tensor_copy(out=Ca1all[:, 4 * gc:4 * gc + 4], in_=Ca[:, :, 1, :, :]), reads=[b_CaBD[sl]], wadd=[b_Ca1])
            for hb in range(2):
                for tt in range(4):
                    tau = 4 * hb + tt
                    for ri in range(2):
                        P.op(PE, lambda e, hb=hb, tt=tt, tau=tau, ri=ri: e.matmul(
                            ps[hb][:, 128 * tt:128 * tt + 128], lhsT=Sb[:, gc, ri, :], rhs=Ca[:, :, tau, ri, :],
                            start=(ri == 0), stop=(ri == 1)),
                            reads=[b_Sb, b_CaBD[sl]] if (tt == 0 and ri == 0) else [],
                            writes=[psb[hb]] if (tt == 0 and ri == 0) else [], sig=(tt == 3 and ri == 1))
                if not P.dead:
                    P.attach(Dep(P.esem[PE], P.cnt[PE]), reads=[b_Sb, b_CaBD[sl]], writes=[psb[hb]])
            KL = KLs[sl]
            bmb3 = bmask.unsqueeze(1).to_broadcast([128, 3, 128]); bmb4 = bmask.unsqueeze(1).to_broadcast([128, 4, 128])
            P.op(DVE, lambda e: e.tensor_tensor(out=KL[:, 1:4, :], in0=ps[0][:, 128:512].rearrange("p (t c) -> p t c", t=3), in1=bmb3, op=ALU.mult),
                 reads=[psb[0], b_bmask], wadd=[b_KL[sl]])
            P.op(DVE, lambda e: e.tensor_tensor(out=tmpK, in0=ps[0][:, 0:128], in1=bmask, op=ALU.mult), reads=[psb[0], b_bmask], writes=[b_tmpK])
            P.op(DVE, lambda e: e.tensor_tensor(out=KL[:, 4:8, :], in0=ps[1][:, 0:512].rearrange("p (t c) -> p t c", t=4), in1=bmb4, op=ALU.mult),
                 reads=[psb[1], b_bmask], wadd=[b_KL[sl]])
            P.op(DVE, lambda e: e.scalar_tensor_tensor(out=KL[:, 0, :], in0=ident, scalar=Dm[:, gc:gc + 1], in1=tmpK, op0=ALU.mult, op1=ALU.add),
                 reads=[b_tmpK, b_ident, b_Dm], wadd=[b_KL[sl]])
            P.op(DVE, lambda e: e.tensor_copy(out=KL0all[:, gc, :], in_=KL[:, 0, :]), reads=[b_KL[sl]], wadd=[b_KL0])

        def emit_x_scan(gc):
            sl = gc % 2
            x_matmuls(gc, XB)
            X = Xs[sl]
            for m in range(4):
                P.op(ACT, lambda e, m=m: e.activation(out=X[:, :, m, :], in_=ps[XB[m]][:, :].rearrange("p (r j) -> p r j", r=2), func=AF.Copy),
                     reads=[psb[XB[m]]], wadd=[b_Xs[sl]])
            E, qi = (POOL, 1) if gc in (1, 4, 6) else (DVE, 0)
            Q1 = Q1e[qi]; Q2 = Q2e[qi]
            X5 = X.rearrange("p r m (s i) -> p r m s i", i=16)
            Ar = pw[:, 8, 0, 4 * gc:4 * gc + 4].unsqueeze(1).unsqueeze(3).to_broadcast([128, 2, 4, 16])
            Ai = pw[:, 8, 1, 4 * gc:4 * gc + 4].unsqueeze(2).to_broadcast([128, 4, 16])
            AiN = coef[:, 4, 4 * gc:4 * gc + 4].unsqueeze(2).to_broadcast([128, 4, 16])
            cview = carry[:, 0:16, :, 4 * gc:4 * gc + 4].rearrange("p s r m -> p r m s")
            for i in range(16):
                prev = cview if i == 0 else X5[:, :, :, :, i - 1]
                cur = X5[:, :, :, :, i]
                rb = [b_carry, b_pw, b_coef, b_Xs[sl]]
                P.op(E, lambda e, prev=prev: e.tensor_tensor(out=Q1, in0=prev, in1=Ar, op=ALU.mult), reads=rb, writes=[b_Q[qi]])
                P.op(E, lambda e, prev=prev: e.tensor_tensor(out=Q2[:, 0], in0=prev[:, 1], in1=AiN, op=ALU.mult), reads=rb, wadd=[b_Q[qi]])
                P.op(E, lambda e, prev=prev: e.tensor_tensor(out=Q2[:, 1], in0=prev[:, 0], in1=Ai, op=ALU.mult), reads=rb, wadd=[b_Q[qi]])
                P.op(E, lambda e, cur=cur: e.tensor_tensor(out=cur, in0=cur, in1=Q1, op=ALU.add), reads=[b_Q[qi]], writes=[b_Xs[sl]])
                P.op(E, lambda e, cur=cur: e.tensor_tensor(out=cur, in0=cur, in1=Q2, op=ALU.add), reads=[b_Q[qi]], writes=[b_Xs[sl]])

        def emit_hb(gc):
            sl = gc % 2
            X = Xs[sl]
            X5 = X.rearrange("p r m (s i) -> p r m s i", i=16)
            cview = carry[:, 0:16, :, 4 * gc:4 * gc + 4].rearrange("p s r m -> p r m s")
            H5 = Hb[sl].rearrange("p r m (s i) -> p r m s i", i=16)
            P.op(ACT, lambda e: e.activation(out=H5[:, :, :, :, 1:16].rearrange("p r m s i -> p (r m) s i"),
                                             in_=X5[:, :, :, :, 0:15].rearrange("p r m s i -> p (r m) s i"), func=AF.Copy),
                 reads=[b_Xs[sl]], writes=[b_Hb[sl]])
            P.op(ACT, lambda e: e.activation(out=H5[:, :, :, :, 0], in_=cview, func=AF.Copy), reads=[b_carry], wadd=[b_Hb[sl]])

        def emit_y(gc):
            sl = gc % 2
            KL = KLs[sl]; Ca = CaBD[sl]
            uview = uT[:, gc, 0:NP].rearrange("p (j s) -> p s j", s=8)
            for half in (1, 0):
                for tl in range(4):
                    t_lo = 4 * half + tl
                    bk_ = YB[tl // 2]
                    reg = ps[bk_][:, 256 * (tl % 2):256 * (tl % 2) + 256]
                    n_mm = (t_lo + 1) + 8
                    idx = 0
                    for s_ in range(t_lo + 1):
                        P.op(PE, lambda e, reg=reg, s_=s_, t_lo=t_lo: e.matmul(
                            reg, lhsT=KL[:, t_lo - s_, :], rhs=uview[:, s_, :], start=(s_ == 0), stop=False),
                            reads=[b_KL[sl], uT_b[gc], b_CaBD[sl], b_Hb[sl]] if idx == 0 else [],
                            writes=[psb[bk_]] if (idx == 0 and tl % 2 == 0) else [], sig=False)
                        idx += 1
                    for m in range(4):
                        for ri in range(2):
                            lastmm = (m == 3 and ri == 1)
                            P.op(PE, lambda e, reg=reg, m=m, ri=ri, t_lo=t_lo, lastmm=lastmm: e.matmul(
                                reg[32 * m:32 * m + 32, :], lhsT=Ca[:, m, t_lo + 1, ri, :], rhs=Hb[sl][:, ri, m, :],
                                start=False, stop=(ri == 1), tile_position=(0, 32 * m)), sig=lastmm)
                    if not P.dead:
                        P.attach(Dep(P.esem[PE], P.cnt[PE]), reads=[b_KL[sl], uT_b[gc], b_CaBD[sl], b_Hb[sl]],
                                 writes=[psb[bk_]] if tl % 2 == 1 else [], wadd=[psb[bk_]] if tl % 2 == 0 else [])
                for bi in range(2):
                    t0_ = 4 * half + 2 * bi
                    P.op(ACT, lambda e, bi=bi, t0_=t0_: e.activation(
                        out=uview[:, t0_:t0_ + 2, :], in_=ps[YB[bi]][:, :].rearrange("p (t j) -> p t j", t=2), func=AF.Gelu_apprx_tanh),
                        reads=[psb[YB[bi]]], writes=[uT_b[gc]])

        for g0 in range(2):
            emit_consts(g0)
            emit_x_scan(g0)
            emit_hb(g0)
        for gc in range(8):
            if gc + 2 < 8:
                emit_x_scan(gc + 2)
            emit_y(gc)
            if gc + 2 < 8:
                emit_consts(gc + 2)
                emit_hb(gc + 2)
        yT = uT
        yT_b = uT_b

        P.phase(9)
        all_ssm_tmp = b_Xs + b_Hb + b_Q + [b_tmpK, b_t, b_p1, b_E, b_Eall, b_te, b_Sn, b_tq, b_pst] + b_Sslot
        stile = [cv_(O_S + 2048 * i, [512], F32) for i in range(2)]
        Hsp = cv_(O_S + 4096, [2, 32, 16], F32); Hn = cv_(O_S + 8192, [2, 32, 16], F32)
        HbS = cv_(O_S + 12288, [2, 32, 16], BF16); Q1s = cv_(O_HB, [2, 32, 16], F32); Q2s = cv_(O_HB + 4096, [2, 32, 16], F32)
        b_stile = Buf(); b_Hsp = Buf(); b_Hn = Buf(); b_HbS = Buf(); b_Qs = Buf()
        handoff([b_stile, b_Hsp, b_Hn, b_HbS, b_Qs], all_ssm_tmp)
        misc_load(SP, stile[0], st_re.rearrange("s (gh f) -> (s gh) f", gh=8), b_stile, wadd=True)
        misc_load(SP, stile[1], st_im.rearrange("s (gh f) -> (s gh) f", gh=8), b_stile, wadd=True)
        for ri in range(2):
            bks = bank()
            for q4 in range(4):
                P.op(PE, lambda e, ri=ri, q4=q4, bks=bks: e.transpose(out=ps[bks][:, 128 * q4:128 * q4 + 128],
                                                                   in_=stile[ri][:, 128 * q4:128 * q4 + 128], identity=ident),
                     reads=[b_stile, b_ident], writes=[psb[bks]] if q4 == 0 else [], wadd=[psb[bks]] if q4 > 0 else [])
            for q4 in range(4):
                P.op(DVE, lambda e, ri=ri, q4=q4, bks=bks: e.tensor_copy(
                    out=Hsp[:, ri, q4:32:4, :], in_=ps[bks][:, 128 * q4:128 * q4 + 128].rearrange("p (s gh) -> p gh s", gh=8)),
                    reads=[psb[bks]], wadd=[b_Hsp])
        P.op(POOL, lambda e: e.tensor_copy(out=HbS, in_=Hsp), reads=[b_Hsp], writes=[b_HbS])
        xsb = [bank() for _ in range(4)]
        for m in range(4):
            for gc in range(8):
                for ri in range(2):
                    first = (gc == 0 and ri == 0); last = (gc == 7 and ri == 1)
                    P.op(PE, lambda e, m=m, gc=gc, ri=ri: e.matmul(
                        ps[xsb[m]][:, 32 * gc + 16 * ri:32 * gc + 16 * ri + 16], lhsT=WinL[32 * m:32 * m + 32, gc, 0, ri, :],
                        rhs=uT[32 * m:32 * m + 32, gc, NP:NT], start=True, stop=True, tile_position=(32 * m, 0)),
                        reads=b_WinL + uT_b if first else [], writes=[psb[xsb[m]]] if first else [], sig=last)
            if not P.dead:
                P.attach(Dep(P.esem[PE], P.cnt[PE]), reads=b_WinL + uT_b, writes=[psb[xsb[m]]])
        Ar1 = pw[:, 1, 0, :].unsqueeze(1).unsqueeze(3).to_broadcast([128, 2, 32, 16])
        Ai1 = pw[:, 1, 1, :].unsqueeze(2).to_broadcast([128, 32, 16])
        P.op(POOL, lambda e: e.tensor_scalar(out=coef[:, 6, :], in0=pw[:, 1, 1, :], scalar1=-1.0, scalar2=0.0, op0=ALU.mult, op1=ALU.add),
             reads=[b_pw], wadd=[b_coef])
        AiN1 = coef[:, 6, :].unsqueeze(2).to_broadcast([128, 32, 16])
        P.op(DVE, lambda e: e.tensor_tensor(out=Q1s, in0=Hsp, in1=Ar1, op=ALU.mult), reads=[b_Hsp, b_pw], writes=[b_Qs])
        P.op(DVE, lambda e: e.tensor_tensor(out=Q2s[:, 0], in0=Hsp[:, 1], in1=AiN1, op=ALU.mult), reads=[b_Hsp, b_coef], wadd=[b_Qs])
        P.op(DVE, lambda e: e.tensor_tensor(out=Q2s[:, 1], in0=Hsp[:, 0], in1=Ai1, op=ALU.mult), reads=[b_Hsp, b_pw], wadd=[b_Qs])
        P.op(DVE, lambda e: e.tensor_tensor(out=Hn, in0=Q1s, in1=Q2s, op=ALU.add), reads=[b_Qs], writes=[b_Hn])
        for m in range(4):
            P.op(DVE, lambda e, m=m: e.tensor_tensor(
                out=Hn[:, :, m:32:4, :], in0=ps[xsb[m]][:, 0:256].rearrange("p (gc r s) -> p r gc s", gc=8, r=2),
                in1=Hn[:, :, m:32:4, :], op=ALU.add), reads=[psb[xsb[m]], b_Hn], writes=[b_Hn])
        stg = cv_(O_HB + 8192 - 8192, [4, 128], F32)
        sout = [cv_(O_S + 2048 * i, [512], F32) for i in range(2)]
        b_stg = Buf(); b_sout = Buf()
        handoff([b_stg], [b_Qs]); handoff([b_sout], [b_stile])
        for ri in range(2):
            for q4 in range(4):
                P.op(POOL, lambda e, ri=ri, q4=q4: e.tensor_copy(out=stg[:, q4, :].rearrange("p (s gh) -> p gh s", gh=8),
                                                               in_=Hn[:, ri, q4:32:4, :]), reads=[b_Hn], writes=[b_stg] if q4 == 0 else [],
                     wadd=[b_stg] if q4 > 0 else [])
            bks = bank()
            for q4 in range(4):
                P.op(PE, lambda e, q4=q4, bks=bks: e.transpose(out=ps[bks][:, 128 * q4:128 * q4 + 128], in_=stg[:, q4, :], identity=ident),
                     reads=[b_stg, b_ident], writes=[psb[bks]] if q4 == 0 else [], wadd=[psb[bks]] if q4 > 0 else [])
            P.op(DVE, lambda e, ri=ri, bks=bks: e.tensor_copy(out=sout[ri], in_=ps[bks][:, 0:512]), reads=[psb[bks]], wadd=[b_sout])
            P.dma(SP, osem_new(f"sst{ri}"), (sst_re if ri == 0 else sst_im).rearrange("s (gh f) -> (s gh) f", gh=8), sout[ri], reads=[b_sout])
        bky = bank()
        for gc in range(8):
            reg = ps[bky][:, 16 * gc:16 * gc + 16]
            P.op(PE, lambda e, gc=gc, reg=reg: e.matmul(reg, lhsT=KL0all[:, gc, :], rhs=uT[:, gc, NP:NT], start=True, stop=False),
                 reads=[b_KL0, b_Ca1, b_HbS] + uT_b if gc == 0 else [], writes=[psb[bky]] if gc == 0 else [], sig=False)
            for m in range(4):
                for ri in range(2):
                    lastmm = (m == 3 and ri == 1)
                    P.op(PE, lambda e, gc=gc, reg=reg, m=m, ri=ri, lastmm=lastmm: e.matmul(
                        reg[32 * m:32 * m + 32, :], lhsT=Ca1all[:, 4 * gc + m, ri, :], rhs=HbS[:, ri, 4 * gc + m, :],
                        start=False, stop=(ri == 1), tile_position=(0, 32 * m)), sig=(lastmm and gc == 7))
        if not P.dead:
            P.attach(Dep(P.esem[PE], P.cnt[PE]), reads=[b_KL0, b_Ca1, b_HbS] + uT_b, writes=[psb[bky]])
        P.op(ACT, lambda e: e.activation(out=uT[:, :, NP:NT], in_=ps[bky][:, 0:128].rearrange("p (g s) -> p g s", g=8),
                                         func=AF.Gelu_apprx_tanh), reads=[psb[bky]], writes=uT_b)

        P.phase(10)
        s2T = RB
        s2_b = [Buf(f"s2_{g}") for g in range(8)]
        handoff(s2_b, b_WinL)
        gtmp = [cv_(O_C + 1024 * i, [512], BF16) for i in range(4)]
        ftmp = [cv_(O_C + 4096 + 2048 * i, [512], F32) for i in range(2)]
        b_gtmp = [Buf() for _ in range(4)]; b_ftmp = [Buf(), Buf()]
        handoff(b_gtmp + b_ftmp, [b_pw, b_bb, b_ccm])
        rhs_y = lambda k, t0, n: yT[:, k, t0:t0 + n]
        yall_b = [yT_b] * 5
        ctr = {"i": 0}

        class AllOf:
            pass
        for blk in range(2):
            def ev_glu(oc, tbi, pap, pb, blk=blk):
                t0, n = TBS[tbi]
                g = 4 * blk + oc
                ctr["i"] += 1
                gi = ctr["i"] % 4
                P.op(ACT, lambda e: e.activation(out=gtmp[gi][:, 0:n], in_=pap, func=AF.Sigmoid, bias=bglu[:, g:g + 1], scale=1.0),
                     reads=[pb, b_bglu], writes=[b_gtmp[gi]])
                P.op(DVE, lambda e: e.tensor_tensor(out=s2T[:, g, t0:t0 + n], in0=yT[:, g, t0:t0 + n], in1=gtmp[gi][:, 0:n], op=ALU.mult),
                     reads=[b_gtmp[gi], yT_b[g]], wadd=[s2_b[g]])
            proj_fm(w_glu[:, 512 * blk:512 * blk + 512], 512, rhs_y, [BufGroup(yT_b)] * 5, ev_glu)

            def ev_zs(oc, tbi, pap, pb, blk=blk):
                t0, n = TBS[tbi]
                g = 4 * blk + oc
                ctr["i"] += 1
                gi = ctr["i"] % 4
                fi = ctr["i"] % 2
                P.op(ACT, lambda e: e.activation(out=gtmp[gi][:, 0:n], in_=pap, func=AF.Sigmoid), reads=[pb], writes=[b_gtmp[gi]])
                P.op(DVE, lambda e: e.tensor_tensor(out=ftmp[fi][:, 0:n], in0=pap, in1=gtmp[gi][:, 0:n], op=ALU.mult),
                     reads=[pb, b_gtmp[gi]], writes=[b_ftmp[fi]])
                P.op(DVE, lambda e: e.tensor_tensor(out=s2T[:, g, t0:t0 + n], in0=s2T[:, g, t0:t0 + n], in1=ftmp[fi][:, 0:n], op=ALU.mult),
                     reads=[b_ftmp[fi], s2_b[g]], wadd=[s2_b[g]])
            proj_fm(w_in[:, OFF_ZS + 512 * blk:OFF_ZS + 512 * blk + 512], 512, rhs_h, hT_b, ev_zs)

        P.phase(11)
        gbs = RA
        gbs_b = [Buf(f"gbs{g}") for g in range(8)]
        handoff(gbs_b, yT_b)
        rhs_s2 = lambda k, t0, n: s2T[:, k, t0:t0 + n]
        for blk in range(2):
            def ev_gs(oc, tbi, pap, pb, blk=blk):
                t0, n = TBS[tbi]
                g = 4 * blk + oc
                P.op(ACT, lambda e: e.activation(out=gbs[:, g, t0:t0 + n], in_=pap, func=AF.Sigmoid), reads=[pb], wadd=[gbs_b[g]])
            proj_fm(w_in[:, OFF_GS + 512 * blk:OFF_GS + 512 * blk + 512], 512, rhs_h, hT_b, ev_gs)

            def ev_bs(oc, tbi, pap, pb, blk=blk):
                t0, n = TBS[tbi]
                g = 4 * blk + oc
                P.op(DVE, lambda e: e.tensor_tensor(out=gbs[:, g, t0:t0 + n], in0=pap, in1=gbs[:, g, t0:t0 + n], op=ALU.mult),
                     reads=[pb, gbs_b[g]], wadd=[gbs_b[g]])
            proj_fm(w_bs[:, 512 * blk:512 * blk + 512], 512, rhs_s2, [BufGroup(s2_b)] * 5, ev_bs)

        P.phase(12)
        oT = RB
        oT_b = [Buf(f"oT{g}") for g in range(8)]
        handoff(oT_b, s2_b)
        oc_ = O_C
        qT = cv_(oc_ + 0, [2, NT], BF16); kT2 = cv_(oc_ + 8256, [128 + NT], BF16); Vaug = cv_(oc_ + 12640, [18, 128], BF16)
        EB = cv_(oc_ + 17248, [2, 16, 128], BF16); EB0 = cv_(oc_ + 25440, [16, 128], BF16)
        Et = [cv_(oc_ + 29536 + 1024 * i, [512], BF16) for i in range(4)]
        PT = [cv_(oc_ + 33632 + 1024 * i, [512], BF16) for i in range(4)]
        rc = [cv_(oc_ + 37728 + 2048 * i, [512], F32) for i in range(2)]
        maskt = cv_(oc_ + 41824, [2, 128], F32); RT = cv_(oc_ + 42848, [384], F32); relb = cv_(oc_ + 44384, [16], F32)
        es16 = cv_(oc_ + 44448, [16], F32); klast = cv_(oc_ + 44512, [256], F32); vlast = cv_(oc_ + 45536, [256], F32)
        knew = cv_(oc_ + 46560, [256], F32); vnew = cv_(oc_ + 47584, [256], F32)
        Kc = cv_(oc_ + 48608, [16, 256], F32)
        KcT = cv_(oc_ + 64992, [16, 2, 128], BF16)
        Vcs = cv_(oc_ + 73184, [16, 256], BF16)
        QsT = cv_(oc_ + 81376, [2, 4, 16], BF16)
        dgt = cv_(oc_ + 81632, [64], F32); vnb = cv_(oc_ + 81888, [256], BF16); pdg = cv_(oc_ + 82400, [64], BF16)
        esr = cv_(oc_ + 82528, [64], BF16); ebs = cv_(oc_ + 82656, [16], F32); rcs = cv_(oc_ + 82720, [128], F32)
        ptS = cv_(oc_ + 83232, [128], BF16); ones_k = cv_(oc_ + 83488, [128], BF16)
        attn_bufs = {n: Buf(n) for n in ["qT", "kT2", "Vaug", "EB", "EB0", "mask", "RT", "relb", "es16", "klast", "vlast", "knew", "vnew",
                                         "Kc", "KcT", "Vcs", "QsT", "dgt", "vnb", "pdg", "esr", "ebs", "rcs", "ptS", "ones_k"]}
        A = attn_bufs
        b_Et = [Buf() for _ in range(4)]; b_PT = [Buf() for _ in range(4)]; b_rc = [Buf(), Buf()]
        prev_c = [b_pw, b_bb, b_ccm, b_R8, b_R128, b_A2k, b_Hend, b_carry, b_Sb, b_coef, b_misc, b_t, b_KL0, b_Ca1,
                  b_stile, b_Hsp, b_Hn, b_HbS, b_Qs, b_stg, b_sout] + b_gtmp + b_ftmp + all_ssm_tmp + b_CaBD + b_KL
        handoff(list(A.values()) + b_Et + b_PT + b_rc, prev_c)
        misc_load(SP, RT[0:32, :], rtab, A["RT"]); misc_load(SP, relb[0:32, :], rel_bias, A["relb"])
        misc_load(SP, maskt.rearrange("p h q -> p (h q)"), maskc, A["mask"])
        misc_load(SP, es16[0:1, :], sinks.rearrange("(o n) -> o n", o=1), A["es16"])
        misc_load(SP, ebs[0:16, :], rel_bias[0:1, :].to_broadcast([16, 16]), A["ebs"])
        misc_load(SP, dgt[0:16, :], diagc, A["dgt"])
        P.op(ACT, lambda e: e.activation(out=es16[0:1, :], in_=es16[0:1, :], func=AF.Exp), reads=[A["es16"]], writes=[A["es16"]])
        P.op(ACT, lambda e: e.activation(out=ebs[0:16, :], in_=ebs[0:16, :], func=AF.Exp), reads=[A["ebs"]], writes=[A["ebs"]])
        for kv in range(4):
            for sl_, i in enumerate([0, 2, 1, 3]):
                h = 4 * kv + i
                P.op(DVE, lambda e, kv=kv, sl_=sl_, h=h: e.tensor_copy(out=ES[0:1, kv, sl_, :], in_=es16[0:1, h:h + 1].to_broadcast([1, 128])),
                     reads=[A["es16"]], wadd=[b_ES])
        P.op(POOL, lambda e: e.memset(ones_k, 1.0), writes=[A["ones_k"]])
        for half in range(2):
            for qb in range(4):
                bke = bank()
                for qq in range(32):
                    q = 32 * qb + qq
                    st_ = (127 - q) if half == 0 else (255 - q)
                    P.op(PE, lambda e, bke=bke, qq=qq, st_=st_: e.matmul(ps[bke][:, 16 * qq:16 * qq + 16], lhsT=RT[0:32, st_:st_ + 128],
                                                                       rhs=relb[0:32, :], start=True, stop=True),
                         reads=[A["RT"], A["relb"]] if qq == 0 else [], writes=[psb[bke]] if qq == 0 else [], sig=(qq == 31))
                if not P.dead:
                    P.attach(Dep(P.esem[PE], P.cnt[PE]), reads=[A["RT"], A["relb"]], writes=[psb[bke]])
                P.op(ACT, lambda e, bke=bke, half=half, qb=qb: e.activation(
                    out=EB[:, half, :, 32 * qb:32 * qb + 32], in_=ps[bke][:, 0:512].rearrange("p (q h) -> p h q", h=16), func=AF.Exp),
                    reads=[psb[bke]], wadd=[A["EB"]])
        P.op(DVE, lambda e: e.tensor_tensor(out=EB, in0=EB, in1=maskt.unsqueeze(2).to_broadcast([128, 2, 16, 128]), op=ALU.mult),
             reads=[A["EB"], A["mask"]], writes=[A["EB"]])
        P.op(DVE, lambda e: e.tensor_scalar(out=EB0, in0=EB[:, 0], scalar1=flags[:, 0:1], scalar2=None, op0=ALU.mult),
             reads=[A["EB"], b_flags], writes=[A["EB0"]])
        P.dma(SP, dout_sem, sck[:, 0:127, :], ck[:, 1:128, :])
        P.dma(SP, dout_sem, scv[:, 0:127, :], cv[:, 1:128, :])
        kc_sem = P.dsem("kc")
        P.dma(SP, kc_sem, Kc, ck.rearrange("s t f -> t s f"), writes=[A["Kc"]])
        for s_ in range(NS):
            bkt = bank()
            for kvp in range(2):
                P.op(PE, lambda e, s_=s_, kvp=kvp, bkt=bkt: e.transpose(out=ps[bkt][:, 128 * kvp:128 * kvp + 128],
                                                                     in_=Kc[:, s_, 128 * kvp:128 * kvp + 128], identity=ident),
                     reads=[A["Kc"], b_ident], writes=[psb[bkt]] if kvp == 0 else [], wadd=[psb[bkt]] if kvp == 1 else [])
            evac_copy(s_, KcT[:, s_].rearrange("p a t -> p (a t)"), ps[bkt][:, 0:256], [psb[bkt]], [], wadd=[A["KcT"]])
        P.dma(SP, kc_sem, Kc, cv.rearrange("s t f -> t s f"), reads=[A["KcT"]], writes=[A["Kc"]])
        P.op(POOL, lambda e: e.tensor_copy(out=Vcs, in_=Kc), reads=[A["Kc"]], writes=[A["Vcs"]])
        P.op(POOL, lambda e: e.memset(Vaug[:, :, 64:128], 1.0), writes=[A["Vaug"]])

        rhs_hh = lambda k, t0, n: hTh[:, k, 0:n]
        TB5 = TBS
        def attn_kv(kv):
            def ev_q(oc, tbi, pap, pb):
                t0, n = TBS[tbi]
                ctr["i"] += 1
                evac_copy(ctr["i"], qT[:, oc, t0:t0 + n], pap, [pb], [], wadd=[A["qT"]])
            A["qT"].r = list(A["qT"].r) + list(A["qT"].w); A["qT"].w = []
            proj_fm(w_in[:, OFF_Q + 256 * kv:OFF_Q + 256 * kv + 256], 256, rhs_h, hT_b, ev_q)
            s = wctr[0] % 2
            wctr[0] += 1
            for dup in range(2):
                P.dma(POOL, wsem[s], wslot[s][:, :, 64 * dup:64 * dup + 64],
                      w_in[:, OFF_K + 64 * kv:OFF_K + 64 * kv + 64].rearrange("(k p) f -> p k f", p=128),
                      writes=[wslot_b[s]] if dup == 0 else [], wadd=[wslot_b[s]] if dup == 1 else [])
            A["kT2"].r = list(A["kT2"].r) + list(A["kT2"].w); A["kT2"].w = []
            kblocks = [(hTh, hTh_b, 0, 128, 0)] + [(hT, hT_b[i], t0, n, 128 + t0) for i, (t0, n) in enumerate(TBS)]
            for bi_, (src, sb_, t0, n, c0) in enumerate(kblocks):
                bkk = bank()
                for k in range(8):
                    P.op(PE, lambda e, bkk=bkk, k=k, src=src, t0=t0, n=n, s=s: e.matmul(
                        ps[bkk][:, 0:n], lhsT=wslot[s][:, k, 0:128], rhs=src[:, k, t0:t0 + n], start=(k == 0), stop=(k == 7)),
                        reads=[wslot_b[s], sb_] if k == 0 else [], writes=[psb[bkk]] if k == 0 else [], sig=(k == 7))
                if not P.dead:
                    P.attach(Dep(P.esem[PE], P.cnt[PE]), reads=[wslot_b[s], sb_], writes=[psb[bkk]])
                evac_copy(bi_, kT2[:, c0:c0 + n], ps[bkk][:, 0:n], [psb[bkk]], [], wadd=[A["kT2"]])
            bkl_ = bank()
            for j_, (c0_, m_) in enumerate([(NP - 128, 128), (NP, NS)]):
                for k in range(8):
                    P.op(PE, lambda e, j_=j_, c0_=c0_, m_=m_, k=k, s=s: e.matmul(
                        ps[bkl_][0:m_, 64 * j_:64 * j_ + 64], lhsT=hT[:, k, c0_:c0_ + m_], rhs=wslot[s][:, k, 0:64],
                        start=(k == 0), stop=(k == 7)),
                        reads=[wslot_b[s], hT_b[3], hT_b[4]] if (k == 0 and j_ == 0) else [],
                        writes=[psb[bkl_]] if (k == 0 and j_ == 0) else [], sig=(k == 7 and j_ == 1))
            if not P.dead:
                P.attach(Dep(P.esem[PE], P.cnt[PE]), reads=[wslot_b[s], hT_b[3], hT_b[4]], writes=[psb[bkl_]])
            P.op(DVE, lambda e, kv=kv: e.tensor_copy(out=klast[:, 64 * kv:64 * kv + 64], in_=ps[bkl_][:, 0:64]), reads=[psb[bkl_]], wadd=[A["klast"]])
            P.op(DVE, lambda e, kv=kv: e.tensor_copy(out=knew[0:NS, 64 * kv:64 * kv + 64], in_=ps[bkl_][0:NS, 64:128]), reads=[psb[bkl_]], wadd=[A["knew"]])
            s = wctr[0] % 2
            wctr[0] += 1
            P.dma(POOL, wsem[s], wslot[s][:, :, 0:64], w_in[:, OFF_V + 64 * kv:OFF_V + 64 * kv + 64].rearrange("(k p) f -> p k f", p=128),
                  writes=[wslot_b[s]])
            A["Vaug"].r = list(A["Vaug"].r) + list(A["Vaug"].w); A["Vaug"].w = []
            vtiles = [(hTh, hTh_b, 0, 128)] + [(hT, hT_b[i // 4], 128 * i, 128) for i in range(16)] + [(hT, hT_b[4], NP, NS)]
            for grp in range(3):
                bkv = bank()
                tl_ = vtiles[8 * grp:8 * grp + 8]
                for j_, (src, sb_, c0_, m_) in enumerate(tl_):
                    for k in range(8):
                        firstg = (j_ == 0 and k == 0)
                        P.op(PE, lambda e, bkv=bkv, j_=j_, src=src, c0_=c0_, m_=m_, k=k, s=s: e.matmul(
                            ps[bkv][0:m_, 64 * j_:64 * j_ + 64], lhsT=src[:, k, c0_:c0_ + m_], rhs=wslot[s][:, k, 0:64],
                            start=(k == 0), stop=(k == 7)),
                            reads=[wslot_b[s], sb_, hTh_b] + hT_b if firstg else [], writes=[psb[bkv]] if firstg else [],
                            sig=(j_ == len(tl_) - 1 and k == 7))
                if not P.dead:
                    P.attach(Dep(P.esem[PE], P.cnt[PE]), reads=[wslot_b[s], hTh_b] + hT_b, writes=[psb[bkv]])
                nt_ = len(tl_)
                if grp < 2:
                    P.op(ACT, lambda e, bkv=bkv, grp=grp: e.activation(out=Vaug[:, 8 * grp:8 * grp + 8, 0:64],
                                                                     in_=ps[bkv][:, 0:512].rearrange("p (t d) -> p t d", d=64), func=AF.Copy),
                         reads=[psb[bkv]], wadd=[A["Vaug"]])
                    if grp == 1:
                        pass
                else:
                    P.op(ACT, lambda e, bkv=bkv: e.activation(out=Vaug[:, 16, 0:64], in_=ps[bkv][:, 0:64], func=AF.Copy),
                         reads=[psb[bkv]], wadd=[A["Vaug"]])
                    P.op(ACT, lambda e, bkv=bkv: e.activation(out=Vaug[0:NS, 17, 0:64], in_=ps[bkv][0:NS, 64:128], func=AF.Copy),
                         reads=[psb[bkv]], wadd=[A["Vaug"]])
                    P.op(ACT, lambda e, bkv=bkv, kv=kv: e.activation(out=vlast[:, 64 * kv:64 * kv + 64], in_=ps[bkv][:, 0:64], func=AF.Copy),
                         reads=[psb[bkv]], wadd=[A["vlast"]])
                    P.op(ACT, lambda e, bkv=bkv, kv=kv: e.activation(out=vnew[0:NS, 64 * kv:64 * kv + 64], in_=ps[bkv][0:NS, 64:128], func=AF.Copy),
                         reads=[psb[bkv]], wadd=[A["vnew"]])
            qv = lambda base, b_: qT[base:base + 64, 0:2, 128 * b_:128 * b_ + 128]
            def attn_s1(b_):
                ia = (2 * b_) % 4; ib = (2 * b_ + 1) % 4
                bA, bB = bank(), bank()
                kprev = slice(128 * b_, 128 * b_ + 128); kcur = slice(128 * b_ + 128, 128 * b_ + 256)
                seq = [(bA, 0, 0, kprev), (bB, 64, 0, kprev), (bA, 0, 1, kcur), (bB, 64, 1, kcur)]
                for (bk_, base, half, ks) in seq:
                    first = (half == 0)
                    P.op(PE, lambda e, bk_=bk_, base=base, half=half, ks=ks, b_=b_: e.matmul(
                        ps[bk_][:, 256 * half:256 * half + 256], lhsT=kT2[base:base + 64, ks], rhs=qv(base, b_), start=True, stop=True),
                        reads=[A["kT2"], A["qT"]] if first else [], writes=[psb[bk_]] if first else [], sig=(half == 1))
                    if half == 1 and not P.dead:
                        P.attach(Dep(P.esem[PE], P.cnt[PE]), reads=[A["kT2"], A["qT"]], writes=[psb[bk_]])
                for (bk_, ie, base_h) in [(bA, ia, 0), (bB, ib, 1)]:
                    P.op(ACT, lambda e, bk_=bk_, ie=ie: e.activation(out=Et[ie], in_=ps[bk_][:, :], func=AF.Exp, scale=0.125),
                         reads=[psb[bk_]], writes=[b_Et[ie]])
                    Ev = Et[ie].rearrange("p (h i q) -> p h i q", h=2, i=2)
                    Pv = PT[ie].rearrange("p (h i q) -> p h i q", h=2, i=2)
                    h0 = 4 * kv + base_h
                    eng = DVE
                    if b_ > 0:
                        P.op(eng, lambda e, Ev=Ev, Pv=Pv, h0=h0: e.tensor_tensor(out=Pv, in0=Ev, in1=EB[:, :, h0:h0 + 3:2, :], op=ALU.mult),
                             reads=[b_Et[ie], A["EB"]], writes=[b_PT[ie]])
                    else:
                        P.op(eng, lambda e, Ev=Ev, Pv=Pv, h0=h0: e.tensor_tensor(out=Pv[:, 0], in0=Ev[:, 0], in1=EB0[:, h0:h0 + 3:2, :], op=ALU.mult),
                             reads=[b_Et[ie], A["EB0"]], writes=[b_PT[ie]])
                        P.op(eng, lambda e, Ev=Ev, Pv=Pv, h0=h0: e.tensor_tensor(out=Pv[:, 1], in0=Ev[:, 1], in1=EB[:, 1, h0:h0 + 3:2, :], op=ALU.mult),
                             reads=[b_Et[ie], A["EB"]], wadd=[b_PT[ie]])

            def attn_s2(b_):
                ia = (2 * b_) % 4; ib = (2 * b_ + 1) % 4
                bO = bank()
                mm = [(ia, 0, b_, True), (ia, 1, b_ + 1, False), (ib, 0, b_, False), (ib, 1, b_ + 1, False)]
                for j_, (ip, half, tile, st_) in enumerate(mm):
                    cols = slice(0, 256) if ip == ia else slice(256, 512)
                    P.op(PE, lambda e, ip=ip, half=half, tile=tile, st_=st_, cols=cols: e.matmul(
                        ps[bO][:, cols], lhsT=Vaug[:, tile, :], rhs=PT[ip][:, 256 * half:256 * half + 256], start=st_, stop=False),
                        reads=[A["Vaug"], b_PT[ia], b_PT[ib], b_ES, b_ones] if j_ == 0 else [], writes=[psb[bO]] if j_ == 0 else [], sig=False)
                P.op(PE, lambda e, kv=kv: e.matmul(ps[bO][:, 0:512], lhsT=onesd[0:1, :], rhs=ES[0:1, kv].rearrange("p i q -> p (i q)"),
                                                   start=False, stop=True), sig=True)
                if not P.dead:
                    P.attach(Dep(P.esem[PE], P.cnt[PE]), reads=[A["Vaug"], b_PT[ia], b_PT[ib], b_ES, b_ones], writes=[psb[bO]])
                ir = b_ % 2
                P.op(ACT, lambda e, ir=ir: e.activation(out=rc[ir][64:128, :], in_=ps[bO][64:128, :], func=AF.Ln), reads=[psb[bO]], writes=[b_rc[ir]])
                P.op(ACT, lambda e, ir=ir: e.activation(out=rc[ir][64:128, :], in_=rc[ir][64:128, :], func=AF.Exp, scale=-1.0),
                     reads=[b_rc[ir]], writes=[b_rc[ir]])
                for par in range(2):
                    P.op(DVE, lambda e, par=par, ir=ir, b_=b_, kv=kv: e.tensor_tensor(
                        out=oT[64 * par:64 * par + 64, 2 * kv:2 * kv + 2, 128 * b_:128 * b_ + 128],
                        in0=ps[bO][0:64, 256 * par:256 * par + 256].rearrange("p (c q) -> p c q", c=2),
                        in1=rc[ir][64:128, 256 * par:256 * par + 256].rearrange("p (c q) -> p c q", c=2), op=ALU.mult),
                        reads=[psb[bO], b_rc[ir]], wadd=[oT_b[2 * kv], oT_b[2 * kv + 1]])

            attn_s1(0)
            for b_ in range(1, 16):
                attn_s1(b_)
                attn_s2(b_ - 1)
            attn_s2(15)
            base = 64 * (kv % 2)
            for i in range(4):
                hsrc = 64 * (i % 2)
                P.op(POOL, lambda e, i=i, hsrc=hsrc, base=base: e.tensor_copy(out=QsT[base:base + 64, 0, i, :], in_=qT[hsrc:hsrc + 64, i // 2, NP:NT]),
                     reads=[A["qT"]], writes=[A["QsT"]] if i == 0 else [], wadd=[A["QsT"]] if i > 0 else [])
            for sl_, i in enumerate([0, 1, 2, 3]):
                P.op(DVE, lambda e, i=i, kv=kv: e.tensor_copy(out=esr[0:1, :].rearrange("p (s i) -> p s i", i=4)[:, :, i],
                                                            in_=es16[0:1, 4 * kv + i:4 * kv + i + 1].to_broadcast([1, 16])),
                     reads=[A["es16"]], writes=[A["esr"]] if i == 0 else [], wadd=[A["esr"]] if i > 0 else [])
            bS, bD, bN = bank(), bank(), bank()
            Qsi = QsT[base:base + 64, 0].rearrange("p i s -> p s i")
            for s_ in range(NS):
                P.op(PE, lambda e, s_=s_, base=base, kv=kv: e.matmul(ps[bS][:, 4 * s_:4 * s_ + 4], lhsT=KcT[base:base + 64, s_, kv // 2, :],
                                                                  rhs=QsT[base:base + 64, 0, :, s_], start=True, stop=True),
                     reads=[A["KcT"], A["QsT"], A["kT2"]] if s_ == 0 else [], writes=[psb[bS]] if s_ == 0 else [], sig=False)
            P.op(PE, lambda e, base=base: e.matmul(ps[bS][0:NS, 64:128], lhsT=kT2[base:base + 64, 128 + NP:128 + NT], rhs=Qsi, start=True, stop=True), sig=True)
            if not P.dead:
                P.attach(Dep(P.esem[PE], P.cnt[PE]), reads=[A["KcT"], A["QsT"], A["kT2"]], writes=[psb[bS]])
            P.op(ACT, lambda e: e.activation(out=ptS[:, 0:64], in_=ps[bS][:, 0:64], func=AF.Exp, scale=0.125), reads=[psb[bS]], writes=[A["ptS"]])
            P.op(ACT, lambda e: e.activation(out=pdg[0:NS, :], in_=ps[bS][0:NS, 64:128], func=AF.Exp, scale=0.125), reads=[psb[bS]], writes=[A["pdg"]])
            P.op(DVE, lambda e, kv=kv: e.tensor_tensor(out=ptS[:, 0:64].rearrange("p (s i) -> p s i", i=4), in0=ptS[:, 0:64].rearrange("p (s i) -> p s i", i=4),
                                                in1=EB[:, 0, 4 * kv:4 * kv + 4, 0].unsqueeze(1).to_broadcast([128, 16, 4]), op=ALU.mult),
                 reads=[A["ptS"], A["EB"]], writes=[A["ptS"]])
            P.op(DVE, lambda e, kv=kv: e.tensor_tensor(out=pdg[0:NS, :].rearrange("p (s i) -> p s i", i=4), in0=pdg[0:NS, :].rearrange("p (s i) -> p s i", i=4),
                                                in1=ebs[0:NS, 4 * kv:4 * kv + 4].unsqueeze(1).to_broadcast([NS, 16, 4]), op=ALU.mult),
                 reads=[A["pdg"], A["ebs"]], writes=[A["pdg"]])
            P.op(DVE, lambda e: e.tensor_tensor(out=pdg[0:NS, :], in0=pdg[0:NS, :], in1=dgt[0:NS, :], op=ALU.mult),
                 reads=[A["pdg"], A["dgt"]], writes=[A["pdg"]])
            P.op(POOL, lambda e, kv=kv: e.tensor_copy(out=vnb[0:NS, 64 * kv:64 * kv + 64], in_=vnew[0:NS, 64 * kv:64 * kv + 64]),
                 reads=[A["vnew"]], writes=[A["vnb"]])
            P.op(PE, lambda e: e.matmul(ps[bD][:, 0:64], lhsT=ones_k, rhs=ptS[:, 0:64], start=True, stop=False),
                 reads=[A["ones_k"], A["ptS"], A["pdg"], A["esr"]], writes=[psb[bD]], sig=False)
            P.op(PE, lambda e: e.matmul(ps[bD][:, 0:64], lhsT=ones_k[0:NS, :], rhs=pdg[0:NS, :], start=False, stop=False), sig=False)
            P.op(PE, lambda e: e.matmul(ps[bD][:, 0:64], lhsT=ones_k[0:1, :], rhs=esr[0:1, :], start=False, stop=True), sig=True)
            if not P.dead:
                P.attach(Dep(P.esem[PE], P.cnt[PE]), reads=[A["ones_k"], A["ptS"], A["pdg"], A["esr"]], writes=[psb[bD]])
            ptv = ptS[:, 0:64].rearrange("p (s i) -> p s i", i=4)
            pdv = pdg[0:NS, :].rearrange("p (s i) -> p s i", i=4)
            psn = ps[bN][:, 0:64].rearrange("p (s i) -> p s i", i=4)
            for par in range(2):
                for s_ in range(NS):
                    P.op(PE, lambda e, par=par, s_=s_, kv=kv: e.matmul(
                        psn[64 * par:64 * par + 64, s_, par:4:2], lhsT=Vcs[:, s_, 64 * kv:64 * kv + 64], rhs=ptv[:, s_, par:4:2],
                        start=(s_ == 0), stop=False, tile_position=(0, 64 * par)),
                        reads=[A["Vcs"], A["ptS"], A["pdg"], A["vnb"]] if (par == 0 and s_ == 0) else [],
                        writes=[psb[bN]] if (par == 0 and s_ == 0) else [], sig=False)
                P.op(PE, lambda e, par=par, kv=kv: e.matmul(
                    psn[64 * par:64 * par + 64, :, par:4:2], lhsT=vnb[0:NS, 64 * kv:64 * kv + 64], rhs=pdv[:, :, par:4:2],
                    start=False, stop=True, tile_position=(0, 64 * par)), sig=(par == 1))
            if not P.dead:
                P.attach(Dep(P.esem[PE], P.cnt[PE]), reads=[A["Vcs"], A["ptS"], A["pdg"], A["vnb"]], writes=[psb[bN]])
            P.op(DVE, lambda e: e.reciprocal(out=rcs[:, 0:64], in_=ps[bD][:, 0:64]), reads=[psb[bD]], writes=[A["rcs"]])
            rcv = rcs[:, 0:64].rearrange("p (s i) -> p s i", i=4)
            for par in range(2):
                P.op(DVE, lambda e, par=par, kv=kv: e.tensor_tensor(
                    out=oT[64 * par:64 * par + 64, 2 * kv:2 * kv + 2, NP:NT],
                    in0=psn[64 * par:64 * par + 64, :, par:4:2].rearrange("p s c -> p c s"),
                    in1=rcv[64 * par:64 * par + 64, :, par:4:2].rearrange("p s c -> p c s"), op=ALU.mult),
                    reads=[psb[bN], A["rcs"]], wadd=[oT_b[2 * kv], oT_b[2 * kv + 1]])
        for kv in range(4):
            attn_kv(kv)
        P.dma(SP, osem_new("pck"), pck, klast, reads=[A["klast"]])
        P.dma(SP, osem_new("pcv"), pcv, vlast, reads=[A["vlast"]])
        P.dma(SP, osem_new("sck"), sck[:, 127, :], knew[0:NS, :], reads=[A["knew"]])
        P.dma(SP, osem_new("scv"), scv[:, 127, :], vnew[0:NS, :], reads=[A["vnew"]])

        P.phase(13)
        sgaT = cv_(O_C + 0, [8, NT], BF16)
        sga_b = [Buf(f"sga{g}") for g in range(8)]
        gt2 = [cv_(O_C + 33024 + 1024 * i, [512], BF16) for i in range(4)]
        ft2 = [cv_(O_C + 37120 + 2048 * i, [512], F32) for i in range(2)]
        b_gt2 = [Buf() for _ in range(4)]; b_ft2 = [Buf(), Buf()]
        handoff(sga_b + b_gt2 + b_ft2, list(A.values()) + b_Et + b_PT + b_rc)
        for blk in range(2):
            def ev_za(oc, tbi, pap, pb, blk=blk):
                t0, n = TBS[tbi]
                g = 4 * blk + oc
                ctr["i"] += 1
                gi = ctr["i"] % 4; fi = ctr["i"] % 2
                P.op(ACT, lambda e: e.activation(out=gt2[gi][:, 0:n], in_=pap, func=AF.Sigmoid), reads=[pb], writes=[b_gt2[gi]])
                P.op(DVE, lambda e: e.tensor_tensor(out=ft2[fi][:, 0:n], in0=pap, in1=gt2[gi][:, 0:n], op=ALU.mult),
                     reads=[pb, b_gt2[gi]], writes=[b_ft2[fi]])
                P.op(DVE, lambda e: e.tensor_tensor(out=oT[:, g, t0:t0 + n], in0=oT[:, g, t0:t0 + n], in1=ft2[fi][:, 0:n], op=ALU.mult),
                     reads=[b_ft2[fi], oT_b[g]], wadd=[oT_b[g]])
            proj_fm(w_in[:, OFF_ZA + 512 * blk:OFF_ZA + 512 * blk + 512], 512, rhs_h, hT_b, ev_za)
        for blk in range(2):
            def ev_ga(oc, tbi, pap, pb, blk=blk):
                t0, n = TBS[tbi]
                g = 4 * blk + oc
                P.op(ACT, lambda e: e.activation(out=sgaT[:, g, t0:t0 + n], in_=pap, func=AF.Sigmoid), reads=[pb], wadd=[sga_b[g]])
            proj_fm(w_in[:, OFF_GA + 512 * blk:OFF_GA + 512 * blk + 512], 512, rhs_h, hT_b, ev_ga)
        mT = RA
        mT_b = gbs_b
        rhs_o = lambda k, t0, n: oT[:, k, t0:t0 + n]
        for blk in range(2):
            def ev_ba(oc, tbi, pap, pb, blk=blk):
                t0, n = TBS[tbi]
                g = 4 * blk + oc
                ctr["i"] += 1
                fi = ctr["i"] % 2
                P.op(DVE, lambda e: e.tensor_tensor(out=ft2[fi][:, 0:n], in0=pap, in1=sgaT[:, g, t0:t0 + n], op=ALU.mult),
                     reads=[pb, sga_b[g]], writes=[b_ft2[fi]])
                P.op(DVE, lambda e: e.tensor_tensor(out=mT[:, g, t0:t0 + n], in0=mT[:, g, t0:t0 + n], in1=ft2[fi][:, 0:n], op=ALU.add),
                     reads=[b_ft2[fi], mT_b[g]], wadd=[mT_b[g]])
            proj_fm(w_ba[:, 512 * blk:512 * blk + 512], 512, rhs_o, [BufGroup(oT_b)] * 5, ev_ba)

        P.phase(14)
        o2 = O_C + 8000
        NSL = 4
        GateB = cv_(o2 + 0, [1024], F32); LnG = cv_(o2 + 4096, [1024], F32); LnB = cv_(o2 + 8192, [1024], F32)
        gateS = cv_(o2 + 12288, [1024], F32); grow = cv_(o2 + 16384, [1024], F32)
        xt = [cv_(o2 + 20480 + 4096 * i, [1024], F32) for i in range(NSL)]
        rt = [cv_(o2 + 36864 + 4096 * i, [1024], F32) for i in range(NSL)]
        stt = cv_(o2 + 53248, [NSL, 2, 6], F32); mvt = cv_(o2 + 53504, [NSL, 2], F32); rsd = cv_(o2 + 53568, [NSL, 2], F32)
        mhalf = cv_(o2 + 53632, [1], F32)
        b_GateB = Buf(); b_LnG = Buf(); b_LnB = Buf(); b_gateS = Buf(); b_grow = Buf(); b_xt = [Buf() for _ in range(NSL)]; b_rt = [Buf() for _ in range(NSL)]
        b_stt = [Buf() for _ in range(NSL)]; b_mh = Buf()
        xsem2 = [P.dsem(f"xt{i}") for i in range(NSL)]; osem = [P.dsem(f"o{i}") for i in range(NSL)]
        handoff([b_GateB, b_LnG, b_LnB, b_gateS, b_grow] + b_xt + b_rt + b_stt + [b_mh], list(A.values()) + b_Et + b_PT + b_rc + sga_b + b_gt2 + b_ft2)
        misc_load(SP, LnG, ln_g.rearrange("(o n) -> o n", o=1).to_broadcast([128, 1024]), b_LnG)
        misc_load(SP, LnB, ln_b.rearrange("(o n) -> o n", o=1).to_broadcast([128, 1024]), b_LnB)
        P.op(POOL, lambda e: e.memset(mhalf, -0.5), writes=[b_mh])
        for hb in range(2):
            bkg = bank(); bkg2 = bank()
            for kk in range(4):
                k = 4 * hb + kk
                P.op(PE, lambda e, k=k, kk=kk, bkg=bkg: e.transpose(out=ps[bkg][0:1, 128 * kk:128 * kk + 128], in_=modT[:, 16 + k, 0:1], identity=ident),
                     reads=[b_modT, b_ident], writes=[psb[bkg]] if kk == 0 else [], wadd=[psb[bkg]] if kk > 0 else [])
                P.op(PE, lambda e, k=k, kk=kk, bkg2=bkg2: e.transpose(out=ps[bkg2][0:NS, 128 * kk:128 * kk + 128], in_=modT[:, 16 + k, 1:17], identity=ident),
                     reads=[b_modT, b_ident], writes=[psb[bkg2]] if kk == 0 else [], wadd=[psb[bkg2]] if kk > 0 else [])
            P.op(DVE, lambda e, hb=hb, bkg=bkg: e.tensor_copy(out=grow[0:1, 512 * hb:512 * hb + 512], in_=ps[bkg][0:1, 0:512]), reads=[psb[bkg]], wadd=[b_grow])
            P.op(DVE, lambda e, hb=hb, bkg2=bkg2: e.tensor_copy(out=gateS[0:NS, 512 * hb:512 * hb + 512], in_=ps[bkg2][0:NS, 0:512]), reads=[psb[bkg2]], wadd=[b_gateS])
        for hb in range(2):
            bkb = bank()
            P.op(PE, lambda e, hb=hb, bkb=bkb: e.matmul(ps[bkb][:, 0:512], lhsT=ones1[0:1, :], rhs=grow[0:1, 512 * hb:512 * hb + 512], start=True, stop=True),
                 reads=[b_grow, b_ones], writes=[psb[bkb]])
            P.op(DVE, lambda e, hb=hb, bkb=bkb: e.tensor_copy(out=GateB[:, 512 * hb:512 * hb + 512], in_=ps[bkb][:, 0:512]), reads=[psb[bkb]], wadd=[b_GateB])
        so = [load_w(w_out[:, 0:512], 512), load_w(w_out[:, 512:1024], 512)]
        def x_load(ti_):
            rows_, c0_ = (128, 128 * ti_) if ti_ < 16 else (NS, NP)
            src_ = xp[c0_:c0_ + 128, :] if ti_ < 16 else xs
            P.dma(SP, xsem2[ti_ % NSL], xt[ti_ % NSL][0:rows_, :], src_, writes=[b_xt[ti_ % NSL]])
        for ti_ in range(NSL):
            x_load(ti_)
        for tt_i in range(17):
            rows, c0 = (128, 128 * tt_i) if tt_i < 16 else (NS, NP)
            sl = tt_i % NSL
            gate_ap = GateB if tt_i < 16 else gateS
            gate_b = b_GateB if tt_i < 16 else b_gateS
            for fb in range(2):
                bko = bank()
                for k in range(8):
                    P.op(PE, lambda e, bko=bko, k=k, fb=fb, rows=rows, c0=c0: e.matmul(
                        ps[bko][0:rows, 0:512], lhsT=mT[:, k, c0:c0 + rows], rhs=wslot[so[fb]][:, k, 0:512], start=(k == 0), stop=(k == 7)),
                        reads=[wslot_b[so[fb]]] + mT_b if k == 0 else [], writes=[psb[bko]] if k == 0 else [], sig=(k == 7))
                if not P.dead:
                    P.attach(Dep(P.esem[PE], P.cnt[PE]), reads=[wslot_b[so[fb]]] + mT_b, writes=[psb[bko]])
                P.op(DVE, lambda e, bko=bko, fb=fb, rows=rows, sl=sl, gate_ap=gate_ap: e.tensor_tensor(
                    out=rt[sl][0:rows, 512 * fb:512 * fb + 512], in0=ps[bko][0:rows, 0:512], in1=gate_ap[0:rows, 512 * fb:512 * fb + 512], op=ALU.mult),
                    reads=[psb[bko], gate_b], writes=[b_rt[sl]] if fb == 0 else [], wadd=[b_rt[sl]] if fb == 1 else [])
            P.op(DVE, lambda e, rows=rows, sl=sl: e.scalar_tensor_tensor(out=rt[sl][0:rows, :], in0=xt[sl][0:rows, :], scalar=float(ALPHA),
                                                                         in1=rt[sl][0:rows, :], op0=ALU.mult, op1=ALU.add),
                 reads=[b_xt[sl], b_rt[sl]], writes=[b_rt[sl]])
            for hf in range(2):
                P.op(DVE, lambda e, rows=rows, sl=sl, hf=hf: e.bn_stats(out=stt[0:rows, sl, hf, :], in_=rt[sl][0:rows, 512 * hf:512 * hf + 512]),
                     reads=[b_rt[sl]], writes=[b_stt[sl]] if hf == 0 else [], wadd=[b_stt[sl]] if hf == 1 else [])
            P.op(DVE, lambda e, rows=rows, sl=sl: e.bn_aggr(out=mvt[0:rows, sl, :], in_=stt[0:rows, sl].rearrange("p a b -> p (a b)")),
                 reads=[b_stt[sl]], writes=[b_stt[sl]])
            P.op(POOL, lambda e, rows=rows, sl=sl: e.tensor_scalar(out=rsd[0:rows, sl, 0:1], in0=mvt[0:rows, sl, 1:2], scalar1=float(LN_EPS), scalar2=0.0,
                                                                   op0=ALU.add, op1=ALU.add), reads=[b_stt[sl]], writes=[b_stt[sl]])
            P.op(POOL, lambda e, rows=rows, sl=sl: e.tensor_tensor(out=rsd[0:rows, sl, 0:1], in0=rsd[0:rows, sl, 0:1], in1=mhalf[0:rows, :], op=ALU.pow),
                 reads=[b_stt[sl], b_mh], writes=[b_stt[sl]])
            P.op(POOL, lambda e, rows=rows, sl=sl: e.tensor_tensor(out=rsd[0:rows, sl, 1:2], in0=mvt[0:rows, sl, 0:1], in1=rsd[0:rows, sl, 0:1], op=ALU.mult),
                 reads=[b_stt[sl]], writes=[b_stt[sl]])
            P.op(POOL, lambda e, rows=rows, sl=sl: e.tensor_scalar(out=rsd[0:rows, sl, 1:2], in0=rsd[0:rows, sl, 1:2], scalar1=-1.0, scalar2=0.0,
                                                                   op0=ALU.mult, op1=ALU.add), reads=[b_stt[sl]], writes=[b_stt[sl]])
            P.op(ACT, lambda e, rows=rows, sl=sl: e.activation(out=xt[sl][0:rows, :], in_=rt[sl][0:rows, :], func=AF.Identity,
                                                               scale=rsd[0:rows, sl, 0:1], bias=rsd[0:rows, sl, 1:2]),
                 reads=[b_rt[sl], b_stt[sl]], writes=[b_xt[sl]])
            P.op(DVE, lambda e, rows=rows, sl=sl: e.tensor_tensor(out=xt[sl][0:rows, :], in0=xt[sl][0:rows, :], in1=LnG[0:rows, :], op=ALU.mult),
                 reads=[b_xt[sl], b_LnG], writes=[b_xt[sl]])
            P.op(POOL, lambda e, rows=rows, sl=sl: e.tensor_tensor(out=xt[sl][0:rows, :], in0=xt[sl][0:rows, :], in1=LnB[0:rows, :], op=ALU.add),
                 reads=[b_xt[sl], b_LnB], writes=[b_xt[sl]])
            dst = yp[c0:c0 + 128, :] if tt_i < 16 else ys
            P.dma(SP, osem[sl], dst, xt[sl][0:rows, :], reads=[b_xt[sl]])
            if tt_i + NSL < 17:
                x_load(tt_i + NSL)
        final_deps = [Dep(o_.h, o_.cnt) for o_ in osem]

        P.dead = False
        if DEBUG:
            pass
        P.wait(SP, [Dep(dout_sem.h, dout_sem.cnt)] + final_deps + [Dep(d_.h, d_.cnt) for d_ in out_sems])
        P.emit()
        print("instruction counts:", P.ninst)
    return nc


def _bucket_np(dist):
    max_exact = 16
    df = np.maximum(dist, 1).astype(np.float32)
    large = max_exact + (np.log(df / np.float32(max_exact)) / np.float32(math.log(128 / max_exact)) * np.float32(16)).astype(np.int32)
    large = np.minimum(large, 31)
    return np.where(dist < max_exact, dist, large)


def _host_consts():
    R = np.zeros((32, 384), np.float32)
    for i in range(384):
        dist = 255 - i
        if 0 <= dist <= 128:
            R[int(_bucket_np(np.array([dist]))[0]), i] = 1.0
    j = np.arange(128)[:, None]
    q = np.arange(128)[None, :]
    mask = np.concatenate([(j >= q), (j <= q)], axis=1).astype(np.float32)
    r = np.arange(128)
    bmask = (r[:, None] // 32 == r[None, :] // 32).astype(np.float32)
    diag = np.zeros((16, 16, 4), np.float32)
    for s_ in range(16):
        diag[s_, s_, :] = 1.0
    return R, mask, bmask, diag.reshape(16, 64)


_NC_CACHE = {}


def kernel(x_prompt, x_sample, c_prompt, c_sample, state_ssm_re, state_ssm_im, cache_swa_k, cache_swa_v,
           w_ada, b_ada, w_in, ssm_lambda_re, ssm_lambda_im, ssm_log_delta, ssm_b_re, ssm_b_im,
           ssm_c_re, ssm_c_im, ssm_d, w_glu, b_glu, attn_sinks, rel_bias, w_branch_s, w_branch_a,
           w_out, ln_g, ln_b):
    f = lambda a: np.ascontiguousarray(np.asarray(a, dtype=np.float32))
    x_prompt = f(x_prompt); x_sample = f(x_sample); c_prompt = f(c_prompt); c_sample = f(c_sample)
    R, mask, bmask, diag = _host_consts()
    shared = {
        "w_ada": f(w_ada)[0], "b_ada": f(b_ada)[0], "w_in": f(w_in)[0],
        "lam_re": f(ssm_lambda_re)[0], "lam_im": f(ssm_lambda_im)[0], "log_delta": f(ssm_log_delta)[0],
        "b_re": f(ssm_b_re)[0].reshape(4096, 16), "b_im": f(ssm_b_im)[0].reshape(4096, 16),
        "c_re": f(ssm_c_re)[0].reshape(1024, 64), "c_im": f(ssm_c_im)[0].reshape(1024, 64),
        "ssm_d": f(ssm_d)[0], "w_glu": f(w_glu)[0], "b_glu": f(b_glu)[0], "sinks": f(attn_sinks)[0],
        "rel_bias": f(rel_bias), "w_bs": f(w_branch_s)[0], "w_ba": f(w_branch_a)[0], "w_out": f(w_out)[0],
        "ln_g": f(ln_g)[0], "ln_b": f(ln_b)[0],
        "rtab": R, "maskc": mask, "bmaskc": bmask, "diagc": diag,
    }
    sre = f(state_ssm_re)[0].reshape(128, 4096); sim = f(state_ssm_im)[0].reshape(128, 4096)
    ckk = f(cache_swa_k)[0].reshape(128, 128, 256); cvv = f(cache_swa_v)[0].reshape(128, 128, 256)
    in_maps = []
    for c in range(NCORES):
        b, qr = c // 4, c % 4
        t0 = NP * qr
        xh = x_prompt[b, t0 - 128:t0] if qr > 0 else np.zeros((128, D), np.float32)
        flags = np.zeros(32, np.float32)
        flags[0] = 1.0 if qr > 0 else 0.0
        xprev = np.zeros((3, NP, D), np.float32)
        for j in range(3):
            qq = qr - 1 - j
            if qq >= 0:
                flags[1 + j] = 1.0
                xprev[j] = x_prompt[b, NP * qq:NP * qq + NP]
        m = dict(shared)
        m.update({
            "xprev": xprev, "xp": np.ascontiguousarray(x_prompt[b, t0:t0 + NP]), "xh": np.ascontiguousarray(xh),
            "xs": np.ascontiguousarray(x_sample[NS * c:NS * c + NS, 0]),
            "cc": np.ascontiguousarray(np.concatenate([c_prompt[b:b + 1], c_sample[NS * c:NS * c + NS]], 0)),
            "st_re": np.ascontiguousarray(sre[NS * c:NS * c + NS]), "st_im": np.ascontiguousarray(sim[NS * c:NS * c + NS]),
            "ck": np.ascontiguousarray(ckk[NS * c:NS * c + NS]), "cv": np.ascontiguousarray(cvv[NS * c:NS * c + NS]),
            "flags": flags,
        })
        in_maps.append(m)
    nc = build()
    res = run_bass_kernel_spmd(nc, in_maps, core_ids=list(range(NCORES)))
    R_ = res.results
    kernel.last_results = R_
    y_prompt = np.stack([np.concatenate([R_[4 * b + q]["yp"] for q in range(4)], 0) for b in range(2)], 0)
    y_sample = np.concatenate([R_[c]["ys"] for c in range(NCORES)], 0).reshape(128, 1, D)
    p_hr = np.stack([R_[4 * b + 3]["pst_re"].reshape(64, 64) for b in range(2)], 0)[None]
    p_hi = np.stack([R_[4 * b + 3]["pst_im"].reshape(64, 64) for b in range(2)], 0)[None]
    p_k = np.stack([R_[4 * b + 3]["pck"].reshape(128, 4, 64) for b in range(2)], 0)[None]
    p_v = np.stack([R_[4 * b + 3]["pcv"].reshape(128, 4, 64) for b in range(2)], 0)[None]
    s_hr = np.concatenate([R_[c]["sst_re"] for c in range(NCORES)], 0).reshape(1, 128, 64, 64)
    s_hi = np.concatenate([R_[c]["sst_im"] for c in range(NCORES)], 0).reshape(1, 128, 64, 64)
    s_k = np.concatenate([R_[c]["sck"] for c in range(NCORES)], 0).reshape(1, 128, 128, 4, 64)
    s_v = np.concatenate([R_[c]["scv"] for c in range(NCORES)], 0).reshape(1, 128, 128, 4, 64)
    return (y_prompt.astype(np.float32), y_sample.astype(np.float32), p_hr.astype(np.float32), p_hi.astype(np.float32),
            p_k.astype(np.float32), p_v.astype(np.float32), s_hr.astype(np.float32), s_hi.astype(np.float32),
            s_k.astype(np.float32), s_v.astype(np.float32))
```
